# Optimizing a Trainium2 kernel written in Bass

```python
import jax, jax.numpy as jnp
from jax import lax
import numpy as np

D_MODEL = 2048
BATCH = 8
SEQ = 2048
DEPTH = 1

GRID_W = 64
MIX_W = D_MODEL
RWKV_W = MIX_W // 2
ATTN_W = MIX_W - RWKV_W
HEAD_DIM = 64
RWKV_HEADS = RWKV_W // HEAD_DIM
ATTN_Q_HEADS = ATTN_W // HEAD_DIM
ATTN_KV_HEADS = 4
KV_GROUPS = ATTN_Q_HEADS // ATTN_KV_HEADS
KV_W = ATTN_KV_HEADS * HEAD_DIM
DECAY_LORA = 96
AAA_LORA = 96
GATE_LORA = 256
GN_EPS = 64e-5
ROPE_THETA = 10000.0
ROPE_AXIS_DIM = HEAD_DIM // 2
Q_BLOCK = 128
NORM_EPS = 1e-6
PEER_HEADS = 8
PEER_KEY_DIM = 256
PEER_HALF = PEER_KEY_DIM // 2
N_KEYS = 128
N_EXPERTS = N_KEYS * N_KEYS
PEER_TOPK = 16
PEER_CHUNK = 128

RWKV_SPLITS = (RWKV_W, 2 * RWKV_W, 3 * RWKV_W,
               3 * RWKV_W + DECAY_LORA, 3 * RWKV_W + 2 * DECAY_LORA,
               3 * RWKV_W + 2 * DECAY_LORA + AAA_LORA, 3 * RWKV_W + 2 * DECAY_LORA + 2 * AAA_LORA)
RWKV_COLS = 3 * RWKV_W + 2 * DECAY_LORA + 2 * AAA_LORA + GATE_LORA
IN_SPLITS = (RWKV_COLS, RWKV_COLS + ATTN_W, RWKV_COLS + ATTN_W + KV_W)
IN_COLS = RWKV_COLS + ATTN_W + 2 * KV_W

kernel_name = "hybrid_rwkv7_axialgqa_peer_encoder"


def rmsnorm(x, w):
    xf = x.astype(jnp.float32)
    y = xf * lax.rsqrt(jnp.mean(xf * xf, axis=-1, keepdims=True) + NORM_EPS)
    return (y * w.astype(jnp.float32)).astype(x.dtype)


def centred_shift(z, mu_prev, mu_next):
    zp = jnp.pad(z[:, :-1], ((0, 0), (1, 0), (0, 0)))
    zn = jnp.pad(z[:, 1:], ((0, 0), (0, 1), (0, 0)))
    return z + mu_prev * (zp - z) + mu_next * (zn - z)


def rwkv7_bidir(z, mu_prev, mu_next, w0, w2, a0, a2, g2, k_k, k_a, r_k, lnx_w, lnx_b):
    B, S, _ = z.shape
    H, N = RWKV_HEADS, HEAD_DIM
    f32 = jnp.float32
    z = centred_shift(z.astype(f32), mu_prev.astype(f32), mu_next.astype(f32))
    r, k, v, wl_f, wl_b, al_f, al_b, gl = jnp.split(z, RWKV_SPLITS, axis=-1)
    wl = jnp.stack([wl_f, wl_b])
    al = jnp.stack([al_f, al_b])
    w_log = -jax.nn.softplus(-(w0.astype(f32)[:, None, None, :]
                               + jnp.einsum('nbsl,nlc->nbsc', jnp.tanh(wl), w2.astype(f32)))) - 0.5
    decay = jnp.exp(-jnp.exp(w_log))
    a = jax.nn.sigmoid(a0.astype(f32)[:, None, None, :]
                       + jnp.einsum('nbsl,nlc->nbsc', al, a2.astype(f32)))
    g = jax.nn.sigmoid(gl) @ g2.astype(f32)
    kk = (k * k_k.astype(f32)).reshape(B, S, H, N)
    kk = kk / jnp.maximum(jnp.sqrt(jnp.sum(kk * kk, axis=-1, keepdims=True)), 1e-12)
    kk = kk.reshape(B, S, RWKV_W)
    k_dir = k[None] * (1.0 + (a - 1.0) * k_a.astype(f32))
    ka = kk[None] * a

    def dir_seq(arr):
        arr = jnp.stack([arr[0], arr[1][:, ::-1]])
        return jnp.moveaxis(arr.reshape(2, B, S, H, N), 2, 0)

    xs = (dir_seq(jnp.stack([r, r])), dir_seq(decay), dir_seq(k_dir),
          dir_seq(jnp.stack([v, v])), dir_seq(jnp.stack([kk, kk])), dir_seq(ka))

    def step(state, inp):
        r_t, w_t, k_t, v_t, kk_t, ka_t = inp
        sa = jnp.einsum('dbhij,dbhj->dbhi', state, -kk_t)
        state = (state * w_t[..., None, :] + sa[..., :, None] * ka_t[..., None, :]
                 + v_t[..., :, None] * k_t[..., None, :])
        y = jnp.einsum('dbhij,dbhj->dbhi', state, r_t)
        return state, y

    state0 = jnp.zeros((2, B, H, N, N), f32)
    _, ys = lax.scan(step, state0, xs)
    ys = jnp.moveaxis(ys, 0, 2)
    y = ys[0] + ys[1][:, ::-1]
    mu = jnp.mean(y, axis=-1, keepdims=True)
    var = jnp.mean(jnp.square(y - mu), axis=-1, keepdims=True)
    y = ((y - mu) * lax.rsqrt(var + GN_EPS)).reshape(B, S, RWKV_W)
    y = y * lnx_w.astype(f32) + lnx_b.astype(f32)
    k_mean = jnp.mean(k_dir, axis=0).reshape(B, S, H, N)
    bonus = jnp.sum(r.reshape(B, S, H, N) * k_mean * r_k.astype(f32), axis=-1, keepdims=True) \
        * v.reshape(B, S, H, N)
    return (y + bonus.reshape(B, S, RWKV_W)) * g


def axial_rope_tables(S):
    rows_count = S // GRID_W
    rows = jnp.repeat(jnp.arange(rows_count), GRID_W).astype(jnp.float32)
    cols = jnp.tile(jnp.arange(GRID_W), rows_count).astype(jnp.float32)
    inv = ROPE_THETA ** (-jnp.arange(0, ROPE_AXIS_DIM, 2, dtype=jnp.float32) / ROPE_AXIS_DIM)
    ang_r = rows[:, None] * inv[None]
    ang_c = cols[:, None] * inv[None]
    return jnp.cos(ang_r), jnp.sin(ang_r), jnp.cos(ang_c), jnp.sin(ang_c)


def rope_rotate(x, cos, sin):
    x1, x2 = jnp.split(x, 2, axis=-1)
    c, s = cos[None, :, None, :], sin[None, :, None, :]
    return jnp.concatenate([x1 * c - x2 * s, x2 * c + x1 * s], axis=-1)


def axial_rope(x, tabs):
    cr, sr, cc, sc = tabs
    xr, xc = x[..., :ROPE_AXIS_DIM], x[..., ROPE_AXIS_DIM:]
    return jnp.concatenate([rope_rotate(xr, cr, sr), rope_rotate(xc, cc, sc)], axis=-1)


def axial_gqa(q, k, v, q_norm_w, k_norm_w):
    B, S, _ = q.shape
    dt = q.dtype
    q = q.reshape(B, S, ATTN_Q_HEADS, HEAD_DIM)
    k = k.reshape(B, S, ATTN_KV_HEADS, HEAD_DIM)
    v = v.reshape(B, S, ATTN_KV_HEADS, HEAD_DIM)
    tabs = axial_rope_tables(S)
    q = axial_rope(rmsnorm(q, q_norm_w).astype(jnp.float32), tabs) * (HEAD_DIM ** -0.5)
    k = axial_rope(rmsnorm(k, k_norm_w).astype(jnp.float32), tabs)
    q = q.astype(dt).reshape(B, S // Q_BLOCK, Q_BLOCK, ATTN_KV_HEADS, KV_GROUPS, HEAD_DIM)
    q = jnp.moveaxis(q, 1, 0)
    k = k.astype(dt)

    def block(q_blk):
        s = jnp.einsum('bqkgd,bskd->bkgqs', q_blk, k).astype(jnp.float32)
        p = jax.nn.softmax(s, axis=-1)
        return jnp.einsum('bkgqs,bskd->bqkgd', p.astype(v.dtype), v)

    o = lax.map(block, q)
    return jnp.moveaxis(o, 0, 1).reshape(B, S, ATTN_W)


def peer_ffn(h, w_pq, sub_keys, u_tab, v_tab):
    B, S, Dm = h.shape
    T = B * S
    hf = h.reshape(T, Dm)
    q = (hf @ w_pq).reshape(T, PEER_HEADS, 2, PEER_HALF)
    s = jnp.einsum('thpc,hpnc->thpn', q, sub_keys).astype(jnp.float32)
    s1, i1 = lax.top_k(s[:, :, 0], PEER_TOPK)
    s2, i2 = lax.top_k(s[:, :, 1], PEER_TOPK)
    cand = (s1[..., :, None] + s2[..., None, :]).reshape(T, PEER_HEADS, PEER_TOPK * PEER_TOPK)
    best, ci = lax.top_k(cand, PEER_TOPK)
    e1 = jnp.take_along_axis(i1, ci // PEER_TOPK, axis=-1)
    e2 = jnp.take_along_axis(i2, ci % PEER_TOPK, axis=-1)
    expert = (e1 * N_KEYS + e2).reshape(T, PEER_HEADS * PEER_TOPK)
    gate = jax.nn.softmax(best, axis=-1).reshape(T, PEER_HEADS * PEER_TOPK).astype(h.dtype)
    nc = T // PEER_CHUNK

    def chunk(args):
        hc, ec, gc = args
        u = jnp.take(u_tab, ec, axis=0)
        act = jax.nn.gelu(jnp.einsum('cd,ckd->ck', hc, u), approximate=False) * gc
        vv = jnp.take(v_tab, ec, axis=0)
        return jnp.einsum('ck,ckd->cd', act, vv)

    out = lax.map(chunk, (hf.reshape(nc, PEER_CHUNK, Dm),
                          expert.reshape(nc, PEER_CHUNK, -1),
                          gate.reshape(nc, PEER_CHUNK, -1)))
    return out.reshape(B, S, Dm)


def setup_inputs(seed: int = 0) -> dict:
    key = jax.random.key(seed)
    ks = jax.random.split(key, 32)
    L, D, C = DEPTH, D_MODEL, RWKV_W
    nrm = lambda k, shape, scale: jax.random.normal(k, shape, jnp.float32) * scale
    return {
        "x": nrm(ks[0], (BATCH, SEQ, D), 1.0),
        "w_in": nrm(ks[1], (L, D, IN_COLS), D ** -0.5),
        "mu_prev": jax.random.uniform(ks[2], (L, RWKV_COLS), jnp.float32, 0.0, 0.5),
        "mu_next": jax.random.uniform(ks[3], (L, RWKV_COLS), jnp.float32, 0.0, 0.5),
        "w0": jax.random.uniform(ks[4], (L, 2, C), jnp.float32, -6.0, -1.0),
        "w2": nrm(ks[5], (L, 2, DECAY_LORA, C), 0.5 * DECAY_LORA ** -0.5),
        "a0": nrm(ks[6], (L, 2, C), 0.1),
        "a2": nrm(ks[7], (L, 2, AAA_LORA, C), AAA_LORA ** -0.5),
        "g2": nrm(ks[8], (L, GATE_LORA, C), GATE_LORA ** -0.5),
        "k_k": 0.85 + nrm(ks[9], (L, C), 0.05),
        "k_a": 1.0 + nrm(ks[10], (L, C), 0.05),
        "r_k": nrm(ks[11], (L, RWKV_HEADS, HEAD_DIM), 0.1),
        "lnx_w": 1.0 + nrm(ks[12], (L, C), 0.02),
        "lnx_b": nrm(ks[13], (L, C), 0.01),
        "q_norm_w": 1.0 + nrm(ks[14], (L, HEAD_DIM), 0.02),
        "k_norm_w": 1.0 + nrm(ks[15], (L, HEAD_DIM), 0.02),
        "w_out": nrm(ks[16], (L, MIX_W, D), MIX_W ** -0.5),
        "norm1_w": 1.0 + nrm(ks[17], (L, D), 0.02),
        "norm2_w": 1.0 + nrm(ks[18], (L, D), 0.02),
        "w_pq": nrm(ks[19], (L, D, PEER_HEADS * PEER_KEY_DIM), D ** -0.5),
        "sub_keys": nrm(ks[20], (L, PEER_HEADS, 2, N_KEYS, PEER_HALF), PEER_HALF ** -0.5),
        "u_tab": nrm(ks[21], (L, N_EXPERTS, D), D ** -0.5),
        "v_tab": nrm(ks[22], (L, N_EXPERTS, D), 0.1),
        "normf_w": 1.0 + nrm(ks[23], (D,), 0.02),
    }


def reference(x, w_in, mu_prev, mu_next, w0, w2, a0, a2, g2, k_k, k_a, r_k, lnx_w, lnx_b,
              q_norm_w, k_norm_w, w_out, norm1_w, norm2_w, w_pq, sub_keys, u_tab, v_tab, normf_w):
    for l in range(DEPTH):
        h = rmsnorm(x, norm1_w[l])
        p = h @ w_in[l]
        z_rwkv, q, k, v = jnp.split(p, IN_SPLITS, axis=-1)
        y_rwkv = rwkv7_bidir(z_rwkv, mu_prev[l], mu_next[l], w0[l], w2[l], a0[l], a2[l], g2[l],
                             k_k[l], k_a[l], r_k[l], lnx_w[l], lnx_b[l]).astype(x.dtype)
        y_attn = axial_gqa(q, k, v, q_norm_w[l], k_norm_w[l])
        x = x + jnp.concatenate([y_rwkv, y_attn], axis=-1) @ w_out[l]
        h2 = rmsnorm(x, norm2_w[l])
        x = x + peer_ffn(h2, w_pq[l], sub_keys[l], u_tab[l], v_tab[l])
    return rmsnorm(x, normf_w)
```

```python
import numpy as np
import contextlib
import concourse.bass as bass
import concourse.mybir as mybir
from concourse.bass_utils import run_bass_kernel_spmd

F32 = mybir.dt.float32
BF16 = mybir.dt.bfloat16
U32 = mybir.dt.uint32
ALU = mybir.AluOpType
AF = mybir.ActivationFunctionType
AX = mybir.AxisListType


class T:
    def __init__(self, h, name):
        self.h = h
        self.name = name
        self.w = None
        self.r = {}
        self.dsem = None
        self.dcnt = 0
        self.is_psum = False

    def __getitem__(self, k):
        return self.h[k]

    def ap(self):
        return self.h.ap() if hasattr(self.h, "ap") else self.h[:]


class KB:
    def __init__(self, nc):
        self.nc = nc
        self.es = contextlib.ExitStack()
        self.engs = {"pe": nc.tensor, "act": nc.scalar, "dve": nc.vector, "pool": nc.gpsimd, "sp": nc.sync}
        self.sem = {}
        self.cnt = {}
        for e in self.engs:
            self.sem[e] = self.es.enter_context(nc.semaphore("s_" + e))
            self.cnt[e] = 0
        self.waited = {}
        self.alltensors = []
        self.dsems = []
        self.n_ins = 0

    def sb(self, name, shape, dt=F32, stack=None):
        h = (stack or self.es).enter_context(self.nc.sbuf_tensor(name, list(shape), dt))
        t = T(h, name)
        return t

    def ps(self, name, shape, dt=F32, stack=None):
        h = (stack or self.es).enter_context(self.nc.psum_tensor(name, list(shape), dt))
        t = T(h, name)
        t.is_psum = True
        return t

    def dram(self, name, shape, dt=F32, kind="Internal"):
        h = self.nc.dram_tensor(name, list(shape), dt, kind=kind)
        return T(h, name)

    def view(self, t, name=None):
        return t

    def _wait(self, eng, tok):
        if tok is None:
            return
        sem, val = tok
        key = (eng, id(sem))
        if self.waited.get(key, 0) >= val:
            return
        self.engs[eng].wait_ge(sem, val)
        self.waited[key] = val

    def _deps(self, eng, reads, writes):
        own = id(self.sem[eng])
        for t in reads:
            if t.w is not None:
                if eng == "pe" and id(t.w[0]) == own:
                    continue
                self._wait(eng, t.w)
        for t in writes:
            if t.w is not None and id(t.w[0]) != own:
                self._wait(eng, t.w)
            for k, tok in t.r.items():
                if k == own:
                    continue
                self._wait(eng, tok)

    def _mark(self, tok, reads, writes):
        for t in reads:
            if t in writes:
                continue
            t.r[id(tok[0])] = tok
        for t in writes:
            t.w = tok
            t.r = {}

    def op(self, eng, fn, reads=(), writes=()):
        psr = [t for t in reads if t.is_psum and t not in writes]
        if psr:
            writes = list(writes) + psr
        self._deps(eng, reads, writes)
        ins = fn(self.engs[eng])
        self.cnt[eng] += 1
        ins.then_inc(self.sem[eng], 1)
        tok = (self.sem[eng], self.cnt[eng])
        self._mark(tok, reads, writes)
        self.n_ins += 1
        return ins

    def dma(self, q, out_ap, in_ap, reads=(), writes=(), **kw):
        assert len(writes) == 1
        dst = writes[0]
        if dst.dsem is None:
            dst.dsem = self.es.enter_context(self.nc.semaphore("d_" + dst.name))
            self.dsems.append(dst)
        self._deps(q, reads, [])
        if dst.w is not None and dst.w[0] is not dst.dsem:
            self._wait(q, dst.w)
        for k, tok in dst.r.items():
            self._wait(q, tok)
        ins = self.engs[q].dma_start(out=out_ap, in_=in_ap, **kw)
        dst.dcnt += 16
        ins.then_inc(dst.dsem, 16)
        tok = (dst.dsem, dst.dcnt)
        for t in reads:
            t.r[id(tok[0])] = tok
        dst.w = tok
        dst.r = {}
        self.n_ins += 1
        return ins

    def barrier(self):
        toks = [(self.sem[e], self.cnt[e]) for e in self.engs if self.cnt[e] > 0]
        toks += [(t.dsem, t.dcnt) for t in self.dsems if t.dcnt > 0]
        for e in self.engs:
            for tok in toks:
                if tok[0] is self.sem[e]:
                    continue
                self._wait(e, tok)

    def finish(self, eng="sp"):
        for t in self.dsems:
            if t.dcnt > 0:
                self._wait(eng, (t.dsem, t.dcnt))
        for e in self.engs:
            if e != eng and self.cnt[e] > 0:
                self._wait(eng, (self.sem[e], self.cnt[e]))


EPS = 1e-6
NT = 16
RCH = [(i * 128, 128) for i in range(24)] + [(3072 + i * 96, 96) for i in range(4)] + [(3456, 128), (3584, 128)]
NRCH = len(RCH)
RW = 3712


class Obj:
    pass


def declare(kb, debug):
    D = Obj()

    def inp(name, shape, dt=F32):
        setattr(D, name, kb.dram(name, shape, dt, kind="ExternalInput"))

    def scr(name, shape, dt=F32):
        kind = "ExternalOutput" if (debug and name in debug) else "Internal"
        setattr(D, name, kb.dram(name, shape, dt, kind=kind))

    inp("x", [2048, 2048])
    inp("w_in", [2048, 5248])
    inp("mu_c", [128, 3 * NRCH])
    inp("norm1_w", [2048])
    inp("norm2_w", [2048])
    inp("normf_w", [2048])
    inp("q_norm_w", [64])
    inp("k_norm_w", [64])
    inp("tabC", [2048, 32])
    inp("tabS", [2048, 32])
    inp("ident", [128, 128])
    inp("w_out", [2048, 2048])
    inp("w_pq", [2048, 2048])
    inp("skT", [128, 16, 128])
    inp("iota", [128, 128])
    inp("u_tabT", [2048, 16384])
    inp("v_tab", [16384, 2048])
    inp("w2", [2, 96, 1024])
    inp("a2", [2, 96, 1024])
    inp("g2", [256, 1024])
    inp("rw_c", [128, 8, 8])
    inp("lnx_w", [1024])
    inp("lnx_b", [1024])
    inp("masks", [128, 6, 128])
    if debug and "in_ymT" in debug:
        inp("ymT_in", [16, 128, 2048], BF16)
    if debug and "in_x1" in debug:
        inp("x1_in", [2048, 2048])
    scr("zs_d", [RW, 2048])
    scr("qkv_d", [2048, 1536])
    scr("ymT_d", [16, 128, 2048], BF16)
    scr("x1_d", [2048, 2048])
    scr("G_d", [128, 128, 2048], BF16)
    scr("peT_d", [2048, 2048])
    scr("A_d", [128, 128, 2048], BF16)
    scr("ld_d", [2, 1024, 2048])
    scr("a_d", [2, 1024, 2048])
    scr("g_d", [2048, 1024])
    inp("rcst", [128, 66 + 2048])
    if debug and "dbgY" in debug:
        setattr(D, "dbgY", kb.dram("dbgY", [2048, 1024], F32, kind="ExternalOutput"))
        setattr(D, "dbgF", kb.dram("dbgF", [2048, 1024], F32, kind="ExternalOutput"))
    scr("S_d", [2048, 16, 128])
    scr("h2T_d", [16, 128, 2048], BF16)
    setattr(D, "out", kb.dram("out", [2048, 2048], F32, kind="ExternalOutput"))
    return D


def consts(kb, D):
    C = Obj()
    C.ident = kb.sb("c_ident", [128, 128], F32)
    kb.dma("sp", C.ident[:], D.ident.ap(), writes=[C.ident])
    C.identb = kb.sb("c_identb", [128, 128], BF16)
    kb.op("dve", lambda e: e.tensor_copy(out=C.identb[:], in_=C.ident[:]), reads=[C.ident], writes=[C.identb])
    return C


def norm_to_T(kb, C, src, wvec, hT, tag):
    with contextlib.ExitStack() as st:
        wb = kb.sb(tag + "wb", [128, 2048], F32, st)
        kb.dma("pool", wb[:], wvec.ap().partition_broadcast(128), writes=[wb])
        xts = [kb.sb(f"{tag}x{i}", [128, 2048], F32, st) for i in range(2)]
        junk = kb.sb(tag + "junk", [128, 2048], BF16, st)
        hb = [kb.sb(f"{tag}h{i}", [128, 2048], BF16, st) for i in range(2)]
        ss = kb.sb(tag + "ss", [128, NT], F32, st)
        rs = kb.sb(tag + "rs", [128, NT], F32, st)
        ptr = [kb.ps(f"{tag}ps{i}", [128, 1024], BF16, st) for i in range(2)]
        kb.op("pool", lambda e: e.memset(ss[:], 0.0), writes=[ss])
        for tt in range(NT):
            xt = xts[tt % 2]
            kb.dma("sp", xt[:], src.ap()[tt * 128:(tt + 1) * 128, :], writes=[xt])
            kb.op("act", lambda e: e.activation(out=junk[:], in_=xt[:], func=AF.Square, accum_out=ss[:, tt:tt + 1]),
                  reads=[xt], writes=[junk, ss])
            kb.op("dve", lambda e: e.tensor_scalar(out=rs[:, tt:tt + 1], in0=ss[:, tt:tt + 1], scalar1=1.0 / 2048, scalar2=EPS,
                                                   op0=ALU.mult, op1=ALU.add), reads=[ss], writes=[rs])
            kb.op("act", lambda e: e.activation(out=rs[:, tt:tt + 1], in_=rs[:, tt:tt + 1], func=AF.Sqrt), reads=[rs], writes=[rs])
            kb.op("dve", lambda e: e.reciprocal(out=rs[:, tt:tt + 1], in_=rs[:, tt:tt + 1]), reads=[rs], writes=[rs])
            h = hb[tt % 2]
            kb.op("dve", lambda e: e.scalar_tensor_tensor(out=h[:], in0=xt[:], scalar=rs[:, tt:tt + 1], in1=wb[:],
                                                          op0=ALU.mult, op1=ALU.mult), reads=[xt, rs, wb], writes=[h])
            for g in range(2):
                p = ptr[g]
                for j in range(8):
                    dc = g * 8 + j
                    kb.op("pe", lambda e: e.transpose(out=p[:, j * 128:(j + 1) * 128], in_=h[:, dc * 128:(dc + 1) * 128],
                                                      identity=C.identb[:]), reads=[h, C.identb], writes=[p])
                eng = "act" if g == 0 else "dve"
                src_ap = p[:].rearrange("p (j t) -> p j t", j=8)
                dst_ap = hT[:, g * 8:(g + 1) * 8, tt * 128:(tt + 1) * 128]
                if eng == "act":
                    kb.op("act", lambda e: e.copy(out=dst_ap, in_=src_ap), reads=[p], writes=[hT])
                else:
                    kb.op("dve", lambda e: e.tensor_copy(out=dst_ap, in_=src_ap), reads=[p], writes=[hT])
        kb.barrier()


def phase_A(kb, C, D):
    with contextlib.ExitStack() as st:
        hT = kb.sb("hT", [128, 16, 2048], BF16, st)
        norm_to_T(kb, C, D.x, D.norm1_w, hT, "n1")
        mu = kb.sb("mu", [128, 3 * NRCH], F32, st)
        kb.dma("sp", mu[:], D.mu_c.ap(), writes=[mu])
        kb.op("dve", lambda e: e.tensor_tensor(out=mu[:, 60:90], in0=mu[:, 0:30], in1=mu[:, 30:60], op=ALU.add), reads=[mu], writes=[mu])
        kb.op("dve", lambda e: e.tensor_scalar(out=mu[:, 60:90], in0=mu[:, 60:90], scalar1=-1.0, scalar2=1.0, op0=ALU.mult, op1=ALU.add),
              reads=[mu], writes=[mu])
        stg = [kb.sb(f"a_stg{i}", [128, 4096], F32, st) for i in range(2)]
        wbf = [kb.sb(f"a_wbf{i}", [128, 16, 128], BF16, st) for i in range(2)]
        accs = [kb.sb(f"a_acc{i}", [128, 2048], F32, st) for i in range(2)]
        pss = [[kb.ps(f"a_ps{i}_{j}", [128, 512], F32, st) for j in range(4)] for i in range(2)]
        w_v = D.w_in.ap().rearrange("(dc p) c -> p dc c", p=128)
        for ci, (c0, cs) in enumerate(RCH):
            sg = stg[ci % 2]
            sgv = sg[:, 0:16 * cs].rearrange("p (dc c) -> p dc c", dc=16)
            kb.dma("sp" if ci % 2 == 0 else "act", sgv, w_v[:, :, c0:c0 + cs], writes=[sg])
            wb = wbf[ci % 2]
            kb.op("pool", lambda e: e.tensor_copy(out=wb[:, :, 0:cs], in_=sgv), reads=[sg], writes=[wb])
            ps = pss[ci % 2]
            acc = accs[ci % 2]
            for tb in range(4):
                for dc in range(16):
                    kb.op("pe", lambda e: e.matmul(ps[tb][0:cs, :], lhsT=wb[:, dc, 0:cs], rhs=hT[:, dc, tb * 512:(tb + 1) * 512],
                                                   start=(dc == 0), stop=(dc == 15)), reads=[wb, hT], writes=[ps[tb]])
            for tb in range(4):
                kb.op("act", lambda e: e.activation(out=acc[0:cs, tb * 512:(tb + 1) * 512], in_=ps[tb][0:cs, :], func=AF.Copy,
                                                    scale=mu[0:cs, 60 + ci:61 + ci]), reads=[ps[tb], mu], writes=[acc])
            for tb in range(4):
                n = 512 if tb < 3 else 511
                d0 = tb * 512 + 1
                kb.op("dve", lambda e: e.scalar_tensor_tensor(out=acc[0:cs, d0:d0 + n], in0=ps[tb][0:cs, 0:n], scalar=mu[0:cs, ci:ci + 1],
                                                              in1=acc[0:cs, d0:d0 + n], op0=ALU.mult, op1=ALU.add),
                      reads=[ps[tb], mu, acc], writes=[acc])
                s0 = 1 if tb == 0 else 0
                n = 512 - s0
                d0 = tb * 512 + s0 - 1
                kb.op("dve", lambda e: e.scalar_tensor_tensor(out=acc[0:cs, d0:d0 + n], in0=ps[tb][0:cs, s0:512], scalar=mu[0:cs, 30 + ci:31 + ci],
                                                              in1=acc[0:cs, d0:d0 + n], op0=ALU.mult, op1=ALU.add),
                      reads=[ps[tb], mu, acc], writes=[acc])
            kb.dma("pool", D.zs_d.ap()[c0:c0 + cs, :], acc[0:cs, :], reads=[acc], writes=[D.zs_d])
        kb.barrier()
        with contextlib.ExitStack() as st2:
            wq = kb.sb("a_wq", [128, 16, 512], BF16, st2)
            ev = [kb.sb(f"a_ev{i}", [128, 512], F32, st2) for i in range(2)]
            for cg in range(3):
                c0 = RW + cg * 512
                for hf in range(2):
                    sg = stg[hf]
                    sgv = sg[:].rearrange("p (dc c) -> p dc c", dc=8)
                    kb.dma("sp" if hf == 0 else "act", sgv, w_v[:, hf * 8:(hf + 1) * 8, c0:c0 + 512], writes=[sg])
                    kb.op("pool", lambda e: e.tensor_copy(out=wq[:, hf * 8:(hf + 1) * 8, :], in_=sgv), reads=[sg], writes=[wq])
                for tt in range(NT):
                    ps = pss[tt % 2][0]
                    for dc in range(16):
                        kb.op("pe", lambda e: e.matmul(ps[:], lhsT=hT[:, dc, tt * 128:(tt + 1) * 128], rhs=wq[:, dc, :],
                                                       start=(dc == 0), stop=(dc == 15)), reads=[wq, hT], writes=[ps])
                    o = ev[tt % 2]
                    kb.op("act", lambda e: e.copy(out=o[:], in_=ps[:]), reads=[ps], writes=[o])
                    kb.dma("sp", D.qkv_d.ap()[tt * 128:(tt + 1) * 128, cg * 512:(cg + 1) * 512], o[:], reads=[o], writes=[D.qkv_d])
            kb.barrier()


def phase_T(kb, C, D):
    with contextlib.ExitStack() as st:
        qT = kb.sb("t_qT", [128, 8, 2048], BF16, st)
        kT2 = kb.sb("t_kT2", [128, 4, 2048], BF16, st)
        vaug = kb.sb("t_vaug", [128, NT, 4, 65], BF16, st)
        wqk = kb.sb("t_wqk", [128, 20, 64], F32, st)
        w64 = kb.sb("t_w64", [128, 2, 64], F32, st)
        kb.dma("pool", w64[:, 0, :], D.q_norm_w.ap().partition_broadcast(128), writes=[w64])
        kb.dma("pool", w64[:, 1, :], D.k_norm_w.ap().partition_broadcast(128), writes=[w64])
        kb.op("dve", lambda e: e.tensor_scalar(out=wqk[:, 0:16, :], in0=w64[:, 0:1, :].to_broadcast([128, 16, 64]), scalar1=0.125, scalar2=None,
                                               op0=ALU.mult), reads=[w64], writes=[wqk])
        kb.op("dve", lambda e: e.tensor_copy(out=wqk[:, 16:20, :], in_=w64[:, 1:2, :].to_broadcast([128, 4, 64])), reads=[w64], writes=[wqk])
        kb.op("pool", lambda e: e.memset(vaug[:], 1.0), writes=[vaug])
        with contextlib.ExitStack() as st2:
            qk = [kb.sb(f"t_qk{i}", [128, 1536], F32, st2) for i in range(2)]
            tC = [kb.sb(f"t_tC{i}", [128, 2, 16], F32, st2) for i in range(2)]
            tS = [kb.sb(f"t_tS{i}", [128, 2, 16], F32, st2) for i in range(2)]
            sq = kb.sb("t_sq", [128, 20, 64], F32, st2)
            ss = kb.sb("t_ss", [128, 20], F32, st2)
            qn = kb.sb("t_qn", [128, 20, 64], F32, st2)
            t1 = kb.sb("t_t1", [128, 20, 2, 16], F32, st2)
            t2 = kb.sb("t_t2", [128, 20, 2, 16], F32, st2)
            t3 = kb.sb("t_t3", [128, 20, 2, 16], F32, st2)
            t4 = kb.sb("t_t4", [128, 20, 2, 16], F32, st2)
            qkr = kb.sb("t_qkr", [128, 20, 64], BF16, st2)
            kd = kb.sb("t_kd", [128, 4, 2, 64], BF16, st2)
            pq = kb.ps("t_pq", [128, 1024], BF16, st2)
            pk = kb.ps("t_pk", [128, 1024], BF16, st2)
            for tt in range(NT):
                q = qk[tt % 2]
                cC = tC[tt % 2]
                cS = tS[tt % 2]
                kb.dma("sp", q[:], D.qkv_d.ap()[tt * 128:(tt + 1) * 128, :], reads=[D.qkv_d], writes=[q])
                kb.dma("act", cC[:].rearrange("p a b -> p (a b)"), D.tabC.ap()[tt * 128:(tt + 1) * 128, :], writes=[cC])
                kb.dma("act", cS[:].rearrange("p a b -> p (a b)"), D.tabS.ap()[tt * 128:(tt + 1) * 128, :], writes=[cS])
                qv = q[:, 0:1280].rearrange("p (h d) -> p h d", h=20)
                kb.op("act", lambda e: e.activation(out=sq[:], in_=qv, func=AF.Square), reads=[q], writes=[sq])
                kb.op("dve", lambda e: e.tensor_reduce(out=ss[:], in_=sq[:], axis=AX.X, op=ALU.add), reads=[sq], writes=[ss])
                kb.op("dve", lambda e: e.tensor_scalar(out=ss[:], in0=ss[:], scalar1=1.0 / 64, scalar2=EPS, op0=ALU.mult, op1=ALU.add),
                      reads=[ss], writes=[ss])
                kb.op("act", lambda e: e.activation(out=ss[:], in_=ss[:], func=AF.Sqrt), reads=[ss], writes=[ss])
                kb.op("dve", lambda e: e.reciprocal(out=ss[:], in_=ss[:]), reads=[ss], writes=[ss])
                kb.op("dve", lambda e: e.tensor_tensor(out=qn[:], in0=qv, in1=ss[:, :, None].to_broadcast([128, 20, 64]), op=ALU.mult),
                      reads=[q, ss], writes=[qn])
                kb.op("pool", lambda e: e.tensor_tensor(out=qn[:], in0=qn[:], in1=wqk[:], op=ALU.mult), reads=[qn, wqk], writes=[qn])
                qn5 = qn[:].rearrange("p h (a b c) -> p h a b c", a=2, b=2)
                x1 = qn5[:, :, :, 0, :]
                x2 = qn5[:, :, :, 1, :]
                Cb = cC[:, None, :, :].to_broadcast([128, 20, 2, 16])
                Sb = cS[:, None, :, :].to_broadcast([128, 20, 2, 16])
                kb.op("dve", lambda e: e.tensor_tensor(out=t1[:], in0=x1, in1=Cb, op=ALU.mult), reads=[qn, cC], writes=[t1])
                kb.op("pool", lambda e: e.tensor_tensor(out=t2[:], in0=x2, in1=Sb, op=ALU.mult), reads=[qn, cS], writes=[t2])
                kb.op("pool", lambda e: e.tensor_tensor(out=t3[:], in0=x2, in1=Cb, op=ALU.mult), reads=[qn, cC], writes=[t3])
                kb.op("dve", lambda e: e.tensor_tensor(out=t4[:], in0=x1, in1=Sb, op=ALU.mult), reads=[qn, cS], writes=[t4])
                r5 = qkr[:].rearrange("p h (a b c) -> p h a b c", a=2, b=2)
                kb.op("dve", lambda e: e.tensor_tensor(out=r5[:, :, :, 0, :], in0=t1[:], in1=t2[:], op=ALU.subtract), reads=[t1, t2], writes=[qkr])
                kb.op("pool", lambda e: e.tensor_tensor(out=r5[:, :, :, 1, :], in0=t3[:], in1=t4[:], op=ALU.add), reads=[t3, t4], writes=[qkr])
                kb.op("pool", lambda e: e.tensor_copy(out=kd[:], in_=qkr[:, 16:20, None, :].to_broadcast([128, 4, 2, 64])), reads=[qkr], writes=[kd])
                for j in range(8):
                    kb.op("pe", lambda e: e.transpose(out=pq[:, j * 128:(j + 1) * 128], in_=qkr[:, 2 * j:2 * j + 2, :].rearrange("p a b -> p (a b)"),
                                                      identity=C.identb[:]), reads=[qkr, C.identb], writes=[pq])
                kb.op("act", lambda e: e.copy(out=qT[:, :, tt * 128:(tt + 1) * 128], in_=pq[:].rearrange("p (j t) -> p j t", j=8)),
                      reads=[pq], writes=[qT])
                for j in range(4):
                    kb.op("pe", lambda e: e.transpose(out=pk[:, j * 128:(j + 1) * 128], in_=kd[:, j, :, :].rearrange("p a b -> p (a b)"),
                                                      identity=C.identb[:]), reads=[kd, C.identb], writes=[pk])
                kb.op("dve", lambda e: e.tensor_copy(out=kT2[:, :, tt * 128:(tt + 1) * 128], in_=pk[:, 0:512].rearrange("p (j t) -> p j t", j=4)),
                      reads=[pk], writes=[kT2])
                kb.op("pool", lambda e: e.tensor_copy(out=vaug[:, tt, :, 0:64], in_=q[:, 1280:1536].rearrange("p (h d) -> p h d", h=4)),
                      reads=[q], writes=[vaug])
            kb.barrier()
        with contextlib.ExitStack() as st3:
            yatt = kb.sb("t_yatt", [128, NT, 1024], BF16, st3)
            pexp = [kb.sb(f"t_pexp{i}", [128, 512], BF16, st3) for i in range(3)]
            rinv = kb.sb("t_rinv", [128, 4], F32, st3)
            pss = [kb.ps(f"t_pss{i}", [128, 512], F32, st3) for i in range(2)]
            po = [kb.ps(f"t_po{i}", [128, 512], F32, st3) for i in range(4)]
            it = 0
            for h in range(16):
                kv = h // 4
                c = h // 2
                b0 = (h % 2) * 64
                for qb in range(4):
                    for kt in range(NT):
                        ps = pss[it % 2]
                        pe_ = pexp[it % 3]
                        it += 1
                        kb.op("pe", lambda e: e.matmul(ps[:], lhsT=kT2[b0:b0 + 64, kv, kt * 128:(kt + 1) * 128],
                                                       rhs=qT[b0:b0 + 64, c, qb * 512:(qb + 1) * 512], start=True, stop=True),
                              reads=[kT2, qT], writes=[ps])
                        kb.op("act", lambda e: e.activation(out=pe_[:], in_=ps[:], func=AF.Exp), reads=[ps], writes=[pe_])
                        for j in range(4):
                            kb.op("pe", lambda e: e.matmul(po[j][:, 0:65], lhsT=pe_[:, j * 128:(j + 1) * 128], rhs=vaug[:, kt, kv, :],
                                                           start=(kt == 0), stop=(kt == NT - 1)), reads=[pe_, vaug], writes=[po[j]])
                    for j in range(4):
                        kb.op("dve", lambda e: e.reciprocal(out=rinv[:, j:j + 1], in_=po[j][:, 64:65]), reads=[po[j]], writes=[rinv])
                        kb.op("dve", lambda e: e.tensor_scalar(out=yatt[:, qb * 4 + j, h * 64:(h + 1) * 64], in0=po[j][:, 0:64],
                                                               scalar1=rinv[:, j:j + 1], scalar2=None, op0=ALU.mult),
                              reads=[po[j], rinv], writes=[yatt])
            pt = [kb.ps(f"t_pt{i}", [128, 1024], BF16, st3) for i in range(2)]
            yT = [kb.sb(f"t_yT{i}", [128, 8, 128], BF16, st3) for i in range(2)]
            for tt in range(NT):
                p = pt[tt % 2]
                o = yT[tt % 2]
                for j in range(8):
                    kb.op("pe", lambda e: e.transpose(out=p[:, j * 128:(j + 1) * 128], in_=yatt[:, tt, j * 128:(j + 1) * 128],
                                                      identity=C.identb[:]), reads=[yatt, C.identb], writes=[p])
                kb.op("act", lambda e: e.copy(out=o[:], in_=p[:].rearrange("p (j t) -> p j t", j=8)), reads=[p], writes=[o])
                kb.dma("sp", D.ymT_d.ap()[8:16, :, tt * 128:(tt + 1) * 128].rearrange("j p t -> p j t"), o[:], reads=[o], writes=[D.ymT_d])
            kb.barrier()


def phase_O(kb, C, D, ym_src):
    with contextlib.ExitStack() as st:
        ymT = kb.sb("o_ymT", [128, 16, 2048], BF16, st)
        for j in range(16):
            kb.dma("sp" if j % 2 == 0 else "act", ymT[:, j, :], ym_src.ap()[j], reads=[ym_src], writes=[ymT])
        stg = [kb.sb(f"o_stg{i}", [128, 4096], F32, st) for i in range(2)]
        wo = kb.sb("o_wo", [128, 16, 512], BF16, st)
        xs = [kb.sb(f"o_xs{i}", [128, 512], F32, st) for i in range(2)]
        pss = [kb.ps(f"o_ps{i}", [128, 512], F32, st) for i in range(2)]
        w_v = D.w_out.ap().rearrange("(kc p) c -> p kc c", p=128)
        for dg in range(4):
            for hf in range(2):
                sg = stg[hf]
                sgv = sg[:].rearrange("p (dc c) -> p dc c", dc=8)
                kb.dma("sp" if hf == 0 else "act", sgv, w_v[:, hf * 8:(hf + 1) * 8, dg * 512:(dg + 1) * 512], writes=[sg])
                kb.op("pool", lambda e: e.tensor_copy(out=wo[:, hf * 8:(hf + 1) * 8, :], in_=sgv), reads=[sg], writes=[wo])
            for tt in range(NT):
                ps = pss[tt % 2]
                xt = xs[tt % 2]
                kb.dma("sp", xt[:], D.x.ap()[tt * 128:(tt + 1) * 128, dg * 512:(dg + 1) * 512], writes=[xt])
                for kc in range(16):
                    kb.op("pe", lambda e: e.matmul(ps[:], lhsT=ymT[:, kc, tt * 128:(tt + 1) * 128], rhs=wo[:, kc, :],
                                                   start=(kc == 0), stop=(kc == 15)), reads=[ymT, wo], writes=[ps])
                kb.op("dve", lambda e: e.tensor_tensor(out=xt[:], in0=ps[:], in1=xt[:], op=ALU.add), reads=[ps, xt], writes=[xt])
                kb.dma("act", D.x1_d.ap()[tt * 128:(tt + 1) * 128, dg * 512:(dg + 1) * 512], xt[:], reads=[xt], writes=[D.x1_d])
        kb.barrier()


def phase_P(kb, C, D):
    with contextlib.ExitStack() as stP:
        ET = kb.sb("p_ET", [128, 3, 2048], F32, stP)
        iota = kb.sb("p_iota", [128, 128], F32, stP)
        kb.dma("sp", iota[:], D.iota.ap(), writes=[iota])
        with contextlib.ExitStack() as st:
            h2T = kb.sb("p_h2T", [128, 16, 2048], BF16, st)
            norm_to_T(kb, C, D.x1_d, D.norm2_w, h2T, "n2")
            for dc in range(16):
                kb.dma("sp" if dc % 2 == 0 else "act", D.h2T_d.ap()[dc], h2T[:, dc, :], reads=[h2T], writes=[D.h2T_d])
            skb = kb.sb("p_sk", [128, 16, 128], F32, st)
            kb.dma("sp", skb[:], D.skT.ap(), writes=[skb])
            stg = [kb.sb(f"p_stg{i}", [128, 16, 128], F32, st) for i in range(2)]
            wb = [kb.sb(f"p_wb{i}", [128, 16, 128], BF16, st) for i in range(2)]
            qTs = [kb.sb(f"p_qT{i}", [128, 2048], F32, st) for i in range(2)]
            sev = [kb.sb(f"p_sev{i}", [128, 16, 128], F32, st) for i in range(2)]
            pss = [[kb.ps(f"p_ps{i}_{j}", [128, 512], F32, st) for j in range(3)] for i in range(2)]
            w_v = D.w_pq.ap().rearrange("(dc p) c -> p dc c", p=128)
            for hp in range(16):
                sg = stg[hp % 2]
                kb.dma("sp" if hp % 2 == 0 else "act", sg[:], w_v[:, :, hp * 128:(hp + 1) * 128], writes=[sg])
                w = wb[hp % 2]
                kb.op("pool", lambda e: e.tensor_copy(out=w[:], in_=sg[:]), reads=[sg], writes=[w])
                qT = qTs[hp % 2]
                for tb in range(4):
                    ps = pss[tb % 2][0]
                    for dc in range(16):
                        kb.op("pe", lambda e: e.matmul(ps[:], lhsT=w[:, dc, :], rhs=h2T[:, dc, tb * 512:(tb + 1) * 512],
                                                       start=(dc == 0), stop=(dc == 15)), reads=[w, h2T], writes=[ps])
                    kb.op("act", lambda e: e.copy(out=qT[:, tb * 512:(tb + 1) * 512], in_=ps[:]), reads=[ps], writes=[qT])
                se = sev[hp % 2]
                for g in range(4):
                    ps = pss[g % 2][1 + (g // 2) % 2]
                    for j in range(4):
                        tt = g * 4 + j
                        kb.op("pe", lambda e: e.matmul(ps[:, j * 128:(j + 1) * 128], lhsT=qT[:, tt * 128:(tt + 1) * 128], rhs=skb[:, hp, :],
                                                       start=True, stop=True), reads=[qT, skb], writes=[ps])
                    kb.op("dve", lambda e: e.tensor_copy(out=se[:, g * 4:(g + 1) * 4, :], in_=ps[:].rearrange("p (j n) -> p j n", j=4)),
                          reads=[ps], writes=[se])
                kb.dma("pool", D.S_d.ap()[:, hp, :].rearrange("(tt p) n -> p tt n", p=128), se[:], reads=[se], writes=[D.S_d])
            kb.barrier()
        with contextlib.ExitStack() as st:
            Ss = [kb.sb(f"p_S{i}", [128, 16, 128], F32, st) for i in range(2)]
            S2 = kb.sb("p_S2", [128, 128], F32, st)
            M16 = kb.sb("p_M16", [128, 16, 16], F32, st)
            I16u = kb.sb("p_I16u", [128, 16, 16], U32, st)
            I16f = kb.sb("p_I16f", [128, 16, 16], F32, st)
            cand = kb.sb("p_cand", [128, 8, 16, 16], F32, st)
            cand2 = kb.sb("p_cand2", [128, 256], F32, st)
            C16 = kb.sb("p_C16", [128, 8, 16], F32, st)
            CIu = kb.sb("p_CIu", [128, 8, 16], U32, st)
            IJu = kb.sb("p_IJu", [128, 2, 8, 16], U32, st)
            IJf = kb.sb("p_IJf", [128, 2, 8, 16], F32, st)
            ex = kb.sb("p_ex", [128, 8, 16], F32, st)
            Z = kb.sb("p_Z", [128, 8], F32, st)
            EG = kb.sb("p_EG", [128, 3, 8, 16], F32, st)
            eq = kb.sb("p_eq", [128, 8, 16, 16], F32, st)
            pt = kb.ps("p_pt", [128, 512], F32, st)
            for tt in range(NT):
                S = Ss[tt % 2]
                kb.dma("sp", S[:].rearrange("p a n -> p (a n)"), D.S_d.ap()[tt * 128:(tt + 1) * 128].rearrange("p a n -> p (a n)"),
                       reads=[D.S_d], writes=[S])
                for hp in range(16):
                    kb.op("dve", lambda e: e.max(out=M16[:, hp, 0:8], in_=S[:, hp, :]), reads=[S], writes=[M16])
                    kb.op("dve", lambda e: e.max_index(out=I16u[:, hp, 0:8], in_max=M16[:, hp, 0:8], in_values=S[:, hp, :]),
                          reads=[S, M16], writes=[I16u])
                    kb.op("dve", lambda e: e.match_replace(out=S2[:], in_to_replace=M16[:, hp, 0:8], in_values=S[:, hp, :], imm_value=-1e30),
                          reads=[S, M16], writes=[S2])
                    kb.op("dve", lambda e: e.max(out=M16[:, hp, 8:16], in_=S2[:]), reads=[S2], writes=[M16])
                    kb.op("dve", lambda e: e.max_index(out=I16u[:, hp, 8:16], in_max=M16[:, hp, 8:16], in_values=S2[:]),
                          reads=[S2, M16], writes=[I16u])
                kb.op("pool", lambda e: e.tensor_copy(out=I16f[:], in_=I16u[:]), reads=[I16u], writes=[I16f])
                M4 = M16[:].rearrange("p (h q) k -> p h q k", q=2)
                I4 = I16f[:].rearrange("p (h q) k -> p h q k", q=2)
                kb.op("pool", lambda e: e.tensor_tensor(out=cand[:], in0=M4[:, :, 0, :, None].to_broadcast([128, 8, 16, 16]),
                                                        in1=M4[:, :, 1, None, :].to_broadcast([128, 8, 16, 16]), op=ALU.add),
                      reads=[M16], writes=[cand])
                for h in range(8):
                    ch = cand[:, h, :, :].rearrange("p a b -> p (a b)")
                    kb.op("dve", lambda e: e.max(out=C16[:, h, 0:8], in_=ch), reads=[cand], writes=[C16])
                    kb.op("dve", lambda e: e.max_index(out=CIu[:, h, 0:8], in_max=C16[:, h, 0:8], in_values=ch), reads=[cand, C16], writes=[CIu])
                    kb.op("dve", lambda e: e.match_replace(out=cand2[:], in_to_replace=C16[:, h, 0:8], in_values=ch, imm_value=-1e30),
                          reads=[cand, C16], writes=[cand2])
                    kb.op("dve", lambda e: e.max(out=C16[:, h, 8:16], in_=cand2[:]), reads=[cand2], writes=[C16])
                    kb.op("dve", lambda e: e.max_index(out=CIu[:, h, 8:16], in_max=C16[:, h, 8:16], in_values=cand2[:]),
                          reads=[cand2, C16], writes=[CIu])
                kb.op("pool", lambda e: e.tensor_tensor(out=ex[:], in0=C16[:], in1=C16[:, :, 0:1].to_broadcast([128, 8, 16]), op=ALU.subtract),
                      reads=[C16], writes=[ex])
                kb.op("act", lambda e: e.activation(out=ex[:], in_=ex[:], func=AF.Exp), reads=[ex], writes=[ex])
                kb.op("dve", lambda e: e.tensor_reduce(out=Z[:], in_=ex[:], axis=AX.X, op=ALU.add), reads=[ex], writes=[Z])
                kb.op("dve", lambda e: e.reciprocal(out=Z[:], in_=Z[:]), reads=[Z], writes=[Z])
                kb.op("dve", lambda e: e.tensor_tensor(out=EG[:, 2], in0=ex[:], in1=Z[:, :, None].to_broadcast([128, 8, 16]), op=ALU.mult),
                      reads=[ex, Z], writes=[EG])
                kb.op("dve", lambda e: e.tensor_single_scalar(out=IJu[:, 0], in_=CIu[:], scalar=4, op=ALU.logical_shift_right),
                      reads=[CIu], writes=[IJu])
                kb.op("dve", lambda e: e.tensor_single_scalar(out=IJu[:, 1], in_=CIu[:], scalar=15, op=ALU.bitwise_and),
                      reads=[CIu], writes=[IJu])
                kb.op("pool", lambda e: e.tensor_copy(out=IJf[:], in_=IJu[:]), reads=[IJu], writes=[IJf])
                for q in range(2):
                    kb.op("dve", lambda e: e.tensor_tensor(out=eq[:], in0=iota[:, None, None, 0:16].to_broadcast([128, 8, 16, 16]),
                                                            in1=IJf[:, q, :, :, None].to_broadcast([128, 8, 16, 16]), op=ALU.is_equal),
                          reads=[iota, IJf], writes=[eq])
                    kb.op("pool", lambda e: e.tensor_tensor(out=eq[:], in0=eq[:], in1=I4[:, :, q, None, :].to_broadcast([128, 8, 16, 16]),
                                                            op=ALU.mult), reads=[eq, I16f], writes=[eq])
                    kb.op("dve", lambda e: e.tensor_reduce(out=EG[:, q], in_=eq[:], axis=AX.X, op=ALU.add), reads=[eq], writes=[EG])
                for a in range(3):
                    kb.op("pe", lambda e: e.transpose(out=pt[:, a * 128:(a + 1) * 128], in_=EG[:, a].rearrange("p h k -> p (h k)"),
                                                      identity=C.ident[:]), reads=[EG, C.ident], writes=[pt])
                kb.op("act", lambda e: e.copy(out=ET[:, :, tt * 128:(tt + 1) * 128], in_=pt[:, 0:384].rearrange("p (a t) -> p a t", a=3)),
                      reads=[pt], writes=[ET])
            kb.barrier()
        with contextlib.ExitStack() as st:
            Gs = kb.sb("p_Gs", [128, 128, 256], BF16, st)
            O1 = [kb.sb(f"p_O1{i}", [128, 32, 128], BF16, st) for i in range(2)]
            O2 = [kb.sb(f"p_O2{i}", [128, 32, 128], BF16, st) for i in range(2)]
            pg = [kb.ps(f"p_pg{i}", [128, 512], F32, st) for i in range(4)]
            it = 0
            for tg in range(8):
                for sub in range(8):
                    t0 = tg * 256 + sub * 32
                    o1 = O1[sub % 2]
                    o2 = O2[sub % 2]
                    iob = iota[:, None, :].to_broadcast([128, 32, 128])
                    kb.op("dve", lambda e: e.tensor_tensor(out=o1[:], in0=iob, in1=ET[:, 0, t0:t0 + 32, None].to_broadcast([128, 32, 128]),
                                                            op=ALU.is_equal), reads=[iota, ET], writes=[o1])
                    kb.op("dve", lambda e: e.tensor_tensor(out=o2[:], in0=iob, in1=ET[:, 1, t0:t0 + 32, None].to_broadcast([128, 32, 128]),
                                                           op=ALU.is_equal), reads=[iota, ET], writes=[o2])
                    kb.op("pool", lambda e: e.tensor_tensor(out=o2[:], in0=o2[:], in1=ET[:, 2, t0:t0 + 32, None].to_broadcast([128, 32, 128]),
                                                            op=ALU.mult), reads=[o2, ET], writes=[o2])
                    for q4 in range(8):
                        p = pg[it % 4]
                        it += 1
                        for j in range(4):
                            tl = q4 * 4 + j
                            kb.op("pe", lambda e: e.matmul(p[:, j * 128:(j + 1) * 128], lhsT=o2[:, tl, :], rhs=o1[:, tl, :], start=True, stop=True),
                                  reads=[o1, o2], writes=[p])
                        tl0 = sub * 32 + q4 * 4
                        dst = Gs[:, :, tl0:tl0 + 4].rearrange("p e t -> p t e")
                        src = p[:].rearrange("p (t e) -> p t e", t=4)
                        if it % 2 == 0:
                            kb.op("act", lambda e: e.copy(out=dst, in_=src), reads=[p], writes=[Gs])
                        else:
                            kb.op("dve", lambda e: e.tensor_copy(out=dst, in_=src), reads=[p], writes=[Gs])
                for k8 in range(8):
                    kb.dma(["sp", "act", "pool"][k8 % 3],
                           D.G_d.ap()[k8 * 16:(k8 + 1) * 16, :, tg * 256:(tg + 1) * 256].rearrange("e1 e2 t -> e2 e1 t"),
                           Gs[:, k8 * 16:(k8 + 1) * 16, :], reads=[Gs], writes=[D.G_d])
            kb.barrier()
        with contextlib.ExitStack() as st:
            h2T = kb.sb("p_h2Tb", [128, 16, 2048], BF16, st)
            for dc in range(16):
                kb.dma("sp" if dc % 2 == 0 else "act", h2T[:, dc, :], D.h2T_d.ap()[dc], reads=[D.h2T_d], writes=[h2T])
            stg = [kb.sb(f"p4_stg{i}", [128, 16, 128], F32, st) for i in range(2)]
            ub = [kb.sb(f"p4_ub{i}", [128, 16, 128], BF16, st) for i in range(2)]
            Gc = [kb.sb(f"p4_Gc{i}", [128, 2048], BF16, st) for i in range(2)]
            ge = [kb.sb(f"p4_ge{i}", [128, 2048], BF16, st) for i in range(2)]
            pss = [[kb.ps(f"p4_ps{i}_{j}", [128, 512], F32, st) for j in range(4)] for i in range(2)]
            u_v = D.u_tabT.ap().rearrange("(dc p) e -> p dc e", p=128)
            for e1 in range(128):
                sg = stg[e1 % 2]
                kb.dma("sp" if e1 % 2 == 0 else "act", sg[:], u_v[:, :, e1 * 128:(e1 + 1) * 128], writes=[sg])
                u = ub[e1 % 2]
                kb.op("pool", lambda e: e.tensor_copy(out=u[:], in_=sg[:]), reads=[sg], writes=[u])
                g = Gc[e1 % 2]
                kb.dma("pool", g[:], D.G_d.ap()[e1], reads=[D.G_d], writes=[g])
                ps = pss[e1 % 2]
                a = ge[e1 % 2]
                for tb in range(4):
                    for dc in range(16):
                        kb.op("pe", lambda e: e.matmul(ps[tb][:], lhsT=u[:, dc, :], rhs=h2T[:, dc, tb * 512:(tb + 1) * 512],
                                                       start=(dc == 0), stop=(dc == 15)), reads=[u, h2T], writes=[ps[tb]])
                    kb.op("act", lambda e: e.activation(out=a[:, tb * 512:(tb + 1) * 512], in_=ps[tb][:], func=AF.Gelu), reads=[ps[tb]], writes=[a])
                kb.op("dve", lambda e: e.tensor_tensor(out=a[:], in0=a[:], in1=g[:], op=ALU.mult), reads=[a, g], writes=[a])
                kb.dma("sp" if e1 % 2 == 1 else "act", D.A_d.ap()[e1], a[:], reads=[a], writes=[D.A_d])
            kb.barrier()
        with contextlib.ExitStack() as st:
            vst = [kb.sb(f"p5_vst{i}", [128, 512], F32, st) for i in range(3)]
            vb = [kb.sb(f"p5_vb{i}", [128, 512], BF16, st) for i in range(3)]
            ac = [kb.sb(f"p5_ac{i}", [128, 512], BF16, st) for i in range(3)]
            ov = [kb.sb(f"p5_ov{i}", [128, 512], F32, st) for i in range(2)]
            pss = [[kb.ps(f"p5_ps{i}_{j}", [128, 512], F32, st) for j in range(4)] for i in range(2)]
            it = 0
            for tb in range(4):
                for dg in range(4):
                    ps = pss[(tb * 4 + dg) % 2]
                    for e1 in range(128):
                        k = it % 3
                        it += 1
                        kb.dma("sp", vst[k][:], D.v_tab.ap()[e1 * 128:(e1 + 1) * 128, dg * 512:(dg + 1) * 512], writes=[vst[k]])
                        kb.dma("act", ac[k][:], D.A_d.ap()[e1][:, tb * 512:(tb + 1) * 512], reads=[D.A_d], writes=[ac[k]])
                        kb.op("pool", lambda e: e.tensor_copy(out=vb[k][:], in_=vst[k][:]), reads=[vst[k]], writes=[vb[k]])
                        for j in range(4):
                            kb.op("pe", lambda e: e.matmul(ps[j][:], lhsT=vb[k][:, j * 128:(j + 1) * 128], rhs=ac[k][:],
                                                           start=(e1 == 0), stop=(e1 == 127)), reads=[vb[k], ac[k]], writes=[ps[j]])
                    for j in range(4):
                        o = ov[j % 2]
                        if j % 2 == 0:
                            kb.op("act", lambda e: e.copy(out=o[:], in_=ps[j][:]), reads=[ps[j]], writes=[o])
                        else:
                            kb.op("dve", lambda e: e.tensor_copy(out=o[:], in_=ps[j][:]), reads=[ps[j]], writes=[o])
                        d0 = dg * 512 + j * 128
                        kb.dma("pool", D.peT_d.ap()[d0:d0 + 128, tb * 512:(tb + 1) * 512], o[:], reads=[o], writes=[D.peT_d])
            kb.barrier()


def phase_F(kb, C, D):
    with contextlib.ExitStack() as st:
        wb = kb.sb("f_wb", [128, 2048], F32, st)
        kb.dma("pool", wb[:], D.normf_w.ap().partition_broadcast(128), writes=[wb])
        xs = [kb.sb(f"f_x{i}", [128, 2048], F32, st) for i in range(2)]
        pes = [kb.sb(f"f_pe{i}", [128, 16, 128], F32, st) for i in range(2)]
        junk = kb.sb("f_junk", [128, 2048], BF16, st)
        ss = kb.sb("f_ss", [128, NT], F32, st)
        rs = kb.sb("f_rs", [128, NT], F32, st)
        kb.op("pool", lambda e: e.memset(ss[:], 0.0), writes=[ss])
        pss = [[kb.ps(f"f_ps{i}_{j}", [128, 512], F32, st) for j in range(4)] for i in range(2)]
        pe_v = D.peT_d.ap().rearrange("(dc p) t -> p dc t", p=128)
        for tt in range(NT):
            xt = xs[tt % 2]
            pe = pes[tt % 2]
            ps = pss[tt % 2]
            kb.dma("sp", xt[:], D.x1_d.ap()[tt * 128:(tt + 1) * 128, :], reads=[D.x1_d], writes=[xt])
            kb.dma("act", pe[:], pe_v[:, :, tt * 128:(tt + 1) * 128], reads=[D.peT_d], writes=[pe])
            for dc in range(16):
                kb.op("pe", lambda e: e.transpose(out=ps[dc // 4][:, (dc % 4) * 128:(dc % 4 + 1) * 128], in_=pe[:, dc, :], identity=C.ident[:]),
                      reads=[pe, C.ident], writes=[ps[dc // 4]])
            for j in range(4):
                kb.op("dve", lambda e: e.tensor_tensor(out=xt[:, j * 512:(j + 1) * 512], in0=ps[j][:], in1=xt[:, j * 512:(j + 1) * 512], op=ALU.add),
                      reads=[ps[j], xt], writes=[xt])
            kb.op("act", lambda e: e.activation(out=junk[:], in_=xt[:], func=AF.Square, accum_out=ss[:, tt:tt + 1]), reads=[xt], writes=[junk, ss])
            kb.op("dve", lambda e: e.tensor_scalar(out=rs[:, tt:tt + 1], in0=ss[:, tt:tt + 1], scalar1=1.0 / 2048, scalar2=EPS,
                                                   op0=ALU.mult, op1=ALU.add), reads=[ss], writes=[rs])
            kb.op("act", lambda e: e.activation(out=rs[:, tt:tt + 1], in_=rs[:, tt:tt + 1], func=AF.Sqrt), reads=[rs], writes=[rs])
            kb.op("dve", lambda e: e.reciprocal(out=rs[:, tt:tt + 1], in_=rs[:, tt:tt + 1]), reads=[rs], writes=[rs])
            kb.op("dve", lambda e: e.scalar_tensor_tensor(out=xt[:], in0=xt[:], scalar=rs[:, tt:tt + 1], in1=wb[:], op0=ALU.mult, op1=ALU.mult),
                  reads=[xt, rs, wb], writes=[xt])
            kb.dma("sp", D.out.ap()[tt * 128:(tt + 1) * 128, :], xt[:], reads=[xt], writes=[D.out])
        kb.barrier()


STOP = 0
class StopBuild(Exception):
    pass
def chk(n):
    if STOP == n:
        raise StopBuild()
GN_EPS = 64e-5
NEG_E05 = -0.6065306597126334


def phase_Rpre(kb, C, D):
    with contextlib.ExitStack() as st:
        rw = kb.sb("rp_rw", [128, 8, 8], F32, st)
        kb.dma("sp", rw[:], D.rw_c.ap(), writes=[rw])
        tmp = kb.sb("rp_tmp", [128, 2048], F32, st)
        lin = [kb.sb(f"rp_lin{i}", [128, 2048], BF16, st) for i in range(4)]
        for i in range(4):
            kb.op("pool", lambda e: e.memset(lin[i][:], 0.0), writes=[lin[i]])
            r0 = 3072 + i * 96
            kb.dma("sp", tmp[0:96, :], D.zs_d.ap()[r0:r0 + 96, :], reads=[D.zs_d], writes=[tmp])
            if i < 2:
                kb.op("act", lambda e: e.activation(out=lin[i][0:96, :], in_=tmp[0:96, :], func=AF.Tanh), reads=[tmp], writes=[lin[i]])
            else:
                kb.op("act", lambda e: e.copy(out=lin[i][0:96, :], in_=tmp[0:96, :]), reads=[tmp], writes=[lin[i]])
        sgl = kb.sb("rp_sgl", [128, 2, 2048], BF16, st)
        for kc in range(2):
            kb.dma("sp", tmp[:], D.zs_d.ap()[3456 + kc * 128:3456 + (kc + 1) * 128, :], reads=[D.zs_d], writes=[tmp])
            kb.op("act", lambda e: e.activation(out=sgl[:, kc, :], in_=tmp[:], func=AF.Sigmoid), reads=[tmp], writes=[sgl])
        wst = kb.sb("rp_wst", [128, 2048], F32, st)
        w2b = kb.sb("rp_w2b", [128, 2, 1024], BF16, st)
        a2b = kb.sb("rp_a2b", [128, 2, 1024], BF16, st)
        g2b = kb.sb("rp_g2b", [128, 2, 1024], BF16, st)
        wv = wst[:].rearrange("p (a c) -> p a c", a=2)
        kb.op("pool", lambda e: e.memset(w2b[:], 0.0), writes=[w2b])
        kb.op("pool", lambda e: e.memset(a2b[:], 0.0), writes=[a2b])
        kb.dma("sp", wv[0:96], D.w2.ap().rearrange("d l c -> l d c"), writes=[wst])
        kb.op("pool", lambda e: e.tensor_copy(out=w2b[0:96], in_=wv[0:96]), reads=[wst], writes=[w2b])
        kb.dma("sp", wv[0:96], D.a2.ap().rearrange("d l c -> l d c"), writes=[wst])
        kb.op("pool", lambda e: e.tensor_copy(out=a2b[0:96], in_=wv[0:96]), reads=[wst], writes=[a2b])
        kb.dma("sp", wv, D.g2.ap().rearrange("(kc p) c -> p kc c", p=128), writes=[wst])
        kb.op("pool", lambda e: e.tensor_copy(out=g2b[:], in_=wv), reads=[wst], writes=[g2b])
        outs = [kb.sb(f"rp_o{i}", [128, 2048], F32, st) for i in range(2)]
        pss = [[kb.ps(f"rp_ps{i}_{j}", [128, 512], F32, st) for j in range(4)] for i in range(2)]
        it = 0
        for cc in range(8):
            for d in range(2):
                for which in range(2):
                    ps = pss[it % 2]
                    o = outs[it % 2]
                    it += 1
                    wmat = w2b if which == 0 else a2b
                    xin = lin[d] if which == 0 else lin[2 + d]
                    bias = rw[:, cc, d:d + 1] if which == 0 else rw[:, cc, 2 + d:3 + d]
                    for tb in range(4):
                        kb.op("pe", lambda e: e.matmul(ps[tb][:], lhsT=wmat[:, d, cc * 128:(cc + 1) * 128], rhs=xin[:, tb * 512:(tb + 1) * 512],
                                                       start=True, stop=True), reads=[wmat, xin], writes=[ps[tb]])
                        kb.op("act", lambda e: e.activation(out=o[:, tb * 512:(tb + 1) * 512], in_=ps[tb][:], func=AF.Sigmoid, bias=bias),
                              reads=[ps[tb], rw], writes=[o])
                    if which == 0:
                        kb.op("pool", lambda e: e.tensor_scalar(out=o[:], in0=o[:], scalar1=NEG_E05, scalar2=None, op0=ALU.mult), reads=[o], writes=[o])
                        kb.dma("sp", D.ld_d.ap()[d, cc * 128:(cc + 1) * 128, :], o[:], reads=[o], writes=[D.ld_d])
                    else:
                        kb.dma("sp", D.a_d.ap()[d, cc * 128:(cc + 1) * 128, :], o[:], reads=[o], writes=[D.a_d])
        for tt in range(NT):
            ps = pss[tt % 2]
            o = outs[tt % 2]
            for hf in range(2):
                for kc in range(2):
                    kb.op("pe", lambda e: e.matmul(ps[hf][:], lhsT=sgl[:, kc, tt * 128:(tt + 1) * 128], rhs=g2b[:, kc, hf * 512:(hf + 1) * 512],
                                                   start=(kc == 0), stop=(kc == 1)), reads=[sgl, g2b], writes=[ps[hf]])
                kb.op("act", lambda e: e.copy(out=o[:, hf * 512:(hf + 1) * 512], in_=ps[hf][:]), reads=[ps[hf]], writes=[o])
            kb.dma("sp", D.g_d.ap()[tt * 128:(tt + 1) * 128, :], o[:, 0:1024], reads=[o], writes=[D.g_d])
        kb.barrier()


def phase_R(kb, C, D, ccs=range(8), dbg=None):
    with contextlib.ExitStack() as st:
        rw = kb.sb("r_rw", [128, 8, 8], F32, st)
        kb.dma("sp", rw[:], D.rw_c.ap(), writes=[rw])
        masks = kb.sb("r_masks", [128, 6, 128], F32, st)
        kb.dma("sp", masks[:], D.masks.ap(), writes=[masks])
        cst = kb.sb("r_cst", [128, 66 + 2048], F32, st)
        kb.dma("sp", cst[:], D.rcst.ap(), writes=[cst])
        ident2 = cst[:, 0:64]
        sel = cst[:, 64:66]
        segm = cst[:, 66:66 + 2048]
        lnw = kb.sb("r_lnw", [128, 1024], F32, st)
        lnb = kb.sb("r_lnb", [128, 1024], F32, st)
        kb.dma("pool", lnw[:], D.lnx_w.ap().partition_broadcast(128), writes=[lnw])
        kb.dma("pool", lnb[:], D.lnx_b.ap().partition_broadcast(128), writes=[lnb])
        Rr = kb.sb("r_R", [128, 2048], F32, st)
        Kk = kb.sb("r_K", [128, 2048], F32, st)
        Vv = kb.sb("r_V", [128, 2048], F32, st)
        KKn = kb.sb("r_KK", [128, 2048], F32, st)
        KS = kb.sb("r_KS", [128, 2048], F32, st)
        E1 = kb.sb("r_E1", [128, 2048], F32, st)
        XI = kb.sb("r_XI", [128, 2048], F32, st)
        XE = kb.sb("r_XE", [128, 2048], F32, st)
        Aa = kb.sb("r_A", [128, 2048], F32, st)
        KD = kb.sb("r_KD", [128, 2048], F32, st)
        KT = kb.sb("r_KT", [128, 2048], F32, st)
        AT = kb.sb("r_AT", [128, 2048], F32, st)
        Vtm = kb.sb("r_Vtm", [128, NT, 128], F32, st)
        Ysum = kb.sb("r_Ysum", [128, NT, 128], F32, st)
        MTa = kb.sb("r_MTa", [128, 32, 64], F32, st)
        Ca = kb.sb("r_Ca", [128, 32, 64], F32, st)
        H = [kb.sb(f"r_H{i}", [128, 64], F32, st) for i in range(2)]
        tot = kb.sb("r_tot", [128, 32], F32, st)
        GL = kb.sb("r_GL", [128, 32], F32, st)
        XA = kb.sb("r_XA", [128, 2, 2, 128], F32, st)
        KBm = kb.sb("r_KBm", [128, 2, 2, 128], F32, st)
        PQ = [kb.sb(f"r_PQ{i}", [128, 2, 2, 128], F32, st) for i in range(2)]
        QT = [kb.sb(f"r_QT{i}", [128, 2, 128], F32, st) for i in range(2)]
        BW = kb.sb("r_BW", [128, 2, 128], F32, st)
        BU = kb.sb("r_BU", [128, 2, 128], F32, st)
        AGtm = kb.sb("r_AGtm", [128, 128], F32, st)
        KGtm = kb.sb("r_KGtm", [128, 128], F32, st)
        psA = kb.ps("r_psA", [128, 2, 2, 128], F32, st)
        psB = kb.ps("r_psB", [128, 2, 2, 128], F32, st)
        psC = kb.ps("r_psC", [128, 2, 128], F32, st)
        psN = kb.ps("r_psN", [128, 2, 2, 128], F32, st)
        b5 = kb.ps("r_b5", [128, 512], F32, st)
        b6 = kb.ps("r_b6", [128, 512], F32, st)
        b7 = kb.ps("r_b7", [128, 512], F32, st)
        b8 = kb.ps("r_b8", [128, 512], F32, st)
        psW = kb.view(b5, "psW"); psBt = kb.view(b5, "psBt"); psU = kb.view(b5, "psU")
        psY = kb.view(b6, "psY"); psR = kb.view(b6, "psR"); psAG = kb.view(b6, "psAG"); psKG = kb.view(b6, "psKG")
        psM = kb.view(b7, "psM"); psCc = kb.view(b7, "psCc"); psYs = kb.view(b7, "psYs"); psH = kb.view(b7, "psH")
        psX = b8
        W_ = lambda: psW[:, 0:128].rearrange("p (h i) -> p h i", h=2)
        Bt_ = lambda: psBt[:, 128:256]
        U_ = lambda: psU[:, 256:512].rearrange("p (h i) -> p h i", h=2)
        Y_ = lambda: psY[:, 0:128].rearrange("p (h i) -> p h i", h=2)
        R_ = lambda: psR[:, 128:256]
        AG_ = lambda: psAG[:, 256:384]
        KG_ = lambda: psKG[:, 384:512]
        M_ = lambda: psM[:, 0:128].rearrange("p (n j) -> p n j", n=2)
        Cc_ = lambda: psCc[:, 128:256].rearrange("p (n j) -> p n j", n=2)
        Ys_ = lambda: psYs[:, 256:384]
        H_ = lambda: psH[:, 384:448]

        def vt(eng_i, fn, reads, writes):
            kb.op("dve" if eng_i == 0 else "pool", fn, reads=reads, writes=writes)

        for cc in ccs:
            rows = slice(cc * 128, (cc + 1) * 128)
            kb.dma("sp", Rr[:], D.zs_d.ap()[cc * 128:(cc + 1) * 128, :], reads=[D.zs_d], writes=[Rr])
            kb.dma("act", Kk[:], D.zs_d.ap()[1024 + cc * 128:1024 + (cc + 1) * 128, :], reads=[D.zs_d], writes=[Kk])
            kb.dma("sp", Vv[:], D.zs_d.ap()[2048 + cc * 128:2048 + (cc + 1) * 128, :], reads=[D.zs_d], writes=[Vv])
            for g in range(4):
                for j in range(4):
                    tt = g * 4 + j
                    kb.op("pe", lambda e: e.transpose(out=psX[:, j * 128:(j + 1) * 128], in_=Vv[:, tt * 128:(tt + 1) * 128], identity=C.ident[:]),
                          reads=[Vv, C.ident], writes=[psX])
                kb.op("act", lambda e: e.copy(out=Vtm[:, g * 4:(g + 1) * 4, :], in_=psX[:].rearrange("p (j c) -> p j c", j=4)), reads=[psX], writes=[Vtm])
            kb.op("pool", lambda e: e.tensor_scalar(out=KKn[:], in0=Kk[:], scalar1=rw[:, cc, 4:5], scalar2=None, op0=ALU.mult),
                  reads=[Kk, rw], writes=[KKn])
            kb.op("act", lambda e: e.activation(out=XI[:], in_=KKn[:], func=AF.Square), reads=[KKn], writes=[XI])
            for tb in range(4):
                kb.op("pe", lambda e: e.matmul(psX[:], lhsT=masks[:, 5, :], rhs=XI[:, tb * 512:(tb + 1) * 512], start=True, stop=True),
                      reads=[masks, XI], writes=[psX])
                kb.op("act", lambda e: e.activation(out=XE[:, tb * 512:(tb + 1) * 512], in_=psX[:], func=AF.Sqrt), reads=[psX], writes=[XE])
            kb.op("dve", lambda e: e.tensor_scalar(out=XE[:], in0=XE[:], scalar1=1e-12, scalar2=None, op0=ALU.max), reads=[XE], writes=[XE])
            kb.op("dve", lambda e: e.reciprocal(out=XE[:], in_=XE[:]), reads=[XE], writes=[XE])
            kb.op("pool", lambda e: e.tensor_tensor(out=KKn[:], in0=KKn[:], in1=XE[:], op=ALU.mult), reads=[KKn, XE], writes=[KKn])
            chk(1)
            for d in range(2):
                M2 = masks[:, 2 * d:2 * d + 2, :]
                MST = masks[:, 2 - 2 * d, :]
                kb.dma("sp", XE[:], D.ld_d.ap()[d, cc * 128:(cc + 1) * 128, :], reads=[D.ld_d], writes=[XE])
                kb.dma("act", Aa[:], D.a_d.ap()[d, cc * 128:(cc + 1) * 128, :], reads=[D.a_d], writes=[Aa])
                kb.op("dve", lambda e: e.tensor_scalar(out=KD[:], in0=Aa[:], scalar1=-1.0, scalar2=rw[:, cc, 5:6], op0=ALU.add, op1=ALU.mult),
                      reads=[Aa, rw], writes=[KD])
                kb.op("dve", lambda e: e.scalar_tensor_tensor(out=KD[:], in0=KD[:], scalar=1.0, in1=Kk[:], op0=ALU.add, op1=ALU.mult),
                      reads=[KD, Kk], writes=[KD])
                if d == 0:
                    kb.op("pool", lambda e: e.tensor_copy(out=KS[:], in_=KD[:]), reads=[KD], writes=[KS])
                else:
                    kb.op("pool", lambda e: e.tensor_tensor(out=KS[:], in0=KS[:], in1=KD[:], op=ALU.add), reads=[KS, KD], writes=[KS])
                kb.op("pool", lambda e: e.tensor_tensor(out=Aa[:], in0=Aa[:], in1=KKn[:], op=ALU.mult), reads=[Aa, KKn], writes=[Aa])
                kb.op("dve", lambda e: e.tensor_tensor_scan(out=XI[:], data0=segm, data1=XE[:], initial=0.0, op0=ALU.mult, op1=ALU.add),
                      reads=[cst, XE], writes=[XI])
                kb.op("pool", lambda e: e.tensor_copy(out=tot[:], in_=XI[:].rearrange("p (n s) -> p n s", s=64)[:, :, 63]), reads=[XI], writes=[tot])
                kb.op("act", lambda e: e.activation(out=GL[:], in_=tot[:], func=AF.Exp), reads=[tot], writes=[GL])
                if d == 0:
                    kb.op("pool", lambda e: e.tensor_tensor(out=XE[:], in0=XI[:], in1=XE[:], op=ALU.subtract), reads=[XI, XE], writes=[XE])
                else:
                    kb.op("dve", lambda e: e.tensor_tensor(out=XI[:].rearrange("p (n s) -> p n s", s=64),
                                                           in0=tot[:, :, None].to_broadcast([128, 32, 64]),
                                                           in1=XI[:].rearrange("p (n s) -> p n s", s=64), op=ALU.subtract),
                          reads=[tot, XI], writes=[XI])
                    kb.op("pool", lambda e: e.tensor_tensor(out=XE[:], in0=XI[:], in1=XE[:], op=ALU.add), reads=[XI, XE], writes=[XE])
                cI, cE = (XI, XE) if d == 0 else (XE, XI)
                kb.op("act", lambda e: e.activation(out=E1[:], in_=cI[:], func=AF.Exp), reads=[cI], writes=[E1])
                kb.op("act", lambda e: e.activation(out=cI[:], in_=cI[:], func=AF.Exp, scale=-1.0), reads=[cI], writes=[cI])
                kb.op("act", lambda e: e.activation(out=cE[:], in_=cE[:], func=AF.Exp), reads=[cE], writes=[cE])
                kb.op("dve", lambda e: e.tensor_tensor(out=E1[:], in0=E1[:], in1=Rr[:], op=ALU.mult), reads=[E1, Rr], writes=[E1])
                kb.op("pool", lambda e: e.tensor_tensor(out=KT[:], in0=KD[:], in1=cI[:], op=ALU.mult), reads=[KD, cI], writes=[KT])
                kb.op("dve", lambda e: e.tensor_tensor(out=AT[:], in0=Aa[:], in1=cI[:], op=ALU.mult), reads=[Aa, cI], writes=[AT])
                kb.op("dve", lambda e: e.scalar_tensor_tensor(out=cE[:], in0=cE[:], scalar=-1.0, in1=KKn[:], op0=ALU.mult, op1=ALU.mult),
                      reads=[cE, KKn], writes=[cE])
                kb.op("pool", lambda e: e.tensor_tensor(out=cI[:].rearrange("p (n s) -> p n s", s=64), in0=cI[:].rearrange("p (n s) -> p n s", s=64),
                                                        in1=GL[:, :, None].to_broadcast([128, 32, 64]), op=ALU.mult), reads=[cI, GL], writes=[cI])
                kb.op("dve", lambda e: e.tensor_tensor(out=KD[:], in0=KD[:], in1=cI[:], op=ALU.mult), reads=[KD, cI], writes=[KD])
                kb.op("pool", lambda e: e.tensor_tensor(out=Aa[:], in0=Aa[:], in1=cI[:], op=ALU.mult), reads=[Aa, cI], writes=[Aa])
                RT, BT = E1, cE
                chk(2)
                for tt in range(NT):
                    cols = slice(tt * 128, (tt + 1) * 128)
                    for hh in range(2):
                        pr = slice(hh * 64, hh * 64 + 64)
                        kb.op("pe", lambda e: e.matmul(psA[:, hh, 0, :], lhsT=AT[pr, cols], rhs=BT[pr, cols], start=True, stop=True),
                              reads=[AT, BT], writes=[psA])
                        kb.op("pe", lambda e: e.matmul(psA[:, hh, 1, :], lhsT=AT[pr, cols], rhs=RT[pr, cols], start=True, stop=True),
                              reads=[AT, RT], writes=[psA])
                        kb.op("pe", lambda e: e.matmul(psB[:, hh, 0, :], lhsT=KT[pr, cols], rhs=BT[pr, cols], start=True, stop=True),
                              reads=[KT, BT], writes=[psB])
                        kb.op("pe", lambda e: e.matmul(psB[:, hh, 1, :], lhsT=KT[pr, cols], rhs=RT[pr, cols], start=True, stop=True),
                              reads=[KT, RT], writes=[psB])
                        kb.op("pe", lambda e: e.matmul(psC[:, hh, :], lhsT=BT[pr, cols], rhs=AT[pr, cols], start=True, stop=True),
                              reads=[AT, BT], writes=[psC])
                    M2b = M2[:, None, :, :].to_broadcast([128, 2, 2, 128])
                    kb.op("dve", lambda e: e.tensor_tensor(out=XA[:], in0=psA[:], in1=M2b, op=ALU.mult), reads=[psA, masks], writes=[XA])
                    kb.op("dve", lambda e: e.tensor_tensor(out=KBm[:], in0=psB[:], in1=M2b, op=ALU.mult), reads=[psB, masks], writes=[KBm])
                    q0, q1 = QT[0], QT[1]
                    kb.op("dve", lambda e: e.tensor_tensor(out=q0[:], in0=psC[:], in1=MST[:, None, :].to_broadcast([128, 2, 128]), op=ALU.mult),
                          reads=[psC, masks], writes=[q0])
                    chk(3)
                    pq = PQ[0]
                    kb.op("pool", lambda e: e.tensor_tensor(out=pq[:, :, 0, :], in0=XA[:, :, 0, :], in1=C.ident[:, None, :].to_broadcast([128, 2, 128]),
                                                            op=ALU.add), reads=[XA, C.ident], writes=[pq])
                    for hh in range(2):
                        kb.op("pe", lambda e: e.matmul(psN[:, hh, 1, :], lhsT=q0[:, hh, :], rhs=XA[:, hh, 0, :], start=True, stop=True),
                              reads=[q0, XA], writes=[psN])
                        kb.op("pe", lambda e: e.matmul(psC[:, hh, :], lhsT=XA[:, hh, 0, :], rhs=q0[:, hh, :], start=True, stop=True),
                              reads=[q0, XA], writes=[psC])
                    kb.op("act", lambda e: e.copy(out=pq[:, :, 1, :], in_=psN[:, :, 1, :]), reads=[psN], writes=[pq])
                    kb.op("act", lambda e: e.copy(out=q1[:], in_=psC[:]), reads=[psC], writes=[q1])
                    cur = 0
                    qcur = 1
                    for lev in range(1, 6):
                        pq = PQ[cur]
                        pqn = PQ[1 - cur]
                        qt = QT[qcur]
                        qtn = QT[1 - qcur]
                        last = (lev == 5)
                        for hh in range(2):
                            if last:
                                kb.op("pe", lambda e: e.matmul(psN[:, hh, 0, :], lhsT=qt[:, hh, :], rhs=pq[:, hh, 0, :], start=True, stop=True),
                                      reads=[qt, pq], writes=[psN])
                            else:
                                kb.op("pe", lambda e: e.matmul(psN[:, hh, :, :], lhsT=qt[:, hh, :], rhs=pq[:, hh, :, :], start=True, stop=True),
                                      reads=[qt, pq], writes=[psN])
                                kb.op("pe", lambda e: e.matmul(psC[:, hh, :], lhsT=pq[:, hh, 1, :], rhs=qt[:, hh, :], start=True, stop=True),
                                      reads=[qt, pq], writes=[psC])
                        kb.op("dve", lambda e: e.tensor_tensor(out=pqn[:, :, 0, :], in0=psN[:, :, 0, :], in1=pq[:, :, 0, :], op=ALU.add),
                              reads=[psN, pq], writes=[pqn])
                        if not last:
                            kb.op("act", lambda e: e.copy(out=pqn[:, :, 1, :], in_=psN[:, :, 1, :]), reads=[psN], writes=[pqn])
                            kb.op("act", lambda e: e.copy(out=qtn[:], in_=psC[:]), reads=[psC], writes=[qtn])
                        cur = 1 - cur
                        qcur = 1 - qcur
                    TT = PQ[cur]
                    chk(4)
                    for hh in range(2):
                        kb.op("pe", lambda e: e.matmul(W_()[:, hh, :], lhsT=KBm[:, hh, 0, :], rhs=Vtm[:, tt, hh * 64:(hh + 1) * 64], start=True, stop=True),
                              reads=[KBm, Vtm], writes=[psW])
                    kb.op("pe", lambda e: e.transpose(out=Bt_(), in_=BT[:, cols], identity=C.ident[:]), reads=[BT, C.ident], writes=[psBt])
                    kb.op("act", lambda e: e.copy(out=BW[:, :, 64:128], in_=W_()), reads=[psW], writes=[BW])
                    kb.op("dve", lambda e: e.tensor_copy(out=BW[:, :, 0:64], in_=Bt_().rearrange("p (h j) -> p h j", h=2)), reads=[psBt], writes=[BW])
                    for hh in range(2):
                        kb.op("pe", lambda e: e.matmul(U_()[:, hh, :], lhsT=TT[:, hh, 0, :], rhs=BW[:, hh, :], start=True, stop=True),
                              reads=[TT, BW], writes=[psU])
                    kb.op("act", lambda e: e.copy(out=BU[:], in_=U_()), reads=[psU], writes=[BU])
                    chk(5)
                    for hh in range(2):
                        kb.op("pe", lambda e: e.matmul(Y_()[:, hh, :], lhsT=XA[:, hh, 1, :], rhs=BU[:, hh, 64:128], start=True, stop=False),
                              reads=[XA, BU], writes=[psY])
                        kb.op("pe", lambda e: e.matmul(Y_()[:, hh, :], lhsT=KBm[:, hh, 1, :], rhs=Vtm[:, tt, hh * 64:(hh + 1) * 64], start=False, stop=True),
                              reads=[KBm, Vtm], writes=[psY])
                    if d == 0:
                        kb.op("act", lambda e: e.copy(out=Ysum[:, tt, :], in_=psY[:, 0:128]), reads=[psY], writes=[Ysum])
                    else:
                        kb.op("dve", lambda e: e.tensor_tensor(out=Ysum[:, tt, :], in0=psY[:, 0:128], in1=Ysum[:, tt, :], op=ALU.add),
                              reads=[psY, Ysum], writes=[Ysum])
                    for hh in range(2):
                        kb.op("pe", lambda e: e.matmul(R_()[hh * 64:(hh + 1) * 64, :], lhsT=BU[:, hh, 0:64], rhs=XA[:, hh, 1, :], start=True, stop=True),
                              reads=[BU, XA], writes=[psR])
                    kb.op("dve", lambda e: e.tensor_tensor(out=RT[:, cols], in0=R_(), in1=RT[:, cols], op=ALU.add), reads=[psR, RT], writes=[RT])
                    chk(6)
                    kb.op("pe", lambda e: e.transpose(out=AG_(), in_=Aa[:, cols], identity=C.ident[:]), reads=[Aa, C.ident], writes=[psAG])
                    kb.op("pe", lambda e: e.transpose(out=KG_(), in_=KD[:, cols], identity=C.ident[:]), reads=[KD, C.ident], writes=[psKG])
                    kb.op("act", lambda e: e.copy(out=AGtm[:], in_=AG_()), reads=[psAG], writes=[AGtm])
                    kb.op("act", lambda e: e.copy(out=KGtm[:], in_=KG_()), reads=[psKG], writes=[KGtm])
                    for n in range(2):
                        tr = slice(n * 64, n * 64 + 64)
                        bk = (b7, b8)[n]
                        for hh in range(2):
                            pr = slice(hh * 64, hh * 64 + 64)
                            kb.op("pe", lambda e: e.matmul(bk[pr, 0:64], lhsT=BU[tr, hh, 0:64], rhs=AGtm[tr, pr], start=True, stop=True),
                                  reads=[BU, AGtm], writes=[bk])
                            kb.op("pe", lambda e: e.matmul(bk[pr, 64:128], lhsT=AGtm[tr, pr], rhs=BU[tr, hh, 64:128], start=True, stop=False),
                                  reads=[BU, AGtm], writes=[bk])
                            kb.op("pe", lambda e: e.matmul(bk[pr, 64:128], lhsT=KGtm[tr, pr], rhs=Vtm[tr, tt, pr], start=False, stop=True),
                                  reads=[KGtm, Vtm], writes=[bk])
                    for n in range(2):
                        ch = tt * 2 + n
                        bk = (b7, b8)[n]
                        kb.op("dve", lambda e: e.scalar_tensor_tensor(out=MTa[:, ch, :], in0=ident2, scalar=GL[:, ch:ch + 1], in1=bk[:, 0:64],
                                                                      op0=ALU.mult, op1=ALU.add), reads=[cst, GL, bk], writes=[MTa])
                        kb.op("act", lambda e: e.copy(out=Ca[:, ch, :], in_=bk[:, 64:128]), reads=[bk], writes=[Ca])
                    chk(7)
                chk(8)
                kb.op("pool", lambda e: e.memset(H[0][:], 0.0), writes=[H[0]])
                order = range(32) if d == 0 else range(31, -1, -1)
                hc = 0
                for ch in order:
                    tt, n = ch // 2, ch % 2
                    ccols = slice(ch * 64, ch * 64 + 64)
                    Hc, Hn = H[hc], H[1 - hc]
                    tr = slice(n * 64, n * 64 + 64)
                    for hh in range(2):
                        pr = slice(hh * 64, hh * 64 + 64)
                        bk = (b7, b8)[hh]
                        kb.op("pe", lambda e: e.matmul(bk[tr, 0:64], lhsT=RT[pr, ccols], rhs=Hc[pr, :], start=True, stop=True),
                              reads=[RT, Hc], writes=[bk])
                        kb.op("pe", lambda e: e.matmul(bk[pr, 64:128], lhsT=MTa[pr, ch, :], rhs=Hc[pr, :], start=True, stop=True),
                              reads=[MTa, Hc], writes=[bk])
                    for hh in range(2):
                        pr = slice(hh * 64, hh * 64 + 64)
                        bk = (b7, b8)[hh]
                        kb.op("dve", lambda e: e.tensor_tensor(out=Hn[pr, :], in0=bk[pr, 64:128], in1=Ca[pr, ch, :], op=ALU.add),
                              reads=[bk, Ca], writes=[Hn])
                        kb.op("dve", lambda e: e.tensor_tensor(out=Ysum[tr, tt, pr], in0=bk[tr, 0:64], in1=Ysum[tr, tt, pr], op=ALU.add),
                              reads=[bk, Ysum], writes=[Ysum])
                    hc = 1 - hc
                chk(9)
            chk(10)
            if dbg is not None and "Ysum" in dbg:
                kb.dma("sp", D.dbgY.ap()[:, cc * 128:(cc + 1) * 128].rearrange("(tt p) c -> p tt c", p=128), Ysum[:], reads=[Ysum], writes=[D.dbgY])
            Y3 = Ysum[:].rearrange("p t (h i) -> p (t h) i", h=2)
            st_mu = XA[:].rearrange("p a b c -> p (a b c)")[:, 0:32]
            st_var = XA[:].rearrange("p a b c -> p (a b c)")[:, 32:64]
            kb.op("dve", lambda e: e.tensor_reduce(out=st_mu, in_=Y3, axis=AX.X, op=ALU.add), reads=[Ysum], writes=[XA])
            kb.op("dve", lambda e: e.tensor_scalar(out=st_mu, in0=st_mu, scalar1=1.0 / 64, scalar2=None, op0=ALU.mult), reads=[XA], writes=[XA])
            kb.op("dve", lambda e: e.tensor_tensor(out=Y3, in0=Y3, in1=st_mu[:, :, None].to_broadcast([128, 32, 64]), op=ALU.subtract),
                  reads=[Ysum, XA], writes=[Ysum])
            sqv = XI[:].rearrange("p (a i) -> p a i", i=64)
            kb.op("act", lambda e: e.activation(out=sqv, in_=Y3, func=AF.Square), reads=[Ysum], writes=[XI])
            kb.op("dve", lambda e: e.tensor_reduce(out=st_var, in_=sqv, axis=AX.X, op=ALU.add), reads=[XI], writes=[XA])
            kb.op("dve", lambda e: e.tensor_scalar(out=st_var, in0=st_var, scalar1=1.0 / 64, scalar2=GN_EPS, op0=ALU.mult, op1=ALU.add),
                  reads=[XA], writes=[XA])
            kb.op("act", lambda e: e.activation(out=st_var, in_=st_var, func=AF.Sqrt), reads=[XA], writes=[XA])
            kb.op("dve", lambda e: e.reciprocal(out=st_var, in_=st_var), reads=[XA], writes=[XA])
            kb.op("dve", lambda e: e.tensor_tensor(out=Y3, in0=Y3, in1=st_var[:, :, None].to_broadcast([128, 32, 64]), op=ALU.mult),
                  reads=[Ysum, XA], writes=[Ysum])
            kb.op("pool", lambda e: e.tensor_tensor(out=Ysum[:], in0=Ysum[:], in1=lnw[:, None, rows].to_broadcast([128, NT, 128]), op=ALU.mult),
                  reads=[Ysum, lnw], writes=[Ysum])
            kb.op("pool", lambda e: e.tensor_tensor(out=Ysum[:], in0=Ysum[:], in1=lnb[:, None, rows].to_broadcast([128, NT, 128]), op=ALU.add),
                  reads=[Ysum, lnb], writes=[Ysum])
            kb.op("dve", lambda e: e.scalar_tensor_tensor(out=KS[:], in0=KS[:], scalar=rw[:, cc, 6:7], in1=Rr[:], op0=ALU.mult, op1=ALU.mult),
                  reads=[KS, rw, Rr], writes=[KS])
            bon = KBm[:].rearrange("p a b c -> p (a b c)")[:, 0:32]
            for tt in range(NT):
                kb.op("pe", lambda e: e.matmul(psX[:, tt * 2:tt * 2 + 2], lhsT=KS[:, tt * 128:(tt + 1) * 128], rhs=sel, start=True, stop=True),
                      reads=[KS, cst], writes=[psX])
            kb.op("act", lambda e: e.activation(out=bon, in_=psX[:, 0:32], func=AF.Copy, scale=0.5), reads=[psX], writes=[KBm])
            V3 = Vtm[:].rearrange("p t (h i) -> p (t h) i", h=2)
            kb.op("pool", lambda e: e.tensor_tensor(out=V3, in0=V3, in1=bon[:, :, None].to_broadcast([128, 32, 64]), op=ALU.mult),
                  reads=[Vtm, KBm], writes=[Vtm])
            kb.op("pool", lambda e: e.tensor_tensor(out=Ysum[:], in0=Ysum[:], in1=Vtm[:], op=ALU.add), reads=[Ysum, Vtm], writes=[Ysum])
            gt = XE[:].rearrange("p (t c) -> p t c", c=128)
            kb.dma("sp", gt, D.g_d.ap()[:, cc * 128:(cc + 1) * 128].rearrange("(tt p) c -> p tt c", p=128), reads=[D.g_d], writes=[XE])
            ybf = KT[:].rearrange("p (t c) -> p t c", c=128).bitcast(BF16) if False else None
            kb.op("dve", lambda e: e.tensor_tensor(out=Ysum[:], in0=Ysum[:], in1=gt, op=ALU.mult), reads=[Ysum, XE], writes=[Ysum])
            if dbg is not None and "yfin" in dbg:
                kb.dma("sp", D.dbgF.ap()[:, cc * 128:(cc + 1) * 128].rearrange("(tt p) c -> p tt c", p=128), Ysum[:], reads=[Ysum], writes=[D.dbgF])
            yo = AT[:]
            for g in range(4):
                for j in range(4):
                    tt = g * 4 + j
                    kb.op("pe", lambda e: e.transpose(out=psX[:, j * 128:(j + 1) * 128], in_=Ysum[:, tt, :], identity=C.ident[:]),
                          reads=[Ysum, C.ident], writes=[psX])
                kb.op("act", lambda e: e.copy(out=AT[:, g * 512:(g + 1) * 512], in_=psX[:]), reads=[psX], writes=[AT])
            ybf = kb.view(BW, "ybf")
            yb = KT[:].bitcast(BF16)[:, 0:2048]
            kb.op("dve", lambda e: e.tensor_copy(out=yb, in_=AT[:]), reads=[AT], writes=[KT])
            kb.dma("sp", D.ymT_d.ap()[cc], yb, reads=[KT], writes=[D.ymT_d])
        kb.barrier()


def host_consts():
    S = 2048
    rows = np.repeat(np.arange(32), 64).astype(np.float32)
    cols = np.tile(np.arange(64), 32).astype(np.float32)
    inv = (10000.0 ** (-np.arange(0, 32, 2, dtype=np.float32) / 32)).astype(np.float32)
    ar = rows[:, None] * inv[None]
    ac = cols[:, None] * inv[None]
    tabC = np.concatenate([np.cos(ar), np.cos(ac)], 1).astype(np.float32)
    tabS = np.concatenate([np.sin(ar), np.sin(ac)], 1).astype(np.float32)
    ident = np.eye(128, dtype=np.float32)
    iota = np.tile(np.arange(128, dtype=np.float32)[None], (128, 1))
    r = np.arange(128)[:, None]
    s = np.arange(128)[None, :]
    same = (r // 64) == (s // 64)
    masks = np.zeros((128, 6, 128), np.float32)
    masks[:, 0] = same & (r < s)
    masks[:, 1] = same & (r <= s)
    masks[:, 2] = same & (r > s)
    masks[:, 3] = same & (r >= s)
    masks[:, 4] = same & (r > s)
    masks[:, 5] = same
    rcst = np.zeros((128, 66 + 2048), np.float32)
    pp = np.arange(128)
    rcst[pp, pp % 64] = 1.0
    rcst[:, 64] = (pp // 64 == 0)
    rcst[:, 65] = (pp // 64 == 1)
    seg = np.ones(2048, np.float32); seg[::64] = 0.0
    rcst[:, 66:] = seg[None]
    return dict(tabC=tabC, tabS=tabS, ident=ident, iota=iota, masks=masks, rcst=rcst)

def prep_shared(inp):
    L = 0
    d = host_consts()
    mu_c = np.zeros((128, 3 * NRCH), np.float32)
    for ci, (c0, cs) in enumerate(RCH):
        mu_c[:cs, ci] = inp["mu_prev"][L, c0:c0 + cs]
        mu_c[:cs, NRCH + ci] = inp["mu_next"][L, c0:c0 + cs]
    d["mu_c"] = mu_c
    d["w_in"] = np.ascontiguousarray(inp["w_in"][L])
    for k in ["norm1_w", "norm2_w", "q_norm_w", "k_norm_w", "w_out", "w_pq", "w2", "a2", "g2", "lnx_w", "lnx_b", "v_tab"]:
        d[k] = np.ascontiguousarray(inp[k][L])
    d["normf_w"] = np.ascontiguousarray(inp["normf_w"])
    d["skT"] = np.ascontiguousarray(inp["sub_keys"][L].reshape(16, 128, 128).transpose(2, 0, 1))
    d["u_tabT"] = np.ascontiguousarray(inp["u_tab"][L].T)
    rw = np.zeros((128, 8, 8), np.float32)
    def ch(v):
        return v.reshape(8, 128).T
    rw[:, :, 0] = ch(inp["w0"][L, 0]); rw[:, :, 1] = ch(inp["w0"][L, 1])
    rw[:, :, 2] = ch(inp["a0"][L, 0]); rw[:, :, 3] = ch(inp["a0"][L, 1])
    rw[:, :, 4] = ch(inp["k_k"][L]); rw[:, :, 5] = ch(inp["k_a"][L]); rw[:, :, 6] = ch(inp["r_k"][L].reshape(-1))
    d["rw_c"] = rw
    return d


_CACHE = {}


def build_program():
    nc = bass.Bass("TRN2", target_bir_lowering=False)
    kb = KB(nc)
    D = declare(kb, None)
    C = consts(kb, D)
    phase_A(kb, C, D)
    phase_Rpre(kb, C, D)
    phase_R(kb, C, D)
    phase_T(kb, C, D)
    phase_O(kb, C, D, D.ymT_d)
    phase_P(kb, C, D)
    phase_F(kb, C, D)
    kb.finish("sp")
    return nc


def kernel(**inputs):
    inp = {k: np.asarray(v) for k, v in inputs.items()}
    shared = prep_shared(inp)
    nc = build_program()
    in_maps = []
    for b in range(8):
        d = dict(shared)
        d["x"] = np.ascontiguousarray(inp["x"][b])
        in_maps.append(d)
    res = run_bass_kernel_spmd(nc, in_maps, core_ids=list(range(8)))
    out = np.stack([np.asarray(r["out"], dtype=np.float32) for r in res.results], axis=0)
    return out
```

```python
import numpy as np
import contextlib
import concourse.bass as bass
import concourse.mybir as mybir
from concourse.bass_utils import run_bass_kernel_spmd

F32 = mybir.dt.float32
BF16 = mybir.dt.bfloat16
U32 = mybir.dt.uint32
ALU = mybir.AluOpType
AF = mybir.ActivationFunctionType
AX = mybir.AxisListType


class T:
    def __init__(self, h, name):
        self.h = h
        self.name = name
        self.w = None
        self.r = {}
        self.dsem = None
        self.dcnt = 0
        self.is_psum = False

    def __getitem__(self, k):
        return self.h[k]

    def ap(self):
        return self.h.ap() if hasattr(self.h, "ap") else self.h[:]


class KB:
    def __init__(self, nc):
        self.nc = nc
        self.es = contextlib.ExitStack()
        self.engs = {"pe": nc.tensor, "act": nc.scalar, "dve": nc.vector, "pool": nc.gpsimd, "sp": nc.sync}
        self.sem = {}
        self.cnt = {}
        for e in self.engs:
            self.sem[e] = self.es.enter_context(nc.semaphore("s_" + e))
            self.cnt[e] = 0
        self.waited = {}
        self.alltensors = []
        self.dsems = []
        self.n_ins = 0

    def sb(self, name, shape, dt=F32, stack=None):
        h = (stack or self.es).enter_context(self.nc.sbuf_tensor(name, list(shape), dt))
        t = T(h, name)
        return t

    def ps(self, name, shape, dt=F32, stack=None):
        h = (stack or self.es).enter_context(self.nc.psum_tensor(name, list(shape), dt))
        t = T(h, name)
        t.is_psum = True
        return t

    def dram(self, name, shape, dt=F32, kind="Internal"):
        h = self.nc.dram_tensor(name, list(shape), dt, kind=kind)
        return T(h, name)

    def view(self, t, name=None):
        return t

    def _wait(self, eng, tok):
        if tok is None:
            return
        sem, val = tok
        key = (eng, id(sem))
        if self.waited.get(key, 0) >= val:
            return
        self.engs[eng].wait_ge(sem, val)
        self.waited[key] = val

    def _deps(self, eng, reads, writes):
        own = id(self.sem[eng])
        for t in reads:
            if t.w is not None:
                if eng == "pe" and id(t.w[0]) == own:
                    continue
                self._wait(eng, t.w)
        for t in writes:
            if t.w is not None and id(t.w[0]) != own:
                self._wait(eng, t.w)
            for k, tok in t.r.items():
                if k == own:
                    continue
                self._wait(eng, tok)

    def _mark(self, tok, reads, writes):
        for t in reads:
            if t in writes:
                continue
            t.r[id(tok[0])] = tok
        for t in writes:
            t.w = tok
            t.r = {}

    def op(self, eng, fn, reads=(), writes=()):
        psr = [t for t in reads if t.is_psum and t not in writes]
        if psr:
            writes = list(writes) + psr
        self._deps(eng, reads, writes)
        ins = fn(self.engs[eng])
        self.cnt[eng] += 1
        ins.then_inc(self.sem[eng], 1)
        tok = (self.sem[eng], self.cnt[eng])
        self._mark(tok, reads, writes)
        self.n_ins += 1
        return ins

    def dma(self, q, out_ap, in_ap, reads=(), writes=(), **kw):
        assert len(writes) == 1
        dst = writes[0]
        if dst.dsem is None:
            dst.dsem = self.es.enter_context(self.nc.semaphore("d_" + dst.name))
            self.dsems.append(dst)
        self._deps(q, reads, [])
        if dst.w is not None and dst.w[0] is not dst.dsem:
            self._wait(q, dst.w)
        for k, tok in dst.r.items():
            self._wait(q, tok)
        ins = self.engs[q].dma_start(out=out_ap, in_=in_ap, **kw)
        dst.dcnt += 16
        ins.then_inc(dst.dsem, 16)
        tok = (dst.dsem, dst.dcnt)
        for t in reads:
            t.r[id(tok[0])] = tok
        dst.w = tok
        dst.r = {}
        self.n_ins += 1
        return ins

    def pe_fence(self):
        if self.cnt["pe"] > 0:
            self._wait("pe", (self.sem["pe"], self.cnt["pe"]))

    def barrier(self):
        toks = [(self.sem[e], self.cnt[e]) for e in self.engs if self.cnt[e] > 0]
        toks += [(t.dsem, t.dcnt) for t in self.dsems if t.dcnt > 0]
        for e in self.engs:
            for tok in toks:
                if tok[0] is self.sem[e]:
                    continue
                self._wait(e, tok)

    def finish(self, eng="sp"):
        for t in self.dsems:
            if t.dcnt > 0:
                self._wait(eng, (t.dsem, t.dcnt))
        for e in self.engs:
            if e != eng and self.cnt[e] > 0:
                self._wait(eng, (self.sem[e], self.cnt[e]))


EPS = 1e-6
NT = 16
RCH = [(i * 128, 128) for i in range(24)] + [(3072 + i * 96, 96) for i in range(4)] + [(3456, 128), (3584, 128)]
NRCH = len(RCH)
RW = 3712


class Obj:
    pass


def declare(kb, debug):
    D = Obj()

    def inp(name, shape, dt=F32):
        setattr(D, name, kb.dram(name, shape, dt, kind="ExternalInput"))

    def scr(name, shape, dt=F32):
        kind = "ExternalOutput" if (debug and name in debug) else "Internal"
        setattr(D, name, kb.dram(name, shape, dt, kind=kind))

    inp("x", [2048, 2048])
    inp("w_in", [2048, 5248])
    inp("mu_c", [128, 3 * NRCH])
    inp("norm1_w", [2048])
    inp("norm2_w", [2048])
    inp("normf_w", [2048])
    inp("q_norm_w", [64])
    inp("k_norm_w", [64])
    inp("tabC", [2048, 32])
    inp("tabS", [2048, 32])
    inp("ident", [128, 128])
    inp("w_out", [2048, 2048])
    inp("w_pq", [2048, 2048])
    inp("skT", [128, 16, 128])
    inp("iota", [128, 128])
    inp("u_tabT", [128, 128, 2048])
    inp("v_tab", [16384, 2048])
    inp("w2", [2, 96, 1024])
    inp("a2", [2, 96, 1024])
    inp("g2", [256, 1024])
    inp("rw_c", [128, 8, 8])
    inp("lnx_w", [1024])
    inp("lnx_b", [1024])
    inp("masks", [128, 6, 128])
    if debug and "in_ymT" in debug:
        inp("ymT_in", [16, 128, 2048], BF16)
    if debug and "in_x1" in debug:
        inp("x1_in", [2048, 2048])
    scr("zs_d", [RW, 2048])
    scr("qkv_d", [2048, 1536])
    scr("ymT_d", [16, 128, 2048], BF16)
    scr("x1_d", [2048, 2048])
    scr("G_d", [128, 128, 2048], BF16)
    scr("peT_d", [2048, 2048])
    scr("A_d", [128, 128, 2048], BF16)
    scr("ld_d", [2, 1024, 2048])
    scr("a_d", [2, 1024, 2048])
    scr("g_d", [2048, 1024])
    inp("rcst", [128, 66 + 2048])
    if debug and "dbgY" in debug:
        setattr(D, "dbgY", kb.dram("dbgY", [2048, 1024], F32, kind="ExternalOutput"))
        setattr(D, "dbgF", kb.dram("dbgF", [2048, 1024], F32, kind="ExternalOutput"))
    scr("S_d", [2048, 16, 128])
    scr("h2T_d", [16, 128, 2048], BF16)
    setattr(D, "out", kb.dram("out", [2048, 2048], F32, kind="ExternalOutput"))
    return D


def consts(kb, D):
    C = Obj()
    C.ident = kb.sb("c_ident", [128, 128], F32)
    kb.dma("sp", C.ident[:], D.ident.ap(), writes=[C.ident])
    C.identb = kb.sb("c_identb", [128, 128], BF16)
    kb.op("dve", lambda e: e.tensor_copy(out=C.identb[:], in_=C.ident[:]), reads=[C.ident], writes=[C.identb])
    return C


def norm_to_T(kb, C, src, wvec, hT, tag):
    with contextlib.ExitStack() as st:
        wb = kb.sb(tag + "wb", [128, 2048], F32, st)
        kb.dma("pool", wb[:], wvec.ap().partition_broadcast(128), writes=[wb])
        xts = [kb.sb(f"{tag}x{i}", [128, 2048], F32, st) for i in range(2)]
        junk = kb.sb(tag + "junk", [128, 2048], BF16, st)
        hb = [kb.sb(f"{tag}h{i}", [128, 2048], BF16, st) for i in range(2)]
        ss = kb.sb(tag + "ss", [128, NT], F32, st)
        rs = kb.sb(tag + "rs", [128, NT], F32, st)
        ptr = [kb.ps(f"{tag}ps{i}", [128, 1024], BF16, st) for i in range(2)]
        kb.op("pool", lambda e: e.memset(ss[:], 0.0), writes=[ss])
        for tt in range(NT):
            xt = xts[tt % 2]
            kb.dma("sp", xt[:], src.ap()[tt * 128:(tt + 1) * 128, :], writes=[xt])
            kb.op("act", lambda e: e.activation(out=junk[:], in_=xt[:], func=AF.Square, accum_out=ss[:, tt:tt + 1]),
                  reads=[xt], writes=[junk, ss])
            kb.op("dve", lambda e: e.tensor_scalar(out=rs[:, tt:tt + 1], in0=ss[:, tt:tt + 1], scalar1=1.0 / 2048, scalar2=EPS,
                                                   op0=ALU.mult, op1=ALU.add), reads=[ss], writes=[rs])
            kb.op("act", lambda e: e.activation(out=rs[:, tt:tt + 1], in_=rs[:, tt:tt + 1], func=AF.Sqrt), reads=[rs], writes=[rs])
            kb.op("dve", lambda e: e.reciprocal(out=rs[:, tt:tt + 1], in_=rs[:, tt:tt + 1]), reads=[rs], writes=[rs])
            h = hb[tt % 2]
            kb.op("dve", lambda e: e.scalar_tensor_tensor(out=h[:], in0=xt[:], scalar=rs[:, tt:tt + 1], in1=wb[:],
                                                          op0=ALU.mult, op1=ALU.mult), reads=[xt, rs, wb], writes=[h])
            for g in range(2):
                p = ptr[g]
                for j in range(8):
                    dc = g * 8 + j
                    kb.op("pe", lambda e: e.transpose(out=p[:, j * 128:(j + 1) * 128], in_=h[:, dc * 128:(dc + 1) * 128],
                                                      identity=C.identb[:]), reads=[h, C.identb], writes=[p])
                eng = "act" if g == 0 else "dve"
                src_ap = p[:].rearrange("p (j t) -> p j t", j=8)
                dst_ap = hT[:, g * 8:(g + 1) * 8, tt * 128:(tt + 1) * 128]
                if eng == "act":
                    kb.op("act", lambda e: e.copy(out=dst_ap, in_=src_ap), reads=[p], writes=[hT])
                else:
                    kb.op("dve", lambda e: e.tensor_copy(out=dst_ap, in_=src_ap), reads=[p], writes=[hT])
        kb.barrier()


def phase_A(kb, C, D):
    with contextlib.ExitStack() as st:
        hT = kb.sb("hT", [128, 16, 2048], BF16, st)
        norm_to_T(kb, C, D.x, D.norm1_w, hT, "n1")
        mu = kb.sb("mu", [128, 3 * NRCH], F32, st)
        kb.dma("sp", mu[:], D.mu_c.ap(), writes=[mu])
        kb.op("dve", lambda e: e.tensor_tensor(out=mu[:, 60:90], in0=mu[:, 0:30], in1=mu[:, 30:60], op=ALU.add), reads=[mu], writes=[mu])
        kb.op("dve", lambda e: e.tensor_scalar(out=mu[:, 60:90], in0=mu[:, 60:90], scalar1=-1.0, scalar2=1.0, op0=ALU.mult, op1=ALU.add),
              reads=[mu], writes=[mu])
        stg = [kb.sb(f"a_stg{i}", [128, 4096], F32, st) for i in range(2)]
        wbf = [kb.sb(f"a_wbf{i}", [128, 16, 128], BF16, st) for i in range(2)]
        accs = [kb.sb(f"a_acc{i}", [128, 2048], F32, st) for i in range(2)]
        pss = [[kb.ps(f"a_ps{i}_{j}", [128, 512], F32, st) for j in range(4)] for i in range(2)]
        w_v = D.w_in.ap().rearrange("(dc p) c -> p dc c", p=128)
        for ci, (c0, cs) in enumerate(RCH):
            sg = stg[ci % 2]
            sgv = sg[:, 0:16 * cs].rearrange("p (dc c) -> p dc c", dc=16)
            kb.dma("sp" if ci % 2 == 0 else "act", sgv, w_v[:, :, c0:c0 + cs], writes=[sg])
            wb = wbf[ci % 2]
            kb.op("pool", lambda e: e.tensor_copy(out=wb[:, :, 0:cs], in_=sgv), reads=[sg], writes=[wb])
            ps = pss[ci % 2]
            acc = accs[ci % 2]
            for tb in range(4):
                for dc in range(16):
                    kb.op("pe", lambda e: e.matmul(ps[tb][0:cs, :], lhsT=wb[:, dc, 0:cs], rhs=hT[:, dc, tb * 512:(tb + 1) * 512],
                                                   start=(dc == 0), stop=(dc == 15)), reads=[wb, hT], writes=[ps[tb]])
            for tb in range(4):
                kb.op("act", lambda e: e.activation(out=acc[0:cs, tb * 512:(tb + 1) * 512], in_=ps[tb][0:cs, :], func=AF.Copy,
                                                    scale=mu[0:cs, 60 + ci:61 + ci]), reads=[ps[tb], mu], writes=[acc])
            for tb in range(4):
                n = 512 if tb < 3 else 511
                d0 = tb * 512 + 1
                kb.op("dve", lambda e: e.scalar_tensor_tensor(out=acc[0:cs, d0:d0 + n], in0=ps[tb][0:cs, 0:n], scalar=mu[0:cs, ci:ci + 1],
                                                              in1=acc[0:cs, d0:d0 + n], op0=ALU.mult, op1=ALU.add),
                      reads=[ps[tb], mu, acc], writes=[acc])
                s0 = 1 if tb == 0 else 0
                n = 512 - s0
                d0 = tb * 512 + s0 - 1
                kb.op("dve", lambda e: e.scalar_tensor_tensor(out=acc[0:cs, d0:d0 + n], in0=ps[tb][0:cs, s0:512], scalar=mu[0:cs, 30 + ci:31 + ci],
                                                              in1=acc[0:cs, d0:d0 + n], op0=ALU.mult, op1=ALU.add),
                      reads=[ps[tb], mu, acc], writes=[acc])
            kb.dma("pool", D.zs_d.ap()[c0:c0 + cs, :], acc[0:cs, :], reads=[acc], writes=[D.zs_d])
        kb.barrier()
        with contextlib.ExitStack() as st2:
            wq = kb.sb("a_wq", [128, 16, 512], BF16, st2)
            ev = [kb.sb(f"a_ev{i}", [128, 512], F32, st2) for i in range(2)]
            for cg in range(3):
                c0 = RW + cg * 512
                for hf in range(2):
                    sg = stg[hf]
                    sgv = sg[:].rearrange("p (dc c) -> p dc c", dc=8)
                    kb.dma("sp" if hf == 0 else "act", sgv, w_v[:, hf * 8:(hf + 1) * 8, c0:c0 + 512], writes=[sg])
                    kb.op("pool", lambda e: e.tensor_copy(out=wq[:, hf * 8:(hf + 1) * 8, :], in_=sgv), reads=[sg], writes=[wq])
                for tt in range(NT):
                    ps = pss[tt % 2][0]
                    for dc in range(16):
                        kb.op("pe", lambda e: e.matmul(ps[:], lhsT=hT[:, dc, tt * 128:(tt + 1) * 128], rhs=wq[:, dc, :],
                                                       start=(dc == 0), stop=(dc == 15)), reads=[wq, hT], writes=[ps])
                    o = ev[tt % 2]
                    kb.op("act", lambda e: e.copy(out=o[:], in_=ps[:]), reads=[ps], writes=[o])
                    kb.dma("sp", D.qkv_d.ap()[tt * 128:(tt + 1) * 128, cg * 512:(cg + 1) * 512], o[:], reads=[o], writes=[D.qkv_d])
            kb.barrier()


def phase_T(kb, C, D):
    with contextlib.ExitStack() as st:
        qT = kb.sb("t_qT", [128, 8, 2048], BF16, st)
        kT2 = kb.sb("t_kT2", [128, 4, 2048], BF16, st)
        vaug = kb.sb("t_vaug", [128, NT, 4, 65], BF16, st)
        wqk = kb.sb("t_wqk", [128, 20, 64], F32, st)
        w64 = kb.sb("t_w64", [128, 2, 64], F32, st)
        kb.dma("pool", w64[:, 0, :], D.q_norm_w.ap().partition_broadcast(128), writes=[w64])
        kb.dma("pool", w64[:, 1, :], D.k_norm_w.ap().partition_broadcast(128), writes=[w64])
        kb.op("dve", lambda e: e.tensor_scalar(out=wqk[:, 0:16, :], in0=w64[:, 0:1, :].to_broadcast([128, 16, 64]), scalar1=0.125, scalar2=None,
                                               op0=ALU.mult), reads=[w64], writes=[wqk])
        kb.op("dve", lambda e: e.tensor_copy(out=wqk[:, 16:20, :], in_=w64[:, 1:2, :].to_broadcast([128, 4, 64])), reads=[w64], writes=[wqk])
        kb.op("pool", lambda e: e.memset(vaug[:], 1.0), writes=[vaug])
        with contextlib.ExitStack() as st2:
            qk = [kb.sb(f"t_qk{i}", [128, 1536], F32, st2) for i in range(2)]
            tC = [kb.sb(f"t_tC{i}", [128, 2, 16], F32, st2) for i in range(2)]
            tS = [kb.sb(f"t_tS{i}", [128, 2, 16], F32, st2) for i in range(2)]
            sq = kb.sb("t_sq", [128, 20, 64], F32, st2)
            ss = kb.sb("t_ss", [128, 20], F32, st2)
            qn = kb.sb("t_qn", [128, 20, 64], F32, st2)
            t1 = kb.sb("t_t1", [128, 20, 2, 16], F32, st2)
            t2 = kb.sb("t_t2", [128, 20, 2, 16], F32, st2)
            t3 = kb.sb("t_t3", [128, 20, 2, 16], F32, st2)
            t4 = kb.sb("t_t4", [128, 20, 2, 16], F32, st2)
            qkr = kb.sb("t_qkr", [128, 20, 64], BF16, st2)
            kd = kb.sb("t_kd", [128, 4, 2, 64], BF16, st2)
            pq = kb.ps("t_pq", [128, 1024], BF16, st2)
            pk = kb.ps("t_pk", [128, 1024], BF16, st2)
            for tt in range(NT):
                q = qk[tt % 2]
                cC = tC[tt % 2]
                cS = tS[tt % 2]
                kb.dma("sp", q[:], D.qkv_d.ap()[tt * 128:(tt + 1) * 128, :], reads=[D.qkv_d], writes=[q])
                kb.dma("act", cC[:].rearrange("p a b -> p (a b)"), D.tabC.ap()[tt * 128:(tt + 1) * 128, :], writes=[cC])
                kb.dma("act", cS[:].rearrange("p a b -> p (a b)"), D.tabS.ap()[tt * 128:(tt + 1) * 128, :], writes=[cS])
                qv = q[:, 0:1280].rearrange("p (h d) -> p h d", h=20)
                kb.op("act", lambda e: e.activation(out=sq[:], in_=qv, func=AF.Square), reads=[q], writes=[sq])
                kb.op("dve", lambda e: e.tensor_reduce(out=ss[:], in_=sq[:], axis=AX.X, op=ALU.add), reads=[sq], writes=[ss])
                kb.op("dve", lambda e: e.tensor_scalar(out=ss[:], in0=ss[:], scalar1=1.0 / 64, scalar2=EPS, op0=ALU.mult, op1=ALU.add),
                      reads=[ss], writes=[ss])
                kb.op("act", lambda e: e.activation(out=ss[:], in_=ss[:], func=AF.Sqrt), reads=[ss], writes=[ss])
                kb.op("dve", lambda e: e.reciprocal(out=ss[:], in_=ss[:]), reads=[ss], writes=[ss])
                kb.op("dve", lambda e: e.tensor_tensor(out=qn[:], in0=qv, in1=ss[:, :, None].to_broadcast([128, 20, 64]), op=ALU.mult),
                      reads=[q, ss], writes=[qn])
                kb.op("pool", lambda e: e.tensor_tensor(out=qn[:], in0=qn[:], in1=wqk[:], op=ALU.mult), reads=[qn, wqk], writes=[qn])
                qn5 = qn[:].rearrange("p h (a b c) -> p h a b c", a=2, b=2)
                x1 = qn5[:, :, :, 0, :]
                x2 = qn5[:, :, :, 1, :]
                Cb = cC[:, None, :, :].to_broadcast([128, 20, 2, 16])
                Sb = cS[:, None, :, :].to_broadcast([128, 20, 2, 16])
                kb.op("dve", lambda e: e.tensor_tensor(out=t1[:], in0=x1, in1=Cb, op=ALU.mult), reads=[qn, cC], writes=[t1])
                kb.op("pool", lambda e: e.tensor_tensor(out=t2[:], in0=x2, in1=Sb, op=ALU.mult), reads=[qn, cS], writes=[t2])
                kb.op("pool", lambda e: e.tensor_tensor(out=t3[:], in0=x2, in1=Cb, op=ALU.mult), reads=[qn, cC], writes=[t3])
                kb.op("dve", lambda e: e.tensor_tensor(out=t4[:], in0=x1, in1=Sb, op=ALU.mult), reads=[qn, cS], writes=[t4])
                r5 = qkr[:].rearrange("p h (a b c) -> p h a b c", a=2, b=2)
                kb.op("dve", lambda e: e.tensor_tensor(out=r5[:, :, :, 0, :], in0=t1[:], in1=t2[:], op=ALU.subtract), reads=[t1, t2], writes=[qkr])
                kb.op("pool", lambda e: e.tensor_tensor(out=r5[:, :, :, 1, :], in0=t3[:], in1=t4[:], op=ALU.add), reads=[t3, t4], writes=[qkr])
                kb.op("pool", lambda e: e.tensor_copy(out=kd[:], in_=qkr[:, 16:20, None, :].to_broadcast([128, 4, 2, 64])), reads=[qkr], writes=[kd])
                for j in range(8):
                    kb.op("pe", lambda e: e.transpose(out=pq[:, j * 128:(j + 1) * 128], in_=qkr[:, 2 * j:2 * j + 2, :].rearrange("p a b -> p (a b)"),
                                                      identity=C.identb[:]), reads=[qkr, C.identb], writes=[pq])
                kb.op("act", lambda e: e.copy(out=qT[:, :, tt * 128:(tt + 1) * 128], in_=pq[:].rearrange("p (j t) -> p j t", j=8)),
                      reads=[pq], writes=[qT])
                for j in range(4):
                    kb.op("pe", lambda e: e.transpose(out=pk[:, j * 128:(j + 1) * 128], in_=kd[:, j, :, :].rearrange("p a b -> p (a b)"),
                                                      identity=C.identb[:]), reads=[kd, C.identb], writes=[pk])
                kb.op("dve", lambda e: e.tensor_copy(out=kT2[:, :, tt * 128:(tt + 1) * 128], in_=pk[:, 0:512].rearrange("p (j t) -> p j t", j=4)),
                      reads=[pk], writes=[kT2])
                kb.op("pool", lambda e: e.tensor_copy(out=vaug[:, tt, :, 0:64], in_=q[:, 1280:1536].rearrange("p (h d) -> p h d", h=4)),
                      reads=[q], writes=[vaug])
            kb.barrier()
        with contextlib.ExitStack() as st3:
            yatt = kb.sb("t_yatt", [128, NT, 1024], BF16, st3)
            pexp = [kb.sb(f"t_pexp{i}", [128, 512], BF16, st3) for i in range(3)]
            rinv = kb.sb("t_rinv", [128, 4], F32, st3)
            pss = [kb.ps(f"t_pss{i}", [128, 512], F32, st3) for i in range(2)]
            po = [kb.ps(f"t_po{i}", [128, 512], F32, st3) for i in range(4)]
            it = 0
            for h in range(16):
                kv = h // 4
                c = h // 2
                b0 = (h % 2) * 64
                for qb in range(4):
                    for kt in range(NT):
                        ps = pss[it % 2]
                        pe_ = pexp[it % 3]
                        it += 1
                        kb.op("pe", lambda e: e.matmul(ps[:], lhsT=kT2[b0:b0 + 64, kv, kt * 128:(kt + 1) * 128],
                                                       rhs=qT[b0:b0 + 64, c, qb * 512:(qb + 1) * 512], start=True, stop=True),
                              reads=[kT2, qT], writes=[ps])
                        kb.op("act", lambda e: e.activation(out=pe_[:], in_=ps[:], func=AF.Exp), reads=[ps], writes=[pe_])
                        for j in range(4):
                            kb.op("pe", lambda e: e.matmul(po[j][:, 0:65], lhsT=pe_[:, j * 128:(j + 1) * 128], rhs=vaug[:, kt, kv, :],
                                                           start=(kt == 0), stop=(kt == NT - 1)), reads=[pe_, vaug], writes=[po[j]])
                    for j in range(4):
                        kb.op("dve", lambda e: e.reciprocal(out=rinv[:, j:j + 1], in_=po[j][:, 64:65]), reads=[po[j]], writes=[rinv])
                        kb.op("dve", lambda e: e.tensor_scalar(out=yatt[:, qb * 4 + j, h * 64:(h + 1) * 64], in0=po[j][:, 0:64],
                                                               scalar1=rinv[:, j:j + 1], scalar2=None, op0=ALU.mult),
                              reads=[po[j], rinv], writes=[yatt])
            pt = [kb.ps(f"t_pt{i}", [128, 1024], BF16, st3) for i in range(2)]
            yT = [kb.sb(f"t_yT{i}", [128, 8, 128], BF16, st3) for i in range(2)]
            for tt in range(NT):
                p = pt[tt % 2]
                o = yT[tt % 2]
                for j in range(8):
                    kb.op("pe", lambda e: e.transpose(out=p[:, j * 128:(j + 1) * 128], in_=yatt[:, tt, j * 128:(j + 1) * 128],
                                                      identity=C.identb[:]), reads=[yatt, C.identb], writes=[p])
                kb.op("act", lambda e: e.copy(out=o[:], in_=p[:].rearrange("p (j t) -> p j t", j=8)), reads=[p], writes=[o])
                kb.dma("sp", D.ymT_d.ap()[8:16, :, tt * 128:(tt + 1) * 128].rearrange("j p t -> p j t"), o[:], reads=[o], writes=[D.ymT_d])
            kb.barrier()


def phase_O(kb, C, D, ym_src):
    with contextlib.ExitStack() as st:
        ymT = kb.sb("o_ymT", [128, 16, 2048], BF16, st)
        for j in range(16):
            kb.dma("sp" if j % 2 == 0 else "act", ymT[:, j, :], ym_src.ap()[j], reads=[ym_src], writes=[ymT])
        stg = [kb.sb(f"o_stg{i}", [128, 4096], F32, st) for i in range(2)]
        wo = kb.sb("o_wo", [128, 16, 512], BF16, st)
        xs = [kb.sb(f"o_xs{i}", [128, 512], F32, st) for i in range(2)]
        pss = [kb.ps(f"o_ps{i}", [128, 512], F32, st) for i in range(2)]
        w_v = D.w_out.ap().rearrange("(kc p) c -> p kc c", p=128)
        for dg in range(4):
            for hf in range(2):
                sg = stg[hf]
                sgv = sg[:].rearrange("p (dc c) -> p dc c", dc=8)
                kb.dma("sp" if hf == 0 else "act", sgv, w_v[:, hf * 8:(hf + 1) * 8, dg * 512:(dg + 1) * 512], writes=[sg])
                kb.op("pool", lambda e: e.tensor_copy(out=wo[:, hf * 8:(hf + 1) * 8, :], in_=sgv), reads=[sg], writes=[wo])
            for tt in range(NT):
                ps = pss[tt % 2]
                xt = xs[tt % 2]
                kb.dma("sp", xt[:], D.x.ap()[tt * 128:(tt + 1) * 128, dg * 512:(dg + 1) * 512], writes=[xt])
                for kc in range(16):
                    kb.op("pe", lambda e: e.matmul(ps[:], lhsT=ymT[:, kc, tt * 128:(tt + 1) * 128], rhs=wo[:, kc, :],
                                                   start=(kc == 0), stop=(kc == 15)), reads=[ymT, wo], writes=[ps])
                kb.op("dve", lambda e: e.tensor_tensor(out=xt[:], in0=ps[:], in1=xt[:], op=ALU.add), reads=[ps, xt], writes=[xt])
                kb.dma("act", D.x1_d.ap()[tt * 128:(tt + 1) * 128, dg * 512:(dg + 1) * 512], xt[:], reads=[xt], writes=[D.x1_d])
        kb.barrier()


def phase_P(kb, C, D):
    with contextlib.ExitStack() as stP:
        ET = kb.sb("p_ET", [128, 3, 2048], F32, stP)
        iota = kb.sb("p_iota", [128, 128], F32, stP)
        kb.dma("sp", iota[:], D.iota.ap(), writes=[iota])
        with contextlib.ExitStack() as st, kb.nc.named_scope("P1"):
            h2T = kb.sb("p_h2T", [128, 16, 2048], BF16, st)
            norm_to_T(kb, C, D.x1_d, D.norm2_w, h2T, "n2")
            for dc in range(16):
                kb.dma("sp" if dc % 2 == 0 else "act", D.h2T_d.ap()[dc], h2T[:, dc, :], reads=[h2T], writes=[D.h2T_d])
            skb = kb.sb("p_sk", [128, 16, 128], F32, st)
            kb.dma("sp", skb[:], D.skT.ap(), writes=[skb])
            stg = [kb.sb(f"p_stg{i}", [128, 16, 128], F32, st) for i in range(2)]
            wb = [kb.sb(f"p_wb{i}", [128, 16, 128], BF16, st) for i in range(2)]
            qTs = [kb.sb(f"p_qT{i}", [128, 2048], F32, st) for i in range(2)]
            sev = [kb.sb(f"p_sev{i}", [128, 16, 128], F32, st) for i in range(2)]
            pss = [[kb.ps(f"p_ps{i}_{j}", [128, 512], F32, st) for j in range(3)] for i in range(2)]
            w_v = D.w_pq.ap().rearrange("(dc p) c -> p dc c", p=128)
            for hp in range(16):
                sg = stg[hp % 2]
                kb.dma("sp" if hp % 2 == 0 else "act", sg[:], w_v[:, :, hp * 128:(hp + 1) * 128], writes=[sg])
                w = wb[hp % 2]
                kb.op("pool", lambda e: e.tensor_copy(out=w[:], in_=sg[:]), reads=[sg], writes=[w])
                qT = qTs[hp % 2]
                for tb in range(4):
                    ps = pss[tb % 2][0]
                    for dc in range(16):
                        kb.op("pe", lambda e: e.matmul(ps[:], lhsT=w[:, dc, :], rhs=h2T[:, dc, tb * 512:(tb + 1) * 512],
                                                       start=(dc == 0), stop=(dc == 15)), reads=[w, h2T], writes=[ps])
                    kb.op("act", lambda e: e.copy(out=qT[:, tb * 512:(tb + 1) * 512], in_=ps[:]), reads=[ps], writes=[qT])
                se = sev[hp % 2]
                for g in range(4):
                    ps = pss[g % 2][1 + (g // 2) % 2]
                    for j in range(4):
                        tt = g * 4 + j
                        kb.op("pe", lambda e: e.matmul(ps[:, j * 128:(j + 1) * 128], lhsT=qT[:, tt * 128:(tt + 1) * 128], rhs=skb[:, hp, :],
                                                       start=True, stop=True), reads=[qT, skb], writes=[ps])
                    kb.op("dve", lambda e: e.tensor_copy(out=se[:, g * 4:(g + 1) * 4, :], in_=ps[:].rearrange("p (j n) -> p j n", j=4)),
                          reads=[ps], writes=[se])
                kb.dma("pool", D.S_d.ap()[:, hp, :].rearrange("(tt p) n -> p tt n", p=128), se[:], reads=[se], writes=[D.S_d])
            kb.barrier()
        with contextlib.ExitStack() as st, kb.nc.named_scope("P2"):
            Ss = [kb.sb(f"p_S{i}", [128, 16, 128], F32, st) for i in range(2)]
            S2 = kb.sb("p_S2", [128, 128], F32, st)
            M16 = kb.sb("p_M16", [128, 16, 16], F32, st)
            I16u = kb.sb("p_I16u", [128, 16, 16], U32, st)
            I16f = kb.sb("p_I16f", [128, 16, 16], F32, st)
            cand = kb.sb("p_cand", [128, 8, 16, 16], F32, st)
            cand2 = kb.sb("p_cand2", [128, 256], F32, st)
            C16 = kb.sb("p_C16", [128, 8, 16], F32, st)
            CIu = kb.sb("p_CIu", [128, 8, 16], U32, st)
            IJu = kb.sb("p_IJu", [128, 2, 8, 16], U32, st)
            IJf = kb.sb("p_IJf", [128, 2, 8, 16], F32, st)
            ex = kb.sb("p_ex", [128, 8, 16], F32, st)
            Z = kb.sb("p_Z", [128, 8], F32, st)
            EG = kb.sb("p_EG", [128, 3, 8, 16], F32, st)
            eq = kb.sb("p_eq", [128, 8, 16, 16], F32, st)
            pt = kb.ps("p_pt", [128, 512], F32, st)
            for tt in range(NT):
                S = Ss[tt % 2]
                kb.dma("sp", S[:].rearrange("p a n -> p (a n)"), D.S_d.ap()[tt * 128:(tt + 1) * 128].rearrange("p a n -> p (a n)"),
                       reads=[D.S_d], writes=[S])
                for hp in range(16):
                    kb.op("dve", lambda e: e.max(out=M16[:, hp, 0:8], in_=S[:, hp, :]), reads=[S], writes=[M16])
                    kb.op("dve", lambda e: e.max_index(out=I16u[:, hp, 0:8], in_max=M16[:, hp, 0:8], in_values=S[:, hp, :]),
                          reads=[S, M16], writes=[I16u])
                    kb.op("dve", lambda e: e.match_replace(out=S2[:], in_to_replace=M16[:, hp, 0:8], in_values=S[:, hp, :], imm_value=-1e30),
                          reads=[S, M16], writes=[S2])
                    kb.op("dve", lambda e: e.max(out=M16[:, hp, 8:16], in_=S2[:]), reads=[S2], writes=[M16])
                    kb.op("dve", lambda e: e.max_index(out=I16u[:, hp, 8:16], in_max=M16[:, hp, 8:16], in_values=S2[:]),
                          reads=[S2, M16], writes=[I16u])
                kb.op("pool", lambda e: e.tensor_copy(out=I16f[:], in_=I16u[:]), reads=[I16u], writes=[I16f])
                M4 = M16[:].rearrange("p (h q) k -> p h q k", q=2)
                I4 = I16f[:].rearrange("p (h q) k -> p h q k", q=2)
                kb.op("pool", lambda e: e.tensor_tensor(out=cand[:], in0=M4[:, :, 0, :, None].to_broadcast([128, 8, 16, 16]),
                                                        in1=M4[:, :, 1, None, :].to_broadcast([128, 8, 16, 16]), op=ALU.add),
                      reads=[M16], writes=[cand])
                for h in range(8):
                    ch = cand[:, h, :, :].rearrange("p a b -> p (a b)")
                    kb.op("dve", lambda e: e.max(out=C16[:, h, 0:8], in_=ch), reads=[cand], writes=[C16])
                    kb.op("dve", lambda e: e.max_index(out=CIu[:, h, 0:8], in_max=C16[:, h, 0:8], in_values=ch), reads=[cand, C16], writes=[CIu])
                    kb.op("dve", lambda e: e.match_replace(out=cand2[:], in_to_replace=C16[:, h, 0:8], in_values=ch, imm_value=-1e30),
                          reads=[cand, C16], writes=[cand2])
                    kb.op("dve", lambda e: e.max(out=C16[:, h, 8:16], in_=cand2[:]), reads=[cand2], writes=[C16])
                    kb.op("dve", lambda e: e.max_index(out=CIu[:, h, 8:16], in_max=C16[:, h, 8:16], in_values=cand2[:]),
                          reads=[cand2, C16], writes=[CIu])
                kb.op("pool", lambda e: e.tensor_tensor(out=ex[:], in0=C16[:], in1=C16[:, :, 0:1].to_broadcast([128, 8, 16]), op=ALU.subtract),
                      reads=[C16], writes=[ex])
                kb.op("act", lambda e: e.activation(out=ex[:], in_=ex[:], func=AF.Exp), reads=[ex], writes=[ex])
                kb.op("dve", lambda e: e.tensor_reduce(out=Z[:], in_=ex[:], axis=AX.X, op=ALU.add), reads=[ex], writes=[Z])
                kb.op("dve", lambda e: e.reciprocal(out=Z[:], in_=Z[:]), reads=[Z], writes=[Z])
                kb.op("dve", lambda e: e.tensor_tensor(out=EG[:, 2], in0=ex[:], in1=Z[:, :, None].to_broadcast([128, 8, 16]), op=ALU.mult),
                      reads=[ex, Z], writes=[EG])
                kb.op("dve", lambda e: e.tensor_single_scalar(out=IJu[:, 0], in_=CIu[:], scalar=4, op=ALU.logical_shift_right),
                      reads=[CIu], writes=[IJu])
                kb.op("dve", lambda e: e.tensor_single_scalar(out=IJu[:, 1], in_=CIu[:], scalar=15, op=ALU.bitwise_and),
                      reads=[CIu], writes=[IJu])
                kb.op("pool", lambda e: e.tensor_copy(out=IJf[:], in_=IJu[:]), reads=[IJu], writes=[IJf])
                for q in range(2):
                    kb.op("dve", lambda e: e.tensor_tensor(out=eq[:], in0=iota[:, None, None, 0:16].to_broadcast([128, 8, 16, 16]),
                                                            in1=IJf[:, q, :, :, None].to_broadcast([128, 8, 16, 16]), op=ALU.is_equal),
                          reads=[iota, IJf], writes=[eq])
                    kb.op("pool", lambda e: e.tensor_tensor(out=eq[:], in0=eq[:], in1=I4[:, :, q, None, :].to_broadcast([128, 8, 16, 16]),
                                                            op=ALU.mult), reads=[eq, I16f], writes=[eq])
                    kb.op("dve", lambda e: e.tensor_reduce(out=EG[:, q], in_=eq[:], axis=AX.X, op=ALU.add), reads=[eq], writes=[EG])
                for a in range(3):
                    kb.op("pe", lambda e: e.transpose(out=pt[:, a * 128:(a + 1) * 128], in_=EG[:, a].rearrange("p h k -> p (h k)"),
                                                      identity=C.ident[:]), reads=[EG, C.ident], writes=[pt])
                kb.op("act", lambda e: e.copy(out=ET[:, :, tt * 128:(tt + 1) * 128], in_=pt[:, 0:384].rearrange("p (a t) -> p a t", a=3)),
                      reads=[pt], writes=[ET])
            kb.barrier()
        with contextlib.ExitStack() as st, kb.nc.named_scope("P3"):
            Gs = kb.sb("p_Gs", [128, 128, 256], BF16, st)
            O1 = [kb.sb(f"p_O1{i}", [128, 32, 128], BF16, st) for i in range(2)]
            O2 = [kb.sb(f"p_O2{i}", [128, 32, 128], BF16, st) for i in range(2)]
            pg = [kb.ps(f"p_pg{i}", [128, 512], F32, st) for i in range(4)]
            it = 0
            for tg in range(8):
                for sub in range(8):
                    t0 = tg * 256 + sub * 32
                    o1 = O1[sub % 2]
                    o2 = O2[sub % 2]
                    iob = iota[:, None, :].to_broadcast([128, 32, 128])
                    kb.op("dve", lambda e: e.tensor_tensor(out=o1[:], in0=iob, in1=ET[:, 0, t0:t0 + 32, None].to_broadcast([128, 32, 128]),
                                                            op=ALU.is_equal), reads=[iota, ET], writes=[o1])
                    kb.op("dve", lambda e: e.tensor_tensor(out=o2[:], in0=iob, in1=ET[:, 1, t0:t0 + 32, None].to_broadcast([128, 32, 128]),
                                                           op=ALU.is_equal), reads=[iota, ET], writes=[o2])
                    kb.op("pool", lambda e: e.tensor_tensor(out=o2[:], in0=o2[:], in1=ET[:, 2, t0:t0 + 32, None].to_broadcast([128, 32, 128]),
                                                            op=ALU.mult), reads=[o2, ET], writes=[o2])
                    for q4 in range(8):
                        p = pg[it % 4]
                        it += 1
                        for j in range(4):
                            tl = q4 * 4 + j
                            kb.op("pe", lambda e: e.matmul(p[:, j * 128:(j + 1) * 128], lhsT=o2[:, tl, :], rhs=o1[:, tl, :], start=True, stop=True),
                                  reads=[o1, o2], writes=[p])
                        tl0 = sub * 32 + q4 * 4
                        dst = Gs[:, :, tl0:tl0 + 4].rearrange("p e t -> p t e")
                        src = p[:].rearrange("p (t e) -> p t e", t=4)
                        if it % 2 == 0:
                            kb.op("act", lambda e: e.copy(out=dst, in_=src), reads=[p], writes=[Gs])
                        else:
                            kb.op("dve", lambda e: e.tensor_copy(out=dst, in_=src), reads=[p], writes=[Gs])
                for k8 in range(8):
                    kb.dma(["sp", "act", "pool"][k8 % 3],
                           D.G_d.ap()[k8 * 16:(k8 + 1) * 16, :, tg * 256:(tg + 1) * 256].rearrange("e1 e2 t -> e2 e1 t"),
                           Gs[:, k8 * 16:(k8 + 1) * 16, :], reads=[Gs], writes=[D.G_d])
            kb.barrier()
        with contextlib.ExitStack() as st, kb.nc.named_scope("P4"):
            h2T = kb.sb("p_h2Tb", [128, 16, 2048], BF16, st)
            for dc in range(16):
                kb.dma("sp" if dc % 2 == 0 else "act", h2T[:, dc, :], D.h2T_d.ap()[dc], reads=[D.h2T_d], writes=[h2T])
            stg = [kb.sb(f"p4_stg{i}", [128, 16, 128], F32, st) for i in range(2)]
            ub = [kb.sb(f"p4_ub{i}", [128, 16, 128], BF16, st) for i in range(2)]
            Gc = [kb.sb(f"p4_Gc{i}", [128, 2048], BF16, st) for i in range(2)]
            ge = [kb.sb(f"p4_ge{i}", [128, 2048], BF16, st) for i in range(2)]
            pss = [[kb.ps(f"p4_ps{i}_{j}", [128, 512], F32, st) for j in range(4)] for i in range(2)]
            def load(e1):
                sg = stg[e1 % 2]
                kb.dma("sp", sg[:].rearrange("p a b -> p (a b)"), D.u_tabT.ap()[e1], writes=[sg])
                g = Gc[e1 % 2]
                kb.dma("pool", g[:], D.G_d.ap()[e1], reads=[D.G_d], writes=[g])
            load(0)
            for e1 in range(128):
                if e1 + 1 < 128:
                    load(e1 + 1)
                sg = stg[e1 % 2]
                u = ub[e1 % 2]
                kb.op("dve", lambda e: e.tensor_copy(out=u[:], in_=sg[:]), reads=[sg], writes=[u])
                g = Gc[e1 % 2]
                ps = pss[e1 % 2]
                a = ge[e1 % 2]
                for tb in range(4):
                    for dc in range(16):
                        kb.op("pe", lambda e: e.matmul(ps[tb][:], lhsT=u[:, dc, :], rhs=h2T[:, dc, tb * 512:(tb + 1) * 512],
                                                       start=(dc == 0), stop=(dc == 15)), reads=[u, h2T], writes=[ps[tb]])
                    kb.op("act", lambda e: e.activation(out=a[:, tb * 512:(tb + 1) * 512], in_=ps[tb][:], func=AF.Gelu), reads=[ps[tb]], writes=[a])
                kb.op("pool", lambda e: e.tensor_tensor(out=a[:], in0=a[:], in1=g[:], op=ALU.mult), reads=[a, g], writes=[a])
                kb.dma("sp", D.A_d.ap()[e1], a[:], reads=[a], writes=[D.A_d])
            kb.barrier()
        with contextlib.ExitStack() as st, kb.nc.named_scope("P5"):
            vst = [kb.sb(f"p5_vst{i}", [128, 512], F32, st) for i in range(3)]
            vb = [kb.sb(f"p5_vb{i}", [128, 512], BF16, st) for i in range(3)]
            ac = [kb.sb(f"p5_ac{i}", [128, 512], BF16, st) for i in range(3)]
            ov = [kb.sb(f"p5_ov{i}", [128, 512], F32, st) for i in range(2)]
            pss = [[kb.ps(f"p5_ps{i}_{j}", [128, 512], F32, st) for j in range(4)] for i in range(2)]
            it = 0
            for tb in range(4):
                for dg in range(4):
                    ps = pss[(tb * 4 + dg) % 2]
                    for e1 in range(128):
                        k = it % 3
                        it += 1
                        kb.dma("sp", vst[k][:], D.v_tab.ap()[e1 * 128:(e1 + 1) * 128, dg * 512:(dg + 1) * 512], writes=[vst[k]])
                        kb.dma("pool", ac[k][:], D.A_d.ap()[e1][:, tb * 512:(tb + 1) * 512], reads=[D.A_d], writes=[ac[k]])
                        if it % 2 == 0:
                            kb.op("dve", lambda e: e.tensor_copy(out=vb[k][:], in_=vst[k][:]), reads=[vst[k]], writes=[vb[k]])
                        else:
                            kb.op("act", lambda e: e.copy(out=vb[k][:], in_=vst[k][:]), reads=[vst[k]], writes=[vb[k]])
                        for j in range(4):
                            kb.op("pe", lambda e: e.matmul(ps[j][:], lhsT=vb[k][:, j * 128:(j + 1) * 128], rhs=ac[k][:],
                                                           start=(e1 == 0), stop=(e1 == 127)), reads=[vb[k], ac[k]], writes=[ps[j]])
                    for j in range(4):
                        o = ov[j % 2]
                        if j % 2 == 0:
                            kb.op("act", lambda e: e.copy(out=o[:], in_=ps[j][:]), reads=[ps[j]], writes=[o])
                        else:
                            kb.op("dve", lambda e: e.tensor_copy(out=o[:], in_=ps[j][:]), reads=[ps[j]], writes=[o])
                        d0 = dg * 512 + j * 128
                        kb.dma("sp", D.peT_d.ap()[d0:d0 + 128, tb * 512:(tb + 1) * 512], o[:], reads=[o], writes=[D.peT_d])
            kb.barrier()


def phase_F(kb, C, D):
    with contextlib.ExitStack() as st:
        wb = kb.sb("f_wb", [128, 2048], F32, st)
        kb.dma("pool", wb[:], D.normf_w.ap().partition_broadcast(128), writes=[wb])
        xs = [kb.sb(f"f_x{i}", [128, 2048], F32, st) for i in range(2)]
        pes = [kb.sb(f"f_pe{i}", [128, 16, 128], F32, st) for i in range(2)]
        junk = kb.sb("f_junk", [128, 2048], BF16, st)
        ss = kb.sb("f_ss", [128, NT], F32, st)
        rs = kb.sb("f_rs", [128, NT], F32, st)
        kb.op("pool", lambda e: e.memset(ss[:], 0.0), writes=[ss])
        pss = [[kb.ps(f"f_ps{i}_{j}", [128, 512], F32, st) for j in range(4)] for i in range(2)]
        pe_v = D.peT_d.ap().rearrange("(dc p) t -> p dc t", p=128)
        for tt in range(NT):
            xt = xs[tt % 2]
            pe = pes[tt % 2]
            ps = pss[tt % 2]
            kb.dma("sp", xt[:], D.x1_d.ap()[tt * 128:(tt + 1) * 128, :], reads=[D.x1_d], writes=[xt])
            kb.dma("act", pe[:], pe_v[:, :, tt * 128:(tt + 1) * 128], reads=[D.peT_d], writes=[pe])
            for dc in range(16):
                kb.op("pe", lambda e: e.transpose(out=ps[dc // 4][:, (dc % 4) * 128:(dc % 4 + 1) * 128], in_=pe[:, dc, :], identity=C.ident[:]),
                      reads=[pe, C.ident], writes=[ps[dc // 4]])
            for j in range(4):
                kb.op("dve", lambda e: e.tensor_tensor(out=xt[:, j * 512:(j + 1) * 512], in0=ps[j][:], in1=xt[:, j * 512:(j + 1) * 512], op=ALU.add),
                      reads=[ps[j], xt], writes=[xt])
            kb.op("act", lambda e: e.activation(out=junk[:], in_=xt[:], func=AF.Square, accum_out=ss[:, tt:tt + 1]), reads=[xt], writes=[junk, ss])
            kb.op("dve", lambda e: e.tensor_scalar(out=rs[:, tt:tt + 1], in0=ss[:, tt:tt + 1], scalar1=1.0 / 2048, scalar2=EPS,
                                                   op0=ALU.mult, op1=ALU.add), reads=[ss], writes=[rs])
            kb.op("act", lambda e: e.activation(out=rs[:, tt:tt + 1], in_=rs[:, tt:tt + 1], func=AF.Sqrt), reads=[rs], writes=[rs])
            kb.op("dve", lambda e: e.reciprocal(out=rs[:, tt:tt + 1], in_=rs[:, tt:tt + 1]), reads=[rs], writes=[rs])
            kb.op("dve", lambda e: e.scalar_tensor_tensor(out=xt[:], in0=xt[:], scalar=rs[:, tt:tt + 1], in1=wb[:], op0=ALU.mult, op1=ALU.mult),
                  reads=[xt, rs, wb], writes=[xt])
            kb.dma("sp", D.out.ap()[tt * 128:(tt + 1) * 128, :], xt[:], reads=[xt], writes=[D.out])
        kb.barrier()


STOP = 0
FAST_F32 = True
F32R = mybir.dt.float32r
def fr(ap):
    return ap.bitcast(F32R) if FAST_F32 else ap
class StopBuild(Exception):
    pass
def chk(n):
    if STOP == n:
        raise StopBuild()
GN_EPS = 64e-5
NEG_E05 = -0.6065306597126334


def phase_Rpre(kb, C, D):
    with contextlib.ExitStack() as st:
        rw = kb.sb("rp_rw", [128, 8, 8], F32, st)
        kb.dma("sp", rw[:], D.rw_c.ap(), writes=[rw])
        tmp = kb.sb("rp_tmp", [128, 2048], F32, st)
        lin = [kb.sb(f"rp_lin{i}", [128, 2048], BF16, st) for i in range(4)]
        for i in range(4):
            kb.op("pool", lambda e: e.memset(lin[i][:], 0.0), writes=[lin[i]])
            r0 = 3072 + i * 96
            kb.dma("sp", tmp[0:96, :], D.zs_d.ap()[r0:r0 + 96, :], reads=[D.zs_d], writes=[tmp])
            if i < 2:
                kb.op("act", lambda e: e.activation(out=lin[i][0:96, :], in_=tmp[0:96, :], func=AF.Tanh), reads=[tmp], writes=[lin[i]])
            else:
                kb.op("act", lambda e: e.copy(out=lin[i][0:96, :], in_=tmp[0:96, :]), reads=[tmp], writes=[lin[i]])
        sgl = kb.sb("rp_sgl", [128, 2, 2048], BF16, st)
        for kc in range(2):
            kb.dma("sp", tmp[:], D.zs_d.ap()[3456 + kc * 128:3456 + (kc + 1) * 128, :], reads=[D.zs_d], writes=[tmp])
            kb.op("act", lambda e: e.activation(out=sgl[:, kc, :], in_=tmp[:], func=AF.Sigmoid), reads=[tmp], writes=[sgl])
        wst = kb.sb("rp_wst", [128, 2048], F32, st)
        w2b = kb.sb("rp_w2b", [128, 2, 1024], BF16, st)
        a2b = kb.sb("rp_a2b", [128, 2, 1024], BF16, st)
        g2b = kb.sb("rp_g2b", [128, 2, 1024], BF16, st)
        wv = wst[:].rearrange("p (a c) -> p a c", a=2)
        kb.op("pool", lambda e: e.memset(w2b[:], 0.0), writes=[w2b])
        kb.op("pool", lambda e: e.memset(a2b[:], 0.0), writes=[a2b])
        kb.dma("sp", wv[0:96], D.w2.ap().rearrange("d l c -> l d c"), writes=[wst])
        kb.op("pool", lambda e: e.tensor_copy(out=w2b[0:96], in_=wv[0:96]), reads=[wst], writes=[w2b])
        kb.dma("sp", wv[0:96], D.a2.ap().rearrange("d l c -> l d c"), writes=[wst])
        kb.op("pool", lambda e: e.tensor_copy(out=a2b[0:96], in_=wv[0:96]), reads=[wst], writes=[a2b])
        kb.dma("sp", wv, D.g2.ap().rearrange("(kc p) c -> p kc c", p=128), writes=[wst])
        kb.op("pool", lambda e: e.tensor_copy(out=g2b[:], in_=wv), reads=[wst], writes=[g2b])
        outs = [kb.sb(f"rp_o{i}", [128, 2048], F32, st) for i in range(2)]
        pss = [[kb.ps(f"rp_ps{i}_{j}", [128, 512], F32, st) for j in range(4)] for i in range(2)]
        it = 0
        for cc in range(8):
            for d in range(2):
                for which in range(2):
                    ps = pss[it % 2]
                    o = outs[it % 2]
                    it += 1
                    wmat = w2b if which == 0 else a2b
                    xin = lin[d] if which == 0 else lin[2 + d]
                    bias = rw[:, cc, d:d + 1] if which == 0 else rw[:, cc, 2 + d:3 + d]
                    for tb in range(4):
                        kb.op("pe", lambda e: e.matmul(ps[tb][:], lhsT=wmat[:, d, cc * 128:(cc + 1) * 128], rhs=xin[:, tb * 512:(tb + 1) * 512],
                                                       start=True, stop=True), reads=[wmat, xin], writes=[ps[tb]])
                        kb.op("act", lambda e: e.activation(out=o[:, tb * 512:(tb + 1) * 512], in_=ps[tb][:], func=AF.Sigmoid, bias=bias),
                              reads=[ps[tb], rw], writes=[o])
                    if which == 0:
                        kb.op("pool", lambda e: e.tensor_scalar(out=o[:], in0=o[:], scalar1=NEG_E05, scalar2=None, op0=ALU.mult), reads=[o], writes=[o])
                        kb.dma("sp", D.ld_d.ap()[d, cc * 128:(cc + 1) * 128, :], o[:], reads=[o], writes=[D.ld_d])
                    else:
                        kb.dma("sp", D.a_d.ap()[d, cc * 128:(cc + 1) * 128, :], o[:], reads=[o], writes=[D.a_d])
        for tt in range(NT):
            ps = pss[tt % 2]
            o = outs[tt % 2]
            for hf in range(2):
                for kc in range(2):
                    kb.op("pe", lambda e: e.matmul(ps[hf][:], lhsT=sgl[:, kc, tt * 128:(tt + 1) * 128], rhs=g2b[:, kc, hf * 512:(hf + 1) * 512],
                                                   start=(kc == 0), stop=(kc == 1)), reads=[sgl, g2b], writes=[ps[hf]])
                kb.op("act", lambda e: e.copy(out=o[:, hf * 512:(hf + 1) * 512], in_=ps[hf][:]), reads=[ps[hf]], writes=[o])
            kb.dma("sp", D.g_d.ap()[tt * 128:(tt + 1) * 128, :], o[:, 0:1024], reads=[o], writes=[D.g_d])
        kb.barrier()


def phase_R(kb, C, D, ccs=range(8), dbg=None):
    with contextlib.ExitStack() as st:
        rw = kb.sb("r_rw", [128, 8, 8], F32, st)
        kb.dma("sp", rw[:], D.rw_c.ap(), writes=[rw])
        masks = kb.sb("r_masks", [128, 6, 128], F32, st)
        kb.dma("sp", masks[:], D.masks.ap(), writes=[masks])
        cst = kb.sb("r_cst", [128, 66 + 2048], F32, st)
        kb.dma("sp", cst[:], D.rcst.ap(), writes=[cst])
        ident2 = cst[:, 0:64]
        sel = cst[:, 64:66]
        segm = cst[:, 66:66 + 2048]
        lnw = kb.sb("r_lnw", [128, 1024], F32, st)
        lnb = kb.sb("r_lnb", [128, 1024], F32, st)
        kb.dma("pool", lnw[:], D.lnx_w.ap().partition_broadcast(128), writes=[lnw])
        kb.dma("pool", lnb[:], D.lnx_b.ap().partition_broadcast(128), writes=[lnb])
        Rr = kb.sb("r_R", [128, 2048], F32, st)
        Kk = kb.sb("r_K", [128, 2048], F32, st)
        Vv = kb.sb("r_V", [128, 2048], F32, st)
        KKn = kb.sb("r_KK", [128, 2048], F32, st)
        KS = kb.sb("r_KS", [128, 2048], F32, st)
        E1 = kb.sb("r_E1", [128, 2048], F32, st)
        XI = kb.sb("r_XI", [128, 2048], F32, st)
        XE = kb.sb("r_XE", [128, 2048], F32, st)
        Aa = kb.sb("r_A", [128, 2048], F32, st)
        KD = kb.sb("r_KD", [128, 2048], F32, st)
        KT = kb.sb("r_KT", [128, 2048], F32, st)
        AT = kb.sb("r_AT", [128, 2048], F32, st)
        BTb = kb.sb("r_BTb", [128, 2048], F32, st)
        stt = kb.sb("r_stt", [128, 96], F32, st)
        Vtm = kb.sb("r_Vtm", [128, NT, 128], F32, st)
        Ysum = kb.sb("r_Ysum", [128, NT, 128], F32, st)
        MTa = kb.sb("r_MTa", [128, 32, 64], F32, st)
        Ca = kb.sb("r_Ca", [128, 32, 64], F32, st)
        H = [kb.sb(f"r_H{i}", [128, 64], F32, st) for i in range(2)]
        tot = kb.sb("r_tot", [128, 32], F32, st)
        GL = kb.sb("r_GL", [128, 32], F32, st)
        XA = kb.sb("r_XA", [128, 2, 2, 128], F32, st)
        KBm = kb.sb("r_KBm", [128, 2, 2, 128], F32, st)
        PQ = [kb.sb(f"r_PQ{i}", [128, 2, 2, 128], F32, st) for i in range(2)]
        QT = [kb.sb(f"r_QT{i}", [128, 2, 128], F32, st) for i in range(2)]
        BW = kb.sb("r_BW", [128, 2, 128], F32, st)
        BU = kb.sb("r_BU", [128, 2, 128], F32, st)
        AGtm = kb.sb("r_AGtm", [128, 128], F32, st)
        KGtm = kb.sb("r_KGtm", [128, 128], F32, st)
        psA = kb.ps("r_psA", [128, 2, 2, 128], F32, st)
        psB = kb.ps("r_psB", [128, 2, 2, 128], F32, st)
        psC = kb.ps("r_psC", [128, 2, 128], F32, st)
        psN = kb.ps("r_psN", [128, 2, 2, 128], F32, st)
        b5 = kb.ps("r_b5", [128, 512], F32, st)
        b6 = kb.ps("r_b6", [128, 512], F32, st)
        b7 = kb.ps("r_b7", [128, 512], F32, st)
        b8 = kb.ps("r_b8", [128, 512], F32, st)
        psW = kb.view(b5, "psW"); psBt = kb.view(b5, "psBt"); psU = kb.view(b5, "psU")
        psY = kb.view(b6, "psY"); psR = kb.view(b6, "psR"); psAG = kb.view(b6, "psAG"); psKG = kb.view(b6, "psKG")
        psM = kb.view(b7, "psM"); psCc = kb.view(b7, "psCc"); psYs = kb.view(b7, "psYs"); psH = kb.view(b7, "psH")
        psX = b8
        W_ = lambda: psW[:, 0:128].rearrange("p (h i) -> p h i", h=2)
        Bt_ = lambda: psBt[:, 128:256]
        U_ = lambda: psU[:, 256:512].rearrange("p (h i) -> p h i", h=2)
        Y_ = lambda: psY[:, 0:128].rearrange("p (h i) -> p h i", h=2)
        R_ = lambda: psR[:, 128:256]
        AG_ = lambda: psAG[:, 256:384]
        KG_ = lambda: psKG[:, 384:512]
        M_ = lambda: psM[:, 0:128].rearrange("p (n j) -> p n j", n=2)
        Cc_ = lambda: psCc[:, 128:256].rearrange("p (n j) -> p n j", n=2)
        Ys_ = lambda: psYs[:, 256:384]
        H_ = lambda: psH[:, 384:448]

        def vt(eng_i, fn, reads, writes):
            kb.op("dve" if eng_i == 0 else "pool", fn, reads=reads, writes=writes)

        for cc in ccs:
            rows = slice(cc * 128, (cc + 1) * 128)
            kb.dma("sp", Rr[:], D.zs_d.ap()[cc * 128:(cc + 1) * 128, :], reads=[D.zs_d], writes=[Rr])
            kb.dma("act", Kk[:], D.zs_d.ap()[1024 + cc * 128:1024 + (cc + 1) * 128, :], reads=[D.zs_d], writes=[Kk])
            kb.dma("sp", Vv[:], D.zs_d.ap()[2048 + cc * 128:2048 + (cc + 1) * 128, :], reads=[D.zs_d], writes=[Vv])
            for g in range(4):
                for j in range(4):
                    tt = g * 4 + j
                    kb.op("pe", lambda e: e.transpose(out=psX[:, j * 128:(j + 1) * 128], in_=Vv[:, tt * 128:(tt + 1) * 128], identity=C.ident[:]),
                          reads=[Vv, C.ident], writes=[psX])
                kb.op("act", lambda e: e.copy(out=fr(Vtm[:, g * 4:(g + 1) * 4, :]), in_=psX[:].rearrange("p (j c) -> p j c", j=4)), reads=[psX], writes=[Vtm])
            kb.op("pool", lambda e: e.tensor_scalar(out=KKn[:], in0=Kk[:], scalar1=rw[:, cc, 4:5], scalar2=None, op0=ALU.mult),
                  reads=[Kk, rw], writes=[KKn])
            kb.op("act", lambda e: e.activation(out=XI[:], in_=KKn[:], func=AF.Square), reads=[KKn], writes=[XI])
            for tb in range(4):
                kb.op("pe", lambda e: e.matmul(psX[:], lhsT=masks[:, 5, :], rhs=XI[:, tb * 512:(tb + 1) * 512], start=True, stop=True),
                      reads=[masks, XI], writes=[psX])
                kb.op("act", lambda e: e.activation(out=XE[:, tb * 512:(tb + 1) * 512], in_=psX[:], func=AF.Sqrt), reads=[psX], writes=[XE])
            kb.op("dve", lambda e: e.tensor_scalar(out=XE[:], in0=XE[:], scalar1=1e-12, scalar2=None, op0=ALU.max), reads=[XE], writes=[XE])
            kb.op("dve", lambda e: e.reciprocal(out=XE[:], in_=XE[:]), reads=[XE], writes=[XE])
            kb.op("pool", lambda e: e.tensor_tensor(out=KKn[:], in0=KKn[:], in1=XE[:], op=ALU.mult), reads=[KKn, XE], writes=[KKn])
            chk(1)
            for d in range(2):
                M2 = masks[:, 2 * d:2 * d + 2, :]
                MST = masks[:, 2 - 2 * d, :]
                kb.dma("sp", XE[:], D.ld_d.ap()[d, cc * 128:(cc + 1) * 128, :], reads=[D.ld_d], writes=[XE])
                kb.dma("act", Aa[:], D.a_d.ap()[d, cc * 128:(cc + 1) * 128, :], reads=[D.a_d], writes=[Aa])
                kb.op("dve", lambda e: e.tensor_scalar(out=KD[:], in0=Aa[:], scalar1=-1.0, scalar2=rw[:, cc, 5:6], op0=ALU.add, op1=ALU.mult),
                      reads=[Aa, rw], writes=[KD])
                kb.op("dve", lambda e: e.scalar_tensor_tensor(out=KD[:], in0=KD[:], scalar=1.0, in1=Kk[:], op0=ALU.add, op1=ALU.mult),
                      reads=[KD, Kk], writes=[KD])
                if d == 0:
                    kb.op("pool", lambda e: e.tensor_copy(out=KS[:], in_=KD[:]), reads=[KD], writes=[KS])
                else:
                    kb.op("pool", lambda e: e.tensor_tensor(out=KS[:], in0=KS[:], in1=KD[:], op=ALU.add), reads=[KS, KD], writes=[KS])
                kb.op("pool", lambda e: e.tensor_tensor(out=Aa[:], in0=Aa[:], in1=KKn[:], op=ALU.mult), reads=[Aa, KKn], writes=[Aa])
                kb.op("dve", lambda e: e.tensor_tensor_scan(out=XI[:], data0=segm, data1=XE[:], initial=0.0, op0=ALU.mult, op1=ALU.add),
                      reads=[cst, XE], writes=[XI])
                kb.op("pool", lambda e: e.tensor_copy(out=tot[:], in_=XI[:].rearrange("p (n s) -> p n s", s=64)[:, :, 63]), reads=[XI], writes=[tot])
                kb.op("act", lambda e: e.activation(out=GL[:], in_=tot[:], func=AF.Exp), reads=[tot], writes=[GL])
                if d == 0:
                    kb.op("pool", lambda e: e.tensor_tensor(out=XE[:], in0=XI[:], in1=XE[:], op=ALU.subtract), reads=[XI, XE], writes=[XE])
                else:
                    kb.op("dve", lambda e: e.tensor_tensor(out=XI[:].rearrange("p (n s) -> p n s", s=64),
                                                           in0=tot[:, :, None].to_broadcast([128, 32, 64]),
                                                           in1=XI[:].rearrange("p (n s) -> p n s", s=64), op=ALU.subtract),
                          reads=[tot, XI], writes=[XI])
                    kb.op("pool", lambda e: e.tensor_tensor(out=XE[:], in0=XI[:], in1=XE[:], op=ALU.add), reads=[XI, XE], writes=[XE])
                cI, cE = (XI, XE) if d == 0 else (XE, XI)
                kb.op("act", lambda e: e.activation(out=fr(E1[:]), in_=cI[:], func=AF.Exp), reads=[cI], writes=[E1])
                kb.op("act", lambda e: e.activation(out=cI[:], in_=cI[:], func=AF.Exp, scale=-1.0), reads=[cI], writes=[cI])
                kb.op("act", lambda e: e.activation(out=cE[:], in_=cE[:], func=AF.Exp), reads=[cE], writes=[cE])
                kb.op("dve", lambda e: e.tensor_tensor(out=fr(E1[:]), in0=E1[:], in1=Rr[:], op=ALU.mult), reads=[E1, Rr], writes=[E1])
                kb.op("pool", lambda e: e.tensor_tensor(out=fr(KT[:]), in0=KD[:], in1=cI[:], op=ALU.mult), reads=[KD, cI], writes=[KT])
                kb.op("dve", lambda e: e.tensor_tensor(out=fr(AT[:]), in0=Aa[:], in1=cI[:], op=ALU.mult), reads=[Aa, cI], writes=[AT])
                kb.op("dve", lambda e: e.scalar_tensor_tensor(out=fr(BTb[:]), in0=cE[:], scalar=-1.0, in1=KKn[:], op0=ALU.mult, op1=ALU.mult),
                      reads=[cE, KKn], writes=[BTb])
                kb.op("pool", lambda e: e.tensor_tensor(out=cI[:].rearrange("p (n s) -> p n s", s=64), in0=cI[:].rearrange("p (n s) -> p n s", s=64),
                                                        in1=GL[:, :, None].to_broadcast([128, 32, 64]), op=ALU.mult), reads=[cI, GL], writes=[cI])
                kb.op("dve", lambda e: e.tensor_tensor(out=KD[:], in0=KD[:], in1=cI[:], op=ALU.mult), reads=[KD, cI], writes=[KD])
                kb.op("pool", lambda e: e.tensor_tensor(out=Aa[:], in0=Aa[:], in1=cI[:], op=ALU.mult), reads=[Aa, cI], writes=[Aa])
                RT, BT = E1, BTb
                chk(2)
                for tt in range(NT):
                    cols = slice(tt * 128, (tt + 1) * 128)
                    for hh in range(2):
                        pr = slice(hh * 64, hh * 64 + 64)
                        kb.pe_fence()
                        kb.op("pe", lambda e: e.matmul(psA[:, hh, 0, :], lhsT=fr(AT[pr, cols]), rhs=fr(BT[pr, cols]), start=True, stop=True),
                              reads=[AT, BT], writes=[psA])
                        kb.op("pe", lambda e: e.matmul(psA[:, hh, 1, :], lhsT=fr(AT[pr, cols]), rhs=fr(RT[pr, cols]), start=True, stop=True),
                              reads=[AT, RT], writes=[psA])
                        kb.op("pe", lambda e: e.matmul(psB[:, hh, 0, :], lhsT=fr(KT[pr, cols]), rhs=fr(BT[pr, cols]), start=True, stop=True),
                              reads=[KT, BT], writes=[psB])
                        kb.op("pe", lambda e: e.matmul(psB[:, hh, 1, :], lhsT=fr(KT[pr, cols]), rhs=fr(RT[pr, cols]), start=True, stop=True),
                              reads=[KT, RT], writes=[psB])
                        kb.op("pe", lambda e: e.matmul(psC[:, hh, :], lhsT=fr(BT[pr, cols]), rhs=fr(AT[pr, cols]), start=True, stop=True),
                              reads=[AT, BT], writes=[psC])
                    M2b = M2[:, None, :, :].to_broadcast([128, 2, 2, 128])
                    kb.op("dve", lambda e: e.tensor_tensor(out=fr(XA[:]), in0=psA[:], in1=M2b, op=ALU.mult), reads=[psA, masks], writes=[XA])
                    kb.op("dve", lambda e: e.tensor_tensor(out=fr(KBm[:]), in0=psB[:], in1=M2b, op=ALU.mult), reads=[psB, masks], writes=[KBm])
                    q0, q1 = QT[0], QT[1]
                    kb.op("dve", lambda e: e.tensor_tensor(out=fr(q0[:]), in0=psC[:], in1=MST[:, None, :].to_broadcast([128, 2, 128]), op=ALU.mult),
                          reads=[psC, masks], writes=[q0])
                    chk(3)
                    pq = PQ[0]
                    kb.op("pool", lambda e: e.tensor_tensor(out=fr(pq[:, :, 0, :]), in0=XA[:, :, 0, :], in1=C.ident[:, None, :].to_broadcast([128, 2, 128]),
                                                            op=ALU.add), reads=[XA, C.ident], writes=[pq])
                    for hh in range(2):
                        kb.op("pe", lambda e: e.matmul(psN[:, hh, 1, :], lhsT=fr(q0[:, hh, :]), rhs=fr(XA[:, hh, 0, :]), start=True, stop=True),
                              reads=[q0, XA], writes=[psN])
                        kb.op("pe", lambda e: e.matmul(psC[:, hh, :], lhsT=fr(XA[:, hh, 0, :]), rhs=fr(q0[:, hh, :]), start=True, stop=True),
                              reads=[q0, XA], writes=[psC])
                    kb.op("act", lambda e: e.copy(out=fr(pq[:, :, 1, :]), in_=psN[:, :, 1, :]), reads=[psN], writes=[pq])
                    kb.op("act", lambda e: e.copy(out=fr(q1[:]), in_=psC[:]), reads=[psC], writes=[q1])
                    cur = 0
                    qcur = 1
                    for lev in range(1, 6):
                        pq = PQ[cur]
                        pqn = PQ[1 - cur]
                        qt = QT[qcur]
                        qtn = QT[1 - qcur]
                        last = (lev == 5)
                        for hh in range(2):
                            if last:
                                kb.op("pe", lambda e: e.matmul(psN[:, hh, 0, :], lhsT=fr(qt[:, hh, :]), rhs=fr(pq[:, hh, 0, :]), start=True, stop=True),
                                      reads=[qt, pq], writes=[psN])
                            else:
                                kb.op("pe", lambda e: e.matmul(psN[:, hh, :, :], lhsT=fr(qt[:, hh, :]), rhs=fr(pq[:, hh, :, :]), start=True, stop=True),
                                      reads=[qt, pq], writes=[psN])
                                kb.op("pe", lambda e: e.matmul(psC[:, hh, :], lhsT=fr(pq[:, hh, 1, :]), rhs=fr(qt[:, hh, :]), start=True, stop=True),
                                      reads=[qt, pq], writes=[psC])
                        kb.op("dve", lambda e: e.tensor_tensor(out=fr(pqn[:, :, 0, :]), in0=psN[:, :, 0, :], in1=pq[:, :, 0, :], op=ALU.add),
                              reads=[psN, pq], writes=[pqn])
                        if not last:
                            kb.op("act", lambda e: e.copy(out=fr(pqn[:, :, 1, :]), in_=psN[:, :, 1, :]), reads=[psN], writes=[pqn])
                            kb.op("act", lambda e: e.copy(out=fr(qtn[:]), in_=psC[:]), reads=[psC], writes=[qtn])
                        cur = 1 - cur
                        qcur = 1 - qcur
                    TT = PQ[cur]
                    chk(4)
                    for hh in range(2):
                        kb.op("pe", lambda e: e.matmul(W_()[:, hh, :], lhsT=fr(KBm[:, hh, 0, :]), rhs=fr(Vtm[:, tt, hh * 64:(hh + 1) * 64]), start=True, stop=True),
                              reads=[KBm, Vtm], writes=[psW])
                    kb.op("pe", lambda e: e.transpose(out=Bt_(), in_=BT[:, cols], identity=C.ident[:]), reads=[BT, C.ident], writes=[psBt])
                    kb.op("act", lambda e: e.copy(out=fr(BW[:, :, 64:128]), in_=W_()), reads=[psW], writes=[BW])
                    kb.op("dve", lambda e: e.tensor_copy(out=fr(BW[:, :, 0:64]), in_=Bt_().rearrange("p (h j) -> p h j", h=2)), reads=[psBt], writes=[BW])
                    for hh in range(2):
                        kb.op("pe", lambda e: e.matmul(U_()[:, hh, :], lhsT=fr(TT[:, hh, 0, :]), rhs=fr(BW[:, hh, :]), start=True, stop=True),
                              reads=[TT, BW], writes=[psU])
                    kb.op("act", lambda e: e.copy(out=fr(BU[:]), in_=U_()), reads=[psU], writes=[BU])
                    chk(5)
                    for hh in range(2):
                        kb.op("pe", lambda e: e.matmul(Y_()[:, hh, :], lhsT=fr(XA[:, hh, 1, :]), rhs=fr(BU[:, hh, 64:128]), start=True, stop=False),
                              reads=[XA, BU], writes=[psY])
                        kb.op("pe", lambda e: e.matmul(Y_()[:, hh, :], lhsT=fr(KBm[:, hh, 1, :]), rhs=fr(Vtm[:, tt, hh * 64:(hh + 1) * 64]), start=False, stop=True),
                              reads=[KBm, Vtm], writes=[psY])
                    if d == 0:
                        kb.op("act", lambda e: e.copy(out=Ysum[:, tt, :], in_=psY[:, 0:128]), reads=[psY], writes=[Ysum])
                    else:
                        kb.op("dve", lambda e: e.tensor_tensor(out=Ysum[:, tt, :], in0=psY[:, 0:128], in1=Ysum[:, tt, :], op=ALU.add),
                              reads=[psY, Ysum], writes=[Ysum])
                    for hh in range(2):
                        kb.op("pe", lambda e: e.matmul(R_()[hh * 64:(hh + 1) * 64, :], lhsT=BU[:, hh, 0:64], rhs=XA[:, hh, 1, :], start=True, stop=True),
                              reads=[BU, XA], writes=[psR])
                    kb.op("dve", lambda e: e.tensor_tensor(out=fr(RT[:, cols]), in0=R_(), in1=RT[:, cols], op=ALU.add), reads=[psR, RT], writes=[RT])
                    chk(6)
                    kb.op("pe", lambda e: e.transpose(out=AG_(), in_=Aa[:, cols], identity=C.ident[:]), reads=[Aa, C.ident], writes=[psAG])
                    kb.op("pe", lambda e: e.transpose(out=KG_(), in_=KD[:, cols], identity=C.ident[:]), reads=[KD, C.ident], writes=[psKG])
                    kb.op("act", lambda e: e.copy(out=AGtm[:], in_=AG_()), reads=[psAG], writes=[AGtm])
                    kb.op("act", lambda e: e.copy(out=KGtm[:], in_=KG_()), reads=[psKG], writes=[KGtm])
                    for n in range(2):
                        tr = slice(n * 64, n * 64 + 64)
                        bk = (b7, b8)[n]
                        for hh in range(2):
                            pr = slice(hh * 64, hh * 64 + 64)
                            kb.op("pe", lambda e: e.matmul(bk[pr, 0:64], lhsT=BU[tr, hh, 0:64], rhs=AGtm[tr, pr], start=True, stop=True),
                                  reads=[BU, AGtm], writes=[bk])
                            kb.op("pe", lambda e: e.matmul(bk[pr, 64:128], lhsT=AGtm[tr, pr], rhs=BU[tr, hh, 64:128], start=True, stop=False),
                                  reads=[BU, AGtm], writes=[bk])
                            kb.op("pe", lambda e: e.matmul(bk[pr, 64:128], lhsT=KGtm[tr, pr], rhs=Vtm[tr, tt, pr], start=False, stop=True),
                                  reads=[KGtm, Vtm], writes=[bk])
                    for n in range(2):
                        ch = tt * 2 + n
                        bk = (b7, b8)[n]
                        kb.op("dve", lambda e: e.scalar_tensor_tensor(out=MTa[:, ch, :], in0=ident2, scalar=GL[:, ch:ch + 1], in1=bk[:, 0:64],
                                                                      op0=ALU.mult, op1=ALU.add), reads=[cst, GL, bk], writes=[MTa])
                        kb.op("act", lambda e: e.copy(out=Ca[:, ch, :], in_=bk[:, 64:128]), reads=[bk], writes=[Ca])
                    chk(7)
                chk(8)
                kb.op("pool", lambda e: e.memset(H[0][:], 0.0), writes=[H[0]])
                order = range(32) if d == 0 else range(31, -1, -1)
                hc = 0
                for ch in order:
                    tt, n = ch // 2, ch % 2
                    ccols = slice(ch * 64, ch * 64 + 64)
                    Hc, Hn = H[hc], H[1 - hc]
                    tr = slice(n * 64, n * 64 + 64)
                    for hh in range(2):
                        pr = slice(hh * 64, hh * 64 + 64)
                        bk = (b7, b8)[hh]
                        kb.op("pe", lambda e: e.matmul(bk[tr, 0:64], lhsT=RT[pr, ccols], rhs=Hc[pr, :], start=True, stop=True),
                              reads=[RT, Hc], writes=[bk])
                        kb.op("pe", lambda e: e.matmul(bk[pr, 64:128], lhsT=MTa[pr, ch, :], rhs=Hc[pr, :], start=True, stop=True),
                              reads=[MTa, Hc], writes=[bk])
                    for hh in range(2):
                        pr = slice(hh * 64, hh * 64 + 64)
                        bk = (b7, b8)[hh]
                        kb.op("dve", lambda e: e.tensor_tensor(out=Hn[pr, :], in0=bk[pr, 64:128], in1=Ca[pr, ch, :], op=ALU.add),
                              reads=[bk, Ca], writes=[Hn])
                        kb.op("dve", lambda e: e.tensor_tensor(out=Ysum[tr, tt, pr], in0=bk[tr, 0:64], in1=Ysum[tr, tt, pr], op=ALU.add),
                              reads=[bk, Ysum], writes=[Ysum])
                    hc = 1 - hc
                chk(9)
            chk(10)
            if dbg is not None and "Ysum" in dbg:
                kb.dma("sp", D.dbgY.ap()[:, cc * 128:(cc + 1) * 128].rearrange("(tt p) c -> p tt c", p=128), Ysum[:], reads=[Ysum], writes=[D.dbgY])
            Y3 = Ysum[:].rearrange("p t (h i) -> p (t h) i", h=2)
            st_mu = stt[:, 0:32]
            st_var = stt[:, 32:64]
            kb.op("dve", lambda e: e.tensor_reduce(out=st_mu, in_=Y3, axis=AX.X, op=ALU.add), reads=[Ysum], writes=[stt])
            kb.op("dve", lambda e: e.tensor_scalar(out=st_mu, in0=st_mu, scalar1=1.0 / 64, scalar2=None, op0=ALU.mult), reads=[stt], writes=[stt])
            kb.op("dve", lambda e: e.tensor_tensor(out=Y3, in0=Y3, in1=st_mu[:, :, None].to_broadcast([128, 32, 64]), op=ALU.subtract),
                  reads=[Ysum, stt], writes=[Ysum])
            sqv = XI[:].rearrange("p (a i) -> p a i", i=64)
            kb.op("act", lambda e: e.activation(out=sqv, in_=Y3, func=AF.Square), reads=[Ysum], writes=[XI])
            kb.op("dve", lambda e: e.tensor_reduce(out=st_var, in_=sqv, axis=AX.X, op=ALU.add), reads=[XI], writes=[stt])
            kb.op("dve", lambda e: e.tensor_scalar(out=st_var, in0=st_var, scalar1=1.0 / 64, scalar2=GN_EPS, op0=ALU.mult, op1=ALU.add),
                  reads=[stt], writes=[stt])
            kb.op("act", lambda e: e.activation(out=st_var, in_=st_var, func=AF.Sqrt), reads=[stt], writes=[stt])
            kb.op("dve", lambda e: e.reciprocal(out=st_var, in_=st_var), reads=[stt], writes=[stt])
            kb.op("dve", lambda e: e.tensor_tensor(out=Y3, in0=Y3, in1=st_var[:, :, None].to_broadcast([128, 32, 64]), op=ALU.mult),
                  reads=[Ysum, stt], writes=[Ysum])
            kb.op("pool", lambda e: e.tensor_tensor(out=Ysum[:], in0=Ysum[:], in1=lnw[:, None, rows].to_broadcast([128, NT, 128]), op=ALU.mult),
                  reads=[Ysum, lnw], writes=[Ysum])
            kb.op("pool", lambda e: e.tensor_tensor(out=Ysum[:], in0=Ysum[:], in1=lnb[:, None, rows].to_broadcast([128, NT, 128]), op=ALU.add),
                  reads=[Ysum, lnb], writes=[Ysum])
            kb.op("dve", lambda e: e.scalar_tensor_tensor(out=KS[:], in0=KS[:], scalar=rw[:, cc, 6:7], in1=Rr[:], op0=ALU.mult, op1=ALU.mult),
                  reads=[KS, rw, Rr], writes=[KS])
            bon = stt[:, 64:96]
            for tt in range(NT):
                kb.op("pe", lambda e: e.matmul(psX[:, tt * 2:tt * 2 + 2], lhsT=KS[:, tt * 128:(tt + 1) * 128], rhs=sel, start=True, stop=True),
                      reads=[KS, cst], writes=[psX])
            kb.op("act", lambda e: e.activation(out=bon, in_=psX[:, 0:32], func=AF.Copy, scale=0.5), reads=[psX], writes=[stt])
            V3 = Vtm[:].rearrange("p t (h i) -> p (t h) i", h=2)
            kb.op("pool", lambda e: e.tensor_tensor(out=sqv, in0=V3, in1=bon[:, :, None].to_broadcast([128, 32, 64]), op=ALU.mult),
                  reads=[Vtm, stt], writes=[XI])
            kb.op("pool", lambda e: e.tensor_tensor(out=Y3, in0=Y3, in1=sqv, op=ALU.add), reads=[Ysum, XI], writes=[Ysum])
            gt = XE[:].rearrange("p (t c) -> p t c", c=128)
            kb.dma("sp", gt, D.g_d.ap()[:, cc * 128:(cc + 1) * 128].rearrange("(tt p) c -> p tt c", p=128), reads=[D.g_d], writes=[XE])
            ybf = KT[:].rearrange("p (t c) -> p t c", c=128).bitcast(BF16) if False else None
            kb.op("dve", lambda e: e.tensor_tensor(out=Ysum[:], in0=Ysum[:], in1=gt, op=ALU.mult), reads=[Ysum, XE], writes=[Ysum])
            if dbg is not None and "yfin" in dbg:
                kb.dma("sp", D.dbgF.ap()[:, cc * 128:(cc + 1) * 128].rearrange("(tt p) c -> p tt c", p=128), Ysum[:], reads=[Ysum], writes=[D.dbgF])
            yo = AT[:]
            for g in range(4):
                for j in range(4):
                    tt = g * 4 + j
                    kb.op("pe", lambda e: e.transpose(out=psX[:, j * 128:(j + 1) * 128], in_=Ysum[:, tt, :], identity=C.ident[:]),
                          reads=[Ysum, C.ident], writes=[psX])
                kb.op("act", lambda e: e.copy(out=KD[:, g * 512:(g + 1) * 512], in_=psX[:]), reads=[psX], writes=[KD])
            ybf = kb.view(BW, "ybf")
            yb = Aa[:].bitcast(BF16)[:, 0:2048]
            kb.op("dve", lambda e: e.tensor_copy(out=yb, in_=KD[:]), reads=[KD], writes=[Aa])
            kb.dma("sp", D.ymT_d.ap()[cc], yb, reads=[Aa], writes=[D.ymT_d])
        kb.barrier()


def host_consts():
    S = 2048
    rows = np.repeat(np.arange(32), 64).astype(np.float32)
    cols = np.tile(np.arange(64), 32).astype(np.float32)
    inv = (10000.0 ** (-np.arange(0, 32, 2, dtype=np.float32) / 32)).astype(np.float32)
    ar = rows[:, None] * inv[None]
    ac = cols[:, None] * inv[None]
    tabC = np.concatenate([np.cos(ar), np.cos(ac)], 1).astype(np.float32)
    tabS = np.concatenate([np.sin(ar), np.sin(ac)], 1).astype(np.float32)
    ident = np.eye(128, dtype=np.float32)
    iota = np.tile(np.arange(128, dtype=np.float32)[None], (128, 1))
    r = np.arange(128)[:, None]
    s = np.arange(128)[None, :]
    same = (r // 64) == (s // 64)
    masks = np.zeros((128, 6, 128), np.float32)
    masks[:, 0] = same & (r < s)
    masks[:, 1] = same & (r <= s)
    masks[:, 2] = same & (r > s)
    masks[:, 3] = same & (r >= s)
    masks[:, 4] = same & (r > s)
    masks[:, 5] = same
    rcst = np.zeros((128, 66 + 2048), np.float32)
    pp = np.arange(128)
    rcst[pp, pp % 64] = 1.0
    rcst[:, 64] = (pp // 64 == 0)
    rcst[:, 65] = (pp // 64 == 1)
    seg = np.ones(2048, np.float32); seg[::64] = 0.0
    rcst[:, 66:] = seg[None]
    return dict(tabC=tabC, tabS=tabS, ident=ident, iota=iota, masks=masks, rcst=rcst)

def prep_shared(inp):
    L = 0
    d = host_consts()
    mu_c = np.zeros((128, 3 * NRCH), np.float32)
    for ci, (c0, cs) in enumerate(RCH):
        mu_c[:cs, ci] = inp["mu_prev"][L, c0:c0 + cs]
        mu_c[:cs, NRCH + ci] = inp["mu_next"][L, c0:c0 + cs]
    d["mu_c"] = mu_c
    d["w_in"] = np.ascontiguousarray(inp["w_in"][L])
    for k in ["norm1_w", "norm2_w", "q_norm_w", "k_norm_w", "w_out", "w_pq", "w2", "a2", "g2", "lnx_w", "lnx_b", "v_tab"]:
        d[k] = np.ascontiguousarray(inp[k][L])
    d["normf_w"] = np.ascontiguousarray(inp["normf_w"])
    d["skT"] = np.ascontiguousarray(inp["sub_keys"][L].reshape(16, 128, 128).transpose(2, 0, 1))
    d["u_tabT"] = np.ascontiguousarray(inp["u_tab"][L].reshape(128, 128, 16, 128).transpose(0, 3, 2, 1)).reshape(128, 128, 2048)
    rw = np.zeros((128, 8, 8), np.float32)
    def ch(v):
        return v.reshape(8, 128).T
    rw[:, :, 0] = ch(inp["w0"][L, 0]); rw[:, :, 1] = ch(inp["w0"][L, 1])
    rw[:, :, 2] = ch(inp["a0"][L, 0]); rw[:, :, 3] = ch(inp["a0"][L, 1])
    rw[:, :, 4] = ch(inp["k_k"][L]); rw[:, :, 5] = ch(inp["k_a"][L]); rw[:, :, 6] = ch(inp["r_k"][L].reshape(-1))
    d["rw_c"] = rw
    return d


def build_program():
    nc = bass.Bass("TRN2", target_bir_lowering=False)
    kb = KB(nc)
    D = declare(kb, None)
    C = consts(kb, D)
    phase_A(kb, C, D)
    phase_Rpre(kb, C, D)
    phase_R(kb, C, D)
    phase_T(kb, C, D)
    phase_O(kb, C, D, D.ymT_d)
    phase_P(kb, C, D)
    phase_F(kb, C, D)
    kb.finish("sp")
    return nc


def kernel(**inputs):
    inp = {k: np.asarray(v) for k, v in inputs.items()}
    shared = prep_shared(inp)
    nc = build_program()
    in_maps = []
    for b in range(8):
        d = dict(shared)
        d["x"] = np.ascontiguousarray(inp["x"][b])
        in_maps.append(d)
    res = run_bass_kernel_spmd(nc, in_maps, core_ids=list(range(8)))
    out = np.stack([np.asarray(r["out"], dtype=np.float32) for r in res.results], axis=0)
    return out
```

```python
import numpy as np
import contextlib
import concourse.bass as bass
import concourse.mybir as mybir
from concourse.bass_utils import run_bass_kernel_spmd

F32 = mybir.dt.float32
BF16 = mybir.dt.bfloat16
U32 = mybir.dt.uint32
ALU = mybir.AluOpType
AF = mybir.ActivationFunctionType
AX = mybir.AxisListType


class T:
    def __init__(self, h, name):
        self.h = h
        self.name = name
        self.w = None
        self.r = {}
        self.dsem = None
        self.dcnt = 0
        self.is_psum = False

    def __getitem__(self, k):
        return self.h[k]

    def ap(self):
        return self.h.ap() if hasattr(self.h, "ap") else self.h[:]


class KB:
    def __init__(self, nc):
        self.nc = nc
        self.es = contextlib.ExitStack()
        self.engs = {"pe": nc.tensor, "act": nc.scalar, "dve": nc.vector, "pool": nc.gpsimd, "sp": nc.sync}
        self.sem = {}
        self.cnt = {}
        for e in self.engs:
            self.sem[e] = self.es.enter_context(nc.semaphore("s_" + e))
            self.cnt[e] = 0
        self.waited = {}
        self.alltensors = []
        self.dsems = []
        self.n_ins = 0

    def sb(self, name, shape, dt=F32, stack=None):
        h = (stack or self.es).enter_context(self.nc.sbuf_tensor(name, list(shape), dt))
        t = T(h, name)
        return t

    def ps(self, name, shape, dt=F32, stack=None):
        h = (stack or self.es).enter_context(self.nc.psum_tensor(name, list(shape), dt))
        t = T(h, name)
        t.is_psum = True
        return t

    def dram(self, name, shape, dt=F32, kind="Internal"):
        h = self.nc.dram_tensor(name, list(shape), dt, kind=kind)
        return T(h, name)

    def view(self, t, name=None):
        return t

    def _wait(self, eng, tok):
        if tok is None:
            return
        sem, val = tok
        key = (eng, id(sem))
        if self.waited.get(key, 0) >= val:
            return
        self.engs[eng].wait_ge(sem, val)
        self.waited[key] = val

    def _deps(self, eng, reads, writes):
        own = id(self.sem[eng])
        for t in reads:
            if t.w is not None:
                if eng == "pe" and id(t.w[0]) == own:
                    continue
                self._wait(eng, t.w)
        for t in writes:
            if t.w is not None and id(t.w[0]) != own:
                self._wait(eng, t.w)
            for k, tok in t.r.items():
                if k == own:
                    continue
                self._wait(eng, tok)

    def _mark(self, tok, reads, writes):
        for t in reads:
            if t in writes:
                continue
            t.r[id(tok[0])] = tok
        for t in writes:
            t.w = tok
            t.r = {}

    def op(self, eng, fn, reads=(), writes=()):
        psr = [t for t in reads if t.is_psum and t not in writes]
        if psr:
            writes = list(writes) + psr
        self._deps(eng, reads, writes)
        ins = fn(self.engs[eng])
        self.cnt[eng] += 1
        ins.then_inc(self.sem[eng], 1)
        tok = (self.sem[eng], self.cnt[eng])
        self._mark(tok, reads, writes)
        self.n_ins += 1
        return ins

    def dma(self, q, out_ap, in_ap, reads=(), writes=(), **kw):
        assert len(writes) == 1
        dst = writes[0]
        if dst.dsem is None:
            dst.dsem = self.es.enter_context(self.nc.semaphore("d_" + dst.name))
            self.dsems.append(dst)
        self._deps(q, reads, [])
        if dst.w is not None and dst.w[0] is not dst.dsem:
            self._wait(q, dst.w)
        for k, tok in dst.r.items():
            self._wait(q, tok)
        ins = self.engs[q].dma_start(out=out_ap, in_=in_ap, **kw)
        dst.dcnt += 16
        ins.then_inc(dst.dsem, 16)
        tok = (dst.dsem, dst.dcnt)
        for t in reads:
            t.r[id(tok[0])] = tok
        dst.w = tok
        dst.r = {}
        self.n_ins += 1
        return ins

    def pe_fence(self):
        if self.cnt["pe"] > 0:
            self._wait("pe", (self.sem["pe"], self.cnt["pe"]))

    def barrier(self):
        toks = [(self.sem[e], self.cnt[e]) for e in self.engs if self.cnt[e] > 0]
        toks += [(t.dsem, t.dcnt) for t in self.dsems if t.dcnt > 0]
        for e in self.engs:
            for tok in toks:
                if tok[0] is self.sem[e]:
                    continue
                self._wait(e, tok)

    def finish(self, eng="sp"):
        for t in self.dsems:
            if t.dcnt > 0:
                self._wait(eng, (t.dsem, t.dcnt))
        for e in self.engs:
            if e != eng and self.cnt[e] > 0:
                self._wait(eng, (self.sem[e], self.cnt[e]))


EPS = 1e-6
NT = 16
RCH = [(i * 128, 128) for i in range(24)] + [(3072 + i * 96, 96) for i in range(4)] + [(3456, 128), (3584, 128)]
NRCH = len(RCH)
RW = 3712


class Obj:
    pass


def declare(kb, debug):
    D = Obj()

    def inp(name, shape, dt=F32):
        setattr(D, name, kb.dram(name, shape, dt, kind="ExternalInput"))

    def scr(name, shape, dt=F32):
        kind = "ExternalOutput" if (debug and name in debug) else "Internal"
        setattr(D, name, kb.dram(name, shape, dt, kind=kind))

    inp("x", [2048, 2048])
    inp("w_in", [2048, 5248])
    inp("mu_c", [128, 3 * NRCH])
    inp("norm1_w", [2048])
    inp("norm2_w", [2048])
    inp("normf_w", [2048])
    inp("q_norm_w", [64])
    inp("k_norm_w", [64])
    inp("tabC", [2048, 32])
    inp("tabS", [2048, 32])
    inp("ident", [128, 128])
    inp("w_out", [2048, 2048])
    inp("w_pq", [2048, 2048])
    inp("skT", [128, 16, 128])
    inp("iota", [128, 128])
    inp("u_tabT", [128, 128, 2048])
    inp("v_tab", [16384, 2048])
    inp("w2", [2, 96, 1024])
    inp("a2", [2, 96, 1024])
    inp("g2", [256, 1024])
    inp("rw_c", [128, 8, 8])
    inp("lnx_w", [1024])
    inp("lnx_b", [1024])
    inp("masks", [128, 6, 128])
    if debug and "in_ymT" in debug:
        inp("ymT_in", [16, 128, 2048], BF16)
    if debug and "in_x1" in debug:
        inp("x1_in", [2048, 2048])
    scr("zs_d", [RW, 2048])
    scr("qkv_d", [2048, 1536])
    scr("ymT_d", [16, 128, 2048], BF16)
    scr("x1_d", [2048, 2048])
    scr("G_d", [128, 128, 2048], BF16)
    scr("peT_d", [2048, 2048])
    scr("A_d", [128, 128, 2048], BF16)
    scr("ld_d", [2, 1024, 2048])
    scr("a_d", [2, 1024, 2048])
    scr("g_d", [2048, 1024])
    inp("rcst", [128, 66 + 2048])
    inp("segm", [128, 2048], BF16)
    if debug and "dbgY" in debug:
        setattr(D, "dbgY", kb.dram("dbgY", [2048, 1024], F32, kind="ExternalOutput"))
        setattr(D, "dbgF", kb.dram("dbgF", [2048, 1024], F32, kind="ExternalOutput"))
    scr("S_d", [2048, 16, 128])
    scr("h2T_d", [16, 128, 2048], BF16)
    setattr(D, "out", kb.dram("out", [2048, 2048], F32, kind="ExternalOutput"))
    return D


def consts(kb, D):
    C = Obj()
    C.ident = kb.sb("c_ident", [128, 128], F32)
    kb.dma("sp", C.ident[:], D.ident.ap(), writes=[C.ident])
    C.identb = kb.sb("c_identb", [128, 128], BF16)
    kb.op("dve", lambda e: e.tensor_copy(out=C.identb[:], in_=C.ident[:]), reads=[C.ident], writes=[C.identb])
    return C


def norm_to_T(kb, C, src, wvec, hT, tag):
    with contextlib.ExitStack() as st:
        wb = kb.sb(tag + "wb", [128, 2048], F32, st)
        kb.dma("pool", wb[:], wvec.ap().partition_broadcast(128), writes=[wb])
        xts = [kb.sb(f"{tag}x{i}", [128, 2048], F32, st) for i in range(2)]
        junk = kb.sb(tag + "junk", [128, 2048], BF16, st)
        hb = [kb.sb(f"{tag}h{i}", [128, 2048], BF16, st) for i in range(2)]
        ss = kb.sb(tag + "ss", [128, NT], F32, st)
        rs = kb.sb(tag + "rs", [128, NT], F32, st)
        ptr = [kb.ps(f"{tag}ps{i}", [128, 1024], BF16, st) for i in range(2)]
        kb.op("pool", lambda e: e.memset(ss[:], 0.0), writes=[ss])
        for tt in range(NT):
            xt = xts[tt % 2]
            kb.dma("sp", xt[:], src.ap()[tt * 128:(tt + 1) * 128, :], writes=[xt])
            kb.op("act", lambda e: e.activation(out=junk[:], in_=xt[:], func=AF.Square, accum_out=ss[:, tt:tt + 1]),
                  reads=[xt], writes=[junk, ss])
            kb.op("dve", lambda e: e.tensor_scalar(out=rs[:, tt:tt + 1], in0=ss[:, tt:tt + 1], scalar1=1.0 / 2048, scalar2=EPS,
                                                   op0=ALU.mult, op1=ALU.add), reads=[ss], writes=[rs])
            kb.op("act", lambda e: e.activation(out=rs[:, tt:tt + 1], in_=rs[:, tt:tt + 1], func=AF.Sqrt), reads=[rs], writes=[rs])
            kb.op("dve", lambda e: e.reciprocal(out=rs[:, tt:tt + 1], in_=rs[:, tt:tt + 1]), reads=[rs], writes=[rs])
            h = hb[tt % 2]
            kb.op("dve", lambda e: e.scalar_tensor_tensor(out=h[:], in0=xt[:], scalar=rs[:, tt:tt + 1], in1=wb[:],
                                                          op0=ALU.mult, op1=ALU.mult), reads=[xt, rs, wb], writes=[h])
            for g in range(2):
                p = ptr[g]
                for j in range(8):
                    dc = g * 8 + j
                    kb.op("pe", lambda e: e.transpose(out=p[:, j * 128:(j + 1) * 128], in_=h[:, dc * 128:(dc + 1) * 128],
                                                      identity=C.identb[:]), reads=[h, C.identb], writes=[p])
                eng = "act" if g == 0 else "dve"
                src_ap = p[:].rearrange("p (j t) -> p j t", j=8)
                dst_ap = hT[:, g * 8:(g + 1) * 8, tt * 128:(tt + 1) * 128]
                if eng == "act":
                    kb.op("act", lambda e: e.copy(out=dst_ap, in_=src_ap), reads=[p], writes=[hT])
                else:
                    kb.op("dve", lambda e: e.tensor_copy(out=dst_ap, in_=src_ap), reads=[p], writes=[hT])
        kb.barrier()


def phase_A(kb, C, D):
    with contextlib.ExitStack() as st:
        hT = kb.sb("hT", [128, 16, 2048], BF16, st)
        norm_to_T(kb, C, D.x, D.norm1_w, hT, "n1")
        mu = kb.sb("mu", [128, 3 * NRCH], F32, st)
        kb.dma("sp", mu[:], D.mu_c.ap(), writes=[mu])
        kb.op("dve", lambda e: e.tensor_tensor(out=mu[:, 60:90], in0=mu[:, 0:30], in1=mu[:, 30:60], op=ALU.add), reads=[mu], writes=[mu])
        kb.op("dve", lambda e: e.tensor_scalar(out=mu[:, 60:90], in0=mu[:, 60:90], scalar1=-1.0, scalar2=1.0, op0=ALU.mult, op1=ALU.add),
              reads=[mu], writes=[mu])
        stg = [kb.sb(f"a_stg{i}", [128, 4096], F32, st) for i in range(2)]
        wbf = [kb.sb(f"a_wbf{i}", [128, 16, 128], BF16, st) for i in range(2)]
        accs = [kb.sb(f"a_acc{i}", [128, 2048], F32, st) for i in range(2)]
        pss = [[kb.ps(f"a_ps{i}_{j}", [128, 512], F32, st) for j in range(4)] for i in range(2)]
        w_v = D.w_in.ap().rearrange("(dc p) c -> p dc c", p=128)
        for ci, (c0, cs) in enumerate(RCH):
            sg = stg[ci % 2]
            sgv = sg[:, 0:16 * cs].rearrange("p (dc c) -> p dc c", dc=16)
            kb.dma("sp" if ci % 2 == 0 else "act", sgv, w_v[:, :, c0:c0 + cs], writes=[sg])
            wb = wbf[ci % 2]
            kb.op("pool", lambda e: e.tensor_copy(out=wb[:, :, 0:cs], in_=sgv), reads=[sg], writes=[wb])
            ps = pss[ci % 2]
            acc = accs[ci % 2]
            for tb in range(4):
                for dc in range(16):
                    kb.op("pe", lambda e: e.matmul(ps[tb][0:cs, :], lhsT=wb[:, dc, 0:cs], rhs=hT[:, dc, tb * 512:(tb + 1) * 512],
                                                   start=(dc == 0), stop=(dc == 15)), reads=[wb, hT], writes=[ps[tb]])
            for tb in range(4):
                kb.op("act", lambda e: e.activation(out=acc[0:cs, tb * 512:(tb + 1) * 512], in_=ps[tb][0:cs, :], func=AF.Copy,
                                                    scale=mu[0:cs, 60 + ci:61 + ci]), reads=[ps[tb], mu], writes=[acc])
            for tb in range(4):
                n = 512 if tb < 3 else 511
                d0 = tb * 512 + 1
                kb.op("dve", lambda e: e.scalar_tensor_tensor(out=acc[0:cs, d0:d0 + n], in0=ps[tb][0:cs, 0:n], scalar=mu[0:cs, ci:ci + 1],
                                                              in1=acc[0:cs, d0:d0 + n], op0=ALU.mult, op1=ALU.add),
                      reads=[ps[tb], mu, acc], writes=[acc])
                s0 = 1 if tb == 0 else 0
                n = 512 - s0
                d0 = tb * 512 + s0 - 1
                kb.op("dve", lambda e: e.scalar_tensor_tensor(out=acc[0:cs, d0:d0 + n], in0=ps[tb][0:cs, s0:512], scalar=mu[0:cs, 30 + ci:31 + ci],
                                                              in1=acc[0:cs, d0:d0 + n], op0=ALU.mult, op1=ALU.add),
                      reads=[ps[tb], mu, acc], writes=[acc])
            kb.dma("pool", D.zs_d.ap()[c0:c0 + cs, :], acc[0:cs, :], reads=[acc], writes=[D.zs_d])
        kb.barrier()
        with contextlib.ExitStack() as st2:
            wq = kb.sb("a_wq", [128, 16, 512], BF16, st2)
            ev = [kb.sb(f"a_ev{i}", [128, 512], F32, st2) for i in range(2)]
            for cg in range(3):
                c0 = RW + cg * 512
                for hf in range(2):
                    sg = stg[hf]
                    sgv = sg[:].rearrange("p (dc c) -> p dc c", dc=8)
                    kb.dma("sp" if hf == 0 else "act", sgv, w_v[:, hf * 8:(hf + 1) * 8, c0:c0 + 512], writes=[sg])
                    kb.op("pool", lambda e: e.tensor_copy(out=wq[:, hf * 8:(hf + 1) * 8, :], in_=sgv), reads=[sg], writes=[wq])
                for tt in range(NT):
                    ps = pss[tt % 2][0]
                    for dc in range(16):
                        kb.op("pe", lambda e: e.matmul(ps[:], lhsT=hT[:, dc, tt * 128:(tt + 1) * 128], rhs=wq[:, dc, :],
                                                       start=(dc == 0), stop=(dc == 15)), reads=[wq, hT], writes=[ps])
                    o = ev[tt % 2]
                    kb.op("act", lambda e: e.copy(out=o[:], in_=ps[:]), reads=[ps], writes=[o])
                    kb.dma("sp", D.qkv_d.ap()[tt * 128:(tt + 1) * 128, cg * 512:(cg + 1) * 512], o[:], reads=[o], writes=[D.qkv_d])
            kb.barrier()


def phase_T(kb, C, D):
    with contextlib.ExitStack() as st:
        qT = kb.sb("t_qT", [128, 8, 2048], BF16, st)
        kT2 = kb.sb("t_kT2", [128, 4, 2048], BF16, st)
        vaug = kb.sb("t_vaug", [128, NT, 4, 65], BF16, st)
        wqk = kb.sb("t_wqk", [128, 20, 64], F32, st)
        w64 = kb.sb("t_w64", [128, 2, 64], F32, st)
        kb.dma("pool", w64[:, 0, :], D.q_norm_w.ap().partition_broadcast(128), writes=[w64])
        kb.dma("pool", w64[:, 1, :], D.k_norm_w.ap().partition_broadcast(128), writes=[w64])
        kb.op("dve", lambda e: e.tensor_scalar(out=wqk[:, 0:16, :], in0=w64[:, 0:1, :].to_broadcast([128, 16, 64]), scalar1=0.125, scalar2=None,
                                               op0=ALU.mult), reads=[w64], writes=[wqk])
        kb.op("dve", lambda e: e.tensor_copy(out=wqk[:, 16:20, :], in_=w64[:, 1:2, :].to_broadcast([128, 4, 64])), reads=[w64], writes=[wqk])
        kb.op("pool", lambda e: e.memset(vaug[:], 1.0), writes=[vaug])
        with contextlib.ExitStack() as st2:
            qk = [kb.sb(f"t_qk{i}", [128, 1536], F32, st2) for i in range(2)]
            tC = [kb.sb(f"t_tC{i}", [128, 2, 16], F32, st2) for i in range(2)]
            tS = [kb.sb(f"t_tS{i}", [128, 2, 16], F32, st2) for i in range(2)]
            sq = kb.sb("t_sq", [128, 20, 64], F32, st2)
            ss = kb.sb("t_ss", [128, 20], F32, st2)
            qn = kb.sb("t_qn", [128, 20, 64], F32, st2)
            t1 = kb.sb("t_t1", [128, 20, 2, 16], F32, st2)
            t2 = kb.sb("t_t2", [128, 20, 2, 16], F32, st2)
            t3 = kb.sb("t_t3", [128, 20, 2, 16], F32, st2)
            t4 = kb.sb("t_t4", [128, 20, 2, 16], F32, st2)
            qkr = kb.sb("t_qkr", [128, 20, 64], BF16, st2)
            kd = kb.sb("t_kd", [128, 4, 2, 64], BF16, st2)
            pq = kb.ps("t_pq", [128, 1024], BF16, st2)
            pk = kb.ps("t_pk", [128, 1024], BF16, st2)
            for tt in range(NT):
                q = qk[tt % 2]
                cC = tC[tt % 2]
                cS = tS[tt % 2]
                kb.dma("sp", q[:], D.qkv_d.ap()[tt * 128:(tt + 1) * 128, :], reads=[D.qkv_d], writes=[q])
                kb.dma("act", cC[:].rearrange("p a b -> p (a b)"), D.tabC.ap()[tt * 128:(tt + 1) * 128, :], writes=[cC])
                kb.dma("act", cS[:].rearrange("p a b -> p (a b)"), D.tabS.ap()[tt * 128:(tt + 1) * 128, :], writes=[cS])
                qv = q[:, 0:1280].rearrange("p (h d) -> p h d", h=20)
                kb.op("act", lambda e: e.activation(out=sq[:], in_=qv, func=AF.Square), reads=[q], writes=[sq])
                kb.op("dve", lambda e: e.tensor_reduce(out=ss[:], in_=sq[:], axis=AX.X, op=ALU.add), reads=[sq], writes=[ss])
                kb.op("dve", lambda e: e.tensor_scalar(out=ss[:], in0=ss[:], scalar1=1.0 / 64, scalar2=EPS, op0=ALU.mult, op1=ALU.add),
                      reads=[ss], writes=[ss])
                kb.op("act", lambda e: e.activation(out=ss[:], in_=ss[:], func=AF.Sqrt), reads=[ss], writes=[ss])
                kb.op("dve", lambda e: e.reciprocal(out=ss[:], in_=ss[:]), reads=[ss], writes=[ss])
                kb.op("dve", lambda e: e.tensor_tensor(out=qn[:], in0=qv, in1=ss[:, :, None].to_broadcast([128, 20, 64]), op=ALU.mult),
                      reads=[q, ss], writes=[qn])
                kb.op("pool", lambda e: e.tensor_tensor(out=qn[:], in0=qn[:], in1=wqk[:], op=ALU.mult), reads=[qn, wqk], writes=[qn])
                qn5 = qn[:].rearrange("p h (a b c) -> p h a b c", a=2, b=2)
                x1 = qn5[:, :, :, 0, :]
                x2 = qn5[:, :, :, 1, :]
                Cb = cC[:, None, :, :].to_broadcast([128, 20, 2, 16])
                Sb = cS[:, None, :, :].to_broadcast([128, 20, 2, 16])
                kb.op("dve", lambda e: e.tensor_tensor(out=t1[:], in0=x1, in1=Cb, op=ALU.mult), reads=[qn, cC], writes=[t1])
                kb.op("pool", lambda e: e.tensor_tensor(out=t2[:], in0=x2, in1=Sb, op=ALU.mult), reads=[qn, cS], writes=[t2])
                kb.op("pool", lambda e: e.tensor_tensor(out=t3[:], in0=x2, in1=Cb, op=ALU.mult), reads=[qn, cC], writes=[t3])
                kb.op("dve", lambda e: e.tensor_tensor(out=t4[:], in0=x1, in1=Sb, op=ALU.mult), reads=[qn, cS], writes=[t4])
                r5 = qkr[:].rearrange("p h (a b c) -> p h a b c", a=2, b=2)
                kb.op("dve", lambda e: e.tensor_tensor(out=r5[:, :, :, 0, :], in0=t1[:], in1=t2[:], op=ALU.subtract), reads=[t1, t2], writes=[qkr])
                kb.op("pool", lambda e: e.tensor_tensor(out=r5[:, :, :, 1, :], in0=t3[:], in1=t4[:], op=ALU.add), reads=[t3, t4], writes=[qkr])
                kb.op("pool", lambda e: e.tensor_copy(out=kd[:], in_=qkr[:, 16:20, None, :].to_broadcast([128, 4, 2, 64])), reads=[qkr], writes=[kd])
                for j in range(8):
                    kb.op("pe", lambda e: e.transpose(out=pq[:, j * 128:(j + 1) * 128], in_=qkr[:, 2 * j:2 * j + 2, :].rearrange("p a b -> p (a b)"),
                                                      identity=C.identb[:]), reads=[qkr, C.identb], writes=[pq])
                kb.op("act", lambda e: e.copy(out=qT[:, :, tt * 128:(tt + 1) * 128], in_=pq[:].rearrange("p (j t) -> p j t", j=8)),
                      reads=[pq], writes=[qT])
                for j in range(4):
                    kb.op("pe", lambda e: e.transpose(out=pk[:, j * 128:(j + 1) * 128], in_=kd[:, j, :, :].rearrange("p a b -> p (a b)"),
                                                      identity=C.identb[:]), reads=[kd, C.identb], writes=[pk])
                kb.op("dve", lambda e: e.tensor_copy(out=kT2[:, :, tt * 128:(tt + 1) * 128], in_=pk[:, 0:512].rearrange("p (j t) -> p j t", j=4)),
                      reads=[pk], writes=[kT2])
                kb.op("pool", lambda e: e.tensor_copy(out=vaug[:, tt, :, 0:64], in_=q[:, 1280:1536].rearrange("p (h d) -> p h d", h=4)),
                      reads=[q], writes=[vaug])
            kb.barrier()
        with contextlib.ExitStack() as st3:
            yatt = kb.sb("t_yatt", [128, NT, 1024], BF16, st3)
            pexp = [kb.sb(f"t_pexp{i}", [128, 512], BF16, st3) for i in range(3)]
            rinv = kb.sb("t_rinv", [128, 4], F32, st3)
            pss = [kb.ps(f"t_pss{i}", [128, 512], F32, st3) for i in range(2)]
            po = [kb.ps(f"t_po{i}", [128, 512], F32, st3) for i in range(4)]
            it = 0
            for h in range(16):
                kv = h // 4
                c = h // 2
                b0 = (h % 2) * 64
                for qb in range(4):
                    for kt in range(NT):
                        ps = pss[it % 2]
                        pe_ = pexp[it % 3]
                        it += 1
                        kb.op("pe", lambda e: e.matmul(ps[:], lhsT=kT2[b0:b0 + 64, kv, kt * 128:(kt + 1) * 128],
                                                       rhs=qT[b0:b0 + 64, c, qb * 512:(qb + 1) * 512], start=True, stop=True),
                              reads=[kT2, qT], writes=[ps])
                        kb.op("act", lambda e: e.activation(out=pe_[:], in_=ps[:], func=AF.Exp), reads=[ps], writes=[pe_])
                        for j in range(4):
                            kb.op("pe", lambda e: e.matmul(po[j][:, 0:65], lhsT=pe_[:, j * 128:(j + 1) * 128], rhs=vaug[:, kt, kv, :],
                                                           start=(kt == 0), stop=(kt == NT - 1)), reads=[pe_, vaug], writes=[po[j]])
                    for j in range(4):
                        kb.op("dve", lambda e: e.reciprocal(out=rinv[:, j:j + 1], in_=po[j][:, 64:65]), reads=[po[j]], writes=[rinv])
                        kb.op("dve", lambda e: e.tensor_scalar(out=yatt[:, qb * 4 + j, h * 64:(h + 1) * 64], in0=po[j][:, 0:64],
                                                               scalar1=rinv[:, j:j + 1], scalar2=None, op0=ALU.mult),
                              reads=[po[j], rinv], writes=[yatt])
            pt = [kb.ps(f"t_pt{i}", [128, 1024], BF16, st3) for i in range(2)]
            yT = [kb.sb(f"t_yT{i}", [128, 8, 128], BF16, st3) for i in range(2)]
            for tt in range(NT):
                p = pt[tt % 2]
                o = yT[tt % 2]
                for j in range(8):
                    kb.op("pe", lambda e: e.transpose(out=p[:, j * 128:(j + 1) * 128], in_=yatt[:, tt, j * 128:(j + 1) * 128],
                                                      identity=C.identb[:]), reads=[yatt, C.identb], writes=[p])
                kb.op("act", lambda e: e.copy(out=o[:], in_=p[:].rearrange("p (j t) -> p j t", j=8)), reads=[p], writes=[o])
                kb.dma("sp", D.ymT_d.ap()[8:16, :, tt * 128:(tt + 1) * 128].rearrange("j p t -> p j t"), o[:], reads=[o], writes=[D.ymT_d])
            kb.barrier()


def phase_O(kb, C, D, ym_src):
    with contextlib.ExitStack() as st:
        ymT = kb.sb("o_ymT", [128, 16, 2048], BF16, st)
        for j in range(16):
            kb.dma("sp" if j % 2 == 0 else "act", ymT[:, j, :], ym_src.ap()[j], reads=[ym_src], writes=[ymT])
        stg = [kb.sb(f"o_stg{i}", [128, 4096], F32, st) for i in range(2)]
        wo = kb.sb("o_wo", [128, 16, 512], BF16, st)
        xs = [kb.sb(f"o_xs{i}", [128, 512], F32, st) for i in range(2)]
        pss = [kb.ps(f"o_ps{i}", [128, 512], F32, st) for i in range(2)]
        w_v = D.w_out.ap().rearrange("(kc p) c -> p kc c", p=128)
        for dg in range(4):
            for hf in range(2):
                sg = stg[hf]
                sgv = sg[:].rearrange("p (dc c) -> p dc c", dc=8)
                kb.dma("sp" if hf == 0 else "act", sgv, w_v[:, hf * 8:(hf + 1) * 8, dg * 512:(dg + 1) * 512], writes=[sg])
                kb.op("pool", lambda e: e.tensor_copy(out=wo[:, hf * 8:(hf + 1) * 8, :], in_=sgv), reads=[sg], writes=[wo])
            for tt in range(NT):
                ps = pss[tt % 2]
                xt = xs[tt % 2]
                kb.dma("sp", xt[:], D.x.ap()[tt * 128:(tt + 1) * 128, dg * 512:(dg + 1) * 512], writes=[xt])
                for kc in range(16):
                    kb.op("pe", lambda e: e.matmul(ps[:], lhsT=ymT[:, kc, tt * 128:(tt + 1) * 128], rhs=wo[:, kc, :],
                                                   start=(kc == 0), stop=(kc == 15)), reads=[ymT, wo], writes=[ps])
                kb.op("dve", lambda e: e.tensor_tensor(out=xt[:], in0=ps[:], in1=xt[:], op=ALU.add), reads=[ps, xt], writes=[xt])
                kb.dma("act", D.x1_d.ap()[tt * 128:(tt + 1) * 128, dg * 512:(dg + 1) * 512], xt[:], reads=[xt], writes=[D.x1_d])
        kb.barrier()


def phase_P(kb, C, D):
    with contextlib.ExitStack() as stP:
        ET = kb.sb("p_ET", [128, 3, 2048], F32, stP)
        iota = kb.sb("p_iota", [128, 128], F32, stP)
        kb.dma("sp", iota[:], D.iota.ap(), writes=[iota])
        with contextlib.ExitStack() as st, kb.nc.named_scope("P1"):
            h2T = kb.sb("p_h2T", [128, 16, 2048], BF16, st)
            norm_to_T(kb, C, D.x1_d, D.norm2_w, h2T, "n2")
            for dc in range(16):
                kb.dma("sp" if dc % 2 == 0 else "act", D.h2T_d.ap()[dc], h2T[:, dc, :], reads=[h2T], writes=[D.h2T_d])
            skb = kb.sb("p_sk", [128, 16, 128], F32, st)
            kb.dma("sp", skb[:], D.skT.ap(), writes=[skb])
            stg = [kb.sb(f"p_stg{i}", [128, 16, 128], F32, st) for i in range(2)]
            wb = [kb.sb(f"p_wb{i}", [128, 16, 128], BF16, st) for i in range(2)]
            qTs = [kb.sb(f"p_qT{i}", [128, 2048], F32, st) for i in range(2)]
            sev = [kb.sb(f"p_sev{i}", [128, 16, 128], F32, st) for i in range(2)]
            pss = [[kb.ps(f"p_ps{i}_{j}", [128, 512], F32, st) for j in range(3)] for i in range(2)]
            w_v = D.w_pq.ap().rearrange("(dc p) c -> p dc c", p=128)
            for hp in range(16):
                sg = stg[hp % 2]
                kb.dma("sp" if hp % 2 == 0 else "act", sg[:], w_v[:, :, hp * 128:(hp + 1) * 128], writes=[sg])
                w = wb[hp % 2]
                kb.op("pool", lambda e: e.tensor_copy(out=w[:], in_=sg[:]), reads=[sg], writes=[w])
                qT = qTs[hp % 2]
                for tb in range(4):
                    ps = pss[tb % 2][0]
                    for dc in range(16):
                        kb.op("pe", lambda e: e.matmul(ps[:], lhsT=w[:, dc, :], rhs=h2T[:, dc, tb * 512:(tb + 1) * 512],
                                                       start=(dc == 0), stop=(dc == 15)), reads=[w, h2T], writes=[ps])
                    kb.op("act", lambda e: e.copy(out=qT[:, tb * 512:(tb + 1) * 512], in_=ps[:]), reads=[ps], writes=[qT])
                se = sev[hp % 2]
                for g in range(4):
                    ps = pss[g % 2][1 + (g // 2) % 2]
                    for j in range(4):
                        tt = g * 4 + j
                        kb.op("pe", lambda e: e.matmul(ps[:, j * 128:(j + 1) * 128], lhsT=qT[:, tt * 128:(tt + 1) * 128], rhs=skb[:, hp, :],
                                                       start=True, stop=True), reads=[qT, skb], writes=[ps])
                    kb.op("dve", lambda e: e.tensor_copy(out=se[:, g * 4:(g + 1) * 4, :], in_=ps[:].rearrange("p (j n) -> p j n", j=4)),
                          reads=[ps], writes=[se])
                kb.dma("pool", D.S_d.ap()[:, hp, :].rearrange("(tt p) n -> p tt n", p=128), se[:], reads=[se], writes=[D.S_d])
            kb.barrier()
        with contextlib.ExitStack() as st, kb.nc.named_scope("P2"):
            Ss = [kb.sb(f"p_S{i}", [128, 16, 128], F32, st) for i in range(2)]
            S2 = kb.sb("p_S2", [128, 128], F32, st)
            M16 = kb.sb("p_M16", [128, 16, 16], F32, st)
            I16u = kb.sb("p_I16u", [128, 16, 16], U32, st)
            I16f = kb.sb("p_I16f", [128, 16, 16], F32, st)
            cand = kb.sb("p_cand", [128, 8, 16, 16], F32, st)
            cand2 = kb.sb("p_cand2", [128, 256], F32, st)
            C16 = kb.sb("p_C16", [128, 8, 16], F32, st)
            CIu = kb.sb("p_CIu", [128, 8, 16], U32, st)
            IJu = kb.sb("p_IJu", [128, 2, 8, 16], U32, st)
            IJf = kb.sb("p_IJf", [128, 2, 8, 16], F32, st)
            ex = kb.sb("p_ex", [128, 8, 16], F32, st)
            Z = kb.sb("p_Z", [128, 8], F32, st)
            EG = kb.sb("p_EG", [128, 3, 8, 16], F32, st)
            eq = kb.sb("p_eq", [128, 8, 16, 16], F32, st)
            pt = kb.ps("p_pt", [128, 512], F32, st)
            for tt in range(NT):
                S = Ss[tt % 2]
                kb.dma("sp", S[:].rearrange("p a n -> p (a n)"), D.S_d.ap()[tt * 128:(tt + 1) * 128].rearrange("p a n -> p (a n)"),
                       reads=[D.S_d], writes=[S])
                for hp in range(16):
                    kb.op("dve", lambda e: e.max(out=M16[:, hp, 0:8], in_=S[:, hp, :]), reads=[S], writes=[M16])
                    kb.op("dve", lambda e: e.max_index(out=I16u[:, hp, 0:8], in_max=M16[:, hp, 0:8], in_values=S[:, hp, :]),
                          reads=[S, M16], writes=[I16u])
                    kb.op("dve", lambda e: e.match_replace(out=S2[:], in_to_replace=M16[:, hp, 0:8], in_values=S[:, hp, :], imm_value=-1e30),
                          reads=[S, M16], writes=[S2])
                    kb.op("dve", lambda e: e.max(out=M16[:, hp, 8:16], in_=S2[:]), reads=[S2], writes=[M16])
                    kb.op("dve", lambda e: e.max_index(out=I16u[:, hp, 8:16], in_max=M16[:, hp, 8:16], in_values=S2[:]),
                          reads=[S2, M16], writes=[I16u])
                kb.op("pool", lambda e: e.tensor_copy(out=I16f[:], in_=I16u[:]), reads=[I16u], writes=[I16f])
                M4 = M16[:].rearrange("p (h q) k -> p h q k", q=2)
                I4 = I16f[:].rearrange("p (h q) k -> p h q k", q=2)
                kb.op("pool", lambda e: e.tensor_tensor(out=cand[:], in0=M4[:, :, 0, :, None].to_broadcast([128, 8, 16, 16]),
                                                        in1=M4[:, :, 1, None, :].to_broadcast([128, 8, 16, 16]), op=ALU.add),
                      reads=[M16], writes=[cand])
                for h in range(8):
                    ch = cand[:, h, :, :].rearrange("p a b -> p (a b)")
                    kb.op("dve", lambda e: e.max(out=C16[:, h, 0:8], in_=ch), reads=[cand], writes=[C16])
                    kb.op("dve", lambda e: e.max_index(out=CIu[:, h, 0:8], in_max=C16[:, h, 0:8], in_values=ch), reads=[cand, C16], writes=[CIu])
                    kb.op("dve", lambda e: e.match_replace(out=cand2[:], in_to_replace=C16[:, h, 0:8], in_values=ch, imm_value=-1e30),
                          reads=[cand, C16], writes=[cand2])
                    kb.op("dve", lambda e: e.max(out=C16[:, h, 8:16], in_=cand2[:]), reads=[cand2], writes=[C16])
                    kb.op("dve", lambda e: e.max_index(out=CIu[:, h, 8:16], in_max=C16[:, h, 8:16], in_values=cand2[:]),
                          reads=[cand2, C16], writes=[CIu])
                kb.op("pool", lambda e: e.tensor_tensor(out=ex[:], in0=C16[:], in1=C16[:, :, 0:1].to_broadcast([128, 8, 16]), op=ALU.subtract),
                      reads=[C16], writes=[ex])
                kb.op("act", lambda e: e.activation(out=ex[:], in_=ex[:], func=AF.Exp), reads=[ex], writes=[ex])
                kb.op("dve", lambda e: e.tensor_reduce(out=Z[:], in_=ex[:], axis=AX.X, op=ALU.add), reads=[ex], writes=[Z])
                kb.op("dve", lambda e: e.reciprocal(out=Z[:], in_=Z[:]), reads=[Z], writes=[Z])
                kb.op("dve", lambda e: e.tensor_tensor(out=EG[:, 2], in0=ex[:], in1=Z[:, :, None].to_broadcast([128, 8, 16]), op=ALU.mult),
                      reads=[ex, Z], writes=[EG])
                kb.op("dve", lambda e: e.tensor_single_scalar(out=IJu[:, 0], in_=CIu[:], scalar=4, op=ALU.logical_shift_right),
                      reads=[CIu], writes=[IJu])
                kb.op("dve", lambda e: e.tensor_single_scalar(out=IJu[:, 1], in_=CIu[:], scalar=15, op=ALU.bitwise_and),
                      reads=[CIu], writes=[IJu])
                kb.op("pool", lambda e: e.tensor_copy(out=IJf[:], in_=IJu[:]), reads=[IJu], writes=[IJf])
                for q in range(2):
                    kb.op("dve", lambda e: e.tensor_tensor(out=eq[:], in0=iota[:, None, None, 0:16].to_broadcast([128, 8, 16, 16]),
                                                            in1=IJf[:, q, :, :, None].to_broadcast([128, 8, 16, 16]), op=ALU.is_equal),
                          reads=[iota, IJf], writes=[eq])
                    kb.op("pool", lambda e: e.tensor_tensor(out=eq[:], in0=eq[:], in1=I4[:, :, q, None, :].to_broadcast([128, 8, 16, 16]),
                                                            op=ALU.mult), reads=[eq, I16f], writes=[eq])
                    kb.op("dve", lambda e: e.tensor_reduce(out=EG[:, q], in_=eq[:], axis=AX.X, op=ALU.add), reads=[eq], writes=[EG])
                for a in range(3):
                    kb.op("pe", lambda e: e.transpose(out=pt[:, a * 128:(a + 1) * 128], in_=EG[:, a].rearrange("p h k -> p (h k)"),
                                                      identity=C.ident[:]), reads=[EG, C.ident], writes=[pt])
                kb.op("act", lambda e: e.copy(out=ET[:, :, tt * 128:(tt + 1) * 128], in_=pt[:, 0:384].rearrange("p (a t) -> p a t", a=3)),
                      reads=[pt], writes=[ET])
            kb.barrier()
        with contextlib.ExitStack() as st, kb.nc.named_scope("P3"):
            Gs = kb.sb("p_Gs", [128, 128, 256], BF16, st)
            O1 = [kb.sb(f"p_O1{i}", [128, 32, 128], BF16, st) for i in range(2)]
            O2 = [kb.sb(f"p_O2{i}", [128, 32, 128], BF16, st) for i in range(2)]
            pg = [kb.ps(f"p_pg{i}", [128, 512], F32, st) for i in range(4)]
            it = 0
            for tg in range(8):
                for sub in range(8):
                    t0 = tg * 256 + sub * 32
                    o1 = O1[sub % 2]
                    o2 = O2[sub % 2]
                    iob = iota[:, None, :].to_broadcast([128, 32, 128])
                    kb.op("dve", lambda e: e.tensor_tensor(out=o1[:], in0=iob, in1=ET[:, 0, t0:t0 + 32, None].to_broadcast([128, 32, 128]),
                                                            op=ALU.is_equal), reads=[iota, ET], writes=[o1])
                    kb.op("dve", lambda e: e.tensor_tensor(out=o2[:], in0=iob, in1=ET[:, 1, t0:t0 + 32, None].to_broadcast([128, 32, 128]),
                                                           op=ALU.is_equal), reads=[iota, ET], writes=[o2])
                    kb.op("pool", lambda e: e.tensor_tensor(out=o2[:], in0=o2[:], in1=ET[:, 2, t0:t0 + 32, None].to_broadcast([128, 32, 128]),
                                                            op=ALU.mult), reads=[o2, ET], writes=[o2])
                    for q4 in range(8):
                        p = pg[it % 4]
                        it += 1
                        for j in range(4):
                            tl = q4 * 4 + j
                            kb.op("pe", lambda e: e.matmul(p[:, j * 128:(j + 1) * 128], lhsT=o2[:, tl, :], rhs=o1[:, tl, :], start=True, stop=True),
                                  reads=[o1, o2], writes=[p])
                        tl0 = sub * 32 + q4 * 4
                        dst = Gs[:, :, tl0:tl0 + 4].rearrange("p e t -> p t e")
                        src = p[:].rearrange("p (t e) -> p t e", t=4)
                        if it % 2 == 0:
                            kb.op("act", lambda e: e.copy(out=dst, in_=src), reads=[p], writes=[Gs])
                        else:
                            kb.op("dve", lambda e: e.tensor_copy(out=dst, in_=src), reads=[p], writes=[Gs])
                for k8 in range(8):
                    kb.dma(["sp", "act", "pool"][k8 % 3],
                           D.G_d.ap()[k8 * 16:(k8 + 1) * 16, :, tg * 256:(tg + 1) * 256].rearrange("e1 e2 t -> e2 e1 t"),
                           Gs[:, k8 * 16:(k8 + 1) * 16, :], reads=[Gs], writes=[D.G_d])
            kb.barrier()
        with contextlib.ExitStack() as st, kb.nc.named_scope("P4"):
            h2T = kb.sb("p_h2Tb", [128, 16, 2048], BF16, st)
            for dc in range(16):
                kb.dma("sp" if dc % 2 == 0 else "act", h2T[:, dc, :], D.h2T_d.ap()[dc], reads=[D.h2T_d], writes=[h2T])
            stg = [kb.sb(f"p4_stg{i}", [128, 16, 128], F32, st) for i in range(2)]
            ub = [kb.sb(f"p4_ub{i}", [128, 16, 128], BF16, st) for i in range(2)]
            Gc = [kb.sb(f"p4_Gc{i}", [128, 2048], BF16, st) for i in range(2)]
            ge = [kb.sb(f"p4_ge{i}", [128, 2048], BF16, st) for i in range(2)]
            pss = [[kb.ps(f"p4_ps{i}_{j}", [128, 512], F32, st) for j in range(4)] for i in range(2)]
            def load(e1):
                sg = stg[e1 % 2]
                kb.dma("sp", sg[:].rearrange("p a b -> p (a b)"), D.u_tabT.ap()[e1], writes=[sg])
                g = Gc[e1 % 2]
                kb.dma("pool", g[:], D.G_d.ap()[e1], reads=[D.G_d], writes=[g])
            load(0)
            for e1 in range(128):
                if e1 + 1 < 128:
                    load(e1 + 1)
                sg = stg[e1 % 2]
                u = ub[e1 % 2]
                kb.op("dve", lambda e: e.tensor_copy(out=u[:], in_=sg[:]), reads=[sg], writes=[u])
                g = Gc[e1 % 2]
                ps = pss[e1 % 2]
                a = ge[e1 % 2]
                for tb in range(4):
                    for dc in range(16):
                        kb.op("pe", lambda e: e.matmul(ps[tb][:], lhsT=u[:, dc, :], rhs=h2T[:, dc, tb * 512:(tb + 1) * 512],
                                                       start=(dc == 0), stop=(dc == 15)), reads=[u, h2T], writes=[ps[tb]])
                    kb.op("act", lambda e: e.activation(out=a[:, tb * 512:(tb + 1) * 512], in_=ps[tb][:], func=AF.Gelu), reads=[ps[tb]], writes=[a])
                kb.op("pool", lambda e: e.tensor_tensor(out=a[:], in0=a[:], in1=g[:], op=ALU.mult), reads=[a, g], writes=[a])
                kb.dma("sp", D.A_d.ap()[e1], a[:], reads=[a], writes=[D.A_d])
            kb.barrier()
        with contextlib.ExitStack() as st, kb.nc.named_scope("P5"):
            NB = 3
            vst = [kb.sb(f"p5_vst{i}", [128, 512], F32, st) for i in range(NB)]
            vb = [kb.sb(f"p5_vb{i}", [128, 512], BF16, st) for i in range(NB)]
            ac = [kb.sb(f"p5_ac{i}", [128, 1024], BF16, st) for i in range(NB)]
            ov = [kb.sb(f"p5_ov{i}", [128, 512], F32, st) for i in range(2)]
            pss = [[kb.ps(f"p5_ps{j}_{h}", [128, 512], F32, st) for h in range(2)] for j in range(4)]
            seq = [(tb2, dg, e1) for tb2 in range(2) for dg in range(4) for e1 in range(128)]

            def load(i):
                tb2, dg, e1 = seq[i]
                k = i % NB
                kb.dma("sp", vst[k][:], D.v_tab.ap()[e1 * 128:(e1 + 1) * 128, dg * 512:(dg + 1) * 512], writes=[vst[k]])
                kb.dma("pool", ac[k][:], D.A_d.ap()[e1][:, tb2 * 1024:(tb2 + 1) * 1024], reads=[D.A_d], writes=[ac[k]])
            load(0)
            load(1)
            oi = 0
            for i, (tb2, dg, e1) in enumerate(seq):
                if i + 2 < len(seq):
                    load(i + 2)
                k = i % NB
                if i % 2 == 0:
                    kb.op("dve", lambda e: e.tensor_copy(out=vb[k][:], in_=vst[k][:]), reads=[vst[k]], writes=[vb[k]])
                else:
                    kb.op("act", lambda e: e.copy(out=vb[k][:], in_=vst[k][:]), reads=[vst[k]], writes=[vb[k]])
                for j in range(4):
                    for h in range(2):
                        kb.op("pe", lambda e: e.matmul(pss[j][h][:], lhsT=vb[k][:, j * 128:(j + 1) * 128], rhs=ac[k][:, h * 512:(h + 1) * 512],
                                                       start=(e1 == 0), stop=(e1 == 127)), reads=[vb[k], ac[k]], writes=[pss[j][h]])
                if e1 == 127:
                    for j in range(4):
                        for h in range(2):
                            o = ov[oi % 2]
                            if oi % 2 == 0:
                                kb.op("act", lambda e: e.copy(out=o[:], in_=pss[j][h][:]), reads=[pss[j][h]], writes=[o])
                            else:
                                kb.op("dve", lambda e: e.tensor_copy(out=o[:], in_=pss[j][h][:]), reads=[pss[j][h]], writes=[o])
                            oi += 1
                            d0 = dg * 512 + j * 128
                            t0 = tb2 * 1024 + h * 512
                            kb.dma("sp", D.peT_d.ap()[d0:d0 + 128, t0:t0 + 512], o[:], reads=[o], writes=[D.peT_d])
            kb.barrier()


def phase_F(kb, C, D):
    with contextlib.ExitStack() as st:
        wb = kb.sb("f_wb", [128, 2048], F32, st)
        kb.dma("pool", wb[:], D.normf_w.ap().partition_broadcast(128), writes=[wb])
        xs = [kb.sb(f"f_x{i}", [128, 2048], F32, st) for i in range(2)]
        pes = [kb.sb(f"f_pe{i}", [128, 16, 128], F32, st) for i in range(2)]
        junk = kb.sb("f_junk", [128, 2048], BF16, st)
        ss = kb.sb("f_ss", [128, NT], F32, st)
        rs = kb.sb("f_rs", [128, NT], F32, st)
        kb.op("pool", lambda e: e.memset(ss[:], 0.0), writes=[ss])
        pss = [[kb.ps(f"f_ps{i}_{j}", [128, 512], F32, st) for j in range(4)] for i in range(2)]
        pe_v = D.peT_d.ap().rearrange("(dc p) t -> p dc t", p=128)
        for tt in range(NT):
            xt = xs[tt % 2]
            pe = pes[tt % 2]
            ps = pss[tt % 2]
            kb.dma("sp", xt[:], D.x1_d.ap()[tt * 128:(tt + 1) * 128, :], reads=[D.x1_d], writes=[xt])
            kb.dma("act", pe[:], pe_v[:, :, tt * 128:(tt + 1) * 128], reads=[D.peT_d], writes=[pe])
            for dc in range(16):
                kb.op("pe", lambda e: e.transpose(out=ps[dc // 4][:, (dc % 4) * 128:(dc % 4 + 1) * 128], in_=pe[:, dc, :], identity=C.ident[:]),
                      reads=[pe, C.ident], writes=[ps[dc // 4]])
            for j in range(4):
                kb.op("dve", lambda e: e.tensor_tensor(out=xt[:, j * 512:(j + 1) * 512], in0=ps[j][:], in1=xt[:, j * 512:(j + 1) * 512], op=ALU.add),
                      reads=[ps[j], xt], writes=[xt])
            kb.op("act", lambda e: e.activation(out=junk[:], in_=xt[:], func=AF.Square, accum_out=ss[:, tt:tt + 1]), reads=[xt], writes=[junk, ss])
            kb.op("dve", lambda e: e.tensor_scalar(out=rs[:, tt:tt + 1], in0=ss[:, tt:tt + 1], scalar1=1.0 / 2048, scalar2=EPS,
                                                   op0=ALU.mult, op1=ALU.add), reads=[ss], writes=[rs])
            kb.op("act", lambda e: e.activation(out=rs[:, tt:tt + 1], in_=rs[:, tt:tt + 1], func=AF.Sqrt), reads=[rs], writes=[rs])
            kb.op("dve", lambda e: e.reciprocal(out=rs[:, tt:tt + 1], in_=rs[:, tt:tt + 1]), reads=[rs], writes=[rs])
            kb.op("dve", lambda e: e.scalar_tensor_tensor(out=xt[:], in0=xt[:], scalar=rs[:, tt:tt + 1], in1=wb[:], op0=ALU.mult, op1=ALU.mult),
                  reads=[xt, rs, wb], writes=[xt])
            kb.dma("sp", D.out.ap()[tt * 128:(tt + 1) * 128, :], xt[:], reads=[xt], writes=[D.out])
        kb.barrier()


STOP = 0
FAST_F32 = True
F32R = mybir.dt.float32r
def fr(ap):
    return ap.bitcast(F32R) if FAST_F32 else ap
class StopBuild(Exception):
    pass
def chk(n):
    if STOP == n:
        raise StopBuild()
GN_EPS = 64e-5
NEG_E05 = -0.6065306597126334


def phase_Rpre(kb, C, D):
    with contextlib.ExitStack() as st:
        rw = kb.sb("rp_rw", [128, 8, 8], F32, st)
        kb.dma("sp", rw[:], D.rw_c.ap(), writes=[rw])
        tmp = kb.sb("rp_tmp", [128, 2048], F32, st)
        lin = [kb.sb(f"rp_lin{i}", [128, 2048], BF16, st) for i in range(4)]
        for i in range(4):
            kb.op("pool", lambda e: e.memset(lin[i][:], 0.0), writes=[lin[i]])
            r0 = 3072 + i * 96
            kb.dma("sp", tmp[0:96, :], D.zs_d.ap()[r0:r0 + 96, :], reads=[D.zs_d], writes=[tmp])
            if i < 2:
                kb.op("act", lambda e: e.activation(out=lin[i][0:96, :], in_=tmp[0:96, :], func=AF.Tanh), reads=[tmp], writes=[lin[i]])
            else:
                kb.op("act", lambda e: e.copy(out=lin[i][0:96, :], in_=tmp[0:96, :]), reads=[tmp], writes=[lin[i]])
        sgl = kb.sb("rp_sgl", [128, 2, 2048], BF16, st)
        for kc in range(2):
            kb.dma("sp", tmp[:], D.zs_d.ap()[3456 + kc * 128:3456 + (kc + 1) * 128, :], reads=[D.zs_d], writes=[tmp])
            kb.op("act", lambda e: e.activation(out=sgl[:, kc, :], in_=tmp[:], func=AF.Sigmoid), reads=[tmp], writes=[sgl])
        wst = kb.sb("rp_wst", [128, 2048], F32, st)
        w2b = kb.sb("rp_w2b", [128, 2, 1024], BF16, st)
        a2b = kb.sb("rp_a2b", [128, 2, 1024], BF16, st)
        g2b = kb.sb("rp_g2b", [128, 2, 1024], BF16, st)
        wv = wst[:].rearrange("p (a c) -> p a c", a=2)
        kb.op("pool", lambda e: e.memset(w2b[:], 0.0), writes=[w2b])
        kb.op("pool", lambda e: e.memset(a2b[:], 0.0), writes=[a2b])
        kb.dma("sp", wv[0:96], D.w2.ap().rearrange("d l c -> l d c"), writes=[wst])
        kb.op("pool", lambda e: e.tensor_copy(out=w2b[0:96], in_=wv[0:96]), reads=[wst], writes=[w2b])
        kb.dma("sp", wv[0:96], D.a2.ap().rearrange("d l c -> l d c"), writes=[wst])
        kb.op("pool", lambda e: e.tensor_copy(out=a2b[0:96], in_=wv[0:96]), reads=[wst], writes=[a2b])
        kb.dma("sp", wv, D.g2.ap().rearrange("(kc p) c -> p kc c", p=128), writes=[wst])
        kb.op("pool", lambda e: e.tensor_copy(out=g2b[:], in_=wv), reads=[wst], writes=[g2b])
        outs = [kb.sb(f"rp_o{i}", [128, 2048], F32, st) for i in range(2)]
        pss = [[kb.ps(f"rp_ps{i}_{j}", [128, 512], F32, st) for j in range(4)] for i in range(2)]
        it = 0
        for cc in range(8):
            for d in range(2):
                for which in range(2):
                    ps = pss[it % 2]
                    o = outs[it % 2]
                    it += 1
                    wmat = w2b if which == 0 else a2b
                    xin = lin[d] if which == 0 else lin[2 + d]
                    bias = rw[:, cc, d:d + 1] if which == 0 else rw[:, cc, 2 + d:3 + d]
                    for tb in range(4):
                        kb.op("pe", lambda e: e.matmul(ps[tb][:], lhsT=wmat[:, d, cc * 128:(cc + 1) * 128], rhs=xin[:, tb * 512:(tb + 1) * 512],
                                                       start=True, stop=True), reads=[wmat, xin], writes=[ps[tb]])
                        kb.op("act", lambda e: e.activation(out=o[:, tb * 512:(tb + 1) * 512], in_=ps[tb][:], func=AF.Sigmoid, bias=bias),
                              reads=[ps[tb], rw], writes=[o])
                    if which == 0:
                        kb.op("pool", lambda e: e.tensor_scalar(out=o[:], in0=o[:], scalar1=NEG_E05, scalar2=None, op0=ALU.mult), reads=[o], writes=[o])
                        kb.dma("sp", D.ld_d.ap()[d, cc * 128:(cc + 1) * 128, :], o[:], reads=[o], writes=[D.ld_d])
                    else:
                        kb.dma("sp", D.a_d.ap()[d, cc * 128:(cc + 1) * 128, :], o[:], reads=[o], writes=[D.a_d])
        for tt in range(NT):
            ps = pss[tt % 2]
            o = outs[tt % 2]
            for hf in range(2):
                for kc in range(2):
                    kb.op("pe", lambda e: e.matmul(ps[hf][:], lhsT=sgl[:, kc, tt * 128:(tt + 1) * 128], rhs=g2b[:, kc, hf * 512:(hf + 1) * 512],
                                                   start=(kc == 0), stop=(kc == 1)), reads=[sgl, g2b], writes=[ps[hf]])
                kb.op("act", lambda e: e.copy(out=o[:, hf * 512:(hf + 1) * 512], in_=ps[hf][:]), reads=[ps[hf]], writes=[o])
            kb.dma("sp", D.g_d.ap()[tt * 128:(tt + 1) * 128, :], o[:, 0:1024], reads=[o], writes=[D.g_d])
        kb.barrier()


def phase_R(kb, C, D, ccs=range(8), dbg=None):
    with contextlib.ExitStack() as st:
        rw = kb.sb("r_rw", [128, 8, 8], F32, st)
        kb.dma("sp", rw[:], D.rw_c.ap(), writes=[rw])
        masks = kb.sb("r_masks", [128, 6, 128], F32, st)
        kb.dma("sp", masks[:], D.masks.ap(), writes=[masks])
        cst = kb.sb("r_cst", [128, 66], F32, st)
        kb.dma("sp", cst[:], D.rcst.ap()[:, 0:66], writes=[cst])
        segt = kb.sb("r_segm", [128, 2048], BF16, st)
        kb.dma("sp", segt[:], D.segm.ap(), writes=[segt])
        ident2 = cst[:, 0:64]
        sel = cst[:, 64:66]
        lnw = kb.sb("r_lnw", [128, 128], F32, st)
        lnb = kb.sb("r_lnb", [128, 128], F32, st)
        Rr = kb.sb("r_R", [128, 2048], F32, st)
        Kk = kb.sb("r_K", [128, 2048], F32, st)
        Vv = kb.sb("r_V", [128, 2048], F32, st)
        KKn = kb.sb("r_KK", [128, 2048], F32, st)
        E1 = kb.sb("r_E1", [128, 2048], F32, st)
        XI = kb.sb("r_XI", [128, 2048], F32, st)
        XE = kb.sb("r_XE", [128, 2048], F32, st)
        Aa = kb.sb("r_A", [128, 2048], F32, st)
        KD = kb.sb("r_KD", [128, 2048], F32, st)
        KT = kb.sb("r_KT", [128, 2048], F32, st)
        AT = kb.sb("r_AT", [128, 2048], F32, st)
        BTb = kb.sb("r_BTb", [128, 2048], F32, st)
        stt = kb.sb("r_stt", [128, 160], F32, st)
        Vtm = kb.sb("r_Vtm", [128, NT, 128], F32, st)
        Ysum = kb.sb("r_Ysum", [128, NT, 128], F32, st)
        MTa = kb.sb("r_MTa", [128, 32, 64], F32, st)
        Ca = kb.sb("r_Ca", [128, 32, 64], F32, st)
        H = [kb.sb(f"r_H{i}", [128, 64], F32, st) for i in range(2)]
        tot = kb.sb("r_tot", [128, 32], F32, st)
        GL = kb.sb("r_GL", [128, 32], F32, st)
        RhT = Vv

        class Set:
            pass
        sets = []
        for p in range(2):
            S = Set()
            S.XA = kb.sb(f"r_XA{p}", [128, 2, 2, 128], F32, st)
            S.KBm = kb.sb(f"r_KBm{p}", [128, 2, 2, 128], F32, st)
            S.PQ = [kb.sb(f"r_PQ{p}_{i}", [128, 2, 2, 128], F32, st) for i in range(2)]
            S.QT = [kb.sb(f"r_QT{p}_{i}", [128, 2, 128], F32, st) for i in range(2)]
            S.BW = kb.sb(f"r_BW{p}", [128, 2, 128], F32, st)
            S.BU = kb.sb(f"r_BU{p}", [128, 2, 128], F32, st)
            S.AGtm = kb.sb(f"r_AGtm{p}", [128, 128], F32, st)
            S.KGtm = kb.sb(f"r_KGtm{p}", [128, 128], F32, st)
            S.B = [kb.ps(f"r_bk{p}_{i}", [128, 512], F32, st) for i in range(4)]
            sets.append(S)
        psX = sets[0].B[0]
        bkH = [sets[0].B[1], sets[0].B[2]]

        v4 = lambda b: b[:].rearrange("p (h q s) -> p h q s", h=2, q=2)
        v3 = lambda b, lo: b[:, lo:lo + 256].rearrange("p (h s) -> p h s", h=2)

        def tile_gen(S, d, tt, RT, BT):
            M2 = masks[:, 2 * d:2 * d + 2, :]
            MST = masks[:, 2 - 2 * d, :]
            XA, KBm, PQ, QT, BW, BU, AGtm, KGtm = S.XA, S.KBm, S.PQ, S.QT, S.BW, S.BU, S.AGtm, S.KGtm
            B0, B1, B2, B3 = S.B
            psA, psB, psN = v4(B0), v4(B1), v4(B3)
            psC = v3(B2, 0)
            cols = slice(tt * 128, (tt + 1) * 128)
            for hh in range(2):
                pr = slice(hh * 64, hh * 64 + 64)
                kb.pe_fence()
                kb.op("pe", lambda e: e.matmul(psA[:, hh, 0, :], lhsT=fr(AT[pr, cols]), rhs=fr(BT[pr, cols]), start=True, stop=True),
                      reads=[AT, BT], writes=[B0])
                kb.op("pe", lambda e: e.matmul(psA[:, hh, 1, :], lhsT=fr(AT[pr, cols]), rhs=fr(RT[pr, cols]), start=True, stop=True),
                      reads=[AT, RT], writes=[B0])
                kb.op("pe", lambda e: e.matmul(psB[:, hh, 0, :], lhsT=fr(KT[pr, cols]), rhs=fr(BT[pr, cols]), start=True, stop=True),
                      reads=[KT, BT], writes=[B1])
                kb.op("pe", lambda e: e.matmul(psB[:, hh, 1, :], lhsT=fr(KT[pr, cols]), rhs=fr(RT[pr, cols]), start=True, stop=True),
                      reads=[KT, RT], writes=[B1])
                kb.op("pe", lambda e: e.matmul(psC[:, hh, :], lhsT=fr(BT[pr, cols]), rhs=fr(AT[pr, cols]), start=True, stop=True),
                      reads=[AT, BT], writes=[B2])
            kb.pe_fence()
            yield
            M2b = M2[:, None, :, :].to_broadcast([128, 2, 2, 128])
            q0, q1 = QT[0], QT[1]
            kb.op("dve", lambda e: e.tensor_tensor(out=fr(XA[:]), in0=psA, in1=M2b, op=ALU.mult), reads=[B0, masks], writes=[XA])
            kb.op("dve", lambda e: e.tensor_tensor(out=fr(q0[:]), in0=psC, in1=MST[:, None, :].to_broadcast([128, 2, 128]), op=ALU.mult),
                  reads=[B2, masks], writes=[q0])
            kb.op("dve", lambda e: e.tensor_tensor(out=fr(KBm[:]), in0=psB, in1=M2b, op=ALU.mult), reads=[B1, masks], writes=[KBm])
            pq = PQ[0]
            kb.op("pool", lambda e: e.tensor_tensor(out=fr(pq[:, :, 0, :]), in0=XA[:, :, 0, :], in1=C.ident[:, None, :].to_broadcast([128, 2, 128]),
                                                    op=ALU.add), reads=[XA, C.ident], writes=[pq])
            yield
            for hh in range(2):
                kb.op("pe", lambda e: e.matmul(psN[:, hh, 1, :], lhsT=fr(q0[:, hh, :]), rhs=fr(XA[:, hh, 0, :]), start=True, stop=True),
                      reads=[q0, XA], writes=[B3])
                kb.op("pe", lambda e: e.matmul(psC[:, hh, :], lhsT=fr(XA[:, hh, 0, :]), rhs=fr(q0[:, hh, :]), start=True, stop=True),
                      reads=[q0, XA], writes=[B2])
            yield
            kb.op("act", lambda e: e.copy(out=fr(pq[:, :, 1, :]), in_=psN[:, :, 1, :]), reads=[B3], writes=[pq])
            kb.op("dve", lambda e: e.tensor_copy(out=fr(q1[:]), in_=psC), reads=[B2], writes=[q1])
            yield
            cur = 0
            qcur = 1
            for lev in range(1, 6):
                pq = PQ[cur]
                pqn = PQ[1 - cur]
                qt = QT[qcur]
                qtn = QT[1 - qcur]
                last = (lev == 5)
                for hh in range(2):
                    if last:
                        kb.op("pe", lambda e: e.matmul(psN[:, hh, 0, :], lhsT=fr(qt[:, hh, :]), rhs=fr(pq[:, hh, 0, :]), start=True, stop=True),
                              reads=[qt, pq], writes=[B3])
                    else:
                        kb.op("pe", lambda e: e.matmul(psN[:, hh, :, :], lhsT=fr(qt[:, hh, :]), rhs=fr(pq[:, hh, :, :]), start=True, stop=True),
                              reads=[qt, pq], writes=[B3])
                        kb.op("pe", lambda e: e.matmul(psC[:, hh, :], lhsT=fr(pq[:, hh, 1, :]), rhs=fr(qt[:, hh, :]), start=True, stop=True),
                              reads=[qt, pq], writes=[B2])
                yield
                kb.op("dve", lambda e: e.tensor_tensor(out=fr(pqn[:, :, 0, :]), in0=psN[:, :, 0, :], in1=pq[:, :, 0, :], op=ALU.add),
                      reads=[B3, pq], writes=[pqn])
                if not last:
                    kb.op("act", lambda e: e.copy(out=fr(pqn[:, :, 1, :]), in_=psN[:, :, 1, :]), reads=[B3], writes=[pqn])
                    kb.op("dve", lambda e: e.tensor_copy(out=fr(qtn[:]), in_=psC), reads=[B2], writes=[qtn])
                yield
                cur = 1 - cur
                qcur = 1 - qcur
            TT = PQ[cur]
            W_ = B0[:, 0:128].rearrange("p (h i) -> p h i", h=2)
            Bt_ = B0[:, 128:256]
            U_ = B0[:, 256:512].rearrange("p (h i) -> p h i", h=2)
            for hh in range(2):
                kb.op("pe", lambda e: e.matmul(W_[:, hh, :], lhsT=fr(KBm[:, hh, 0, :]), rhs=fr(Vtm[:, tt, hh * 64:(hh + 1) * 64]), start=True, stop=True),
                      reads=[KBm, Vtm], writes=[B0])
            kb.op("pe", lambda e: e.transpose(out=Bt_, in_=BT[:, cols], identity=C.ident[:]), reads=[BT, C.ident], writes=[B0])
            AG_ = B1[:, 256:384]
            KG_ = B1[:, 384:512]
            kb.op("pe", lambda e: e.transpose(out=AG_, in_=Aa[:, cols], identity=C.ident[:]), reads=[Aa, C.ident], writes=[B1])
            kb.op("pe", lambda e: e.transpose(out=KG_, in_=KD[:, cols], identity=C.ident[:]), reads=[KD, C.ident], writes=[B1])
            yield
            kb.op("act", lambda e: e.copy(out=fr(BW[:, :, 64:128]), in_=W_), reads=[B0], writes=[BW])
            kb.op("dve", lambda e: e.tensor_copy(out=fr(BW[:, :, 0:64]), in_=Bt_.rearrange("p (h j) -> p h j", h=2)), reads=[B0], writes=[BW])
            kb.op("act", lambda e: e.copy(out=AGtm[:], in_=AG_), reads=[B1], writes=[AGtm])
            kb.op("act", lambda e: e.copy(out=KGtm[:], in_=KG_), reads=[B1], writes=[KGtm])
            yield
            for hh in range(2):
                kb.op("pe", lambda e: e.matmul(U_[:, hh, :], lhsT=fr(TT[:, hh, 0, :]), rhs=fr(BW[:, hh, :]), start=True, stop=True),
                      reads=[TT, BW], writes=[B0])
            yield
            kb.op("dve", lambda e: e.tensor_copy(out=fr(BU[:]), in_=U_), reads=[B0], writes=[BU])
            yield
            Y_ = B1[:, 0:128].rearrange("p (h i) -> p h i", h=2)
            R_ = B1[:, 128:256]
            for hh in range(2):
                kb.op("pe", lambda e: e.matmul(Y_[:, hh, :], lhsT=fr(XA[:, hh, 1, :]), rhs=fr(BU[:, hh, 64:128]), start=True, stop=False),
                      reads=[XA, BU], writes=[B1])
                kb.op("pe", lambda e: e.matmul(Y_[:, hh, :], lhsT=fr(KBm[:, hh, 1, :]), rhs=fr(Vtm[:, tt, hh * 64:(hh + 1) * 64]), start=False, stop=True),
                      reads=[KBm, Vtm], writes=[B1])
            for hh in range(2):
                kb.op("pe", lambda e: e.matmul(R_[hh * 64:(hh + 1) * 64, :], lhsT=BU[:, hh, 0:64], rhs=XA[:, hh, 1, :], start=True, stop=True),
                      reads=[BU, XA], writes=[B1])
            for n in range(2):
                tr = slice(n * 64, n * 64 + 64)
                bk = (B2, B3)[n]
                for hh in range(2):
                    pr = slice(hh * 64, hh * 64 + 64)
                    kb.op("pe", lambda e: e.matmul(bk[pr, 0:64], lhsT=BU[tr, hh, 0:64], rhs=AGtm[tr, pr], start=True, stop=True),
                          reads=[BU, AGtm], writes=[bk])
                    kb.op("pe", lambda e: e.matmul(bk[pr, 64:128], lhsT=AGtm[tr, pr], rhs=BU[tr, hh, 64:128], start=True, stop=False),
                          reads=[BU, AGtm], writes=[bk])
                    kb.op("pe", lambda e: e.matmul(bk[pr, 64:128], lhsT=KGtm[tr, pr], rhs=Vtm[tr, tt, pr], start=False, stop=True),
                          reads=[KGtm, Vtm], writes=[bk])
            kb.pe_fence()
            yield
            if d == 0:
                kb.op("act", lambda e: e.copy(out=Ysum[:, tt, :], in_=B1[:, 0:128]), reads=[B1], writes=[Ysum])
            else:
                kb.op("dve", lambda e: e.tensor_tensor(out=Ysum[:, tt, :], in0=B1[:, 0:128], in1=Ysum[:, tt, :], op=ALU.add),
                      reads=[B1, Ysum], writes=[Ysum])
            kb.op("dve", lambda e: e.tensor_tensor(out=RhT[:, cols], in0=R_, in1=RT[:, cols], op=ALU.add), reads=[B1, RT], writes=[RhT])
            for n in range(2):
                ch = tt * 2 + n
                bk = (B2, B3)[n]
                kb.op("dve", lambda e: e.scalar_tensor_tensor(out=MTa[:, ch, :], in0=ident2, scalar=GL[:, ch:ch + 1], in1=bk[:, 0:64],
                                                              op0=ALU.mult, op1=ALU.add), reads=[cst, GL, bk], writes=[MTa])
                kb.op("act", lambda e: e.copy(out=Ca[:, ch, :], in_=bk[:, 64:128]), reads=[bk], writes=[Ca])
            yield

        def run_tiles(d, RT, BT):
            pending = list(range(NT))
            active = []
            free_sets = [sets[0], sets[1]]
            S = free_sets.pop(0)
            g = tile_gen(S, d, pending.pop(0), RT, BT)
            active.append((g, S))
            for _ in range(8):
                next(g)
            while active or pending:
                if pending and free_sets:
                    S = free_sets.pop(0)
                    active.append((tile_gen(S, d, pending.pop(0), RT, BT), S))
                for item in list(active):
                    g, S = item
                    try:
                        next(g)
                    except StopIteration:
                        active.remove(item)
                        free_sets.append(S)

        for cc in ccs:
            rows = slice(cc * 128, (cc + 1) * 128)
            kb.dma("sp", Rr[:], D.zs_d.ap()[cc * 128:(cc + 1) * 128, :], reads=[D.zs_d], writes=[Rr])
            kb.dma("act", Kk[:], D.zs_d.ap()[1024 + cc * 128:1024 + (cc + 1) * 128, :], reads=[D.zs_d], writes=[Kk])
            kb.dma("sp", Vv[:], D.zs_d.ap()[2048 + cc * 128:2048 + (cc + 1) * 128, :], reads=[D.zs_d], writes=[Vv])
            kb.dma("pool", lnw[:], D.lnx_w.ap()[cc * 128:(cc + 1) * 128].partition_broadcast(128), writes=[lnw])
            kb.dma("pool", lnb[:], D.lnx_b.ap()[cc * 128:(cc + 1) * 128].partition_broadcast(128), writes=[lnb])
            for g in range(4):
                for j in range(4):
                    tt = g * 4 + j
                    kb.op("pe", lambda e: e.transpose(out=psX[:, j * 128:(j + 1) * 128], in_=Vv[:, tt * 128:(tt + 1) * 128], identity=C.ident[:]),
                          reads=[Vv, C.ident], writes=[psX])
                kb.op("act", lambda e: e.copy(out=fr(Vtm[:, g * 4:(g + 1) * 4, :]), in_=psX[:].rearrange("p (j c) -> p j c", j=4)), reads=[psX], writes=[Vtm])
            kb.op("pool", lambda e: e.tensor_scalar(out=KKn[:], in0=Kk[:], scalar1=rw[:, cc, 4:5], scalar2=None, op0=ALU.mult),
                  reads=[Kk, rw], writes=[KKn])
            kb.op("act", lambda e: e.activation(out=XI[:], in_=KKn[:], func=AF.Square), reads=[KKn], writes=[XI])
            for tb in range(4):
                kb.op("pe", lambda e: e.matmul(psX[:], lhsT=masks[:, 5, :], rhs=XI[:, tb * 512:(tb + 1) * 512], start=True, stop=True),
                      reads=[masks, XI], writes=[psX])
                kb.op("act", lambda e: e.activation(out=XE[:, tb * 512:(tb + 1) * 512], in_=psX[:], func=AF.Sqrt), reads=[psX], writes=[XE])
            kb.op("dve", lambda e: e.tensor_scalar(out=XE[:], in0=XE[:], scalar1=1e-12, scalar2=None, op0=ALU.max), reads=[XE], writes=[XE])
            kb.op("dve", lambda e: e.reciprocal(out=XE[:], in_=XE[:]), reads=[XE], writes=[XE])
            kb.op("pool", lambda e: e.tensor_tensor(out=KKn[:], in0=KKn[:], in1=XE[:], op=ALU.mult), reads=[KKn, XE], writes=[KKn])
            chk(1)
            for d in range(2):
                kb.dma("sp", XE[:], D.ld_d.ap()[d, cc * 128:(cc + 1) * 128, :], reads=[D.ld_d], writes=[XE])
                kb.dma("act", Aa[:], D.a_d.ap()[d, cc * 128:(cc + 1) * 128, :], reads=[D.a_d], writes=[Aa])
                kb.op("dve", lambda e: e.tensor_scalar(out=KD[:], in0=Aa[:], scalar1=-1.0, scalar2=rw[:, cc, 5:6], op0=ALU.add, op1=ALU.mult),
                      reads=[Aa, rw], writes=[KD])
                kb.op("dve", lambda e: e.scalar_tensor_tensor(out=KD[:], in0=KD[:], scalar=1.0, in1=Kk[:], op0=ALU.add, op1=ALU.mult),
                      reads=[KD, Kk], writes=[KD])
                kb.op("dve", lambda e: e.scalar_tensor_tensor(out=XI[:], in0=KD[:], scalar=rw[:, cc, 6:7], in1=Rr[:], op0=ALU.mult, op1=ALU.mult),
                      reads=[KD, rw, Rr], writes=[XI])
                for tt in range(NT):
                    kb.op("pe", lambda e: e.matmul(psX[:, tt * 2:tt * 2 + 2], lhsT=XI[:, tt * 128:(tt + 1) * 128], rhs=sel, start=True, stop=True),
                          reads=[XI, cst], writes=[psX])
                kb.op("act", lambda e: e.copy(out=stt[:, 64 + 32 * d:96 + 32 * d], in_=psX[:, 0:32]), reads=[psX], writes=[stt])
                kb.op("pool", lambda e: e.tensor_tensor(out=Aa[:], in0=Aa[:], in1=KKn[:], op=ALU.mult), reads=[Aa, KKn], writes=[Aa])
                kb.op("dve", lambda e: e.tensor_tensor_scan(out=XI[:], data0=segt[:], data1=XE[:], initial=0.0, op0=ALU.mult, op1=ALU.add),
                      reads=[segt, XE], writes=[XI])
                kb.op("pool", lambda e: e.tensor_copy(out=tot[:], in_=XI[:].rearrange("p (n s) -> p n s", s=64)[:, :, 63]), reads=[XI], writes=[tot])
                kb.op("act", lambda e: e.activation(out=GL[:], in_=tot[:], func=AF.Exp), reads=[tot], writes=[GL])
                if d == 0:
                    kb.op("pool", lambda e: e.tensor_tensor(out=XE[:], in0=XI[:], in1=XE[:], op=ALU.subtract), reads=[XI, XE], writes=[XE])
                else:
                    kb.op("dve", lambda e: e.tensor_tensor(out=XI[:].rearrange("p (n s) -> p n s", s=64),
                                                           in0=tot[:, :, None].to_broadcast([128, 32, 64]),
                                                           in1=XI[:].rearrange("p (n s) -> p n s", s=64), op=ALU.subtract),
                          reads=[tot, XI], writes=[XI])
                    kb.op("pool", lambda e: e.tensor_tensor(out=XE[:], in0=XI[:], in1=XE[:], op=ALU.add), reads=[XI, XE], writes=[XE])
                cI, cE = (XI, XE) if d == 0 else (XE, XI)
                kb.op("act", lambda e: e.activation(out=fr(E1[:]), in_=cI[:], func=AF.Exp), reads=[cI], writes=[E1])
                kb.op("act", lambda e: e.activation(out=cI[:], in_=cI[:], func=AF.Exp, scale=-1.0), reads=[cI], writes=[cI])
                kb.op("act", lambda e: e.activation(out=cE[:], in_=cE[:], func=AF.Exp), reads=[cE], writes=[cE])
                kb.op("dve", lambda e: e.tensor_tensor(out=fr(E1[:]), in0=E1[:], in1=Rr[:], op=ALU.mult), reads=[E1, Rr], writes=[E1])
                kb.op("pool", lambda e: e.tensor_tensor(out=fr(KT[:]), in0=KD[:], in1=cI[:], op=ALU.mult), reads=[KD, cI], writes=[KT])
                kb.op("dve", lambda e: e.tensor_tensor(out=fr(AT[:]), in0=Aa[:], in1=cI[:], op=ALU.mult), reads=[Aa, cI], writes=[AT])
                kb.op("dve", lambda e: e.scalar_tensor_tensor(out=fr(BTb[:]), in0=cE[:], scalar=-1.0, in1=KKn[:], op0=ALU.mult, op1=ALU.mult),
                      reads=[cE, KKn], writes=[BTb])
                kb.op("pool", lambda e: e.tensor_tensor(out=cI[:].rearrange("p (n s) -> p n s", s=64), in0=cI[:].rearrange("p (n s) -> p n s", s=64),
                                                        in1=GL[:, :, None].to_broadcast([128, 32, 64]), op=ALU.mult), reads=[cI, GL], writes=[cI])
                kb.op("dve", lambda e: e.tensor_tensor(out=KD[:], in0=KD[:], in1=cI[:], op=ALU.mult), reads=[KD, cI], writes=[KD])
                kb.op("pool", lambda e: e.tensor_tensor(out=Aa[:], in0=Aa[:], in1=cI[:], op=ALU.mult), reads=[Aa, cI], writes=[Aa])
                chk(2)
                run_tiles(d, E1, BTb)
                chk(8)
                kb.op("pool", lambda e: e.memset(H[0][:], 0.0), writes=[H[0]])
                order = range(32) if d == 0 else range(31, -1, -1)
                hc = 0
                for ch in order:
                    tt, n = ch // 2, ch % 2
                    ccols = slice(ch * 64, ch * 64 + 64)
                    Hc, Hn = H[hc], H[1 - hc]
                    tr = slice(n * 64, n * 64 + 64)
                    for hh in range(2):
                        pr = slice(hh * 64, hh * 64 + 64)
                        bk = bkH[hh]
                        kb.op("pe", lambda e: e.matmul(bk[pr, 64:128], lhsT=MTa[pr, ch, :], rhs=Hc[pr, :], start=True, stop=True),
                              reads=[MTa, Hc], writes=[bk])
                        kb.op("pe", lambda e: e.matmul(bk[tr, 0:64], lhsT=RhT[pr, ccols], rhs=Hc[pr, :], start=True, stop=True),
                              reads=[RhT, Hc], writes=[bk])
                    for hh in range(2):
                        pr = slice(hh * 64, hh * 64 + 64)
                        bk = bkH[hh]
                        kb.op("dve", lambda e: e.tensor_tensor(out=Hn[pr, :], in0=bk[pr, 64:128], in1=Ca[pr, ch, :], op=ALU.add),
                              reads=[bk, Ca], writes=[Hn])
                    for hh in range(2):
                        pr = slice(hh * 64, hh * 64 + 64)
                        bk = bkH[hh]
                        kb.op("act" if False else "dve", lambda e: e.tensor_tensor(out=Ysum[tr, tt, pr], in0=bk[tr, 0:64], in1=Ysum[tr, tt, pr], op=ALU.add),
                              reads=[bk, Ysum], writes=[Ysum])
                    hc = 1 - hc
                chk(9)
            chk(10)
            if dbg is not None and "Ysum" in dbg:
                kb.dma("sp", D.dbgY.ap()[:, cc * 128:(cc + 1) * 128].rearrange("(tt p) c -> p tt c", p=128), Ysum[:], reads=[Ysum], writes=[D.dbgY])
            Y3 = Ysum[:].rearrange("p t (h i) -> p (t h) i", h=2)
            st_mu = stt[:, 0:32]
            st_var = stt[:, 32:64]
            bon = stt[:, 128:160]
            kb.op("dve", lambda e: e.tensor_reduce(out=st_mu, in_=Y3, axis=AX.X, op=ALU.add), reads=[Ysum], writes=[stt])
            kb.op("dve", lambda e: e.tensor_scalar(out=st_mu, in0=st_mu, scalar1=1.0 / 64, scalar2=None, op0=ALU.mult), reads=[stt], writes=[stt])
            kb.op("dve", lambda e: e.tensor_tensor(out=Y3, in0=Y3, in1=st_mu[:, :, None].to_broadcast([128, 32, 64]), op=ALU.subtract),
                  reads=[Ysum, stt], writes=[Ysum])
            sqv = XI[:].rearrange("p (a i) -> p a i", i=64)
            kb.op("act", lambda e: e.activation(out=sqv, in_=Y3, func=AF.Square), reads=[Ysum], writes=[XI])
            kb.op("dve", lambda e: e.tensor_reduce(out=st_var, in_=sqv, axis=AX.X, op=ALU.add), reads=[XI], writes=[stt])
            kb.op("dve", lambda e: e.tensor_scalar(out=st_var, in0=st_var, scalar1=1.0 / 64, scalar2=GN_EPS, op0=ALU.mult, op1=ALU.add),
                  reads=[stt], writes=[stt])
            kb.op("act", lambda e: e.activation(out=st_var, in_=st_var, func=AF.Sqrt), reads=[stt], writes=[stt])
            kb.op("dve", lambda e: e.reciprocal(out=st_var, in_=st_var), reads=[stt], writes=[stt])
            kb.op("dve", lambda e: e.tensor_tensor(out=Y3, in0=Y3, in1=st_var[:, :, None].to_broadcast([128, 32, 64]), op=ALU.mult),
                  reads=[Ysum, stt], writes=[Ysum])
            kb.op("pool", lambda e: e.tensor_tensor(out=Ysum[:], in0=Ysum[:], in1=lnw[:, None, :].to_broadcast([128, NT, 128]), op=ALU.mult),
                  reads=[Ysum, lnw], writes=[Ysum])
            kb.op("pool", lambda e: e.tensor_tensor(out=Ysum[:], in0=Ysum[:], in1=lnb[:, None, :].to_broadcast([128, NT, 128]), op=ALU.add),
                  reads=[Ysum, lnb], writes=[Ysum])
            kb.op("dve", lambda e: e.tensor_tensor(out=bon, in0=stt[:, 64:96], in1=stt[:, 96:128], op=ALU.add), reads=[stt], writes=[stt])
            kb.op("dve", lambda e: e.tensor_scalar(out=bon, in0=bon, scalar1=0.5, scalar2=None, op0=ALU.mult), reads=[stt], writes=[stt])
            V3 = Vtm[:].rearrange("p t (h i) -> p (t h) i", h=2)
            kb.op("pool", lambda e: e.tensor_tensor(out=sqv, in0=V3, in1=bon[:, :, None].to_broadcast([128, 32, 64]), op=ALU.mult),
                  reads=[Vtm, stt], writes=[XI])
            kb.op("pool", lambda e: e.tensor_tensor(out=Y3, in0=Y3, in1=sqv, op=ALU.add), reads=[Ysum, XI], writes=[Ysum])
            gt = XE[:].rearrange("p (t c) -> p t c", c=128)
            kb.dma("sp", gt, D.g_d.ap()[:, cc * 128:(cc + 1) * 128].rearrange("(tt p) c -> p tt c", p=128), reads=[D.g_d], writes=[XE])
            kb.op("dve", lambda e: e.tensor_tensor(out=Ysum[:], in0=Ysum[:], in1=gt, op=ALU.mult), reads=[Ysum, XE], writes=[Ysum])
            if dbg is not None and "yfin" in dbg:
                kb.dma("sp", D.dbgF.ap()[:, cc * 128:(cc + 1) * 128].rearrange("(tt p) c -> p tt c", p=128), Ysum[:], reads=[Ysum], writes=[D.dbgF])
            for g in range(4):
                for j in range(4):
                    tt = g * 4 + j
                    kb.op("pe", lambda e: e.transpose(out=psX[:, j * 128:(j + 1) * 128], in_=Ysum[:, tt, :], identity=C.ident[:]),
                          reads=[Ysum, C.ident], writes=[psX])
                kb.op("act", lambda e: e.copy(out=KD[:, g * 512:(g + 1) * 512], in_=psX[:]), reads=[psX], writes=[KD])
            yb = Aa[:].bitcast(BF16)[:, 0:2048]
            kb.op("dve", lambda e: e.tensor_copy(out=yb, in_=KD[:]), reads=[KD], writes=[Aa])
            kb.dma("sp", D.ymT_d.ap()[cc], yb, reads=[Aa], writes=[D.ymT_d])
        kb.barrier()


def host_consts():
    S = 2048
    rows = np.repeat(np.arange(32), 64).astype(np.float32)
    cols = np.tile(np.arange(64), 32).astype(np.float32)
    inv = (10000.0 ** (-np.arange(0, 32, 2, dtype=np.float32) / 32)).astype(np.float32)
    ar = rows[:, None] * inv[None]
    ac = cols[:, None] * inv[None]
    tabC = np.concatenate([np.cos(ar), np.cos(ac)], 1).astype(np.float32)
    tabS = np.concatenate([np.sin(ar), np.sin(ac)], 1).astype(np.float32)
    ident = np.eye(128, dtype=np.float32)
    iota = np.tile(np.arange(128, dtype=np.float32)[None], (128, 1))
    r = np.arange(128)[:, None]
    s = np.arange(128)[None, :]
    same = (r // 64) == (s // 64)
    masks = np.zeros((128, 6, 128), np.float32)
    masks[:, 0] = same & (r < s)
    masks[:, 1] = same & (r <= s)
    masks[:, 2] = same & (r > s)
    masks[:, 3] = same & (r >= s)
    masks[:, 4] = same & (r > s)
    masks[:, 5] = same
    rcst = np.zeros((128, 66 + 2048), np.float32)
    pp = np.arange(128)
    rcst[pp, pp % 64] = 1.0
    rcst[:, 64] = (pp // 64 == 0)
    rcst[:, 65] = (pp // 64 == 1)
    seg = np.ones(2048, np.float32); seg[::64] = 0.0
    rcst[:, 66:] = seg[None]
    import ml_dtypes
    segm = np.ascontiguousarray(np.broadcast_to(seg[None], (128, 2048))).astype(ml_dtypes.bfloat16)
    return dict(tabC=tabC, tabS=tabS, ident=ident, iota=iota, masks=masks, rcst=rcst, segm=segm)

def prep_shared(inp):
    L = 0
    d = host_consts()
    mu_c = np.zeros((128, 3 * NRCH), np.float32)
    for ci, (c0, cs) in enumerate(RCH):
        mu_c[:cs, ci] = inp["mu_prev"][L, c0:c0 + cs]
        mu_c[:cs, NRCH + ci] = inp["mu_next"][L, c0:c0 + cs]
    d["mu_c"] = mu_c
    d["w_in"] = np.ascontiguousarray(inp["w_in"][L])
    for k in ["norm1_w", "norm2_w", "q_norm_w", "k_norm_w", "w_out", "w_pq", "w2", "a2", "g2", "lnx_w", "lnx_b", "v_tab"]:
        d[k] = np.ascontiguousarray(inp[k][L])
    d["normf_w"] = np.ascontiguousarray(inp["normf_w"])
    d["skT"] = np.ascontiguousarray(inp["sub_keys"][L].reshape(16, 128, 128).transpose(2, 0, 1))
    d["u_tabT"] = np.ascontiguousarray(inp["u_tab"][L].reshape(128, 128, 16, 128).transpose(0, 3, 2, 1)).reshape(128, 128, 2048)
    rw = np.zeros((128, 8, 8), np.float32)
    def ch(v):
        return v.reshape(8, 128).T
    rw[:, :, 0] = ch(inp["w0"][L, 0]); rw[:, :, 1] = ch(inp["w0"][L, 1])
    rw[:, :, 2] = ch(inp["a0"][L, 0]); rw[:, :, 3] = ch(inp["a0"][L, 1])
    rw[:, :, 4] = ch(inp["k_k"][L]); rw[:, :, 5] = ch(inp["k_a"][L]); rw[:, :, 6] = ch(inp["r_k"][L].reshape(-1))
    d["rw_c"] = rw
    return d


def build_program():
    nc = bass.Bass("TRN2", target_bir_lowering=False)
    kb = KB(nc)
    D = declare(kb, None)
    C = consts(kb, D)
    phase_A(kb, C, D)
    phase_Rpre(kb, C, D)
    phase_R(kb, C, D)
    phase_T(kb, C, D)
    phase_O(kb, C, D, D.ymT_d)
    phase_P(kb, C, D)
    phase_F(kb, C, D)
    kb.finish("sp")
    return nc


def kernel(**inputs):
    inp = {k: np.asarray(v) for k, v in inputs.items()}
    shared = prep_shared(inp)
    nc = build_program()
    in_maps = []
    for b in range(8):
        d = dict(shared)
        d["x"] = np.ascontiguousarray(inp["x"][b])
        in_maps.append(d)
    res = run_bass_kernel_spmd(nc, in_maps, core_ids=list(range(8)))
    out = np.stack([np.asarray(r["out"], dtype=np.float32) for r in res.results], axis=0)
    return out
```

```python
import numpy as np
import contextlib
import concourse.bass as bass
import concourse.mybir as mybir
from concourse.bass_utils import run_bass_kernel_spmd

F32 = mybir.dt.float32
BF16 = mybir.dt.bfloat16
U32 = mybir.dt.uint32
ALU = mybir.AluOpType
AF = mybir.ActivationFunctionType
AX = mybir.AxisListType


class T:
    def __init__(self, h, name):
        self.h = h
        self.name = name
        self.w = None
        self.r = {}
        self.dsem = None
        self.dcnt = 0
        self.is_psum = False

    def __getitem__(self, k):
        return self.h[k]

    def ap(self):
        return self.h.ap() if hasattr(self.h, "ap") else self.h[:]


class KB:
    def __init__(self, nc):
        self.nc = nc
        self.es = contextlib.ExitStack()
        self.engs = {"pe": nc.tensor, "act": nc.scalar, "dve": nc.vector, "pool": nc.gpsimd, "sp": nc.sync}
        self.sem = {}
        self.cnt = {}
        for e in self.engs:
            self.sem[e] = self.es.enter_context(nc.semaphore("s_" + e))
            self.cnt[e] = 0
        self.waited = {}
        self.alltensors = []
        self.dsems = []
        self.n_ins = 0

    def sb(self, name, shape, dt=F32, stack=None):
        h = (stack or self.es).enter_context(self.nc.sbuf_tensor(name, list(shape), dt))
        t = T(h, name)
        return t

    def ps(self, name, shape, dt=F32, stack=None):
        h = (stack or self.es).enter_context(self.nc.psum_tensor(name, list(shape), dt))
        t = T(h, name)
        t.is_psum = True
        return t

    def dram(self, name, shape, dt=F32, kind="Internal"):
        h = self.nc.dram_tensor(name, list(shape), dt, kind=kind)
        return T(h, name)

    def view(self, t, name=None):
        return t

    def _wait(self, eng, tok):
        if tok is None:
            return
        sem, val = tok
        key = (eng, id(sem))
        if self.waited.get(key, 0) >= val:
            return
        self.engs[eng].wait_ge(sem, val)
        self.waited[key] = val

    def _deps(self, eng, reads, writes):
        own = id(self.sem[eng])
        for t in reads:
            if t.w is not None:
                if eng == "pe" and id(t.w[0]) == own:
                    continue
                self._wait(eng, t.w)
        for t in writes:
            if t.w is not None and id(t.w[0]) != own:
                self._wait(eng, t.w)
            for k, tok in t.r.items():
                if k == own:
                    continue
                self._wait(eng, tok)

    def _mark(self, tok, reads, writes):
        for t in reads:
            if t in writes:
                continue
            t.r[id(tok[0])] = tok
        for t in writes:
            t.w = tok
            t.r = {}

    def op(self, eng, fn, reads=(), writes=()):
        psr = [t for t in reads if t.is_psum and t not in writes]
        if psr:
            writes = list(writes) + psr
        self._deps(eng, reads, writes)
        ins = fn(self.engs[eng])
        self.cnt[eng] += 1
        ins.then_inc(self.sem[eng], 1)
        tok = (self.sem[eng], self.cnt[eng])
        self._mark(tok, reads, writes)
        self.n_ins += 1
        return ins

    def dma(self, q, out_ap, in_ap, reads=(), writes=(), **kw):
        assert len(writes) == 1
        dst = writes[0]
        if dst.dsem is None:
            dst.dsem = self.es.enter_context(self.nc.semaphore("d_" + dst.name))
            self.dsems.append(dst)
        self._deps(q, reads, [])
        if dst.w is not None and dst.w[0] is not dst.dsem:
            self._wait(q, dst.w)
        for k, tok in dst.r.items():
            self._wait(q, tok)
        ins = self.engs[q].dma_start(out=out_ap, in_=in_ap, **kw)
        dst.dcnt += 16
        ins.then_inc(dst.dsem, 16)
        tok = (dst.dsem, dst.dcnt)
        for t in reads:
            t.r[id(tok[0])] = tok
        dst.w = tok
        dst.r = {}
        self.n_ins += 1
        return ins

    def pe_fence(self):
        if self.cnt["pe"] > 0:
            self._wait("pe", (self.sem["pe"], self.cnt["pe"]))

    def barrier(self):
        toks = [(self.sem[e], self.cnt[e]) for e in self.engs if self.cnt[e] > 0]
        toks += [(t.dsem, t.dcnt) for t in self.dsems if t.dcnt > 0]
        for e in self.engs:
            for tok in toks:
                if tok[0] is self.sem[e]:
                    continue
                self._wait(e, tok)

    def finish(self, eng="sp"):
        for t in self.dsems:
            if t.dcnt > 0:
                self._wait(eng, (t.dsem, t.dcnt))
        for e in self.engs:
            if e != eng and self.cnt[e] > 0:
                self._wait(eng, (self.sem[e], self.cnt[e]))


EPS = 1e-6
NT = 16
RCH = [(i * 128, 128) for i in range(24)] + [(3072 + i * 96, 96) for i in range(4)] + [(3456, 128), (3584, 128)]
NRCH = len(RCH)
RW = 3712


class Obj:
    pass


def declare(kb, debug):
    D = Obj()

    def inp(name, shape, dt=F32):
        setattr(D, name, kb.dram(name, shape, dt, kind="ExternalInput"))

    def scr(name, shape, dt=F32):
        kind = "ExternalOutput" if (debug and name in debug) else "Internal"
        setattr(D, name, kb.dram(name, shape, dt, kind=kind))

    inp("x", [2048, 2048])
    inp("w_in", [128, 16 * 5248])
    inp("mu_c", [128, 3 * NRCH])
    inp("norm1_w", [2048])
    inp("norm2_w", [2048])
    inp("normf_w", [2048])
    inp("q_norm_w", [64])
    inp("k_norm_w", [64])
    inp("tabC", [2048, 32])
    inp("tabS", [2048, 32])
    inp("ident", [128, 128])
    inp("w_out", [2048, 2048])
    inp("w_pq", [2048, 2048])
    inp("skT", [128, 16, 128])
    inp("iota", [128, 128])
    inp("u_tabT", [128, 128, 2048])
    inp("v_tab", [16384, 2048])
    inp("w2", [2, 96, 1024])
    inp("a2", [2, 96, 1024])
    inp("g2", [256, 1024])
    inp("rw_c", [128, 8, 8])
    inp("lnx_w", [1024])
    inp("lnx_b", [1024])
    inp("masks", [128, 6, 128])
    if debug and "in_ymT" in debug:
        inp("ymT_in", [16, 128, 2048], BF16)
    if debug and "in_x1" in debug:
        inp("x1_in", [2048, 2048])
    scr("zs_d", [RW, 2048])
    scr("qkv_d", [2048, 1536])
    scr("ymT_d", [16, 128, 2048], BF16)
    scr("x1_d", [2048, 2048])
    scr("G_d", [128, 128, 2048], BF16)
    scr("peT_d", [2048, 2048])
    scr("A_d", [128, 128, 2048], BF16)
    scr("ld_d", [2, 1024, 2048])
    scr("a_d", [2, 1024, 2048])
    scr("g_d", [2048, 1024])
    inp("rcst", [128, 66 + 2048])
    inp("segm", [128, 2048], BF16)
    if debug and "dbgY" in debug:
        setattr(D, "dbgY", kb.dram("dbgY", [2048, 1024], F32, kind="ExternalOutput"))
        setattr(D, "dbgF", kb.dram("dbgF", [2048, 1024], F32, kind="ExternalOutput"))
    scr("S_d", [2048, 16, 128])
    scr("h2T_d", [16, 128, 2048], BF16)
    setattr(D, "out", kb.dram("out", [2048, 2048], F32, kind="ExternalOutput"))
    return D


def consts(kb, D):
    C = Obj()
    C.ident = kb.sb("c_ident", [128, 128], F32)
    kb.dma("sp", C.ident[:], D.ident.ap(), writes=[C.ident])
    C.identb = kb.sb("c_identb", [128, 128], BF16)
    kb.op("dve", lambda e: e.tensor_copy(out=C.identb[:], in_=C.ident[:]), reads=[C.ident], writes=[C.identb])
    return C


def norm_to_T(kb, C, src, wvec, hT, tag):
    with contextlib.ExitStack() as st:
        wb = kb.sb(tag + "wb", [128, 2048], F32, st)
        kb.dma("pool", wb[:], wvec.ap().partition_broadcast(128), writes=[wb])
        xts = [kb.sb(f"{tag}x{i}", [128, 2048], F32, st) for i in range(2)]
        junk = kb.sb(tag + "junk", [128, 2048], BF16, st)
        hb = [kb.sb(f"{tag}h{i}", [128, 2048], BF16, st) for i in range(2)]
        ss = kb.sb(tag + "ss", [128, NT], F32, st)
        rs = kb.sb(tag + "rs", [128, NT], F32, st)
        ptr = [kb.ps(f"{tag}ps{i}", [128, 1024], BF16, st) for i in range(2)]
        kb.op("pool", lambda e: e.memset(ss[:], 0.0), writes=[ss])
        for tt in range(NT):
            xt = xts[tt % 2]
            kb.dma("sp", xt[:], src.ap()[tt * 128:(tt + 1) * 128, :], writes=[xt])
            kb.op("act", lambda e: e.activation(out=junk[:], in_=xt[:], func=AF.Square, accum_out=ss[:, tt:tt + 1]),
                  reads=[xt], writes=[junk, ss])
            kb.op("dve", lambda e: e.tensor_scalar(out=rs[:, tt:tt + 1], in0=ss[:, tt:tt + 1], scalar1=1.0 / 2048, scalar2=EPS,
                                                   op0=ALU.mult, op1=ALU.add), reads=[ss], writes=[rs])
            kb.op("act", lambda e: e.activation(out=rs[:, tt:tt + 1], in_=rs[:, tt:tt + 1], func=AF.Sqrt), reads=[rs], writes=[rs])
            kb.op("dve", lambda e: e.reciprocal(out=rs[:, tt:tt + 1], in_=rs[:, tt:tt + 1]), reads=[rs], writes=[rs])
            h = hb[tt % 2]
            kb.op("dve", lambda e: e.scalar_tensor_tensor(out=h[:], in0=xt[:], scalar=rs[:, tt:tt + 1], in1=wb[:],
                                                          op0=ALU.mult, op1=ALU.mult), reads=[xt, rs, wb], writes=[h])
            for g in range(2):
                p = ptr[g]
                for j in range(8):
                    dc = g * 8 + j
                    kb.op("pe", lambda e: e.transpose(out=p[:, j * 128:(j + 1) * 128], in_=h[:, dc * 128:(dc + 1) * 128],
                                                      identity=C.identb[:]), reads=[h, C.identb], writes=[p])
                eng = "act" if g == 0 else "dve"
                src_ap = p[:].rearrange("p (j t) -> p j t", j=8)
                dst_ap = hT[:, g * 8:(g + 1) * 8, tt * 128:(tt + 1) * 128]
                if eng == "act":
                    kb.op("act", lambda e: e.copy(out=dst_ap, in_=src_ap), reads=[p], writes=[hT])
                else:
                    kb.op("dve", lambda e: e.tensor_copy(out=dst_ap, in_=src_ap), reads=[p], writes=[hT])
        kb.barrier()


def phase_A(kb, C, D):
    with contextlib.ExitStack() as st:
        hT = kb.sb("hT", [128, 16, 2048], BF16, st)
        norm_to_T(kb, C, D.x, D.norm1_w, hT, "n1")
        mu = kb.sb("mu", [128, 3 * NRCH], F32, st)
        kb.dma("sp", mu[:], D.mu_c.ap(), writes=[mu])
        kb.op("dve", lambda e: e.tensor_tensor(out=mu[:, 60:90], in0=mu[:, 0:30], in1=mu[:, 30:60], op=ALU.add), reads=[mu], writes=[mu])
        kb.op("dve", lambda e: e.tensor_scalar(out=mu[:, 60:90], in0=mu[:, 60:90], scalar1=-1.0, scalar2=1.0, op0=ALU.mult, op1=ALU.add),
              reads=[mu], writes=[mu])
        stg = [kb.sb(f"a_stg{i}", [128, 4096], F32, st) for i in range(2)]
        wbf = [kb.sb(f"a_wbf{i}", [128, 16, 128], BF16, st) for i in range(2)]
        accs = [kb.sb(f"a_acc{i}", [128, 2048], F32, st) for i in range(2)]
        pss = [[kb.ps(f"a_ps{i}_{j}", [128, 512], F32, st) for j in range(4)] for i in range(2)]
        for ci, (c0, cs) in enumerate(RCH):
            sg = stg[ci % 2]
            sgv = sg[:, 0:16 * cs].rearrange("p (dc c) -> p dc c", dc=16)
            kb.dma("sp", sg[:, 0:16 * cs], D.w_in.ap()[:, 16 * c0:16 * (c0 + cs)], writes=[sg])
            wb = wbf[ci % 2]
            kb.op("pool", lambda e: e.tensor_copy(out=wb[:, :, 0:cs], in_=sgv), reads=[sg], writes=[wb])
            ps = pss[ci % 2]
            acc = accs[ci % 2]
            for tb in range(4):
                for dc in range(16):
                    kb.op("pe", lambda e: e.matmul(ps[tb][0:cs, :], lhsT=wb[:, dc, 0:cs], rhs=hT[:, dc, tb * 512:(tb + 1) * 512],
                                                   start=(dc == 0), stop=(dc == 15)), reads=[wb, hT], writes=[ps[tb]])
            for tb in range(4):
                kb.op("act", lambda e: e.activation(out=acc[0:cs, tb * 512:(tb + 1) * 512], in_=ps[tb][0:cs, :], func=AF.Copy,
                                                    scale=mu[0:cs, 60 + ci:61 + ci]), reads=[ps[tb], mu], writes=[acc])
            for tb in range(4):
                n = 512 if tb < 3 else 511
                d0 = tb * 512 + 1
                kb.op("dve", lambda e: e.scalar_tensor_tensor(out=acc[0:cs, d0:d0 + n], in0=ps[tb][0:cs, 0:n], scalar=mu[0:cs, ci:ci + 1],
                                                              in1=acc[0:cs, d0:d0 + n], op0=ALU.mult, op1=ALU.add),
                      reads=[ps[tb], mu, acc], writes=[acc])
                s0 = 1 if tb == 0 else 0
                n = 512 - s0
                d0 = tb * 512 + s0 - 1
                kb.op("dve", lambda e: e.scalar_tensor_tensor(out=acc[0:cs, d0:d0 + n], in0=ps[tb][0:cs, s0:512], scalar=mu[0:cs, 30 + ci:31 + ci],
                                                              in1=acc[0:cs, d0:d0 + n], op0=ALU.mult, op1=ALU.add),
                      reads=[ps[tb], mu, acc], writes=[acc])
            kb.dma("sp", D.zs_d.ap()[c0:c0 + cs, :], acc[0:cs, :], reads=[acc], writes=[D.zs_d])
        kb.barrier()
        with contextlib.ExitStack() as st2:
            wq = kb.sb("a_wq", [128, 16, 512], BF16, st2)
            ev = [kb.sb(f"a_ev{i}", [128, 512], F32, st2) for i in range(2)]
            for cg in range(3):
                c0 = RW + cg * 512
                for hf in range(2):
                    sg = stg[hf]
                    sgv = sg[:].rearrange("p (dc c) -> p dc c", dc=8)
                    o0 = 16 * RW + (cg * 2 + hf) * 4096
                    kb.dma("sp", sg[:], D.w_in.ap()[:, o0:o0 + 4096], writes=[sg])
                    kb.op("pool", lambda e: e.tensor_copy(out=wq[:, hf * 8:(hf + 1) * 8, :], in_=sgv), reads=[sg], writes=[wq])
                for tt in range(NT):
                    ps = pss[tt % 2][0]
                    for dc in range(16):
                        kb.op("pe", lambda e: e.matmul(ps[:], lhsT=hT[:, dc, tt * 128:(tt + 1) * 128], rhs=wq[:, dc, :],
                                                       start=(dc == 0), stop=(dc == 15)), reads=[wq, hT], writes=[ps])
                    o = ev[tt % 2]
                    kb.op("act", lambda e: e.copy(out=o[:], in_=ps[:]), reads=[ps], writes=[o])
                    kb.dma("sp", D.qkv_d.ap()[tt * 128:(tt + 1) * 128, cg * 512:(cg + 1) * 512], o[:], reads=[o], writes=[D.qkv_d])
            kb.barrier()


def phase_T(kb, C, D):
    with contextlib.ExitStack() as st:
        qT = kb.sb("t_qT", [128, 8, 2048], BF16, st)
        kT2 = kb.sb("t_kT2", [128, 4, 2048], BF16, st)
        vaug = kb.sb("t_vaug", [128, NT, 4, 65], BF16, st)
        wqk = kb.sb("t_wqk", [128, 20, 64], F32, st)
        w64 = kb.sb("t_w64", [128, 2, 64], F32, st)
        kb.dma("pool", w64[:, 0, :], D.q_norm_w.ap().partition_broadcast(128), writes=[w64])
        kb.dma("pool", w64[:, 1, :], D.k_norm_w.ap().partition_broadcast(128), writes=[w64])
        kb.op("dve", lambda e: e.tensor_scalar(out=wqk[:, 0:16, :], in0=w64[:, 0:1, :].to_broadcast([128, 16, 64]), scalar1=0.125, scalar2=None,
                                               op0=ALU.mult), reads=[w64], writes=[wqk])
        kb.op("dve", lambda e: e.tensor_copy(out=wqk[:, 16:20, :], in_=w64[:, 1:2, :].to_broadcast([128, 4, 64])), reads=[w64], writes=[wqk])
        kb.op("pool", lambda e: e.memset(vaug[:], 1.0), writes=[vaug])
        with contextlib.ExitStack() as st2:
            qk = [kb.sb(f"t_qk{i}", [128, 1536], F32, st2) for i in range(2)]
            tC = [kb.sb(f"t_tC{i}", [128, 2, 16], F32, st2) for i in range(2)]
            tS = [kb.sb(f"t_tS{i}", [128, 2, 16], F32, st2) for i in range(2)]
            sq = kb.sb("t_sq", [128, 20, 64], F32, st2)
            ss = kb.sb("t_ss", [128, 20], F32, st2)
            qn = kb.sb("t_qn", [128, 20, 64], F32, st2)
            t1 = kb.sb("t_t1", [128, 20, 2, 16], F32, st2)
            t2 = kb.sb("t_t2", [128, 20, 2, 16], F32, st2)
            t3 = kb.sb("t_t3", [128, 20, 2, 16], F32, st2)
            t4 = kb.sb("t_t4", [128, 20, 2, 16], F32, st2)
            qkr = kb.sb("t_qkr", [128, 20, 64], BF16, st2)
            kd = kb.sb("t_kd", [128, 4, 2, 64], BF16, st2)
            pq = kb.ps("t_pq", [128, 1024], BF16, st2)
            pk = kb.ps("t_pk", [128, 1024], BF16, st2)
            for tt in range(NT):
                q = qk[tt % 2]
                cC = tC[tt % 2]
                cS = tS[tt % 2]
                kb.dma("sp", q[:], D.qkv_d.ap()[tt * 128:(tt + 1) * 128, :], reads=[D.qkv_d], writes=[q])
                kb.dma("act", cC[:].rearrange("p a b -> p (a b)"), D.tabC.ap()[tt * 128:(tt + 1) * 128, :], writes=[cC])
                kb.dma("act", cS[:].rearrange("p a b -> p (a b)"), D.tabS.ap()[tt * 128:(tt + 1) * 128, :], writes=[cS])
                qv = q[:, 0:1280].rearrange("p (h d) -> p h d", h=20)
                kb.op("act", lambda e: e.activation(out=sq[:], in_=qv, func=AF.Square), reads=[q], writes=[sq])
                kb.op("dve", lambda e: e.tensor_reduce(out=ss[:], in_=sq[:], axis=AX.X, op=ALU.add), reads=[sq], writes=[ss])
                kb.op("dve", lambda e: e.tensor_scalar(out=ss[:], in0=ss[:], scalar1=1.0 / 64, scalar2=EPS, op0=ALU.mult, op1=ALU.add),
                      reads=[ss], writes=[ss])
                kb.op("act", lambda e: e.activation(out=ss[:], in_=ss[:], func=AF.Sqrt), reads=[ss], writes=[ss])
                kb.op("dve", lambda e: e.reciprocal(out=ss[:], in_=ss[:]), reads=[ss], writes=[ss])
                kb.op("dve", lambda e: e.tensor_tensor(out=qn[:], in0=qv, in1=ss[:, :, None].to_broadcast([128, 20, 64]), op=ALU.mult),
                      reads=[q, ss], writes=[qn])
                kb.op("pool", lambda e: e.tensor_tensor(out=qn[:], in0=qn[:], in1=wqk[:], op=ALU.mult), reads=[qn, wqk], writes=[qn])
                qn5 = qn[:].rearrange("p h (a b c) -> p h a b c", a=2, b=2)
                x1 = qn5[:, :, :, 0, :]
                x2 = qn5[:, :, :, 1, :]
                Cb = cC[:, None, :, :].to_broadcast([128, 20, 2, 16])
                Sb = cS[:, None, :, :].to_broadcast([128, 20, 2, 16])
                kb.op("dve", lambda e: e.tensor_tensor(out=t1[:], in0=x1, in1=Cb, op=ALU.mult), reads=[qn, cC], writes=[t1])
                kb.op("pool", lambda e: e.tensor_tensor(out=t2[:], in0=x2, in1=Sb, op=ALU.mult), reads=[qn, cS], writes=[t2])
                kb.op("pool", lambda e: e.tensor_tensor(out=t3[:], in0=x2, in1=Cb, op=ALU.mult), reads=[qn, cC], writes=[t3])
                kb.op("dve", lambda e: e.tensor_tensor(out=t4[:], in0=x1, in1=Sb, op=ALU.mult), reads=[qn, cS], writes=[t4])
                r5 = qkr[:].rearrange("p h (a b c) -> p h a b c", a=2, b=2)
                kb.op("dve", lambda e: e.tensor_tensor(out=r5[:, :, :, 0, :], in0=t1[:], in1=t2[:], op=ALU.subtract), reads=[t1, t2], writes=[qkr])
                kb.op("pool", lambda e: e.tensor_tensor(out=r5[:, :, :, 1, :], in0=t3[:], in1=t4[:], op=ALU.add), reads=[t3, t4], writes=[qkr])
                kb.op("pool", lambda e: e.tensor_copy(out=kd[:], in_=qkr[:, 16:20, None, :].to_broadcast([128, 4, 2, 64])), reads=[qkr], writes=[kd])
                for j in range(8):
                    kb.op("pe", lambda e: e.transpose(out=pq[:, j * 128:(j + 1) * 128], in_=qkr[:, 2 * j:2 * j + 2, :].rearrange("p a b -> p (a b)"),
                                                      identity=C.identb[:]), reads=[qkr, C.identb], writes=[pq])
                kb.op("act", lambda e: e.copy(out=qT[:, :, tt * 128:(tt + 1) * 128], in_=pq[:].rearrange("p (j t) -> p j t", j=8)),
                      reads=[pq], writes=[qT])
                for j in range(4):
                    kb.op("pe", lambda e: e.transpose(out=pk[:, j * 128:(j + 1) * 128], in_=kd[:, j, :, :].rearrange("p a b -> p (a b)"),
                                                      identity=C.identb[:]), reads=[kd, C.identb], writes=[pk])
                kb.op("dve", lambda e: e.tensor_copy(out=kT2[:, :, tt * 128:(tt + 1) * 128], in_=pk[:, 0:512].rearrange("p (j t) -> p j t", j=4)),
                      reads=[pk], writes=[kT2])
                kb.op("pool", lambda e: e.tensor_copy(out=vaug[:, tt, :, 0:64], in_=q[:, 1280:1536].rearrange("p (h d) -> p h d", h=4)),
                      reads=[q], writes=[vaug])
            kb.barrier()
        with contextlib.ExitStack() as st3:
            yatt = kb.sb("t_yatt", [128, NT, 1024], BF16, st3)
            pexp = [kb.sb(f"t_pexp{i}", [128, 512], BF16, st3) for i in range(3)]
            rinv = kb.sb("t_rinv", [128, 4], F32, st3)
            pss = [kb.ps(f"t_pss{i}", [128, 512], F32, st3) for i in range(2)]
            po = [kb.ps(f"t_po{i}", [128, 512], F32, st3) for i in range(4)]
            seq = [(h, qb, kt) for h in range(16) for qb in range(4) for kt in range(NT)]

            def emitS(i):
                h, qb, kt = seq[i]
                kv, c, b0 = h // 4, h // 2, (h % 2) * 64
                ps = pss[i % 2]
                kb.op("pe", lambda e: e.matmul(ps[:], lhsT=kT2[b0:b0 + 64, kv, kt * 128:(kt + 1) * 128],
                                               rhs=qT[b0:b0 + 64, c, qb * 512:(qb + 1) * 512], start=True, stop=True),
                      reads=[kT2, qT], writes=[ps])

            def emitE(i):
                ps = pss[i % 2]
                pe_ = pexp[i % 3]
                kb.op("act", lambda e: e.activation(out=pe_[:], in_=ps[:], func=AF.Exp), reads=[ps], writes=[pe_])

            def emitPV(i):
                h, qb, kt = seq[i]
                kv = h // 4
                pe_ = pexp[i % 3]
                for j in range(4):
                    kb.op("pe", lambda e: e.matmul(po[j][:, 0:65], lhsT=pe_[:, j * 128:(j + 1) * 128], rhs=vaug[:, kt, kv, :],
                                                   start=(kt == 0), stop=(kt == NT - 1)), reads=[pe_, vaug], writes=[po[j]])
                if kt == NT - 1:
                    for j in range(4):
                        kb.op("dve", lambda e: e.reciprocal(out=rinv[:, j:j + 1], in_=po[j][:, 64:65]), reads=[po[j]], writes=[rinv])
                        kb.op("dve", lambda e: e.tensor_scalar(out=yatt[:, qb * 4 + j, h * 64:(h + 1) * 64], in0=po[j][:, 0:64],
                                                               scalar1=rinv[:, j:j + 1], scalar2=None, op0=ALU.mult),
                              reads=[po[j], rinv], writes=[yatt])

            emitS(0)
            for i in range(len(seq)):
                if i + 1 < len(seq):
                    emitS(i + 1)
                emitE(i)
                emitPV(i)
            pt = [kb.ps(f"t_pt{i}", [128, 1024], BF16, st3) for i in range(2)]
            yT = [kb.sb(f"t_yT{i}", [128, 8, 128], BF16, st3) for i in range(2)]
            for tt in range(NT):
                p = pt[tt % 2]
                o = yT[tt % 2]
                for j in range(8):
                    kb.op("pe", lambda e: e.transpose(out=p[:, j * 128:(j + 1) * 128], in_=yatt[:, tt, j * 128:(j + 1) * 128],
                                                      identity=C.identb[:]), reads=[yatt, C.identb], writes=[p])
                kb.op("act", lambda e: e.copy(out=o[:], in_=p[:].rearrange("p (j t) -> p j t", j=8)), reads=[p], writes=[o])
                kb.dma("sp", D.ymT_d.ap()[8:16, :, tt * 128:(tt + 1) * 128].rearrange("j p t -> p j t"), o[:], reads=[o], writes=[D.ymT_d])
            kb.barrier()


def phase_O(kb, C, D, ym_src):
    with contextlib.ExitStack() as st:
        ymT = kb.sb("o_ymT", [128, 16, 2048], BF16, st)
        for j in range(16):
            kb.dma("sp" if j % 2 == 0 else "act", ymT[:, j, :], ym_src.ap()[j], reads=[ym_src], writes=[ymT])
        stg = [kb.sb(f"o_stg{i}", [128, 4096], F32, st) for i in range(2)]
        wo = kb.sb("o_wo", [128, 16, 512], BF16, st)
        xs = [kb.sb(f"o_xs{i}", [128, 512], F32, st) for i in range(2)]
        pss = [kb.ps(f"o_ps{i}", [128, 512], F32, st) for i in range(2)]
        w_v = D.w_out.ap().rearrange("(kc p) c -> p kc c", p=128)
        for dg in range(4):
            for hf in range(2):
                sg = stg[hf]
                sgv = sg[:].rearrange("p (dc c) -> p dc c", dc=8)
                kb.dma("sp" if hf == 0 else "act", sgv, w_v[:, hf * 8:(hf + 1) * 8, dg * 512:(dg + 1) * 512], writes=[sg])
                kb.op("pool", lambda e: e.tensor_copy(out=wo[:, hf * 8:(hf + 1) * 8, :], in_=sgv), reads=[sg], writes=[wo])
            for tt in range(NT):
                ps = pss[tt % 2]
                xt = xs[tt % 2]
                kb.dma("sp", xt[:], D.x.ap()[tt * 128:(tt + 1) * 128, dg * 512:(dg + 1) * 512], writes=[xt])
                for kc in range(16):
                    kb.op("pe", lambda e: e.matmul(ps[:], lhsT=ymT[:, kc, tt * 128:(tt + 1) * 128], rhs=wo[:, kc, :],
                                                   start=(kc == 0), stop=(kc == 15)), reads=[ymT, wo], writes=[ps])
                kb.op("dve", lambda e: e.tensor_tensor(out=xt[:], in0=ps[:], in1=xt[:], op=ALU.add), reads=[ps, xt], writes=[xt])
                kb.dma("act", D.x1_d.ap()[tt * 128:(tt + 1) * 128, dg * 512:(dg + 1) * 512], xt[:], reads=[xt], writes=[D.x1_d])
        kb.barrier()


def phase_P(kb, C, D):
    with contextlib.ExitStack() as stP:
        ET = kb.sb("p_ET", [128, 3, 2048], F32, stP)
        iota = kb.sb("p_iota", [128, 128], F32, stP)
        kb.dma("sp", iota[:], D.iota.ap(), writes=[iota])
        with contextlib.ExitStack() as st, kb.nc.named_scope("P1"):
            h2T = kb.sb("p_h2T", [128, 16, 2048], BF16, st)
            norm_to_T(kb, C, D.x1_d, D.norm2_w, h2T, "n2")
            for dc in range(16):
                kb.dma("sp" if dc % 2 == 0 else "act", D.h2T_d.ap()[dc], h2T[:, dc, :], reads=[h2T], writes=[D.h2T_d])
            skb = kb.sb("p_sk", [128, 16, 128], F32, st)
            kb.dma("sp", skb[:], D.skT.ap(), writes=[skb])
            stg = [kb.sb(f"p_stg{i}", [128, 16, 128], F32, st) for i in range(2)]
            wb = [kb.sb(f"p_wb{i}", [128, 16, 128], BF16, st) for i in range(2)]
            qTs = [kb.sb(f"p_qT{i}", [128, 2048], F32, st) for i in range(2)]
            sev = [kb.sb(f"p_sev{i}", [128, 16, 128], F32, st) for i in range(2)]
            pss = [[kb.ps(f"p_ps{i}_{j}", [128, 512], F32, st) for j in range(3)] for i in range(2)]
            w_v = D.w_pq.ap().rearrange("(dc p) c -> p dc c", p=128)
            for hp in range(16):
                sg = stg[hp % 2]
                kb.dma("sp" if hp % 2 == 0 else "act", sg[:], w_v[:, :, hp * 128:(hp + 1) * 128], writes=[sg])
                w = wb[hp % 2]
                kb.op("pool", lambda e: e.tensor_copy(out=w[:], in_=sg[:]), reads=[sg], writes=[w])
                qT = qTs[hp % 2]
                for tb in range(4):
                    ps = pss[tb % 2][0]
                    for dc in range(16):
                        kb.op("pe", lambda e: e.matmul(ps[:], lhsT=w[:, dc, :], rhs=h2T[:, dc, tb * 512:(tb + 1) * 512],
                                                       start=(dc == 0), stop=(dc == 15)), reads=[w, h2T], writes=[ps])
                    kb.op("act", lambda e: e.copy(out=qT[:, tb * 512:(tb + 1) * 512], in_=ps[:]), reads=[ps], writes=[qT])
                se = sev[hp % 2]
                for g in range(4):
                    ps = pss[g % 2][1 + (g // 2) % 2]
                    for j in range(4):
                        tt = g * 4 + j
                        kb.op("pe", lambda e: e.matmul(ps[:, j * 128:(j + 1) * 128], lhsT=qT[:, tt * 128:(tt + 1) * 128], rhs=skb[:, hp, :],
                                                       start=True, stop=True), reads=[qT, skb], writes=[ps])
                    kb.op("dve", lambda e: e.tensor_copy(out=se[:, g * 4:(g + 1) * 4, :], in_=ps[:].rearrange("p (j n) -> p j n", j=4)),
                          reads=[ps], writes=[se])
                kb.dma("pool", D.S_d.ap()[:, hp, :].rearrange("(tt p) n -> p tt n", p=128), se[:], reads=[se], writes=[D.S_d])
            kb.barrier()
        with contextlib.ExitStack() as st, kb.nc.named_scope("P2"):
            Ss = [kb.sb(f"p_S{i}", [128, 16, 128], F32, st) for i in range(2)]
            S2 = kb.sb("p_S2", [128, 128], F32, st)
            M16 = kb.sb("p_M16", [128, 16, 16], F32, st)
            I16u = kb.sb("p_I16u", [128, 16, 16], U32, st)
            I16f = kb.sb("p_I16f", [128, 16, 16], F32, st)
            cand = kb.sb("p_cand", [128, 8, 16, 16], F32, st)
            cand2 = kb.sb("p_cand2", [128, 256], F32, st)
            C16 = kb.sb("p_C16", [128, 8, 16], F32, st)
            CIu = kb.sb("p_CIu", [128, 8, 16], U32, st)
            IJu = kb.sb("p_IJu", [128, 2, 8, 16], U32, st)
            IJf = kb.sb("p_IJf", [128, 2, 8, 16], F32, st)
            ex = kb.sb("p_ex", [128, 8, 16], F32, st)
            Z = kb.sb("p_Z", [128, 8], F32, st)
            EG = kb.sb("p_EG", [128, 3, 8, 16], F32, st)
            eq = kb.sb("p_eq", [128, 8, 16, 16], F32, st)
            pt = kb.ps("p_pt", [128, 512], F32, st)
            for tt in range(NT):
                S = Ss[tt % 2]
                kb.dma("sp", S[:].rearrange("p a n -> p (a n)"), D.S_d.ap()[tt * 128:(tt + 1) * 128].rearrange("p a n -> p (a n)"),
                       reads=[D.S_d], writes=[S])
                for hp in range(16):
                    kb.op("dve", lambda e: e.max(out=M16[:, hp, 0:8], in_=S[:, hp, :]), reads=[S], writes=[M16])
                    kb.op("dve", lambda e: e.max_index(out=I16u[:, hp, 0:8], in_max=M16[:, hp, 0:8], in_values=S[:, hp, :]),
                          reads=[S, M16], writes=[I16u])
                    kb.op("dve", lambda e: e.match_replace(out=S2[:], in_to_replace=M16[:, hp, 0:8], in_values=S[:, hp, :], imm_value=-1e30),
                          reads=[S, M16], writes=[S2])
                    kb.op("dve", lambda e: e.max(out=M16[:, hp, 8:16], in_=S2[:]), reads=[S2], writes=[M16])
                    kb.op("dve", lambda e: e.max_index(out=I16u[:, hp, 8:16], in_max=M16[:, hp, 8:16], in_values=S2[:]),
                          reads=[S2, M16], writes=[I16u])
                kb.op("pool", lambda e: e.tensor_copy(out=I16f[:], in_=I16u[:]), reads=[I16u], writes=[I16f])
                M4 = M16[:].rearrange("p (h q) k -> p h q k", q=2)
                I4 = I16f[:].rearrange("p (h q) k -> p h q k", q=2)
                kb.op("pool", lambda e: e.tensor_tensor(out=cand[:], in0=M4[:, :, 0, :, None].to_broadcast([128, 8, 16, 16]),
                                                        in1=M4[:, :, 1, None, :].to_broadcast([128, 8, 16, 16]), op=ALU.add),
                      reads=[M16], writes=[cand])
                for h in range(8):
                    ch = cand[:, h, :, :].rearrange("p a b -> p (a b)")
                    kb.op("dve", lambda e: e.max(out=C16[:, h, 0:8], in_=ch), reads=[cand], writes=[C16])
                    kb.op("dve", lambda e: e.max_index(out=CIu[:, h, 0:8], in_max=C16[:, h, 0:8], in_values=ch), reads=[cand, C16], writes=[CIu])
                    kb.op("dve", lambda e: e.match_replace(out=cand2[:], in_to_replace=C16[:, h, 0:8], in_values=ch, imm_value=-1e30),
                          reads=[cand, C16], writes=[cand2])
                    kb.op("dve", lambda e: e.max(out=C16[:, h, 8:16], in_=cand2[:]), reads=[cand2], writes=[C16])
                    kb.op("dve", lambda e: e.max_index(out=CIu[:, h, 8:16], in_max=C16[:, h, 8:16], in_values=cand2[:]),
                          reads=[cand2, C16], writes=[CIu])
                kb.op("pool", lambda e: e.tensor_tensor(out=ex[:], in0=C16[:], in1=C16[:, :, 0:1].to_broadcast([128, 8, 16]), op=ALU.subtract),
                      reads=[C16], writes=[ex])
                kb.op("act", lambda e: e.activation(out=ex[:], in_=ex[:], func=AF.Exp), reads=[ex], writes=[ex])
                kb.op("dve", lambda e: e.tensor_reduce(out=Z[:], in_=ex[:], axis=AX.X, op=ALU.add), reads=[ex], writes=[Z])
                kb.op("dve", lambda e: e.reciprocal(out=Z[:], in_=Z[:]), reads=[Z], writes=[Z])
                kb.op("dve", lambda e: e.tensor_tensor(out=EG[:, 2], in0=ex[:], in1=Z[:, :, None].to_broadcast([128, 8, 16]), op=ALU.mult),
                      reads=[ex, Z], writes=[EG])
                kb.op("dve", lambda e: e.tensor_single_scalar(out=IJu[:, 0], in_=CIu[:], scalar=4, op=ALU.logical_shift_right),
                      reads=[CIu], writes=[IJu])
                kb.op("dve", lambda e: e.tensor_single_scalar(out=IJu[:, 1], in_=CIu[:], scalar=15, op=ALU.bitwise_and),
                      reads=[CIu], writes=[IJu])
                kb.op("pool", lambda e: e.tensor_copy(out=IJf[:], in_=IJu[:]), reads=[IJu], writes=[IJf])
                for q in range(2):
                    kb.op("dve", lambda e: e.tensor_tensor(out=eq[:], in0=iota[:, None, None, 0:16].to_broadcast([128, 8, 16, 16]),
                                                            in1=IJf[:, q, :, :, None].to_broadcast([128, 8, 16, 16]), op=ALU.is_equal),
                          reads=[iota, IJf], writes=[eq])
                    kb.op("pool", lambda e: e.tensor_tensor(out=eq[:], in0=eq[:], in1=I4[:, :, q, None, :].to_broadcast([128, 8, 16, 16]),
                                                            op=ALU.mult), reads=[eq, I16f], writes=[eq])
                    kb.op("dve", lambda e: e.tensor_reduce(out=EG[:, q], in_=eq[:], axis=AX.X, op=ALU.add), reads=[eq], writes=[EG])
                for a in range(3):
                    kb.op("pe", lambda e: e.transpose(out=pt[:, a * 128:(a + 1) * 128], in_=EG[:, a].rearrange("p h k -> p (h k)"),
                                                      identity=C.ident[:]), reads=[EG, C.ident], writes=[pt])
                kb.op("act", lambda e: e.copy(out=ET[:, :, tt * 128:(tt + 1) * 128], in_=pt[:, 0:384].rearrange("p (a t) -> p a t", a=3)),
                      reads=[pt], writes=[ET])
            kb.barrier()
        with contextlib.ExitStack() as st, kb.nc.named_scope("P3"):
            Gs = kb.sb("p_Gs", [128, 128, 256], BF16, st)
            O1 = [kb.sb(f"p_O1{i}", [128, 32, 128], BF16, st) for i in range(2)]
            O2 = [kb.sb(f"p_O2{i}", [128, 32, 128], BF16, st) for i in range(2)]
            pg = [kb.ps(f"p_pg{i}", [128, 512], F32, st) for i in range(4)]
            it = 0
            for tg in range(8):
                for sub in range(8):
                    t0 = tg * 256 + sub * 32
                    o1 = O1[sub % 2]
                    o2 = O2[sub % 2]
                    iob = iota[:, None, :].to_broadcast([128, 32, 128])
                    kb.op("dve", lambda e: e.tensor_tensor(out=o1[:], in0=iob, in1=ET[:, 0, t0:t0 + 32, None].to_broadcast([128, 32, 128]),
                                                            op=ALU.is_equal), reads=[iota, ET], writes=[o1])
                    kb.op("dve", lambda e: e.tensor_tensor(out=o2[:], in0=iob, in1=ET[:, 1, t0:t0 + 32, None].to_broadcast([128, 32, 128]),
                                                           op=ALU.is_equal), reads=[iota, ET], writes=[o2])
                    kb.op("pool", lambda e: e.tensor_tensor(out=o2[:], in0=o2[:], in1=ET[:, 2, t0:t0 + 32, None].to_broadcast([128, 32, 128]),
                                                            op=ALU.mult), reads=[o2, ET], writes=[o2])
                    for q4 in range(8):
                        p = pg[it % 4]
                        it += 1
                        for j in range(4):
                            tl = q4 * 4 + j
                            kb.op("pe", lambda e: e.matmul(p[:, j * 128:(j + 1) * 128], lhsT=o2[:, tl, :], rhs=o1[:, tl, :], start=True, stop=True),
                                  reads=[o1, o2], writes=[p])
                        tl0 = sub * 32 + q4 * 4
                        dst = Gs[:, :, tl0:tl0 + 4].rearrange("p e t -> p t e")
                        src = p[:].rearrange("p (t e) -> p t e", t=4)
                        if it % 2 == 0:
                            kb.op("act", lambda e: e.copy(out=dst, in_=src), reads=[p], writes=[Gs])
                        else:
                            kb.op("dve", lambda e: e.tensor_copy(out=dst, in_=src), reads=[p], writes=[Gs])
                for k8 in range(8):
                    kb.dma(["sp", "act", "pool"][k8 % 3],
                           D.G_d.ap()[k8 * 16:(k8 + 1) * 16, :, tg * 256:(tg + 1) * 256].rearrange("e1 e2 t -> e2 e1 t"),
                           Gs[:, k8 * 16:(k8 + 1) * 16, :], reads=[Gs], writes=[D.G_d])
            kb.barrier()
        with contextlib.ExitStack() as st, kb.nc.named_scope("P4"):
            h2T = kb.sb("p_h2Tb", [128, 16, 2048], BF16, st)
            for dc in range(16):
                kb.dma("sp" if dc % 2 == 0 else "act", h2T[:, dc, :], D.h2T_d.ap()[dc], reads=[D.h2T_d], writes=[h2T])
            stg = [kb.sb(f"p4_stg{i}", [128, 16, 128], F32, st) for i in range(2)]
            ub = [kb.sb(f"p4_ub{i}", [128, 16, 128], BF16, st) for i in range(2)]
            Gc = [kb.sb(f"p4_Gc{i}", [128, 2048], BF16, st) for i in range(2)]
            ge = [kb.sb(f"p4_ge{i}", [128, 2048], BF16, st) for i in range(2)]
            pss = [[kb.ps(f"p4_ps{i}_{j}", [128, 512], F32, st) for j in range(4)] for i in range(2)]
            def load(e1):
                sg = stg[e1 % 2]
                kb.dma("sp", sg[:].rearrange("p a b -> p (a b)"), D.u_tabT.ap()[e1], writes=[sg])
                g = Gc[e1 % 2]
                kb.dma("pool", g[:], D.G_d.ap()[e1], reads=[D.G_d], writes=[g])
            load(0)
            for e1 in range(128):
                if e1 + 1 < 128:
                    load(e1 + 1)
                sg = stg[e1 % 2]
                u = ub[e1 % 2]
                kb.op("dve", lambda e: e.tensor_copy(out=u[:], in_=sg[:]), reads=[sg], writes=[u])
                g = Gc[e1 % 2]
                ps = pss[e1 % 2]
                a = ge[e1 % 2]
                for tb in range(4):
                    for dc in range(16):
                        kb.op("pe", lambda e: e.matmul(ps[tb][:], lhsT=u[:, dc, :], rhs=h2T[:, dc, tb * 512:(tb + 1) * 512],
                                                       start=(dc == 0), stop=(dc == 15)), reads=[u, h2T], writes=[ps[tb]])
                    kb.op("act", lambda e: e.activation(out=a[:, tb * 512:(tb + 1) * 512], in_=ps[tb][:], func=AF.Gelu), reads=[ps[tb]], writes=[a])
                kb.op("pool", lambda e: e.tensor_tensor(out=a[:], in0=a[:], in1=g[:], op=ALU.mult), reads=[a, g], writes=[a])
                kb.dma("sp", D.A_d.ap()[e1], a[:], reads=[a], writes=[D.A_d])
            kb.barrier()
        with contextlib.ExitStack() as st, kb.nc.named_scope("P5"):
            NB = 3
            vst = [kb.sb(f"p5_vst{i}", [128, 512], F32, st) for i in range(NB)]
            vb = [kb.sb(f"p5_vb{i}", [128, 512], BF16, st) for i in range(NB)]
            ac = [kb.sb(f"p5_ac{i}", [128, 1024], BF16, st) for i in range(NB)]
            ov = [kb.sb(f"p5_ov{i}", [128, 512], F32, st) for i in range(2)]
            pss = [[kb.ps(f"p5_ps{j}_{h}", [128, 512], F32, st) for h in range(2)] for j in range(4)]
            seq = [(tb2, dg, e1) for tb2 in range(2) for dg in range(4) for e1 in range(128)]

            def load(i):
                tb2, dg, e1 = seq[i]
                k = i % NB
                kb.dma("sp", vst[k][:], D.v_tab.ap()[e1 * 128:(e1 + 1) * 128, dg * 512:(dg + 1) * 512], writes=[vst[k]])
                kb.dma("pool", ac[k][:], D.A_d.ap()[e1][:, tb2 * 1024:(tb2 + 1) * 1024], reads=[D.A_d], writes=[ac[k]])
            load(0)
            load(1)
            oi = 0
            for i, (tb2, dg, e1) in enumerate(seq):
                if i + 2 < len(seq):
                    load(i + 2)
                k = i % NB
                if i % 2 == 0:
                    kb.op("dve", lambda e: e.tensor_copy(out=vb[k][:], in_=vst[k][:]), reads=[vst[k]], writes=[vb[k]])
                else:
                    kb.op("act", lambda e: e.copy(out=vb[k][:], in_=vst[k][:]), reads=[vst[k]], writes=[vb[k]])
                for j in range(4):
                    for h in range(2):
                        kb.op("pe", lambda e: e.matmul(pss[j][h][:], lhsT=vb[k][:, j * 128:(j + 1) * 128], rhs=ac[k][:, h * 512:(h + 1) * 512],
                                                       start=(e1 == 0), stop=(e1 == 127)), reads=[vb[k], ac[k]], writes=[pss[j][h]])
                if e1 == 127:
                    for j in range(4):
                        for h in range(2):
                            o = ov[oi % 2]
                            if oi % 2 == 0:
                                kb.op("act", lambda e: e.copy(out=o[:], in_=pss[j][h][:]), reads=[pss[j][h]], writes=[o])
                            else:
                                kb.op("dve", lambda e: e.tensor_copy(out=o[:], in_=pss[j][h][:]), reads=[pss[j][h]], writes=[o])
                            oi += 1
                            d0 = dg * 512 + j * 128
                            t0 = tb2 * 1024 + h * 512
                            kb.dma("sp", D.peT_d.ap()[d0:d0 + 128, t0:t0 + 512], o[:], reads=[o], writes=[D.peT_d])
            kb.barrier()


def phase_F(kb, C, D):
    with contextlib.ExitStack() as st:
        wb = kb.sb("f_wb", [128, 2048], F32, st)
        kb.dma("pool", wb[:], D.normf_w.ap().partition_broadcast(128), writes=[wb])
        xs = [kb.sb(f"f_x{i}", [128, 2048], F32, st) for i in range(2)]
        pes = [kb.sb(f"f_pe{i}", [128, 16, 128], F32, st) for i in range(2)]
        junk = kb.sb("f_junk", [128, 2048], BF16, st)
        ss = kb.sb("f_ss", [128, NT], F32, st)
        rs = kb.sb("f_rs", [128, NT], F32, st)
        kb.op("pool", lambda e: e.memset(ss[:], 0.0), writes=[ss])
        pss = [[kb.ps(f"f_ps{i}_{j}", [128, 512], F32, st) for j in range(4)] for i in range(2)]
        pe_v = D.peT_d.ap().rearrange("(dc p) t -> p dc t", p=128)
        for tt in range(NT):
            xt = xs[tt % 2]
            pe = pes[tt % 2]
            ps = pss[tt % 2]
            kb.dma("sp", xt[:], D.x1_d.ap()[tt * 128:(tt + 1) * 128, :], reads=[D.x1_d], writes=[xt])
            kb.dma("act", pe[:], pe_v[:, :, tt * 128:(tt + 1) * 128], reads=[D.peT_d], writes=[pe])
            for dc in range(16):
                kb.op("pe", lambda e: e.transpose(out=ps[dc // 4][:, (dc % 4) * 128:(dc % 4 + 1) * 128], in_=pe[:, dc, :], identity=C.ident[:]),
                      reads=[pe, C.ident], writes=[ps[dc // 4]])
            for j in range(4):
                kb.op("dve", lambda e: e.tensor_tensor(out=xt[:, j * 512:(j + 1) * 512], in0=ps[j][:], in1=xt[:, j * 512:(j + 1) * 512], op=ALU.add),
                      reads=[ps[j], xt], writes=[xt])
            kb.op("act", lambda e: e.activation(out=junk[:], in_=xt[:], func=AF.Square, accum_out=ss[:, tt:tt + 1]), reads=[xt], writes=[junk, ss])
            kb.op("dve", lambda e: e.tensor_scalar(out=rs[:, tt:tt + 1], in0=ss[:, tt:tt + 1], scalar1=1.0 / 2048, scalar2=EPS,
                                                   op0=ALU.mult, op1=ALU.add), reads=[ss], writes=[rs])
            kb.op("act", lambda e: e.activation(out=rs[:, tt:tt + 1], in_=rs[:, tt:tt + 1], func=AF.Sqrt), reads=[rs], writes=[rs])
            kb.op("dve", lambda e: e.reciprocal(out=rs[:, tt:tt + 1], in_=rs[:, tt:tt + 1]), reads=[rs], writes=[rs])
            kb.op("dve", lambda e: e.scalar_tensor_tensor(out=xt[:], in0=xt[:], scalar=rs[:, tt:tt + 1], in1=wb[:], op0=ALU.mult, op1=ALU.mult),
                  reads=[xt, rs, wb], writes=[xt])
            kb.dma("sp", D.out.ap()[tt * 128:(tt + 1) * 128, :], xt[:], reads=[xt], writes=[D.out])
        kb.barrier()


STOP = 0
FAST_F32 = True
F32R = mybir.dt.float32r
def fr(ap):
    return ap.bitcast(F32R) if FAST_F32 else ap
class StopBuild(Exception):
    pass
def chk(n):
    if STOP == n:
        raise StopBuild()
GN_EPS = 64e-5
NEG_E05 = -0.6065306597126334


def phase_Rpre(kb, C, D):
    with contextlib.ExitStack() as st:
        rw = kb.sb("rp_rw", [128, 8, 8], F32, st)
        kb.dma("sp", rw[:], D.rw_c.ap(), writes=[rw])
        tmp = kb.sb("rp_tmp", [128, 2048], F32, st)
        lin = [kb.sb(f"rp_lin{i}", [128, 2048], BF16, st) for i in range(4)]
        for i in range(4):
            kb.op("pool", lambda e: e.memset(lin[i][:], 0.0), writes=[lin[i]])
            r0 = 3072 + i * 96
            kb.dma("sp", tmp[0:96, :], D.zs_d.ap()[r0:r0 + 96, :], reads=[D.zs_d], writes=[tmp])
            if i < 2:
                kb.op("act", lambda e: e.activation(out=lin[i][0:96, :], in_=tmp[0:96, :], func=AF.Tanh), reads=[tmp], writes=[lin[i]])
            else:
                kb.op("act", lambda e: e.copy(out=lin[i][0:96, :], in_=tmp[0:96, :]), reads=[tmp], writes=[lin[i]])
        sgl = kb.sb("rp_sgl", [128, 2, 2048], BF16, st)
        for kc in range(2):
            kb.dma("sp", tmp[:], D.zs_d.ap()[3456 + kc * 128:3456 + (kc + 1) * 128, :], reads=[D.zs_d], writes=[tmp])
            kb.op("act", lambda e: e.activation(out=sgl[:, kc, :], in_=tmp[:], func=AF.Sigmoid), reads=[tmp], writes=[sgl])
        wst = kb.sb("rp_wst", [128, 2048], F32, st)
        w2b = kb.sb("rp_w2b", [128, 2, 1024], BF16, st)
        a2b = kb.sb("rp_a2b", [128, 2, 1024], BF16, st)
        g2b = kb.sb("rp_g2b", [128, 2, 1024], BF16, st)
        wv = wst[:].rearrange("p (a c) -> p a c", a=2)
        kb.op("pool", lambda e: e.memset(w2b[:], 0.0), writes=[w2b])
        kb.op("pool", lambda e: e.memset(a2b[:], 0.0), writes=[a2b])
        kb.dma("sp", wv[0:96], D.w2.ap().rearrange("d l c -> l d c"), writes=[wst])
        kb.op("pool", lambda e: e.tensor_copy(out=w2b[0:96], in_=wv[0:96]), reads=[wst], writes=[w2b])
        kb.dma("sp", wv[0:96], D.a2.ap().rearrange("d l c -> l d c"), writes=[wst])
        kb.op("pool", lambda e: e.tensor_copy(out=a2b[0:96], in_=wv[0:96]), reads=[wst], writes=[a2b])
        kb.dma("sp", wv, D.g2.ap().rearrange("(kc p) c -> p kc c", p=128), writes=[wst])
        kb.op("pool", lambda e: e.tensor_copy(out=g2b[:], in_=wv), reads=[wst], writes=[g2b])
        outs = [kb.sb(f"rp_o{i}", [128, 2048], F32, st) for i in range(2)]
        pss = [[kb.ps(f"rp_ps{i}_{j}", [128, 512], F32, st) for j in range(4)] for i in range(2)]
        it = 0
        for cc in range(8):
            for d in range(2):
                for which in range(2):
                    ps = pss[it % 2]
                    o = outs[it % 2]
                    it += 1
                    wmat = w2b if which == 0 else a2b
                    xin = lin[d] if which == 0 else lin[2 + d]
                    bias = rw[:, cc, d:d + 1] if which == 0 else rw[:, cc, 2 + d:3 + d]
                    for tb in range(4):
                        kb.op("pe", lambda e: e.matmul(ps[tb][:], lhsT=wmat[:, d, cc * 128:(cc + 1) * 128], rhs=xin[:, tb * 512:(tb + 1) * 512],
                                                       start=True, stop=True), reads=[wmat, xin], writes=[ps[tb]])
                        kb.op("act", lambda e: e.activation(out=o[:, tb * 512:(tb + 1) * 512], in_=ps[tb][:], func=AF.Sigmoid, bias=bias),
                              reads=[ps[tb], rw], writes=[o])
                    if which == 0:
                        kb.op("dve", lambda e: e.tensor_scalar(out=o[:], in0=o[:], scalar1=NEG_E05, scalar2=None, op0=ALU.mult), reads=[o], writes=[o])
                        kb.dma("sp", D.ld_d.ap()[d, cc * 128:(cc + 1) * 128, :], o[:], reads=[o], writes=[D.ld_d])
                    else:
                        kb.dma("sp", D.a_d.ap()[d, cc * 128:(cc + 1) * 128, :], o[:], reads=[o], writes=[D.a_d])
        for tt in range(NT):
            ps = pss[tt % 2]
            o = outs[tt % 2]
            for hf in range(2):
                for kc in range(2):
                    kb.op("pe", lambda e: e.matmul(ps[hf][:], lhsT=sgl[:, kc, tt * 128:(tt + 1) * 128], rhs=g2b[:, kc, hf * 512:(hf + 1) * 512],
                                                   start=(kc == 0), stop=(kc == 1)), reads=[sgl, g2b], writes=[ps[hf]])
                kb.op("act", lambda e: e.copy(out=o[:, hf * 512:(hf + 1) * 512], in_=ps[hf][:]), reads=[ps[hf]], writes=[o])
            kb.dma("sp", D.g_d.ap()[tt * 128:(tt + 1) * 128, :], o[:, 0:1024], reads=[o], writes=[D.g_d])
        kb.barrier()


def phase_R(kb, C, D, ccs=range(8), dbg=None):
    with contextlib.ExitStack() as st:
        rw = kb.sb("r_rw", [128, 8, 8], F32, st)
        kb.dma("sp", rw[:], D.rw_c.ap(), writes=[rw])
        masks = kb.sb("r_masks", [128, 6, 128], F32, st)
        kb.dma("sp", masks[:], D.masks.ap(), writes=[masks])
        cst = kb.sb("r_cst", [128, 66], F32, st)
        kb.dma("sp", cst[:], D.rcst.ap()[:, 0:66], writes=[cst])
        segt = kb.sb("r_segm", [128, 2048], BF16, st)
        kb.dma("sp", segt[:], D.segm.ap(), writes=[segt])
        ident2 = cst[:, 0:64]
        sel = cst[:, 64:66]
        lnw = kb.sb("r_lnw", [128, 128], F32, st)
        lnb = kb.sb("r_lnb", [128, 128], F32, st)
        Rr = kb.sb("r_R", [128, 2048], F32, st)
        Kk = kb.sb("r_K", [128, 2048], F32, st)
        Vv = kb.sb("r_V", [128, 2048], F32, st)
        KKn = kb.sb("r_KK", [128, 2048], F32, st)
        E1 = kb.sb("r_E1", [128, 2048], F32, st)
        XI = kb.sb("r_XI", [128, 2048], F32, st)
        XE = kb.sb("r_XE", [128, 2048], F32, st)
        Aa = kb.sb("r_A", [128, 2048], F32, st)
        KD = kb.sb("r_KD", [128, 2048], F32, st)
        KT = kb.sb("r_KT", [128, 2048], F32, st)
        AT = kb.sb("r_AT", [128, 2048], F32, st)
        BTb = kb.sb("r_BTb", [128, 2048], F32, st)
        stt = kb.sb("r_stt", [128, 160], F32, st)
        Vtm = kb.sb("r_Vtm", [128, NT, 128], F32, st)
        Ysum = kb.sb("r_Ysum", [128, NT, 128], F32, st)
        MTa = kb.sb("r_MTa", [128, 32, 64], F32, st)
        Ca = kb.sb("r_Ca", [128, 32, 64], F32, st)
        H = [kb.sb(f"r_H{i}", [128, 64], F32, st) for i in range(2)]
        tot = kb.sb("r_tot", [128, 32], F32, st)
        GL = kb.sb("r_GL", [128, 32], F32, st)
        RhT = Vv

        class Set:
            pass
        sets = []
        for p in range(2):
            S = Set()
            S.XA = kb.sb(f"r_XA{p}", [128, 2, 2, 128], F32, st)
            S.KBm = kb.sb(f"r_KBm{p}", [128, 2, 2, 128], F32, st)
            S.PQ = [kb.sb(f"r_PQ{p}_{i}", [128, 2, 2, 128], F32, st) for i in range(2)]
            S.QT = [kb.sb(f"r_QT{p}_{i}", [128, 2, 128], F32, st) for i in range(2)]
            S.BW = kb.sb(f"r_BW{p}", [128, 2, 128], F32, st)
            S.BU = kb.sb(f"r_BU{p}", [128, 2, 128], F32, st)
            S.AGtm = kb.sb(f"r_AGtm{p}", [128, 128], F32, st)
            S.KGtm = kb.sb(f"r_KGtm{p}", [128, 128], F32, st)
            S.B = [kb.ps(f"r_bk{p}_{i}", [128, 512], F32, st) for i in range(4)]
            sets.append(S)
        psX = sets[0].B[0]
        bkH = [sets[0].B[1], sets[0].B[2]]

        v4 = lambda b: b[:].rearrange("p (h q s) -> p h q s", h=2, q=2)
        v3 = lambda b, lo: b[:, lo:lo + 256].rearrange("p (h s) -> p h s", h=2)

        def tile_gen(S, d, tt, RT, BT):
            M2 = masks[:, 2 * d:2 * d + 2, :]
            MST = masks[:, 2 - 2 * d, :]
            XA, KBm, PQ, QT, BW, BU, AGtm, KGtm = S.XA, S.KBm, S.PQ, S.QT, S.BW, S.BU, S.AGtm, S.KGtm
            B0, B1, B2, B3 = S.B
            psA, psB, psN = v4(B0), v4(B1), v4(B3)
            psC = v3(B2, 0)
            cols = slice(tt * 128, (tt + 1) * 128)
            for hh in range(2):
                pr = slice(hh * 64, hh * 64 + 64)
                kb.pe_fence()
                kb.op("pe", lambda e: e.matmul(psA[:, hh, 0, :], lhsT=fr(AT[pr, cols]), rhs=fr(BT[pr, cols]), start=True, stop=True),
                      reads=[AT, BT], writes=[B0])
                kb.op("pe", lambda e: e.matmul(psA[:, hh, 1, :], lhsT=fr(AT[pr, cols]), rhs=fr(RT[pr, cols]), start=True, stop=True),
                      reads=[AT, RT], writes=[B0])
                kb.op("pe", lambda e: e.matmul(psB[:, hh, 0, :], lhsT=fr(KT[pr, cols]), rhs=fr(BT[pr, cols]), start=True, stop=True),
                      reads=[KT, BT], writes=[B1])
                kb.op("pe", lambda e: e.matmul(psB[:, hh, 1, :], lhsT=fr(KT[pr, cols]), rhs=fr(RT[pr, cols]), start=True, stop=True),
                      reads=[KT, RT], writes=[B1])
                kb.op("pe", lambda e: e.matmul(psC[:, hh, :], lhsT=fr(BT[pr, cols]), rhs=fr(AT[pr, cols]), start=True, stop=True),
                      reads=[AT, BT], writes=[B2])
            kb.pe_fence()
            yield
            M2b = M2[:, None, :, :].to_broadcast([128, 2, 2, 128])
            q0, q1 = QT[0], QT[1]
            kb.op("dve", lambda e: e.tensor_tensor(out=fr(XA[:]), in0=psA, in1=M2b, op=ALU.mult), reads=[B0, masks], writes=[XA])
            kb.op("dve", lambda e: e.tensor_tensor(out=fr(q0[:]), in0=psC, in1=MST[:, None, :].to_broadcast([128, 2, 128]), op=ALU.mult),
                  reads=[B2, masks], writes=[q0])
            kb.op("dve", lambda e: e.tensor_tensor(out=fr(KBm[:]), in0=psB, in1=M2b, op=ALU.mult), reads=[B1, masks], writes=[KBm])
            pq = PQ[0]
            kb.op("pool", lambda e: e.tensor_tensor(out=fr(pq[:, :, 0, :]), in0=XA[:, :, 0, :], in1=C.ident[:, None, :].to_broadcast([128, 2, 128]),
                                                    op=ALU.add), reads=[XA, C.ident], writes=[pq])
            yield
            for hh in range(2):
                kb.op("pe", lambda e: e.matmul(psN[:, hh, 1, :], lhsT=fr(q0[:, hh, :]), rhs=fr(XA[:, hh, 0, :]), start=True, stop=True),
                      reads=[q0, XA], writes=[B3])
                kb.op("pe", lambda e: e.matmul(psC[:, hh, :], lhsT=fr(XA[:, hh, 0, :]), rhs=fr(q0[:, hh, :]), start=True, stop=True),
                      reads=[q0, XA], writes=[B2])
            yield
            kb.op("act", lambda e: e.copy(out=fr(pq[:, :, 1, :]), in_=psN[:, :, 1, :]), reads=[B3], writes=[pq])
            kb.op("dve", lambda e: e.tensor_copy(out=fr(q1[:]), in_=psC), reads=[B2], writes=[q1])
            yield
            cur = 0
            qcur = 1
            for lev in range(1, 6):
                pq = PQ[cur]
                pqn = PQ[1 - cur]
                qt = QT[qcur]
                qtn = QT[1 - qcur]
                last = (lev == 5)
                for hh in range(2):
                    if last:
                        kb.op("pe", lambda e: e.matmul(psN[:, hh, 0, :], lhsT=fr(qt[:, hh, :]), rhs=fr(pq[:, hh, 0, :]), start=True, stop=True),
                              reads=[qt, pq], writes=[B3])
                    else:
                        kb.op("pe", lambda e: e.matmul(psN[:, hh, :, :], lhsT=fr(qt[:, hh, :]), rhs=fr(pq[:, hh, :, :]), start=True, stop=True),
                              reads=[qt, pq], writes=[B3])
                        kb.op("pe", lambda e: e.matmul(psC[:, hh, :], lhsT=fr(pq[:, hh, 1, :]), rhs=fr(qt[:, hh, :]), start=True, stop=True),
                              reads=[qt, pq], writes=[B2])
                yield
                kb.op("dve", lambda e: e.tensor_tensor(out=fr(pqn[:, :, 0, :]), in0=psN[:, :, 0, :], in1=pq[:, :, 0, :], op=ALU.add),
                      reads=[B3, pq], writes=[pqn])
                if not last:
                    kb.op("act", lambda e: e.copy(out=fr(pqn[:, :, 1, :]), in_=psN[:, :, 1, :]), reads=[B3], writes=[pqn])
                    kb.op("dve", lambda e: e.tensor_copy(out=fr(qtn[:]), in_=psC), reads=[B2], writes=[qtn])
                yield
                cur = 1 - cur
                qcur = 1 - qcur
            TT = PQ[cur]
            W_ = B0[:, 0:128].rearrange("p (h i) -> p h i", h=2)
            Bt_ = B0[:, 128:256]
            U_ = B0[:, 256:512].rearrange("p (h i) -> p h i", h=2)
            for hh in range(2):
                kb.op("pe", lambda e: e.matmul(W_[:, hh, :], lhsT=fr(KBm[:, hh, 0, :]), rhs=fr(Vtm[:, tt, hh * 64:(hh + 1) * 64]), start=True, stop=True),
                      reads=[KBm, Vtm], writes=[B0])
            kb.op("pe", lambda e: e.transpose(out=Bt_, in_=BT[:, cols], identity=C.ident[:]), reads=[BT, C.ident], writes=[B0])
            AG_ = B1[:, 256:384]
            KG_ = B1[:, 384:512]
            kb.op("pe", lambda e: e.transpose(out=AG_, in_=Aa[:, cols], identity=C.ident[:]), reads=[Aa, C.ident], writes=[B1])
            kb.op("pe", lambda e: e.transpose(out=KG_, in_=KD[:, cols], identity=C.ident[:]), reads=[KD, C.ident], writes=[B1])
            yield
            kb.op("act", lambda e: e.copy(out=fr(BW[:, :, 64:128]), in_=W_), reads=[B0], writes=[BW])
            kb.op("dve", lambda e: e.tensor_copy(out=fr(BW[:, :, 0:64]), in_=Bt_.rearrange("p (h j) -> p h j", h=2)), reads=[B0], writes=[BW])
            kb.op("act", lambda e: e.copy(out=AGtm[:], in_=AG_), reads=[B1], writes=[AGtm])
            kb.op("act", lambda e: e.copy(out=KGtm[:], in_=KG_), reads=[B1], writes=[KGtm])
            yield
            for hh in range(2):
                kb.op("pe", lambda e: e.matmul(U_[:, hh, :], lhsT=fr(TT[:, hh, 0, :]), rhs=fr(BW[:, hh, :]), start=True, stop=True),
                      reads=[TT, BW], writes=[B0])
            yield
            kb.op("dve", lambda e: e.tensor_copy(out=fr(BU[:]), in_=U_), reads=[B0], writes=[BU])
            yield
            Y_ = B1[:, 0:128].rearrange("p (h i) -> p h i", h=2)
            R_ = B1[:, 128:256]
            for hh in range(2):
                kb.op("pe", lambda e: e.matmul(Y_[:, hh, :], lhsT=fr(XA[:, hh, 1, :]), rhs=fr(BU[:, hh, 64:128]), start=True, stop=False),
                      reads=[XA, BU], writes=[B1])
                kb.op("pe", lambda e: e.matmul(Y_[:, hh, :], lhsT=fr(KBm[:, hh, 1, :]), rhs=fr(Vtm[:, tt, hh * 64:(hh + 1) * 64]), start=False, stop=True),
                      reads=[KBm, Vtm], writes=[B1])
            for hh in range(2):
                kb.op("pe", lambda e: e.matmul(R_[hh * 64:(hh + 1) * 64, :], lhsT=BU[:, hh, 0:64], rhs=XA[:, hh, 1, :], start=True, stop=True),
                      reads=[BU, XA], writes=[B1])
            for n in range(2):
                tr = slice(n * 64, n * 64 + 64)
                bk = (B2, B3)[n]
                for hh in range(2):
                    pr = slice(hh * 64, hh * 64 + 64)
                    kb.op("pe", lambda e: e.matmul(bk[pr, 0:64], lhsT=BU[tr, hh, 0:64], rhs=AGtm[tr, pr], start=True, stop=True),
                          reads=[BU, AGtm], writes=[bk])
                    kb.op("pe", lambda e: e.matmul(bk[pr, 64:128], lhsT=AGtm[tr, pr], rhs=BU[tr, hh, 64:128], start=True, stop=False),
                          reads=[BU, AGtm], writes=[bk])
                    kb.op("pe", lambda e: e.matmul(bk[pr, 64:128], lhsT=KGtm[tr, pr], rhs=Vtm[tr, tt, pr], start=False, stop=True),
                          reads=[KGtm, Vtm], writes=[bk])
            kb.pe_fence()
            yield
            if d == 0:
                kb.op("act", lambda e: e.copy(out=Ysum[:, tt, :], in_=B1[:, 0:128]), reads=[B1], writes=[Ysum])
            else:
                kb.op("dve", lambda e: e.tensor_tensor(out=Ysum[:, tt, :], in0=B1[:, 0:128], in1=Ysum[:, tt, :], op=ALU.add),
                      reads=[B1, Ysum], writes=[Ysum])
            kb.op("dve", lambda e: e.tensor_tensor(out=RhT[:, cols], in0=R_, in1=RT[:, cols], op=ALU.add), reads=[B1, RT], writes=[RhT])
            for n in range(2):
                ch = tt * 2 + n
                bk = (B2, B3)[n]
                kb.op("dve", lambda e: e.scalar_tensor_tensor(out=MTa[:, ch, :], in0=ident2, scalar=GL[:, ch:ch + 1], in1=bk[:, 0:64],
                                                              op0=ALU.mult, op1=ALU.add), reads=[cst, GL, bk], writes=[MTa])
                kb.op("act", lambda e: e.copy(out=Ca[:, ch, :], in_=bk[:, 64:128]), reads=[bk], writes=[Ca])
            yield

        def run_tiles(d, RT, BT):
            pending = list(range(NT))
            active = []
            free_sets = [sets[0], sets[1]]
            S = free_sets.pop(0)
            g = tile_gen(S, d, pending.pop(0), RT, BT)
            active.append((g, S))
            for _ in range(8):
                next(g)
            while active or pending:
                if pending and free_sets:
                    S = free_sets.pop(0)
                    active.append((tile_gen(S, d, pending.pop(0), RT, BT), S))
                for item in list(active):
                    g, S = item
                    try:
                        next(g)
                    except StopIteration:
                        active.remove(item)
                        free_sets.append(S)

        for cc in ccs:
            rows = slice(cc * 128, (cc + 1) * 128)
            kb.dma("sp", Rr[:], D.zs_d.ap()[cc * 128:(cc + 1) * 128, :], reads=[D.zs_d], writes=[Rr])
            kb.dma("act", Kk[:], D.zs_d.ap()[1024 + cc * 128:1024 + (cc + 1) * 128, :], reads=[D.zs_d], writes=[Kk])
            kb.dma("sp", Vv[:], D.zs_d.ap()[2048 + cc * 128:2048 + (cc + 1) * 128, :], reads=[D.zs_d], writes=[Vv])
            kb.dma("pool", lnw[:], D.lnx_w.ap()[cc * 128:(cc + 1) * 128].partition_broadcast(128), writes=[lnw])
            kb.dma("pool", lnb[:], D.lnx_b.ap()[cc * 128:(cc + 1) * 128].partition_broadcast(128), writes=[lnb])
            for g in range(4):
                for j in range(4):
                    tt = g * 4 + j
                    kb.op("pe", lambda e: e.transpose(out=psX[:, j * 128:(j + 1) * 128], in_=Vv[:, tt * 128:(tt + 1) * 128], identity=C.ident[:]),
                          reads=[Vv, C.ident], writes=[psX])
                kb.op("act", lambda e: e.copy(out=fr(Vtm[:, g * 4:(g + 1) * 4, :]), in_=psX[:].rearrange("p (j c) -> p j c", j=4)), reads=[psX], writes=[Vtm])
            kb.op("pool", lambda e: e.tensor_scalar(out=KKn[:], in0=Kk[:], scalar1=rw[:, cc, 4:5], scalar2=None, op0=ALU.mult),
                  reads=[Kk, rw], writes=[KKn])
            kb.op("act", lambda e: e.activation(out=XI[:], in_=KKn[:], func=AF.Square), reads=[KKn], writes=[XI])
            for tb in range(4):
                kb.op("pe", lambda e: e.matmul(psX[:], lhsT=masks[:, 5, :], rhs=XI[:, tb * 512:(tb + 1) * 512], start=True, stop=True),
                      reads=[masks, XI], writes=[psX])
                kb.op("act", lambda e: e.activation(out=XE[:, tb * 512:(tb + 1) * 512], in_=psX[:], func=AF.Sqrt), reads=[psX], writes=[XE])
            kb.op("dve", lambda e: e.tensor_scalar(out=XE[:], in0=XE[:], scalar1=1e-12, scalar2=None, op0=ALU.max), reads=[XE], writes=[XE])
            kb.op("dve", lambda e: e.reciprocal(out=XE[:], in_=XE[:]), reads=[XE], writes=[XE])
            kb.op("pool", lambda e: e.tensor_tensor(out=KKn[:], in0=KKn[:], in1=XE[:], op=ALU.mult), reads=[KKn, XE], writes=[KKn])
            chk(1)
            def prep_gen(d):
                yield
                kb.dma("sp", XE[:], D.ld_d.ap()[d, cc * 128:(cc + 1) * 128, :], reads=[D.ld_d], writes=[XE])
                yield
                kb.dma("act", Aa[:], D.a_d.ap()[d, cc * 128:(cc + 1) * 128, :], reads=[D.a_d], writes=[Aa])
                yield
                kb.op("dve", lambda e: e.tensor_scalar(out=KD[:], in0=Aa[:], scalar1=-1.0, scalar2=rw[:, cc, 5:6], op0=ALU.add, op1=ALU.mult),
                      reads=[Aa, rw], writes=[KD])
                yield
                kb.op("dve", lambda e: e.scalar_tensor_tensor(out=KD[:], in0=KD[:], scalar=1.0, in1=Kk[:], op0=ALU.add, op1=ALU.mult),
                      reads=[KD, Kk], writes=[KD])
                yield
                kb.op("dve", lambda e: e.scalar_tensor_tensor(out=XI[:], in0=KD[:], scalar=rw[:, cc, 6:7], in1=Rr[:], op0=ALU.mult, op1=ALU.mult),
                      reads=[KD, rw, Rr], writes=[XI])
                for tt in range(NT):
                    kb.op("pe", lambda e: e.matmul(psX[:, tt * 2:tt * 2 + 2], lhsT=XI[:, tt * 128:(tt + 1) * 128], rhs=sel, start=True, stop=True),
                          reads=[XI, cst], writes=[psX])
                yield
                kb.op("act", lambda e: e.copy(out=stt[:, 64 + 32 * d:96 + 32 * d], in_=psX[:, 0:32]), reads=[psX], writes=[stt])
                yield
                kb.op("pool", lambda e: e.tensor_tensor(out=Aa[:], in0=Aa[:], in1=KKn[:], op=ALU.mult), reads=[Aa, KKn], writes=[Aa])
                yield
                kb.op("dve", lambda e: e.tensor_tensor_scan(out=XI[:], data0=segt[:], data1=XE[:], initial=0.0, op0=ALU.mult, op1=ALU.add),
                      reads=[segt, XE], writes=[XI])
                yield
                kb.op("pool", lambda e: e.tensor_copy(out=tot[:], in_=XI[:].rearrange("p (n s) -> p n s", s=64)[:, :, 63]), reads=[XI], writes=[tot])
                yield
                kb.op("act", lambda e: e.activation(out=GL[:], in_=tot[:], func=AF.Exp), reads=[tot], writes=[GL])
                if d == 0:
                    kb.op("pool", lambda e: e.tensor_tensor(out=XE[:], in0=XI[:], in1=XE[:], op=ALU.subtract), reads=[XI, XE], writes=[XE])
                else:
                    kb.op("dve", lambda e: e.tensor_tensor(out=XI[:].rearrange("p (n s) -> p n s", s=64),
                                                           in0=tot[:, :, None].to_broadcast([128, 32, 64]),
                                                           in1=XI[:].rearrange("p (n s) -> p n s", s=64), op=ALU.subtract),
                          reads=[tot, XI], writes=[XI])
                    kb.op("pool", lambda e: e.tensor_tensor(out=XE[:], in0=XI[:], in1=XE[:], op=ALU.add), reads=[XI, XE], writes=[XE])
                cI, cE = (XI, XE) if d == 0 else (XE, XI)
                yield
                kb.op("act", lambda e: e.activation(out=fr(E1[:]), in_=cI[:], func=AF.Exp), reads=[cI], writes=[E1])
                yield
                kb.op("act", lambda e: e.activation(out=cI[:], in_=cI[:], func=AF.Exp, scale=-1.0), reads=[cI], writes=[cI])
                yield
                kb.op("act", lambda e: e.activation(out=cE[:], in_=cE[:], func=AF.Exp), reads=[cE], writes=[cE])
                yield
                kb.op("dve", lambda e: e.tensor_tensor(out=fr(E1[:]), in0=E1[:], in1=Rr[:], op=ALU.mult), reads=[E1, Rr], writes=[E1])
                yield
                kb.op("pool", lambda e: e.tensor_tensor(out=fr(KT[:]), in0=KD[:], in1=cI[:], op=ALU.mult), reads=[KD, cI], writes=[KT])
                yield
                kb.op("dve", lambda e: e.tensor_tensor(out=fr(AT[:]), in0=Aa[:], in1=cI[:], op=ALU.mult), reads=[Aa, cI], writes=[AT])
                yield
                kb.op("dve", lambda e: e.scalar_tensor_tensor(out=fr(BTb[:]), in0=cE[:], scalar=-1.0, in1=KKn[:], op0=ALU.mult, op1=ALU.mult),
                      reads=[cE, KKn], writes=[BTb])
                yield
                kb.op("pool", lambda e: e.tensor_tensor(out=cI[:].rearrange("p (n s) -> p n s", s=64), in0=cI[:].rearrange("p (n s) -> p n s", s=64),
                                                        in1=GL[:, :, None].to_broadcast([128, 32, 64]), op=ALU.mult), reads=[cI, GL], writes=[cI])
                yield
                kb.op("dve", lambda e: e.tensor_tensor(out=KD[:], in0=KD[:], in1=cI[:], op=ALU.mult), reads=[KD, cI], writes=[KD])
                yield
                kb.op("pool", lambda e: e.tensor_tensor(out=Aa[:], in0=Aa[:], in1=cI[:], op=ALU.mult), reads=[Aa, cI], writes=[Aa])
                yield

            def seq_gen(d):
                kb.op("pool", lambda e: e.memset(H[0][:], 0.0), writes=[H[0]])
                order = range(32) if d == 0 else range(31, -1, -1)
                hc = 0
                for ch in order:
                    tt, n = ch // 2, ch % 2
                    ccols = slice(ch * 64, ch * 64 + 64)
                    Hc, Hn = H[hc], H[1 - hc]
                    tr = slice(n * 64, n * 64 + 64)
                    for hh in range(2):
                        pr = slice(hh * 64, hh * 64 + 64)
                        bk = bkH[hh]
                        kb.op("pe", lambda e: e.matmul(bk[pr, 64:128], lhsT=MTa[pr, ch, :], rhs=Hc[pr, :], start=True, stop=True),
                              reads=[MTa, Hc], writes=[bk])
                        kb.op("pe", lambda e: e.matmul(bk[tr, 0:64], lhsT=RhT[pr, ccols], rhs=Hc[pr, :], start=True, stop=True),
                              reads=[RhT, Hc], writes=[bk])
                    for hh in range(2):
                        pr = slice(hh * 64, hh * 64 + 64)
                        bk = bkH[hh]
                        kb.op("dve", lambda e: e.tensor_tensor(out=Hn[pr, :], in0=bk[pr, 64:128], in1=Ca[pr, ch, :], op=ALU.add),
                              reads=[bk, Ca], writes=[Hn])
                    for hh in range(2):
                        pr = slice(hh * 64, hh * 64 + 64)
                        bk = bkH[hh]
                        kb.op("act" if False else "dve", lambda e: e.tensor_tensor(out=Ysum[tr, tt, pr], in0=bk[tr, 0:64], in1=Ysum[tr, tt, pr], op=ALU.add),
                              reads=[bk, Ysum], writes=[Ysum])
                    hc = 1 - hc
                    yield
                yield

            def exhaust(g):
                for _ in g:
                    pass

            exhaust(prep_gen(0))
            run_tiles(0, E1, BTb)
            sg = seq_gen(0)
            pg = prep_gen(1)
            done_p = done_s = False
            while not (done_p and done_s):
                if not done_p:
                    try:
                        next(pg)
                    except StopIteration:
                        done_p = True
                for _ in range(2):
                    if not done_s:
                        try:
                            next(sg)
                        except StopIteration:
                            done_s = True
            run_tiles(1, E1, BTb)
            exhaust(seq_gen(1))
            chk(10)
            if dbg is not None and "Ysum" in dbg:
                kb.dma("sp", D.dbgY.ap()[:, cc * 128:(cc + 1) * 128].rearrange("(tt p) c -> p tt c", p=128), Ysum[:], reads=[Ysum], writes=[D.dbgY])
            Y3 = Ysum[:].rearrange("p t (h i) -> p (t h) i", h=2)
            st_mu = stt[:, 0:32]
            st_var = stt[:, 32:64]
            bon = stt[:, 128:160]
            kb.op("dve", lambda e: e.tensor_reduce(out=st_mu, in_=Y3, axis=AX.X, op=ALU.add), reads=[Ysum], writes=[stt])
            kb.op("dve", lambda e: e.tensor_scalar(out=st_mu, in0=st_mu, scalar1=1.0 / 64, scalar2=None, op0=ALU.mult), reads=[stt], writes=[stt])
            kb.op("dve", lambda e: e.tensor_tensor(out=Y3, in0=Y3, in1=st_mu[:, :, None].to_broadcast([128, 32, 64]), op=ALU.subtract),
                  reads=[Ysum, stt], writes=[Ysum])
            sqv = XI[:].rearrange("p (a i) -> p a i", i=64)
            kb.op("act", lambda e: e.activation(out=sqv, in_=Y3, func=AF.Square), reads=[Ysum], writes=[XI])
            kb.op("dve", lambda e: e.tensor_reduce(out=st_var, in_=sqv, axis=AX.X, op=ALU.add), reads=[XI], writes=[stt])
            kb.op("dve", lambda e: e.tensor_scalar(out=st_var, in0=st_var, scalar1=1.0 / 64, scalar2=GN_EPS, op0=ALU.mult, op1=ALU.add),
                  reads=[stt], writes=[stt])
            kb.op("act", lambda e: e.activation(out=st_var, in_=st_var, func=AF.Sqrt), reads=[stt], writes=[stt])
            kb.op("dve", lambda e: e.reciprocal(out=st_var, in_=st_var), reads=[stt], writes=[stt])
            kb.op("dve", lambda e: e.tensor_tensor(out=Y3, in0=Y3, in1=st_var[:, :, None].to_broadcast([128, 32, 64]), op=ALU.mult),
                  reads=[Ysum, stt], writes=[Ysum])
            kb.op("pool", lambda e: e.tensor_tensor(out=Ysum[:], in0=Ysum[:], in1=lnw[:, None, :].to_broadcast([128, NT, 128]), op=ALU.mult),
                  reads=[Ysum, lnw], writes=[Ysum])
            kb.op("pool", lambda e: e.tensor_tensor(out=Ysum[:], in0=Ysum[:], in1=lnb[:, None, :].to_broadcast([128, NT, 128]), op=ALU.add),
                  reads=[Ysum, lnb], writes=[Ysum])
            kb.op("dve", lambda e: e.tensor_tensor(out=bon, in0=stt[:, 64:96], in1=stt[:, 96:128], op=ALU.add), reads=[stt], writes=[stt])
            kb.op("dve", lambda e: e.tensor_scalar(out=bon, in0=bon, scalar1=0.5, scalar2=None, op0=ALU.mult), reads=[stt], writes=[stt])
            V3 = Vtm[:].rearrange("p t (h i) -> p (t h) i", h=2)
            kb.op("pool", lambda e: e.tensor_tensor(out=sqv, in0=V3, in1=bon[:, :, None].to_broadcast([128, 32, 64]), op=ALU.mult),
                  reads=[Vtm, stt], writes=[XI])
            kb.op("pool", lambda e: e.tensor_tensor(out=Y3, in0=Y3, in1=sqv, op=ALU.add), reads=[Ysum, XI], writes=[Ysum])
            gt = XE[:].rearrange("p (t c) -> p t c", c=128)
            kb.dma("sp", gt, D.g_d.ap()[:, cc * 128:(cc + 1) * 128].rearrange("(tt p) c -> p tt c", p=128), reads=[D.g_d], writes=[XE])
            kb.op("dve", lambda e: e.tensor_tensor(out=Ysum[:], in0=Ysum[:], in1=gt, op=ALU.mult), reads=[Ysum, XE], writes=[Ysum])
            if dbg is not None and "yfin" in dbg:
                kb.dma("sp", D.dbgF.ap()[:, cc * 128:(cc + 1) * 128].rearrange("(tt p) c -> p tt c", p=128), Ysum[:], reads=[Ysum], writes=[D.dbgF])
            for g in range(4):
                for j in range(4):
                    tt = g * 4 + j
                    kb.op("pe", lambda e: e.transpose(out=psX[:, j * 128:(j + 1) * 128], in_=Ysum[:, tt, :], identity=C.ident[:]),
                          reads=[Ysum, C.ident], writes=[psX])
                kb.op("act", lambda e: e.copy(out=KD[:, g * 512:(g + 1) * 512], in_=psX[:]), reads=[psX], writes=[KD])
            yb = Aa[:].bitcast(BF16)[:, 0:2048]
            kb.op("dve", lambda e: e.tensor_copy(out=yb, in_=KD[:]), reads=[KD], writes=[Aa])
            kb.dma("sp", D.ymT_d.ap()[cc], yb, reads=[Aa], writes=[D.ymT_d])
        kb.barrier()


def host_consts():
    S = 2048
    rows = np.repeat(np.arange(32), 64).astype(np.float32)
    cols = np.tile(np.arange(64), 32).astype(np.float32)
    inv = (10000.0 ** (-np.arange(0, 32, 2, dtype=np.float32) / 32)).astype(np.float32)
    ar = rows[:, None] * inv[None]
    ac = cols[:, None] * inv[None]
    tabC = np.concatenate([np.cos(ar), np.cos(ac)], 1).astype(np.float32)
    tabS = np.concatenate([np.sin(ar), np.sin(ac)], 1).astype(np.float32)
    ident = np.eye(128, dtype=np.float32)
    iota = np.tile(np.arange(128, dtype=np.float32)[None], (128, 1))
    r = np.arange(128)[:, None]
    s = np.arange(128)[None, :]
    same = (r // 64) == (s // 64)
    masks = np.zeros((128, 6, 128), np.float32)
    masks[:, 0] = same & (r < s)
    masks[:, 1] = same & (r <= s)
    masks[:, 2] = same & (r > s)
    masks[:, 3] = same & (r >= s)
    masks[:, 4] = same & (r > s)
    masks[:, 5] = same
    rcst = np.zeros((128, 66 + 2048), np.float32)
    pp = np.arange(128)
    rcst[pp, pp % 64] = 1.0
    rcst[:, 64] = (pp // 64 == 0)
    rcst[:, 65] = (pp // 64 == 1)
    seg = np.ones(2048, np.float32); seg[::64] = 0.0
    rcst[:, 66:] = seg[None]
    import ml_dtypes
    segm = np.ascontiguousarray(np.broadcast_to(seg[None], (128, 2048))).astype(ml_dtypes.bfloat16)
    return dict(tabC=tabC, tabS=tabS, ident=ident, iota=iota, masks=masks, rcst=rcst, segm=segm)

def prep_shared(inp):
    L = 0
    d = host_consts()
    mu_c = np.zeros((128, 3 * NRCH), np.float32)
    for ci, (c0, cs) in enumerate(RCH):
        mu_c[:cs, ci] = inp["mu_prev"][L, c0:c0 + cs]
        mu_c[:cs, NRCH + ci] = inp["mu_next"][L, c0:c0 + cs]
    d["mu_c"] = mu_c
    w = inp["w_in"][L].reshape(16, 128, 5248)
    wt = np.empty((128, 16 * 5248), np.float32)
    for (c0, cs) in RCH:
        wt[:, 16 * c0:16 * (c0 + cs)] = w[:, :, c0:c0 + cs].transpose(1, 0, 2).reshape(128, 16 * cs)
    RWK = 3712
    for cg in range(3):
        for hf in range(2):
            o0 = 16 * RWK + (cg * 2 + hf) * 4096
            wt[:, o0:o0 + 4096] = w[hf * 8:(hf + 1) * 8, :, RWK + cg * 512:RWK + (cg + 1) * 512].transpose(1, 0, 2).reshape(128, 4096)
    d["w_in"] = wt
    for k in ["norm1_w", "norm2_w", "q_norm_w", "k_norm_w", "w_out", "w_pq", "w2", "a2", "g2", "lnx_w", "lnx_b", "v_tab"]:
        d[k] = np.ascontiguousarray(inp[k][L])
    d["normf_w"] = np.ascontiguousarray(inp["normf_w"])
    d["skT"] = np.ascontiguousarray(inp["sub_keys"][L].reshape(16, 128, 128).transpose(2, 0, 1))
    d["u_tabT"] = np.ascontiguousarray(inp["u_tab"][L].reshape(128, 128, 16, 128).transpose(0, 3, 2, 1)).reshape(128, 128, 2048)
    rw = np.zeros((128, 8, 8), np.float32)
    def ch(v):
        return v.reshape(8, 128).T
    rw[:, :, 0] = ch(inp["w0"][L, 0]); rw[:, :, 1] = ch(inp["w0"][L, 1])
    rw[:, :, 2] = ch(inp["a0"][L, 0]); rw[:, :, 3] = ch(inp["a0"][L, 1])
    rw[:, :, 4] = ch(inp["k_k"][L]); rw[:, :, 5] = ch(inp["k_a"][L]); rw[:, :, 6] = ch(inp["r_k"][L].reshape(-1))
    d["rw_c"] = rw
    return d


def build_program():
    nc = bass.Bass("TRN2", target_bir_lowering=False)
    kb = KB(nc)
    D = declare(kb, None)
    C = consts(kb, D)
    phase_A(kb, C, D)
    phase_Rpre(kb, C, D)
    phase_R(kb, C, D)
    phase_T(kb, C, D)
    phase_O(kb, C, D, D.ymT_d)
    phase_P(kb, C, D)
    phase_F(kb, C, D)
    kb.finish("sp")
    return nc


def kernel(**inputs):
    inp = {k: np.asarray(v) for k, v in inputs.items()}
    shared = prep_shared(inp)
    nc = build_program()
    in_maps = []
    for b in range(8):
        d = dict(shared)
        d["x"] = np.ascontiguousarray(inp["x"][b])
        in_maps.append(d)
    res = run_bass_kernel_spmd(nc, in_maps, core_ids=list(range(8)))
    out = np.stack([np.asarray(r["out"], dtype=np.float32) for r in res.results], axis=0)
    return out
```

```python
import numpy as np
import contextlib
import concourse.bass as bass
import concourse.mybir as mybir
from concourse.bass_utils import run_bass_kernel_spmd

F32 = mybir.dt.float32
BF16 = mybir.dt.bfloat16
U32 = mybir.dt.uint32
ALU = mybir.AluOpType
AF = mybir.ActivationFunctionType
AX = mybir.AxisListType


class T:
    def __init__(self, h, name):
        self.h = h
        self.name = name
        self.w = None
        self.r = {}
        self.dsem = None
        self.dcnt = 0
        self.is_psum = False

    def __getitem__(self, k):
        return self.h[k]

    def ap(self):
        return self.h.ap() if hasattr(self.h, "ap") else self.h[:]


class KB:
    def __init__(self, nc):
        self.nc = nc
        self.es = contextlib.ExitStack()
        self.engs = {"pe": nc.tensor, "act": nc.scalar, "dve": nc.vector, "pool": nc.gpsimd, "sp": nc.sync}
        self.sem = {}
        self.cnt = {}
        for e in self.engs:
            self.sem[e] = self.es.enter_context(nc.semaphore("s_" + e))
            self.cnt[e] = 0
        self.waited = {}
        self.alltensors = []
        self.dsems = []
        self.n_ins = 0

    def sb(self, name, shape, dt=F32, stack=None):
        h = (stack or self.es).enter_context(self.nc.sbuf_tensor(name, list(shape), dt))
        t = T(h, name)
        return t

    def ps(self, name, shape, dt=F32, stack=None):
        h = (stack or self.es).enter_context(self.nc.psum_tensor(name, list(shape), dt))
        t = T(h, name)
        t.is_psum = True
        return t

    def dram(self, name, shape, dt=F32, kind="Internal"):
        h = self.nc.dram_tensor(name, list(shape), dt, kind=kind)
        return T(h, name)

    def view(self, t, name=None):
        return t

    def _wait(self, eng, tok):
        if tok is None:
            return
        sem, val = tok
        key = (eng, id(sem))
        if self.waited.get(key, 0) >= val:
            return
        self.engs[eng].wait_ge(sem, val)
        self.waited[key] = val

    def _deps(self, eng, reads, writes):
        own = id(self.sem[eng])
        for t in reads:
            if t.w is not None:
                if eng == "pe" and id(t.w[0]) == own:
                    continue
                self._wait(eng, t.w)
        for t in writes:
            if t.w is not None and id(t.w[0]) != own:
                self._wait(eng, t.w)
            for k, tok in t.r.items():
                if k == own:
                    continue
                self._wait(eng, tok)

    def _mark(self, tok, reads, writes):
        for t in reads:
            if t in writes:
                continue
            t.r[id(tok[0])] = tok
        for t in writes:
            t.w = tok
            t.r = {}

    def op(self, eng, fn, reads=(), writes=()):
        psr = [t for t in reads if t.is_psum and t not in writes]
        if psr:
            writes = list(writes) + psr
        self._deps(eng, reads, writes)
        ins = fn(self.engs[eng])
        self.cnt[eng] += 1
        ins.then_inc(self.sem[eng], 1)
        tok = (self.sem[eng], self.cnt[eng])
        self._mark(tok, reads, writes)
        self.n_ins += 1
        return ins

    def dma(self, q, out_ap, in_ap, reads=(), writes=(), **kw):
        assert len(writes) == 1
        dst = writes[0]
        if dst.dsem is None:
            dst.dsem = self.es.enter_context(self.nc.semaphore("d_" + dst.name))
            self.dsems.append(dst)
        self._deps(q, reads, [])
        if dst.w is not None and dst.w[0] is not dst.dsem:
            self._wait(q, dst.w)
        for k, tok in dst.r.items():
            self._wait(q, tok)
        ins = self.engs[q].dma_start(out=out_ap, in_=in_ap, **kw)
        dst.dcnt += 16
        ins.then_inc(dst.dsem, 16)
        tok = (dst.dsem, dst.dcnt)
        for t in reads:
            t.r[id(tok[0])] = tok
        dst.w = tok
        dst.r = {}
        self.n_ins += 1
        return ins

    def pe_fence(self):
        if self.cnt["pe"] > 0:
            self._wait("pe", (self.sem["pe"], self.cnt["pe"]))

    def barrier(self):
        toks = [(self.sem[e], self.cnt[e]) for e in self.engs if self.cnt[e] > 0]
        toks += [(t.dsem, t.dcnt) for t in self.dsems if t.dcnt > 0]
        for e in self.engs:
            for tok in toks:
                if tok[0] is self.sem[e]:
                    continue
                self._wait(e, tok)

    def finish(self, eng="sp"):
        for t in self.dsems:
            if t.dcnt > 0:
                self._wait(eng, (t.dsem, t.dcnt))
        for e in self.engs:
            if e != eng and self.cnt[e] > 0:
                self._wait(eng, (self.sem[e], self.cnt[e]))


EPS = 1e-6
NT = 16
RCH = [(i * 128, 128) for i in range(24)] + [(3072 + i * 96, 96) for i in range(4)] + [(3456, 128), (3584, 128)]
NRCH = len(RCH)
RW = 3712


class Obj:
    pass


def declare(kb, debug):
    D = Obj()

    def inp(name, shape, dt=F32):
        setattr(D, name, kb.dram(name, shape, dt, kind="ExternalInput"))

    def scr(name, shape, dt=F32):
        kind = "ExternalOutput" if (debug and name in debug) else "Internal"
        setattr(D, name, kb.dram(name, shape, dt, kind=kind))

    inp("x", [2048, 2048])
    inp("w_in", [128, 16 * 5248])
    inp("mu_c", [128, 3 * NRCH])
    inp("norm1_w", [2048])
    inp("norm2_w", [2048])
    inp("normf_w", [2048])
    inp("q_norm_w", [64])
    inp("k_norm_w", [64])
    inp("tabC", [2048, 32])
    inp("tabS", [2048, 32])
    inp("ident", [128, 128])
    inp("w_out", [2048, 2048])
    inp("w_pq", [2048, 2048])
    inp("skT", [128, 16, 128])
    inp("iota", [128, 128])
    inp("u_tabT", [128, 128, 2048])
    inp("v_tab", [16384, 2048])
    inp("w2", [2, 96, 1024])
    inp("a2", [2, 96, 1024])
    inp("g2", [256, 1024])
    inp("rw_c", [128, 8, 8])
    inp("lnx_w", [1024])
    inp("lnx_b", [1024])
    inp("masks", [128, 6, 128])
    if debug and "in_ymT" in debug:
        inp("ymT_in", [16, 128, 2048], BF16)
    if debug and "in_x1" in debug:
        inp("x1_in", [2048, 2048])
    scr("zs_d", [RW, 2048])
    scr("qkv_d", [2048, 1536])
    scr("ymT_d", [16, 128, 2048], BF16)
    scr("x1_d", [2048, 2048])
    scr("G_d", [128, 128, 2048], BF16)
    scr("peT_d", [2048, 2048])
    scr("A_d", [128, 128, 2048], BF16)
    scr("ld_d", [2, 1024, 2048])
    scr("a_d", [2, 1024, 2048])
    scr("g_d", [2048, 1024])
    inp("rcst", [128, 66 + 2048])
    inp("segm", [128, 2048], BF16)
    if debug and "dbgY" in debug:
        setattr(D, "dbgY", kb.dram("dbgY", [2048, 1024], F32, kind="ExternalOutput"))
        setattr(D, "dbgF", kb.dram("dbgF", [2048, 1024], F32, kind="ExternalOutput"))
    scr("S_d", [2048, 16, 128])
    scr("h2T_d", [16, 128, 2048], BF16)
    setattr(D, "out", kb.dram("out", [2048, 2048], F32, kind="ExternalOutput"))
    return D


def consts(kb, D):
    C = Obj()
    C.ident = kb.sb("c_ident", [128, 128], F32)
    kb.dma("sp", C.ident[:], D.ident.ap(), writes=[C.ident])
    C.identb = kb.sb("c_identb", [128, 128], BF16)
    kb.op("dve", lambda e: e.tensor_copy(out=C.identb[:], in_=C.ident[:]), reads=[C.ident], writes=[C.identb])
    return C


def norm_to_T(kb, C, src, wvec, hT, tag):
    with contextlib.ExitStack() as st:
        wb = kb.sb(tag + "wb", [128, 2048], F32, st)
        kb.dma("pool", wb[:], wvec.ap().partition_broadcast(128), writes=[wb])
        xts = [kb.sb(f"{tag}x{i}", [128, 2048], F32, st) for i in range(2)]
        junk = kb.sb(tag + "junk", [128, 2048], BF16, st)
        hb = [kb.sb(f"{tag}h{i}", [128, 2048], BF16, st) for i in range(2)]
        ss = kb.sb(tag + "ss", [128, NT], F32, st)
        rs = kb.sb(tag + "rs", [128, NT], F32, st)
        ptr = [kb.ps(f"{tag}ps{i}", [128, 1024], BF16, st) for i in range(2)]
        kb.op("pool", lambda e: e.memset(ss[:], 0.0), writes=[ss])
        for tt in range(NT):
            xt = xts[tt % 2]
            kb.dma("sp", xt[:], src.ap()[tt * 128:(tt + 1) * 128, :], writes=[xt])
            kb.op("act", lambda e: e.activation(out=junk[:], in_=xt[:], func=AF.Square, accum_out=ss[:, tt:tt + 1]),
                  reads=[xt], writes=[junk, ss])
            kb.op("dve", lambda e: e.tensor_scalar(out=rs[:, tt:tt + 1], in0=ss[:, tt:tt + 1], scalar1=1.0 / 2048, scalar2=EPS,
                                                   op0=ALU.mult, op1=ALU.add), reads=[ss], writes=[rs])
            kb.op("act", lambda e: e.activation(out=rs[:, tt:tt + 1], in_=rs[:, tt:tt + 1], func=AF.Sqrt), reads=[rs], writes=[rs])
            kb.op("dve", lambda e: e.reciprocal(out=rs[:, tt:tt + 1], in_=rs[:, tt:tt + 1]), reads=[rs], writes=[rs])
            h = hb[tt % 2]
            kb.op("dve", lambda e: e.scalar_tensor_tensor(out=h[:], in0=xt[:], scalar=rs[:, tt:tt + 1], in1=wb[:],
                                                          op0=ALU.mult, op1=ALU.mult), reads=[xt, rs, wb], writes=[h])
            for g in range(2):
                p = ptr[g]
                for j in range(8):
                    dc = g * 8 + j
                    kb.op("pe", lambda e: e.transpose(out=p[:, j * 128:(j + 1) * 128], in_=h[:, dc * 128:(dc + 1) * 128],
                                                      identity=C.identb[:]), reads=[h, C.identb], writes=[p])
                eng = "act" if g == 0 else "dve"
                src_ap = p[:].rearrange("p (j t) -> p j t", j=8)
                dst_ap = hT[:, g * 8:(g + 1) * 8, tt * 128:(tt + 1) * 128]
                if eng == "act":
                    kb.op("act", lambda e: e.copy(out=dst_ap, in_=src_ap), reads=[p], writes=[hT])
                else:
                    kb.op("dve", lambda e: e.tensor_copy(out=dst_ap, in_=src_ap), reads=[p], writes=[hT])
        kb.barrier()


def phase_A(kb, C, D):
    with contextlib.ExitStack() as st:
        hT = kb.sb("hT", [128, 16, 2048], BF16, st)
        norm_to_T(kb, C, D.x, D.norm1_w, hT, "n1")
        mu = kb.sb("mu", [128, 3 * NRCH], F32, st)
        kb.dma("sp", mu[:], D.mu_c.ap(), writes=[mu])
        kb.op("dve", lambda e: e.tensor_tensor(out=mu[:, 60:90], in0=mu[:, 0:30], in1=mu[:, 30:60], op=ALU.add), reads=[mu], writes=[mu])
        kb.op("dve", lambda e: e.tensor_scalar(out=mu[:, 60:90], in0=mu[:, 60:90], scalar1=-1.0, scalar2=1.0, op0=ALU.mult, op1=ALU.add),
              reads=[mu], writes=[mu])
        stg = [kb.sb(f"a_stg{i}", [128, 4096], F32, st) for i in range(2)]
        wbf = [kb.sb(f"a_wbf{i}", [128, 16, 128], BF16, st) for i in range(2)]
        accs = [kb.sb(f"a_acc{i}", [128, 2048], F32, st) for i in range(2)]
        pss = [[kb.ps(f"a_ps{i}_{j}", [128, 512], F32, st) for j in range(4)] for i in range(2)]
        def loadw(ci):
            c0, cs = RCH[ci]
            kb.dma("sp", stg[ci % 2][:, 0:16 * cs], D.w_in.ap()[:, 16 * c0:16 * (c0 + cs)], writes=[stg[ci % 2]])
        loadw(0)
        for ci, (c0, cs) in enumerate(RCH):
            sg = stg[ci % 2]
            sgv = sg[:, 0:16 * cs].rearrange("p (dc c) -> p dc c", dc=16)
            wb = wbf[ci % 2]
            kb.op("pool", lambda e: e.tensor_copy(out=wb[:, :, 0:cs], in_=sgv), reads=[sg], writes=[wb])
            if ci + 1 < NRCH:
                loadw(ci + 1)
            ps = pss[ci % 2]
            acc = accs[ci % 2]
            for tb in range(4):
                for dc in range(16):
                    kb.op("pe", lambda e: e.matmul(ps[tb][0:cs, :], lhsT=wb[:, dc, 0:cs], rhs=hT[:, dc, tb * 512:(tb + 1) * 512],
                                                   start=(dc == 0), stop=(dc == 15)), reads=[wb, hT], writes=[ps[tb]])
            for tb in range(4):
                kb.op("act", lambda e: e.activation(out=acc[0:cs, tb * 512:(tb + 1) * 512], in_=ps[tb][0:cs, :], func=AF.Copy,
                                                    scale=mu[0:cs, 60 + ci:61 + ci]), reads=[ps[tb], mu], writes=[acc])
            for tb in range(4):
                n = 512 if tb < 3 else 511
                d0 = tb * 512 + 1
                kb.op("dve", lambda e: e.scalar_tensor_tensor(out=acc[0:cs, d0:d0 + n], in0=ps[tb][0:cs, 0:n], scalar=mu[0:cs, ci:ci + 1],
                                                              in1=acc[0:cs, d0:d0 + n], op0=ALU.mult, op1=ALU.add),
                      reads=[ps[tb], mu, acc], writes=[acc])
                s0 = 1 if tb == 0 else 0
                n = 512 - s0
                d0 = tb * 512 + s0 - 1
                kb.op("dve", lambda e: e.scalar_tensor_tensor(out=acc[0:cs, d0:d0 + n], in0=ps[tb][0:cs, s0:512], scalar=mu[0:cs, 30 + ci:31 + ci],
                                                              in1=acc[0:cs, d0:d0 + n], op0=ALU.mult, op1=ALU.add),
                      reads=[ps[tb], mu, acc], writes=[acc])
            kb.dma("pool", D.zs_d.ap()[c0:c0 + cs, :], acc[0:cs, :], reads=[acc], writes=[D.zs_d])
        kb.barrier()
        with contextlib.ExitStack() as st2:
            wq = kb.sb("a_wq", [128, 16, 512], BF16, st2)
            ev = [kb.sb(f"a_ev{i}", [128, 512], F32, st2) for i in range(2)]
            for cg in range(3):
                c0 = RW + cg * 512
                for hf in range(2):
                    sg = stg[hf]
                    sgv = sg[:].rearrange("p (dc c) -> p dc c", dc=8)
                    o0 = 16 * RW + (cg * 2 + hf) * 4096
                    kb.dma("sp", sg[:], D.w_in.ap()[:, o0:o0 + 4096], writes=[sg])
                    kb.op("pool", lambda e: e.tensor_copy(out=wq[:, hf * 8:(hf + 1) * 8, :], in_=sgv), reads=[sg], writes=[wq])
                for tt in range(NT):
                    ps = pss[tt % 2][0]
                    for dc in range(16):
                        kb.op("pe", lambda e: e.matmul(ps[:], lhsT=hT[:, dc, tt * 128:(tt + 1) * 128], rhs=wq[:, dc, :],
                                                       start=(dc == 0), stop=(dc == 15)), reads=[wq, hT], writes=[ps])
                    o = ev[tt % 2]
                    kb.op("act", lambda e: e.copy(out=o[:], in_=ps[:]), reads=[ps], writes=[o])
                    kb.dma("sp", D.qkv_d.ap()[tt * 128:(tt + 1) * 128, cg * 512:(cg + 1) * 512], o[:], reads=[o], writes=[D.qkv_d])
            kb.barrier()


def phase_T(kb, C, D):
    with contextlib.ExitStack() as st:
        qT = kb.sb("t_qT", [128, 8, 2048], BF16, st)
        kT2 = kb.sb("t_kT2", [128, 4, 2048], BF16, st)
        vaug = kb.sb("t_vaug", [128, NT, 4, 65], BF16, st)
        wqk = kb.sb("t_wqk", [128, 20, 64], F32, st)
        w64 = kb.sb("t_w64", [128, 2, 64], F32, st)
        kb.dma("pool", w64[:, 0, :], D.q_norm_w.ap().partition_broadcast(128), writes=[w64])
        kb.dma("pool", w64[:, 1, :], D.k_norm_w.ap().partition_broadcast(128), writes=[w64])
        kb.op("dve", lambda e: e.tensor_scalar(out=wqk[:, 0:16, :], in0=w64[:, 0:1, :].to_broadcast([128, 16, 64]), scalar1=0.125, scalar2=None,
                                               op0=ALU.mult), reads=[w64], writes=[wqk])
        kb.op("dve", lambda e: e.tensor_copy(out=wqk[:, 16:20, :], in_=w64[:, 1:2, :].to_broadcast([128, 4, 64])), reads=[w64], writes=[wqk])
        kb.op("pool", lambda e: e.memset(vaug[:], 1.0), writes=[vaug])
        with contextlib.ExitStack() as st2:
            qk = [kb.sb(f"t_qk{i}", [128, 1536], F32, st2) for i in range(2)]
            tC = [kb.sb(f"t_tC{i}", [128, 2, 16], F32, st2) for i in range(2)]
            tS = [kb.sb(f"t_tS{i}", [128, 2, 16], F32, st2) for i in range(2)]
            sq = kb.sb("t_sq", [128, 20, 64], F32, st2)
            ss = kb.sb("t_ss", [128, 20], F32, st2)
            qn = kb.sb("t_qn", [128, 20, 64], F32, st2)
            t1 = kb.sb("t_t1", [128, 20, 2, 16], F32, st2)
            t2 = kb.sb("t_t2", [128, 20, 2, 16], F32, st2)
            t3 = kb.sb("t_t3", [128, 20, 2, 16], F32, st2)
            t4 = kb.sb("t_t4", [128, 20, 2, 16], F32, st2)
            qkr = kb.sb("t_qkr", [128, 20, 64], BF16, st2)
            kd = kb.sb("t_kd", [128, 4, 2, 64], BF16, st2)
            pq = kb.ps("t_pq", [128, 1024], BF16, st2)
            pk = kb.ps("t_pk", [128, 1024], BF16, st2)
            for tt in range(NT):
                q = qk[tt % 2]
                cC = tC[tt % 2]
                cS = tS[tt % 2]
                kb.dma("sp", q[:], D.qkv_d.ap()[tt * 128:(tt + 1) * 128, :], reads=[D.qkv_d], writes=[q])
                kb.dma("act", cC[:].rearrange("p a b -> p (a b)"), D.tabC.ap()[tt * 128:(tt + 1) * 128, :], writes=[cC])
                kb.dma("act", cS[:].rearrange("p a b -> p (a b)"), D.tabS.ap()[tt * 128:(tt + 1) * 128, :], writes=[cS])
                qv = q[:, 0:1280].rearrange("p (h d) -> p h d", h=20)
                kb.op("act", lambda e: e.activation(out=sq[:], in_=qv, func=AF.Square), reads=[q], writes=[sq])
                kb.op("dve", lambda e: e.tensor_reduce(out=ss[:], in_=sq[:], axis=AX.X, op=ALU.add), reads=[sq], writes=[ss])
                kb.op("dve", lambda e: e.tensor_scalar(out=ss[:], in0=ss[:], scalar1=1.0 / 64, scalar2=EPS, op0=ALU.mult, op1=ALU.add),
                      reads=[ss], writes=[ss])
                kb.op("act", lambda e: e.activation(out=ss[:], in_=ss[:], func=AF.Sqrt), reads=[ss], writes=[ss])
                kb.op("dve", lambda e: e.reciprocal(out=ss[:], in_=ss[:]), reads=[ss], writes=[ss])
                kb.op("dve", lambda e: e.tensor_tensor(out=qn[:], in0=qv, in1=ss[:, :, None].to_broadcast([128, 20, 64]), op=ALU.mult),
                      reads=[q, ss], writes=[qn])
                kb.op("pool", lambda e: e.tensor_tensor(out=qn[:], in0=qn[:], in1=wqk[:], op=ALU.mult), reads=[qn, wqk], writes=[qn])
                qn5 = qn[:].rearrange("p h (a b c) -> p h a b c", a=2, b=2)
                x1 = qn5[:, :, :, 0, :]
                x2 = qn5[:, :, :, 1, :]
                Cb = cC[:, None, :, :].to_broadcast([128, 20, 2, 16])
                Sb = cS[:, None, :, :].to_broadcast([128, 20, 2, 16])
                kb.op("dve", lambda e: e.tensor_tensor(out=t1[:], in0=x1, in1=Cb, op=ALU.mult), reads=[qn, cC], writes=[t1])
                kb.op("pool", lambda e: e.tensor_tensor(out=t2[:], in0=x2, in1=Sb, op=ALU.mult), reads=[qn, cS], writes=[t2])
                kb.op("pool", lambda e: e.tensor_tensor(out=t3[:], in0=x2, in1=Cb, op=ALU.mult), reads=[qn, cC], writes=[t3])
                kb.op("dve", lambda e: e.tensor_tensor(out=t4[:], in0=x1, in1=Sb, op=ALU.mult), reads=[qn, cS], writes=[t4])
                r5 = qkr[:].rearrange("p h (a b c) -> p h a b c", a=2, b=2)
                kb.op("dve", lambda e: e.tensor_tensor(out=r5[:, :, :, 0, :], in0=t1[:], in1=t2[:], op=ALU.subtract), reads=[t1, t2], writes=[qkr])
                kb.op("pool", lambda e: e.tensor_tensor(out=r5[:, :, :, 1, :], in0=t3[:], in1=t4[:], op=ALU.add), reads=[t3, t4], writes=[qkr])
                kb.op("pool", lambda e: e.tensor_copy(out=kd[:], in_=qkr[:, 16:20, None, :].to_broadcast([128, 4, 2, 64])), reads=[qkr], writes=[kd])
                for j in range(8):
                    kb.op("pe", lambda e: e.transpose(out=pq[:, j * 128:(j + 1) * 128], in_=qkr[:, 2 * j:2 * j + 2, :].rearrange("p a b -> p (a b)"),
                                                      identity=C.identb[:]), reads=[qkr, C.identb], writes=[pq])
                kb.op("act", lambda e: e.copy(out=qT[:, :, tt * 128:(tt + 1) * 128], in_=pq[:].rearrange("p (j t) -> p j t", j=8)),
                      reads=[pq], writes=[qT])
                for j in range(4):
                    kb.op("pe", lambda e: e.transpose(out=pk[:, j * 128:(j + 1) * 128], in_=kd[:, j, :, :].rearrange("p a b -> p (a b)"),
                                                      identity=C.identb[:]), reads=[kd, C.identb], writes=[pk])
                kb.op("dve", lambda e: e.tensor_copy(out=kT2[:, :, tt * 128:(tt + 1) * 128], in_=pk[:, 0:512].rearrange("p (j t) -> p j t", j=4)),
                      reads=[pk], writes=[kT2])
                kb.op("pool", lambda e: e.tensor_copy(out=vaug[:, tt, :, 0:64], in_=q[:, 1280:1536].rearrange("p (h d) -> p h d", h=4)),
                      reads=[q], writes=[vaug])
            kb.barrier()
        with contextlib.ExitStack() as st3:
            yatt = kb.sb("t_yatt", [128, NT, 1024], BF16, st3)
            pexp = [kb.sb(f"t_pexp{i}", [128, 512], BF16, st3) for i in range(3)]
            rinv = kb.sb("t_rinv", [128, 4], F32, st3)
            pss = [kb.ps(f"t_pss{i}", [128, 512], F32, st3) for i in range(2)]
            po = [kb.ps(f"t_po{i}", [128, 512], F32, st3) for i in range(4)]
            seq = [(h, qb, kt) for h in range(16) for qb in range(4) for kt in range(NT)]

            def emitS(i):
                h, qb, kt = seq[i]
                kv, c, b0 = h // 4, h // 2, (h % 2) * 64
                ps = pss[i % 2]
                kb.op("pe", lambda e: e.matmul(ps[:], lhsT=kT2[b0:b0 + 64, kv, kt * 128:(kt + 1) * 128],
                                               rhs=qT[b0:b0 + 64, c, qb * 512:(qb + 1) * 512], start=True, stop=True),
                      reads=[kT2, qT], writes=[ps])

            def emitE(i):
                ps = pss[i % 2]
                pe_ = pexp[i % 3]
                kb.op("act", lambda e: e.activation(out=pe_[:], in_=ps[:], func=AF.Exp), reads=[ps], writes=[pe_])

            def emitPV(i):
                h, qb, kt = seq[i]
                kv = h // 4
                pe_ = pexp[i % 3]
                for j in range(4):
                    kb.op("pe", lambda e: e.matmul(po[j][:, 0:65], lhsT=pe_[:, j * 128:(j + 1) * 128], rhs=vaug[:, kt, kv, :],
                                                   start=(kt == 0), stop=(kt == NT - 1)), reads=[pe_, vaug], writes=[po[j]])
                if kt == NT - 1:
                    for j in range(4):
                        kb.op("dve", lambda e: e.reciprocal(out=rinv[:, j:j + 1], in_=po[j][:, 64:65]), reads=[po[j]], writes=[rinv])
                        kb.op("dve", lambda e: e.tensor_scalar(out=yatt[:, qb * 4 + j, h * 64:(h + 1) * 64], in0=po[j][:, 0:64],
                                                               scalar1=rinv[:, j:j + 1], scalar2=None, op0=ALU.mult),
                              reads=[po[j], rinv], writes=[yatt])

            emitS(0)
            for i in range(len(seq)):
                if i + 1 < len(seq):
                    emitS(i + 1)
                emitE(i)
                emitPV(i)
            pt = [kb.ps(f"t_pt{i}", [128, 1024], BF16, st3) for i in range(2)]
            yT = [kb.sb(f"t_yT{i}", [128, 8, 128], BF16, st3) for i in range(2)]
            for tt in range(NT):
                p = pt[tt % 2]
                o = yT[tt % 2]
                for j in range(8):
                    kb.op("pe", lambda e: e.transpose(out=p[:, j * 128:(j + 1) * 128], in_=yatt[:, tt, j * 128:(j + 1) * 128],
                                                      identity=C.identb[:]), reads=[yatt, C.identb], writes=[p])
                kb.op("act", lambda e: e.copy(out=o[:], in_=p[:].rearrange("p (j t) -> p j t", j=8)), reads=[p], writes=[o])
                kb.dma("sp", D.ymT_d.ap()[8:16, :, tt * 128:(tt + 1) * 128].rearrange("j p t -> p j t"), o[:], reads=[o], writes=[D.ymT_d])
            kb.barrier()


def phase_O(kb, C, D, ym_src):
    with contextlib.ExitStack() as st:
        ymT = kb.sb("o_ymT", [128, 16, 2048], BF16, st)
        for j in range(16):
            kb.dma("sp" if j % 2 == 0 else "act", ymT[:, j, :], ym_src.ap()[j], reads=[ym_src], writes=[ymT])
        stg = [kb.sb(f"o_stg{i}", [128, 4096], F32, st) for i in range(2)]
        wo = kb.sb("o_wo", [128, 16, 512], BF16, st)
        xs = [kb.sb(f"o_xs{i}", [128, 512], F32, st) for i in range(2)]
        pss = [kb.ps(f"o_ps{i}", [128, 512], F32, st) for i in range(2)]
        w_v = D.w_out.ap().rearrange("(kc p) c -> p kc c", p=128)
        for dg in range(4):
            for hf in range(2):
                sg = stg[hf]
                sgv = sg[:].rearrange("p (dc c) -> p dc c", dc=8)
                kb.dma("sp" if hf == 0 else "act", sgv, w_v[:, hf * 8:(hf + 1) * 8, dg * 512:(dg + 1) * 512], writes=[sg])
                kb.op("pool", lambda e: e.tensor_copy(out=wo[:, hf * 8:(hf + 1) * 8, :], in_=sgv), reads=[sg], writes=[wo])
            for tt in range(NT):
                ps = pss[tt % 2]
                xt = xs[tt % 2]
                kb.dma("sp", xt[:], D.x.ap()[tt * 128:(tt + 1) * 128, dg * 512:(dg + 1) * 512], writes=[xt])
                for kc in range(16):
                    kb.op("pe", lambda e: e.matmul(ps[:], lhsT=ymT[:, kc, tt * 128:(tt + 1) * 128], rhs=wo[:, kc, :],
                                                   start=(kc == 0), stop=(kc == 15)), reads=[ymT, wo], writes=[ps])
                kb.op("dve", lambda e: e.tensor_tensor(out=xt[:], in0=ps[:], in1=xt[:], op=ALU.add), reads=[ps, xt], writes=[xt])
                kb.dma("act", D.x1_d.ap()[tt * 128:(tt + 1) * 128, dg * 512:(dg + 1) * 512], xt[:], reads=[xt], writes=[D.x1_d])
        kb.barrier()


def phase_P(kb, C, D):
    with contextlib.ExitStack() as stP:
        ET = kb.sb("p_ET", [128, 3, 2048], F32, stP)
        iota = kb.sb("p_iota", [128, 128], F32, stP)
        kb.dma("sp", iota[:], D.iota.ap(), writes=[iota])
        with contextlib.ExitStack() as st, kb.nc.named_scope("P1"):
            h2T = kb.sb("p_h2T", [128, 16, 2048], BF16, st)
            norm_to_T(kb, C, D.x1_d, D.norm2_w, h2T, "n2")
            for dc in range(16):
                kb.dma("sp" if dc % 2 == 0 else "act", D.h2T_d.ap()[dc], h2T[:, dc, :], reads=[h2T], writes=[D.h2T_d])
            skb = kb.sb("p_sk", [128, 16, 128], F32, st)
            kb.dma("sp", skb[:], D.skT.ap(), writes=[skb])
            stg = [kb.sb(f"p_stg{i}", [128, 16, 128], F32, st) for i in range(2)]
            wb = [kb.sb(f"p_wb{i}", [128, 16, 128], BF16, st) for i in range(2)]
            qTs = [kb.sb(f"p_qT{i}", [128, 2048], F32, st) for i in range(2)]
            sev = [kb.sb(f"p_sev{i}", [128, 16, 128], F32, st) for i in range(2)]
            pss = [[kb.ps(f"p_ps{i}_{j}", [128, 512], F32, st) for j in range(3)] for i in range(2)]
            w_v = D.w_pq.ap().rearrange("(dc p) c -> p dc c", p=128)
            for hp in range(16):
                sg = stg[hp % 2]
                kb.dma("sp" if hp % 2 == 0 else "act", sg[:], w_v[:, :, hp * 128:(hp + 1) * 128], writes=[sg])
                w = wb[hp % 2]
                kb.op("pool", lambda e: e.tensor_copy(out=w[:], in_=sg[:]), reads=[sg], writes=[w])
                qT = qTs[hp % 2]
                for tb in range(4):
                    ps = pss[tb % 2][0]
                    for dc in range(16):
                        kb.op("pe", lambda e: e.matmul(ps[:], lhsT=w[:, dc, :], rhs=h2T[:, dc, tb * 512:(tb + 1) * 512],
                                                       start=(dc == 0), stop=(dc == 15)), reads=[w, h2T], writes=[ps])
                    kb.op("act", lambda e: e.copy(out=qT[:, tb * 512:(tb + 1) * 512], in_=ps[:]), reads=[ps], writes=[qT])
                se = sev[hp % 2]
                for g in range(4):
                    ps = pss[g % 2][1 + (g // 2) % 2]
                    for j in range(4):
                        tt = g * 4 + j
                        kb.op("pe", lambda e: e.matmul(ps[:, j * 128:(j + 1) * 128], lhsT=qT[:, tt * 128:(tt + 1) * 128], rhs=skb[:, hp, :],
                                                       start=True, stop=True), reads=[qT, skb], writes=[ps])
                    kb.op("dve", lambda e: e.tensor_copy(out=se[:, g * 4:(g + 1) * 4, :], in_=ps[:].rearrange("p (j n) -> p j n", j=4)),
                          reads=[ps], writes=[se])
                kb.dma("pool", D.S_d.ap()[:, hp, :].rearrange("(tt p) n -> p tt n", p=128), se[:], reads=[se], writes=[D.S_d])
            kb.barrier()
        with contextlib.ExitStack() as st, kb.nc.named_scope("P2"):
            Ss = [kb.sb(f"p_S{i}", [128, 16, 128], F32, st) for i in range(2)]
            S2 = kb.sb("p_S2", [128, 128], F32, st)
            M16 = kb.sb("p_M16", [128, 16, 16], F32, st)
            I16u = kb.sb("p_I16u", [128, 16, 16], U32, st)
            I16f = kb.sb("p_I16f", [128, 16, 16], F32, st)
            cand = kb.sb("p_cand", [128, 8, 16, 16], F32, st)
            cand2 = kb.sb("p_cand2", [128, 256], F32, st)
            C16 = kb.sb("p_C16", [128, 8, 16], F32, st)
            CIu = kb.sb("p_CIu", [128, 8, 16], U32, st)
            IJu = kb.sb("p_IJu", [128, 2, 8, 16], U32, st)
            IJf = kb.sb("p_IJf", [128, 2, 8, 16], F32, st)
            ex = kb.sb("p_ex", [128, 8, 16], F32, st)
            Z = kb.sb("p_Z", [128, 8], F32, st)
            EG = kb.sb("p_EG", [128, 3, 8, 16], F32, st)
            eq = kb.sb("p_eq", [128, 8, 16, 16], F32, st)
            pt = kb.ps("p_pt", [128, 512], F32, st)
            for tt in range(NT):
                S = Ss[tt % 2]
                kb.dma("sp", S[:].rearrange("p a n -> p (a n)"), D.S_d.ap()[tt * 128:(tt + 1) * 128].rearrange("p a n -> p (a n)"),
                       reads=[D.S_d], writes=[S])
                for hp in range(16):
                    kb.op("dve", lambda e: e.max(out=M16[:, hp, 0:8], in_=S[:, hp, :]), reads=[S], writes=[M16])
                    kb.op("dve", lambda e: e.max_index(out=I16u[:, hp, 0:8], in_max=M16[:, hp, 0:8], in_values=S[:, hp, :]),
                          reads=[S, M16], writes=[I16u])
                    kb.op("dve", lambda e: e.match_replace(out=S2[:], in_to_replace=M16[:, hp, 0:8], in_values=S[:, hp, :], imm_value=-1e30),
                          reads=[S, M16], writes=[S2])
                    kb.op("dve", lambda e: e.max(out=M16[:, hp, 8:16], in_=S2[:]), reads=[S2], writes=[M16])
                    kb.op("dve", lambda e: e.max_index(out=I16u[:, hp, 8:16], in_max=M16[:, hp, 8:16], in_values=S2[:]),
                          reads=[S2, M16], writes=[I16u])
                kb.op("pool", lambda e: e.tensor_copy(out=I16f[:], in_=I16u[:]), reads=[I16u], writes=[I16f])
                M4 = M16[:].rearrange("p (h q) k -> p h q k", q=2)
                I4 = I16f[:].rearrange("p (h q) k -> p h q k", q=2)
                kb.op("pool", lambda e: e.tensor_tensor(out=cand[:], in0=M4[:, :, 0, :, None].to_broadcast([128, 8, 16, 16]),
                                                        in1=M4[:, :, 1, None, :].to_broadcast([128, 8, 16, 16]), op=ALU.add),
                      reads=[M16], writes=[cand])
                for h in range(8):
                    ch = cand[:, h, :, :].rearrange("p a b -> p (a b)")
                    kb.op("dve", lambda e: e.max(out=C16[:, h, 0:8], in_=ch), reads=[cand], writes=[C16])
                    kb.op("dve", lambda e: e.max_index(out=CIu[:, h, 0:8], in_max=C16[:, h, 0:8], in_values=ch), reads=[cand, C16], writes=[CIu])
                    kb.op("dve", lambda e: e.match_replace(out=cand2[:], in_to_replace=C16[:, h, 0:8], in_values=ch, imm_value=-1e30),
                          reads=[cand, C16], writes=[cand2])
                    kb.op("dve", lambda e: e.max(out=C16[:, h, 8:16], in_=cand2[:]), reads=[cand2], writes=[C16])
                    kb.op("dve", lambda e: e.max_index(out=CIu[:, h, 8:16], in_max=C16[:, h, 8:16], in_values=cand2[:]),
                          reads=[cand2, C16], writes=[CIu])
                kb.op("pool", lambda e: e.tensor_tensor(out=ex[:], in0=C16[:], in1=C16[:, :, 0:1].to_broadcast([128, 8, 16]), op=ALU.subtract),
                      reads=[C16], writes=[ex])
                kb.op("act", lambda e: e.activation(out=ex[:], in_=ex[:], func=AF.Exp), reads=[ex], writes=[ex])
                kb.op("dve", lambda e: e.tensor_reduce(out=Z[:], in_=ex[:], axis=AX.X, op=ALU.add), reads=[ex], writes=[Z])
                kb.op("dve", lambda e: e.reciprocal(out=Z[:], in_=Z[:]), reads=[Z], writes=[Z])
                kb.op("dve", lambda e: e.tensor_tensor(out=EG[:, 2], in0=ex[:], in1=Z[:, :, None].to_broadcast([128, 8, 16]), op=ALU.mult),
                      reads=[ex, Z], writes=[EG])
                kb.op("dve", lambda e: e.tensor_single_scalar(out=IJu[:, 0], in_=CIu[:], scalar=4, op=ALU.logical_shift_right),
                      reads=[CIu], writes=[IJu])
                kb.op("dve", lambda e: e.tensor_single_scalar(out=IJu[:, 1], in_=CIu[:], scalar=15, op=ALU.bitwise_and),
                      reads=[CIu], writes=[IJu])
                kb.op("pool", lambda e: e.tensor_copy(out=IJf[:], in_=IJu[:]), reads=[IJu], writes=[IJf])
                for q in range(2):
                    kb.op("dve", lambda e: e.tensor_tensor(out=eq[:], in0=iota[:, None, None, 0:16].to_broadcast([128, 8, 16, 16]),
                                                            in1=IJf[:, q, :, :, None].to_broadcast([128, 8, 16, 16]), op=ALU.is_equal),
                          reads=[iota, IJf], writes=[eq])
                    kb.op("pool", lambda e: e.tensor_tensor(out=eq[:], in0=eq[:], in1=I4[:, :, q, None, :].to_broadcast([128, 8, 16, 16]),
                                                            op=ALU.mult), reads=[eq, I16f], writes=[eq])
                    kb.op("dve", lambda e: e.tensor_reduce(out=EG[:, q], in_=eq[:], axis=AX.X, op=ALU.add), reads=[eq], writes=[EG])
                for a in range(3):
                    kb.op("pe", lambda e: e.transpose(out=pt[:, a * 128:(a + 1) * 128], in_=EG[:, a].rearrange("p h k -> p (h k)"),
                                                      identity=C.ident[:]), reads=[EG, C.ident], writes=[pt])
                kb.op("act", lambda e: e.copy(out=ET[:, :, tt * 128:(tt + 1) * 128], in_=pt[:, 0:384].rearrange("p (a t) -> p a t", a=3)),
                      reads=[pt], writes=[ET])
            kb.barrier()
        with contextlib.ExitStack() as st, kb.nc.named_scope("P3"):
            Gs = kb.sb("p_Gs", [128, 128, 256], BF16, st)
            O1 = [kb.sb(f"p_O1{i}", [128, 32, 128], BF16, st) for i in range(2)]
            O2 = [kb.sb(f"p_O2{i}", [128, 32, 128], BF16, st) for i in range(2)]
            pg = [kb.ps(f"p_pg{i}", [128, 512], F32, st) for i in range(4)]
            it = 0
            for tg in range(8):
                for sub in range(8):
                    t0 = tg * 256 + sub * 32
                    o1 = O1[sub % 2]
                    o2 = O2[sub % 2]
                    iob = iota[:, None, :].to_broadcast([128, 32, 128])
                    kb.op("dve", lambda e: e.tensor_tensor(out=o1[:], in0=iob, in1=ET[:, 0, t0:t0 + 32, None].to_broadcast([128, 32, 128]),
                                                            op=ALU.is_equal), reads=[iota, ET], writes=[o1])
                    kb.op("dve", lambda e: e.tensor_tensor(out=o2[:], in0=iob, in1=ET[:, 1, t0:t0 + 32, None].to_broadcast([128, 32, 128]),
                                                           op=ALU.is_equal), reads=[iota, ET], writes=[o2])
                    kb.op("pool", lambda e: e.tensor_tensor(out=o2[:], in0=o2[:], in1=ET[:, 2, t0:t0 + 32, None].to_broadcast([128, 32, 128]),
                                                            op=ALU.mult), reads=[o2, ET], writes=[o2])
                    for q4 in range(8):
                        p = pg[it % 4]
                        it += 1
                        for j in range(4):
                            tl = q4 * 4 + j
                            kb.op("pe", lambda e: e.matmul(p[:].rearrange("p (e t) -> p e t", t=4)[:, :, j], lhsT=o2[:, tl, :], rhs=o1[:, tl, :],
                                                           start=True, stop=True), reads=[o1, o2], writes=[p])
                        tl0 = sub * 32 + q4 * 4
                        dst = Gs[:, :, tl0:tl0 + 4]
                        src = p[:].rearrange("p (e t) -> p e t", t=4)
                        if it % 2 == 0:
                            kb.op("act", lambda e: e.copy(out=dst, in_=src), reads=[p], writes=[Gs])
                        else:
                            kb.op("dve", lambda e: e.tensor_copy(out=dst, in_=src), reads=[p], writes=[Gs])
                for k8 in range(8):
                    kb.dma(["sp", "act", "pool"][k8 % 3],
                           D.G_d.ap()[k8 * 16:(k8 + 1) * 16, :, tg * 256:(tg + 1) * 256].rearrange("e1 e2 t -> e2 e1 t"),
                           Gs[:, k8 * 16:(k8 + 1) * 16, :], reads=[Gs], writes=[D.G_d])
            kb.barrier()
        with contextlib.ExitStack() as st, kb.nc.named_scope("P4"):
            h2T = kb.sb("p_h2Tb", [128, 16, 2048], BF16, st)
            for dc in range(16):
                kb.dma("sp" if dc % 2 == 0 else "act", h2T[:, dc, :], D.h2T_d.ap()[dc], reads=[D.h2T_d], writes=[h2T])
            stg = [kb.sb(f"p4_stg{i}", [128, 16, 128], F32, st) for i in range(2)]
            ub = [kb.sb(f"p4_ub{i}", [128, 16, 128], BF16, st) for i in range(2)]
            Gc = [kb.sb(f"p4_Gc{i}", [128, 2048], BF16, st) for i in range(2)]
            ge = [kb.sb(f"p4_ge{i}", [128, 2048], BF16, st) for i in range(2)]
            pss = [[kb.ps(f"p4_ps{i}_{j}", [128, 512], F32, st) for j in range(4)] for i in range(2)]
            def load(e1):
                sg = stg[e1 % 2]
                kb.dma("sp", sg[:].rearrange("p a b -> p (a b)"), D.u_tabT.ap()[e1], writes=[sg])
                g = Gc[e1 % 2]
                kb.dma("pool", g[:], D.G_d.ap()[e1], reads=[D.G_d], writes=[g])
            load(0)
            for e1 in range(128):
                if e1 + 1 < 128:
                    load(e1 + 1)
                sg = stg[e1 % 2]
                u = ub[e1 % 2]
                kb.op("dve", lambda e: e.tensor_copy(out=u[:], in_=sg[:]), reads=[sg], writes=[u])
                g = Gc[e1 % 2]
                ps = pss[e1 % 2]
                a = ge[e1 % 2]
                for tb in range(4):
                    for dc in range(16):
                        kb.op("pe", lambda e: e.matmul(ps[tb][:], lhsT=u[:, dc, :], rhs=h2T[:, dc, tb * 512:(tb + 1) * 512],
                                                       start=(dc == 0), stop=(dc == 15)), reads=[u, h2T], writes=[ps[tb]])
                    kb.op("act", lambda e: e.activation(out=a[:, tb * 512:(tb + 1) * 512], in_=ps[tb][:], func=AF.Gelu), reads=[ps[tb]], writes=[a])
                kb.op("pool", lambda e: e.tensor_tensor(out=a[:], in0=a[:], in1=g[:], op=ALU.mult), reads=[a, g], writes=[a])
                kb.dma("sp", D.A_d.ap()[e1], a[:], reads=[a], writes=[D.A_d])
            kb.barrier()
        with contextlib.ExitStack() as st, kb.nc.named_scope("P5"):
            NB = 3
            vst = [kb.sb(f"p5_vst{i}", [128, 512], F32, st) for i in range(NB)]
            vb = [kb.sb(f"p5_vb{i}", [128, 512], BF16, st) for i in range(NB)]
            ac = [kb.sb(f"p5_ac{i}", [128, 1024], BF16, st) for i in range(NB)]
            ov = [kb.sb(f"p5_ov{i}", [128, 512], F32, st) for i in range(2)]
            pss = [[kb.ps(f"p5_ps{j}_{h}", [128, 512], F32, st) for h in range(2)] for j in range(4)]
            seq = [(tb2, dg, e1) for tb2 in range(2) for dg in range(4) for e1 in range(128)]

            def load(i):
                tb2, dg, e1 = seq[i]
                k = i % NB
                kb.dma("sp", vst[k][:], D.v_tab.ap()[e1 * 128:(e1 + 1) * 128, dg * 512:(dg + 1) * 512], writes=[vst[k]])
                kb.dma("pool", ac[k][:], D.A_d.ap()[e1][:, tb2 * 1024:(tb2 + 1) * 1024], reads=[D.A_d], writes=[ac[k]])
            load(0)
            load(1)
            oi = 0
            for i, (tb2, dg, e1) in enumerate(seq):
                if i + 2 < len(seq):
                    load(i + 2)
                k = i % NB
                if i % 2 == 0:
                    kb.op("dve", lambda e: e.tensor_copy(out=vb[k][:], in_=vst[k][:]), reads=[vst[k]], writes=[vb[k]])
                else:
                    kb.op("act", lambda e: e.copy(out=vb[k][:], in_=vst[k][:]), reads=[vst[k]], writes=[vb[k]])
                for j in range(4):
                    for h in range(2):
                        kb.op("pe", lambda e: e.matmul(pss[j][h][:], lhsT=vb[k][:, j * 128:(j + 1) * 128], rhs=ac[k][:, h * 512:(h + 1) * 512],
                                                       start=(e1 == 0), stop=(e1 == 127)), reads=[vb[k], ac[k]], writes=[pss[j][h]])
                if e1 == 127:
                    for j in range(4):
                        for h in range(2):
                            o = ov[oi % 2]
                            if oi % 2 == 0:
                                kb.op("act", lambda e: e.copy(out=o[:], in_=pss[j][h][:]), reads=[pss[j][h]], writes=[o])
                            else:
                                kb.op("dve", lambda e: e.tensor_copy(out=o[:], in_=pss[j][h][:]), reads=[pss[j][h]], writes=[o])
                            oi += 1
                            d0 = dg * 512 + j * 128
                            t0 = tb2 * 1024 + h * 512
                            kb.dma("sp", D.peT_d.ap()[d0:d0 + 128, t0:t0 + 512], o[:], reads=[o], writes=[D.peT_d])
            kb.barrier()


def phase_F(kb, C, D):
    with contextlib.ExitStack() as st:
        wb = kb.sb("f_wb", [128, 2048], F32, st)
        kb.dma("pool", wb[:], D.normf_w.ap().partition_broadcast(128), writes=[wb])
        xs = [kb.sb(f"f_x{i}", [128, 2048], F32, st) for i in range(2)]
        pes = [kb.sb(f"f_pe{i}", [128, 16, 128], F32, st) for i in range(2)]
        junk = kb.sb("f_junk", [128, 2048], BF16, st)
        ss = kb.sb("f_ss", [128, NT], F32, st)
        rs = kb.sb("f_rs", [128, NT], F32, st)
        kb.op("pool", lambda e: e.memset(ss[:], 0.0), writes=[ss])
        pss = [[kb.ps(f"f_ps{i}_{j}", [128, 512], F32, st) for j in range(4)] for i in range(2)]
        pe_v = D.peT_d.ap().rearrange("(dc p) t -> p dc t", p=128)
        for tt in range(NT):
            xt = xs[tt % 2]
            pe = pes[tt % 2]
            ps = pss[tt % 2]
            kb.dma("sp", xt[:], D.x1_d.ap()[tt * 128:(tt + 1) * 128, :], reads=[D.x1_d], writes=[xt])
            kb.dma("act", pe[:], pe_v[:, :, tt * 128:(tt + 1) * 128], reads=[D.peT_d], writes=[pe])
            for dc in range(16):
                kb.op("pe", lambda e: e.transpose(out=ps[dc // 4][:, (dc % 4) * 128:(dc % 4 + 1) * 128], in_=pe[:, dc, :], identity=C.ident[:]),
                      reads=[pe, C.ident], writes=[ps[dc // 4]])
            for j in range(4):
                kb.op("dve", lambda e: e.tensor_tensor(out=xt[:, j * 512:(j + 1) * 512], in0=ps[j][:], in1=xt[:, j * 512:(j + 1) * 512], op=ALU.add),
                      reads=[ps[j], xt], writes=[xt])
            kb.op("act", lambda e: e.activation(out=junk[:], in_=xt[:], func=AF.Square, accum_out=ss[:, tt:tt + 1]), reads=[xt], writes=[junk, ss])
            kb.op("dve", lambda e: e.tensor_scalar(out=rs[:, tt:tt + 1], in0=ss[:, tt:tt + 1], scalar1=1.0 / 2048, scalar2=EPS,
                                                   op0=ALU.mult, op1=ALU.add), reads=[ss], writes=[rs])
            kb.op("act", lambda e: e.activation(out=rs[:, tt:tt + 1], in_=rs[:, tt:tt + 1], func=AF.Sqrt), reads=[rs], writes=[rs])
            kb.op("dve", lambda e: e.reciprocal(out=rs[:, tt:tt + 1], in_=rs[:, tt:tt + 1]), reads=[rs], writes=[rs])
            kb.op("dve", lambda e: e.scalar_tensor_tensor(out=xt[:], in0=xt[:], scalar=rs[:, tt:tt + 1], in1=wb[:], op0=ALU.mult, op1=ALU.mult),
                  reads=[xt, rs, wb], writes=[xt])
            kb.dma("sp", D.out.ap()[tt * 128:(tt + 1) * 128, :], xt[:], reads=[xt], writes=[D.out])
        kb.barrier()


STOP = 0
FAST_F32 = True
F32R = mybir.dt.float32r
def fr(ap):
    return ap.bitcast(F32R) if FAST_F32 else ap
class StopBuild(Exception):
    pass
def chk(n):
    if STOP == n:
        raise StopBuild()
GN_EPS = 64e-5
NEG_E05 = -0.6065306597126334


def phase_Rpre(kb, C, D):
    with contextlib.ExitStack() as st:
        rw = kb.sb("rp_rw", [128, 8, 8], F32, st)
        kb.dma("sp", rw[:], D.rw_c.ap(), writes=[rw])
        tmp = kb.sb("rp_tmp", [128, 2048], F32, st)
        lin = [kb.sb(f"rp_lin{i}", [128, 2048], BF16, st) for i in range(4)]
        for i in range(4):
            kb.op("pool", lambda e: e.memset(lin[i][:], 0.0), writes=[lin[i]])
            r0 = 3072 + i * 96
            kb.dma("sp", tmp[0:96, :], D.zs_d.ap()[r0:r0 + 96, :], reads=[D.zs_d], writes=[tmp])
            if i < 2:
                kb.op("act", lambda e: e.activation(out=lin[i][0:96, :], in_=tmp[0:96, :], func=AF.Tanh), reads=[tmp], writes=[lin[i]])
            else:
                kb.op("act", lambda e: e.copy(out=lin[i][0:96, :], in_=tmp[0:96, :]), reads=[tmp], writes=[lin[i]])
        sgl = kb.sb("rp_sgl", [128, 2, 2048], BF16, st)
        for kc in range(2):
            kb.dma("sp", tmp[:], D.zs_d.ap()[3456 + kc * 128:3456 + (kc + 1) * 128, :], reads=[D.zs_d], writes=[tmp])
            kb.op("act", lambda e: e.activation(out=sgl[:, kc, :], in_=tmp[:], func=AF.Sigmoid), reads=[tmp], writes=[sgl])
        wst = kb.sb("rp_wst", [128, 2048], F32, st)
        w2b = kb.sb("rp_w2b", [128, 2, 1024], BF16, st)
        a2b = kb.sb("rp_a2b", [128, 2, 1024], BF16, st)
        g2b = kb.sb("rp_g2b", [128, 2, 1024], BF16, st)
        wv = wst[:].rearrange("p (a c) -> p a c", a=2)
        kb.op("pool", lambda e: e.memset(w2b[:], 0.0), writes=[w2b])
        kb.op("pool", lambda e: e.memset(a2b[:], 0.0), writes=[a2b])
        kb.dma("sp", wv[0:96], D.w2.ap().rearrange("d l c -> l d c"), writes=[wst])
        kb.op("pool", lambda e: e.tensor_copy(out=w2b[0:96], in_=wv[0:96]), reads=[wst], writes=[w2b])
        kb.dma("sp", wv[0:96], D.a2.ap().rearrange("d l c -> l d c"), writes=[wst])
        kb.op("pool", lambda e: e.tensor_copy(out=a2b[0:96], in_=wv[0:96]), reads=[wst], writes=[a2b])
        kb.dma("sp", wv, D.g2.ap().rearrange("(kc p) c -> p kc c", p=128), writes=[wst])
        kb.op("pool", lambda e: e.tensor_copy(out=g2b[:], in_=wv), reads=[wst], writes=[g2b])
        outs = [kb.sb(f"rp_o{i}", [128, 2048], F32, st) for i in range(2)]
        pss = [[kb.ps(f"rp_ps{i}_{j}", [128, 512], F32, st) for j in range(4)] for i in range(2)]
        it = 0
        for cc in range(8):
            for d in range(2):
                for which in range(2):
                    ps = pss[it % 2]
                    o = outs[it % 2]
                    it += 1
                    wmat = w2b if which == 0 else a2b
                    xin = lin[d] if which == 0 else lin[2 + d]
                    bias = rw[:, cc, d:d + 1] if which == 0 else rw[:, cc, 2 + d:3 + d]
                    for tb in range(4):
                        kb.op("pe", lambda e: e.matmul(ps[tb][:], lhsT=wmat[:, d, cc * 128:(cc + 1) * 128], rhs=xin[:, tb * 512:(tb + 1) * 512],
                                                       start=True, stop=True), reads=[wmat, xin], writes=[ps[tb]])
                        kb.op("act", lambda e: e.activation(out=o[:, tb * 512:(tb + 1) * 512], in_=ps[tb][:], func=AF.Sigmoid, bias=bias),
                              reads=[ps[tb], rw], writes=[o])
                    if which == 0:
                        kb.op("dve", lambda e: e.tensor_scalar(out=o[:], in0=o[:], scalar1=NEG_E05, scalar2=None, op0=ALU.mult), reads=[o], writes=[o])
                        kb.dma("sp", D.ld_d.ap()[d, cc * 128:(cc + 1) * 128, :], o[:], reads=[o], writes=[D.ld_d])
                    else:
                        kb.dma("sp", D.a_d.ap()[d, cc * 128:(cc + 1) * 128, :], o[:], reads=[o], writes=[D.a_d])
        for tt in range(NT):
            ps = pss[tt % 2]
            o = outs[tt % 2]
            for hf in range(2):
                for kc in range(2):
                    kb.op("pe", lambda e: e.matmul(ps[hf][:], lhsT=sgl[:, kc, tt * 128:(tt + 1) * 128], rhs=g2b[:, kc, hf * 512:(hf + 1) * 512],
                                                   start=(kc == 0), stop=(kc == 1)), reads=[sgl, g2b], writes=[ps[hf]])
                kb.op("act", lambda e: e.copy(out=o[:, hf * 512:(hf + 1) * 512], in_=ps[hf][:]), reads=[ps[hf]], writes=[o])
            kb.dma("sp", D.g_d.ap()[tt * 128:(tt + 1) * 128, :], o[:, 0:1024], reads=[o], writes=[D.g_d])
        kb.barrier()


def phase_R(kb, C, D, ccs=range(8), dbg=None):
    with contextlib.ExitStack() as st:
        rw = kb.sb("r_rw", [128, 8, 8], F32, st)
        kb.dma("sp", rw[:], D.rw_c.ap(), writes=[rw])
        masks = kb.sb("r_masks", [128, 6, 128], F32, st)
        kb.dma("sp", masks[:], D.masks.ap(), writes=[masks])
        cst = kb.sb("r_cst", [128, 66], F32, st)
        kb.dma("sp", cst[:], D.rcst.ap()[:, 0:66], writes=[cst])
        segt = kb.sb("r_segm", [128, 2048], BF16, st)
        kb.dma("sp", segt[:], D.segm.ap(), writes=[segt])
        ident2 = cst[:, 0:64]
        sel = cst[:, 64:66]
        lnw = kb.sb("r_lnw", [128, 128], F32, st)
        lnb = kb.sb("r_lnb", [128, 128], F32, st)
        Rr = kb.sb("r_R", [128, 2048], F32, st)
        Kk = kb.sb("r_K", [128, 2048], F32, st)
        Vv = kb.sb("r_V", [128, 2048], F32, st)
        KKn = kb.sb("r_KK", [128, 2048], F32, st)
        E1 = kb.sb("r_E1", [128, 2048], F32, st)
        XI = kb.sb("r_XI", [128, 2048], F32, st)
        XE = kb.sb("r_XE", [128, 2048], F32, st)
        Aa = kb.sb("r_A", [128, 2048], F32, st)
        KD = kb.sb("r_KD", [128, 2048], F32, st)
        KT = kb.sb("r_KT", [128, 2048], F32, st)
        AT = kb.sb("r_AT", [128, 2048], F32, st)
        BTb = kb.sb("r_BTb", [128, 2048], F32, st)
        stt = kb.sb("r_stt", [128, 160], F32, st)
        Vtm = kb.sb("r_Vtm", [128, NT, 128], F32, st)
        Ysum = kb.sb("r_Ysum", [128, NT, 128], F32, st)
        MTa = kb.sb("r_MTa", [128, 32, 64], F32, st)
        Ca = kb.sb("r_Ca", [128, 32, 64], F32, st)
        H = [kb.sb(f"r_H{i}", [128, 64], F32, st) for i in range(2)]
        tot = kb.sb("r_tot", [128, 32], F32, st)
        GL = kb.sb("r_GL", [128, 32], F32, st)
        RhT = Vv

        class Set:
            pass
        sets = []
        for p in range(2):
            S = Set()
            S.XA = kb.sb(f"r_XA{p}", [128, 2, 2, 128], F32, st)
            S.KBm = kb.sb(f"r_KBm{p}", [128, 2, 2, 128], F32, st)
            S.PQ = [kb.sb(f"r_PQ{p}_{i}", [128, 2, 2, 128], F32, st) for i in range(2)]
            S.QT = [kb.sb(f"r_QT{p}_{i}", [128, 2, 128], F32, st) for i in range(2)]
            S.BW = kb.sb(f"r_BW{p}", [128, 2, 128], F32, st)
            S.BU = kb.sb(f"r_BU{p}", [128, 2, 128], F32, st)
            S.AGtm = kb.sb(f"r_AGtm{p}", [128, 128], F32, st)
            S.KGtm = kb.sb(f"r_KGtm{p}", [128, 128], F32, st)
            S.B = [kb.ps(f"r_bk{p}_{i}", [128, 512], F32, st) for i in range(4)]
            sets.append(S)
        psX = sets[0].B[0]
        bkH = [sets[0].B[1], sets[0].B[2]]

        v4 = lambda b: b[:].rearrange("p (h q s) -> p h q s", h=2, q=2)
        v3 = lambda b, lo: b[:, lo:lo + 256].rearrange("p (h s) -> p h s", h=2)

        def tile_gen(S, d, tt, RT, BT):
            M2 = masks[:, 2 * d:2 * d + 2, :]
            MST = masks[:, 2 - 2 * d, :]
            XA, KBm, PQ, QT, BW, BU, AGtm, KGtm = S.XA, S.KBm, S.PQ, S.QT, S.BW, S.BU, S.AGtm, S.KGtm
            B0, B1, B2, B3 = S.B
            psA, psB, psN = v4(B0), v4(B1), v4(B3)
            psC = v3(B2, 0)
            cols = slice(tt * 128, (tt + 1) * 128)
            for hh in range(2):
                pr = slice(hh * 64, hh * 64 + 64)
                kb.pe_fence()
                kb.op("pe", lambda e: e.matmul(psA[:, hh, 0, :], lhsT=fr(AT[pr, cols]), rhs=fr(BT[pr, cols]), start=True, stop=True),
                      reads=[AT, BT], writes=[B0])
                kb.op("pe", lambda e: e.matmul(psA[:, hh, 1, :], lhsT=fr(AT[pr, cols]), rhs=fr(RT[pr, cols]), start=True, stop=True),
                      reads=[AT, RT], writes=[B0])
                kb.op("pe", lambda e: e.matmul(psB[:, hh, 0, :], lhsT=fr(KT[pr, cols]), rhs=fr(BT[pr, cols]), start=True, stop=True),
                      reads=[KT, BT], writes=[B1])
                kb.op("pe", lambda e: e.matmul(psB[:, hh, 1, :], lhsT=fr(KT[pr, cols]), rhs=fr(RT[pr, cols]), start=True, stop=True),
                      reads=[KT, RT], writes=[B1])
                kb.op("pe", lambda e: e.matmul(psC[:, hh, :], lhsT=fr(BT[pr, cols]), rhs=fr(AT[pr, cols]), start=True, stop=True),
                      reads=[AT, BT], writes=[B2])
            kb.pe_fence()
            yield
            M2b = M2[:, None, :, :].to_broadcast([128, 2, 2, 128])
            q0, q1 = QT[0], QT[1]
            kb.op("dve", lambda e: e.tensor_tensor(out=fr(XA[:]), in0=psA, in1=M2b, op=ALU.mult), reads=[B0, masks], writes=[XA])
            kb.op("dve", lambda e: e.tensor_tensor(out=fr(q0[:]), in0=psC, in1=MST[:, None, :].to_broadcast([128, 2, 128]), op=ALU.mult),
                  reads=[B2, masks], writes=[q0])
            kb.op("dve", lambda e: e.tensor_tensor(out=fr(KBm[:]), in0=psB, in1=M2b, op=ALU.mult), reads=[B1, masks], writes=[KBm])
            pq = PQ[0]
            kb.op("pool", lambda e: e.tensor_tensor(out=fr(pq[:, :, 0, :]), in0=XA[:, :, 0, :], in1=C.ident[:, None, :].to_broadcast([128, 2, 128]),
                                                    op=ALU.add), reads=[XA, C.ident], writes=[pq])
            yield
            for hh in range(2):
                kb.op("pe", lambda e: e.matmul(psN[:, hh, 1, :], lhsT=fr(q0[:, hh, :]), rhs=fr(XA[:, hh, 0, :]), start=True, stop=True),
                      reads=[q0, XA], writes=[B3])
                kb.op("pe", lambda e: e.matmul(psC[:, hh, :], lhsT=fr(XA[:, hh, 0, :]), rhs=fr(q0[:, hh, :]), start=True, stop=True),
                      reads=[q0, XA], writes=[B2])
            yield
            kb.op("act", lambda e: e.copy(out=fr(pq[:, :, 1, :]), in_=psN[:, :, 1, :]), reads=[B3], writes=[pq])
            kb.op("dve", lambda e: e.tensor_copy(out=fr(q1[:]), in_=psC), reads=[B2], writes=[q1])
            yield
            cur = 0
            qcur = 1
            for lev in range(1, 6):
                pq = PQ[cur]
                pqn = PQ[1 - cur]
                qt = QT[qcur]
                qtn = QT[1 - qcur]
                last = (lev == 5)
                for hh in range(2):
                    if last:
                        kb.op("pe", lambda e: e.matmul(psN[:, hh, 0, :], lhsT=fr(qt[:, hh, :]), rhs=fr(pq[:, hh, 0, :]), start=True, stop=True),
                              reads=[qt, pq], writes=[B3])
                    else:
                        kb.op("pe", lambda e: e.matmul(psN[:, hh, :, :], lhsT=fr(qt[:, hh, :]), rhs=fr(pq[:, hh, :, :]), start=True, stop=True),
                              reads=[qt, pq], writes=[B3])
                        kb.op("pe", lambda e: e.matmul(psC[:, hh, :], lhsT=fr(pq[:, hh, 1, :]), rhs=fr(qt[:, hh, :]), start=True, stop=True),
                              reads=[qt, pq], writes=[B2])
                yield
                kb.op("dve", lambda e: e.tensor_tensor(out=fr(pqn[:, :, 0, :]), in0=psN[:, :, 0, :], in1=pq[:, :, 0, :], op=ALU.add),
                      reads=[B3, pq], writes=[pqn])
                if not last:
                    kb.op("act", lambda e: e.copy(out=fr(pqn[:, :, 1, :]), in_=psN[:, :, 1, :]), reads=[B3], writes=[pqn])
                    kb.op("dve", lambda e: e.tensor_copy(out=fr(qtn[:]), in_=psC), reads=[B2], writes=[qtn])
                yield
                cur = 1 - cur
                qcur = 1 - qcur
            TT = PQ[cur]
            W_ = B0[:, 0:128].rearrange("p (h i) -> p h i", h=2)
            Bt_ = B0[:, 128:256]
            U_ = B0[:, 256:512].rearrange("p (h i) -> p h i", h=2)
            for hh in range(2):
                kb.op("pe", lambda e: e.matmul(W_[:, hh, :], lhsT=fr(KBm[:, hh, 0, :]), rhs=fr(Vtm[:, tt, hh * 64:(hh + 1) * 64]), start=True, stop=True),
                      reads=[KBm, Vtm], writes=[B0])
            kb.op("pe", lambda e: e.transpose(out=Bt_, in_=BT[:, cols], identity=C.ident[:]), reads=[BT, C.ident], writes=[B0])
            AG_ = B1[:, 256:384]
            KG_ = B1[:, 384:512]
            kb.op("pe", lambda e: e.transpose(out=AG_, in_=Aa[:, cols], identity=C.ident[:]), reads=[Aa, C.ident], writes=[B1])
            kb.op("pe", lambda e: e.transpose(out=KG_, in_=KD[:, cols], identity=C.ident[:]), reads=[KD, C.ident], writes=[B1])
            yield
            kb.op("act", lambda e: e.copy(out=fr(BW[:, :, 64:128]), in_=W_), reads=[B0], writes=[BW])
            kb.op("dve", lambda e: e.tensor_copy(out=fr(BW[:, :, 0:64]), in_=Bt_.rearrange("p (h j) -> p h j", h=2)), reads=[B0], writes=[BW])
            kb.op("act", lambda e: e.copy(out=AGtm[:], in_=AG_), reads=[B1], writes=[AGtm])
            kb.op("act", lambda e: e.copy(out=KGtm[:], in_=KG_), reads=[B1], writes=[KGtm])
            yield
            for hh in range(2):
                kb.op("pe", lambda e: e.matmul(U_[:, hh, :], lhsT=fr(TT[:, hh, 0, :]), rhs=fr(BW[:, hh, :]), start=True, stop=True),
                      reads=[TT, BW], writes=[B0])
            yield
            kb.op("dve", lambda e: e.tensor_copy(out=fr(BU[:]), in_=U_), reads=[B0], writes=[BU])
            yield
            Y_ = B1[:, 0:128].rearrange("p (h i) -> p h i", h=2)
            R_ = B1[:, 128:256]
            for hh in range(2):
                kb.op("pe", lambda e: e.matmul(Y_[:, hh, :], lhsT=fr(XA[:, hh, 1, :]), rhs=fr(BU[:, hh, 64:128]), start=True, stop=False),
                      reads=[XA, BU], writes=[B1])
                kb.op("pe", lambda e: e.matmul(Y_[:, hh, :], lhsT=fr(KBm[:, hh, 1, :]), rhs=fr(Vtm[:, tt, hh * 64:(hh + 1) * 64]), start=False, stop=True),
                      reads=[KBm, Vtm], writes=[B1])
            for hh in range(2):
                kb.op("pe", lambda e: e.matmul(R_[hh * 64:(hh + 1) * 64, :], lhsT=BU[:, hh, 0:64], rhs=XA[:, hh, 1, :], start=True, stop=True),
                      reads=[BU, XA], writes=[B1])
            for n in range(2):
                tr = slice(n * 64, n * 64 + 64)
                bk = (B2, B3)[n]
                for hh in range(2):
                    pr = slice(hh * 64, hh * 64 + 64)
                    kb.op("pe", lambda e: e.matmul(bk[pr, 0:64], lhsT=BU[tr, hh, 0:64], rhs=AGtm[tr, pr], start=True, stop=True),
                          reads=[BU, AGtm], writes=[bk])
                    kb.op("pe", lambda e: e.matmul(bk[pr, 64:128], lhsT=AGtm[tr, pr], rhs=BU[tr, hh, 64:128], start=True, stop=False),
                          reads=[BU, AGtm], writes=[bk])
                    kb.op("pe", lambda e: e.matmul(bk[pr, 64:128], lhsT=KGtm[tr, pr], rhs=Vtm[tr, tt, pr], start=False, stop=True),
                          reads=[KGtm, Vtm], writes=[bk])
            kb.pe_fence()
            yield
            if d == 0:
                kb.op("act", lambda e: e.copy(out=Ysum[:, tt, :], in_=B1[:, 0:128]), reads=[B1], writes=[Ysum])
            else:
                kb.op("dve", lambda e: e.tensor_tensor(out=Ysum[:, tt, :], in0=B1[:, 0:128], in1=Ysum[:, tt, :], op=ALU.add),
                      reads=[B1, Ysum], writes=[Ysum])
            kb.op("dve", lambda e: e.tensor_tensor(out=RhT[:, cols], in0=R_, in1=RT[:, cols], op=ALU.add), reads=[B1, RT], writes=[RhT])
            for n in range(2):
                ch = tt * 2 + n
                bk = (B2, B3)[n]
                kb.op("dve", lambda e: e.scalar_tensor_tensor(out=MTa[:, ch, :], in0=ident2, scalar=GL[:, ch:ch + 1], in1=bk[:, 0:64],
                                                              op0=ALU.mult, op1=ALU.add), reads=[cst, GL, bk], writes=[MTa])
                kb.op("act", lambda e: e.copy(out=Ca[:, ch, :], in_=bk[:, 64:128]), reads=[bk], writes=[Ca])
            yield

        def run_tiles(d, RT, BT):
            pending = list(range(NT))
            active = []
            free_sets = [sets[0], sets[1]]
            S = free_sets.pop(0)
            g = tile_gen(S, d, pending.pop(0), RT, BT)
            active.append((g, S))
            for _ in range(8):
                next(g)
            while active or pending:
                if pending and free_sets:
                    S = free_sets.pop(0)
                    active.append((tile_gen(S, d, pending.pop(0), RT, BT), S))
                for item in list(active):
                    g, S = item
                    try:
                        next(g)
                    except StopIteration:
                        active.remove(item)
                        free_sets.append(S)

        for cc in ccs:
            rows = slice(cc * 128, (cc + 1) * 128)
            kb.dma("sp", Rr[:], D.zs_d.ap()[cc * 128:(cc + 1) * 128, :], reads=[D.zs_d], writes=[Rr])
            kb.dma("act", Kk[:], D.zs_d.ap()[1024 + cc * 128:1024 + (cc + 1) * 128, :], reads=[D.zs_d], writes=[Kk])
            kb.dma("sp", Vv[:], D.zs_d.ap()[2048 + cc * 128:2048 + (cc + 1) * 128, :], reads=[D.zs_d], writes=[Vv])
            kb.dma("pool", lnw[:], D.lnx_w.ap()[cc * 128:(cc + 1) * 128].partition_broadcast(128), writes=[lnw])
            kb.dma("pool", lnb[:], D.lnx_b.ap()[cc * 128:(cc + 1) * 128].partition_broadcast(128), writes=[lnb])
            for g in range(4):
                for j in range(4):
                    tt = g * 4 + j
                    kb.op("pe", lambda e: e.transpose(out=psX[:, j * 128:(j + 1) * 128], in_=Vv[:, tt * 128:(tt + 1) * 128], identity=C.ident[:]),
                          reads=[Vv, C.ident], writes=[psX])
                kb.op("act", lambda e: e.copy(out=fr(Vtm[:, g * 4:(g + 1) * 4, :]), in_=psX[:].rearrange("p (j c) -> p j c", j=4)), reads=[psX], writes=[Vtm])
            kb.op("act", lambda e: e.activation(out=KKn[:], in_=Kk[:], func=AF.Copy, scale=rw[:, cc, 4:5]),
                  reads=[Kk, rw], writes=[KKn])
            kb.op("act", lambda e: e.activation(out=XI[:], in_=KKn[:], func=AF.Square), reads=[KKn], writes=[XI])
            for tb in range(4):
                kb.op("pe", lambda e: e.matmul(psX[:], lhsT=masks[:, 5, :], rhs=XI[:, tb * 512:(tb + 1) * 512], start=True, stop=True),
                      reads=[masks, XI], writes=[psX])
                kb.op("act", lambda e: e.activation(out=XE[:, tb * 512:(tb + 1) * 512], in_=psX[:], func=AF.Sqrt), reads=[psX], writes=[XE])
            kb.op("dve", lambda e: e.tensor_scalar(out=XE[:], in0=XE[:], scalar1=1e-12, scalar2=None, op0=ALU.max), reads=[XE], writes=[XE])
            kb.op("dve", lambda e: e.reciprocal(out=XE[:], in_=XE[:]), reads=[XE], writes=[XE])
            kb.op("pool", lambda e: e.tensor_tensor(out=KKn[:], in0=KKn[:], in1=XE[:], op=ALU.mult), reads=[KKn, XE], writes=[KKn])
            chk(1)
            def prep_gen(d):
                yield
                kb.dma("sp", XE[:], D.ld_d.ap()[d, cc * 128:(cc + 1) * 128, :], reads=[D.ld_d], writes=[XE])
                yield
                kb.dma("act", Aa[:], D.a_d.ap()[d, cc * 128:(cc + 1) * 128, :], reads=[D.a_d], writes=[Aa])
                yield
                kb.op("dve", lambda e: e.tensor_scalar(out=KD[:], in0=Aa[:], scalar1=-1.0, scalar2=rw[:, cc, 5:6], op0=ALU.add, op1=ALU.mult),
                      reads=[Aa, rw], writes=[KD])
                yield
                kb.op("dve", lambda e: e.scalar_tensor_tensor(out=KD[:], in0=KD[:], scalar=1.0, in1=Kk[:], op0=ALU.add, op1=ALU.mult),
                      reads=[KD, Kk], writes=[KD])
                yield
                kb.op("dve", lambda e: e.scalar_tensor_tensor(out=XI[:], in0=KD[:], scalar=rw[:, cc, 6:7], in1=Rr[:], op0=ALU.mult, op1=ALU.mult),
                      reads=[KD, rw, Rr], writes=[XI])
                for tt in range(NT):
                    kb.op("pe", lambda e: e.matmul(psX[:, tt * 2:tt * 2 + 2], lhsT=XI[:, tt * 128:(tt + 1) * 128], rhs=sel, start=True, stop=True),
                          reads=[XI, cst], writes=[psX])
                yield
                kb.op("act", lambda e: e.copy(out=stt[:, 64 + 32 * d:96 + 32 * d], in_=psX[:, 0:32]), reads=[psX], writes=[stt])
                yield
                kb.op("pool", lambda e: e.tensor_tensor(out=Aa[:], in0=Aa[:], in1=KKn[:], op=ALU.mult), reads=[Aa, KKn], writes=[Aa])
                yield
                kb.op("dve", lambda e: e.tensor_tensor_scan(out=XI[:], data0=segt[:], data1=XE[:], initial=0.0, op0=ALU.mult, op1=ALU.add),
                      reads=[segt, XE], writes=[XI])
                yield
                kb.op("pool", lambda e: e.tensor_copy(out=tot[:], in_=XI[:].rearrange("p (n s) -> p n s", s=64)[:, :, 63]), reads=[XI], writes=[tot])
                yield
                kb.op("act", lambda e: e.activation(out=GL[:], in_=tot[:], func=AF.Exp), reads=[tot], writes=[GL])
                if d == 0:
                    kb.op("pool", lambda e: e.tensor_tensor(out=XE[:], in0=XI[:], in1=XE[:], op=ALU.subtract), reads=[XI, XE], writes=[XE])
                else:
                    kb.op("dve", lambda e: e.tensor_tensor(out=XI[:].rearrange("p (n s) -> p n s", s=64),
                                                           in0=tot[:, :, None].to_broadcast([128, 32, 64]),
                                                           in1=XI[:].rearrange("p (n s) -> p n s", s=64), op=ALU.subtract),
                          reads=[tot, XI], writes=[XI])
                    kb.op("pool", lambda e: e.tensor_tensor(out=XE[:], in0=XI[:], in1=XE[:], op=ALU.add), reads=[XI, XE], writes=[XE])
                cI, cE = (XI, XE) if d == 0 else (XE, XI)
                yield
                kb.op("act", lambda e: e.activation(out=fr(E1[:]), in_=cI[:], func=AF.Exp), reads=[cI], writes=[E1])
                yield
                kb.op("act", lambda e: e.activation(out=cI[:], in_=cI[:], func=AF.Exp, scale=-1.0), reads=[cI], writes=[cI])
                yield
                kb.op("act", lambda e: e.activation(out=cE[:], in_=cE[:], func=AF.Exp), reads=[cE], writes=[cE])
                yield
                kb.op("dve", lambda e: e.tensor_tensor(out=fr(E1[:]), in0=E1[:], in1=Rr[:], op=ALU.mult), reads=[E1, Rr], writes=[E1])
                yield
                kb.op("pool", lambda e: e.tensor_tensor(out=fr(KT[:]), in0=KD[:], in1=cI[:], op=ALU.mult), reads=[KD, cI], writes=[KT])
                yield
                kb.op("dve", lambda e: e.tensor_tensor(out=fr(AT[:]), in0=Aa[:], in1=cI[:], op=ALU.mult), reads=[Aa, cI], writes=[AT])
                yield
                kb.op("dve", lambda e: e.scalar_tensor_tensor(out=fr(BTb[:]), in0=cE[:], scalar=-1.0, in1=KKn[:], op0=ALU.mult, op1=ALU.mult),
                      reads=[cE, KKn], writes=[BTb])
                yield
                kb.op("pool", lambda e: e.tensor_tensor(out=cI[:].rearrange("p (n s) -> p n s", s=64), in0=cI[:].rearrange("p (n s) -> p n s", s=64),
                                                        in1=GL[:, :, None].to_broadcast([128, 32, 64]), op=ALU.mult), reads=[cI, GL], writes=[cI])
                yield
                kb.op("dve", lambda e: e.tensor_tensor(out=KD[:], in0=KD[:], in1=cI[:], op=ALU.mult), reads=[KD, cI], writes=[KD])
                yield
                kb.op("pool", lambda e: e.tensor_tensor(out=Aa[:], in0=Aa[:], in1=cI[:], op=ALU.mult), reads=[Aa, cI], writes=[Aa])
                yield

            def seq_gen(d):
                kb.op("pool", lambda e: e.memset(H[0][:], 0.0), writes=[H[0]])
                order = range(32) if d == 0 else range(31, -1, -1)
                hc = 0
                for ch in order:
                    tt, n = ch // 2, ch % 2
                    ccols = slice(ch * 64, ch * 64 + 64)
                    Hc, Hn = H[hc], H[1 - hc]
                    tr = slice(n * 64, n * 64 + 64)
                    for hh in range(2):
                        pr = slice(hh * 64, hh * 64 + 64)
                        bk = bkH[hh]
                        kb.op("pe", lambda e: e.matmul(bk[pr, 64:128], lhsT=MTa[pr, ch, :], rhs=Hc[pr, :], start=True, stop=True),
                              reads=[MTa, Hc], writes=[bk])
                        kb.op("pe", lambda e: e.matmul(bk[tr, 0:64], lhsT=RhT[pr, ccols], rhs=Hc[pr, :], start=True, stop=True),
                              reads=[RhT, Hc], writes=[bk])
                    for hh in range(2):
                        pr = slice(hh * 64, hh * 64 + 64)
                        bk = bkH[hh]
                        kb.op("dve", lambda e: e.tensor_tensor(out=Hn[pr, :], in0=bk[pr, 64:128], in1=Ca[pr, ch, :], op=ALU.add),
                              reads=[bk, Ca], writes=[Hn])
                    for hh in range(2):
                        pr = slice(hh * 64, hh * 64 + 64)
                        bk = bkH[hh]
                        kb.op("act" if False else "dve", lambda e: e.tensor_tensor(out=Ysum[tr, tt, pr], in0=bk[tr, 0:64], in1=Ysum[tr, tt, pr], op=ALU.add),
                              reads=[bk, Ysum], writes=[Ysum])
                    hc = 1 - hc
                    yield
                yield

            def exhaust(g):
                for _ in g:
                    pass

            exhaust(prep_gen(0))
            run_tiles(0, E1, BTb)
            sg = seq_gen(0)
            pg = prep_gen(1)
            done_p = done_s = False
            while not (done_p and done_s):
                if not done_p:
                    try:
                        next(pg)
                    except StopIteration:
                        done_p = True
                for _ in range(2):
                    if not done_s:
                        try:
                            next(sg)
                        except StopIteration:
                            done_s = True
            run_tiles(1, E1, BTb)
            exhaust(seq_gen(1))
            chk(10)
            if dbg is not None and "Ysum" in dbg:
                kb.dma("sp", D.dbgY.ap()[:, cc * 128:(cc + 1) * 128].rearrange("(tt p) c -> p tt c", p=128), Ysum[:], reads=[Ysum], writes=[D.dbgY])
            Y3 = Ysum[:].rearrange("p t (h i) -> p (t h) i", h=2)
            st_mu = stt[:, 0:32]
            st_var = stt[:, 32:64]
            bon = stt[:, 128:160]
            kb.op("dve", lambda e: e.tensor_reduce(out=st_mu, in_=Y3, axis=AX.X, op=ALU.add), reads=[Ysum], writes=[stt])
            kb.op("dve", lambda e: e.tensor_scalar(out=st_mu, in0=st_mu, scalar1=1.0 / 64, scalar2=None, op0=ALU.mult), reads=[stt], writes=[stt])
            kb.op("dve", lambda e: e.tensor_tensor(out=Y3, in0=Y3, in1=st_mu[:, :, None].to_broadcast([128, 32, 64]), op=ALU.subtract),
                  reads=[Ysum, stt], writes=[Ysum])
            sqv = XI[:].rearrange("p (a i) -> p a i", i=64)
            kb.op("act", lambda e: e.activation(out=sqv, in_=Y3, func=AF.Square), reads=[Ysum], writes=[XI])
            kb.op("dve", lambda e: e.tensor_reduce(out=st_var, in_=sqv, axis=AX.X, op=ALU.add), reads=[XI], writes=[stt])
            kb.op("dve", lambda e: e.tensor_scalar(out=st_var, in0=st_var, scalar1=1.0 / 64, scalar2=GN_EPS, op0=ALU.mult, op1=ALU.add),
                  reads=[stt], writes=[stt])
            kb.op("act", lambda e: e.activation(out=st_var, in_=st_var, func=AF.Sqrt), reads=[stt], writes=[stt])
            kb.op("dve", lambda e: e.reciprocal(out=st_var, in_=st_var), reads=[stt], writes=[stt])
            kb.op("dve", lambda e: e.tensor_tensor(out=Y3, in0=Y3, in1=st_var[:, :, None].to_broadcast([128, 32, 64]), op=ALU.mult),
                  reads=[Ysum, stt], writes=[Ysum])
            kb.op("pool", lambda e: e.tensor_tensor(out=Ysum[:], in0=Ysum[:], in1=lnw[:, None, :].to_broadcast([128, NT, 128]), op=ALU.mult),
                  reads=[Ysum, lnw], writes=[Ysum])
            kb.op("pool", lambda e: e.tensor_tensor(out=Ysum[:], in0=Ysum[:], in1=lnb[:, None, :].to_broadcast([128, NT, 128]), op=ALU.add),
                  reads=[Ysum, lnb], writes=[Ysum])
            kb.op("dve", lambda e: e.tensor_tensor(out=bon, in0=stt[:, 64:96], in1=stt[:, 96:128], op=ALU.add), reads=[stt], writes=[stt])
            kb.op("dve", lambda e: e.tensor_scalar(out=bon, in0=bon, scalar1=0.5, scalar2=None, op0=ALU.mult), reads=[stt], writes=[stt])
            V3 = Vtm[:].rearrange("p t (h i) -> p (t h) i", h=2)
            kb.op("pool", lambda e: e.tensor_tensor(out=sqv, in0=V3, in1=bon[:, :, None].to_broadcast([128, 32, 64]), op=ALU.mult),
                  reads=[Vtm, stt], writes=[XI])
            kb.op("pool", lambda e: e.tensor_tensor(out=Y3, in0=Y3, in1=sqv, op=ALU.add), reads=[Ysum, XI], writes=[Ysum])
            gt = XE[:].rearrange("p (t c) -> p t c", c=128)
            kb.dma("sp", gt, D.g_d.ap()[:, cc * 128:(cc + 1) * 128].rearrange("(tt p) c -> p tt c", p=128), reads=[D.g_d], writes=[XE])
            kb.op("dve", lambda e: e.tensor_tensor(out=Ysum[:], in0=Ysum[:], in1=gt, op=ALU.mult), reads=[Ysum, XE], writes=[Ysum])
            if dbg is not None and "yfin" in dbg:
                kb.dma("sp", D.dbgF.ap()[:, cc * 128:(cc + 1) * 128].rearrange("(tt p) c -> p tt c", p=128), Ysum[:], reads=[Ysum], writes=[D.dbgF])
            for g in range(4):
                for j in range(4):
                    tt = g * 4 + j
                    kb.op("pe", lambda e: e.transpose(out=psX[:, j * 128:(j + 1) * 128], in_=Ysum[:, tt, :], identity=C.ident[:]),
                          reads=[Ysum, C.ident], writes=[psX])
                kb.op("act", lambda e: e.copy(out=KD[:, g * 512:(g + 1) * 512], in_=psX[:]), reads=[psX], writes=[KD])
            yb = Aa[:].bitcast(BF16)[:, 0:2048]
            kb.op("dve", lambda e: e.tensor_copy(out=yb, in_=KD[:]), reads=[KD], writes=[Aa])
            kb.dma("sp", D.ymT_d.ap()[cc], yb, reads=[Aa], writes=[D.ymT_d])
        kb.barrier()


def host_consts():
    S = 2048
    rows = np.repeat(np.arange(32), 64).astype(np.float32)
    cols = np.tile(np.arange(64), 32).astype(np.float32)
    inv = (10000.0 ** (-np.arange(0, 32, 2, dtype=np.float32) / 32)).astype(np.float32)
    ar = rows[:, None] * inv[None]
    ac = cols[:, None] * inv[None]
    tabC = np.concatenate([np.cos(ar), np.cos(ac)], 1).astype(np.float32)
    tabS = np.concatenate([np.sin(ar), np.sin(ac)], 1).astype(np.float32)
    ident = np.eye(128, dtype=np.float32)
    iota = np.tile(np.arange(128, dtype=np.float32)[None], (128, 1))
    r = np.arange(128)[:, None]
    s = np.arange(128)[None, :]
    same = (r // 64) == (s // 64)
    masks = np.zeros((128, 6, 128), np.float32)
    masks[:, 0] = same & (r < s)
    masks[:, 1] = same & (r <= s)
    masks[:, 2] = same & (r > s)
    masks[:, 3] = same & (r >= s)
    masks[:, 4] = same & (r > s)
    masks[:, 5] = same
    rcst = np.zeros((128, 66 + 2048), np.float32)
    pp = np.arange(128)
    rcst[pp, pp % 64] = 1.0
    rcst[:, 64] = (pp // 64 == 0)
    rcst[:, 65] = (pp // 64 == 1)
    seg = np.ones(2048, np.float32); seg[::64] = 0.0
    rcst[:, 66:] = seg[None]
    import ml_dtypes
    segm = np.ascontiguousarray(np.broadcast_to(seg[None], (128, 2048))).astype(ml_dtypes.bfloat16)
    return dict(tabC=tabC, tabS=tabS, ident=ident, iota=iota, masks=masks, rcst=rcst, segm=segm)

def prep_shared(inp):
    L = 0
    d = host_consts()
    mu_c = np.zeros((128, 3 * NRCH), np.float32)
    for ci, (c0, cs) in enumerate(RCH):
        mu_c[:cs, ci] = inp["mu_prev"][L, c0:c0 + cs]
        mu_c[:cs, NRCH + ci] = inp["mu_next"][L, c0:c0 + cs]
    d["mu_c"] = mu_c
    w = inp["w_in"][L].reshape(16, 128, 5248)
    wt = np.empty((128, 16 * 5248), np.float32)
    for (c0, cs) in RCH:
        wt[:, 16 * c0:16 * (c0 + cs)] = w[:, :, c0:c0 + cs].transpose(1, 0, 2).reshape(128, 16 * cs)
    RWK = 3712
    for cg in range(3):
        for hf in range(2):
            o0 = 16 * RWK + (cg * 2 + hf) * 4096
            wt[:, o0:o0 + 4096] = w[hf * 8:(hf + 1) * 8, :, RWK + cg * 512:RWK + (cg + 1) * 512].transpose(1, 0, 2).reshape(128, 4096)
    d["w_in"] = wt
    for k in ["norm1_w", "norm2_w", "q_norm_w", "k_norm_w", "w_out", "w_pq", "w2", "a2", "g2", "lnx_w", "lnx_b", "v_tab"]:
        d[k] = np.ascontiguousarray(inp[k][L])
    d["normf_w"] = np.ascontiguousarray(inp["normf_w"])
    d["skT"] = np.ascontiguousarray(inp["sub_keys"][L].reshape(16, 128, 128).transpose(2, 0, 1))
    d["u_tabT"] = np.ascontiguousarray(inp["u_tab"][L].reshape(128, 128, 16, 128).transpose(0, 3, 2, 1)).reshape(128, 128, 2048)
    rw = np.zeros((128, 8, 8), np.float32)
    def ch(v):
        return v.reshape(8, 128).T
    rw[:, :, 0] = ch(inp["w0"][L, 0]); rw[:, :, 1] = ch(inp["w0"][L, 1])
    rw[:, :, 2] = ch(inp["a0"][L, 0]); rw[:, :, 3] = ch(inp["a0"][L, 1])
    rw[:, :, 4] = ch(inp["k_k"][L]); rw[:, :, 5] = ch(inp["k_a"][L]); rw[:, :, 6] = ch(inp["r_k"][L].reshape(-1))
    d["rw_c"] = rw
    return d


def build_program():
    nc = bass.Bass("TRN2", target_bir_lowering=False)
    kb = KB(nc)
    D = declare(kb, None)
    C = consts(kb, D)
    phase_A(kb, C, D)
    phase_Rpre(kb, C, D)
    phase_R(kb, C, D)
    phase_T(kb, C, D)
    phase_O(kb, C, D, D.ymT_d)
    phase_P(kb, C, D)
    phase_F(kb, C, D)
    kb.finish("sp")
    return nc


def kernel(**inputs):
    inp = {k: np.asarray(v) for k, v in inputs.items()}
    shared = prep_shared(inp)
    nc = build_program()
    in_maps = []
    for b in range(8):
        d = dict(shared)
        d["x"] = np.ascontiguousarray(inp["x"][b])
        in_maps.append(d)
    res = run_bass_kernel_spmd(nc, in_maps, core_ids=list(range(8)))
    out = np.stack([np.asarray(r["out"], dtype=np.float32) for r in res.results], axis=0)
    return out
```

```python
import numpy as np
import contextlib
import concourse.bass as bass
import concourse.mybir as mybir
from concourse.bass_utils import run_bass_kernel_spmd

F32 = mybir.dt.float32
BF16 = mybir.dt.bfloat16
U32 = mybir.dt.uint32
ALU = mybir.AluOpType
AF = mybir.ActivationFunctionType
AX = mybir.AxisListType


class T:
    def __init__(self, h, name):
        self.h = h
        self.name = name
        self.w = None
        self.r = {}
        self.dsem = None
        self.dcnt = 0
        self.is_psum = False

    def __getitem__(self, k):
        return self.h[k]

    def ap(self):
        return self.h.ap() if hasattr(self.h, "ap") else self.h[:]


class KB:
    def __init__(self, nc):
        self.nc = nc
        self.es = contextlib.ExitStack()
        self.engs = {"pe": nc.tensor, "act": nc.scalar, "dve": nc.vector, "pool": nc.gpsimd, "sp": nc.sync}
        self.sem = {}
        self.cnt = {}
        for e in self.engs:
            self.sem[e] = self.es.enter_context(nc.semaphore("s_" + e))
            self.cnt[e] = 0
        self.waited = {}
        self.alltensors = []
        self.dsems = []
        self.n_ins = 0

    def sb(self, name, shape, dt=F32, stack=None):
        h = (stack or self.es).enter_context(self.nc.sbuf_tensor(name, list(shape), dt))
        t = T(h, name)
        return t

    def ps(self, name, shape, dt=F32, stack=None):
        h = (stack or self.es).enter_context(self.nc.psum_tensor(name, list(shape), dt))
        t = T(h, name)
        t.is_psum = True
        return t

    def dram(self, name, shape, dt=F32, kind="Internal"):
        h = self.nc.dram_tensor(name, list(shape), dt, kind=kind)
        return T(h, name)

    def view(self, t, name=None):
        return t

    def _wait(self, eng, tok):
        if tok is None:
            return
        sem, val = tok
        key = (eng, id(sem))
        if self.waited.get(key, 0) >= val:
            return
        self.engs[eng].wait_ge(sem, val)
        self.waited[key] = val

    def _deps(self, eng, reads, writes):
        own = id(self.sem[eng])
        for t in reads:
            if t.w is not None:
                if eng == "pe" and id(t.w[0]) == own:
                    continue
                self._wait(eng, t.w)
        for t in writes:
            if t.w is not None and id(t.w[0]) != own:
                self._wait(eng, t.w)
            for k, tok in t.r.items():
                if k == own:
                    continue
                self._wait(eng, tok)

    def _mark(self, tok, reads, writes):
        for t in reads:
            if t in writes:
                continue
            t.r[id(tok[0])] = tok
        for t in writes:
            t.w = tok
            t.r = {}

    def op(self, eng, fn, reads=(), writes=()):
        psr = [t for t in reads if t.is_psum and t not in writes]
        if psr:
            writes = list(writes) + psr
        self._deps(eng, reads, writes)
        ins = fn(self.engs[eng])
        self.cnt[eng] += 1
        ins.then_inc(self.sem[eng], 1)
        tok = (self.sem[eng], self.cnt[eng])
        self._mark(tok, reads, writes)
        self.n_ins += 1
        return ins

    def dma(self, q, out_ap, in_ap, reads=(), writes=(), **kw):
        assert len(writes) == 1
        dst = writes[0]
        if dst.dsem is None:
            dst.dsem = self.es.enter_context(self.nc.semaphore("d_" + dst.name))
            self.dsems.append(dst)
        self._deps(q, reads, [])
        if dst.w is not None and dst.w[0] is not dst.dsem:
            self._wait(q, dst.w)
        for k, tok in dst.r.items():
            self._wait(q, tok)
        ins = self.engs[q].dma_start(out=out_ap, in_=in_ap, **kw)
        dst.dcnt += 16
        ins.then_inc(dst.dsem, 16)
        tok = (dst.dsem, dst.dcnt)
        for t in reads:
            t.r[id(tok[0])] = tok
        dst.w = tok
        dst.r = {}
        self.n_ins += 1
        return ins

    def pe_fence(self):
        if self.cnt["pe"] > 0:
            self._wait("pe", (self.sem["pe"], self.cnt["pe"]))

    def barrier(self):
        toks = [(self.sem[e], self.cnt[e]) for e in self.engs if self.cnt[e] > 0]
        toks += [(t.dsem, t.dcnt) for t in self.dsems if t.dcnt > 0]
        for e in self.engs:
            for tok in toks:
                if tok[0] is self.sem[e]:
                    continue
                self._wait(e, tok)

    def finish(self, eng="sp"):
        for t in self.dsems:
            if t.dcnt > 0:
                self._wait(eng, (t.dsem, t.dcnt))
        for e in self.engs:
            if e != eng and self.cnt[e] > 0:
                self._wait(eng, (self.sem[e], self.cnt[e]))


EPS = 1e-6
NT = 16
RCH = [(i * 128, 128) for i in range(24)] + [(3072 + i * 96, 96) for i in range(4)] + [(3456, 128), (3584, 128)]
NRCH = len(RCH)
RW = 3712


class Obj:
    pass


def declare(kb, debug):
    D = Obj()

    def inp(name, shape, dt=F32):
        setattr(D, name, kb.dram(name, shape, dt, kind="ExternalInput"))

    def scr(name, shape, dt=F32):
        kind = "ExternalOutput" if (debug and name in debug) else "Internal"
        setattr(D, name, kb.dram(name, shape, dt, kind=kind))

    inp("x", [2048, 2048])
    inp("w_in", [128, 16 * 5248])
    inp("mu_c", [128, 3 * NRCH])
    inp("norm1_w", [2048])
    inp("norm2_w", [2048])
    inp("normf_w", [2048])
    inp("q_norm_w", [64])
    inp("k_norm_w", [64])
    inp("tabC", [2048, 32])
    inp("tabS", [2048, 32])
    inp("ident", [128, 128])
    inp("w_out", [2048, 2048])
    inp("w_pq", [2048, 2048])
    inp("skT", [128, 16, 128])
    inp("iota", [128, 128])
    inp("u_tabT", [128, 128, 2048])
    inp("v_tab", [16384, 2048])
    inp("w2", [2, 96, 1024])
    inp("a2", [2, 96, 1024])
    inp("g2", [256, 1024])
    inp("rw_c", [128, 8, 8])
    inp("lnx_w", [1024])
    inp("lnx_b", [1024])
    inp("masks", [128, 6, 128])
    if debug and "in_ymT" in debug:
        inp("ymT_in", [16, 128, 2048], BF16)
    if debug and "in_x1" in debug:
        inp("x1_in", [2048, 2048])
    scr("zs_d", [RW, 2048])
    scr("qkv_d", [2048, 1536])
    scr("ymT_d", [16, 128, 2048], BF16)
    scr("x1_d", [2048, 2048])
    scr("G_d", [128, 128, 2048], BF16)
    scr("peT_d", [2048, 2048])
    scr("A_d", [128, 128, 2048], BF16)
    scr("ld_d", [2, 1024, 2048])
    scr("a_d", [2, 1024, 2048])
    scr("g_d", [2048, 1024])
    inp("rcst", [128, 66 + 2048])
    inp("segm", [128, 2048], BF16)
    if debug and "dbgY" in debug:
        setattr(D, "dbgY", kb.dram("dbgY", [2048, 1024], F32, kind="ExternalOutput"))
        setattr(D, "dbgF", kb.dram("dbgF", [2048, 1024], F32, kind="ExternalOutput"))
    scr("S_d", [2048, 16, 128])
    scr("h2T_d", [16, 128, 2048], BF16)
    scr("vb_d", [16384, 2048], BF16)
    setattr(D, "out", kb.dram("out", [2048, 2048], F32, kind="ExternalOutput"))
    return D


def consts(kb, D):
    C = Obj()
    C.ident = kb.sb("c_ident", [128, 128], F32)
    kb.dma("sp", C.ident[:], D.ident.ap(), writes=[C.ident])
    C.identb = kb.sb("c_identb", [128, 128], BF16)
    kb.op("dve", lambda e: e.tensor_copy(out=C.identb[:], in_=C.ident[:]), reads=[C.ident], writes=[C.identb])
    return C


def norm_to_T(kb, C, src, wvec, hT, tag):
    with contextlib.ExitStack() as st:
        wb = kb.sb(tag + "wb", [128, 2048], F32, st)
        kb.dma("pool", wb[:], wvec.ap().partition_broadcast(128), writes=[wb])
        xts = [kb.sb(f"{tag}x{i}", [128, 2048], F32, st) for i in range(2)]
        junk = kb.sb(tag + "junk", [128, 2048], BF16, st)
        hb = [kb.sb(f"{tag}h{i}", [128, 2048], BF16, st) for i in range(2)]
        ss = kb.sb(tag + "ss", [128, NT], F32, st)
        rs = kb.sb(tag + "rs", [128, NT], F32, st)
        ptr = [kb.ps(f"{tag}ps{i}", [128, 1024], BF16, st) for i in range(2)]
        kb.op("pool", lambda e: e.memset(ss[:], 0.0), writes=[ss])
        for tt in range(NT):
            xt = xts[tt % 2]
            kb.dma("sp", xt[:], src.ap()[tt * 128:(tt + 1) * 128, :], writes=[xt])
            kb.op("act", lambda e: e.activation(out=junk[:], in_=xt[:], func=AF.Square, accum_out=ss[:, tt:tt + 1]),
                  reads=[xt], writes=[junk, ss])
            kb.op("dve", lambda e: e.tensor_scalar(out=rs[:, tt:tt + 1], in0=ss[:, tt:tt + 1], scalar1=1.0 / 2048, scalar2=EPS,
                                                   op0=ALU.mult, op1=ALU.add), reads=[ss], writes=[rs])
            kb.op("act", lambda e: e.activation(out=rs[:, tt:tt + 1], in_=rs[:, tt:tt + 1], func=AF.Sqrt), reads=[rs], writes=[rs])
            kb.op("dve", lambda e: e.reciprocal(out=rs[:, tt:tt + 1], in_=rs[:, tt:tt + 1]), reads=[rs], writes=[rs])
            h = hb[tt % 2]
            kb.op("dve", lambda e: e.scalar_tensor_tensor(out=h[:], in0=xt[:], scalar=rs[:, tt:tt + 1], in1=wb[:],
                                                          op0=ALU.mult, op1=ALU.mult), reads=[xt, rs, wb], writes=[h])
            for g in range(2):
                p = ptr[g]
                for j in range(8):
                    dc = g * 8 + j
                    kb.op("pe", lambda e: e.transpose(out=p[:, j * 128:(j + 1) * 128], in_=h[:, dc * 128:(dc + 1) * 128],
                                                      identity=C.identb[:]), reads=[h, C.identb], writes=[p])
                eng = "act" if g == 0 else "dve"
                src_ap = p[:].rearrange("p (j t) -> p j t", j=8)
                dst_ap = hT[:, g * 8:(g + 1) * 8, tt * 128:(tt + 1) * 128]
                if eng == "act":
                    kb.op("act", lambda e: e.copy(out=dst_ap, in_=src_ap), reads=[p], writes=[hT])
                else:
                    kb.op("dve", lambda e: e.tensor_copy(out=dst_ap, in_=src_ap), reads=[p], writes=[hT])
        kb.barrier()


def phase_A(kb, C, D):
    with contextlib.ExitStack() as st:
        hT = kb.sb("hT", [128, 16, 2048], BF16, st)
        norm_to_T(kb, C, D.x, D.norm1_w, hT, "n1")
        mu = kb.sb("mu", [128, 3 * NRCH], F32, st)
        kb.dma("sp", mu[:], D.mu_c.ap(), writes=[mu])
        kb.op("dve", lambda e: e.tensor_tensor(out=mu[:, 60:90], in0=mu[:, 0:30], in1=mu[:, 30:60], op=ALU.add), reads=[mu], writes=[mu])
        kb.op("dve", lambda e: e.tensor_scalar(out=mu[:, 60:90], in0=mu[:, 60:90], scalar1=-1.0, scalar2=1.0, op0=ALU.mult, op1=ALU.add),
              reads=[mu], writes=[mu])
        stg = [kb.sb(f"a_stg{i}", [128, 4096], F32, st) for i in range(2)]
        wbf = [kb.sb(f"a_wbf{i}", [128, 16, 128], BF16, st) for i in range(2)]
        accs = [kb.sb(f"a_acc{i}", [128, 2048], F32, st) for i in range(2)]
        pss = [[kb.ps(f"a_ps{i}_{j}", [128, 512], F32, st) for j in range(4)] for i in range(2)]
        def loadw(ci):
            c0, cs = RCH[ci]
            kb.dma("sp", stg[ci % 2][:, 0:16 * cs], D.w_in.ap()[:, 16 * c0:16 * (c0 + cs)], writes=[stg[ci % 2]])
        loadw(0)
        for ci, (c0, cs) in enumerate(RCH):
            sg = stg[ci % 2]
            sgv = sg[:, 0:16 * cs].rearrange("p (dc c) -> p dc c", dc=16)
            wb = wbf[ci % 2]
            kb.op("pool", lambda e: e.tensor_copy(out=wb[:, :, 0:cs], in_=sgv), reads=[sg], writes=[wb])
            if ci + 1 < NRCH:
                loadw(ci + 1)
            ps = pss[ci % 2]
            acc = accs[ci % 2]
            for tb in range(4):
                for dc in range(16):
                    kb.op("pe", lambda e: e.matmul(ps[tb][0:cs, :], lhsT=wb[:, dc, 0:cs], rhs=hT[:, dc, tb * 512:(tb + 1) * 512],
                                                   start=(dc == 0), stop=(dc == 15)), reads=[wb, hT], writes=[ps[tb]])
            for tb in range(4):
                kb.op("act", lambda e: e.activation(out=acc[0:cs, tb * 512:(tb + 1) * 512], in_=ps[tb][0:cs, :], func=AF.Copy,
                                                    scale=mu[0:cs, 60 + ci:61 + ci]), reads=[ps[tb], mu], writes=[acc])
            for tb in range(4):
                n = 512 if tb < 3 else 511
                d0 = tb * 512 + 1
                kb.op("dve", lambda e: e.scalar_tensor_tensor(out=acc[0:cs, d0:d0 + n], in0=ps[tb][0:cs, 0:n], scalar=mu[0:cs, ci:ci + 1],
                                                              in1=acc[0:cs, d0:d0 + n], op0=ALU.mult, op1=ALU.add),
                      reads=[ps[tb], mu, acc], writes=[acc])
                s0 = 1 if tb == 0 else 0
                n = 512 - s0
                d0 = tb * 512 + s0 - 1
                kb.op("dve", lambda e: e.scalar_tensor_tensor(out=acc[0:cs, d0:d0 + n], in0=ps[tb][0:cs, s0:512], scalar=mu[0:cs, 30 + ci:31 + ci],
                                                              in1=acc[0:cs, d0:d0 + n], op0=ALU.mult, op1=ALU.add),
                      reads=[ps[tb], mu, acc], writes=[acc])
            kb.dma("pool", D.zs_d.ap()[c0:c0 + cs, :], acc[0:cs, :], reads=[acc], writes=[D.zs_d])
        kb.barrier()
        with contextlib.ExitStack() as st2:
            wq = kb.sb("a_wq", [128, 16, 512], BF16, st2)
            ev = [kb.sb(f"a_ev{i}", [128, 512], F32, st2) for i in range(2)]
            for cg in range(3):
                c0 = RW + cg * 512
                for hf in range(2):
                    sg = stg[hf]
                    sgv = sg[:].rearrange("p (dc c) -> p dc c", dc=8)
                    o0 = 16 * RW + (cg * 2 + hf) * 4096
                    kb.dma("sp", sg[:], D.w_in.ap()[:, o0:o0 + 4096], writes=[sg])
                    kb.op("pool", lambda e: e.tensor_copy(out=wq[:, hf * 8:(hf + 1) * 8, :], in_=sgv), reads=[sg], writes=[wq])
                for tt in range(NT):
                    ps = pss[tt % 2][0]
                    for dc in range(16):
                        kb.op("pe", lambda e: e.matmul(ps[:], lhsT=hT[:, dc, tt * 128:(tt + 1) * 128], rhs=wq[:, dc, :],
                                                       start=(dc == 0), stop=(dc == 15)), reads=[wq, hT], writes=[ps])
                    o = ev[tt % 2]
                    kb.op("act", lambda e: e.copy(out=o[:], in_=ps[:]), reads=[ps], writes=[o])
                    kb.dma("sp", D.qkv_d.ap()[tt * 128:(tt + 1) * 128, cg * 512:(cg + 1) * 512], o[:], reads=[o], writes=[D.qkv_d])
            kb.barrier()


def phase_T(kb, C, D):
    with contextlib.ExitStack() as st:
        qT = kb.sb("t_qT", [128, 8, 2048], BF16, st)
        kT2 = kb.sb("t_kT2", [128, 4, 2048], BF16, st)
        vaug = kb.sb("t_vaug", [128, NT, 4, 65], BF16, st)
        wqk = kb.sb("t_wqk", [128, 20, 64], F32, st)
        w64 = kb.sb("t_w64", [128, 2, 64], F32, st)
        kb.dma("pool", w64[:, 0, :], D.q_norm_w.ap().partition_broadcast(128), writes=[w64])
        kb.dma("pool", w64[:, 1, :], D.k_norm_w.ap().partition_broadcast(128), writes=[w64])
        kb.op("dve", lambda e: e.tensor_scalar(out=wqk[:, 0:16, :], in0=w64[:, 0:1, :].to_broadcast([128, 16, 64]), scalar1=0.125, scalar2=None,
                                               op0=ALU.mult), reads=[w64], writes=[wqk])
        kb.op("dve", lambda e: e.tensor_copy(out=wqk[:, 16:20, :], in_=w64[:, 1:2, :].to_broadcast([128, 4, 64])), reads=[w64], writes=[wqk])
        kb.op("pool", lambda e: e.memset(vaug[:], 1.0), writes=[vaug])
        with contextlib.ExitStack() as st2:
            qk = [kb.sb(f"t_qk{i}", [128, 1536], F32, st2) for i in range(2)]
            tC = [kb.sb(f"t_tC{i}", [128, 2, 16], F32, st2) for i in range(2)]
            tS = [kb.sb(f"t_tS{i}", [128, 2, 16], F32, st2) for i in range(2)]
            sq = kb.sb("t_sq", [128, 20, 64], F32, st2)
            ss = kb.sb("t_ss", [128, 20], F32, st2)
            qn = kb.sb("t_qn", [128, 20, 64], F32, st2)
            t1 = kb.sb("t_t1", [128, 20, 2, 16], F32, st2)
            t2 = kb.sb("t_t2", [128, 20, 2, 16], F32, st2)
            t3 = kb.sb("t_t3", [128, 20, 2, 16], F32, st2)
            t4 = kb.sb("t_t4", [128, 20, 2, 16], F32, st2)
            qkr = kb.sb("t_qkr", [128, 20, 64], BF16, st2)
            kd = kb.sb("t_kd", [128, 4, 2, 64], BF16, st2)
            pq = kb.ps("t_pq", [128, 1024], BF16, st2)
            pk = kb.ps("t_pk", [128, 1024], BF16, st2)
            for tt in range(NT):
                q = qk[tt % 2]
                cC = tC[tt % 2]
                cS = tS[tt % 2]
                kb.dma("sp", q[:], D.qkv_d.ap()[tt * 128:(tt + 1) * 128, :], reads=[D.qkv_d], writes=[q])
                kb.dma("act", cC[:].rearrange("p a b -> p (a b)"), D.tabC.ap()[tt * 128:(tt + 1) * 128, :], writes=[cC])
                kb.dma("act", cS[:].rearrange("p a b -> p (a b)"), D.tabS.ap()[tt * 128:(tt + 1) * 128, :], writes=[cS])
                qv = q[:, 0:1280].rearrange("p (h d) -> p h d", h=20)
                kb.op("act", lambda e: e.activation(out=sq[:], in_=qv, func=AF.Square), reads=[q], writes=[sq])
                kb.op("dve", lambda e: e.tensor_reduce(out=ss[:], in_=sq[:], axis=AX.X, op=ALU.add), reads=[sq], writes=[ss])
                kb.op("dve", lambda e: e.tensor_scalar(out=ss[:], in0=ss[:], scalar1=1.0 / 64, scalar2=EPS, op0=ALU.mult, op1=ALU.add),
                      reads=[ss], writes=[ss])
                kb.op("act", lambda e: e.activation(out=ss[:], in_=ss[:], func=AF.Sqrt), reads=[ss], writes=[ss])
                kb.op("dve", lambda e: e.reciprocal(out=ss[:], in_=ss[:]), reads=[ss], writes=[ss])
                kb.op("dve", lambda e: e.tensor_tensor(out=qn[:], in0=qv, in1=ss[:, :, None].to_broadcast([128, 20, 64]), op=ALU.mult),
                      reads=[q, ss], writes=[qn])
                kb.op("pool", lambda e: e.tensor_tensor(out=qn[:], in0=qn[:], in1=wqk[:], op=ALU.mult), reads=[qn, wqk], writes=[qn])
                qn5 = qn[:].rearrange("p h (a b c) -> p h a b c", a=2, b=2)
                x1 = qn5[:, :, :, 0, :]
                x2 = qn5[:, :, :, 1, :]
                Cb = cC[:, None, :, :].to_broadcast([128, 20, 2, 16])
                Sb = cS[:, None, :, :].to_broadcast([128, 20, 2, 16])
                kb.op("dve", lambda e: e.tensor_tensor(out=t1[:], in0=x1, in1=Cb, op=ALU.mult), reads=[qn, cC], writes=[t1])
                kb.op("pool", lambda e: e.tensor_tensor(out=t2[:], in0=x2, in1=Sb, op=ALU.mult), reads=[qn, cS], writes=[t2])
                kb.op("pool", lambda e: e.tensor_tensor(out=t3[:], in0=x2, in1=Cb, op=ALU.mult), reads=[qn, cC], writes=[t3])
                kb.op("dve", lambda e: e.tensor_tensor(out=t4[:], in0=x1, in1=Sb, op=ALU.mult), reads=[qn, cS], writes=[t4])
                r5 = qkr[:].rearrange("p h (a b c) -> p h a b c", a=2, b=2)
                kb.op("dve", lambda e: e.tensor_tensor(out=r5[:, :, :, 0, :], in0=t1[:], in1=t2[:], op=ALU.subtract), reads=[t1, t2], writes=[qkr])
                kb.op("pool", lambda e: e.tensor_tensor(out=r5[:, :, :, 1, :], in0=t3[:], in1=t4[:], op=ALU.add), reads=[t3, t4], writes=[qkr])
                kb.op("pool", lambda e: e.tensor_copy(out=kd[:], in_=qkr[:, 16:20, None, :].to_broadcast([128, 4, 2, 64])), reads=[qkr], writes=[kd])
                for j in range(8):
                    kb.op("pe", lambda e: e.transpose(out=pq[:, j * 128:(j + 1) * 128], in_=qkr[:, 2 * j:2 * j + 2, :].rearrange("p a b -> p (a b)"),
                                                      identity=C.identb[:]), reads=[qkr, C.identb], writes=[pq])
                kb.op("act", lambda e: e.copy(out=qT[:, :, tt * 128:(tt + 1) * 128], in_=pq[:].rearrange("p (j t) -> p j t", j=8)),
                      reads=[pq], writes=[qT])
                for j in range(4):
                    kb.op("pe", lambda e: e.transpose(out=pk[:, j * 128:(j + 1) * 128], in_=kd[:, j, :, :].rearrange("p a b -> p (a b)"),
                                                      identity=C.identb[:]), reads=[kd, C.identb], writes=[pk])
                kb.op("dve", lambda e: e.tensor_copy(out=kT2[:, :, tt * 128:(tt + 1) * 128], in_=pk[:, 0:512].rearrange("p (j t) -> p j t", j=4)),
                      reads=[pk], writes=[kT2])
                kb.op("pool", lambda e: e.tensor_copy(out=vaug[:, tt, :, 0:64], in_=q[:, 1280:1536].rearrange("p (h d) -> p h d", h=4)),
                      reads=[q], writes=[vaug])
            kb.barrier()
        with contextlib.ExitStack() as st3:
            yatt = kb.sb("t_yatt", [128, NT, 1024], BF16, st3)
            pexp = [kb.sb(f"t_pexp{i}", [128, 512], BF16, st3) for i in range(3)]
            rinv = kb.sb("t_rinv", [128, 4], F32, st3)
            pss = [kb.ps(f"t_pss{i}", [128, 512], F32, st3) for i in range(2)]
            po = [kb.ps(f"t_po{i}", [128, 512], F32, st3) for i in range(4)]
            seq = [(h, qb, kt) for h in range(16) for qb in range(4) for kt in range(NT)]
            cv_st = [kb.sb(f"t_cvst{i}", [128, 2048], F32, st3) for i in range(2)]
            cv_bf = [kb.sb(f"t_cvbf{i}", [128, 2048], BF16, st3) for i in range(2)]

            def vconv(c):
                a, b = cv_st[c % 2], cv_bf[c % 2]
                kb.dma("sp", a[:], D.v_tab.ap()[c * 128:(c + 1) * 128, :], writes=[a])
                kb.op("dve", lambda e: e.tensor_copy(out=b[:], in_=a[:]), reads=[a], writes=[b])
                kb.dma("pool", D.vb_d.ap()[c * 128:(c + 1) * 128, :], b[:], reads=[b], writes=[D.vb_d])

            def emitS(i):
                h, qb, kt = seq[i]
                kv, c, b0 = h // 4, h // 2, (h % 2) * 64
                ps = pss[i % 2]
                kb.op("pe", lambda e: e.matmul(ps[:], lhsT=kT2[b0:b0 + 64, kv, kt * 128:(kt + 1) * 128],
                                               rhs=qT[b0:b0 + 64, c, qb * 512:(qb + 1) * 512], start=True, stop=True),
                      reads=[kT2, qT], writes=[ps])

            def emitE(i):
                ps = pss[i % 2]
                pe_ = pexp[i % 3]
                kb.op("act", lambda e: e.activation(out=pe_[:], in_=ps[:], func=AF.Exp), reads=[ps], writes=[pe_])

            def emitPV(i):
                h, qb, kt = seq[i]
                kv = h // 4
                pe_ = pexp[i % 3]
                for j in range(4):
                    kb.op("pe", lambda e: e.matmul(po[j][:, 0:65], lhsT=pe_[:, j * 128:(j + 1) * 128], rhs=vaug[:, kt, kv, :],
                                                   start=(kt == 0), stop=(kt == NT - 1)), reads=[pe_, vaug], writes=[po[j]])
                if kt == NT - 1:
                    for j in range(4):
                        kb.op("dve", lambda e: e.reciprocal(out=rinv[:, j:j + 1], in_=po[j][:, 64:65]), reads=[po[j]], writes=[rinv])
                        kb.op("dve", lambda e: e.tensor_scalar(out=yatt[:, qb * 4 + j, h * 64:(h + 1) * 64], in0=po[j][:, 0:64],
                                                               scalar1=rinv[:, j:j + 1], scalar2=None, op0=ALU.mult),
                              reads=[po[j], rinv], writes=[yatt])

            emitS(0)
            for i in range(len(seq)):
                if i + 1 < len(seq):
                    emitS(i + 1)
                emitE(i)
                emitPV(i)
                if i % 8 == 0:
                    vconv(i // 8)
            pt = [kb.ps(f"t_pt{i}", [128, 1024], BF16, st3) for i in range(2)]
            yT = [kb.sb(f"t_yT{i}", [128, 8, 128], BF16, st3) for i in range(2)]
            for tt in range(NT):
                p = pt[tt % 2]
                o = yT[tt % 2]
                for j in range(8):
                    kb.op("pe", lambda e: e.transpose(out=p[:, j * 128:(j + 1) * 128], in_=yatt[:, tt, j * 128:(j + 1) * 128],
                                                      identity=C.identb[:]), reads=[yatt, C.identb], writes=[p])
                kb.op("act", lambda e: e.copy(out=o[:], in_=p[:].rearrange("p (j t) -> p j t", j=8)), reads=[p], writes=[o])
                kb.dma("sp", D.ymT_d.ap()[8:16, :, tt * 128:(tt + 1) * 128].rearrange("j p t -> p j t"), o[:], reads=[o], writes=[D.ymT_d])
            kb.barrier()


def phase_O(kb, C, D, ym_src):
    with contextlib.ExitStack() as st:
        ymT = kb.sb("o_ymT", [128, 16, 2048], BF16, st)
        for j in range(16):
            kb.dma("sp" if j % 2 == 0 else "act", ymT[:, j, :], ym_src.ap()[j], reads=[ym_src], writes=[ymT])
        stg = [kb.sb(f"o_stg{i}", [128, 4096], F32, st) for i in range(2)]
        wo = kb.sb("o_wo", [128, 16, 512], BF16, st)
        xs = [kb.sb(f"o_xs{i}", [128, 512], F32, st) for i in range(2)]
        pss = [kb.ps(f"o_ps{i}", [128, 512], F32, st) for i in range(2)]
        w_v = D.w_out.ap().rearrange("(kc p) c -> p kc c", p=128)
        for dg in range(4):
            for hf in range(2):
                sg = stg[hf]
                sgv = sg[:].rearrange("p (dc c) -> p dc c", dc=8)
                kb.dma("sp" if hf == 0 else "act", sgv, w_v[:, hf * 8:(hf + 1) * 8, dg * 512:(dg + 1) * 512], writes=[sg])
                kb.op("pool", lambda e: e.tensor_copy(out=wo[:, hf * 8:(hf + 1) * 8, :], in_=sgv), reads=[sg], writes=[wo])
            for tt in range(NT):
                ps = pss[tt % 2]
                xt = xs[tt % 2]
                kb.dma("sp", xt[:], D.x.ap()[tt * 128:(tt + 1) * 128, dg * 512:(dg + 1) * 512], writes=[xt])
                for kc in range(16):
                    kb.op("pe", lambda e: e.matmul(ps[:], lhsT=ymT[:, kc, tt * 128:(tt + 1) * 128], rhs=wo[:, kc, :],
                                                   start=(kc == 0), stop=(kc == 15)), reads=[ymT, wo], writes=[ps])
                kb.op("dve", lambda e: e.tensor_tensor(out=xt[:], in0=ps[:], in1=xt[:], op=ALU.add), reads=[ps, xt], writes=[xt])
                kb.dma("act", D.x1_d.ap()[tt * 128:(tt + 1) * 128, dg * 512:(dg + 1) * 512], xt[:], reads=[xt], writes=[D.x1_d])
        kb.barrier()


def phase_P(kb, C, D):
    with contextlib.ExitStack() as stP:
        iota = kb.sb("p_iota", [128, 128], F32, stP)
        kb.dma("sp", iota[:], D.iota.ap(), writes=[iota])
        with contextlib.ExitStack() as st, kb.nc.named_scope("P1"):
            h2T = kb.sb("p_h2T", [128, 16, 2048], BF16, st)
            norm_to_T(kb, C, D.x1_d, D.norm2_w, h2T, "n2")
            for dc in range(16):
                kb.dma("sp" if dc % 2 == 0 else "act", D.h2T_d.ap()[dc], h2T[:, dc, :], reads=[h2T], writes=[D.h2T_d])
            skb = kb.sb("p_sk", [128, 16, 128], F32, st)
            kb.dma("sp", skb[:], D.skT.ap(), writes=[skb])
            stg = [kb.sb(f"p_stg{i}", [128, 16, 128], F32, st) for i in range(2)]
            wb = [kb.sb(f"p_wb{i}", [128, 16, 128], BF16, st) for i in range(2)]
            qTs = [kb.sb(f"p_qT{i}", [128, 2048], F32, st) for i in range(2)]
            sev = [kb.sb(f"p_sev{i}", [128, 16, 128], F32, st) for i in range(2)]
            pss = [[kb.ps(f"p_ps{i}_{j}", [128, 512], F32, st) for j in range(3)] for i in range(2)]
            w_v = D.w_pq.ap().rearrange("(dc p) c -> p dc c", p=128)
            for hp in range(16):
                sg = stg[hp % 2]
                kb.dma("sp" if hp % 2 == 0 else "act", sg[:], w_v[:, :, hp * 128:(hp + 1) * 128], writes=[sg])
                w = wb[hp % 2]
                kb.op("pool", lambda e: e.tensor_copy(out=w[:], in_=sg[:]), reads=[sg], writes=[w])
                qT = qTs[hp % 2]
                for tb in range(4):
                    ps = pss[tb % 2][0]
                    for dc in range(16):
                        kb.op("pe", lambda e: e.matmul(ps[:], lhsT=w[:, dc, :], rhs=h2T[:, dc, tb * 512:(tb + 1) * 512],
                                                       start=(dc == 0), stop=(dc == 15)), reads=[w, h2T], writes=[ps])
                    kb.op("act", lambda e: e.copy(out=qT[:, tb * 512:(tb + 1) * 512], in_=ps[:]), reads=[ps], writes=[qT])
                se = sev[hp % 2]
                for g in range(4):
                    ps = pss[g % 2][1 + (g // 2) % 2]
                    for j in range(4):
                        tt = g * 4 + j
                        kb.op("pe", lambda e: e.matmul(ps[:, j * 128:(j + 1) * 128], lhsT=qT[:, tt * 128:(tt + 1) * 128], rhs=skb[:, hp, :],
                                                       start=True, stop=True), reads=[qT, skb], writes=[ps])
                    kb.op("dve", lambda e: e.tensor_copy(out=se[:, g * 4:(g + 1) * 4, :], in_=ps[:].rearrange("p (j n) -> p j n", j=4)),
                          reads=[ps], writes=[se])
                kb.dma("pool", D.S_d.ap()[:, hp, :].rearrange("(tt p) n -> p tt n", p=128), se[:], reads=[se], writes=[D.S_d])
            kb.barrier()
        with contextlib.ExitStack() as st, kb.nc.named_scope("P23"):
            ETg = [kb.sb(f"p_ETg{g}", [128, 3, 256], F32, st) for g in range(8)]
            Ss = [kb.sb(f"p_S{i}", [128, 16, 128], F32, st) for i in range(2)]
            S2 = kb.sb("p_S2", [128, 128], F32, st)
            M16 = kb.sb("p_M16", [128, 16, 16], F32, st)
            I16u = kb.sb("p_I16u", [128, 16, 16], U32, st)
            I16f = kb.sb("p_I16f", [128, 16, 16], F32, st)
            cand = kb.sb("p_cand", [128, 8, 16, 16], F32, st)
            cand2 = kb.sb("p_cand2", [128, 256], F32, st)
            C16 = kb.sb("p_C16", [128, 8, 16], F32, st)
            CIu = kb.sb("p_CIu", [128, 8, 16], U32, st)
            IJu = kb.sb("p_IJu", [128, 2, 8, 16], U32, st)
            IJf = kb.sb("p_IJf", [128, 2, 8, 16], F32, st)
            ex = kb.sb("p_ex", [128, 8, 16], F32, st)
            Z = kb.sb("p_Z", [128, 8], F32, st)
            EG = kb.sb("p_EG", [128, 3, 8, 16], F32, st)
            eq = kb.sb("p_eq", [128, 8, 16, 16], F32, st)
            pt = kb.ps("p_pt", [128, 512], F32, st)
            Gs = kb.sb("p_Gs", [128, 128, 256], BF16, st)
            O1 = [kb.sb(f"p_O1{i}", [128, 32, 128], BF16, st) for i in range(2)]
            O2 = [kb.sb(f"p_O2{i}", [128, 32, 128], BF16, st) for i in range(2)]
            pg = [kb.ps(f"p_pg{i}", [128, 512], F32, st) for i in range(4)]
            def p2_tile(tt):
                S = Ss[tt % 2]
                kb.dma("sp", S[:].rearrange("p a n -> p (a n)"), D.S_d.ap()[tt * 128:(tt + 1) * 128].rearrange("p a n -> p (a n)"),
                       reads=[D.S_d], writes=[S])
                for hp in range(16):
                    kb.op("dve", lambda e: e.max(out=M16[:, hp, 0:8], in_=S[:, hp, :]), reads=[S], writes=[M16])
                    kb.op("dve", lambda e: e.max_index(out=I16u[:, hp, 0:8], in_max=M16[:, hp, 0:8], in_values=S[:, hp, :]),
                          reads=[S, M16], writes=[I16u])
                    kb.op("dve", lambda e: e.match_replace(out=S2[:], in_to_replace=M16[:, hp, 0:8], in_values=S[:, hp, :], imm_value=-1e30),
                          reads=[S, M16], writes=[S2])
                    kb.op("dve", lambda e: e.max(out=M16[:, hp, 8:16], in_=S2[:]), reads=[S2], writes=[M16])
                    kb.op("dve", lambda e: e.max_index(out=I16u[:, hp, 8:16], in_max=M16[:, hp, 8:16], in_values=S2[:]),
                          reads=[S2, M16], writes=[I16u])
                kb.op("pool", lambda e: e.tensor_copy(out=I16f[:], in_=I16u[:]), reads=[I16u], writes=[I16f])
                M4 = M16[:].rearrange("p (h q) k -> p h q k", q=2)
                I4 = I16f[:].rearrange("p (h q) k -> p h q k", q=2)
                kb.op("pool", lambda e: e.tensor_tensor(out=cand[:], in0=M4[:, :, 0, :, None].to_broadcast([128, 8, 16, 16]),
                                                        in1=M4[:, :, 1, None, :].to_broadcast([128, 8, 16, 16]), op=ALU.add),
                      reads=[M16], writes=[cand])
                for h in range(8):
                    ch = cand[:, h, :, :].rearrange("p a b -> p (a b)")
                    kb.op("dve", lambda e: e.max(out=C16[:, h, 0:8], in_=ch), reads=[cand], writes=[C16])
                    kb.op("dve", lambda e: e.max_index(out=CIu[:, h, 0:8], in_max=C16[:, h, 0:8], in_values=ch), reads=[cand, C16], writes=[CIu])
                    kb.op("dve", lambda e: e.match_replace(out=cand2[:], in_to_replace=C16[:, h, 0:8], in_values=ch, imm_value=-1e30),
                          reads=[cand, C16], writes=[cand2])
                    kb.op("dve", lambda e: e.max(out=C16[:, h, 8:16], in_=cand2[:]), reads=[cand2], writes=[C16])
                    kb.op("dve", lambda e: e.max_index(out=CIu[:, h, 8:16], in_max=C16[:, h, 8:16], in_values=cand2[:]),
                          reads=[cand2, C16], writes=[CIu])
                kb.op("pool", lambda e: e.tensor_tensor(out=ex[:], in0=C16[:], in1=C16[:, :, 0:1].to_broadcast([128, 8, 16]), op=ALU.subtract),
                      reads=[C16], writes=[ex])
                kb.op("act", lambda e: e.activation(out=ex[:], in_=ex[:], func=AF.Exp), reads=[ex], writes=[ex])
                kb.op("dve", lambda e: e.tensor_reduce(out=Z[:], in_=ex[:], axis=AX.X, op=ALU.add), reads=[ex], writes=[Z])
                kb.op("dve", lambda e: e.reciprocal(out=Z[:], in_=Z[:]), reads=[Z], writes=[Z])
                kb.op("dve", lambda e: e.tensor_tensor(out=EG[:, 2], in0=ex[:], in1=Z[:, :, None].to_broadcast([128, 8, 16]), op=ALU.mult),
                      reads=[ex, Z], writes=[EG])
                kb.op("dve", lambda e: e.tensor_single_scalar(out=IJu[:, 0], in_=CIu[:], scalar=4, op=ALU.logical_shift_right),
                      reads=[CIu], writes=[IJu])
                kb.op("dve", lambda e: e.tensor_single_scalar(out=IJu[:, 1], in_=CIu[:], scalar=15, op=ALU.bitwise_and),
                      reads=[CIu], writes=[IJu])
                kb.op("pool", lambda e: e.tensor_copy(out=IJf[:], in_=IJu[:]), reads=[IJu], writes=[IJf])
                for q in range(2):
                    kb.op("dve", lambda e: e.tensor_tensor(out=eq[:], in0=iota[:, None, None, 0:16].to_broadcast([128, 8, 16, 16]),
                                                            in1=IJf[:, q, :, :, None].to_broadcast([128, 8, 16, 16]), op=ALU.is_equal),
                          reads=[iota, IJf], writes=[eq])
                    kb.op("pool", lambda e: e.tensor_tensor(out=eq[:], in0=eq[:], in1=I4[:, :, q, None, :].to_broadcast([128, 8, 16, 16]),
                                                            op=ALU.mult), reads=[eq, I16f], writes=[eq])
                    kb.op("dve", lambda e: e.tensor_reduce(out=EG[:, q], in_=eq[:], axis=AX.X, op=ALU.add), reads=[eq], writes=[EG])
                for a in range(3):
                    kb.op("pe", lambda e: e.transpose(out=pt[:, a * 128:(a + 1) * 128], in_=EG[:, a].rearrange("p h k -> p (h k)"),
                                                      identity=C.ident[:]), reads=[EG, C.ident], writes=[pt])
                kb.op("act", lambda e: e.copy(out=ETg[tt // 2][:, :, (tt % 2) * 128:(tt % 2 + 1) * 128], in_=pt[:, 0:384].rearrange("p (a t) -> p a t", a=3)),
                      reads=[pt], writes=[ETg[tt // 2]])

            itc = [0]

            def p3_group(tg):
                for sub in range(8):
                    t0 = sub * 32
                    ET = ETg[tg]
                    o1 = O1[sub % 2]
                    o2 = O2[sub % 2]
                    iob = iota[:, None, :].to_broadcast([128, 32, 128])
                    kb.op("dve", lambda e: e.tensor_tensor(out=o1[:], in0=iob, in1=ET[:, 0, t0:t0 + 32, None].to_broadcast([128, 32, 128]),
                                                            op=ALU.is_equal), reads=[iota, ETg[tg]], writes=[o1])
                    kb.op("dve", lambda e: e.tensor_tensor(out=o2[:], in0=iob, in1=ET[:, 1, t0:t0 + 32, None].to_broadcast([128, 32, 128]),
                                                           op=ALU.is_equal), reads=[iota, ETg[tg]], writes=[o2])
                    kb.op("pool", lambda e: e.tensor_tensor(out=o2[:], in0=o2[:], in1=ET[:, 2, t0:t0 + 32, None].to_broadcast([128, 32, 128]),
                                                            op=ALU.mult), reads=[o2, ETg[tg]], writes=[o2])
                    for q4 in range(8):
                        p = pg[itc[0] % 4]
                        itc[0] += 1
                        for j in range(4):
                            tl = q4 * 4 + j
                            kb.op("pe", lambda e: e.matmul(p[:].rearrange("p (e t) -> p e t", t=4)[:, :, j], lhsT=o2[:, tl, :], rhs=o1[:, tl, :],
                                                           start=True, stop=True), reads=[o1, o2], writes=[p])
                        tl0 = sub * 32 + q4 * 4
                        dst = Gs[:, :, tl0:tl0 + 4]
                        src = p[:].rearrange("p (e t) -> p e t", t=4)
                        kb.op("act", lambda e: e.copy(out=dst, in_=src), reads=[p], writes=[Gs])
                for k8 in range(8):
                    kb.dma(["sp", "act", "pool"][k8 % 3],
                           D.G_d.ap()[k8 * 16:(k8 + 1) * 16, :, tg * 256:(tg + 1) * 256].rearrange("e1 e2 t -> e2 e1 t"),
                           Gs[:, k8 * 16:(k8 + 1) * 16, :], reads=[Gs], writes=[D.G_d])

            for g in range(8):
                p2_tile(2 * g)
                p2_tile(2 * g + 1)
                if g >= 1:
                    p3_group(g - 1)
            p3_group(7)
            kb.barrier()
        with contextlib.ExitStack() as st, kb.nc.named_scope("P4"):
            h2T = kb.sb("p_h2Tb", [128, 16, 2048], BF16, st)
            for dc in range(16):
                kb.dma("sp" if dc % 2 == 0 else "act", h2T[:, dc, :], D.h2T_d.ap()[dc], reads=[D.h2T_d], writes=[h2T])
            stg = [kb.sb(f"p4_stg{i}", [128, 16, 128], F32, st) for i in range(2)]
            ub = [kb.sb(f"p4_ub{i}", [128, 16, 128], BF16, st) for i in range(2)]
            Gc = [kb.sb(f"p4_Gc{i}", [128, 2048], BF16, st) for i in range(2)]
            ge = [kb.sb(f"p4_ge{i}", [128, 2048], BF16, st) for i in range(2)]
            pss = [[kb.ps(f"p4_ps{i}_{j}", [128, 512], F32, st) for j in range(4)] for i in range(2)]
            def load(e1):
                sg = stg[e1 % 2]
                kb.dma("sp", sg[:].rearrange("p a b -> p (a b)"), D.u_tabT.ap()[e1], writes=[sg])
                g = Gc[e1 % 2]
                kb.dma("pool", g[:], D.G_d.ap()[e1], reads=[D.G_d], writes=[g])
            load(0)
            for e1 in range(128):
                if e1 + 1 < 128:
                    load(e1 + 1)
                sg = stg[e1 % 2]
                u = ub[e1 % 2]
                kb.op("dve", lambda e: e.tensor_copy(out=u[:], in_=sg[:]), reads=[sg], writes=[u])
                g = Gc[e1 % 2]
                ps = pss[e1 % 2]
                a = ge[e1 % 2]
                for tb in range(4):
                    for dc in range(16):
                        kb.op("pe", lambda e: e.matmul(ps[tb][:], lhsT=u[:, dc, :], rhs=h2T[:, dc, tb * 512:(tb + 1) * 512],
                                                       start=(dc == 0), stop=(dc == 15)), reads=[u, h2T], writes=[ps[tb]])
                    kb.op("act", lambda e: e.activation(out=a[:, tb * 512:(tb + 1) * 512], in_=ps[tb][:], func=AF.Gelu), reads=[ps[tb]], writes=[a])
                kb.op("pool", lambda e: e.tensor_tensor(out=a[:], in0=a[:], in1=g[:], op=ALU.mult), reads=[a, g], writes=[a])
                kb.dma("sp", D.A_d.ap()[e1], a[:], reads=[a], writes=[D.A_d])
            kb.barrier()
        with contextlib.ExitStack() as st, kb.nc.named_scope("P5"):
            NB = 4
            vb = [kb.sb(f"p5_vb{i}", [128, 512], BF16, st) for i in range(NB)]
            ac = [kb.sb(f"p5_ac{i}", [128, 1024], BF16, st) for i in range(NB)]
            ov = [kb.sb(f"p5_ov{i}", [128, 512], F32, st) for i in range(2)]
            pss = [[kb.ps(f"p5_ps{j}_{h}", [128, 512], F32, st) for h in range(2)] for j in range(4)]
            seq = [(tb2, dg, e1) for tb2 in range(2) for dg in range(4) for e1 in range(128)]

            def load(i):
                tb2, dg, e1 = seq[i]
                k = i % NB
                kb.dma("sp", vb[k][:], D.vb_d.ap()[e1 * 128:(e1 + 1) * 128, dg * 512:(dg + 1) * 512], reads=[D.vb_d], writes=[vb[k]])
                kb.dma("pool", ac[k][:], D.A_d.ap()[e1][:, tb2 * 1024:(tb2 + 1) * 1024], reads=[D.A_d], writes=[ac[k]])
            load(0)
            load(1)
            load(2)
            oi = 0
            for i, (tb2, dg, e1) in enumerate(seq):
                if i + 3 < len(seq):
                    load(i + 3)
                k = i % NB
                for j in range(4):
                    for h in range(2):
                        kb.op("pe", lambda e: e.matmul(pss[j][h][:], lhsT=vb[k][:, j * 128:(j + 1) * 128], rhs=ac[k][:, h * 512:(h + 1) * 512],
                                                       start=(e1 == 0), stop=(e1 == 127)), reads=[vb[k], ac[k]], writes=[pss[j][h]])
                if e1 == 127:
                    for j in range(4):
                        for h in range(2):
                            o = ov[oi % 2]
                            if oi % 2 == 0:
                                kb.op("act", lambda e: e.copy(out=o[:], in_=pss[j][h][:]), reads=[pss[j][h]], writes=[o])
                            else:
                                kb.op("dve", lambda e: e.tensor_copy(out=o[:], in_=pss[j][h][:]), reads=[pss[j][h]], writes=[o])
                            oi += 1
                            d0 = dg * 512 + j * 128
                            t0 = tb2 * 1024 + h * 512
                            kb.dma("sp", D.peT_d.ap()[d0:d0 + 128, t0:t0 + 512], o[:], reads=[o], writes=[D.peT_d])
            kb.barrier()


def phase_F(kb, C, D):
    with contextlib.ExitStack() as st:
        wb = kb.sb("f_wb", [128, 2048], F32, st)
        kb.dma("pool", wb[:], D.normf_w.ap().partition_broadcast(128), writes=[wb])
        xs = [kb.sb(f"f_x{i}", [128, 2048], F32, st) for i in range(2)]
        pes = [kb.sb(f"f_pe{i}", [128, 16, 128], F32, st) for i in range(2)]
        junk = kb.sb("f_junk", [128, 2048], BF16, st)
        ss = kb.sb("f_ss", [128, NT], F32, st)
        rs = kb.sb("f_rs", [128, NT], F32, st)
        kb.op("pool", lambda e: e.memset(ss[:], 0.0), writes=[ss])
        pss = [[kb.ps(f"f_ps{i}_{j}", [128, 512], F32, st) for j in range(4)] for i in range(2)]
        pe_v = D.peT_d.ap().rearrange("(dc p) t -> p dc t", p=128)
        for tt in range(NT):
            xt = xs[tt % 2]
            pe = pes[tt % 2]
            ps = pss[tt % 2]
            kb.dma("sp", xt[:], D.x1_d.ap()[tt * 128:(tt + 1) * 128, :], reads=[D.x1_d], writes=[xt])
            kb.dma("act", pe[:], pe_v[:, :, tt * 128:(tt + 1) * 128], reads=[D.peT_d], writes=[pe])
            for dc in range(16):
                kb.op("pe", lambda e: e.transpose(out=ps[dc // 4][:, (dc % 4) * 128:(dc % 4 + 1) * 128], in_=pe[:, dc, :], identity=C.ident[:]),
                      reads=[pe, C.ident], writes=[ps[dc // 4]])
            for j in range(4):
                kb.op("dve", lambda e: e.tensor_tensor(out=xt[:, j * 512:(j + 1) * 512], in0=ps[j][:], in1=xt[:, j * 512:(j + 1) * 512], op=ALU.add),
                      reads=[ps[j], xt], writes=[xt])
            kb.op("act", lambda e: e.activation(out=junk[:], in_=xt[:], func=AF.Square, accum_out=ss[:, tt:tt + 1]), reads=[xt], writes=[junk, ss])
            kb.op("dve", lambda e: e.tensor_scalar(out=rs[:, tt:tt + 1], in0=ss[:, tt:tt + 1], scalar1=1.0 / 2048, scalar2=EPS,
                                                   op0=ALU.mult, op1=ALU.add), reads=[ss], writes=[rs])
            kb.op("act", lambda e: e.activation(out=rs[:, tt:tt + 1], in_=rs[:, tt:tt + 1], func=AF.Sqrt), reads=[rs], writes=[rs])
            kb.op("dve", lambda e: e.reciprocal(out=rs[:, tt:tt + 1], in_=rs[:, tt:tt + 1]), reads=[rs], writes=[rs])
            kb.op("dve", lambda e: e.scalar_tensor_tensor(out=xt[:], in0=xt[:], scalar=rs[:, tt:tt + 1], in1=wb[:], op0=ALU.mult, op1=ALU.mult),
                  reads=[xt, rs, wb], writes=[xt])
            kb.dma("sp", D.out.ap()[tt * 128:(tt + 1) * 128, :], xt[:], reads=[xt], writes=[D.out])
        kb.barrier()


STOP = 0
FAST_F32 = True
F32R = mybir.dt.float32r
def fr(ap):
    return ap.bitcast(F32R) if FAST_F32 else ap
class StopBuild(Exception):
    pass
def chk(n):
    if STOP == n:
        raise StopBuild()
GN_EPS = 64e-5
NEG_E05 = -0.6065306597126334


def phase_Rpre(kb, C, D):
    with contextlib.ExitStack() as st:
        rw = kb.sb("rp_rw", [128, 8, 8], F32, st)
        kb.dma("sp", rw[:], D.rw_c.ap(), writes=[rw])
        tmp = kb.sb("rp_tmp", [128, 2048], F32, st)
        lin = [kb.sb(f"rp_lin{i}", [128, 2048], BF16, st) for i in range(4)]
        for i in range(4):
            kb.op("pool", lambda e: e.memset(lin[i][:], 0.0), writes=[lin[i]])
            r0 = 3072 + i * 96
            kb.dma("sp", tmp[0:96, :], D.zs_d.ap()[r0:r0 + 96, :], reads=[D.zs_d], writes=[tmp])
            if i < 2:
                kb.op("act", lambda e: e.activation(out=lin[i][0:96, :], in_=tmp[0:96, :], func=AF.Tanh), reads=[tmp], writes=[lin[i]])
            else:
                kb.op("act", lambda e: e.copy(out=lin[i][0:96, :], in_=tmp[0:96, :]), reads=[tmp], writes=[lin[i]])
        sgl = kb.sb("rp_sgl", [128, 2, 2048], BF16, st)
        for kc in range(2):
            kb.dma("sp", tmp[:], D.zs_d.ap()[3456 + kc * 128:3456 + (kc + 1) * 128, :], reads=[D.zs_d], writes=[tmp])
            kb.op("act", lambda e: e.activation(out=sgl[:, kc, :], in_=tmp[:], func=AF.Sigmoid), reads=[tmp], writes=[sgl])
        wst = kb.sb("rp_wst", [128, 2048], F32, st)
        w2b = kb.sb("rp_w2b", [128, 2, 1024], BF16, st)
        a2b = kb.sb("rp_a2b", [128, 2, 1024], BF16, st)
        g2b = kb.sb("rp_g2b", [128, 2, 1024], BF16, st)
        wv = wst[:].rearrange("p (a c) -> p a c", a=2)
        kb.op("pool", lambda e: e.memset(w2b[:], 0.0), writes=[w2b])
        kb.op("pool", lambda e: e.memset(a2b[:], 0.0), writes=[a2b])
        kb.dma("sp", wv[0:96], D.w2.ap().rearrange("d l c -> l d c"), writes=[wst])
        kb.op("pool", lambda e: e.tensor_copy(out=w2b[0:96], in_=wv[0:96]), reads=[wst], writes=[w2b])
        kb.dma("sp", wv[0:96], D.a2.ap().rearrange("d l c -> l d c"), writes=[wst])
        kb.op("pool", lambda e: e.tensor_copy(out=a2b[0:96], in_=wv[0:96]), reads=[wst], writes=[a2b])
        kb.dma("sp", wv, D.g2.ap().rearrange("(kc p) c -> p kc c", p=128), writes=[wst])
        kb.op("pool", lambda e: e.tensor_copy(out=g2b[:], in_=wv), reads=[wst], writes=[g2b])
        outs = [kb.sb(f"rp_o{i}", [128, 2048], F32, st) for i in range(2)]
        pss = [[kb.ps(f"rp_ps{i}_{j}", [128, 512], F32, st) for j in range(4)] for i in range(2)]
        it = 0
        for cc in range(8):
            for d in range(2):
                for which in range(2):
                    ps = pss[it % 2]
                    o = outs[it % 2]
                    it += 1
                    wmat = w2b if which == 0 else a2b
                    xin = lin[d] if which == 0 else lin[2 + d]
                    bias = rw[:, cc, d:d + 1] if which == 0 else rw[:, cc, 2 + d:3 + d]
                    for tb in range(4):
                        kb.op("pe", lambda e: e.matmul(ps[tb][:], lhsT=wmat[:, d, cc * 128:(cc + 1) * 128], rhs=xin[:, tb * 512:(tb + 1) * 512],
                                                       start=True, stop=True), reads=[wmat, xin], writes=[ps[tb]])
                        kb.op("act", lambda e: e.activation(out=o[:, tb * 512:(tb + 1) * 512], in_=ps[tb][:], func=AF.Sigmoid, bias=bias),
                              reads=[ps[tb], rw], writes=[o])
                    if which == 0:
                        kb.op("dve", lambda e: e.tensor_scalar(out=o[:], in0=o[:], scalar1=NEG_E05, scalar2=None, op0=ALU.mult), reads=[o], writes=[o])
                        kb.dma("sp", D.ld_d.ap()[d, cc * 128:(cc + 1) * 128, :], o[:], reads=[o], writes=[D.ld_d])
                    else:
                        kb.dma("sp", D.a_d.ap()[d, cc * 128:(cc + 1) * 128, :], o[:], reads=[o], writes=[D.a_d])
        for tt in range(NT):
            ps = pss[tt % 2]
            o = outs[tt % 2]
            for hf in range(2):
                for kc in range(2):
                    kb.op("pe", lambda e: e.matmul(ps[hf][:], lhsT=sgl[:, kc, tt * 128:(tt + 1) * 128], rhs=g2b[:, kc, hf * 512:(hf + 1) * 512],
                                                   start=(kc == 0), stop=(kc == 1)), reads=[sgl, g2b], writes=[ps[hf]])
                kb.op("act", lambda e: e.copy(out=o[:, hf * 512:(hf + 1) * 512], in_=ps[hf][:]), reads=[ps[hf]], writes=[o])
            kb.dma("sp", D.g_d.ap()[tt * 128:(tt + 1) * 128, :], o[:, 0:1024], reads=[o], writes=[D.g_d])
        kb.barrier()


def phase_R(kb, C, D, ccs=range(8), dbg=None):
    with contextlib.ExitStack() as st:
        rw = kb.sb("r_rw", [128, 8, 8], F32, st)
        kb.dma("sp", rw[:], D.rw_c.ap(), writes=[rw])
        masks = kb.sb("r_masks", [128, 6, 128], F32, st)
        kb.dma("sp", masks[:], D.masks.ap(), writes=[masks])
        cst = kb.sb("r_cst", [128, 66], F32, st)
        kb.dma("sp", cst[:], D.rcst.ap()[:, 0:66], writes=[cst])
        segt = kb.sb("r_segm", [128, 2048], BF16, st)
        kb.dma("sp", segt[:], D.segm.ap(), writes=[segt])
        ident2 = cst[:, 0:64]
        sel = cst[:, 64:66]
        lnw = kb.sb("r_lnw", [128, 128], F32, st)
        lnb = kb.sb("r_lnb", [128, 128], F32, st)
        Rr = kb.sb("r_R", [128, 2048], F32, st)
        Kk = kb.sb("r_K", [128, 2048], F32, st)
        Vv = kb.sb("r_V", [128, 2048], F32, st)
        KKn = kb.sb("r_KK", [128, 2048], F32, st)
        E1 = kb.sb("r_E1", [128, 2048], F32, st)
        XI = kb.sb("r_XI", [128, 2048], F32, st)
        XE = kb.sb("r_XE", [128, 2048], F32, st)
        Aa = kb.sb("r_A", [128, 2048], F32, st)
        KD = kb.sb("r_KD", [128, 2048], F32, st)
        KT = kb.sb("r_KT", [128, 2048], F32, st)
        AT = kb.sb("r_AT", [128, 2048], F32, st)
        BTb = kb.sb("r_BTb", [128, 2048], F32, st)
        stt = kb.sb("r_stt", [128, 160], F32, st)
        Vtm = kb.sb("r_Vtm", [128, NT, 128], F32, st)
        Ysum = kb.sb("r_Ysum", [128, NT, 128], F32, st)
        MTa = kb.sb("r_MTa", [128, 32, 64], F32, st)
        Ca = kb.sb("r_Ca", [128, 32, 64], F32, st)
        H = [kb.sb(f"r_H{i}", [128, 64], F32, st) for i in range(2)]
        tot = kb.sb("r_tot", [128, 32], F32, st)
        GL = kb.sb("r_GL", [128, 32], F32, st)
        RhT = Vv

        class Set:
            pass
        sets = []
        for p in range(2):
            S = Set()
            S.XA = kb.sb(f"r_XA{p}", [128, 2, 2, 128], F32, st)
            S.KBm = kb.sb(f"r_KBm{p}", [128, 2, 2, 128], F32, st)
            S.PQ = [kb.sb(f"r_PQ{p}_{i}", [128, 2, 2, 128], F32, st) for i in range(2)]
            S.QT = [kb.sb(f"r_QT{p}_{i}", [128, 2, 128], F32, st) for i in range(2)]
            S.BW = kb.sb(f"r_BW{p}", [128, 2, 128], F32, st)
            S.BU = kb.sb(f"r_BU{p}", [128, 2, 128], F32, st)
            S.AGtm = kb.sb(f"r_AGtm{p}", [128, 128], F32, st)
            S.KGtm = kb.sb(f"r_KGtm{p}", [128, 128], F32, st)
            S.B = [kb.ps(f"r_bk{p}_{i}", [128, 512], F32, st) for i in range(4)]
            sets.append(S)
        psX = sets[0].B[0]
        bkH = [sets[0].B[1], sets[0].B[2]]

        v4 = lambda b: b[:].rearrange("p (h q s) -> p h q s", h=2, q=2)
        v3 = lambda b, lo: b[:, lo:lo + 256].rearrange("p (h s) -> p h s", h=2)

        def tile_gen(S, d, tt, RT, BT):
            M2 = masks[:, 2 * d:2 * d + 2, :]
            MST = masks[:, 2 - 2 * d, :]
            XA, KBm, PQ, QT, BW, BU, AGtm, KGtm = S.XA, S.KBm, S.PQ, S.QT, S.BW, S.BU, S.AGtm, S.KGtm
            B0, B1, B2, B3 = S.B
            psA, psB, psN = v4(B0), v4(B1), v4(B3)
            psC = v3(B2, 0)
            cols = slice(tt * 128, (tt + 1) * 128)
            for hh in range(2):
                pr = slice(hh * 64, hh * 64 + 64)
                kb.pe_fence()
                kb.op("pe", lambda e: e.matmul(psA[:, hh, 0, :], lhsT=fr(AT[pr, cols]), rhs=fr(BT[pr, cols]), start=True, stop=True),
                      reads=[AT, BT], writes=[B0])
                kb.op("pe", lambda e: e.matmul(psA[:, hh, 1, :], lhsT=fr(AT[pr, cols]), rhs=fr(RT[pr, cols]), start=True, stop=True),
                      reads=[AT, RT], writes=[B0])
                kb.op("pe", lambda e: e.matmul(psB[:, hh, 0, :], lhsT=fr(KT[pr, cols]), rhs=fr(BT[pr, cols]), start=True, stop=True),
                      reads=[KT, BT], writes=[B1])
                kb.op("pe", lambda e: e.matmul(psB[:, hh, 1, :], lhsT=fr(KT[pr, cols]), rhs=fr(RT[pr, cols]), start=True, stop=True),
                      reads=[KT, RT], writes=[B1])
                kb.op("pe", lambda e: e.matmul(psC[:, hh, :], lhsT=fr(BT[pr, cols]), rhs=fr(AT[pr, cols]), start=True, stop=True),
                      reads=[AT, BT], writes=[B2])
            kb.pe_fence()
            yield
            M2b = M2[:, None, :, :].to_broadcast([128, 2, 2, 128])
            q0, q1 = QT[0], QT[1]
            kb.op("dve", lambda e: e.tensor_tensor(out=fr(XA[:]), in0=psA, in1=M2b, op=ALU.mult), reads=[B0, masks], writes=[XA])
            kb.op("dve", lambda e: e.tensor_tensor(out=fr(q0[:]), in0=psC, in1=MST[:, None, :].to_broadcast([128, 2, 128]), op=ALU.mult),
                  reads=[B2, masks], writes=[q0])
            kb.op("dve", lambda e: e.tensor_tensor(out=fr(KBm[:]), in0=psB, in1=M2b, op=ALU.mult), reads=[B1, masks], writes=[KBm])
            pq = PQ[0]
            kb.op("pool", lambda e: e.tensor_tensor(out=fr(pq[:, :, 0, :]), in0=XA[:, :, 0, :], in1=C.ident[:, None, :].to_broadcast([128, 2, 128]),
                                                    op=ALU.add), reads=[XA, C.ident], writes=[pq])
            yield
            for hh in range(2):
                kb.op("pe", lambda e: e.matmul(psN[:, hh, 1, :], lhsT=fr(q0[:, hh, :]), rhs=fr(XA[:, hh, 0, :]), start=True, stop=True),
                      reads=[q0, XA], writes=[B3])
                kb.op("pe", lambda e: e.matmul(psC[:, hh, :], lhsT=fr(XA[:, hh, 0, :]), rhs=fr(q0[:, hh, :]), start=True, stop=True),
                      reads=[q0, XA], writes=[B2])
            yield
            kb.op("act", lambda e: e.copy(out=fr(pq[:, :, 1, :]), in_=psN[:, :, 1, :]), reads=[B3], writes=[pq])
            kb.op("dve", lambda e: e.tensor_copy(out=fr(q1[:]), in_=psC), reads=[B2], writes=[q1])
            yield
            cur = 0
            qcur = 1
            for lev in range(1, 6):
                pq = PQ[cur]
                pqn = PQ[1 - cur]
                qt = QT[qcur]
                qtn = QT[1 - qcur]
                last = (lev == 5)
                for hh in range(2):
                    if last:
                        kb.op("pe", lambda e: e.matmul(psN[:, hh, 0, :], lhsT=fr(qt[:, hh, :]), rhs=fr(pq[:, hh, 0, :]), start=True, stop=True),
                              reads=[qt, pq], writes=[B3])
                    else:
                        kb.op("pe", lambda e: e.matmul(psN[:, hh, :, :], lhsT=fr(qt[:, hh, :]), rhs=fr(pq[:, hh, :, :]), start=True, stop=True),
                              reads=[qt, pq], writes=[B3])
                        kb.op("pe", lambda e: e.matmul(psC[:, hh, :], lhsT=fr(pq[:, hh, 1, :]), rhs=fr(qt[:, hh, :]), start=True, stop=True),
                              reads=[qt, pq], writes=[B2])
                yield
                kb.op("dve", lambda e: e.tensor_tensor(out=fr(pqn[:, :, 0, :]), in0=psN[:, :, 0, :], in1=pq[:, :, 0, :], op=ALU.add),
                      reads=[B3, pq], writes=[pqn])
                if not last:
                    kb.op("act", lambda e: e.copy(out=fr(pqn[:, :, 1, :]), in_=psN[:, :, 1, :]), reads=[B3], writes=[pqn])
                    kb.op("dve", lambda e: e.tensor_copy(out=fr(qtn[:]), in_=psC), reads=[B2], writes=[qtn])
                yield
                cur = 1 - cur
                qcur = 1 - qcur
            TT = PQ[cur]
            W_ = B0[:, 0:128].rearrange("p (h i) -> p h i", h=2)
            Bt_ = B0[:, 128:256]
            U_ = B0[:, 256:512].rearrange("p (h i) -> p h i", h=2)
            for hh in range(2):
                kb.op("pe", lambda e: e.matmul(W_[:, hh, :], lhsT=fr(KBm[:, hh, 0, :]), rhs=fr(Vtm[:, tt, hh * 64:(hh + 1) * 64]), start=True, stop=True),
                      reads=[KBm, Vtm], writes=[B0])
            kb.op("pe", lambda e: e.transpose(out=Bt_, in_=BT[:, cols], identity=C.ident[:]), reads=[BT, C.ident], writes=[B0])
            AG_ = B1[:, 256:384]
            KG_ = B1[:, 384:512]
            kb.op("pe", lambda e: e.transpose(out=AG_, in_=Aa[:, cols], identity=C.ident[:]), reads=[Aa, C.ident], writes=[B1])
            kb.op("pe", lambda e: e.transpose(out=KG_, in_=KD[:, cols], identity=C.ident[:]), reads=[KD, C.ident], writes=[B1])
            yield
            kb.op("act", lambda e: e.copy(out=fr(BW[:, :, 64:128]), in_=W_), reads=[B0], writes=[BW])
            kb.op("dve", lambda e: e.tensor_copy(out=fr(BW[:, :, 0:64]), in_=Bt_.rearrange("p (h j) -> p h j", h=2)), reads=[B0], writes=[BW])
            kb.op("act", lambda e: e.copy(out=AGtm[:], in_=AG_), reads=[B1], writes=[AGtm])
            kb.op("act", lambda e: e.copy(out=KGtm[:], in_=KG_), reads=[B1], writes=[KGtm])
            yield
            for hh in range(2):
                kb.op("pe", lambda e: e.matmul(U_[:, hh, :], lhsT=fr(TT[:, hh, 0, :]), rhs=fr(BW[:, hh, :]), start=True, stop=True),
                      reads=[TT, BW], writes=[B0])
            yield
            kb.op("dve", lambda e: e.tensor_copy(out=fr(BU[:]), in_=U_), reads=[B0], writes=[BU])
            yield
            Y_ = B1[:, 0:128].rearrange("p (h i) -> p h i", h=2)
            R_ = B1[:, 128:256]
            for hh in range(2):
                kb.op("pe", lambda e: e.matmul(Y_[:, hh, :], lhsT=fr(XA[:, hh, 1, :]), rhs=fr(BU[:, hh, 64:128]), start=True, stop=False),
                      reads=[XA, BU], writes=[B1])
                kb.op("pe", lambda e: e.matmul(Y_[:, hh, :], lhsT=fr(KBm[:, hh, 1, :]), rhs=fr(Vtm[:, tt, hh * 64:(hh + 1) * 64]), start=False, stop=True),
                      reads=[KBm, Vtm], writes=[B1])
            for hh in range(2):
                kb.op("pe", lambda e: e.matmul(R_[hh * 64:(hh + 1) * 64, :], lhsT=BU[:, hh, 0:64], rhs=XA[:, hh, 1, :], start=True, stop=True),
                      reads=[BU, XA], writes=[B1])
            for n in range(2):
                tr = slice(n * 64, n * 64 + 64)
                bk = (B2, B3)[n]
                for hh in range(2):
                    pr = slice(hh * 64, hh * 64 + 64)
                    kb.op("pe", lambda e: e.matmul(bk[pr, 0:64], lhsT=BU[tr, hh, 0:64], rhs=AGtm[tr, pr], start=True, stop=True),
                          reads=[BU, AGtm], writes=[bk])
                    kb.op("pe", lambda e: e.matmul(bk[pr, 64:128], lhsT=AGtm[tr, pr], rhs=BU[tr, hh, 64:128], start=True, stop=False),
                          reads=[BU, AGtm], writes=[bk])
                    kb.op("pe", lambda e: e.matmul(bk[pr, 64:128], lhsT=KGtm[tr, pr], rhs=Vtm[tr, tt, pr], start=False, stop=True),
                          reads=[KGtm, Vtm], writes=[bk])
            kb.pe_fence()
            yield
            if d == 0:
                kb.op("act", lambda e: e.copy(out=Ysum[:, tt, :], in_=B1[:, 0:128]), reads=[B1], writes=[Ysum])
            else:
                kb.op("dve", lambda e: e.tensor_tensor(out=Ysum[:, tt, :], in0=B1[:, 0:128], in1=Ysum[:, tt, :], op=ALU.add),
                      reads=[B1, Ysum], writes=[Ysum])
            kb.op("dve", lambda e: e.tensor_tensor(out=RhT[:, cols], in0=R_, in1=RT[:, cols], op=ALU.add), reads=[B1, RT], writes=[RhT])
            for n in range(2):
                ch = tt * 2 + n
                bk = (B2, B3)[n]
                kb.op("dve", lambda e: e.scalar_tensor_tensor(out=MTa[:, ch, :], in0=ident2, scalar=GL[:, ch:ch + 1], in1=bk[:, 0:64],
                                                              op0=ALU.mult, op1=ALU.add), reads=[cst, GL, bk], writes=[MTa])
                kb.op("act", lambda e: e.copy(out=Ca[:, ch, :], in_=bk[:, 64:128]), reads=[bk], writes=[Ca])
            yield

        def run_tiles(d, RT, BT):
            pending = list(range(NT))
            active = []
            free_sets = [sets[0], sets[1]]
            S = free_sets.pop(0)
            g = tile_gen(S, d, pending.pop(0), RT, BT)
            active.append((g, S))
            for _ in range(8):
                next(g)
            while active or pending:
                if pending and free_sets:
                    S = free_sets.pop(0)
                    active.append((tile_gen(S, d, pending.pop(0), RT, BT), S))
                for item in list(active):
                    g, S = item
                    try:
                        next(g)
                    except StopIteration:
                        active.remove(item)
                        free_sets.append(S)

        for cc in ccs:
            rows = slice(cc * 128, (cc + 1) * 128)
            kb.dma("sp", Rr[:], D.zs_d.ap()[cc * 128:(cc + 1) * 128, :], reads=[D.zs_d], writes=[Rr])
            kb.dma("act", Kk[:], D.zs_d.ap()[1024 + cc * 128:1024 + (cc + 1) * 128, :], reads=[D.zs_d], writes=[Kk])
            kb.dma("sp", Vv[:], D.zs_d.ap()[2048 + cc * 128:2048 + (cc + 1) * 128, :], reads=[D.zs_d], writes=[Vv])
            kb.dma("pool", lnw[:], D.lnx_w.ap()[cc * 128:(cc + 1) * 128].partition_broadcast(128), writes=[lnw])
            kb.dma("pool", lnb[:], D.lnx_b.ap()[cc * 128:(cc + 1) * 128].partition_broadcast(128), writes=[lnb])
            for g in range(4):
                for j in range(4):
                    tt = g * 4 + j
                    kb.op("pe", lambda e: e.transpose(out=psX[:, j * 128:(j + 1) * 128], in_=Vv[:, tt * 128:(tt + 1) * 128], identity=C.ident[:]),
                          reads=[Vv, C.ident], writes=[psX])
                kb.op("act", lambda e: e.copy(out=fr(Vtm[:, g * 4:(g + 1) * 4, :]), in_=psX[:].rearrange("p (j c) -> p j c", j=4)), reads=[psX], writes=[Vtm])
            kb.op("act", lambda e: e.activation(out=KKn[:], in_=Kk[:], func=AF.Copy, scale=rw[:, cc, 4:5]),
                  reads=[Kk, rw], writes=[KKn])
            kb.op("act", lambda e: e.activation(out=XI[:], in_=KKn[:], func=AF.Square), reads=[KKn], writes=[XI])
            for tb in range(4):
                kb.op("pe", lambda e: e.matmul(psX[:], lhsT=masks[:, 5, :], rhs=XI[:, tb * 512:(tb + 1) * 512], start=True, stop=True),
                      reads=[masks, XI], writes=[psX])
                kb.op("act", lambda e: e.activation(out=XE[:, tb * 512:(tb + 1) * 512], in_=psX[:], func=AF.Sqrt), reads=[psX], writes=[XE])
            kb.op("dve", lambda e: e.tensor_scalar(out=XE[:], in0=XE[:], scalar1=1e-12, scalar2=None, op0=ALU.max), reads=[XE], writes=[XE])
            kb.op("dve", lambda e: e.reciprocal(out=XE[:], in_=XE[:]), reads=[XE], writes=[XE])
            kb.op("pool", lambda e: e.tensor_tensor(out=KKn[:], in0=KKn[:], in1=XE[:], op=ALU.mult), reads=[KKn, XE], writes=[KKn])
            chk(1)
            def prep_gen(d):
                yield
                kb.dma("sp", XE[:], D.ld_d.ap()[d, cc * 128:(cc + 1) * 128, :], reads=[D.ld_d], writes=[XE])
                yield
                kb.dma("act", Aa[:], D.a_d.ap()[d, cc * 128:(cc + 1) * 128, :], reads=[D.a_d], writes=[Aa])
                yield
                kb.op("dve", lambda e: e.tensor_scalar(out=KD[:], in0=Aa[:], scalar1=-1.0, scalar2=rw[:, cc, 5:6], op0=ALU.add, op1=ALU.mult),
                      reads=[Aa, rw], writes=[KD])
                yield
                kb.op("dve", lambda e: e.scalar_tensor_tensor(out=KD[:], in0=KD[:], scalar=1.0, in1=Kk[:], op0=ALU.add, op1=ALU.mult),
                      reads=[KD, Kk], writes=[KD])
                yield
                kb.op("dve", lambda e: e.scalar_tensor_tensor(out=XI[:], in0=KD[:], scalar=rw[:, cc, 6:7], in1=Rr[:], op0=ALU.mult, op1=ALU.mult),
                      reads=[KD, rw, Rr], writes=[XI])
                for tt in range(NT):
                    kb.op("pe", lambda e: e.matmul(psX[:, tt * 2:tt * 2 + 2], lhsT=XI[:, tt * 128:(tt + 1) * 128], rhs=sel, start=True, stop=True),
                          reads=[XI, cst], writes=[psX])
                yield
                kb.op("act", lambda e: e.copy(out=stt[:, 64 + 32 * d:96 + 32 * d], in_=psX[:, 0:32]), reads=[psX], writes=[stt])
                yield
                kb.op("pool", lambda e: e.tensor_tensor(out=Aa[:], in0=Aa[:], in1=KKn[:], op=ALU.mult), reads=[Aa, KKn], writes=[Aa])
                yield
                kb.op("dve", lambda e: e.tensor_tensor_scan(out=XI[:], data0=segt[:], data1=XE[:], initial=0.0, op0=ALU.mult, op1=ALU.add),
                      reads=[segt, XE], writes=[XI])
                yield
                kb.op("pool", lambda e: e.tensor_copy(out=tot[:], in_=XI[:].rearrange("p (n s) -> p n s", s=64)[:, :, 63]), reads=[XI], writes=[tot])
                yield
                kb.op("act", lambda e: e.activation(out=GL[:], in_=tot[:], func=AF.Exp), reads=[tot], writes=[GL])
                if d == 0:
                    kb.op("pool", lambda e: e.tensor_tensor(out=XE[:], in0=XI[:], in1=XE[:], op=ALU.subtract), reads=[XI, XE], writes=[XE])
                else:
                    kb.op("dve", lambda e: e.tensor_tensor(out=XI[:].rearrange("p (n s) -> p n s", s=64),
                                                           in0=tot[:, :, None].to_broadcast([128, 32, 64]),
                                                           in1=XI[:].rearrange("p (n s) -> p n s", s=64), op=ALU.subtract),
                          reads=[tot, XI], writes=[XI])
                    kb.op("pool", lambda e: e.tensor_tensor(out=XE[:], in0=XI[:], in1=XE[:], op=ALU.add), reads=[XI, XE], writes=[XE])
                cI, cE = (XI, XE) if d == 0 else (XE, XI)
                yield
                kb.op("act", lambda e: e.activation(out=fr(E1[:]), in_=cI[:], func=AF.Exp), reads=[cI], writes=[E1])
                yield
                kb.op("act", lambda e: e.activation(out=cI[:], in_=cI[:], func=AF.Exp, scale=-1.0), reads=[cI], writes=[cI])
                yield
                kb.op("act", lambda e: e.activation(out=cE[:], in_=cE[:], func=AF.Exp), reads=[cE], writes=[cE])
                yield
                kb.op("dve", lambda e: e.tensor_tensor(out=fr(E1[:]), in0=E1[:], in1=Rr[:], op=ALU.mult), reads=[E1, Rr], writes=[E1])
                yield
                kb.op("pool", lambda e: e.tensor_tensor(out=fr(KT[:]), in0=KD[:], in1=cI[:], op=ALU.mult), reads=[KD, cI], writes=[KT])
                yield
                kb.op("dve", lambda e: e.tensor_tensor(out=fr(AT[:]), in0=Aa[:], in1=cI[:], op=ALU.mult), reads=[Aa, cI], writes=[AT])
                yield
                kb.op("dve", lambda e: e.scalar_tensor_tensor(out=fr(BTb[:]), in0=cE[:], scalar=-1.0, in1=KKn[:], op0=ALU.mult, op1=ALU.mult),
                      reads=[cE, KKn], writes=[BTb])
                yield
                kb.op("pool", lambda e: e.tensor_tensor(out=cI[:].rearrange("p (n s) -> p n s", s=64), in0=cI[:].rearrange("p (n s) -> p n s", s=64),
                                                        in1=GL[:, :, None].to_broadcast([128, 32, 64]), op=ALU.mult), reads=[cI, GL], writes=[cI])
                yield
                kb.op("dve", lambda e: e.tensor_tensor(out=KD[:], in0=KD[:], in1=cI[:], op=ALU.mult), reads=[KD, cI], writes=[KD])
                yield
                kb.op("pool", lambda e: e.tensor_tensor(out=Aa[:], in0=Aa[:], in1=cI[:], op=ALU.mult), reads=[Aa, cI], writes=[Aa])
                yield

            def seq_gen(d):
                kb.op("pool", lambda e: e.memset(H[0][:], 0.0), writes=[H[0]])
                order = range(32) if d == 0 else range(31, -1, -1)
                hc = 0
                for ch in order:
                    tt, n = ch // 2, ch % 2
                    ccols = slice(ch * 64, ch * 64 + 64)
                    Hc, Hn = H[hc], H[1 - hc]
                    tr = slice(n * 64, n * 64 + 64)
                    for hh in range(2):
                        pr = slice(hh * 64, hh * 64 + 64)
                        bk = bkH[hh]
                        kb.op("pe", lambda e: e.matmul(bk[pr, 64:128], lhsT=MTa[pr, ch, :], rhs=Hc[pr, :], start=True, stop=True),
                              reads=[MTa, Hc], writes=[bk])
                        kb.op("pe", lambda e: e.matmul(bk[tr, 0:64], lhsT=RhT[pr, ccols], rhs=Hc[pr, :], start=True, stop=True),
                              reads=[RhT, Hc], writes=[bk])
                    for hh in range(2):
                        pr = slice(hh * 64, hh * 64 + 64)
                        bk = bkH[hh]
                        kb.op("dve", lambda e: e.tensor_tensor(out=Hn[pr, :], in0=bk[pr, 64:128], in1=Ca[pr, ch, :], op=ALU.add),
                              reads=[bk, Ca], writes=[Hn])
                    for hh in range(2):
                        pr = slice(hh * 64, hh * 64 + 64)
                        bk = bkH[hh]
                        kb.op("act" if False else "dve", lambda e: e.tensor_tensor(out=Ysum[tr, tt, pr], in0=bk[tr, 0:64], in1=Ysum[tr, tt, pr], op=ALU.add),
                              reads=[bk, Ysum], writes=[Ysum])
                    hc = 1 - hc
                    yield
                yield

            def exhaust(g):
                for _ in g:
                    pass

            exhaust(prep_gen(0))
            run_tiles(0, E1, BTb)
            sg = seq_gen(0)
            pg = prep_gen(1)
            done_p = done_s = False
            while not (done_p and done_s):
                if not done_p:
                    try:
                        next(pg)
                    except StopIteration:
                        done_p = True
                for _ in range(2):
                    if not done_s:
                        try:
                            next(sg)
                        except StopIteration:
                            done_s = True
            run_tiles(1, E1, BTb)
            exhaust(seq_gen(1))
            chk(10)
            if dbg is not None and "Ysum" in dbg:
                kb.dma("sp", D.dbgY.ap()[:, cc * 128:(cc + 1) * 128].rearrange("(tt p) c -> p tt c", p=128), Ysum[:], reads=[Ysum], writes=[D.dbgY])
            Y3 = Ysum[:].rearrange("p t (h i) -> p (t h) i", h=2)
            st_mu = stt[:, 0:32]
            st_var = stt[:, 32:64]
            bon = stt[:, 128:160]
            kb.op("dve", lambda e: e.tensor_reduce(out=st_mu, in_=Y3, axis=AX.X, op=ALU.add), reads=[Ysum], writes=[stt])
            kb.op("dve", lambda e: e.tensor_scalar(out=st_mu, in0=st_mu, scalar1=1.0 / 64, scalar2=None, op0=ALU.mult), reads=[stt], writes=[stt])
            kb.op("dve", lambda e: e.tensor_tensor(out=Y3, in0=Y3, in1=st_mu[:, :, None].to_broadcast([128, 32, 64]), op=ALU.subtract),
                  reads=[Ysum, stt], writes=[Ysum])
            sqv = XI[:].rearrange("p (a i) -> p a i", i=64)
            kb.op("act", lambda e: e.activation(out=sqv, in_=Y3, func=AF.Square), reads=[Ysum], writes=[XI])
            kb.op("dve", lambda e: e.tensor_reduce(out=st_var, in_=sqv, axis=AX.X, op=ALU.add), reads=[XI], writes=[stt])
            kb.op("dve", lambda e: e.tensor_scalar(out=st_var, in0=st_var, scalar1=1.0 / 64, scalar2=GN_EPS, op0=ALU.mult, op1=ALU.add),
                  reads=[stt], writes=[stt])
            kb.op("act", lambda e: e.activation(out=st_var, in_=st_var, func=AF.Sqrt), reads=[stt], writes=[stt])
            kb.op("dve", lambda e: e.reciprocal(out=st_var, in_=st_var), reads=[stt], writes=[stt])
            kb.op("dve", lambda e: e.tensor_tensor(out=Y3, in0=Y3, in1=st_var[:, :, None].to_broadcast([128, 32, 64]), op=ALU.mult),
                  reads=[Ysum, stt], writes=[Ysum])
            kb.op("pool", lambda e: e.tensor_tensor(out=Ysum[:], in0=Ysum[:], in1=lnw[:, None, :].to_broadcast([128, NT, 128]), op=ALU.mult),
                  reads=[Ysum, lnw], writes=[Ysum])
            kb.op("pool", lambda e: e.tensor_tensor(out=Ysum[:], in0=Ysum[:], in1=lnb[:, None, :].to_broadcast([128, NT, 128]), op=ALU.add),
                  reads=[Ysum, lnb], writes=[Ysum])
            kb.op("dve", lambda e: e.tensor_tensor(out=bon, in0=stt[:, 64:96], in1=stt[:, 96:128], op=ALU.add), reads=[stt], writes=[stt])
            kb.op("dve", lambda e: e.tensor_scalar(out=bon, in0=bon, scalar1=0.5, scalar2=None, op0=ALU.mult), reads=[stt], writes=[stt])
            V3 = Vtm[:].rearrange("p t (h i) -> p (t h) i", h=2)
            kb.op("pool", lambda e: e.tensor_tensor(out=sqv, in0=V3, in1=bon[:, :, None].to_broadcast([128, 32, 64]), op=ALU.mult),
                  reads=[Vtm, stt], writes=[XI])
            kb.op("pool", lambda e: e.tensor_tensor(out=Y3, in0=Y3, in1=sqv, op=ALU.add), reads=[Ysum, XI], writes=[Ysum])
            gt = XE[:].rearrange("p (t c) -> p t c", c=128)
            kb.dma("sp", gt, D.g_d.ap()[:, cc * 128:(cc + 1) * 128].rearrange("(tt p) c -> p tt c", p=128), reads=[D.g_d], writes=[XE])
            kb.op("dve", lambda e: e.tensor_tensor(out=Ysum[:], in0=Ysum[:], in1=gt, op=ALU.mult), reads=[Ysum, XE], writes=[Ysum])
            if dbg is not None and "yfin" in dbg:
                kb.dma("sp", D.dbgF.ap()[:, cc * 128:(cc + 1) * 128].rearrange("(tt p) c -> p tt c", p=128), Ysum[:], reads=[Ysum], writes=[D.dbgF])
            for g in range(4):
                for j in range(4):
                    tt = g * 4 + j
                    kb.op("pe", lambda e: e.transpose(out=psX[:, j * 128:(j + 1) * 128], in_=Ysum[:, tt, :], identity=C.ident[:]),
                          reads=[Ysum, C.ident], writes=[psX])
                kb.op("act", lambda e: e.copy(out=KD[:, g * 512:(g + 1) * 512], in_=psX[:]), reads=[psX], writes=[KD])
            yb = Aa[:].bitcast(BF16)[:, 0:2048]
            kb.op("dve", lambda e: e.tensor_copy(out=yb, in_=KD[:]), reads=[KD], writes=[Aa])
            kb.dma("sp", D.ymT_d.ap()[cc], yb, reads=[Aa], writes=[D.ymT_d])
        kb.barrier()


def host_consts():
    S = 2048
    rows = np.repeat(np.arange(32), 64).astype(np.float32)
    cols = np.tile(np.arange(64), 32).astype(np.float32)
    inv = (10000.0 ** (-np.arange(0, 32, 2, dtype=np.float32) / 32)).astype(np.float32)
    ar = rows[:, None] * inv[None]
    ac = cols[:, None] * inv[None]
    tabC = np.concatenate([np.cos(ar), np.cos(ac)], 1).astype(np.float32)
    tabS = np.concatenate([np.sin(ar), np.sin(ac)], 1).astype(np.float32)
    ident = np.eye(128, dtype=np.float32)
    iota = np.tile(np.arange(128, dtype=np.float32)[None], (128, 1))
    r = np.arange(128)[:, None]
    s = np.arange(128)[None, :]
    same = (r // 64) == (s // 64)
    masks = np.zeros((128, 6, 128), np.float32)
    masks[:, 0] = same & (r < s)
    masks[:, 1] = same & (r <= s)
    masks[:, 2] = same & (r > s)
    masks[:, 3] = same & (r >= s)
    masks[:, 4] = same & (r > s)
    masks[:, 5] = same
    rcst = np.zeros((128, 66 + 2048), np.float32)
    pp = np.arange(128)
    rcst[pp, pp % 64] = 1.0
    rcst[:, 64] = (pp // 64 == 0)
    rcst[:, 65] = (pp // 64 == 1)
    seg = np.ones(2048, np.float32); seg[::64] = 0.0
    rcst[:, 66:] = seg[None]
    import ml_dtypes
    segm = np.ascontiguousarray(np.broadcast_to(seg[None], (128, 2048))).astype(ml_dtypes.bfloat16)
    return dict(tabC=tabC, tabS=tabS, ident=ident, iota=iota, masks=masks, rcst=rcst, segm=segm)

def prep_shared(inp):
    L = 0
    d = host_consts()
    mu_c = np.zeros((128, 3 * NRCH), np.float32)
    for ci, (c0, cs) in enumerate(RCH):
        mu_c[:cs, ci] = inp["mu_prev"][L, c0:c0 + cs]
        mu_c[:cs, NRCH + ci] = inp["mu_next"][L, c0:c0 + cs]
    d["mu_c"] = mu_c
    w = inp["w_in"][L].reshape(16, 128, 5248)
    wt = np.empty((128, 16 * 5248), np.float32)
    for (c0, cs) in RCH:
        wt[:, 16 * c0:16 * (c0 + cs)] = w[:, :, c0:c0 + cs].transpose(1, 0, 2).reshape(128, 16 * cs)
    RWK = 3712
    for cg in range(3):
        for hf in range(2):
            o0 = 16 * RWK + (cg * 2 + hf) * 4096
            wt[:, o0:o0 + 4096] = w[hf * 8:(hf + 1) * 8, :, RWK + cg * 512:RWK + (cg + 1) * 512].transpose(1, 0, 2).reshape(128, 4096)
    d["w_in"] = wt
    for k in ["norm1_w", "norm2_w", "q_norm_w", "k_norm_w", "w_out", "w_pq", "w2", "a2", "g2", "lnx_w", "lnx_b", "v_tab"]:
        d[k] = np.ascontiguousarray(inp[k][L])
    d["normf_w"] = np.ascontiguousarray(inp["normf_w"])
    d["skT"] = np.ascontiguousarray(inp["sub_keys"][L].reshape(16, 128, 128).transpose(2, 0, 1))
    d["u_tabT"] = np.ascontiguousarray(inp["u_tab"][L].reshape(128, 128, 16, 128).transpose(0, 3, 2, 1)).reshape(128, 128, 2048)
    rw = np.zeros((128, 8, 8), np.float32)
    def ch(v):
        return v.reshape(8, 128).T
    rw[:, :, 0] = ch(inp["w0"][L, 0]); rw[:, :, 1] = ch(inp["w0"][L, 1])
    rw[:, :, 2] = ch(inp["a0"][L, 0]); rw[:, :, 3] = ch(inp["a0"][L, 1])
    rw[:, :, 4] = ch(inp["k_k"][L]); rw[:, :, 5] = ch(inp["k_a"][L]); rw[:, :, 6] = ch(inp["r_k"][L].reshape(-1))
    d["rw_c"] = rw
    return d


def build_program():
    nc = bass.Bass("TRN2", target_bir_lowering=False)
    kb = KB(nc)
    D = declare(kb, None)
    C = consts(kb, D)
    phase_A(kb, C, D)
    phase_Rpre(kb, C, D)
    phase_R(kb, C, D)
    phase_T(kb, C, D)
    phase_O(kb, C, D, D.ymT_d)
    phase_P(kb, C, D)
    phase_F(kb, C, D)
    kb.finish("sp")
    return nc


def kernel(**inputs):
    inp = {k: np.asarray(v) for k, v in inputs.items()}
    shared = prep_shared(inp)
    nc = build_program()
    in_maps = []
    for b in range(8):
        d = dict(shared)
        d["x"] = np.ascontiguousarray(inp["x"][b])
        in_maps.append(d)
    res = run_bass_kernel_spmd(nc, in_maps, core_ids=list(range(8)))
    out = np.stack([np.asarray(r["out"], dtype=np.float32) for r in res.results], axis=0)
    return out
```

```python
import numpy as np
import contextlib
import concourse.bass as bass
import concourse.mybir as mybir
from concourse.bass_utils import run_bass_kernel_spmd

F32 = mybir.dt.float32
BF16 = mybir.dt.bfloat16
U32 = mybir.dt.uint32
ALU = mybir.AluOpType
AF = mybir.ActivationFunctionType
AX = mybir.AxisListType


class T:
    def __init__(self, h, name):
        self.h = h
        self.name = name
        self.w = None
        self.r = {}
        self.dsem = None
        self.dcnt = 0
        self.is_psum = False

    def __getitem__(self, k):
        return self.h[k]

    def ap(self):
        return self.h.ap() if hasattr(self.h, "ap") else self.h[:]


class KB:
    def __init__(self, nc):
        self.nc = nc
        self.es = contextlib.ExitStack()
        self.engs = {"pe": nc.tensor, "act": nc.scalar, "dve": nc.vector, "pool": nc.gpsimd, "sp": nc.sync}
        self.sem = {}
        self.cnt = {}
        for e in self.engs:
            self.sem[e] = self.es.enter_context(nc.semaphore("s_" + e))
            self.cnt[e] = 0
        self.waited = {}
        self.alltensors = []
        self.dsems = []
        self.n_ins = 0

    def sb(self, name, shape, dt=F32, stack=None):
        h = (stack or self.es).enter_context(self.nc.sbuf_tensor(name, list(shape), dt))
        t = T(h, name)
        return t

    def ps(self, name, shape, dt=F32, stack=None):
        h = (stack or self.es).enter_context(self.nc.psum_tensor(name, list(shape), dt))
        t = T(h, name)
        t.is_psum = True
        return t

    def dram(self, name, shape, dt=F32, kind="Internal"):
        h = self.nc.dram_tensor(name, list(shape), dt, kind=kind)
        return T(h, name)

    def view(self, t, name=None):
        return t

    def _wait(self, eng, tok):
        if tok is None:
            return
        sem, val = tok
        key = (eng, id(sem))
        if self.waited.get(key, 0) >= val:
            return
        self.engs[eng].wait_ge(sem, val)
        self.waited[key] = val

    def _deps(self, eng, reads, writes):
        own = id(self.sem[eng])
        for t in reads:
            if t.w is not None:
                if eng == "pe" and id(t.w[0]) == own:
                    continue
                self._wait(eng, t.w)
        for t in writes:
            if t.w is not None and id(t.w[0]) != own:
                self._wait(eng, t.w)
            for k, tok in t.r.items():
                if k == own:
                    continue
                self._wait(eng, tok)

    def _mark(self, tok, reads, writes):
        for t in reads:
            if t in writes:
                continue
            t.r[id(tok[0])] = tok
        for t in writes:
            t.w = tok
            t.r = {}

    def op(self, eng, fn, reads=(), writes=()):
        psr = [t for t in reads if t.is_psum and t not in writes]
        if psr:
            writes = list(writes) + psr
        self._deps(eng, reads, writes)
        ins = fn(self.engs[eng])
        self.cnt[eng] += 1
        ins.then_inc(self.sem[eng], 1)
        tok = (self.sem[eng], self.cnt[eng])
        self._mark(tok, reads, writes)
        self.n_ins += 1
        return ins

    def dma(self, q, out_ap, in_ap, reads=(), writes=(), **kw):
        assert len(writes) == 1
        dst = writes[0]
        if dst.dsem is None:
            dst.dsem = self.es.enter_context(self.nc.semaphore("d_" + dst.name))
            self.dsems.append(dst)
        self._deps(q, reads, [])
        if dst.w is not None and dst.w[0] is not dst.dsem:
            self._wait(q, dst.w)
        for k, tok in dst.r.items():
            self._wait(q, tok)
        ins = self.engs[q].dma_start(out=out_ap, in_=in_ap, **kw)
        dst.dcnt += 16
        ins.then_inc(dst.dsem, 16)
        tok = (dst.dsem, dst.dcnt)
        for t in reads:
            t.r[id(tok[0])] = tok
        dst.w = tok
        dst.r = {}
        self.n_ins += 1
        return ins

    def pe_fence(self):
        if self.cnt["pe"] > 0:
            self._wait("pe", (self.sem["pe"], self.cnt["pe"]))

    def barrier(self):
        toks = [(self.sem[e], self.cnt[e]) for e in self.engs if self.cnt[e] > 0]
        toks += [(t.dsem, t.dcnt) for t in self.dsems if t.dcnt > 0]
        for e in self.engs:
            for tok in toks:
                if tok[0] is self.sem[e]:
                    continue
                self._wait(e, tok)

    def finish(self, eng="sp"):
        for t in self.dsems:
            if t.dcnt > 0:
                self._wait(eng, (t.dsem, t.dcnt))
        for e in self.engs:
            if e != eng and self.cnt[e] > 0:
                self._wait(eng, (self.sem[e], self.cnt[e]))


EPS = 1e-6
NT = 16
RCH = [(i * 128, 128) for i in range(24)] + [(3072 + i * 96, 96) for i in range(4)] + [(3456, 128), (3584, 128)]
NRCH = len(RCH)
RW = 3712


class Obj:
    pass


def declare(kb, debug):
    D = Obj()

    def inp(name, shape, dt=F32):
        setattr(D, name, kb.dram(name, shape, dt, kind="ExternalInput"))

    def scr(name, shape, dt=F32):
        kind = "ExternalOutput" if (debug and name in debug) else "Internal"
        setattr(D, name, kb.dram(name, shape, dt, kind=kind))

    inp("x", [2048, 2048])
    inp("w_in", [128, 16 * 5248])
    inp("mu_c", [128, 3 * NRCH])
    inp("norm1_w", [2048])
    inp("norm2_w", [2048])
    inp("normf_w", [2048])
    inp("q_norm_w", [64])
    inp("k_norm_w", [64])
    inp("tabC", [2048, 32])
    inp("tabS", [2048, 32])
    inp("ident", [128, 128])
    inp("w_out", [2048, 2048])
    inp("w_pq", [2048, 2048])
    inp("skT", [128, 16, 128])
    inp("iota", [128, 128])
    inp("u_tabT", [128, 128, 2048])
    inp("v_tab", [16384, 2048])
    inp("w2", [2, 96, 1024])
    inp("a2", [2, 96, 1024])
    inp("g2", [256, 1024])
    inp("rw_c", [128, 8, 8])
    inp("lnx_w", [1024])
    inp("lnx_b", [1024])
    inp("masks", [128, 6, 128])
    if debug and "in_ymT" in debug:
        inp("ymT_in", [16, 128, 2048], BF16)
    if debug and "in_x1" in debug:
        inp("x1_in", [2048, 2048])
    scr("zs_d", [RW, 2048])
    scr("qkv_d", [2048, 1536])
    scr("ymT_d", [16, 128, 2048], BF16)
    scr("x1_d", [2048, 2048])
    scr("G_d", [128, 128, 2048], BF16)
    scr("peT_d", [2048, 2048])
    scr("A_d", [128, 128, 2048], BF16)
    scr("ld_d", [2, 1024, 2048])
    scr("a_d", [2, 1024, 2048])
    scr("g_d", [2048, 1024])
    inp("rcst", [128, 66 + 2048])
    inp("segm", [128, 2048], BF16)
    if debug and "dbgY" in debug:
        setattr(D, "dbgY", kb.dram("dbgY", [2048, 1024], F32, kind="ExternalOutput"))
        setattr(D, "dbgF", kb.dram("dbgF", [2048, 1024], F32, kind="ExternalOutput"))
    scr("S_d", [2048, 16, 128])
    scr("h2T_d", [16, 128, 2048], BF16)
    scr("vb_d", [16384, 2048], BF16)
    setattr(D, "out", kb.dram("out", [2048, 2048], F32, kind="ExternalOutput"))
    return D


def consts(kb, D):
    C = Obj()
    C.ident = kb.sb("c_ident", [128, 128], F32)
    kb.dma("sp", C.ident[:], D.ident.ap(), writes=[C.ident])
    C.identb = kb.sb("c_identb", [128, 128], BF16)
    kb.op("dve", lambda e: e.tensor_copy(out=C.identb[:], in_=C.ident[:]), reads=[C.ident], writes=[C.identb])
    return C


def norm_to_T(kb, C, src, wvec, hT, tag):
    with contextlib.ExitStack() as st:
        wb = kb.sb(tag + "wb", [128, 2048], F32, st)
        kb.dma("pool", wb[:], wvec.ap().partition_broadcast(128), writes=[wb])
        xts = [kb.sb(f"{tag}x{i}", [128, 2048], F32, st) for i in range(2)]
        junk = kb.sb(tag + "junk", [128, 2048], BF16, st)
        hb = [kb.sb(f"{tag}h{i}", [128, 2048], BF16, st) for i in range(2)]
        ss = kb.sb(tag + "ss", [128, NT], F32, st)
        rs = kb.sb(tag + "rs", [128, NT], F32, st)
        ptr = [kb.ps(f"{tag}ps{i}", [128, 1024], BF16, st) for i in range(2)]
        kb.op("pool", lambda e: e.memset(ss[:], 0.0), writes=[ss])
        for tt in range(NT):
            xt = xts[tt % 2]
            kb.dma("sp", xt[:], src.ap()[tt * 128:(tt + 1) * 128, :], writes=[xt])
            kb.op("act", lambda e: e.activation(out=junk[:], in_=xt[:], func=AF.Square, accum_out=ss[:, tt:tt + 1]),
                  reads=[xt], writes=[junk, ss])
            kb.op("dve", lambda e: e.tensor_scalar(out=rs[:, tt:tt + 1], in0=ss[:, tt:tt + 1], scalar1=1.0 / 2048, scalar2=EPS,
                                                   op0=ALU.mult, op1=ALU.add), reads=[ss], writes=[rs])
            kb.op("act", lambda e: e.activation(out=rs[:, tt:tt + 1], in_=rs[:, tt:tt + 1], func=AF.Sqrt), reads=[rs], writes=[rs])
            kb.op("dve", lambda e: e.reciprocal(out=rs[:, tt:tt + 1], in_=rs[:, tt:tt + 1]), reads=[rs], writes=[rs])
            h = hb[tt % 2]
            kb.op("dve", lambda e: e.scalar_tensor_tensor(out=h[:], in0=xt[:], scalar=rs[:, tt:tt + 1], in1=wb[:],
                                                          op0=ALU.mult, op1=ALU.mult), reads=[xt, rs, wb], writes=[h])
            for g in range(2):
                p = ptr[g]
                for j in range(8):
                    dc = g * 8 + j
                    kb.op("pe", lambda e: e.transpose(out=p[:, j * 128:(j + 1) * 128], in_=h[:, dc * 128:(dc + 1) * 128],
                                                      identity=C.identb[:]), reads=[h, C.identb], writes=[p])
                eng = "act" if g == 0 else "dve"
                src_ap = p[:].rearrange("p (j t) -> p j t", j=8)
                dst_ap = hT[:, g * 8:(g + 1) * 8, tt * 128:(tt + 1) * 128]
                if eng == "act":
                    kb.op("act", lambda e: e.copy(out=dst_ap, in_=src_ap), reads=[p], writes=[hT])
                else:
                    kb.op("dve", lambda e: e.tensor_copy(out=dst_ap, in_=src_ap), reads=[p], writes=[hT])
        kb.barrier()


def phase_A(kb, C, D):
    with contextlib.ExitStack() as st:
        hT = kb.sb("hT", [128, 16, 2048], BF16, st)
        norm_to_T(kb, C, D.x, D.norm1_w, hT, "n1")
        mu = kb.sb("mu", [128, 3 * NRCH], F32, st)
        kb.dma("sp", mu[:], D.mu_c.ap(), writes=[mu])
        kb.op("dve", lambda e: e.tensor_tensor(out=mu[:, 60:90], in0=mu[:, 0:30], in1=mu[:, 30:60], op=ALU.add), reads=[mu], writes=[mu])
        kb.op("dve", lambda e: e.tensor_scalar(out=mu[:, 60:90], in0=mu[:, 60:90], scalar1=-1.0, scalar2=1.0, op0=ALU.mult, op1=ALU.add),
              reads=[mu], writes=[mu])
        stg = [kb.sb(f"a_stg{i}", [128, 4096], F32, st) for i in range(2)]
        wbf = [kb.sb(f"a_wbf{i}", [128, 16, 128], BF16, st) for i in range(2)]
        accs = [kb.sb(f"a_acc{i}", [128, 2048], F32, st) for i in range(2)]
        pss = [[kb.ps(f"a_ps{i}_{j}", [128, 512], F32, st) for j in range(4)] for i in range(2)]
        def loadw(ci):
            c0, cs = RCH[ci]
            kb.dma("sp", stg[ci % 2][:, 0:16 * cs], D.w_in.ap()[:, 16 * c0:16 * (c0 + cs)], writes=[stg[ci % 2]])
        loadw(0)
        for ci, (c0, cs) in enumerate(RCH):
            sg = stg[ci % 2]
            sgv = sg[:, 0:16 * cs].rearrange("p (dc c) -> p dc c", dc=16)
            wb = wbf[ci % 2]
            kb.op("pool", lambda e: e.tensor_copy(out=wb[:, :, 0:cs], in_=sgv), reads=[sg], writes=[wb])
            if ci + 1 < NRCH:
                loadw(ci + 1)
            ps = pss[ci % 2]
            acc = accs[ci % 2]
            for tb in range(4):
                for dc in range(16):
                    kb.op("pe", lambda e: e.matmul(ps[tb][0:cs, :], lhsT=wb[:, dc, 0:cs], rhs=hT[:, dc, tb * 512:(tb + 1) * 512],
                                                   start=(dc == 0), stop=(dc == 15)), reads=[wb, hT], writes=[ps[tb]])
            for tb in range(4):
                kb.op("act", lambda e: e.activation(out=acc[0:cs, tb * 512:(tb + 1) * 512], in_=ps[tb][0:cs, :], func=AF.Copy,
                                                    scale=mu[0:cs, 60 + ci:61 + ci]), reads=[ps[tb], mu], writes=[acc])
            for tb in range(4):
                n = 512 if tb < 3 else 511
                d0 = tb * 512 + 1
                kb.op("dve", lambda e: e.scalar_tensor_tensor(out=acc[0:cs, d0:d0 + n], in0=ps[tb][0:cs, 0:n], scalar=mu[0:cs, ci:ci + 1],
                                                              in1=acc[0:cs, d0:d0 + n], op0=ALU.mult, op1=ALU.add),
                      reads=[ps[tb], mu, acc], writes=[acc])
                s0 = 1 if tb == 0 else 0
                n = 512 - s0
                d0 = tb * 512 + s0 - 1
                kb.op("dve", lambda e: e.scalar_tensor_tensor(out=acc[0:cs, d0:d0 + n], in0=ps[tb][0:cs, s0:512], scalar=mu[0:cs, 30 + ci:31 + ci],
                                                              in1=acc[0:cs, d0:d0 + n], op0=ALU.mult, op1=ALU.add),
                      reads=[ps[tb], mu, acc], writes=[acc])
            kb.dma("pool", D.zs_d.ap()[c0:c0 + cs, :], acc[0:cs, :], reads=[acc], writes=[D.zs_d])
        kb.barrier()
        with contextlib.ExitStack() as st2:
            wq = kb.sb("a_wq", [128, 16, 512], BF16, st2)
            ev = [kb.sb(f"a_ev{i}", [128, 512], F32, st2) for i in range(2)]
            for cg in range(3):
                c0 = RW + cg * 512
                for hf in range(2):
                    sg = stg[hf]
                    sgv = sg[:].rearrange("p (dc c) -> p dc c", dc=8)
                    o0 = 16 * RW + (cg * 2 + hf) * 4096
                    kb.dma("sp", sg[:], D.w_in.ap()[:, o0:o0 + 4096], writes=[sg])
                    kb.op("pool", lambda e: e.tensor_copy(out=wq[:, hf * 8:(hf + 1) * 8, :], in_=sgv), reads=[sg], writes=[wq])
                for tt in range(NT):
                    ps = pss[tt % 2][0]
                    for dc in range(16):
                        kb.op("pe", lambda e: e.matmul(ps[:], lhsT=hT[:, dc, tt * 128:(tt + 1) * 128], rhs=wq[:, dc, :],
                                                       start=(dc == 0), stop=(dc == 15)), reads=[wq, hT], writes=[ps])
                    o = ev[tt % 2]
                    kb.op("act", lambda e: e.copy(out=o[:], in_=ps[:]), reads=[ps], writes=[o])
                    kb.dma("sp", D.qkv_d.ap()[tt * 128:(tt + 1) * 128, cg * 512:(cg + 1) * 512], o[:], reads=[o], writes=[D.qkv_d])
            kb.barrier()


def phase_T(kb, C, D):
    with contextlib.ExitStack() as st:
        qT = kb.sb("t_qT", [128, 8, 2048], BF16, st)
        kT2 = kb.sb("t_kT2", [128, 4, 2048], BF16, st)
        vaug = kb.sb("t_vaug", [128, NT, 4, 65], BF16, st)
        wqk = kb.sb("t_wqk", [128, 20, 64], F32, st)
        w64 = kb.sb("t_w64", [128, 2, 64], F32, st)
        kb.dma("pool", w64[:, 0, :], D.q_norm_w.ap().partition_broadcast(128), writes=[w64])
        kb.dma("pool", w64[:, 1, :], D.k_norm_w.ap().partition_broadcast(128), writes=[w64])
        kb.op("dve", lambda e: e.tensor_scalar(out=wqk[:, 0:16, :], in0=w64[:, 0:1, :].to_broadcast([128, 16, 64]), scalar1=0.125, scalar2=None,
                                               op0=ALU.mult), reads=[w64], writes=[wqk])
        kb.op("dve", lambda e: e.tensor_copy(out=wqk[:, 16:20, :], in_=w64[:, 1:2, :].to_broadcast([128, 4, 64])), reads=[w64], writes=[wqk])
        kb.op("pool", lambda e: e.memset(vaug[:], 1.0), writes=[vaug])
        with contextlib.ExitStack() as st2:
            qk = [kb.sb(f"t_qk{i}", [128, 1536], F32, st2) for i in range(2)]
            tC = [kb.sb(f"t_tC{i}", [128, 2, 16], F32, st2) for i in range(2)]
            tS = [kb.sb(f"t_tS{i}", [128, 2, 16], F32, st2) for i in range(2)]
            sq = kb.sb("t_sq", [128, 20, 64], F32, st2)
            ss = kb.sb("t_ss", [128, 20], F32, st2)
            qn = kb.sb("t_qn", [128, 20, 64], F32, st2)
            t1 = kb.sb("t_t1", [128, 20, 2, 16], F32, st2)
            t2 = kb.sb("t_t2", [128, 20, 2, 16], F32, st2)
            t3 = kb.sb("t_t3", [128, 20, 2, 16], F32, st2)
            t4 = kb.sb("t_t4", [128, 20, 2, 16], F32, st2)
            qkr = kb.sb("t_qkr", [128, 20, 64], BF16, st2)
            kd = kb.sb("t_kd", [128, 4, 2, 64], BF16, st2)
            pq = kb.ps("t_pq", [128, 1024], BF16, st2)
            pk = kb.ps("t_pk", [128, 1024], BF16, st2)
            for tt in range(NT):
                q = qk[tt % 2]
                cC = tC[tt % 2]
                cS = tS[tt % 2]
                kb.dma("sp", q[:], D.qkv_d.ap()[tt * 128:(tt + 1) * 128, :], reads=[D.qkv_d], writes=[q])
                kb.dma("act", cC[:].rearrange("p a b -> p (a b)"), D.tabC.ap()[tt * 128:(tt + 1) * 128, :], writes=[cC])
                kb.dma("act", cS[:].rearrange("p a b -> p (a b)"), D.tabS.ap()[tt * 128:(tt + 1) * 128, :], writes=[cS])
                qv = q[:, 0:1280].rearrange("p (h d) -> p h d", h=20)
                kb.op("act", lambda e: e.activation(out=sq[:], in_=qv, func=AF.Square), reads=[q], writes=[sq])
                kb.op("dve", lambda e: e.tensor_reduce(out=ss[:], in_=sq[:], axis=AX.X, op=ALU.add), reads=[sq], writes=[ss])
                kb.op("dve", lambda e: e.tensor_scalar(out=ss[:], in0=ss[:], scalar1=1.0 / 64, scalar2=EPS, op0=ALU.mult, op1=ALU.add),
                      reads=[ss], writes=[ss])
                kb.op("act", lambda e: e.activation(out=ss[:], in_=ss[:], func=AF.Sqrt), reads=[ss], writes=[ss])
                kb.op("dve", lambda e: e.reciprocal(out=ss[:], in_=ss[:]), reads=[ss], writes=[ss])
                kb.op("dve", lambda e: e.tensor_tensor(out=qn[:], in0=qv, in1=ss[:, :, None].to_broadcast([128, 20, 64]), op=ALU.mult),
                      reads=[q, ss], writes=[qn])
                kb.op("pool", lambda e: e.tensor_tensor(out=qn[:], in0=qn[:], in1=wqk[:], op=ALU.mult), reads=[qn, wqk], writes=[qn])
                qn5 = qn[:].rearrange("p h (a b c) -> p h a b c", a=2, b=2)
                x1 = qn5[:, :, :, 0, :]
                x2 = qn5[:, :, :, 1, :]
                Cb = cC[:, None, :, :].to_broadcast([128, 20, 2, 16])
                Sb = cS[:, None, :, :].to_broadcast([128, 20, 2, 16])
                kb.op("dve", lambda e: e.tensor_tensor(out=t1[:], in0=x1, in1=Cb, op=ALU.mult), reads=[qn, cC], writes=[t1])
                kb.op("pool", lambda e: e.tensor_tensor(out=t2[:], in0=x2, in1=Sb, op=ALU.mult), reads=[qn, cS], writes=[t2])
                kb.op("pool", lambda e: e.tensor_tensor(out=t3[:], in0=x2, in1=Cb, op=ALU.mult), reads=[qn, cC], writes=[t3])
                kb.op("dve", lambda e: e.tensor_tensor(out=t4[:], in0=x1, in1=Sb, op=ALU.mult), reads=[qn, cS], writes=[t4])
                r5 = qkr[:].rearrange("p h (a b c) -> p h a b c", a=2, b=2)
                kb.op("dve", lambda e: e.tensor_tensor(out=r5[:, :, :, 0, :], in0=t1[:], in1=t2[:], op=ALU.subtract), reads=[t1, t2], writes=[qkr])
                kb.op("pool", lambda e: e.tensor_tensor(out=r5[:, :, :, 1, :], in0=t3[:], in1=t4[:], op=ALU.add), reads=[t3, t4], writes=[qkr])
                kb.op("pool", lambda e: e.tensor_copy(out=kd[:], in_=qkr[:, 16:20, None, :].to_broadcast([128, 4, 2, 64])), reads=[qkr], writes=[kd])
                for j in range(8):
                    kb.op("pe", lambda e: e.transpose(out=pq[:, j * 128:(j + 1) * 128], in_=qkr[:, 2 * j:2 * j + 2, :].rearrange("p a b -> p (a b)"),
                                                      identity=C.identb[:]), reads=[qkr, C.identb], writes=[pq])
                kb.op("act", lambda e: e.copy(out=qT[:, :, tt * 128:(tt + 1) * 128], in_=pq[:].rearrange("p (j t) -> p j t", j=8)),
                      reads=[pq], writes=[qT])
                for j in range(4):
                    kb.op("pe", lambda e: e.transpose(out=pk[:, j * 128:(j + 1) * 128], in_=kd[:, j, :, :].rearrange("p a b -> p (a b)"),
                                                      identity=C.identb[:]), reads=[kd, C.identb], writes=[pk])
                kb.op("dve", lambda e: e.tensor_copy(out=kT2[:, :, tt * 128:(tt + 1) * 128], in_=pk[:, 0:512].rearrange("p (j t) -> p j t", j=4)),
                      reads=[pk], writes=[kT2])
                kb.op("pool", lambda e: e.tensor_copy(out=vaug[:, tt, :, 0:64], in_=q[:, 1280:1536].rearrange("p (h d) -> p h d", h=4)),
                      reads=[q], writes=[vaug])
            kb.barrier()
        with contextlib.ExitStack() as st3:
            yatt = kb.sb("t_yatt", [128, NT, 1024], BF16, st3)
            pexp = [kb.sb(f"t_pexp{i}", [128, 512], BF16, st3) for i in range(3)]
            rinv = kb.sb("t_rinv", [128, 4], F32, st3)
            pss = [kb.ps(f"t_pss{i}", [128, 512], F32, st3) for i in range(2)]
            po = [kb.ps(f"t_po{i}", [128, 512], F32, st3) for i in range(4)]
            seq = [(h, qb, kt) for h in range(16) for qb in range(4) for kt in range(NT)]
            cv_st = [kb.sb(f"t_cvst{i}", [128, 2048], F32, st3) for i in range(2)]
            cv_bf = [kb.sb(f"t_cvbf{i}", [128, 2048], BF16, st3) for i in range(2)]

            def vconv(c):
                a, b = cv_st[c % 2], cv_bf[c % 2]
                kb.dma("sp", a[:], D.v_tab.ap()[c * 128:(c + 1) * 128, :], writes=[a])
                kb.op("dve", lambda e: e.tensor_copy(out=b[:], in_=a[:]), reads=[a], writes=[b])
                kb.dma("pool", D.vb_d.ap()[c * 128:(c + 1) * 128, :], b[:], reads=[b], writes=[D.vb_d])

            def emitS(i):
                h, qb, kt = seq[i]
                kv, c, b0 = h // 4, h // 2, (h % 2) * 64
                ps = pss[i % 2]
                kb.op("pe", lambda e: e.matmul(ps[:], lhsT=kT2[b0:b0 + 64, kv, kt * 128:(kt + 1) * 128],
                                               rhs=qT[b0:b0 + 64, c, qb * 512:(qb + 1) * 512], start=True, stop=True),
                      reads=[kT2, qT], writes=[ps])

            def emitE(i):
                ps = pss[i % 2]
                pe_ = pexp[i % 3]
                kb.op("act", lambda e: e.activation(out=pe_[:], in_=ps[:], func=AF.Exp), reads=[ps], writes=[pe_])

            def emitPV(i):
                h, qb, kt = seq[i]
                kv = h // 4
                pe_ = pexp[i % 3]
                for j in range(4):
                    kb.op("pe", lambda e: e.matmul(po[j][:, 0:65], lhsT=pe_[:, j * 128:(j + 1) * 128], rhs=vaug[:, kt, kv, :],
                                                   start=(kt == 0), stop=(kt == NT - 1)), reads=[pe_, vaug], writes=[po[j]])
                if kt == NT - 1:
                    for j in range(4):
                        kb.op("dve", lambda e: e.reciprocal(out=rinv[:, j:j + 1], in_=po[j][:, 64:65]), reads=[po[j]], writes=[rinv])
                        kb.op("dve", lambda e: e.tensor_scalar(out=yatt[:, qb * 4 + j, h * 64:(h + 1) * 64], in0=po[j][:, 0:64],
                                                               scalar1=rinv[:, j:j + 1], scalar2=None, op0=ALU.mult),
                              reads=[po[j], rinv], writes=[yatt])

            emitS(0)
            for i in range(len(seq)):
                if i + 1 < len(seq):
                    emitS(i + 1)
                emitE(i)
                emitPV(i)
                if i % 8 == 0:
                    vconv(i // 8)
            pt = [kb.ps(f"t_pt{i}", [128, 1024], BF16, st3) for i in range(2)]
            yT = [kb.sb(f"t_yT{i}", [128, 8, 128], BF16, st3) for i in range(2)]
            for tt in range(NT):
                p = pt[tt % 2]
                o = yT[tt % 2]
                for j in range(8):
                    kb.op("pe", lambda e: e.transpose(out=p[:, j * 128:(j + 1) * 128], in_=yatt[:, tt, j * 128:(j + 1) * 128],
                                                      identity=C.identb[:]), reads=[yatt, C.identb], writes=[p])
                kb.op("act", lambda e: e.copy(out=o[:], in_=p[:].rearrange("p (j t) -> p j t", j=8)), reads=[p], writes=[o])
                kb.dma("sp", D.ymT_d.ap()[8:16, :, tt * 128:(tt + 1) * 128].rearrange("j p t -> p j t"), o[:], reads=[o], writes=[D.ymT_d])
            kb.barrier()


def phase_O(kb, C, D, ym_src):
    with contextlib.ExitStack() as st:
        ymT = kb.sb("o_ymT", [128, 16, 2048], BF16, st)
        for j in range(16):
            kb.dma("sp" if j % 2 == 0 else "act", ymT[:, j, :], ym_src.ap()[j], reads=[ym_src], writes=[ymT])
        stg = [kb.sb(f"o_stg{i}", [128, 4096], F32, st) for i in range(2)]
        wo = kb.sb("o_wo", [128, 16, 512], BF16, st)
        xs = [kb.sb(f"o_xs{i}", [128, 512], F32, st) for i in range(2)]
        pss = [kb.ps(f"o_ps{i}", [128, 512], F32, st) for i in range(2)]
        w_v = D.w_out.ap().rearrange("(kc p) c -> p kc c", p=128)
        for dg in range(4):
            for hf in range(2):
                sg = stg[hf]
                sgv = sg[:].rearrange("p (dc c) -> p dc c", dc=8)
                kb.dma("sp" if hf == 0 else "act", sgv, w_v[:, hf * 8:(hf + 1) * 8, dg * 512:(dg + 1) * 512], writes=[sg])
                kb.op("pool", lambda e: e.tensor_copy(out=wo[:, hf * 8:(hf + 1) * 8, :], in_=sgv), reads=[sg], writes=[wo])
            for tt in range(NT):
                ps = pss[tt % 2]
                xt = xs[tt % 2]
                kb.dma("sp", xt[:], D.x.ap()[tt * 128:(tt + 1) * 128, dg * 512:(dg + 1) * 512], writes=[xt])
                for kc in range(16):
                    kb.op("pe", lambda e: e.matmul(ps[:], lhsT=ymT[:, kc, tt * 128:(tt + 1) * 128], rhs=wo[:, kc, :],
                                                   start=(kc == 0), stop=(kc == 15)), reads=[ymT, wo], writes=[ps])
                kb.op("dve", lambda e: e.tensor_tensor(out=xt[:], in0=ps[:], in1=xt[:], op=ALU.add), reads=[ps, xt], writes=[xt])
                kb.dma("act", D.x1_d.ap()[tt * 128:(tt + 1) * 128, dg * 512:(dg + 1) * 512], xt[:], reads=[xt], writes=[D.x1_d])
        kb.barrier()


def phase_P(kb, C, D):
    with contextlib.ExitStack() as stP:
        iota = kb.sb("p_iota", [128, 128], F32, stP)
        kb.dma("sp", iota[:], D.iota.ap(), writes=[iota])
        with contextlib.ExitStack() as st, kb.nc.named_scope("P1"):
            h2T = kb.sb("p_h2T", [128, 16, 2048], BF16, st)
            norm_to_T(kb, C, D.x1_d, D.norm2_w, h2T, "n2")
            for dc in range(16):
                kb.dma("sp" if dc % 2 == 0 else "act", D.h2T_d.ap()[dc], h2T[:, dc, :], reads=[h2T], writes=[D.h2T_d])
            skb = kb.sb("p_sk", [128, 16, 128], F32, st)
            kb.dma("sp", skb[:], D.skT.ap(), writes=[skb])
            stg = [kb.sb(f"p_stg{i}", [128, 16, 128], F32, st) for i in range(2)]
            wb = [kb.sb(f"p_wb{i}", [128, 16, 128], BF16, st) for i in range(2)]
            qTs = [kb.sb(f"p_qT{i}", [128, 2048], F32, st) for i in range(2)]
            sev = [kb.sb(f"p_sev{i}", [128, 16, 128], F32, st) for i in range(2)]
            pss = [[kb.ps(f"p_ps{i}_{j}", [128, 512], F32, st) for j in range(3)] for i in range(2)]
            w_v = D.w_pq.ap().rearrange("(dc p) c -> p dc c", p=128)
            for hp in range(16):
                sg = stg[hp % 2]
                kb.dma("sp" if hp % 2 == 0 else "act", sg[:], w_v[:, :, hp * 128:(hp + 1) * 128], writes=[sg])
                w = wb[hp % 2]
                kb.op("pool", lambda e: e.tensor_copy(out=w[:], in_=sg[:]), reads=[sg], writes=[w])
                qT = qTs[hp % 2]
                for tb in range(4):
                    ps = pss[tb % 2][0]
                    for dc in range(16):
                        kb.op("pe", lambda e: e.matmul(ps[:], lhsT=w[:, dc, :], rhs=h2T[:, dc, tb * 512:(tb + 1) * 512],
                                                       start=(dc == 0), stop=(dc == 15)), reads=[w, h2T], writes=[ps])
                    kb.op("act", lambda e: e.copy(out=qT[:, tb * 512:(tb + 1) * 512], in_=ps[:]), reads=[ps], writes=[qT])
                se = sev[hp % 2]
                for g in range(4):
                    ps = pss[g % 2][1 + (g // 2) % 2]
                    for j in range(4):
                        tt = g * 4 + j
                        kb.op("pe", lambda e: e.matmul(ps[:, j * 128:(j + 1) * 128], lhsT=qT[:, tt * 128:(tt + 1) * 128], rhs=skb[:, hp, :],
                                                       start=True, stop=True), reads=[qT, skb], writes=[ps])
                    kb.op("dve", lambda e: e.tensor_copy(out=se[:, g * 4:(g + 1) * 4, :], in_=ps[:].rearrange("p (j n) -> p j n", j=4)),
                          reads=[ps], writes=[se])
                kb.dma("pool", D.S_d.ap()[:, hp, :].rearrange("(tt p) n -> p tt n", p=128), se[:], reads=[se], writes=[D.S_d])
            kb.barrier()
        with contextlib.ExitStack() as st, kb.nc.named_scope("P23"):
            ETg = [kb.sb(f"p_ETg{g}", [128, 3, 256], F32, st) for g in range(8)]
            Ss = [kb.sb(f"p_S{i}", [128, 16, 128], F32, st) for i in range(2)]
            S2 = kb.sb("p_S2", [128, 128], F32, st)
            M16 = kb.sb("p_M16", [128, 16, 16], F32, st)
            I16u = kb.sb("p_I16u", [128, 16, 16], U32, st)
            I16f = kb.sb("p_I16f", [128, 16, 16], F32, st)
            cand = kb.sb("p_cand", [128, 8, 16, 16], F32, st)
            cand2 = kb.sb("p_cand2", [128, 256], F32, st)
            C16 = kb.sb("p_C16", [128, 8, 16], F32, st)
            CIu = kb.sb("p_CIu", [128, 8, 16], U32, st)
            IJu = kb.sb("p_IJu", [128, 2, 8, 16], U32, st)
            IJf = kb.sb("p_IJf", [128, 2, 8, 16], F32, st)
            ex = kb.sb("p_ex", [128, 8, 16], F32, st)
            Z = kb.sb("p_Z", [128, 8], F32, st)
            EG = kb.sb("p_EG", [128, 3, 8, 16], F32, st)
            eq = kb.sb("p_eq", [128, 8, 16, 16], F32, st)
            pt = kb.ps("p_pt", [128, 512], F32, st)
            Gs = kb.sb("p_Gs", [128, 128, 256], BF16, st)
            O1 = [kb.sb(f"p_O1{i}", [128, 32, 128], BF16, st) for i in range(2)]
            O2 = [kb.sb(f"p_O2{i}", [128, 32, 128], BF16, st) for i in range(2)]
            pg = [kb.ps(f"p_pg{i}", [128, 512], F32, st) for i in range(4)]
            def p2_tile(tt):
                S = Ss[tt % 2]
                kb.dma("sp", S[:].rearrange("p a n -> p (a n)"), D.S_d.ap()[tt * 128:(tt + 1) * 128].rearrange("p a n -> p (a n)"),
                       reads=[D.S_d], writes=[S])
                for hp in range(16):
                    kb.op("dve", lambda e: e.max(out=M16[:, hp, 0:8], in_=S[:, hp, :]), reads=[S], writes=[M16])
                    kb.op("dve", lambda e: e.max_index(out=I16u[:, hp, 0:8], in_max=M16[:, hp, 0:8], in_values=S[:, hp, :]),
                          reads=[S, M16], writes=[I16u])
                    kb.op("dve", lambda e: e.match_replace(out=S2[:], in_to_replace=M16[:, hp, 0:8], in_values=S[:, hp, :], imm_value=-1e30),
                          reads=[S, M16], writes=[S2])
                    kb.op("dve", lambda e: e.max(out=M16[:, hp, 8:16], in_=S2[:]), reads=[S2], writes=[M16])
                    kb.op("dve", lambda e: e.max_index(out=I16u[:, hp, 8:16], in_max=M16[:, hp, 8:16], in_values=S2[:]),
                          reads=[S2, M16], writes=[I16u])
                kb.op("pool", lambda e: e.tensor_copy(out=I16f[:], in_=I16u[:]), reads=[I16u], writes=[I16f])
                M4 = M16[:].rearrange("p (h q) k -> p h q k", q=2)
                I4 = I16f[:].rearrange("p (h q) k -> p h q k", q=2)
                kb.op("pool", lambda e: e.tensor_tensor(out=cand[:], in0=M4[:, :, 0, :, None].to_broadcast([128, 8, 16, 16]),
                                                        in1=M4[:, :, 1, None, :].to_broadcast([128, 8, 16, 16]), op=ALU.add),
                      reads=[M16], writes=[cand])
                for h in range(8):
                    ch = cand[:, h, :, :].rearrange("p a b -> p (a b)")
                    kb.op("dve", lambda e: e.max(out=C16[:, h, 0:8], in_=ch), reads=[cand], writes=[C16])
                    kb.op("dve", lambda e: e.max_index(out=CIu[:, h, 0:8], in_max=C16[:, h, 0:8], in_values=ch), reads=[cand, C16], writes=[CIu])
                    kb.op("dve", lambda e: e.match_replace(out=cand2[:], in_to_replace=C16[:, h, 0:8], in_values=ch, imm_value=-1e30),
                          reads=[cand, C16], writes=[cand2])
                    kb.op("dve", lambda e: e.max(out=C16[:, h, 8:16], in_=cand2[:]), reads=[cand2], writes=[C16])
                    kb.op("dve", lambda e: e.max_index(out=CIu[:, h, 8:16], in_max=C16[:, h, 8:16], in_values=cand2[:]),
                          reads=[cand2, C16], writes=[CIu])
                kb.op("pool", lambda e: e.tensor_tensor(out=ex[:], in0=C16[:], in1=C16[:, :, 0:1].to_broadcast([128, 8, 16]), op=ALU.subtract),
                      reads=[C16], writes=[ex])
                kb.op("act", lambda e: e.activation(out=ex[:], in_=ex[:], func=AF.Exp), reads=[ex], writes=[ex])
                kb.op("dve", lambda e: e.tensor_reduce(out=Z[:], in_=ex[:], axis=AX.X, op=ALU.add), reads=[ex], writes=[Z])
                kb.op("dve", lambda e: e.reciprocal(out=Z[:], in_=Z[:]), reads=[Z], writes=[Z])
                kb.op("dve", lambda e: e.tensor_tensor(out=EG[:, 2], in0=ex[:], in1=Z[:, :, None].to_broadcast([128, 8, 16]), op=ALU.mult),
                      reads=[ex, Z], writes=[EG])
                kb.op("dve", lambda e: e.tensor_single_scalar(out=IJu[:, 0], in_=CIu[:], scalar=4, op=ALU.logical_shift_right),
                      reads=[CIu], writes=[IJu])
                kb.op("dve", lambda e: e.tensor_single_scalar(out=IJu[:, 1], in_=CIu[:], scalar=15, op=ALU.bitwise_and),
                      reads=[CIu], writes=[IJu])
                kb.op("pool", lambda e: e.tensor_copy(out=IJf[:], in_=IJu[:]), reads=[IJu], writes=[IJf])
                for q in range(2):
                    kb.op("dve", lambda e: e.tensor_tensor(out=eq[:], in0=iota[:, None, None, 0:16].to_broadcast([128, 8, 16, 16]),
                                                            in1=IJf[:, q, :, :, None].to_broadcast([128, 8, 16, 16]), op=ALU.is_equal),
                          reads=[iota, IJf], writes=[eq])
                    kb.op("pool", lambda e: e.tensor_tensor(out=eq[:], in0=eq[:], in1=I4[:, :, q, None, :].to_broadcast([128, 8, 16, 16]),
                                                            op=ALU.mult), reads=[eq, I16f], writes=[eq])
                    kb.op("dve", lambda e: e.tensor_reduce(out=EG[:, q], in_=eq[:], axis=AX.X, op=ALU.add), reads=[eq], writes=[EG])
                for a in range(3):
                    kb.op("pe", lambda e: e.transpose(out=pt[:, a * 128:(a + 1) * 128], in_=EG[:, a].rearrange("p h k -> p (h k)"),
                                                      identity=C.ident[:]), reads=[EG, C.ident], writes=[pt])
                kb.op("act", lambda e: e.copy(out=ETg[tt // 2][:, :, (tt % 2) * 128:(tt % 2 + 1) * 128], in_=pt[:, 0:384].rearrange("p (a t) -> p a t", a=3)),
                      reads=[pt], writes=[ETg[tt // 2]])

            itc = [0]

            def p3_group(tg):
                for sub in range(8):
                    t0 = sub * 32
                    ET = ETg[tg]
                    o1 = O1[sub % 2]
                    o2 = O2[sub % 2]
                    iob = iota[:, None, :].to_broadcast([128, 32, 128])
                    kb.op("dve", lambda e: e.tensor_tensor(out=o1[:], in0=iob, in1=ET[:, 0, t0:t0 + 32, None].to_broadcast([128, 32, 128]),
                                                            op=ALU.is_equal), reads=[iota, ETg[tg]], writes=[o1])
                    kb.op("dve", lambda e: e.tensor_tensor(out=o2[:], in0=iob, in1=ET[:, 1, t0:t0 + 32, None].to_broadcast([128, 32, 128]),
                                                           op=ALU.is_equal), reads=[iota, ETg[tg]], writes=[o2])
                    kb.op("pool", lambda e: e.tensor_tensor(out=o2[:], in0=o2[:], in1=ET[:, 2, t0:t0 + 32, None].to_broadcast([128, 32, 128]),
                                                            op=ALU.mult), reads=[o2, ETg[tg]], writes=[o2])
                    for q4 in range(8):
                        p = pg[itc[0] % 4]
                        itc[0] += 1
                        for j in range(4):
                            tl = q4 * 4 + j
                            kb.op("pe", lambda e: e.matmul(p[:].rearrange("p (e t) -> p e t", t=4)[:, :, j], lhsT=o2[:, tl, :], rhs=o1[:, tl, :],
                                                           start=True, stop=True), reads=[o1, o2], writes=[p])
                        tl0 = sub * 32 + q4 * 4
                        dst = Gs[:, :, tl0:tl0 + 4]
                        src = p[:].rearrange("p (e t) -> p e t", t=4)
                        kb.op("act", lambda e: e.copy(out=dst, in_=src), reads=[p], writes=[Gs])
                for k8 in range(8):
                    kb.dma(["sp", "act", "pool"][k8 % 3],
                           D.G_d.ap()[k8 * 16:(k8 + 1) * 16, :, tg * 256:(tg + 1) * 256].rearrange("e1 e2 t -> e2 e1 t"),
                           Gs[:, k8 * 16:(k8 + 1) * 16, :], reads=[Gs], writes=[D.G_d])

            for g in range(8):
                p2_tile(2 * g)
                p2_tile(2 * g + 1)
                if g >= 1:
                    p3_group(g - 1)
            p3_group(7)
            kb.barrier()
        with contextlib.ExitStack() as st, kb.nc.named_scope("P4"):
            h2T = kb.sb("p_h2Tb", [128, 16, 2048], BF16, st)
            for dc in range(16):
                kb.dma("sp" if dc % 2 == 0 else "act", h2T[:, dc, :], D.h2T_d.ap()[dc], reads=[D.h2T_d], writes=[h2T])
            stg = [kb.sb(f"p4_stg{i}", [128, 16, 128], F32, st) for i in range(2)]
            ub = [kb.sb(f"p4_ub{i}", [128, 16, 128], BF16, st) for i in range(2)]
            Gc = [kb.sb(f"p4_Gc{i}", [128, 2048], BF16, st) for i in range(2)]
            ge = [kb.sb(f"p4_ge{i}", [128, 2048], BF16, st) for i in range(2)]
            pss = [[kb.ps(f"p4_ps{i}_{j}", [128, 512], F32, st) for j in range(4)] for i in range(2)]
            def load(e1):
                sg = stg[e1 % 2]
                kb.dma("sp", sg[:].rearrange("p a b -> p (a b)"), D.u_tabT.ap()[e1], writes=[sg])
                g = Gc[e1 % 2]
                kb.dma("pool", g[:], D.G_d.ap()[e1], reads=[D.G_d], writes=[g])
            load(0)
            for e1 in range(128):
                if e1 + 1 < 128:
                    load(e1 + 1)
                sg = stg[e1 % 2]
                u = ub[e1 % 2]
                kb.op("dve", lambda e: e.tensor_copy(out=u[:], in_=sg[:]), reads=[sg], writes=[u])
                g = Gc[e1 % 2]
                ps = pss[e1 % 2]
                a = ge[e1 % 2]
                for tb in range(4):
                    for dc in range(16):
                        kb.op("pe", lambda e: e.matmul(ps[tb][:], lhsT=u[:, dc, :], rhs=h2T[:, dc, tb * 512:(tb + 1) * 512],
                                                       start=(dc == 0), stop=(dc == 15)), reads=[u, h2T], writes=[ps[tb]])
                    kb.op("act", lambda e: e.activation(out=a[:, tb * 512:(tb + 1) * 512], in_=ps[tb][:], func=AF.Gelu), reads=[ps[tb]], writes=[a])
                kb.op("pool", lambda e: e.tensor_tensor(out=a[:], in0=a[:], in1=g[:], op=ALU.mult), reads=[a, g], writes=[a])
                kb.dma("sp", D.A_d.ap()[e1], a[:], reads=[a], writes=[D.A_d])
            kb.barrier()
        with contextlib.ExitStack() as st, kb.nc.named_scope("P5"):
            NB = 4
            vb = [kb.sb(f"p5_vb{i}", [128, 512], BF16, st) for i in range(NB)]
            ac = [kb.sb(f"p5_ac{i}", [128, 1024], BF16, st) for i in range(NB)]
            ov = [kb.sb(f"p5_ov{i}", [128, 512], F32, st) for i in range(2)]
            pss = [[kb.ps(f"p5_ps{j}_{h}", [128, 512], F32, st) for h in range(2)] for j in range(4)]
            seq = [(tb2, dg, e1) for tb2 in range(2) for dg in range(4) for e1 in range(128)]

            def load(i):
                tb2, dg, e1 = seq[i]
                k = i % NB
                kb.dma("sp", vb[k][:], D.vb_d.ap()[e1 * 128:(e1 + 1) * 128, dg * 512:(dg + 1) * 512], reads=[D.vb_d], writes=[vb[k]])
                kb.dma("pool", ac[k][:], D.A_d.ap()[e1][:, tb2 * 1024:(tb2 + 1) * 1024], reads=[D.A_d], writes=[ac[k]])
            load(0)
            load(1)
            load(2)
            oi = 0
            for i, (tb2, dg, e1) in enumerate(seq):
                if i + 3 < len(seq):
                    load(i + 3)
                k = i % NB
                for j in range(4):
                    for h in range(2):
                        kb.op("pe", lambda e: e.matmul(pss[j][h][:], lhsT=vb[k][:, j * 128:(j + 1) * 128], rhs=ac[k][:, h * 512:(h + 1) * 512],
                                                       start=(e1 == 0), stop=(e1 == 127)), reads=[vb[k], ac[k]], writes=[pss[j][h]])
                if e1 == 127:
                    for j in range(4):
                        for h in range(2):
                            o = ov[oi % 2]
                            if oi % 2 == 0:
                                kb.op("act", lambda e: e.copy(out=o[:], in_=pss[j][h][:]), reads=[pss[j][h]], writes=[o])
                            else:
                                kb.op("dve", lambda e: e.tensor_copy(out=o[:], in_=pss[j][h][:]), reads=[pss[j][h]], writes=[o])
                            oi += 1
                            d0 = dg * 512 + j * 128
                            t0 = tb2 * 1024 + h * 512
                            kb.dma("sp", D.peT_d.ap()[d0:d0 + 128, t0:t0 + 512], o[:], reads=[o], writes=[D.peT_d])
            kb.barrier()


def phase_F(kb, C, D):
    with contextlib.ExitStack() as st:
        wb = kb.sb("f_wb", [128, 2048], F32, st)
        kb.dma("pool", wb[:], D.normf_w.ap().partition_broadcast(128), writes=[wb])
        xs = [kb.sb(f"f_x{i}", [128, 2048], F32, st) for i in range(2)]
        pes = [kb.sb(f"f_pe{i}", [128, 16, 128], F32, st) for i in range(2)]
        junk = kb.sb("f_junk", [128, 2048], BF16, st)
        ss = kb.sb("f_ss", [128, NT], F32, st)
        rs = kb.sb("f_rs", [128, NT], F32, st)
        kb.op("pool", lambda e: e.memset(ss[:], 0.0), writes=[ss])
        pss = [[kb.ps(f"f_ps{i}_{j}", [128, 512], F32, st) for j in range(4)] for i in range(2)]
        pe_v = D.peT_d.ap().rearrange("(dc p) t -> p dc t", p=128)
        for tt in range(NT):
            xt = xs[tt % 2]
            pe = pes[tt % 2]
            ps = pss[tt % 2]
            kb.dma("sp", xt[:], D.x1_d.ap()[tt * 128:(tt + 1) * 128, :], reads=[D.x1_d], writes=[xt])
            kb.dma("act", pe[:], pe_v[:, :, tt * 128:(tt + 1) * 128], reads=[D.peT_d], writes=[pe])
            for dc in range(16):
                kb.op("pe", lambda e: e.transpose(out=ps[dc // 4][:, (dc % 4) * 128:(dc % 4 + 1) * 128], in_=pe[:, dc, :], identity=C.ident[:]),
                      reads=[pe, C.ident], writes=[ps[dc // 4]])
            for j in range(4):
                kb.op("dve", lambda e: e.tensor_tensor(out=xt[:, j * 512:(j + 1) * 512], in0=ps[j][:], in1=xt[:, j * 512:(j + 1) * 512], op=ALU.add),
                      reads=[ps[j], xt], writes=[xt])
            kb.op("act", lambda e: e.activation(out=junk[:], in_=xt[:], func=AF.Square, accum_out=ss[:, tt:tt + 1]), reads=[xt], writes=[junk, ss])
            kb.op("dve", lambda e: e.tensor_scalar(out=rs[:, tt:tt + 1], in0=ss[:, tt:tt + 1], scalar1=1.0 / 2048, scalar2=EPS,
                                                   op0=ALU.mult, op1=ALU.add), reads=[ss], writes=[rs])
            kb.op("act", lambda e: e.activation(out=rs[:, tt:tt + 1], in_=rs[:, tt:tt + 1], func=AF.Sqrt), reads=[rs], writes=[rs])
            kb.op("dve", lambda e: e.reciprocal(out=rs[:, tt:tt + 1], in_=rs[:, tt:tt + 1]), reads=[rs], writes=[rs])
            kb.op("dve", lambda e: e.scalar_tensor_tensor(out=xt[:], in0=xt[:], scalar=rs[:, tt:tt + 1], in1=wb[:], op0=ALU.mult, op1=ALU.mult),
                  reads=[xt, rs, wb], writes=[xt])
            kb.dma("sp", D.out.ap()[tt * 128:(tt + 1) * 128, :], xt[:], reads=[xt], writes=[D.out])
        kb.barrier()


STOP = 0
FAST_F32 = True
F32R = mybir.dt.float32r
def fr(ap):
    return ap.bitcast(F32R) if FAST_F32 else ap
class StopBuild(Exception):
    pass
def chk(n):
    if STOP == n:
        raise StopBuild()
GN_EPS = 64e-5
NEG_E05 = -0.6065306597126334


def phase_Rpre(kb, C, D):
    with contextlib.ExitStack() as st:
        rw = kb.sb("rp_rw", [128, 8, 8], F32, st)
        kb.dma("sp", rw[:], D.rw_c.ap(), writes=[rw])
        tmp = kb.sb("rp_tmp", [128, 2048], F32, st)
        lin = [kb.sb(f"rp_lin{i}", [128, 2048], BF16, st) for i in range(4)]
        for i in range(4):
            kb.op("pool", lambda e: e.memset(lin[i][:], 0.0), writes=[lin[i]])
            r0 = 3072 + i * 96
            kb.dma("sp", tmp[0:96, :], D.zs_d.ap()[r0:r0 + 96, :], reads=[D.zs_d], writes=[tmp])
            if i < 2:
                kb.op("act", lambda e: e.activation(out=lin[i][0:96, :], in_=tmp[0:96, :], func=AF.Tanh), reads=[tmp], writes=[lin[i]])
            else:
                kb.op("act", lambda e: e.copy(out=lin[i][0:96, :], in_=tmp[0:96, :]), reads=[tmp], writes=[lin[i]])
        sgl = kb.sb("rp_sgl", [128, 2, 2048], BF16, st)
        for kc in range(2):
            kb.dma("sp", tmp[:], D.zs_d.ap()[3456 + kc * 128:3456 + (kc + 1) * 128, :], reads=[D.zs_d], writes=[tmp])
            kb.op("act", lambda e: e.activation(out=sgl[:, kc, :], in_=tmp[:], func=AF.Sigmoid), reads=[tmp], writes=[sgl])
        wst = kb.sb("rp_wst", [128, 2048], F32, st)
        w2b = kb.sb("rp_w2b", [128, 2, 1024], BF16, st)
        a2b = kb.sb("rp_a2b", [128, 2, 1024], BF16, st)
        g2b = kb.sb("rp_g2b", [128, 2, 1024], BF16, st)
        wv = wst[:].rearrange("p (a c) -> p a c", a=2)
        kb.op("pool", lambda e: e.memset(w2b[:], 0.0), writes=[w2b])
        kb.op("pool", lambda e: e.memset(a2b[:], 0.0), writes=[a2b])
        kb.dma("sp", wv[0:96], D.w2.ap().rearrange("d l c -> l d c"), writes=[wst])
        kb.op("pool", lambda e: e.tensor_copy(out=w2b[0:96], in_=wv[0:96]), reads=[wst], writes=[w2b])
        kb.dma("sp", wv[0:96], D.a2.ap().rearrange("d l c -> l d c"), writes=[wst])
        kb.op("pool", lambda e: e.tensor_copy(out=a2b[0:96], in_=wv[0:96]), reads=[wst], writes=[a2b])
        kb.dma("sp", wv, D.g2.ap().rearrange("(kc p) c -> p kc c", p=128), writes=[wst])
        kb.op("pool", lambda e: e.tensor_copy(out=g2b[:], in_=wv), reads=[wst], writes=[g2b])
        outs = [kb.sb(f"rp_o{i}", [128, 2048], F32, st) for i in range(2)]
        pss = [[kb.ps(f"rp_ps{i}_{j}", [128, 512], F32, st) for j in range(4)] for i in range(2)]
        it = 0
        for cc in range(8):
            for d in range(2):
                for which in range(2):
                    ps = pss[it % 2]
                    o = outs[it % 2]
                    it += 1
                    wmat = w2b if which == 0 else a2b
                    xin = lin[d] if which == 0 else lin[2 + d]
                    bias = rw[:, cc, d:d + 1] if which == 0 else rw[:, cc, 2 + d:3 + d]
                    for tb in range(4):
                        kb.op("pe", lambda e: e.matmul(ps[tb][:], lhsT=wmat[:, d, cc * 128:(cc + 1) * 128], rhs=xin[:, tb * 512:(tb + 1) * 512],
                                                       start=True, stop=True), reads=[wmat, xin], writes=[ps[tb]])
                        kb.op("act", lambda e: e.activation(out=o[:, tb * 512:(tb + 1) * 512], in_=ps[tb][:], func=AF.Sigmoid, bias=bias),
                              reads=[ps[tb], rw], writes=[o])
                    if which == 0:
                        kb.op("dve", lambda e: e.tensor_scalar(out=o[:], in0=o[:], scalar1=NEG_E05, scalar2=None, op0=ALU.mult), reads=[o], writes=[o])
                        kb.dma("sp", D.ld_d.ap()[d, cc * 128:(cc + 1) * 128, :], o[:], reads=[o], writes=[D.ld_d])
                    else:
                        kb.dma("sp", D.a_d.ap()[d, cc * 128:(cc + 1) * 128, :], o[:], reads=[o], writes=[D.a_d])
        for tt in range(NT):
            ps = pss[tt % 2]
            o = outs[tt % 2]
            for hf in range(2):
                for kc in range(2):
                    kb.op("pe", lambda e: e.matmul(ps[hf][:], lhsT=sgl[:, kc, tt * 128:(tt + 1) * 128], rhs=g2b[:, kc, hf * 512:(hf + 1) * 512],
                                                   start=(kc == 0), stop=(kc == 1)), reads=[sgl, g2b], writes=[ps[hf]])
                kb.op("act", lambda e: e.copy(out=o[:, hf * 512:(hf + 1) * 512], in_=ps[hf][:]), reads=[ps[hf]], writes=[o])
            kb.dma("sp", D.g_d.ap()[tt * 128:(tt + 1) * 128, :], o[:, 0:1024], reads=[o], writes=[D.g_d])
        kb.barrier()


def phase_R(kb, C, D, ccs=range(8), dbg=None):
    with contextlib.ExitStack() as st:
        rw = kb.sb("r_rw", [128, 8, 8], F32, st)
        kb.dma("sp", rw[:], D.rw_c.ap(), writes=[rw])
        masks = kb.sb("r_masks", [128, 6, 128], F32, st)
        kb.dma("sp", masks[:], D.masks.ap(), writes=[masks])
        cst = kb.sb("r_cst", [128, 66], F32, st)
        kb.dma("sp", cst[:], D.rcst.ap()[:, 0:66], writes=[cst])
        segt = kb.sb("r_segm", [128, 2048], BF16, st)
        kb.dma("sp", segt[:], D.segm.ap(), writes=[segt])
        ident2 = cst[:, 0:64]
        sel = cst[:, 64:66]
        lnw = kb.sb("r_lnw", [128, 128], F32, st)
        lnb = kb.sb("r_lnb", [128, 128], F32, st)
        Rr = kb.sb("r_R", [128, 2048], F32, st)
        Kk = kb.sb("r_K", [128, 2048], F32, st)
        Vv = kb.sb("r_V", [128, 2048], F32, st)
        KKn = kb.sb("r_KK", [128, 2048], F32, st)
        E1 = kb.sb("r_E1", [128, 2048], F32, st)
        XI = kb.sb("r_XI", [128, 2048], F32, st)
        XE = kb.sb("r_XE", [128, 2048], F32, st)
        Aa = kb.sb("r_A", [128, 2048], F32, st)
        KD = kb.sb("r_KD", [128, 2048], F32, st)
        KT = kb.sb("r_KT", [128, 2048], F32, st)
        AT = kb.sb("r_AT", [128, 2048], F32, st)
        BTb = kb.sb("r_BTb", [128, 2048], F32, st)
        stt = kb.sb("r_stt", [128, 160], F32, st)
        Vtm = kb.sb("r_Vtm", [128, NT, 128], F32, st)
        Ysum = kb.sb("r_Ysum", [128, NT, 128], F32, st)
        MTa = kb.sb("r_MTa", [128, 32, 64], F32, st)
        Ca = kb.sb("r_Ca", [128, 32, 64], F32, st)
        H = [kb.sb(f"r_H{i}", [128, 64], F32, st) for i in range(2)]
        tot = kb.sb("r_tot", [128, 32], F32, st)
        GL = kb.sb("r_GL", [128, 32], F32, st)
        RhT = Vv

        class Set:
            pass
        sets = []
        for p in range(2):
            S = Set()
            S.XA = kb.sb(f"r_XA{p}", [128, 2, 2, 128], F32, st)
            S.KBm = kb.sb(f"r_KBm{p}", [128, 2, 2, 128], F32, st)
            S.PQ = [kb.sb(f"r_PQ{p}_{i}", [128, 2, 2, 128], BF16, st) for i in range(2)]
            S.QT = [kb.sb(f"r_QT{p}_{i}", [128, 2, 128], BF16, st) for i in range(2)]
            S.XT = kb.sb(f"r_XT{p}", [128, 2, 128], F32, st)
            S.TT = kb.sb(f"r_TT{p}", [128, 2, 128], F32, st)
            S.BW = kb.sb(f"r_BW{p}", [128, 2, 128], F32, st)
            S.BU = kb.sb(f"r_BU{p}", [128, 2, 128], F32, st)
            S.AGtm = kb.sb(f"r_AGtm{p}", [128, 128], F32, st)
            S.KGtm = kb.sb(f"r_KGtm{p}", [128, 128], F32, st)
            S.B = [kb.ps(f"r_bk{p}_{i}", [128, 512], F32, st) for i in range(4)]
            sets.append(S)
        psX = sets[0].B[0]
        bkH = [sets[0].B[1], sets[0].B[2]]

        v4 = lambda b: b[:].rearrange("p (h q s) -> p h q s", h=2, q=2)
        v3 = lambda b, lo: b[:, lo:lo + 256].rearrange("p (h s) -> p h s", h=2)

        def tile_gen(S, d, tt, RT, BT):
            M2 = masks[:, 2 * d:2 * d + 2, :]
            MST = masks[:, 2 - 2 * d, :]
            XA, KBm, PQ, QT, BW, BU, AGtm, KGtm = S.XA, S.KBm, S.PQ, S.QT, S.BW, S.BU, S.AGtm, S.KGtm
            B0, B1, B2, B3 = S.B
            psA, psB, psN = v4(B0), v4(B1), v4(B3)
            psC = v3(B2, 0)
            cols = slice(tt * 128, (tt + 1) * 128)
            for hh in range(2):
                pr = slice(hh * 64, hh * 64 + 64)
                kb.pe_fence()
                kb.op("pe", lambda e: e.matmul(psA[:, hh, 0, :], lhsT=fr(AT[pr, cols]), rhs=fr(BT[pr, cols]), start=True, stop=True),
                      reads=[AT, BT], writes=[B0])
                kb.op("pe", lambda e: e.matmul(psA[:, hh, 1, :], lhsT=fr(AT[pr, cols]), rhs=fr(RT[pr, cols]), start=True, stop=True),
                      reads=[AT, RT], writes=[B0])
                kb.op("pe", lambda e: e.matmul(psB[:, hh, 0, :], lhsT=fr(KT[pr, cols]), rhs=fr(BT[pr, cols]), start=True, stop=True),
                      reads=[KT, BT], writes=[B1])
                kb.op("pe", lambda e: e.matmul(psB[:, hh, 1, :], lhsT=fr(KT[pr, cols]), rhs=fr(RT[pr, cols]), start=True, stop=True),
                      reads=[KT, RT], writes=[B1])
                kb.op("pe", lambda e: e.matmul(psC[:, hh, :], lhsT=fr(BT[pr, cols]), rhs=fr(AT[pr, cols]), start=True, stop=True),
                      reads=[AT, BT], writes=[B2])
            kb.pe_fence()
            yield
            M2b = M2[:, None, :, :].to_broadcast([128, 2, 2, 128])
            q0, q1 = S.XT, QT[1]
            kb.op("dve", lambda e: e.tensor_tensor(out=fr(XA[:]), in0=psA, in1=M2b, op=ALU.mult), reads=[B0, masks], writes=[XA])
            kb.op("dve", lambda e: e.tensor_tensor(out=fr(q0[:]), in0=psC, in1=MST[:, None, :].to_broadcast([128, 2, 128]), op=ALU.mult),
                  reads=[B2, masks], writes=[q0])
            kb.op("dve", lambda e: e.tensor_tensor(out=fr(KBm[:]), in0=psB, in1=M2b, op=ALU.mult), reads=[B1, masks], writes=[KBm])
            pq = PQ[0]
            kb.op("pool", lambda e: e.tensor_tensor(out=pq[:, :, 0, :], in0=XA[:, :, 0, :], in1=C.ident[:, None, :].to_broadcast([128, 2, 128]),
                                                    op=ALU.add), reads=[XA, C.ident], writes=[pq])
            yield
            for hh in range(2):
                kb.op("pe", lambda e: e.matmul(psN[:, hh, 1, :], lhsT=fr(q0[:, hh, :]), rhs=fr(XA[:, hh, 0, :]), start=True, stop=True),
                      reads=[q0, XA], writes=[B3])
                kb.op("pe", lambda e: e.matmul(psC[:, hh, :], lhsT=fr(XA[:, hh, 0, :]), rhs=fr(q0[:, hh, :]), start=True, stop=True),
                      reads=[q0, XA], writes=[B2])
            yield
            kb.op("act", lambda e: e.copy(out=pq[:, :, 1, :], in_=psN[:, :, 1, :]), reads=[B3], writes=[pq])
            kb.op("dve", lambda e: e.tensor_copy(out=q1[:], in_=psC), reads=[B2], writes=[q1])
            yield
            cur = 0
            qcur = 1
            for lev in range(1, 6):
                pq = PQ[cur]
                pqn = PQ[1 - cur]
                qt = QT[qcur]
                qtn = QT[1 - qcur]
                last = (lev == 5)
                for hh in range(2):
                    if last:
                        kb.op("pe", lambda e: e.matmul(psN[:, hh, 0, :], lhsT=qt[:, hh, :], rhs=pq[:, hh, 0, :], start=True, stop=True),
                              reads=[qt, pq], writes=[B3])
                    else:
                        kb.op("pe", lambda e: e.matmul(psN[:, hh, :, :], lhsT=qt[:, hh, :], rhs=pq[:, hh, :, :], start=True, stop=True),
                              reads=[qt, pq], writes=[B3])
                        kb.op("pe", lambda e: e.matmul(psC[:, hh, :], lhsT=pq[:, hh, 1, :], rhs=qt[:, hh, :], start=True, stop=True),
                              reads=[qt, pq], writes=[B2])
                yield
                if last:
                    kb.op("dve", lambda e: e.tensor_tensor(out=fr(S.TT[:]), in0=psN[:, :, 0, :], in1=pq[:, :, 0, :], op=ALU.add),
                          reads=[B3, pq], writes=[S.TT])
                else:
                    kb.op("dve", lambda e: e.tensor_tensor(out=pqn[:, :, 0, :], in0=psN[:, :, 0, :], in1=pq[:, :, 0, :], op=ALU.add),
                          reads=[B3, pq], writes=[pqn])
                    kb.op("act", lambda e: e.copy(out=pqn[:, :, 1, :], in_=psN[:, :, 1, :]), reads=[B3], writes=[pqn])
                    kb.op("dve", lambda e: e.tensor_copy(out=qtn[:], in_=psC), reads=[B2], writes=[qtn])
                yield
                cur = 1 - cur
                qcur = 1 - qcur
            TT = S.TT
            W_ = B0[:, 0:128].rearrange("p (h i) -> p h i", h=2)
            Bt_ = B0[:, 128:256]
            U_ = B0[:, 256:512].rearrange("p (h i) -> p h i", h=2)
            for hh in range(2):
                kb.op("pe", lambda e: e.matmul(W_[:, hh, :], lhsT=fr(KBm[:, hh, 0, :]), rhs=fr(Vtm[:, tt, hh * 64:(hh + 1) * 64]), start=True, stop=True),
                      reads=[KBm, Vtm], writes=[B0])
            kb.op("pe", lambda e: e.transpose(out=Bt_, in_=BT[:, cols], identity=C.ident[:]), reads=[BT, C.ident], writes=[B0])
            AG_ = B1[:, 256:384]
            KG_ = B1[:, 384:512]
            kb.op("pe", lambda e: e.transpose(out=AG_, in_=Aa[:, cols], identity=C.ident[:]), reads=[Aa, C.ident], writes=[B1])
            kb.op("pe", lambda e: e.transpose(out=KG_, in_=KD[:, cols], identity=C.ident[:]), reads=[KD, C.ident], writes=[B1])
            yield
            kb.op("act", lambda e: e.copy(out=fr(BW[:, :, 64:128]), in_=W_), reads=[B0], writes=[BW])
            kb.op("dve", lambda e: e.tensor_copy(out=fr(BW[:, :, 0:64]), in_=Bt_.rearrange("p (h j) -> p h j", h=2)), reads=[B0], writes=[BW])
            kb.op("act", lambda e: e.copy(out=AGtm[:], in_=AG_), reads=[B1], writes=[AGtm])
            kb.op("act", lambda e: e.copy(out=KGtm[:], in_=KG_), reads=[B1], writes=[KGtm])
            yield
            for hh in range(2):
                kb.op("pe", lambda e: e.matmul(U_[:, hh, :], lhsT=fr(TT[:, hh, :]), rhs=fr(BW[:, hh, :]), start=True, stop=True),
                      reads=[TT, BW], writes=[B0])
            yield
            kb.op("dve", lambda e: e.tensor_copy(out=fr(BU[:]), in_=U_), reads=[B0], writes=[BU])
            yield
            Y_ = B1[:, 0:128].rearrange("p (h i) -> p h i", h=2)
            R_ = B1[:, 128:256]
            for hh in range(2):
                kb.op("pe", lambda e: e.matmul(Y_[:, hh, :], lhsT=fr(XA[:, hh, 1, :]), rhs=fr(BU[:, hh, 64:128]), start=True, stop=False),
                      reads=[XA, BU], writes=[B1])
                kb.op("pe", lambda e: e.matmul(Y_[:, hh, :], lhsT=fr(KBm[:, hh, 1, :]), rhs=fr(Vtm[:, tt, hh * 64:(hh + 1) * 64]), start=False, stop=True),
                      reads=[KBm, Vtm], writes=[B1])
            for hh in range(2):
                kb.op("pe", lambda e: e.matmul(R_[hh * 64:(hh + 1) * 64, :], lhsT=BU[:, hh, 0:64], rhs=XA[:, hh, 1, :], start=True, stop=True),
                      reads=[BU, XA], writes=[B1])
            for n in range(2):
                tr = slice(n * 64, n * 64 + 64)
                bk = (B2, B3)[n]
                for hh in range(2):
                    pr = slice(hh * 64, hh * 64 + 64)
                    kb.op("pe", lambda e: e.matmul(bk[pr, 0:64], lhsT=BU[tr, hh, 0:64], rhs=AGtm[tr, pr], start=True, stop=True),
                          reads=[BU, AGtm], writes=[bk])
                    kb.op("pe", lambda e: e.matmul(bk[pr, 64:128], lhsT=AGtm[tr, pr], rhs=BU[tr, hh, 64:128], start=True, stop=False),
                          reads=[BU, AGtm], writes=[bk])
                    kb.op("pe", lambda e: e.matmul(bk[pr, 64:128], lhsT=KGtm[tr, pr], rhs=Vtm[tr, tt, pr], start=False, stop=True),
                          reads=[KGtm, Vtm], writes=[bk])
            kb.pe_fence()
            yield
            if d == 0:
                kb.op("act", lambda e: e.copy(out=Ysum[:, tt, :], in_=B1[:, 0:128]), reads=[B1], writes=[Ysum])
            else:
                kb.op("dve", lambda e: e.tensor_tensor(out=Ysum[:, tt, :], in0=B1[:, 0:128], in1=Ysum[:, tt, :], op=ALU.add),
                      reads=[B1, Ysum], writes=[Ysum])
            kb.op("dve", lambda e: e.tensor_tensor(out=RhT[:, cols], in0=R_, in1=RT[:, cols], op=ALU.add), reads=[B1, RT], writes=[RhT])
            for n in range(2):
                ch = tt * 2 + n
                bk = (B2, B3)[n]
                kb.op("dve", lambda e: e.scalar_tensor_tensor(out=MTa[:, ch, :], in0=ident2, scalar=GL[:, ch:ch + 1], in1=bk[:, 0:64],
                                                              op0=ALU.mult, op1=ALU.add), reads=[cst, GL, bk], writes=[MTa])
                kb.op("act", lambda e: e.copy(out=Ca[:, ch, :], in_=bk[:, 64:128]), reads=[bk], writes=[Ca])
            yield

        def run_tiles(d, RT, BT):
            pending = list(range(NT))
            active = []
            free_sets = [sets[0], sets[1]]
            S = free_sets.pop(0)
            g = tile_gen(S, d, pending.pop(0), RT, BT)
            active.append((g, S))
            for _ in range(8):
                next(g)
            while active or pending:
                if pending and free_sets:
                    S = free_sets.pop(0)
                    active.append((tile_gen(S, d, pending.pop(0), RT, BT), S))
                for item in list(active):
                    g, S = item
                    try:
                        next(g)
                    except StopIteration:
                        active.remove(item)
                        free_sets.append(S)

        for cc in ccs:
            rows = slice(cc * 128, (cc + 1) * 128)
            kb.dma("sp", Rr[:], D.zs_d.ap()[cc * 128:(cc + 1) * 128, :], reads=[D.zs_d], writes=[Rr])
            kb.dma("act", Kk[:], D.zs_d.ap()[1024 + cc * 128:1024 + (cc + 1) * 128, :], reads=[D.zs_d], writes=[Kk])
            kb.dma("sp", Vv[:], D.zs_d.ap()[2048 + cc * 128:2048 + (cc + 1) * 128, :], reads=[D.zs_d], writes=[Vv])
            kb.dma("pool", lnw[:], D.lnx_w.ap()[cc * 128:(cc + 1) * 128].partition_broadcast(128), writes=[lnw])
            kb.dma("pool", lnb[:], D.lnx_b.ap()[cc * 128:(cc + 1) * 128].partition_broadcast(128), writes=[lnb])
            for g in range(4):
                for j in range(4):
                    tt = g * 4 + j
                    kb.op("pe", lambda e: e.transpose(out=psX[:, j * 128:(j + 1) * 128], in_=Vv[:, tt * 128:(tt + 1) * 128], identity=C.ident[:]),
                          reads=[Vv, C.ident], writes=[psX])
                kb.op("act", lambda e: e.copy(out=fr(Vtm[:, g * 4:(g + 1) * 4, :]), in_=psX[:].rearrange("p (j c) -> p j c", j=4)), reads=[psX], writes=[Vtm])
            kb.op("act", lambda e: e.activation(out=KKn[:], in_=Kk[:], func=AF.Copy, scale=rw[:, cc, 4:5]),
                  reads=[Kk, rw], writes=[KKn])
            kb.op("act", lambda e: e.activation(out=XI[:], in_=KKn[:], func=AF.Square), reads=[KKn], writes=[XI])
            for tb in range(4):
                kb.op("pe", lambda e: e.matmul(psX[:], lhsT=masks[:, 5, :], rhs=XI[:, tb * 512:(tb + 1) * 512], start=True, stop=True),
                      reads=[masks, XI], writes=[psX])
                kb.op("act", lambda e: e.activation(out=XE[:, tb * 512:(tb + 1) * 512], in_=psX[:], func=AF.Sqrt), reads=[psX], writes=[XE])
            kb.op("dve", lambda e: e.tensor_scalar(out=XE[:], in0=XE[:], scalar1=1e-12, scalar2=None, op0=ALU.max), reads=[XE], writes=[XE])
            kb.op("dve", lambda e: e.reciprocal(out=XE[:], in_=XE[:]), reads=[XE], writes=[XE])
            kb.op("pool", lambda e: e.tensor_tensor(out=KKn[:], in0=KKn[:], in1=XE[:], op=ALU.mult), reads=[KKn, XE], writes=[KKn])
            chk(1)
            def prep_gen(d):
                yield
                kb.dma("sp", XE[:], D.ld_d.ap()[d, cc * 128:(cc + 1) * 128, :], reads=[D.ld_d], writes=[XE])
                yield
                kb.dma("act", Aa[:], D.a_d.ap()[d, cc * 128:(cc + 1) * 128, :], reads=[D.a_d], writes=[Aa])
                yield
                kb.op("dve", lambda e: e.tensor_scalar(out=KD[:], in0=Aa[:], scalar1=-1.0, scalar2=rw[:, cc, 5:6], op0=ALU.add, op1=ALU.mult),
                      reads=[Aa, rw], writes=[KD])
                yield
                kb.op("dve", lambda e: e.scalar_tensor_tensor(out=KD[:], in0=KD[:], scalar=1.0, in1=Kk[:], op0=ALU.add, op1=ALU.mult),
                      reads=[KD, Kk], writes=[KD])
                yield
                kb.op("dve", lambda e: e.scalar_tensor_tensor(out=XI[:], in0=KD[:], scalar=rw[:, cc, 6:7], in1=Rr[:], op0=ALU.mult, op1=ALU.mult),
                      reads=[KD, rw, Rr], writes=[XI])
                for tt in range(NT):
                    kb.op("pe", lambda e: e.matmul(psX[:, tt * 2:tt * 2 + 2], lhsT=XI[:, tt * 128:(tt + 1) * 128], rhs=sel, start=True, stop=True),
                          reads=[XI, cst], writes=[psX])
                yield
                kb.op("act", lambda e: e.copy(out=stt[:, 64 + 32 * d:96 + 32 * d], in_=psX[:, 0:32]), reads=[psX], writes=[stt])
                yield
                kb.op("pool", lambda e: e.tensor_tensor(out=Aa[:], in0=Aa[:], in1=KKn[:], op=ALU.mult), reads=[Aa, KKn], writes=[Aa])
                yield
                kb.op("dve", lambda e: e.tensor_tensor_scan(out=XI[:], data0=segt[:], data1=XE[:], initial=0.0, op0=ALU.mult, op1=ALU.add),
                      reads=[segt, XE], writes=[XI])
                yield
                kb.op("pool", lambda e: e.tensor_copy(out=tot[:], in_=XI[:].rearrange("p (n s) -> p n s", s=64)[:, :, 63]), reads=[XI], writes=[tot])
                yield
                kb.op("act", lambda e: e.activation(out=GL[:], in_=tot[:], func=AF.Exp), reads=[tot], writes=[GL])
                if d == 0:
                    kb.op("pool", lambda e: e.tensor_tensor(out=XE[:], in0=XI[:], in1=XE[:], op=ALU.subtract), reads=[XI, XE], writes=[XE])
                else:
                    kb.op("dve", lambda e: e.tensor_tensor(out=XI[:].rearrange("p (n s) -> p n s", s=64),
                                                           in0=tot[:, :, None].to_broadcast([128, 32, 64]),
                                                           in1=XI[:].rearrange("p (n s) -> p n s", s=64), op=ALU.subtract),
                          reads=[tot, XI], writes=[XI])
                    kb.op("pool", lambda e: e.tensor_tensor(out=XE[:], in0=XI[:], in1=XE[:], op=ALU.add), reads=[XI, XE], writes=[XE])
                cI, cE = (XI, XE) if d == 0 else (XE, XI)
                yield
                kb.op("act", lambda e: e.activation(out=fr(E1[:]), in_=cI[:], func=AF.Exp), reads=[cI], writes=[E1])
                yield
                kb.op("act", lambda e: e.activation(out=cI[:], in_=cI[:], func=AF.Exp, scale=-1.0), reads=[cI], writes=[cI])
                yield
                kb.op("act", lambda e: e.activation(out=cE[:], in_=cE[:], func=AF.Exp), reads=[cE], writes=[cE])
                yield
                kb.op("dve", lambda e: e.tensor_tensor(out=fr(E1[:]), in0=E1[:], in1=Rr[:], op=ALU.mult), reads=[E1, Rr], writes=[E1])
                yield
                kb.op("pool", lambda e: e.tensor_tensor(out=fr(KT[:]), in0=KD[:], in1=cI[:], op=ALU.mult), reads=[KD, cI], writes=[KT])
                yield
                kb.op("dve", lambda e: e.tensor_tensor(out=fr(AT[:]), in0=Aa[:], in1=cI[:], op=ALU.mult), reads=[Aa, cI], writes=[AT])
                yield
                kb.op("dve", lambda e: e.scalar_tensor_tensor(out=fr(BTb[:]), in0=cE[:], scalar=-1.0, in1=KKn[:], op0=ALU.mult, op1=ALU.mult),
                      reads=[cE, KKn], writes=[BTb])
                yield
                kb.op("pool", lambda e: e.tensor_tensor(out=cI[:].rearrange("p (n s) -> p n s", s=64), in0=cI[:].rearrange("p (n s) -> p n s", s=64),
                                                        in1=GL[:, :, None].to_broadcast([128, 32, 64]), op=ALU.mult), reads=[cI, GL], writes=[cI])
                yield
                kb.op("dve", lambda e: e.tensor_tensor(out=KD[:], in0=KD[:], in1=cI[:], op=ALU.mult), reads=[KD, cI], writes=[KD])
                yield
                kb.op("pool", lambda e: e.tensor_tensor(out=Aa[:], in0=Aa[:], in1=cI[:], op=ALU.mult), reads=[Aa, cI], writes=[Aa])
                yield

            def seq_gen(d):
                kb.op("pool", lambda e: e.memset(H[0][:], 0.0), writes=[H[0]])
                order = range(32) if d == 0 else range(31, -1, -1)
                hc = 0
                for ch in order:
                    tt, n = ch // 2, ch % 2
                    ccols = slice(ch * 64, ch * 64 + 64)
                    Hc, Hn = H[hc], H[1 - hc]
                    tr = slice(n * 64, n * 64 + 64)
                    for hh in range(2):
                        pr = slice(hh * 64, hh * 64 + 64)
                        bk = bkH[hh]
                        kb.op("pe", lambda e: e.matmul(bk[pr, 64:128], lhsT=MTa[pr, ch, :], rhs=Hc[pr, :], start=True, stop=True),
                              reads=[MTa, Hc], writes=[bk])
                        kb.op("pe", lambda e: e.matmul(bk[tr, 0:64], lhsT=RhT[pr, ccols], rhs=Hc[pr, :], start=True, stop=True),
                              reads=[RhT, Hc], writes=[bk])
                    for hh in range(2):
                        pr = slice(hh * 64, hh * 64 + 64)
                        bk = bkH[hh]
                        kb.op("dve", lambda e: e.tensor_tensor(out=Hn[pr, :], in0=bk[pr, 64:128], in1=Ca[pr, ch, :], op=ALU.add),
                              reads=[bk, Ca], writes=[Hn])
                    for hh in range(2):
                        pr = slice(hh * 64, hh * 64 + 64)
                        bk = bkH[hh]
                        kb.op("act" if False else "dve", lambda e: e.tensor_tensor(out=Ysum[tr, tt, pr], in0=bk[tr, 0:64], in1=Ysum[tr, tt, pr], op=ALU.add),
                              reads=[bk, Ysum], writes=[Ysum])
                    hc = 1 - hc
                    yield
                yield

            def exhaust(g):
                for _ in g:
                    pass

            exhaust(prep_gen(0))
            run_tiles(0, E1, BTb)
            sg = seq_gen(0)
            pg = prep_gen(1)
            done_p = done_s = False
            while not (done_p and done_s):
                if not done_p:
                    try:
                        next(pg)
                    except StopIteration:
                        done_p = True
                for _ in range(2):
                    if not done_s:
                        try:
                            next(sg)
                        except StopIteration:
                            done_s = True
            run_tiles(1, E1, BTb)
            exhaust(seq_gen(1))
            chk(10)
            if dbg is not None and "Ysum" in dbg:
                kb.dma("sp", D.dbgY.ap()[:, cc * 128:(cc + 1) * 128].rearrange("(tt p) c -> p tt c", p=128), Ysum[:], reads=[Ysum], writes=[D.dbgY])
            Y3 = Ysum[:].rearrange("p t (h i) -> p (t h) i", h=2)
            st_mu = stt[:, 0:32]
            st_var = stt[:, 32:64]
            bon = stt[:, 128:160]
            kb.op("dve", lambda e: e.tensor_reduce(out=st_mu, in_=Y3, axis=AX.X, op=ALU.add), reads=[Ysum], writes=[stt])
            kb.op("dve", lambda e: e.tensor_scalar(out=st_mu, in0=st_mu, scalar1=1.0 / 64, scalar2=None, op0=ALU.mult), reads=[stt], writes=[stt])
            kb.op("dve", lambda e: e.tensor_tensor(out=Y3, in0=Y3, in1=st_mu[:, :, None].to_broadcast([128, 32, 64]), op=ALU.subtract),
                  reads=[Ysum, stt], writes=[Ysum])
            sqv = XI[:].rearrange("p (a i) -> p a i", i=64)
            kb.op("act", lambda e: e.activation(out=sqv, in_=Y3, func=AF.Square), reads=[Ysum], writes=[XI])
            kb.op("dve", lambda e: e.tensor_reduce(out=st_var, in_=sqv, axis=AX.X, op=ALU.add), reads=[XI], writes=[stt])
            kb.op("dve", lambda e: e.tensor_scalar(out=st_var, in0=st_var, scalar1=1.0 / 64, scalar2=GN_EPS, op0=ALU.mult, op1=ALU.add),
                  reads=[stt], writes=[stt])
            kb.op("act", lambda e: e.activation(out=st_var, in_=st_var, func=AF.Sqrt), reads=[stt], writes=[stt])
            kb.op("dve", lambda e: e.reciprocal(out=st_var, in_=st_var), reads=[stt], writes=[stt])
            kb.op("dve", lambda e: e.tensor_tensor(out=Y3, in0=Y3, in1=st_var[:, :, None].to_broadcast([128, 32, 64]), op=ALU.mult),
                  reads=[Ysum, stt], writes=[Ysum])
            kb.op("pool", lambda e: e.tensor_tensor(out=Ysum[:], in0=Ysum[:], in1=lnw[:, None, :].to_broadcast([128, NT, 128]), op=ALU.mult),
                  reads=[Ysum, lnw], writes=[Ysum])
            kb.op("pool", lambda e: e.tensor_tensor(out=Ysum[:], in0=Ysum[:], in1=lnb[:, None, :].to_broadcast([128, NT, 128]), op=ALU.add),
                  reads=[Ysum, lnb], writes=[Ysum])
            kb.op("dve", lambda e: e.tensor_tensor(out=bon, in0=stt[:, 64:96], in1=stt[:, 96:128], op=ALU.add), reads=[stt], writes=[stt])
            kb.op("dve", lambda e: e.tensor_scalar(out=bon, in0=bon, scalar1=0.5, scalar2=None, op0=ALU.mult), reads=[stt], writes=[stt])
            V3 = Vtm[:].rearrange("p t (h i) -> p (t h) i", h=2)
            kb.op("pool", lambda e: e.tensor_tensor(out=sqv, in0=V3, in1=bon[:, :, None].to_broadcast([128, 32, 64]), op=ALU.mult),
                  reads=[Vtm, stt], writes=[XI])
            kb.op("pool", lambda e: e.tensor_tensor(out=Y3, in0=Y3, in1=sqv, op=ALU.add), reads=[Ysum, XI], writes=[Ysum])
            gt = XE[:].rearrange("p (t c) -> p t c", c=128)
            kb.dma("sp", gt, D.g_d.ap()[:, cc * 128:(cc + 1) * 128].rearrange("(tt p) c -> p tt c", p=128), reads=[D.g_d], writes=[XE])
            kb.op("dve", lambda e: e.tensor_tensor(out=Ysum[:], in0=Ysum[:], in1=gt, op=ALU.mult), reads=[Ysum, XE], writes=[Ysum])
            if dbg is not None and "yfin" in dbg:
                kb.dma("sp", D.dbgF.ap()[:, cc * 128:(cc + 1) * 128].rearrange("(tt p) c -> p tt c", p=128), Ysum[:], reads=[Ysum], writes=[D.dbgF])
            for g in range(4):
                for j in range(4):
                    tt = g * 4 + j
                    kb.op("pe", lambda e: e.transpose(out=psX[:, j * 128:(j + 1) * 128], in_=Ysum[:, tt, :], identity=C.ident[:]),
                          reads=[Ysum, C.ident], writes=[psX])
                kb.op("act", lambda e: e.copy(out=KD[:, g * 512:(g + 1) * 512], in_=psX[:]), reads=[psX], writes=[KD])
            yb = Aa[:].bitcast(BF16)[:, 0:2048]
            kb.op("dve", lambda e: e.tensor_copy(out=yb, in_=KD[:]), reads=[KD], writes=[Aa])
            kb.dma("sp", D.ymT_d.ap()[cc], yb, reads=[Aa], writes=[D.ymT_d])
        kb.barrier()


def host_consts():
    S = 2048
    rows = np.repeat(np.arange(32), 64).astype(np.float32)
    cols = np.tile(np.arange(64), 32).astype(np.float32)
    inv = (10000.0 ** (-np.arange(0, 32, 2, dtype=np.float32) / 32)).astype(np.float32)
    ar = rows[:, None] * inv[None]
    ac = cols[:, None] * inv[None]
    tabC = np.concatenate([np.cos(ar), np.cos(ac)], 1).astype(np.float32)
    tabS = np.concatenate([np.sin(ar), np.sin(ac)], 1).astype(np.float32)
    ident = np.eye(128, dtype=np.float32)
    iota = np.tile(np.arange(128, dtype=np.float32)[None], (128, 1))
    r = np.arange(128)[:, None]
    s = np.arange(128)[None, :]
    same = (r // 64) == (s // 64)
    masks = np.zeros((128, 6, 128), np.float32)
    masks[:, 0] = same & (r < s)
    masks[:, 1] = same & (r <= s)
    masks[:, 2] = same & (r > s)
    masks[:, 3] = same & (r >= s)
    masks[:, 4] = same & (r > s)
    masks[:, 5] = same
    rcst = np.zeros((128, 66 + 2048), np.float32)
    pp = np.arange(128)
    rcst[pp, pp % 64] = 1.0
    rcst[:, 64] = (pp // 64 == 0)
    rcst[:, 65] = (pp // 64 == 1)
    seg = np.ones(2048, np.float32); seg[::64] = 0.0
    rcst[:, 66:] = seg[None]
    import ml_dtypes
    segm = np.ascontiguousarray(np.broadcast_to(seg[None], (128, 2048))).astype(ml_dtypes.bfloat16)
    return dict(tabC=tabC, tabS=tabS, ident=ident, iota=iota, masks=masks, rcst=rcst, segm=segm)

def prep_shared(inp):
    L = 0
    d = host_consts()
    mu_c = np.zeros((128, 3 * NRCH), np.float32)
    for ci, (c0, cs) in enumerate(RCH):
        mu_c[:cs, ci] = inp["mu_prev"][L, c0:c0 + cs]
        mu_c[:cs, NRCH + ci] = inp["mu_next"][L, c0:c0 + cs]
    d["mu_c"] = mu_c
    w = inp["w_in"][L].reshape(16, 128, 5248)
    wt = np.empty((128, 16 * 5248), np.float32)
    for (c0, cs) in RCH:
        wt[:, 16 * c0:16 * (c0 + cs)] = w[:, :, c0:c0 + cs].transpose(1, 0, 2).reshape(128, 16 * cs)
    RWK = 3712
    for cg in range(3):
        for hf in range(2):
            o0 = 16 * RWK + (cg * 2 + hf) * 4096
            wt[:, o0:o0 + 4096] = w[hf * 8:(hf + 1) * 8, :, RWK + cg * 512:RWK + (cg + 1) * 512].transpose(1, 0, 2).reshape(128, 4096)
    d["w_in"] = wt
    for k in ["norm1_w", "norm2_w", "q_norm_w", "k_norm_w", "w_out", "w_pq", "w2", "a2", "g2", "lnx_w", "lnx_b", "v_tab"]:
        d[k] = np.ascontiguousarray(inp[k][L])
    d["normf_w"] = np.ascontiguousarray(inp["normf_w"])
    d["skT"] = np.ascontiguousarray(inp["sub_keys"][L].reshape(16, 128, 128).transpose(2, 0, 1))
    d["u_tabT"] = np.ascontiguousarray(inp["u_tab"][L].reshape(128, 128, 16, 128).transpose(0, 3, 2, 1)).reshape(128, 128, 2048)
    rw = np.zeros((128, 8, 8), np.float32)
    def ch(v):
        return v.reshape(8, 128).T
    rw[:, :, 0] = ch(inp["w0"][L, 0]); rw[:, :, 1] = ch(inp["w0"][L, 1])
    rw[:, :, 2] = ch(inp["a0"][L, 0]); rw[:, :, 3] = ch(inp["a0"][L, 1])
    rw[:, :, 4] = ch(inp["k_k"][L]); rw[:, :, 5] = ch(inp["k_a"][L]); rw[:, :, 6] = ch(inp["r_k"][L].reshape(-1))
    d["rw_c"] = rw
    return d


def build_program():
    nc = bass.Bass("TRN2", target_bir_lowering=False)
    kb = KB(nc)
    D = declare(kb, None)
    C = consts(kb, D)
    phase_A(kb, C, D)
    phase_Rpre(kb, C, D)
    phase_R(kb, C, D)
    phase_T(kb, C, D)
    phase_O(kb, C, D, D.ymT_d)
    phase_P(kb, C, D)
    phase_F(kb, C, D)
    kb.finish("sp")
    return nc


def kernel(**inputs):
    inp = {k: np.asarray(v) for k, v in inputs.items()}
    shared = prep_shared(inp)
    nc = build_program()
    in_maps = []
    for b in range(8):
        d = dict(shared)
        d["x"] = np.ascontiguousarray(inp["x"][b])
        in_maps.append(d)
    res = run_bass_kernel_spmd(nc, in_maps, core_ids=list(range(8)))
    out = np.stack([np.asarray(r["out"], dtype=np.float32) for r in res.results], axis=0)
    return out
```

```python
import numpy as np
import contextlib
import concourse.bass as bass
import concourse.mybir as mybir
from concourse.bass_utils import run_bass_kernel_spmd

F32 = mybir.dt.float32
BF16 = mybir.dt.bfloat16
U32 = mybir.dt.uint32
ALU = mybir.AluOpType
AF = mybir.ActivationFunctionType
AX = mybir.AxisListType


class T:
    def __init__(self, h, name):
        self.h = h
        self.name = name
        self.w = None
        self.r = {}
        self.dsem = None
        self.dcnt = 0
        self.is_psum = False

    def __getitem__(self, k):
        return self.h[k]

    def ap(self):
        return self.h.ap() if hasattr(self.h, "ap") else self.h[:]


class KB:
    def __init__(self, nc):
        self.nc = nc
        self.es = contextlib.ExitStack()
        self.engs = {"pe": nc.tensor, "act": nc.scalar, "dve": nc.vector, "pool": nc.gpsimd, "sp": nc.sync}
        self.sem = {}
        self.cnt = {}
        for e in self.engs:
            self.sem[e] = self.es.enter_context(nc.semaphore("s_" + e))
            self.cnt[e] = 0
        self.waited = {}
        self.alltensors = []
        self.dsems = []
        self.n_ins = 0

    def sb(self, name, shape, dt=F32, stack=None):
        h = (stack or self.es).enter_context(self.nc.sbuf_tensor(name, list(shape), dt))
        t = T(h, name)
        return t

    def ps(self, name, shape, dt=F32, stack=None):
        h = (stack or self.es).enter_context(self.nc.psum_tensor(name, list(shape), dt))
        t = T(h, name)
        t.is_psum = True
        return t

    def dram(self, name, shape, dt=F32, kind="Internal"):
        h = self.nc.dram_tensor(name, list(shape), dt, kind=kind)
        return T(h, name)

    def view(self, t, name=None):
        return t

    def _wait(self, eng, tok):
        if tok is None:
            return
        sem, val = tok
        key = (eng, id(sem))
        if self.waited.get(key, 0) >= val:
            return
        self.engs[eng].wait_ge(sem, val)
        self.waited[key] = val

    def _deps(self, eng, reads, writes):
        own = id(self.sem[eng])
        for t in reads:
            if t.w is not None:
                if eng == "pe" and id(t.w[0]) == own:
                    continue
                self._wait(eng, t.w)
        for t in writes:
            if t.w is not None and id(t.w[0]) != own:
                self._wait(eng, t.w)
            for k, tok in t.r.items():
                if k == own:
                    continue
                self._wait(eng, tok)

    def _mark(self, tok, reads, writes):
        for t in reads:
            if t in writes:
                continue
            t.r[id(tok[0])] = tok
        for t in writes:
            t.w = tok
            t.r = {}

    def op(self, eng, fn, reads=(), writes=()):
        psr = [t for t in reads if t.is_psum and t not in writes]
        if psr:
            writes = list(writes) + psr
        self._deps(eng, reads, writes)
        ins = fn(self.engs[eng])
        self.cnt[eng] += 1
        ins.then_inc(self.sem[eng], 1)
        tok = (self.sem[eng], self.cnt[eng])
        self._mark(tok, reads, writes)
        self.n_ins += 1
        return ins

    def dma(self, q, out_ap, in_ap, reads=(), writes=(), **kw):
        assert len(writes) == 1
        dst = writes[0]
        if dst.dsem is None:
            dst.dsem = self.es.enter_context(self.nc.semaphore("d_" + dst.name))
            self.dsems.append(dst)
        self._deps(q, reads, [])
        if dst.w is not None and dst.w[0] is not dst.dsem:
            self._wait(q, dst.w)
        for k, tok in dst.r.items():
            self._wait(q, tok)
        ins = self.engs[q].dma_start(out=out_ap, in_=in_ap, **kw)
        dst.dcnt += 16
        ins.then_inc(dst.dsem, 16)
        tok = (dst.dsem, dst.dcnt)
        for t in reads:
            t.r[id(tok[0])] = tok
        dst.w = tok
        dst.r = {}
        self.n_ins += 1
        return ins

    def pe_fence(self):
        if self.cnt["pe"] > 0:
            self._wait("pe", (self.sem["pe"], self.cnt["pe"]))

    def barrier(self):
        toks = [(self.sem[e], self.cnt[e]) for e in self.engs if self.cnt[e] > 0]
        toks += [(t.dsem, t.dcnt) for t in self.dsems if t.dcnt > 0]
        for e in self.engs:
            for tok in toks:
                if tok[0] is self.sem[e]:
                    continue
                self._wait(e, tok)

    def finish(self, eng="sp"):
        for t in self.dsems:
            if t.dcnt > 0:
                self._wait(eng, (t.dsem, t.dcnt))
        for e in self.engs:
            if e != eng and self.cnt[e] > 0:
                self._wait(eng, (self.sem[e], self.cnt[e]))


EPS = 1e-6
NT = 16
RCH = [(i * 128, 128) for i in range(24)] + [(3072 + i * 96, 96) for i in range(4)] + [(3456, 128), (3584, 128)]
NRCH = len(RCH)
RW = 3712


class Obj:
    pass


def declare(kb, debug):
    D = Obj()

    def inp(name, shape, dt=F32):
        setattr(D, name, kb.dram(name, shape, dt, kind="ExternalInput"))

    def scr(name, shape, dt=F32):
        kind = "ExternalOutput" if (debug and name in debug) else "Internal"
        setattr(D, name, kb.dram(name, shape, dt, kind=kind))

    inp("x", [2048, 2048])
    inp("w_in", [128, 16 * 5248])
    inp("mu_c", [128, 3 * NRCH])
    inp("norm1_w", [2048])
    inp("norm2_w", [2048])
    inp("normf_w", [2048])
    inp("q_norm_w", [64])
    inp("k_norm_w", [64])
    inp("tabC", [2048, 32])
    inp("tabS", [2048, 32])
    inp("ident", [128, 128])
    inp("w_out", [2048, 2048])
    inp("w_pq", [2048, 2048])
    inp("skT", [128, 16, 128])
    inp("iota", [128, 128])
    inp("u_tabT", [128, 128, 2048])
    inp("v_tab", [16384, 2048])
    inp("w2", [2, 96, 1024])
    inp("a2", [2, 96, 1024])
    inp("g2", [256, 1024])
    inp("rw_c", [128, 8, 8])
    inp("lnx_w", [1024])
    inp("lnx_b", [1024])
    inp("masks", [128, 6, 128])
    if debug and "in_ymT" in debug:
        inp("ymT_in", [16, 128, 2048], BF16)
    if debug and "in_x1" in debug:
        inp("x1_in", [2048, 2048])
    scr("zs_d", [RW, 2048])
    scr("qkv_d", [2048, 1536])
    scr("ymT_d", [16, 128, 2048], BF16)
    scr("x1_d", [2048, 2048])
    scr("G_d", [128, 128, 2048], BF16)
    scr("peT_d", [2048, 2048])
    scr("A_d", [128, 128, 2048], BF16)
    scr("ld_d", [2, 1024, 2048])
    scr("a_d", [2, 1024, 2048])
    scr("g_d", [2048, 1024])
    inp("rcst", [128, 66 + 2048])
    inp("segm", [128, 2048], BF16)
    if debug and "dbgY" in debug:
        setattr(D, "dbgY", kb.dram("dbgY", [2048, 1024], F32, kind="ExternalOutput"))
        setattr(D, "dbgF", kb.dram("dbgF", [2048, 1024], F32, kind="ExternalOutput"))
    scr("S_d", [2048, 16, 128])
    scr("h2T_d", [16, 128, 2048], BF16)
    scr("vb_d", [16384, 2048], BF16)
    setattr(D, "out", kb.dram("out", [2048, 2048], F32, kind="ExternalOutput"))
    return D


def consts(kb, D):
    C = Obj()
    C.ident = kb.sb("c_ident", [128, 128], F32)
    kb.dma("sp", C.ident[:], D.ident.ap(), writes=[C.ident])
    C.identb = kb.sb("c_identb", [128, 128], BF16)
    kb.op("dve", lambda e: e.tensor_copy(out=C.identb[:], in_=C.ident[:]), reads=[C.ident], writes=[C.identb])
    return C


def norm_to_T(kb, C, src, wvec, hT, tag):
    with contextlib.ExitStack() as st:
        wb = kb.sb(tag + "wb", [128, 2048], F32, st)
        kb.dma("pool", wb[:], wvec.ap().partition_broadcast(128), writes=[wb])
        xts = [kb.sb(f"{tag}x{i}", [128, 2048], F32, st) for i in range(2)]
        junk = kb.sb(tag + "junk", [128, 2048], BF16, st)
        hb = [kb.sb(f"{tag}h{i}", [128, 2048], BF16, st) for i in range(2)]
        ss = kb.sb(tag + "ss", [128, NT], F32, st)
        rs = kb.sb(tag + "rs", [128, NT], F32, st)
        ptr = [kb.ps(f"{tag}ps{i}", [128, 1024], BF16, st) for i in range(2)]
        kb.op("pool", lambda e: e.memset(ss[:], 0.0), writes=[ss])
        for tt in range(NT):
            xt = xts[tt % 2]
            kb.dma("sp", xt[:], src.ap()[tt * 128:(tt + 1) * 128, :], writes=[xt])
            kb.op("act", lambda e: e.activation(out=junk[:], in_=xt[:], func=AF.Square, accum_out=ss[:, tt:tt + 1]),
                  reads=[xt], writes=[junk, ss])
            kb.op("dve", lambda e: e.tensor_scalar(out=rs[:, tt:tt + 1], in0=ss[:, tt:tt + 1], scalar1=1.0 / 2048, scalar2=EPS,
                                                   op0=ALU.mult, op1=ALU.add), reads=[ss], writes=[rs])
            kb.op("act", lambda e: e.activation(out=rs[:, tt:tt + 1], in_=rs[:, tt:tt + 1], func=AF.Sqrt), reads=[rs], writes=[rs])
            kb.op("dve", lambda e: e.reciprocal(out=rs[:, tt:tt + 1], in_=rs[:, tt:tt + 1]), reads=[rs], writes=[rs])
            h = hb[tt % 2]
            kb.op("dve", lambda e: e.scalar_tensor_tensor(out=h[:], in0=xt[:], scalar=rs[:, tt:tt + 1], in1=wb[:],
                                                          op0=ALU.mult, op1=ALU.mult), reads=[xt, rs, wb], writes=[h])
            for g in range(2):
                p = ptr[g]
                for j in range(8):
                    dc = g * 8 + j
                    kb.op("pe", lambda e: e.transpose(out=p[:, j * 128:(j + 1) * 128], in_=h[:, dc * 128:(dc + 1) * 128],
                                                      identity=C.identb[:]), reads=[h, C.identb], writes=[p])
                eng = "act" if g == 0 else "dve"
                src_ap = p[:].rearrange("p (j t) -> p j t", j=8)
                dst_ap = hT[:, g * 8:(g + 1) * 8, tt * 128:(tt + 1) * 128]
                if eng == "act":
                    kb.op("act", lambda e: e.copy(out=dst_ap, in_=src_ap), reads=[p], writes=[hT])
                else:
                    kb.op("dve", lambda e: e.tensor_copy(out=dst_ap, in_=src_ap), reads=[p], writes=[hT])
        kb.barrier()


def phase_A(kb, C, D):
    with contextlib.ExitStack() as st:
        hT = kb.sb("hT", [128, 16, 2048], BF16, st)
        norm_to_T(kb, C, D.x, D.norm1_w, hT, "n1")
        mu = kb.sb("mu", [128, 3 * NRCH], F32, st)
        kb.dma("sp", mu[:], D.mu_c.ap(), writes=[mu])
        kb.op("dve", lambda e: e.tensor_tensor(out=mu[:, 60:90], in0=mu[:, 0:30], in1=mu[:, 30:60], op=ALU.add), reads=[mu], writes=[mu])
        kb.op("dve", lambda e: e.tensor_scalar(out=mu[:, 60:90], in0=mu[:, 60:90], scalar1=-1.0, scalar2=1.0, op0=ALU.mult, op1=ALU.add),
              reads=[mu], writes=[mu])
        stg = [kb.sb(f"a_stg{i}", [128, 4096], F32, st) for i in range(2)]
        wbf = [kb.sb(f"a_wbf{i}", [128, 16, 128], BF16, st) for i in range(2)]
        accs = [kb.sb(f"a_acc{i}", [128, 2048], F32, st) for i in range(2)]
        pss = [[kb.ps(f"a_ps{i}_{j}", [128, 512], F32, st) for j in range(4)] for i in range(2)]
        def loadw(ci):
            c0, cs = RCH[ci]
            kb.dma("sp", stg[ci % 2][:, 0:16 * cs], D.w_in.ap()[:, 16 * c0:16 * (c0 + cs)], writes=[stg[ci % 2]])
        def convw(ci):
            c0, cs = RCH[ci]
            sg = stg[ci % 2]
            sgv = sg[:, 0:16 * cs].rearrange("p (dc c) -> p dc c", dc=16)
            kb.op("dve", lambda e: e.tensor_copy(out=wbf[ci % 2][:, :, 0:cs], in_=sgv), reads=[sg], writes=[wbf[ci % 2]])
        loadw(0)
        convw(0)
        loadw(1)
        for ci, (c0, cs) in enumerate(RCH):
            wb = wbf[ci % 2]
            ps = pss[ci % 2]
            acc = accs[ci % 2]
            for tb in range(4):
                for dc in range(16):
                    kb.op("pe", lambda e: e.matmul(ps[tb][0:cs, :], lhsT=wb[:, dc, 0:cs], rhs=hT[:, dc, tb * 512:(tb + 1) * 512],
                                                   start=(dc == 0), stop=(dc == 15)), reads=[wb, hT], writes=[ps[tb]])
            if ci + 1 < NRCH:
                convw(ci + 1)
            if ci + 2 < NRCH:
                loadw(ci + 2)
            for tb in range(4):
                kb.op("act", lambda e: e.activation(out=acc[0:cs, tb * 512:(tb + 1) * 512], in_=ps[tb][0:cs, :], func=AF.Copy,
                                                    scale=mu[0:cs, 60 + ci:61 + ci]), reads=[ps[tb], mu], writes=[acc])
            for tb in range(4):
                n = 512 if tb < 3 else 511
                d0 = tb * 512 + 1
                kb.op("dve", lambda e: e.scalar_tensor_tensor(out=acc[0:cs, d0:d0 + n], in0=ps[tb][0:cs, 0:n], scalar=mu[0:cs, ci:ci + 1],
                                                              in1=acc[0:cs, d0:d0 + n], op0=ALU.mult, op1=ALU.add),
                      reads=[ps[tb], mu, acc], writes=[acc])
                s0 = 1 if tb == 0 else 0
                n = 512 - s0
                d0 = tb * 512 + s0 - 1
                kb.op("dve", lambda e: e.scalar_tensor_tensor(out=acc[0:cs, d0:d0 + n], in0=ps[tb][0:cs, s0:512], scalar=mu[0:cs, 30 + ci:31 + ci],
                                                              in1=acc[0:cs, d0:d0 + n], op0=ALU.mult, op1=ALU.add),
                      reads=[ps[tb], mu, acc], writes=[acc])
            kb.dma("pool", D.zs_d.ap()[c0:c0 + cs, :], acc[0:cs, :], reads=[acc], writes=[D.zs_d])
        kb.barrier()
        with contextlib.ExitStack() as st2:
            wq = kb.sb("a_wq", [128, 16, 512], BF16, st2)
            ev = [kb.sb(f"a_ev{i}", [128, 512], F32, st2) for i in range(2)]
            for cg in range(3):
                c0 = RW + cg * 512
                for hf in range(2):
                    sg = stg[hf]
                    sgv = sg[:].rearrange("p (dc c) -> p dc c", dc=8)
                    o0 = 16 * RW + (cg * 2 + hf) * 4096
                    kb.dma("sp", sg[:], D.w_in.ap()[:, o0:o0 + 4096], writes=[sg])
                    kb.op("dve", lambda e: e.tensor_copy(out=wq[:, hf * 8:(hf + 1) * 8, :], in_=sgv), reads=[sg], writes=[wq])
                for tt in range(NT):
                    ps = pss[tt % 2][0]
                    for dc in range(16):
                        kb.op("pe", lambda e: e.matmul(ps[:], lhsT=hT[:, dc, tt * 128:(tt + 1) * 128], rhs=wq[:, dc, :],
                                                       start=(dc == 0), stop=(dc == 15)), reads=[wq, hT], writes=[ps])
                    o = ev[tt % 2]
                    kb.op("act", lambda e: e.copy(out=o[:], in_=ps[:]), reads=[ps], writes=[o])
                    kb.dma("sp", D.qkv_d.ap()[tt * 128:(tt + 1) * 128, cg * 512:(cg + 1) * 512], o[:], reads=[o], writes=[D.qkv_d])
            kb.barrier()


def phase_T(kb, C, D):
    with contextlib.ExitStack() as st:
        qT = kb.sb("t_qT", [128, 8, 2048], BF16, st)
        kT2 = kb.sb("t_kT2", [128, 4, 2048], BF16, st)
        vaug = kb.sb("t_vaug", [128, NT, 4, 65], BF16, st)
        wqk = kb.sb("t_wqk", [128, 20, 64], F32, st)
        w64 = kb.sb("t_w64", [128, 2, 64], F32, st)
        kb.dma("pool", w64[:, 0, :], D.q_norm_w.ap().partition_broadcast(128), writes=[w64])
        kb.dma("pool", w64[:, 1, :], D.k_norm_w.ap().partition_broadcast(128), writes=[w64])
        kb.op("dve", lambda e: e.tensor_scalar(out=wqk[:, 0:16, :], in0=w64[:, 0:1, :].to_broadcast([128, 16, 64]), scalar1=0.125, scalar2=None,
                                               op0=ALU.mult), reads=[w64], writes=[wqk])
        kb.op("dve", lambda e: e.tensor_copy(out=wqk[:, 16:20, :], in_=w64[:, 1:2, :].to_broadcast([128, 4, 64])), reads=[w64], writes=[wqk])
        kb.op("pool", lambda e: e.memset(vaug[:], 1.0), writes=[vaug])
        with contextlib.ExitStack() as st2:
            qk = [kb.sb(f"t_qk{i}", [128, 1536], F32, st2) for i in range(2)]
            tC = [kb.sb(f"t_tC{i}", [128, 2, 16], F32, st2) for i in range(2)]
            tS = [kb.sb(f"t_tS{i}", [128, 2, 16], F32, st2) for i in range(2)]
            sq = kb.sb("t_sq", [128, 20, 64], F32, st2)
            ss = kb.sb("t_ss", [128, 20], F32, st2)
            qn = kb.sb("t_qn", [128, 20, 64], F32, st2)
            t1 = kb.sb("t_t1", [128, 20, 2, 16], F32, st2)
            t2 = kb.sb("t_t2", [128, 20, 2, 16], F32, st2)
            t3 = kb.sb("t_t3", [128, 20, 2, 16], F32, st2)
            t4 = kb.sb("t_t4", [128, 20, 2, 16], F32, st2)
            qkr = kb.sb("t_qkr", [128, 20, 64], BF16, st2)
            kd = kb.sb("t_kd", [128, 4, 2, 64], BF16, st2)
            pq = kb.ps("t_pq", [128, 1024], BF16, st2)
            pk = kb.ps("t_pk", [128, 1024], BF16, st2)
            for tt in range(NT):
                q = qk[tt % 2]
                cC = tC[tt % 2]
                cS = tS[tt % 2]
                kb.dma("sp", q[:], D.qkv_d.ap()[tt * 128:(tt + 1) * 128, :], reads=[D.qkv_d], writes=[q])
                kb.dma("act", cC[:].rearrange("p a b -> p (a b)"), D.tabC.ap()[tt * 128:(tt + 1) * 128, :], writes=[cC])
                kb.dma("act", cS[:].rearrange("p a b -> p (a b)"), D.tabS.ap()[tt * 128:(tt + 1) * 128, :], writes=[cS])
                qv = q[:, 0:1280].rearrange("p (h d) -> p h d", h=20)
                kb.op("act", lambda e: e.activation(out=sq[:], in_=qv, func=AF.Square), reads=[q], writes=[sq])
                kb.op("dve", lambda e: e.tensor_reduce(out=ss[:], in_=sq[:], axis=AX.X, op=ALU.add), reads=[sq], writes=[ss])
                kb.op("dve", lambda e: e.tensor_scalar(out=ss[:], in0=ss[:], scalar1=1.0 / 64, scalar2=EPS, op0=ALU.mult, op1=ALU.add),
                      reads=[ss], writes=[ss])
                kb.op("act", lambda e: e.activation(out=ss[:], in_=ss[:], func=AF.Sqrt), reads=[ss], writes=[ss])
                kb.op("dve", lambda e: e.reciprocal(out=ss[:], in_=ss[:]), reads=[ss], writes=[ss])
                kb.op("dve", lambda e: e.tensor_tensor(out=qn[:], in0=qv, in1=ss[:, :, None].to_broadcast([128, 20, 64]), op=ALU.mult),
                      reads=[q, ss], writes=[qn])
                kb.op("pool", lambda e: e.tensor_tensor(out=qn[:], in0=qn[:], in1=wqk[:], op=ALU.mult), reads=[qn, wqk], writes=[qn])
                qn5 = qn[:].rearrange("p h (a b c) -> p h a b c", a=2, b=2)
                x1 = qn5[:, :, :, 0, :]
                x2 = qn5[:, :, :, 1, :]
                Cb = cC[:, None, :, :].to_broadcast([128, 20, 2, 16])
                Sb = cS[:, None, :, :].to_broadcast([128, 20, 2, 16])
                kb.op("dve", lambda e: e.tensor_tensor(out=t1[:], in0=x1, in1=Cb, op=ALU.mult), reads=[qn, cC], writes=[t1])
                kb.op("pool", lambda e: e.tensor_tensor(out=t2[:], in0=x2, in1=Sb, op=ALU.mult), reads=[qn, cS], writes=[t2])
                kb.op("pool", lambda e: e.tensor_tensor(out=t3[:], in0=x2, in1=Cb, op=ALU.mult), reads=[qn, cC], writes=[t3])
                kb.op("dve", lambda e: e.tensor_tensor(out=t4[:], in0=x1, in1=Sb, op=ALU.mult), reads=[qn, cS], writes=[t4])
                r5 = qkr[:].rearrange("p h (a b c) -> p h a b c", a=2, b=2)
                kb.op("dve", lambda e: e.tensor_tensor(out=r5[:, :, :, 0, :], in0=t1[:], in1=t2[:], op=ALU.subtract), reads=[t1, t2], writes=[qkr])
                kb.op("pool", lambda e: e.tensor_tensor(out=r5[:, :, :, 1, :], in0=t3[:], in1=t4[:], op=ALU.add), reads=[t3, t4], writes=[qkr])
                kb.op("pool", lambda e: e.tensor_copy(out=kd[:], in_=qkr[:, 16:20, None, :].to_broadcast([128, 4, 2, 64])), reads=[qkr], writes=[kd])
                for j in range(8):
                    kb.op("pe", lambda e: e.transpose(out=pq[:, j * 128:(j + 1) * 128], in_=qkr[:, 2 * j:2 * j + 2, :].rearrange("p a b -> p (a b)"),
                                                      identity=C.identb[:]), reads=[qkr, C.identb], writes=[pq])
                kb.op("act", lambda e: e.copy(out=qT[:, :, tt * 128:(tt + 1) * 128], in_=pq[:].rearrange("p (j t) -> p j t", j=8)),
                      reads=[pq], writes=[qT])
                for j in range(4):
                    kb.op("pe", lambda e: e.transpose(out=pk[:, j * 128:(j + 1) * 128], in_=kd[:, j, :, :].rearrange("p a b -> p (a b)"),
                                                      identity=C.identb[:]), reads=[kd, C.identb], writes=[pk])
                kb.op("dve", lambda e: e.tensor_copy(out=kT2[:, :, tt * 128:(tt + 1) * 128], in_=pk[:, 0:512].rearrange("p (j t) -> p j t", j=4)),
                      reads=[pk], writes=[kT2])
                kb.op("pool", lambda e: e.tensor_copy(out=vaug[:, tt, :, 0:64], in_=q[:, 1280:1536].rearrange("p (h d) -> p h d", h=4)),
                      reads=[q], writes=[vaug])
            kb.barrier()
        with contextlib.ExitStack() as st3:
            yatt = kb.sb("t_yatt", [128, NT, 1024], BF16, st3)
            pexp = [kb.sb(f"t_pexp{i}", [128, 512], BF16, st3) for i in range(3)]
            rinv = kb.sb("t_rinv", [128, 4], F32, st3)
            pss = [kb.ps(f"t_pss{i}", [128, 512], F32, st3) for i in range(2)]
            po = [kb.ps(f"t_po{i}", [128, 512], F32, st3) for i in range(4)]
            seq = [(h, qb, kt) for h in range(16) for qb in range(4) for kt in range(NT)]
            cv_st = [kb.sb(f"t_cvst{i}", [128, 2048], F32, st3) for i in range(2)]
            cv_bf = [kb.sb(f"t_cvbf{i}", [128, 2048], BF16, st3) for i in range(2)]

            def vconv(c):
                a, b = cv_st[c % 2], cv_bf[c % 2]
                kb.dma("sp", a[:], D.v_tab.ap()[c * 128:(c + 1) * 128, :], writes=[a])
                kb.op("dve", lambda e: e.tensor_copy(out=b[:], in_=a[:]), reads=[a], writes=[b])
                kb.dma("pool", D.vb_d.ap()[c * 128:(c + 1) * 128, :], b[:], reads=[b], writes=[D.vb_d])

            def emitS(i):
                h, qb, kt = seq[i]
                kv, c, b0 = h // 4, h // 2, (h % 2) * 64
                ps = pss[i % 2]
                kb.op("pe", lambda e: e.matmul(ps[:], lhsT=kT2[b0:b0 + 64, kv, kt * 128:(kt + 1) * 128],
                                               rhs=qT[b0:b0 + 64, c, qb * 512:(qb + 1) * 512], start=True, stop=True),
                      reads=[kT2, qT], writes=[ps])

            def emitE(i):
                ps = pss[i % 2]
                pe_ = pexp[i % 3]
                kb.op("act", lambda e: e.activation(out=pe_[:], in_=ps[:], func=AF.Exp), reads=[ps], writes=[pe_])

            def emitPV(i):
                h, qb, kt = seq[i]
                kv = h // 4
                pe_ = pexp[i % 3]
                for j in range(4):
                    kb.op("pe", lambda e: e.matmul(po[j][:, 0:65], lhsT=pe_[:, j * 128:(j + 1) * 128], rhs=vaug[:, kt, kv, :],
                                                   start=(kt == 0), stop=(kt == NT - 1)), reads=[pe_, vaug], writes=[po[j]])
                if kt == NT - 1:
                    for j in range(4):
                        kb.op("dve", lambda e: e.reciprocal(out=rinv[:, j:j + 1], in_=po[j][:, 64:65]), reads=[po[j]], writes=[rinv])
                        kb.op("dve", lambda e: e.tensor_scalar(out=yatt[:, qb * 4 + j, h * 64:(h + 1) * 64], in0=po[j][:, 0:64],
                                                               scalar1=rinv[:, j:j + 1], scalar2=None, op0=ALU.mult),
                              reads=[po[j], rinv], writes=[yatt])

            emitS(0)
            for i in range(len(seq)):
                if i + 1 < len(seq):
                    emitS(i + 1)
                emitE(i)
                emitPV(i)
                if i % 8 == 0:
                    vconv(i // 8)
            pt = [kb.ps(f"t_pt{i}", [128, 1024], BF16, st3) for i in range(2)]
            yT = [kb.sb(f"t_yT{i}", [128, 8, 128], BF16, st3) for i in range(2)]
            for tt in range(NT):
                p = pt[tt % 2]
                o = yT[tt % 2]
                for j in range(8):
                    kb.op("pe", lambda e: e.transpose(out=p[:, j * 128:(j + 1) * 128], in_=yatt[:, tt, j * 128:(j + 1) * 128],
                                                      identity=C.identb[:]), reads=[yatt, C.identb], writes=[p])
                kb.op("act", lambda e: e.copy(out=o[:], in_=p[:].rearrange("p (j t) -> p j t", j=8)), reads=[p], writes=[o])
                kb.dma("sp", D.ymT_d.ap()[8:16, :, tt * 128:(tt + 1) * 128].rearrange("j p t -> p j t"), o[:], reads=[o], writes=[D.ymT_d])
            kb.barrier()


def phase_O(kb, C, D, ym_src):
    with contextlib.ExitStack() as st:
        ymT = kb.sb("o_ymT", [128, 16, 2048], BF16, st)
        for j in range(16):
            kb.dma("sp" if j % 2 == 0 else "act", ymT[:, j, :], ym_src.ap()[j], reads=[ym_src], writes=[ymT])
        stg = [kb.sb(f"o_stg{i}", [128, 4096], F32, st) for i in range(2)]
        wo = kb.sb("o_wo", [128, 16, 512], BF16, st)
        xs = [kb.sb(f"o_xs{i}", [128, 512], F32, st) for i in range(2)]
        pss = [kb.ps(f"o_ps{i}", [128, 512], F32, st) for i in range(2)]
        w_v = D.w_out.ap().rearrange("(kc p) c -> p kc c", p=128)
        for dg in range(4):
            for hf in range(2):
                sg = stg[hf]
                sgv = sg[:].rearrange("p (dc c) -> p dc c", dc=8)
                kb.dma("sp" if hf == 0 else "act", sgv, w_v[:, hf * 8:(hf + 1) * 8, dg * 512:(dg + 1) * 512], writes=[sg])
                kb.op("dve", lambda e: e.tensor_copy(out=wo[:, hf * 8:(hf + 1) * 8, :], in_=sgv), reads=[sg], writes=[wo])
            for tt in range(NT):
                ps = pss[tt % 2]
                xt = xs[tt % 2]
                kb.dma("sp", xt[:], D.x.ap()[tt * 128:(tt + 1) * 128, dg * 512:(dg + 1) * 512], writes=[xt])
                for kc in range(16):
                    kb.op("pe", lambda e: e.matmul(ps[:], lhsT=ymT[:, kc, tt * 128:(tt + 1) * 128], rhs=wo[:, kc, :],
                                                   start=(kc == 0), stop=(kc == 15)), reads=[ymT, wo], writes=[ps])
                kb.op("dve", lambda e: e.tensor_tensor(out=xt[:], in0=ps[:], in1=xt[:], op=ALU.add), reads=[ps, xt], writes=[xt])
                kb.dma("act", D.x1_d.ap()[tt * 128:(tt + 1) * 128, dg * 512:(dg + 1) * 512], xt[:], reads=[xt], writes=[D.x1_d])
        kb.barrier()


def phase_P(kb, C, D):
    with contextlib.ExitStack() as stP:
        iota = kb.sb("p_iota", [128, 128], F32, stP)
        kb.dma("sp", iota[:], D.iota.ap(), writes=[iota])
        with contextlib.ExitStack() as st, kb.nc.named_scope("P1"):
            h2T = kb.sb("p_h2T", [128, 16, 2048], BF16, st)
            norm_to_T(kb, C, D.x1_d, D.norm2_w, h2T, "n2")
            for dc in range(16):
                kb.dma("sp" if dc % 2 == 0 else "act", D.h2T_d.ap()[dc], h2T[:, dc, :], reads=[h2T], writes=[D.h2T_d])
            skb = kb.sb("p_sk", [128, 16, 128], F32, st)
            kb.dma("sp", skb[:], D.skT.ap(), writes=[skb])
            stg = [kb.sb(f"p_stg{i}", [128, 16, 128], F32, st) for i in range(2)]
            wb = [kb.sb(f"p_wb{i}", [128, 16, 128], BF16, st) for i in range(2)]
            qTs = [kb.sb(f"p_qT{i}", [128, 2048], F32, st) for i in range(2)]
            sev = [kb.sb(f"p_sev{i}", [128, 16, 128], F32, st) for i in range(2)]
            pss = [[kb.ps(f"p_ps{i}_{j}", [128, 512], F32, st) for j in range(3)] for i in range(2)]
            w_v = D.w_pq.ap().rearrange("(dc p) c -> p dc c", p=128)
            for hp in range(16):
                sg = stg[hp % 2]
                kb.dma("sp" if hp % 2 == 0 else "act", sg[:], w_v[:, :, hp * 128:(hp + 1) * 128], writes=[sg])
                w = wb[hp % 2]
                kb.op("dve", lambda e: e.tensor_copy(out=w[:], in_=sg[:]), reads=[sg], writes=[w])
                qT = qTs[hp % 2]
                for tb in range(4):
                    ps = pss[tb % 2][0]
                    for dc in range(16):
                        kb.op("pe", lambda e: e.matmul(ps[:], lhsT=w[:, dc, :], rhs=h2T[:, dc, tb * 512:(tb + 1) * 512],
                                                       start=(dc == 0), stop=(dc == 15)), reads=[w, h2T], writes=[ps])
                    kb.op("act", lambda e: e.copy(out=qT[:, tb * 512:(tb + 1) * 512], in_=ps[:]), reads=[ps], writes=[qT])
                se = sev[hp % 2]
                for g in range(4):
                    ps = pss[g % 2][1 + (g // 2) % 2]
                    for j in range(4):
                        tt = g * 4 + j
                        kb.op("pe", lambda e: e.matmul(ps[:, j * 128:(j + 1) * 128], lhsT=qT[:, tt * 128:(tt + 1) * 128], rhs=skb[:, hp, :],
                                                       start=True, stop=True), reads=[qT, skb], writes=[ps])
                    kb.op("dve", lambda e: e.tensor_copy(out=se[:, g * 4:(g + 1) * 4, :], in_=ps[:].rearrange("p (j n) -> p j n", j=4)),
                          reads=[ps], writes=[se])
                kb.dma("pool", D.S_d.ap()[:, hp, :].rearrange("(tt p) n -> p tt n", p=128), se[:], reads=[se], writes=[D.S_d])
            kb.barrier()
        with contextlib.ExitStack() as st, kb.nc.named_scope("P23"):
            ETg = [kb.sb(f"p_ETg{g}", [128, 3, 256], F32, st) for g in range(8)]
            Ss = [kb.sb(f"p_S{i}", [128, 16, 128], F32, st) for i in range(2)]
            S2 = kb.sb("p_S2", [128, 128], F32, st)
            M16 = kb.sb("p_M16", [128, 16, 16], F32, st)
            I16u = kb.sb("p_I16u", [128, 16, 16], U32, st)
            I16f = kb.sb("p_I16f", [128, 16, 16], F32, st)
            cand = kb.sb("p_cand", [128, 8, 16, 16], F32, st)
            cand2 = kb.sb("p_cand2", [128, 256], F32, st)
            C16 = kb.sb("p_C16", [128, 8, 16], F32, st)
            CIu = kb.sb("p_CIu", [128, 8, 16], U32, st)
            IJu = kb.sb("p_IJu", [128, 2, 8, 16], U32, st)
            IJf = kb.sb("p_IJf", [128, 2, 8, 16], F32, st)
            ex = kb.sb("p_ex", [128, 8, 16], F32, st)
            Z = kb.sb("p_Z", [128, 8], F32, st)
            EG = kb.sb("p_EG", [128, 3, 8, 16], F32, st)
            eq = kb.sb("p_eq", [128, 8, 16, 16], F32, st)
            pt = kb.ps("p_pt", [128, 512], F32, st)
            Gs = kb.sb("p_Gs", [128, 128, 256], BF16, st)
            O1 = [kb.sb(f"p_O1{i}", [128, 32, 128], BF16, st) for i in range(2)]
            O2 = [kb.sb(f"p_O2{i}", [128, 32, 128], BF16, st) for i in range(2)]
            pg = [kb.ps(f"p_pg{i}", [128, 512], F32, st) for i in range(4)]
            def p2_tile(tt):
                S = Ss[tt % 2]
                kb.dma("sp", S[:].rearrange("p a n -> p (a n)"), D.S_d.ap()[tt * 128:(tt + 1) * 128].rearrange("p a n -> p (a n)"),
                       reads=[D.S_d], writes=[S])
                for hp in range(16):
                    kb.op("dve", lambda e: e.max(out=M16[:, hp, 0:8], in_=S[:, hp, :]), reads=[S], writes=[M16])
                    kb.op("dve", lambda e: e.max_index(out=I16u[:, hp, 0:8], in_max=M16[:, hp, 0:8], in_values=S[:, hp, :]),
                          reads=[S, M16], writes=[I16u])
                    kb.op("dve", lambda e: e.match_replace(out=S2[:], in_to_replace=M16[:, hp, 0:8], in_values=S[:, hp, :], imm_value=-1e30),
                          reads=[S, M16], writes=[S2])
                    kb.op("dve", lambda e: e.max(out=M16[:, hp, 8:16], in_=S2[:]), reads=[S2], writes=[M16])
                    kb.op("dve", lambda e: e.max_index(out=I16u[:, hp, 8:16], in_max=M16[:, hp, 8:16], in_values=S2[:]),
                          reads=[S2, M16], writes=[I16u])
                kb.op("pool", lambda e: e.tensor_copy(out=I16f[:], in_=I16u[:]), reads=[I16u], writes=[I16f])
                M4 = M16[:].rearrange("p (h q) k -> p h q k", q=2)
                I4 = I16f[:].rearrange("p (h q) k -> p h q k", q=2)
                kb.op("pool", lambda e: e.tensor_tensor(out=cand[:], in0=M4[:, :, 0, :, None].to_broadcast([128, 8, 16, 16]),
                                                        in1=M4[:, :, 1, None, :].to_broadcast([128, 8, 16, 16]), op=ALU.add),
                      reads=[M16], writes=[cand])
                for h in range(8):
                    ch = cand[:, h, :, :].rearrange("p a b -> p (a b)")
                    kb.op("dve", lambda e: e.max(out=C16[:, h, 0:8], in_=ch), reads=[cand], writes=[C16])
                    kb.op("dve", lambda e: e.max_index(out=CIu[:, h, 0:8], in_max=C16[:, h, 0:8], in_values=ch), reads=[cand, C16], writes=[CIu])
                    kb.op("dve", lambda e: e.match_replace(out=cand2[:], in_to_replace=C16[:, h, 0:8], in_values=ch, imm_value=-1e30),
                          reads=[cand, C16], writes=[cand2])
                    kb.op("dve", lambda e: e.max(out=C16[:, h, 8:16], in_=cand2[:]), reads=[cand2], writes=[C16])
                    kb.op("dve", lambda e: e.max_index(out=CIu[:, h, 8:16], in_max=C16[:, h, 8:16], in_values=cand2[:]),
                          reads=[cand2, C16], writes=[CIu])
                kb.op("pool", lambda e: e.tensor_tensor(out=ex[:], in0=C16[:], in1=C16[:, :, 0:1].to_broadcast([128, 8, 16]), op=ALU.subtract),
                      reads=[C16], writes=[ex])
                kb.op("act", lambda e: e.activation(out=ex[:], in_=ex[:], func=AF.Exp), reads=[ex], writes=[ex])
                kb.op("dve", lambda e: e.tensor_reduce(out=Z[:], in_=ex[:], axis=AX.X, op=ALU.add), reads=[ex], writes=[Z])
                kb.op("dve", lambda e: e.reciprocal(out=Z[:], in_=Z[:]), reads=[Z], writes=[Z])
                kb.op("dve", lambda e: e.tensor_tensor(out=EG[:, 2], in0=ex[:], in1=Z[:, :, None].to_broadcast([128, 8, 16]), op=ALU.mult),
                      reads=[ex, Z], writes=[EG])
                kb.op("dve", lambda e: e.tensor_single_scalar(out=IJu[:, 0], in_=CIu[:], scalar=4, op=ALU.logical_shift_right),
                      reads=[CIu], writes=[IJu])
                kb.op("dve", lambda e: e.tensor_single_scalar(out=IJu[:, 1], in_=CIu[:], scalar=15, op=ALU.bitwise_and),
                      reads=[CIu], writes=[IJu])
                kb.op("pool", lambda e: e.tensor_copy(out=IJf[:], in_=IJu[:]), reads=[IJu], writes=[IJf])
                for q in range(2):
                    kb.op("dve", lambda e: e.tensor_tensor(out=eq[:], in0=iota[:, None, None, 0:16].to_broadcast([128, 8, 16, 16]),
                                                            in1=IJf[:, q, :, :, None].to_broadcast([128, 8, 16, 16]), op=ALU.is_equal),
                          reads=[iota, IJf], writes=[eq])
                    kb.op("pool", lambda e: e.tensor_tensor(out=eq[:], in0=eq[:], in1=I4[:, :, q, None, :].to_broadcast([128, 8, 16, 16]),
                                                            op=ALU.mult), reads=[eq, I16f], writes=[eq])
                    kb.op("dve", lambda e: e.tensor_reduce(out=EG[:, q], in_=eq[:], axis=AX.X, op=ALU.add), reads=[eq], writes=[EG])
                for a in range(3):
                    kb.op("pe", lambda e: e.transpose(out=pt[:, a * 128:(a + 1) * 128], in_=EG[:, a].rearrange("p h k -> p (h k)"),
                                                      identity=C.ident[:]), reads=[EG, C.ident], writes=[pt])
                kb.op("act", lambda e: e.copy(out=ETg[tt // 2][:, :, (tt % 2) * 128:(tt % 2 + 1) * 128], in_=pt[:, 0:384].rearrange("p (a t) -> p a t", a=3)),
                      reads=[pt], writes=[ETg[tt // 2]])

            itc = [0]

            def p3_group(tg):
                for sub in range(8):
                    t0 = sub * 32
                    ET = ETg[tg]
                    o1 = O1[sub % 2]
                    o2 = O2[sub % 2]
                    iob = iota[:, None, :].to_broadcast([128, 32, 128])
                    kb.op("dve", lambda e: e.tensor_tensor(out=o1[:], in0=iob, in1=ET[:, 0, t0:t0 + 32, None].to_broadcast([128, 32, 128]),
                                                            op=ALU.is_equal), reads=[iota, ETg[tg]], writes=[o1])
                    kb.op("dve", lambda e: e.tensor_tensor(out=o2[:], in0=iob, in1=ET[:, 1, t0:t0 + 32, None].to_broadcast([128, 32, 128]),
                                                           op=ALU.is_equal), reads=[iota, ETg[tg]], writes=[o2])
                    kb.op("pool", lambda e: e.tensor_tensor(out=o2[:], in0=o2[:], in1=ET[:, 2, t0:t0 + 32, None].to_broadcast([128, 32, 128]),
                                                            op=ALU.mult), reads=[o2, ETg[tg]], writes=[o2])
                    for q4 in range(8):
                        p = pg[itc[0] % 4]
                        itc[0] += 1
                        for j in range(4):
                            tl = q4 * 4 + j
                            kb.op("pe", lambda e: e.matmul(p[:].rearrange("p (e t) -> p e t", t=4)[:, :, j], lhsT=o2[:, tl, :], rhs=o1[:, tl, :],
                                                           start=True, stop=True), reads=[o1, o2], writes=[p])
                        tl0 = sub * 32 + q4 * 4
                        dst = Gs[:, :, tl0:tl0 + 4]
                        src = p[:].rearrange("p (e t) -> p e t", t=4)
                        kb.op("act", lambda e: e.copy(out=dst, in_=src), reads=[p], writes=[Gs])
                for k8 in range(8):
                    kb.dma(["sp", "act", "pool"][k8 % 3],
                           D.G_d.ap()[k8 * 16:(k8 + 1) * 16, :, tg * 256:(tg + 1) * 256].rearrange("e1 e2 t -> e2 e1 t"),
                           Gs[:, k8 * 16:(k8 + 1) * 16, :], reads=[Gs], writes=[D.G_d])

            for g in range(8):
                p2_tile(2 * g)
                p2_tile(2 * g + 1)
                if g >= 1:
                    p3_group(g - 1)
            p3_group(7)
            kb.barrier()
        with contextlib.ExitStack() as st, kb.nc.named_scope("P4"):
            h2T = kb.sb("p_h2Tb", [128, 16, 2048], BF16, st)
            for dc in range(16):
                kb.dma("sp" if dc % 2 == 0 else "act", h2T[:, dc, :], D.h2T_d.ap()[dc], reads=[D.h2T_d], writes=[h2T])
            stg = [kb.sb(f"p4_stg{i}", [128, 16, 128], F32, st) for i in range(2)]
            ub = [kb.sb(f"p4_ub{i}", [128, 16, 128], BF16, st) for i in range(2)]
            Gc = [kb.sb(f"p4_Gc{i}", [128, 2048], BF16, st) for i in range(2)]
            ge = [kb.sb(f"p4_ge{i}", [128, 2048], BF16, st) for i in range(2)]
            pss = [[kb.ps(f"p4_ps{i}_{j}", [128, 512], F32, st) for j in range(4)] for i in range(2)]
            def load(e1):
                sg = stg[e1 % 2]
                kb.dma("sp", sg[:].rearrange("p a b -> p (a b)"), D.u_tabT.ap()[e1], writes=[sg])
                g = Gc[e1 % 2]
                kb.dma("pool", g[:], D.G_d.ap()[e1], reads=[D.G_d], writes=[g])
            load(0)
            for e1 in range(128):
                if e1 + 1 < 128:
                    load(e1 + 1)
                sg = stg[e1 % 2]
                u = ub[e1 % 2]
                kb.op("dve", lambda e: e.tensor_copy(out=u[:], in_=sg[:]), reads=[sg], writes=[u])
                g = Gc[e1 % 2]
                ps = pss[e1 % 2]
                a = ge[e1 % 2]
                for tb in range(4):
                    for dc in range(16):
                        kb.op("pe", lambda e: e.matmul(ps[tb][:], lhsT=u[:, dc, :], rhs=h2T[:, dc, tb * 512:(tb + 1) * 512],
                                                       start=(dc == 0), stop=(dc == 15)), reads=[u, h2T], writes=[ps[tb]])
                    kb.op("act", lambda e: e.activation(out=a[:, tb * 512:(tb + 1) * 512], in_=ps[tb][:], func=AF.Gelu), reads=[ps[tb]], writes=[a])
                kb.op("pool", lambda e: e.tensor_tensor(out=a[:], in0=a[:], in1=g[:], op=ALU.mult), reads=[a, g], writes=[a])
                kb.dma("sp", D.A_d.ap()[e1], a[:], reads=[a], writes=[D.A_d])
            kb.barrier()
        with contextlib.ExitStack() as st, kb.nc.named_scope("P5"):
            NB = 4
            vb = [kb.sb(f"p5_vb{i}", [128, 512], BF16, st) for i in range(NB)]
            ac = [kb.sb(f"p5_ac{i}", [128, 1024], BF16, st) for i in range(NB)]
            ov = [kb.sb(f"p5_ov{i}", [128, 512], F32, st) for i in range(2)]
            pss = [[kb.ps(f"p5_ps{j}_{h}", [128, 512], F32, st) for h in range(2)] for j in range(4)]
            seq = [(tb2, dg, e1) for tb2 in range(2) for dg in range(4) for e1 in range(128)]

            def load(i):
                tb2, dg, e1 = seq[i]
                k = i % NB
                kb.dma("sp", vb[k][:], D.vb_d.ap()[e1 * 128:(e1 + 1) * 128, dg * 512:(dg + 1) * 512], reads=[D.vb_d], writes=[vb[k]])
                kb.dma("pool", ac[k][:], D.A_d.ap()[e1][:, tb2 * 1024:(tb2 + 1) * 1024], reads=[D.A_d], writes=[ac[k]])
            load(0)
            load(1)
            load(2)
            oi = 0
            for i, (tb2, dg, e1) in enumerate(seq):
                if i + 3 < len(seq):
                    load(i + 3)
                k = i % NB
                for j in range(4):
                    for h in range(2):
                        kb.op("pe", lambda e: e.matmul(pss[j][h][:], lhsT=vb[k][:, j * 128:(j + 1) * 128], rhs=ac[k][:, h * 512:(h + 1) * 512],
                                                       start=(e1 == 0), stop=(e1 == 127)), reads=[vb[k], ac[k]], writes=[pss[j][h]])
                if e1 == 127:
                    for j in range(4):
                        for h in range(2):
                            o = ov[oi % 2]
                            if oi % 2 == 0:
                                kb.op("act", lambda e: e.copy(out=o[:], in_=pss[j][h][:]), reads=[pss[j][h]], writes=[o])
                            else:
                                kb.op("dve", lambda e: e.tensor_copy(out=o[:], in_=pss[j][h][:]), reads=[pss[j][h]], writes=[o])
                            oi += 1
                            d0 = dg * 512 + j * 128
                            t0 = tb2 * 1024 + h * 512
                            kb.dma("sp", D.peT_d.ap()[d0:d0 + 128, t0:t0 + 512], o[:], reads=[o], writes=[D.peT_d])
            kb.barrier()


def phase_F(kb, C, D):
    with contextlib.ExitStack() as st:
        wb = kb.sb("f_wb", [128, 2048], F32, st)
        kb.dma("pool", wb[:], D.normf_w.ap().partition_broadcast(128), writes=[wb])
        xs = [kb.sb(f"f_x{i}", [128, 2048], F32, st) for i in range(2)]
        pes = [kb.sb(f"f_pe{i}", [128, 16, 128], F32, st) for i in range(2)]
        junk = kb.sb("f_junk", [128, 2048], BF16, st)
        ss = kb.sb("f_ss", [128, NT], F32, st)
        rs = kb.sb("f_rs", [128, NT], F32, st)
        kb.op("pool", lambda e: e.memset(ss[:], 0.0), writes=[ss])
        pss = [[kb.ps(f"f_ps{i}_{j}", [128, 512], F32, st) for j in range(4)] for i in range(2)]
        pe_v = D.peT_d.ap().rearrange("(dc p) t -> p dc t", p=128)
        for tt in range(NT):
            xt = xs[tt % 2]
            pe = pes[tt % 2]
            ps = pss[tt % 2]
            kb.dma("sp", xt[:], D.x1_d.ap()[tt * 128:(tt + 1) * 128, :], reads=[D.x1_d], writes=[xt])
            kb.dma("act", pe[:], pe_v[:, :, tt * 128:(tt + 1) * 128], reads=[D.peT_d], writes=[pe])
            for dc in range(16):
                kb.op("pe", lambda e: e.transpose(out=ps[dc // 4][:, (dc % 4) * 128:(dc % 4 + 1) * 128], in_=pe[:, dc, :], identity=C.ident[:]),
                      reads=[pe, C.ident], writes=[ps[dc // 4]])
            for j in range(4):
                kb.op("dve", lambda e: e.tensor_tensor(out=xt[:, j * 512:(j + 1) * 512], in0=ps[j][:], in1=xt[:, j * 512:(j + 1) * 512], op=ALU.add),
                      reads=[ps[j], xt], writes=[xt])
            kb.op("act", lambda e: e.activation(out=junk[:], in_=xt[:], func=AF.Square, accum_out=ss[:, tt:tt + 1]), reads=[xt], writes=[junk, ss])
            kb.op("dve", lambda e: e.tensor_scalar(out=rs[:, tt:tt + 1], in0=ss[:, tt:tt + 1], scalar1=1.0 / 2048, scalar2=EPS,
                                                   op0=ALU.mult, op1=ALU.add), reads=[ss], writes=[rs])
            kb.op("act", lambda e: e.activation(out=rs[:, tt:tt + 1], in_=rs[:, tt:tt + 1], func=AF.Sqrt), reads=[rs], writes=[rs])
            kb.op("dve", lambda e: e.reciprocal(out=rs[:, tt:tt + 1], in_=rs[:, tt:tt + 1]), reads=[rs], writes=[rs])
            kb.op("dve", lambda e: e.scalar_tensor_tensor(out=xt[:], in0=xt[:], scalar=rs[:, tt:tt + 1], in1=wb[:], op0=ALU.mult, op1=ALU.mult),
                  reads=[xt, rs, wb], writes=[xt])
            kb.dma("sp", D.out.ap()[tt * 128:(tt + 1) * 128, :], xt[:], reads=[xt], writes=[D.out])
        kb.barrier()


STOP = 0
FAST_F32 = True
F32R = mybir.dt.float32r
def fr(ap):
    return ap.bitcast(F32R) if FAST_F32 else ap
class StopBuild(Exception):
    pass
def chk(n):
    if STOP == n:
        raise StopBuild()
GN_EPS = 64e-5
NEG_E05 = -0.6065306597126334


def phase_Rpre(kb, C, D):
    with contextlib.ExitStack() as st:
        rw = kb.sb("rp_rw", [128, 8, 8], F32, st)
        kb.dma("sp", rw[:], D.rw_c.ap(), writes=[rw])
        tmp = kb.sb("rp_tmp", [128, 2048], F32, st)
        lin = [kb.sb(f"rp_lin{i}", [128, 2048], BF16, st) for i in range(4)]
        for i in range(4):
            kb.op("pool", lambda e: e.memset(lin[i][:], 0.0), writes=[lin[i]])
            r0 = 3072 + i * 96
            kb.dma("sp", tmp[0:96, :], D.zs_d.ap()[r0:r0 + 96, :], reads=[D.zs_d], writes=[tmp])
            if i < 2:
                kb.op("act", lambda e: e.activation(out=lin[i][0:96, :], in_=tmp[0:96, :], func=AF.Tanh), reads=[tmp], writes=[lin[i]])
            else:
                kb.op("act", lambda e: e.copy(out=lin[i][0:96, :], in_=tmp[0:96, :]), reads=[tmp], writes=[lin[i]])
        sgl = kb.sb("rp_sgl", [128, 2, 2048], BF16, st)
        for kc in range(2):
            kb.dma("sp", tmp[:], D.zs_d.ap()[3456 + kc * 128:3456 + (kc + 1) * 128, :], reads=[D.zs_d], writes=[tmp])
            kb.op("act", lambda e: e.activation(out=sgl[:, kc, :], in_=tmp[:], func=AF.Sigmoid), reads=[tmp], writes=[sgl])
        wst = kb.sb("rp_wst", [128, 2048], F32, st)
        w2b = kb.sb("rp_w2b", [128, 2, 1024], BF16, st)
        a2b = kb.sb("rp_a2b", [128, 2, 1024], BF16, st)
        g2b = kb.sb("rp_g2b", [128, 2, 1024], BF16, st)
        wv = wst[:].rearrange("p (a c) -> p a c", a=2)
        kb.op("pool", lambda e: e.memset(w2b[:], 0.0), writes=[w2b])
        kb.op("pool", lambda e: e.memset(a2b[:], 0.0), writes=[a2b])
        kb.dma("sp", wv[0:96], D.w2.ap().rearrange("d l c -> l d c"), writes=[wst])
        kb.op("pool", lambda e: e.tensor_copy(out=w2b[0:96], in_=wv[0:96]), reads=[wst], writes=[w2b])
        kb.dma("sp", wv[0:96], D.a2.ap().rearrange("d l c -> l d c"), writes=[wst])
        kb.op("pool", lambda e: e.tensor_copy(out=a2b[0:96], in_=wv[0:96]), reads=[wst], writes=[a2b])
        kb.dma("sp", wv, D.g2.ap().rearrange("(kc p) c -> p kc c", p=128), writes=[wst])
        kb.op("pool", lambda e: e.tensor_copy(out=g2b[:], in_=wv), reads=[wst], writes=[g2b])
        outs = [kb.sb(f"rp_o{i}", [128, 2048], F32, st) for i in range(2)]
        pss = [[kb.ps(f"rp_ps{i}_{j}", [128, 512], F32, st) for j in range(4)] for i in range(2)]
        it = 0
        for cc in range(8):
            for d in range(2):
                for which in range(2):
                    ps = pss[it % 2]
                    o = outs[it % 2]
                    it += 1
                    wmat = w2b if which == 0 else a2b
                    xin = lin[d] if which == 0 else lin[2 + d]
                    bias = rw[:, cc, d:d + 1] if which == 0 else rw[:, cc, 2 + d:3 + d]
                    for tb in range(4):
                        kb.op("pe", lambda e: e.matmul(ps[tb][:], lhsT=wmat[:, d, cc * 128:(cc + 1) * 128], rhs=xin[:, tb * 512:(tb + 1) * 512],
                                                       start=True, stop=True), reads=[wmat, xin], writes=[ps[tb]])
                        kb.op("act", lambda e: e.activation(out=o[:, tb * 512:(tb + 1) * 512], in_=ps[tb][:], func=AF.Sigmoid, bias=bias),
                              reads=[ps[tb], rw], writes=[o])
                    if which == 0:
                        kb.op("dve", lambda e: e.tensor_scalar(out=o[:], in0=o[:], scalar1=NEG_E05, scalar2=None, op0=ALU.mult), reads=[o], writes=[o])
                        kb.dma("sp", D.ld_d.ap()[d, cc * 128:(cc + 1) * 128, :], o[:], reads=[o], writes=[D.ld_d])
                    else:
                        kb.dma("sp", D.a_d.ap()[d, cc * 128:(cc + 1) * 128, :], o[:], reads=[o], writes=[D.a_d])
        for tt in range(NT):
            ps = pss[tt % 2]
            o = outs[tt % 2]
            for hf in range(2):
                for kc in range(2):
                    kb.op("pe", lambda e: e.matmul(ps[hf][:], lhsT=sgl[:, kc, tt * 128:(tt + 1) * 128], rhs=g2b[:, kc, hf * 512:(hf + 1) * 512],
                                                   start=(kc == 0), stop=(kc == 1)), reads=[sgl, g2b], writes=[ps[hf]])
                kb.op("act", lambda e: e.copy(out=o[:, hf * 512:(hf + 1) * 512], in_=ps[hf][:]), reads=[ps[hf]], writes=[o])
            kb.dma("sp", D.g_d.ap()[tt * 128:(tt + 1) * 128, :], o[:, 0:1024], reads=[o], writes=[D.g_d])
        kb.barrier()


def phase_R(kb, C, D, ccs=range(8), dbg=None):
    with contextlib.ExitStack() as st:
        rw = kb.sb("r_rw", [128, 8, 8], F32, st)
        kb.dma("sp", rw[:], D.rw_c.ap(), writes=[rw])
        masks = kb.sb("r_masks", [128, 6, 128], F32, st)
        kb.dma("sp", masks[:], D.masks.ap(), writes=[masks])
        cst = kb.sb("r_cst", [128, 66], F32, st)
        kb.dma("sp", cst[:], D.rcst.ap()[:, 0:66], writes=[cst])
        segt = kb.sb("r_segm", [128, 2048], BF16, st)
        kb.dma("sp", segt[:], D.segm.ap(), writes=[segt])
        ident2 = cst[:, 0:64]
        sel = cst[:, 64:66]
        lnw = kb.sb("r_lnw", [128, 128], F32, st)
        lnb = kb.sb("r_lnb", [128, 128], F32, st)
        Rr = kb.sb("r_R", [128, 2048], F32, st)
        Kk = kb.sb("r_K", [128, 2048], F32, st)
        Vv = kb.sb("r_V", [128, 2048], F32, st)
        KKn = kb.sb("r_KK", [128, 2048], F32, st)
        E1 = kb.sb("r_E1", [128, 2048], F32, st)
        XI = kb.sb("r_XI", [128, 2048], F32, st)
        XE = kb.sb("r_XE", [128, 2048], F32, st)
        Aa = kb.sb("r_A", [128, 2048], F32, st)
        KD = kb.sb("r_KD", [128, 2048], F32, st)
        KT = kb.sb("r_KT", [128, 2048], F32, st)
        AT = kb.sb("r_AT", [128, 2048], F32, st)
        BTb = kb.sb("r_BTb", [128, 2048], F32, st)
        stt = kb.sb("r_stt", [128, 160], F32, st)
        Vtm = kb.sb("r_Vtm", [128, NT, 128], F32, st)
        Ysum = kb.sb("r_Ysum", [128, NT, 128], F32, st)
        MTa = kb.sb("r_MTa", [128, 32, 64], F32, st)
        Ca = kb.sb("r_Ca", [128, 32, 64], F32, st)
        H = [kb.sb(f"r_H{i}", [128, 64], F32, st) for i in range(2)]
        tot = kb.sb("r_tot", [128, 32], F32, st)
        GL = kb.sb("r_GL", [128, 32], F32, st)
        RhT = Vv

        class Set:
            pass
        sets = []
        for p in range(2):
            S = Set()
            S.XA = kb.sb(f"r_XA{p}", [128, 2, 2, 128], F32, st)
            S.KBm = kb.sb(f"r_KBm{p}", [128, 2, 2, 128], F32, st)
            S.PQ = [kb.sb(f"r_PQ{p}_{i}", [128, 2, 2, 128], BF16, st) for i in range(2)]
            S.QT = [kb.sb(f"r_QT{p}_{i}", [128, 2, 128], BF16, st) for i in range(2)]
            S.XT = kb.sb(f"r_XT{p}", [128, 2, 128], F32, st)
            S.TT = kb.sb(f"r_TT{p}", [128, 2, 128], F32, st)
            S.BW = kb.sb(f"r_BW{p}", [128, 2, 128], F32, st)
            S.BU = kb.sb(f"r_BU{p}", [128, 2, 128], F32, st)
            S.AGtm = kb.sb(f"r_AGtm{p}", [128, 128], F32, st)
            S.KGtm = kb.sb(f"r_KGtm{p}", [128, 128], F32, st)
            S.B = [kb.ps(f"r_bk{p}_{i}", [128, 512], F32, st) for i in range(4)]
            sets.append(S)
        psX = sets[0].B[0]
        bkH = [sets[0].B[1], sets[0].B[2]]

        v4 = lambda b: b[:].rearrange("p (h q s) -> p h q s", h=2, q=2)
        v3 = lambda b, lo: b[:, lo:lo + 256].rearrange("p (h s) -> p h s", h=2)

        def tile_gen(S, d, tt, RT, BT):
            M2 = masks[:, 2 * d:2 * d + 2, :]
            MST = masks[:, 2 - 2 * d, :]
            XA, KBm, PQ, QT, BW, BU, AGtm, KGtm = S.XA, S.KBm, S.PQ, S.QT, S.BW, S.BU, S.AGtm, S.KGtm
            B0, B1, B2, B3 = S.B
            psA, psB, psN = v4(B0), v4(B1), v4(B3)
            psC = v3(B2, 0)
            cols = slice(tt * 128, (tt + 1) * 128)
            for hh in range(2):
                pr = slice(hh * 64, hh * 64 + 64)
                kb.pe_fence()
                kb.op("pe", lambda e: e.matmul(psA[:, hh, 0, :], lhsT=fr(AT[pr, cols]), rhs=fr(BT[pr, cols]), start=True, stop=True),
                      reads=[AT, BT], writes=[B0])
                kb.op("pe", lambda e: e.matmul(psA[:, hh, 1, :], lhsT=fr(AT[pr, cols]), rhs=fr(RT[pr, cols]), start=True, stop=True),
                      reads=[AT, RT], writes=[B0])
                kb.op("pe", lambda e: e.matmul(psB[:, hh, 0, :], lhsT=fr(KT[pr, cols]), rhs=fr(BT[pr, cols]), start=True, stop=True),
                      reads=[KT, BT], writes=[B1])
                kb.op("pe", lambda e: e.matmul(psB[:, hh, 1, :], lhsT=fr(KT[pr, cols]), rhs=fr(RT[pr, cols]), start=True, stop=True),
                      reads=[KT, RT], writes=[B1])
                kb.op("pe", lambda e: e.matmul(psC[:, hh, :], lhsT=fr(BT[pr, cols]), rhs=fr(AT[pr, cols]), start=True, stop=True),
                      reads=[AT, BT], writes=[B2])
            kb.pe_fence()
            yield
            M2b = M2[:, None, :, :].to_broadcast([128, 2, 2, 128])
            q0, q1 = S.XT, QT[1]
            kb.op("dve", lambda e: e.tensor_tensor(out=fr(XA[:]), in0=psA, in1=M2b, op=ALU.mult), reads=[B0, masks], writes=[XA])
            kb.op("dve", lambda e: e.tensor_tensor(out=fr(q0[:]), in0=psC, in1=MST[:, None, :].to_broadcast([128, 2, 128]), op=ALU.mult),
                  reads=[B2, masks], writes=[q0])
            kb.op("dve", lambda e: e.tensor_tensor(out=fr(KBm[:]), in0=psB, in1=M2b, op=ALU.mult), reads=[B1, masks], writes=[KBm])
            pq = PQ[0]
            kb.op("pool", lambda e: e.tensor_tensor(out=pq[:, :, 0, :], in0=XA[:, :, 0, :], in1=C.ident[:, None, :].to_broadcast([128, 2, 128]),
                                                    op=ALU.add), reads=[XA, C.ident], writes=[pq])
            yield
            for hh in range(2):
                kb.op("pe", lambda e: e.matmul(psN[:, hh, 1, :], lhsT=fr(q0[:, hh, :]), rhs=fr(XA[:, hh, 0, :]), start=True, stop=True),
                      reads=[q0, XA], writes=[B3])
                kb.op("pe", lambda e: e.matmul(psC[:, hh, :], lhsT=fr(XA[:, hh, 0, :]), rhs=fr(q0[:, hh, :]), start=True, stop=True),
                      reads=[q0, XA], writes=[B2])
            yield
            kb.op("act", lambda e: e.copy(out=pq[:, :, 1, :], in_=psN[:, :, 1, :]), reads=[B3], writes=[pq])
            kb.op("dve", lambda e: e.tensor_copy(out=q1[:], in_=psC), reads=[B2], writes=[q1])
            yield
            cur = 0
            qcur = 1
            for lev in range(1, 6):
                pq = PQ[cur]
                pqn = PQ[1 - cur]
                qt = QT[qcur]
                qtn = QT[1 - qcur]
                last = (lev == 5)
                for hh in range(2):
                    if last:
                        kb.op("pe", lambda e: e.matmul(psN[:, hh, 0, :], lhsT=qt[:, hh, :], rhs=pq[:, hh, 0, :], start=True, stop=True),
                              reads=[qt, pq], writes=[B3])
                    else:
                        kb.op("pe", lambda e: e.matmul(psN[:, hh, :, :], lhsT=qt[:, hh, :], rhs=pq[:, hh, :, :], start=True, stop=True),
                              reads=[qt, pq], writes=[B3])
                        kb.op("pe", lambda e: e.matmul(psC[:, hh, :], lhsT=pq[:, hh, 1, :], rhs=qt[:, hh, :], start=True, stop=True),
                              reads=[qt, pq], writes=[B2])
                yield
                if last:
                    kb.op("dve", lambda e: e.tensor_tensor(out=fr(S.TT[:]), in0=psN[:, :, 0, :], in1=pq[:, :, 0, :], op=ALU.add),
                          reads=[B3, pq], writes=[S.TT])
                else:
                    kb.op("dve", lambda e: e.tensor_tensor(out=pqn[:, :, 0, :], in0=psN[:, :, 0, :], in1=pq[:, :, 0, :], op=ALU.add),
                          reads=[B3, pq], writes=[pqn])
                    kb.op("act", lambda e: e.copy(out=pqn[:, :, 1, :], in_=psN[:, :, 1, :]), reads=[B3], writes=[pqn])
                    kb.op("dve", lambda e: e.tensor_copy(out=qtn[:], in_=psC), reads=[B2], writes=[qtn])
                yield
                cur = 1 - cur
                qcur = 1 - qcur
            TT = S.TT
            W_ = B0[:, 0:128].rearrange("p (h i) -> p h i", h=2)
            Bt_ = B0[:, 128:256]
            U_ = B0[:, 256:512].rearrange("p (h i) -> p h i", h=2)
            for hh in range(2):
                kb.op("pe", lambda e: e.matmul(W_[:, hh, :], lhsT=fr(KBm[:, hh, 0, :]), rhs=fr(Vtm[:, tt, hh * 64:(hh + 1) * 64]), start=True, stop=True),
                      reads=[KBm, Vtm], writes=[B0])
            kb.op("pe", lambda e: e.transpose(out=Bt_, in_=BT[:, cols], identity=C.ident[:]), reads=[BT, C.ident], writes=[B0])
            AG_ = B1[:, 256:384]
            KG_ = B1[:, 384:512]
            kb.op("pe", lambda e: e.transpose(out=AG_, in_=Aa[:, cols], identity=C.ident[:]), reads=[Aa, C.ident], writes=[B1])
            kb.op("pe", lambda e: e.transpose(out=KG_, in_=KD[:, cols], identity=C.ident[:]), reads=[KD, C.ident], writes=[B1])
            yield
            kb.op("act", lambda e: e.copy(out=fr(BW[:, :, 64:128]), in_=W_), reads=[B0], writes=[BW])
            kb.op("dve", lambda e: e.tensor_copy(out=fr(BW[:, :, 0:64]), in_=Bt_.rearrange("p (h j) -> p h j", h=2)), reads=[B0], writes=[BW])
            kb.op("act", lambda e: e.copy(out=AGtm[:], in_=AG_), reads=[B1], writes=[AGtm])
            kb.op("act", lambda e: e.copy(out=KGtm[:], in_=KG_), reads=[B1], writes=[KGtm])
            yield
            for hh in range(2):
                kb.op("pe", lambda e: e.matmul(U_[:, hh, :], lhsT=fr(TT[:, hh, :]), rhs=fr(BW[:, hh, :]), start=True, stop=True),
                      reads=[TT, BW], writes=[B0])
            yield
            kb.op("dve", lambda e: e.tensor_copy(out=fr(BU[:]), in_=U_), reads=[B0], writes=[BU])
            yield
            Y_ = B1[:, 0:128].rearrange("p (h i) -> p h i", h=2)
            R_ = B1[:, 128:256]
            for hh in range(2):
                kb.op("pe", lambda e: e.matmul(Y_[:, hh, :], lhsT=fr(XA[:, hh, 1, :]), rhs=fr(BU[:, hh, 64:128]), start=True, stop=False),
                      reads=[XA, BU], writes=[B1])
                kb.op("pe", lambda e: e.matmul(Y_[:, hh, :], lhsT=fr(KBm[:, hh, 1, :]), rhs=fr(Vtm[:, tt, hh * 64:(hh + 1) * 64]), start=False, stop=True),
                      reads=[KBm, Vtm], writes=[B1])
            for hh in range(2):
                kb.op("pe", lambda e: e.matmul(R_[hh * 64:(hh + 1) * 64, :], lhsT=BU[:, hh, 0:64], rhs=XA[:, hh, 1, :], start=True, stop=True),
                      reads=[BU, XA], writes=[B1])
            for n in range(2):
                tr = slice(n * 64, n * 64 + 64)
                bk = (B2, B3)[n]
                for hh in range(2):
                    pr = slice(hh * 64, hh * 64 + 64)
                    kb.op("pe", lambda e: e.matmul(bk[pr, 0:64], lhsT=BU[tr, hh, 0:64], rhs=AGtm[tr, pr], start=True, stop=True),
                          reads=[BU, AGtm], writes=[bk])
                    kb.op("pe", lambda e: e.matmul(bk[pr, 64:128], lhsT=AGtm[tr, pr], rhs=BU[tr, hh, 64:128], start=True, stop=False),
                          reads=[BU, AGtm], writes=[bk])
                    kb.op("pe", lambda e: e.matmul(bk[pr, 64:128], lhsT=KGtm[tr, pr], rhs=Vtm[tr, tt, pr], start=False, stop=True),
                          reads=[KGtm, Vtm], writes=[bk])
            kb.pe_fence()
            yield
            if d == 0:
                kb.op("act", lambda e: e.copy(out=Ysum[:, tt, :], in_=B1[:, 0:128]), reads=[B1], writes=[Ysum])
            else:
                kb.op("dve", lambda e: e.tensor_tensor(out=Ysum[:, tt, :], in0=B1[:, 0:128], in1=Ysum[:, tt, :], op=ALU.add),
                      reads=[B1, Ysum], writes=[Ysum])
            kb.op("dve", lambda e: e.tensor_tensor(out=RhT[:, cols], in0=R_, in1=RT[:, cols], op=ALU.add), reads=[B1, RT], writes=[RhT])
            for n in range(2):
                ch = tt * 2 + n
                bk = (B2, B3)[n]
                kb.op("dve", lambda e: e.scalar_tensor_tensor(out=MTa[:, ch, :], in0=ident2, scalar=GL[:, ch:ch + 1], in1=bk[:, 0:64],
                                                              op0=ALU.mult, op1=ALU.add), reads=[cst, GL, bk], writes=[MTa])
                kb.op("act", lambda e: e.copy(out=Ca[:, ch, :], in_=bk[:, 64:128]), reads=[bk], writes=[Ca])
            yield

        def run_tiles(d, RT, BT):
            pending = list(range(NT))
            active = []
            free_sets = [sets[0], sets[1]]
            S = free_sets.pop(0)
            g = tile_gen(S, d, pending.pop(0), RT, BT)
            active.append((g, S))
            for _ in range(8):
                next(g)
            while active or pending:
                if pending and free_sets:
                    S = free_sets.pop(0)
                    active.append((tile_gen(S, d, pending.pop(0), RT, BT), S))
                for item in list(active):
                    g, S = item
                    try:
                        next(g)
                    except StopIteration:
                        active.remove(item)
                        free_sets.append(S)

        for cc in ccs:
            rows = slice(cc * 128, (cc + 1) * 128)
            kb.dma("sp", Rr[:], D.zs_d.ap()[cc * 128:(cc + 1) * 128, :], reads=[D.zs_d], writes=[Rr])
            kb.dma("act", Kk[:], D.zs_d.ap()[1024 + cc * 128:1024 + (cc + 1) * 128, :], reads=[D.zs_d], writes=[Kk])
            kb.dma("sp", Vv[:], D.zs_d.ap()[2048 + cc * 128:2048 + (cc + 1) * 128, :], reads=[D.zs_d], writes=[Vv])
            kb.dma("pool", lnw[:], D.lnx_w.ap()[cc * 128:(cc + 1) * 128].partition_broadcast(128), writes=[lnw])
            kb.dma("pool", lnb[:], D.lnx_b.ap()[cc * 128:(cc + 1) * 128].partition_broadcast(128), writes=[lnb])
            for g in range(4):
                for j in range(4):
                    tt = g * 4 + j
                    kb.op("pe", lambda e: e.transpose(out=psX[:, j * 128:(j + 1) * 128], in_=Vv[:, tt * 128:(tt + 1) * 128], identity=C.ident[:]),
                          reads=[Vv, C.ident], writes=[psX])
                kb.op("act", lambda e: e.copy(out=fr(Vtm[:, g * 4:(g + 1) * 4, :]), in_=psX[:].rearrange("p (j c) -> p j c", j=4)), reads=[psX], writes=[Vtm])
            kb.op("act", lambda e: e.activation(out=KKn[:], in_=Kk[:], func=AF.Copy, scale=rw[:, cc, 4:5]),
                  reads=[Kk, rw], writes=[KKn])
            kb.op("act", lambda e: e.activation(out=XI[:], in_=KKn[:], func=AF.Square), reads=[KKn], writes=[XI])
            for tb in range(4):
                kb.op("pe", lambda e: e.matmul(psX[:], lhsT=masks[:, 5, :], rhs=XI[:, tb * 512:(tb + 1) * 512], start=True, stop=True),
                      reads=[masks, XI], writes=[psX])
                kb.op("act", lambda e: e.activation(out=XE[:, tb * 512:(tb + 1) * 512], in_=psX[:], func=AF.Sqrt), reads=[psX], writes=[XE])
            kb.op("dve", lambda e: e.tensor_scalar(out=XE[:], in0=XE[:], scalar1=1e-12, scalar2=None, op0=ALU.max), reads=[XE], writes=[XE])
            kb.op("dve", lambda e: e.reciprocal(out=XE[:], in_=XE[:]), reads=[XE], writes=[XE])
            kb.op("pool", lambda e: e.tensor_tensor(out=KKn[:], in0=KKn[:], in1=XE[:], op=ALU.mult), reads=[KKn, XE], writes=[KKn])
            chk(1)
            def prep_gen(d):
                yield
                kb.dma("sp", XE[:], D.ld_d.ap()[d, cc * 128:(cc + 1) * 128, :], reads=[D.ld_d], writes=[XE])
                yield
                kb.dma("act", Aa[:], D.a_d.ap()[d, cc * 128:(cc + 1) * 128, :], reads=[D.a_d], writes=[Aa])
                yield
                kb.op("dve", lambda e: e.tensor_scalar(out=KD[:], in0=Aa[:], scalar1=-1.0, scalar2=rw[:, cc, 5:6], op0=ALU.add, op1=ALU.mult),
                      reads=[Aa, rw], writes=[KD])
                yield
                kb.op("dve", lambda e: e.scalar_tensor_tensor(out=KD[:], in0=KD[:], scalar=1.0, in1=Kk[:], op0=ALU.add, op1=ALU.mult),
                      reads=[KD, Kk], writes=[KD])
                yield
                kb.op("dve", lambda e: e.scalar_tensor_tensor(out=XI[:], in0=KD[:], scalar=rw[:, cc, 6:7], in1=Rr[:], op0=ALU.mult, op1=ALU.mult),
                      reads=[KD, rw, Rr], writes=[XI])
                for tt in range(NT):
                    kb.op("pe", lambda e: e.matmul(psX[:, tt * 2:tt * 2 + 2], lhsT=XI[:, tt * 128:(tt + 1) * 128], rhs=sel, start=True, stop=True),
                          reads=[XI, cst], writes=[psX])
                yield
                kb.op("act", lambda e: e.copy(out=stt[:, 64 + 32 * d:96 + 32 * d], in_=psX[:, 0:32]), reads=[psX], writes=[stt])
                yield
                kb.op("pool", lambda e: e.tensor_tensor(out=Aa[:], in0=Aa[:], in1=KKn[:], op=ALU.mult), reads=[Aa, KKn], writes=[Aa])
                yield
                kb.op("dve", lambda e: e.tensor_tensor_scan(out=XI[:], data0=segt[:], data1=XE[:], initial=0.0, op0=ALU.mult, op1=ALU.add),
                      reads=[segt, XE], writes=[XI])
                yield
                kb.op("pool", lambda e: e.tensor_copy(out=tot[:], in_=XI[:].rearrange("p (n s) -> p n s", s=64)[:, :, 63]), reads=[XI], writes=[tot])
                yield
                kb.op("act", lambda e: e.activation(out=GL[:], in_=tot[:], func=AF.Exp), reads=[tot], writes=[GL])
                if d == 0:
                    kb.op("pool", lambda e: e.tensor_tensor(out=XE[:], in0=XI[:], in1=XE[:], op=ALU.subtract), reads=[XI, XE], writes=[XE])
                else:
                    kb.op("dve", lambda e: e.tensor_tensor(out=XI[:].rearrange("p (n s) -> p n s", s=64),
                                                           in0=tot[:, :, None].to_broadcast([128, 32, 64]),
                                                           in1=XI[:].rearrange("p (n s) -> p n s", s=64), op=ALU.subtract),
                          reads=[tot, XI], writes=[XI])
                    kb.op("pool", lambda e: e.tensor_tensor(out=XE[:], in0=XI[:], in1=XE[:], op=ALU.add), reads=[XI, XE], writes=[XE])
                cI, cE = (XI, XE) if d == 0 else (XE, XI)
                yield
                kb.op("act", lambda e: e.activation(out=fr(E1[:]), in_=cI[:], func=AF.Exp), reads=[cI], writes=[E1])
                yield
                kb.op("act", lambda e: e.activation(out=cI[:], in_=cI[:], func=AF.Exp, scale=-1.0), reads=[cI], writes=[cI])
                yield
                kb.op("act", lambda e: e.activation(out=cE[:], in_=cE[:], func=AF.Exp), reads=[cE], writes=[cE])
                yield
                kb.op("dve", lambda e: e.tensor_tensor(out=fr(E1[:]), in0=E1[:], in1=Rr[:], op=ALU.mult), reads=[E1, Rr], writes=[E1])
                yield
                kb.op("pool", lambda e: e.tensor_tensor(out=fr(KT[:]), in0=KD[:], in1=cI[:], op=ALU.mult), reads=[KD, cI], writes=[KT])
                yield
                kb.op("dve", lambda e: e.tensor_tensor(out=fr(AT[:]), in0=Aa[:], in1=cI[:], op=ALU.mult), reads=[Aa, cI], writes=[AT])
                yield
                kb.op("dve", lambda e: e.scalar_tensor_tensor(out=fr(BTb[:]), in0=cE[:], scalar=-1.0, in1=KKn[:], op0=ALU.mult, op1=ALU.mult),
                      reads=[cE, KKn], writes=[BTb])
                yield
                kb.op("pool", lambda e: e.tensor_tensor(out=cI[:].rearrange("p (n s) -> p n s", s=64), in0=cI[:].rearrange("p (n s) -> p n s", s=64),
                                                        in1=GL[:, :, None].to_broadcast([128, 32, 64]), op=ALU.mult), reads=[cI, GL], writes=[cI])
                yield
                kb.op("dve", lambda e: e.tensor_tensor(out=KD[:], in0=KD[:], in1=cI[:], op=ALU.mult), reads=[KD, cI], writes=[KD])
                yield
                kb.op("pool", lambda e: e.tensor_tensor(out=Aa[:], in0=Aa[:], in1=cI[:], op=ALU.mult), reads=[Aa, cI], writes=[Aa])
                yield

            def seq_gen(d):
                kb.op("pool", lambda e: e.memset(H[0][:], 0.0), writes=[H[0]])
                order = range(32) if d == 0 else range(31, -1, -1)
                hc = 0
                for ch in order:
                    tt, n = ch // 2, ch % 2
                    ccols = slice(ch * 64, ch * 64 + 64)
                    Hc, Hn = H[hc], H[1 - hc]
                    tr = slice(n * 64, n * 64 + 64)
                    for hh in range(2):
                        pr = slice(hh * 64, hh * 64 + 64)
                        bk = bkH[hh]
                        kb.op("pe", lambda e: e.matmul(bk[pr, 64:128], lhsT=MTa[pr, ch, :], rhs=Hc[pr, :], start=True, stop=True),
                              reads=[MTa, Hc], writes=[bk])
                        kb.op("pe", lambda e: e.matmul(bk[tr, 0:64], lhsT=RhT[pr, ccols], rhs=Hc[pr, :], start=True, stop=True),
                              reads=[RhT, Hc], writes=[bk])
                    for hh in range(2):
                        pr = slice(hh * 64, hh * 64 + 64)
                        bk = bkH[hh]
                        kb.op("dve", lambda e: e.tensor_tensor(out=Hn[pr, :], in0=bk[pr, 64:128], in1=Ca[pr, ch, :], op=ALU.add),
                              reads=[bk, Ca], writes=[Hn])
                    for hh in range(2):
                        pr = slice(hh * 64, hh * 64 + 64)
                        bk = bkH[hh]
                        kb.op("act" if False else "dve", lambda e: e.tensor_tensor(out=Ysum[tr, tt, pr], in0=bk[tr, 0:64], in1=Ysum[tr, tt, pr], op=ALU.add),
                              reads=[bk, Ysum], writes=[Ysum])
                    hc = 1 - hc
                    yield
                yield

            def exhaust(g):
                for _ in g:
                    pass

            exhaust(prep_gen(0))
            run_tiles(0, E1, BTb)
            sg = seq_gen(0)
            pg = prep_gen(1)
            done_p = done_s = False
            while not (done_p and done_s):
                if not done_p:
                    try:
                        next(pg)
                    except StopIteration:
                        done_p = True
                for _ in range(2):
                    if not done_s:
                        try:
                            next(sg)
                        except StopIteration:
                            done_s = True
            run_tiles(1, E1, BTb)
            exhaust(seq_gen(1))
            chk(10)
            if dbg is not None and "Ysum" in dbg:
                kb.dma("sp", D.dbgY.ap()[:, cc * 128:(cc + 1) * 128].rearrange("(tt p) c -> p tt c", p=128), Ysum[:], reads=[Ysum], writes=[D.dbgY])
            Y3 = Ysum[:].rearrange("p t (h i) -> p (t h) i", h=2)
            st_mu = stt[:, 0:32]
            st_var = stt[:, 32:64]
            bon = stt[:, 128:160]
            kb.op("dve", lambda e: e.tensor_reduce(out=st_mu, in_=Y3, axis=AX.X, op=ALU.add), reads=[Ysum], writes=[stt])
            kb.op("dve", lambda e: e.tensor_scalar(out=st_mu, in0=st_mu, scalar1=1.0 / 64, scalar2=None, op0=ALU.mult), reads=[stt], writes=[stt])
            kb.op("dve", lambda e: e.tensor_tensor(out=Y3, in0=Y3, in1=st_mu[:, :, None].to_broadcast([128, 32, 64]), op=ALU.subtract),
                  reads=[Ysum, stt], writes=[Ysum])
            sqv = XI[:].rearrange("p (a i) -> p a i", i=64)
            kb.op("act", lambda e: e.activation(out=sqv, in_=Y3, func=AF.Square), reads=[Ysum], writes=[XI])
            kb.op("dve", lambda e: e.tensor_reduce(out=st_var, in_=sqv, axis=AX.X, op=ALU.add), reads=[XI], writes=[stt])
            kb.op("dve", lambda e: e.tensor_scalar(out=st_var, in0=st_var, scalar1=1.0 / 64, scalar2=GN_EPS, op0=ALU.mult, op1=ALU.add),
                  reads=[stt], writes=[stt])
            kb.op("act", lambda e: e.activation(out=st_var, in_=st_var, func=AF.Sqrt), reads=[stt], writes=[stt])
            kb.op("dve", lambda e: e.reciprocal(out=st_var, in_=st_var), reads=[stt], writes=[stt])
            kb.op("dve", lambda e: e.tensor_tensor(out=Y3, in0=Y3, in1=st_var[:, :, None].to_broadcast([128, 32, 64]), op=ALU.mult),
                  reads=[Ysum, stt], writes=[Ysum])
            kb.op("pool", lambda e: e.tensor_tensor(out=Ysum[:], in0=Ysum[:], in1=lnw[:, None, :].to_broadcast([128, NT, 128]), op=ALU.mult),
                  reads=[Ysum, lnw], writes=[Ysum])
            kb.op("pool", lambda e: e.tensor_tensor(out=Ysum[:], in0=Ysum[:], in1=lnb[:, None, :].to_broadcast([128, NT, 128]), op=ALU.add),
                  reads=[Ysum, lnb], writes=[Ysum])
            kb.op("dve", lambda e: e.tensor_tensor(out=bon, in0=stt[:, 64:96], in1=stt[:, 96:128], op=ALU.add), reads=[stt], writes=[stt])
            kb.op("dve", lambda e: e.tensor_scalar(out=bon, in0=bon, scalar1=0.5, scalar2=None, op0=ALU.mult), reads=[stt], writes=[stt])
            V3 = Vtm[:].rearrange("p t (h i) -> p (t h) i", h=2)
            kb.op("pool", lambda e: e.tensor_tensor(out=sqv, in0=V3, in1=bon[:, :, None].to_broadcast([128, 32, 64]), op=ALU.mult),
                  reads=[Vtm, stt], writes=[XI])
            kb.op("pool", lambda e: e.tensor_tensor(out=Y3, in0=Y3, in1=sqv, op=ALU.add), reads=[Ysum, XI], writes=[Ysum])
            gt = XE[:].rearrange("p (t c) -> p t c", c=128)
            kb.dma("sp", gt, D.g_d.ap()[:, cc * 128:(cc + 1) * 128].rearrange("(tt p) c -> p tt c", p=128), reads=[D.g_d], writes=[XE])
            kb.op("dve", lambda e: e.tensor_tensor(out=Ysum[:], in0=Ysum[:], in1=gt, op=ALU.mult), reads=[Ysum, XE], writes=[Ysum])
            if dbg is not None and "yfin" in dbg:
                kb.dma("sp", D.dbgF.ap()[:, cc * 128:(cc + 1) * 128].rearrange("(tt p) c -> p tt c", p=128), Ysum[:], reads=[Ysum], writes=[D.dbgF])
            for g in range(4):
                for j in range(4):
                    tt = g * 4 + j
                    kb.op("pe", lambda e: e.transpose(out=psX[:, j * 128:(j + 1) * 128], in_=Ysum[:, tt, :], identity=C.ident[:]),
                          reads=[Ysum, C.ident], writes=[psX])
                kb.op("act", lambda e: e.copy(out=KD[:, g * 512:(g + 1) * 512], in_=psX[:]), reads=[psX], writes=[KD])
            yb = Aa[:].bitcast(BF16)[:, 0:2048]
            kb.op("dve", lambda e: e.tensor_copy(out=yb, in_=KD[:]), reads=[KD], writes=[Aa])
            kb.dma("sp", D.ymT_d.ap()[cc], yb, reads=[Aa], writes=[D.ymT_d])
        kb.barrier()


def host_consts():
    S = 2048
    rows = np.repeat(np.arange(32), 64).astype(np.float32)
    cols = np.tile(np.arange(64), 32).astype(np.float32)
    inv = (10000.0 ** (-np.arange(0, 32, 2, dtype=np.float32) / 32)).astype(np.float32)
    ar = rows[:, None] * inv[None]
    ac = cols[:, None] * inv[None]
    tabC = np.concatenate([np.cos(ar), np.cos(ac)], 1).astype(np.float32)
    tabS = np.concatenate([np.sin(ar), np.sin(ac)], 1).astype(np.float32)
    ident = np.eye(128, dtype=np.float32)
    iota = np.tile(np.arange(128, dtype=np.float32)[None], (128, 1))
    r = np.arange(128)[:, None]
    s = np.arange(128)[None, :]
    same = (r // 64) == (s // 64)
    masks = np.zeros((128, 6, 128), np.float32)
    masks[:, 0] = same & (r < s)
    masks[:, 1] = same & (r <= s)
    masks[:, 2] = same & (r > s)
    masks[:, 3] = same & (r >= s)
    masks[:, 4] = same & (r > s)
    masks[:, 5] = same
    rcst = np.zeros((128, 66 + 2048), np.float32)
    pp = np.arange(128)
    rcst[pp, pp % 64] = 1.0
    rcst[:, 64] = (pp // 64 == 0)
    rcst[:, 65] = (pp // 64 == 1)
    seg = np.ones(2048, np.float32); seg[::64] = 0.0
    rcst[:, 66:] = seg[None]
    import ml_dtypes
    segm = np.ascontiguousarray(np.broadcast_to(seg[None], (128, 2048))).astype(ml_dtypes.bfloat16)
    return dict(tabC=tabC, tabS=tabS, ident=ident, iota=iota, masks=masks, rcst=rcst, segm=segm)

def prep_shared(inp):
    L = 0
    d = host_consts()
    mu_c = np.zeros((128, 3 * NRCH), np.float32)
    for ci, (c0, cs) in enumerate(RCH):
        mu_c[:cs, ci] = inp["mu_prev"][L, c0:c0 + cs]
        mu_c[:cs, NRCH + ci] = inp["mu_next"][L, c0:c0 + cs]
    d["mu_c"] = mu_c
    w = inp["w_in"][L].reshape(16, 128, 5248)
    wt = np.empty((128, 16 * 5248), np.float32)
    for (c0, cs) in RCH:
        wt[:, 16 * c0:16 * (c0 + cs)] = w[:, :, c0:c0 + cs].transpose(1, 0, 2).reshape(128, 16 * cs)
    RWK = 3712
    for cg in range(3):
        for hf in range(2):
            o0 = 16 * RWK + (cg * 2 + hf) * 4096
            wt[:, o0:o0 + 4096] = w[hf * 8:(hf + 1) * 8, :, RWK + cg * 512:RWK + (cg + 1) * 512].transpose(1, 0, 2).reshape(128, 4096)
    d["w_in"] = wt
    for k in ["norm1_w", "norm2_w", "q_norm_w", "k_norm_w", "w_out", "w_pq", "w2", "a2", "g2", "lnx_w", "lnx_b", "v_tab"]:
        d[k] = np.ascontiguousarray(inp[k][L])
    d["normf_w"] = np.ascontiguousarray(inp["normf_w"])
    d["skT"] = np.ascontiguousarray(inp["sub_keys"][L].reshape(16, 128, 128).transpose(2, 0, 1))
    d["u_tabT"] = np.ascontiguousarray(inp["u_tab"][L].reshape(128, 128, 16, 128).transpose(0, 3, 2, 1)).reshape(128, 128, 2048)
    rw = np.zeros((128, 8, 8), np.float32)
    def ch(v):
        return v.reshape(8, 128).T
    rw[:, :, 0] = ch(inp["w0"][L, 0]); rw[:, :, 1] = ch(inp["w0"][L, 1])
    rw[:, :, 2] = ch(inp["a0"][L, 0]); rw[:, :, 3] = ch(inp["a0"][L, 1])
    rw[:, :, 4] = ch(inp["k_k"][L]); rw[:, :, 5] = ch(inp["k_a"][L]); rw[:, :, 6] = ch(inp["r_k"][L].reshape(-1))
    d["rw_c"] = rw
    return d


def build_program():
    nc = bass.Bass("TRN2", target_bir_lowering=False)
    kb = KB(nc)
    D = declare(kb, None)
    C = consts(kb, D)
    phase_A(kb, C, D)
    phase_Rpre(kb, C, D)
    phase_R(kb, C, D)
    phase_T(kb, C, D)
    phase_O(kb, C, D, D.ymT_d)
    phase_P(kb, C, D)
    phase_F(kb, C, D)
    kb.finish("sp")
    return nc


def kernel(**inputs):
    inp = {k: np.asarray(v) for k, v in inputs.items()}
    shared = prep_shared(inp)
    nc = build_program()
    in_maps = []
    for b in range(8):
        d = dict(shared)
        d["x"] = np.ascontiguousarray(inp["x"][b])
        in_maps.append(d)
    res = run_bass_kernel_spmd(nc, in_maps, core_ids=list(range(8)))
    out = np.stack([np.asarray(r["out"], dtype=np.float32) for r in res.results], axis=0)
    return out
```

```python
import numpy as np
import contextlib
import concourse.bass as bass
import concourse.mybir as mybir
from concourse.bass_utils import run_bass_kernel_spmd

F32 = mybir.dt.float32
BF16 = mybir.dt.bfloat16
U32 = mybir.dt.uint32
ALU = mybir.AluOpType
AF = mybir.ActivationFunctionType
AX = mybir.AxisListType


class T:
    def __init__(self, h, name):
        self.h = h
        self.name = name
        self.w = None
        self.r = {}
        self.dsem = None
        self.dcnt = 0
        self.is_psum = False

    def __getitem__(self, k):
        return self.h[k]

    def ap(self):
        return self.h.ap() if hasattr(self.h, "ap") else self.h[:]


class KB:
    def __init__(self, nc):
        self.nc = nc
        self.es = contextlib.ExitStack()
        self.engs = {"pe": nc.tensor, "act": nc.scalar, "dve": nc.vector, "pool": nc.gpsimd, "sp": nc.sync}
        self.sem = {}
        self.cnt = {}
        for e in self.engs:
            self.sem[e] = self.es.enter_context(nc.semaphore("s_" + e))
            self.cnt[e] = 0
        self.waited = {}
        self.alltensors = []
        self.dsems = []
        self.n_ins = 0

    def sb(self, name, shape, dt=F32, stack=None):
        h = (stack or self.es).enter_context(self.nc.sbuf_tensor(name, list(shape), dt))
        t = T(h, name)
        return t

    def ps(self, name, shape, dt=F32, stack=None):
        h = (stack or self.es).enter_context(self.nc.psum_tensor(name, list(shape), dt))
        t = T(h, name)
        t.is_psum = True
        return t

    def dram(self, name, shape, dt=F32, kind="Internal"):
        h = self.nc.dram_tensor(name, list(shape), dt, kind=kind)
        return T(h, name)

    def view(self, t, name=None):
        return t

    def _wait(self, eng, tok):
        if tok is None:
            return
        sem, val = tok
        key = (eng, id(sem))
        if self.waited.get(key, 0) >= val:
            return
        self.engs[eng].wait_ge(sem, val)
        self.waited[key] = val

    def _deps(self, eng, reads, writes):
        own = id(self.sem[eng])
        for t in reads:
            if t.w is not None:
                if eng == "pe" and id(t.w[0]) == own:
                    continue
                self._wait(eng, t.w)
        for t in writes:
            if t.w is not None and id(t.w[0]) != own:
                self._wait(eng, t.w)
            for k, tok in t.r.items():
                if k == own:
                    continue
                self._wait(eng, tok)

    def _mark(self, tok, reads, writes):
        for t in reads:
            if t in writes:
                continue
            t.r[id(tok[0])] = tok
        for t in writes:
            t.w = tok
            t.r = {}

    def op(self, eng, fn, reads=(), writes=()):
        psr = [t for t in reads if t.is_psum and t not in writes]
        if psr:
            writes = list(writes) + psr
        self._deps(eng, reads, writes)
        ins = fn(self.engs[eng])
        self.cnt[eng] += 1
        ins.then_inc(self.sem[eng], 1)
        tok = (self.sem[eng], self.cnt[eng])
        self._mark(tok, reads, writes)
        self.n_ins += 1
        return ins

    def dma(self, q, out_ap, in_ap, reads=(), writes=(), **kw):
        assert len(writes) == 1
        dst = writes[0]
        if dst.dsem is None:
            dst.dsem = self.es.enter_context(self.nc.semaphore("d_" + dst.name))
            self.dsems.append(dst)
        self._deps(q, reads, [])
        if dst.w is not None and dst.w[0] is not dst.dsem:
            self._wait(q, dst.w)
        for k, tok in dst.r.items():
            self._wait(q, tok)
        ins = self.engs[q].dma_start(out=out_ap, in_=in_ap, **kw)
        dst.dcnt += 16
        ins.then_inc(dst.dsem, 16)
        tok = (dst.dsem, dst.dcnt)
        for t in reads:
            t.r[id(tok[0])] = tok
        dst.w = tok
        dst.r = {}
        self.n_ins += 1
        return ins

    def pe_fence(self):
        if self.cnt["pe"] > 0:
            self._wait("pe", (self.sem["pe"], self.cnt["pe"]))

    def barrier(self):
        toks = [(self.sem[e], self.cnt[e]) for e in self.engs if self.cnt[e] > 0]
        toks += [(t.dsem, t.dcnt) for t in self.dsems if t.dcnt > 0]
        for e in self.engs:
            for tok in toks:
                if tok[0] is self.sem[e]:
                    continue
                self._wait(e, tok)

    def finish(self, eng="sp"):
        for t in self.dsems:
            if t.dcnt > 0:
                self._wait(eng, (t.dsem, t.dcnt))
        for e in self.engs:
            if e != eng and self.cnt[e] > 0:
                self._wait(eng, (self.sem[e], self.cnt[e]))


EPS = 1e-6
NT = 16
RCH = [(i * 128, 128) for i in range(24)] + [(3072 + i * 96, 96) for i in range(4)] + [(3456, 128), (3584, 128)]
NRCH = len(RCH)
RW = 3712


class Obj:
    pass


def declare(kb, debug):
    D = Obj()

    def inp(name, shape, dt=F32):
        setattr(D, name, kb.dram(name, shape, dt, kind="ExternalInput"))

    def scr(name, shape, dt=F32):
        kind = "ExternalOutput" if (debug and name in debug) else "Internal"
        setattr(D, name, kb.dram(name, shape, dt, kind=kind))

    inp("x", [2048, 2048])
    inp("w_in", [128, 16 * 5248])
    inp("mu_c", [128, 3 * NRCH])
    inp("norm1_w", [2048])
    inp("norm2_w", [2048])
    inp("normf_w", [2048])
    inp("q_norm_w", [64])
    inp("k_norm_w", [64])
    inp("tabC", [2048, 32])
    inp("tabS", [2048, 32])
    inp("ident", [128, 128])
    inp("w_out", [2048, 2048])
    inp("w_pq", [2048, 2048])
    inp("skT", [128, 16, 128])
    inp("iota", [128, 128])
    inp("u_tabT", [128, 128, 2048])
    inp("v_tab", [16384, 2048])
    inp("w2", [2, 96, 1024])
    inp("a2", [2, 96, 1024])
    inp("g2", [256, 1024])
    inp("rw_c", [128, 8, 8])
    inp("lnx_w", [1024])
    inp("lnx_b", [1024])
    inp("masks", [128, 6, 128])
    if debug and "in_ymT" in debug:
        inp("ymT_in", [16, 128, 2048], BF16)
    if debug and "in_x1" in debug:
        inp("x1_in", [2048, 2048])
    scr("zs_d", [RW, 2048])
    scr("qkv_d", [2048, 1536])
    scr("ymT_d", [16, 128, 2048], BF16)
    scr("x1_d", [2048, 2048])
    scr("G_d", [128, 128, 2048], BF16)
    scr("peT_d", [2048, 2048])
    scr("A_d", [128, 128, 2048], BF16)
    scr("ld_d", [2, 1024, 2048])
    scr("a_d", [2, 1024, 2048])
    scr("g_d", [2048, 1024])
    inp("rcst", [128, 66 + 2048])
    inp("segm", [128, 2048], BF16)
    if debug and "dbgY" in debug:
        setattr(D, "dbgY", kb.dram("dbgY", [2048, 1024], F32, kind="ExternalOutput"))
        setattr(D, "dbgF", kb.dram("dbgF", [2048, 1024], F32, kind="ExternalOutput"))
    scr("S_d", [2048, 16, 128])
    scr("h2T_d", [16, 128, 2048], BF16)
    scr("vb_d", [16384, 2048], BF16)
    setattr(D, "out", kb.dram("out", [2048, 2048], F32, kind="ExternalOutput"))
    return D


def consts(kb, D):
    C = Obj()
    C.ident = kb.sb("c_ident", [128, 128], F32)
    kb.dma("sp", C.ident[:], D.ident.ap(), writes=[C.ident])
    C.identb = kb.sb("c_identb", [128, 128], BF16)
    kb.op("dve", lambda e: e.tensor_copy(out=C.identb[:], in_=C.ident[:]), reads=[C.ident], writes=[C.identb])
    return C


def norm_to_T(kb, C, src, wvec, hT, tag):
    with contextlib.ExitStack() as st:
        wb = kb.sb(tag + "wb", [128, 2048], F32, st)
        kb.dma("pool", wb[:], wvec.ap().partition_broadcast(128), writes=[wb])
        xts = [kb.sb(f"{tag}x{i}", [128, 2048], F32, st) for i in range(2)]
        junk = kb.sb(tag + "junk", [128, 2048], BF16, st)
        hb = [kb.sb(f"{tag}h{i}", [128, 2048], BF16, st) for i in range(2)]
        ss = kb.sb(tag + "ss", [128, NT], F32, st)
        rs = kb.sb(tag + "rs", [128, NT], F32, st)
        ptr = [kb.ps(f"{tag}ps{i}", [128, 1024], BF16, st) for i in range(2)]
        kb.op("pool", lambda e: e.memset(ss[:], 0.0), writes=[ss])
        for tt in range(NT):
            xt = xts[tt % 2]
            kb.dma("sp", xt[:], src.ap()[tt * 128:(tt + 1) * 128, :], writes=[xt])
            kb.op("act", lambda e: e.activation(out=junk[:], in_=xt[:], func=AF.Square, accum_out=ss[:, tt:tt + 1]),
                  reads=[xt], writes=[junk, ss])
            kb.op("dve", lambda e: e.tensor_scalar(out=rs[:, tt:tt + 1], in0=ss[:, tt:tt + 1], scalar1=1.0 / 2048, scalar2=EPS,
                                                   op0=ALU.mult, op1=ALU.add), reads=[ss], writes=[rs])
            kb.op("act", lambda e: e.activation(out=rs[:, tt:tt + 1], in_=rs[:, tt:tt + 1], func=AF.Sqrt), reads=[rs], writes=[rs])
            kb.op("dve", lambda e: e.reciprocal(out=rs[:, tt:tt + 1], in_=rs[:, tt:tt + 1]), reads=[rs], writes=[rs])
            h = hb[tt % 2]
            kb.op("dve", lambda e: e.scalar_tensor_tensor(out=h[:], in0=xt[:], scalar=rs[:, tt:tt + 1], in1=wb[:],
                                                          op0=ALU.mult, op1=ALU.mult), reads=[xt, rs, wb], writes=[h])
            for g in range(2):
                p = ptr[g]
                for j in range(8):
                    dc = g * 8 + j
                    kb.op("pe", lambda e: e.transpose(out=p[:, j * 128:(j + 1) * 128], in_=h[:, dc * 128:(dc + 1) * 128],
                                                      identity=C.identb[:]), reads=[h, C.identb], writes=[p])
                eng = "act" if g == 0 else "dve"
                src_ap = p[:].rearrange("p (j t) -> p j t", j=8)
                dst_ap = hT[:, g * 8:(g + 1) * 8, tt * 128:(tt + 1) * 128]
                if eng == "act":
                    kb.op("act", lambda e: e.copy(out=dst_ap, in_=src_ap), reads=[p], writes=[hT])
                else:
                    kb.op("dve", lambda e: e.tensor_copy(out=dst_ap, in_=src_ap), reads=[p], writes=[hT])
        kb.barrier()


def phase_A(kb, C, D):
    with contextlib.ExitStack() as st:
        hT = kb.sb("hT", [128, 16, 2048], BF16, st)
        norm_to_T(kb, C, D.x, D.norm1_w, hT, "n1")
        mu = kb.sb("mu", [128, 3 * NRCH], F32, st)
        kb.dma("sp", mu[:], D.mu_c.ap(), writes=[mu])
        kb.op("dve", lambda e: e.tensor_tensor(out=mu[:, 60:90], in0=mu[:, 0:30], in1=mu[:, 30:60], op=ALU.add), reads=[mu], writes=[mu])
        kb.op("dve", lambda e: e.tensor_scalar(out=mu[:, 60:90], in0=mu[:, 60:90], scalar1=-1.0, scalar2=1.0, op0=ALU.mult, op1=ALU.add),
              reads=[mu], writes=[mu])
        stg = [kb.sb(f"a_stg{i}", [128, 4096], F32, st) for i in range(2)]
        wbf = [kb.sb(f"a_wbf{i}", [128, 16, 128], BF16, st) for i in range(2)]
        accs = [kb.sb(f"a_acc{i}", [128, 2048], F32, st) for i in range(2)]
        pss = [[kb.ps(f"a_ps{i}_{j}", [128, 512], F32, st) for j in range(4)] for i in range(2)]
        def loadw(ci):
            c0, cs = RCH[ci]
            kb.dma("sp", stg[ci % 2][:, 0:16 * cs], D.w_in.ap()[:, 16 * c0:16 * (c0 + cs)], writes=[stg[ci % 2]])
        def convw(ci):
            c0, cs = RCH[ci]
            sg = stg[ci % 2]
            sgv = sg[:, 0:16 * cs].rearrange("p (dc c) -> p dc c", dc=16)
            kb.op("dve", lambda e: e.tensor_copy(out=wbf[ci % 2][:, :, 0:cs], in_=sgv), reads=[sg], writes=[wbf[ci % 2]])
        loadw(0)
        convw(0)
        loadw(1)
        for ci, (c0, cs) in enumerate(RCH):
            wb = wbf[ci % 2]
            ps = pss[ci % 2]
            acc = accs[ci % 2]
            for tb in range(4):
                for dc in range(16):
                    kb.op("pe", lambda e: e.matmul(ps[tb][0:cs, :], lhsT=wb[:, dc, 0:cs], rhs=hT[:, dc, tb * 512:(tb + 1) * 512],
                                                   start=(dc == 0), stop=(dc == 15)), reads=[wb, hT], writes=[ps[tb]])
            if ci + 1 < NRCH:
                convw(ci + 1)
            if ci + 2 < NRCH:
                loadw(ci + 2)
            for tb in range(4):
                kb.op("act", lambda e: e.activation(out=acc[0:cs, tb * 512:(tb + 1) * 512], in_=ps[tb][0:cs, :], func=AF.Copy,
                                                    scale=mu[0:cs, 60 + ci:61 + ci]), reads=[ps[tb], mu], writes=[acc])
            for tb in range(4):
                n = 512 if tb < 3 else 511
                d0 = tb * 512 + 1
                kb.op("dve", lambda e: e.scalar_tensor_tensor(out=acc[0:cs, d0:d0 + n], in0=ps[tb][0:cs, 0:n], scalar=mu[0:cs, ci:ci + 1],
                                                              in1=acc[0:cs, d0:d0 + n], op0=ALU.mult, op1=ALU.add),
                      reads=[ps[tb], mu, acc], writes=[acc])
                s0 = 1 if tb == 0 else 0
                n = 512 - s0
                d0 = tb * 512 + s0 - 1
                kb.op("dve", lambda e: e.scalar_tensor_tensor(out=acc[0:cs, d0:d0 + n], in0=ps[tb][0:cs, s0:512], scalar=mu[0:cs, 30 + ci:31 + ci],
                                                              in1=acc[0:cs, d0:d0 + n], op0=ALU.mult, op1=ALU.add),
                      reads=[ps[tb], mu, acc], writes=[acc])
            kb.dma("pool", D.zs_d.ap()[c0:c0 + cs, :], acc[0:cs, :], reads=[acc], writes=[D.zs_d])
        kb.barrier()
        with contextlib.ExitStack() as st2:
            wq = kb.sb("a_wq", [128, 16, 512], BF16, st2)
            ev = [kb.sb(f"a_ev{i}", [128, 512], F32, st2) for i in range(2)]
            for cg in range(3):
                c0 = RW + cg * 512
                for hf in range(2):
                    sg = stg[hf]
                    sgv = sg[:].rearrange("p (dc c) -> p dc c", dc=8)
                    o0 = 16 * RW + (cg * 2 + hf) * 4096
                    kb.dma("sp", sg[:], D.w_in.ap()[:, o0:o0 + 4096], writes=[sg])
                    kb.op("dve", lambda e: e.tensor_copy(out=wq[:, hf * 8:(hf + 1) * 8, :], in_=sgv), reads=[sg], writes=[wq])
                for tt in range(NT):
                    ps = pss[tt % 2][0]
                    for dc in range(16):
                        kb.op("pe", lambda e: e.matmul(ps[:], lhsT=hT[:, dc, tt * 128:(tt + 1) * 128], rhs=wq[:, dc, :],
                                                       start=(dc == 0), stop=(dc == 15)), reads=[wq, hT], writes=[ps])
                    o = ev[tt % 2]
                    kb.op("act", lambda e: e.copy(out=o[:], in_=ps[:]), reads=[ps], writes=[o])
                    kb.dma("sp", D.qkv_d.ap()[tt * 128:(tt + 1) * 128, cg * 512:(cg + 1) * 512], o[:], reads=[o], writes=[D.qkv_d])
            kb.barrier()


def phase_T(kb, C, D):
    with contextlib.ExitStack() as st:
        qT = kb.sb("t_qT", [128, 8, 2048], BF16, st)
        kT2 = kb.sb("t_kT2", [128, 4, 2048], BF16, st)
        vaug = kb.sb("t_vaug", [128, NT, 4, 65], BF16, st)
        wqk = kb.sb("t_wqk", [128, 20, 64], F32, st)
        w64 = kb.sb("t_w64", [128, 2, 64], F32, st)
        kb.dma("pool", w64[:, 0, :], D.q_norm_w.ap().partition_broadcast(128), writes=[w64])
        kb.dma("pool", w64[:, 1, :], D.k_norm_w.ap().partition_broadcast(128), writes=[w64])
        kb.op("dve", lambda e: e.tensor_scalar(out=wqk[:, 0:16, :], in0=w64[:, 0:1, :].to_broadcast([128, 16, 64]), scalar1=0.125, scalar2=None,
                                               op0=ALU.mult), reads=[w64], writes=[wqk])
        kb.op("dve", lambda e: e.tensor_copy(out=wqk[:, 16:20, :], in_=w64[:, 1:2, :].to_broadcast([128, 4, 64])), reads=[w64], writes=[wqk])
        kb.op("pool", lambda e: e.memset(vaug[:], 1.0), writes=[vaug])
        with contextlib.ExitStack() as st2:
            qk = [kb.sb(f"t_qk{i}", [128, 1536], F32, st2) for i in range(2)]
            tC = [kb.sb(f"t_tC{i}", [128, 2, 16], F32, st2) for i in range(2)]
            tS = [kb.sb(f"t_tS{i}", [128, 2, 16], F32, st2) for i in range(2)]
            sq = kb.sb("t_sq", [128, 20, 64], F32, st2)
            ss = kb.sb("t_ss", [128, 20], F32, st2)
            qn = kb.sb("t_qn", [128, 20, 64], F32, st2)
            t1 = kb.sb("t_t1", [128, 20, 2, 16], F32, st2)
            t2 = kb.sb("t_t2", [128, 20, 2, 16], F32, st2)
            t3 = kb.sb("t_t3", [128, 20, 2, 16], F32, st2)
            t4 = kb.sb("t_t4", [128, 20, 2, 16], F32, st2)
            qkr = kb.sb("t_qkr", [128, 20, 64], BF16, st2)
            kd = kb.sb("t_kd", [128, 4, 2, 64], BF16, st2)
            pq = kb.ps("t_pq", [128, 1024], BF16, st2)
            pk = kb.ps("t_pk", [128, 1024], BF16, st2)
            for tt in range(NT):
                q = qk[tt % 2]
                cC = tC[tt % 2]
                cS = tS[tt % 2]
                kb.dma("sp", q[:], D.qkv_d.ap()[tt * 128:(tt + 1) * 128, :], reads=[D.qkv_d], writes=[q])
                kb.dma("act", cC[:].rearrange("p a b -> p (a b)"), D.tabC.ap()[tt * 128:(tt + 1) * 128, :], writes=[cC])
                kb.dma("act", cS[:].rearrange("p a b -> p (a b)"), D.tabS.ap()[tt * 128:(tt + 1) * 128, :], writes=[cS])
                qv = q[:, 0:1280].rearrange("p (h d) -> p h d", h=20)
                kb.op("act", lambda e: e.activation(out=sq[:], in_=qv, func=AF.Square), reads=[q], writes=[sq])
                kb.op("dve", lambda e: e.tensor_reduce(out=ss[:], in_=sq[:], axis=AX.X, op=ALU.add), reads=[sq], writes=[ss])
                kb.op("dve", lambda e: e.tensor_scalar(out=ss[:], in0=ss[:], scalar1=1.0 / 64, scalar2=EPS, op0=ALU.mult, op1=ALU.add),
                      reads=[ss], writes=[ss])
                kb.op("act", lambda e: e.activation(out=ss[:], in_=ss[:], func=AF.Sqrt), reads=[ss], writes=[ss])
                kb.op("dve", lambda e: e.reciprocal(out=ss[:], in_=ss[:]), reads=[ss], writes=[ss])
                kb.op("dve", lambda e: e.tensor_tensor(out=qn[:], in0=qv, in1=ss[:, :, None].to_broadcast([128, 20, 64]), op=ALU.mult),
                      reads=[q, ss], writes=[qn])
                kb.op("pool", lambda e: e.tensor_tensor(out=qn[:], in0=qn[:], in1=wqk[:], op=ALU.mult), reads=[qn, wqk], writes=[qn])
                qn5 = qn[:].rearrange("p h (a b c) -> p h a b c", a=2, b=2)
                x1 = qn5[:, :, :, 0, :]
                x2 = qn5[:, :, :, 1, :]
                Cb = cC[:, None, :, :].to_broadcast([128, 20, 2, 16])
                Sb = cS[:, None, :, :].to_broadcast([128, 20, 2, 16])
                kb.op("dve", lambda e: e.tensor_tensor(out=t1[:], in0=x1, in1=Cb, op=ALU.mult), reads=[qn, cC], writes=[t1])
                kb.op("pool", lambda e: e.tensor_tensor(out=t2[:], in0=x2, in1=Sb, op=ALU.mult), reads=[qn, cS], writes=[t2])
                kb.op("pool", lambda e: e.tensor_tensor(out=t3[:], in0=x2, in1=Cb, op=ALU.mult), reads=[qn, cC], writes=[t3])
                kb.op("dve", lambda e: e.tensor_tensor(out=t4[:], in0=x1, in1=Sb, op=ALU.mult), reads=[qn, cS], writes=[t4])
                r5 = qkr[:].rearrange("p h (a b c) -> p h a b c", a=2, b=2)
                kb.op("dve", lambda e: e.tensor_tensor(out=r5[:, :, :, 0, :], in0=t1[:], in1=t2[:], op=ALU.subtract), reads=[t1, t2], writes=[qkr])
                kb.op("pool", lambda e: e.tensor_tensor(out=r5[:, :, :, 1, :], in0=t3[:], in1=t4[:], op=ALU.add), reads=[t3, t4], writes=[qkr])
                kb.op("pool", lambda e: e.tensor_copy(out=kd[:], in_=qkr[:, 16:20, None, :].to_broadcast([128, 4, 2, 64])), reads=[qkr], writes=[kd])
                for j in range(8):
                    kb.op("pe", lambda e: e.transpose(out=pq[:, j * 128:(j + 1) * 128], in_=qkr[:, 2 * j:2 * j + 2, :].rearrange("p a b -> p (a b)"),
                                                      identity=C.identb[:]), reads=[qkr, C.identb], writes=[pq])
                kb.op("act", lambda e: e.copy(out=qT[:, :, tt * 128:(tt + 1) * 128], in_=pq[:].rearrange("p (j t) -> p j t", j=8)),
                      reads=[pq], writes=[qT])
                for j in range(4):
                    kb.op("pe", lambda e: e.transpose(out=pk[:, j * 128:(j + 1) * 128], in_=kd[:, j, :, :].rearrange("p a b -> p (a b)"),
                                                      identity=C.identb[:]), reads=[kd, C.identb], writes=[pk])
                kb.op("dve", lambda e: e.tensor_copy(out=kT2[:, :, tt * 128:(tt + 1) * 128], in_=pk[:, 0:512].rearrange("p (j t) -> p j t", j=4)),
                      reads=[pk], writes=[kT2])
                kb.op("pool", lambda e: e.tensor_copy(out=vaug[:, tt, :, 0:64], in_=q[:, 1280:1536].rearrange("p (h d) -> p h d", h=4)),
                      reads=[q], writes=[vaug])
            kb.barrier()
        with contextlib.ExitStack() as st3:
            yatt = kb.sb("t_yatt", [128, NT, 1024], BF16, st3)
            pexp = [kb.sb(f"t_pexp{i}", [128, 512], BF16, st3) for i in range(3)]
            rinv = kb.sb("t_rinv", [128, 4], F32, st3)
            pss = [kb.ps(f"t_pss{i}", [128, 512], F32, st3) for i in range(2)]
            po = [kb.ps(f"t_po{i}", [128, 512], F32, st3) for i in range(4)]
            seq = [(h, qb, kt) for h in range(16) for qb in range(4) for kt in range(NT)]
            cv_st = [kb.sb(f"t_cvst{i}", [128, 2048], F32, st3) for i in range(2)]
            cv_bf = [kb.sb(f"t_cvbf{i}", [128, 2048], BF16, st3) for i in range(2)]

            def vconv(c):
                a, b = cv_st[c % 2], cv_bf[c % 2]
                kb.dma("sp", a[:], D.v_tab.ap()[c * 128:(c + 1) * 128, :], writes=[a])
                kb.op("dve", lambda e: e.tensor_copy(out=b[:], in_=a[:]), reads=[a], writes=[b])
                kb.dma("pool", D.vb_d.ap()[c * 128:(c + 1) * 128, :], b[:], reads=[b], writes=[D.vb_d])

            def emitS(i):
                h, qb, kt = seq[i]
                kv, c, b0 = h // 4, h // 2, (h % 2) * 64
                ps = pss[i % 2]
                kb.op("pe", lambda e: e.matmul(ps[:], lhsT=kT2[b0:b0 + 64, kv, kt * 128:(kt + 1) * 128],
                                               rhs=qT[b0:b0 + 64, c, qb * 512:(qb + 1) * 512], start=True, stop=True),
                      reads=[kT2, qT], writes=[ps])

            def emitE(i):
                ps = pss[i % 2]
                pe_ = pexp[i % 3]
                kb.op("act", lambda e: e.activation(out=pe_[:], in_=ps[:], func=AF.Exp), reads=[ps], writes=[pe_])

            def emitPV(i):
                h, qb, kt = seq[i]
                kv = h // 4
                pe_ = pexp[i % 3]
                for j in range(4):
                    kb.op("pe", lambda e: e.matmul(po[j][:, 0:65], lhsT=pe_[:, j * 128:(j + 1) * 128], rhs=vaug[:, kt, kv, :],
                                                   start=(kt == 0), stop=(kt == NT - 1)), reads=[pe_, vaug], writes=[po[j]])
                if kt == NT - 1:
                    for j in range(4):
                        kb.op("dve", lambda e: e.reciprocal(out=rinv[:, j:j + 1], in_=po[j][:, 64:65]), reads=[po[j]], writes=[rinv])
                        kb.op("dve", lambda e: e.tensor_scalar(out=yatt[:, qb * 4 + j, h * 64:(h + 1) * 64], in0=po[j][:, 0:64],
                                                               scalar1=rinv[:, j:j + 1], scalar2=None, op0=ALU.mult),
                              reads=[po[j], rinv], writes=[yatt])

            emitS(0)
            for i in range(len(seq)):
                if i + 1 < len(seq):
                    emitS(i + 1)
                emitE(i)
                emitPV(i)
                if i % 8 == 0:
                    vconv(i // 8)
            pt = [kb.ps(f"t_pt{i}", [128, 1024], BF16, st3) for i in range(2)]
            yT = [kb.sb(f"t_yT{i}", [128, 8, 128], BF16, st3) for i in range(2)]
            for tt in range(NT):
                p = pt[tt % 2]
                o = yT[tt % 2]
                for j in range(8):
                    kb.op("pe", lambda e: e.transpose(out=p[:, j * 128:(j + 1) * 128], in_=yatt[:, tt, j * 128:(j + 1) * 128],
                                                      identity=C.identb[:]), reads=[yatt, C.identb], writes=[p])
                kb.op("act", lambda e: e.copy(out=o[:], in_=p[:].rearrange("p (j t) -> p j t", j=8)), reads=[p], writes=[o])
                kb.dma("sp", D.ymT_d.ap()[8:16, :, tt * 128:(tt + 1) * 128].rearrange("j p t -> p j t"), o[:], reads=[o], writes=[D.ymT_d])
            kb.barrier()


def phase_O(kb, C, D, ym_src):
    with contextlib.ExitStack() as st:
        ymT = kb.sb("o_ymT", [128, 16, 2048], BF16, st)
        for j in range(16):
            kb.dma("sp" if j % 2 == 0 else "act", ymT[:, j, :], ym_src.ap()[j], reads=[ym_src], writes=[ymT])
        stg = [kb.sb(f"o_stg{i}", [128, 4096], F32, st) for i in range(2)]
        wo = kb.sb("o_wo", [128, 16, 512], BF16, st)
        xs = [kb.sb(f"o_xs{i}", [128, 512], F32, st) for i in range(2)]
        pss = [kb.ps(f"o_ps{i}", [128, 512], F32, st) for i in range(2)]
        w_v = D.w_out.ap().rearrange("(kc p) c -> p kc c", p=128)
        for dg in range(4):
            for hf in range(2):
                sg = stg[hf]
                sgv = sg[:].rearrange("p (dc c) -> p dc c", dc=8)
                kb.dma("sp" if hf == 0 else "act", sgv, w_v[:, hf * 8:(hf + 1) * 8, dg * 512:(dg + 1) * 512], writes=[sg])
                kb.op("dve", lambda e: e.tensor_copy(out=wo[:, hf * 8:(hf + 1) * 8, :], in_=sgv), reads=[sg], writes=[wo])
            for tt in range(NT):
                ps = pss[tt % 2]
                xt = xs[tt % 2]
                kb.dma("sp", xt[:], D.x.ap()[tt * 128:(tt + 1) * 128, dg * 512:(dg + 1) * 512], writes=[xt])
                for kc in range(16):
                    kb.op("pe", lambda e: e.matmul(ps[:], lhsT=ymT[:, kc, tt * 128:(tt + 1) * 128], rhs=wo[:, kc, :],
                                                   start=(kc == 0), stop=(kc == 15)), reads=[ymT, wo], writes=[ps])
                kb.op("dve", lambda e: e.tensor_tensor(out=xt[:], in0=ps[:], in1=xt[:], op=ALU.add), reads=[ps, xt], writes=[xt])
                kb.dma("act", D.x1_d.ap()[tt * 128:(tt + 1) * 128, dg * 512:(dg + 1) * 512], xt[:], reads=[xt], writes=[D.x1_d])
        kb.barrier()


def phase_P(kb, C, D):
    with contextlib.ExitStack() as stP:
        iota = kb.sb("p_iota", [128, 128], F32, stP)
        kb.dma("sp", iota[:], D.iota.ap(), writes=[iota])
        with contextlib.ExitStack() as st, kb.nc.named_scope("P1"):
            h2T = kb.sb("p_h2T", [128, 16, 2048], BF16, st)
            norm_to_T(kb, C, D.x1_d, D.norm2_w, h2T, "n2")
            for dc in range(16):
                kb.dma("sp" if dc % 2 == 0 else "act", D.h2T_d.ap()[dc], h2T[:, dc, :], reads=[h2T], writes=[D.h2T_d])
            skb = kb.sb("p_sk", [128, 16, 128], F32, st)
            kb.dma("sp", skb[:], D.skT.ap(), writes=[skb])
            stg = [kb.sb(f"p_stg{i}", [128, 16, 128], F32, st) for i in range(2)]
            wb = [kb.sb(f"p_wb{i}", [128, 16, 128], BF16, st) for i in range(2)]
            qTs = [kb.sb(f"p_qT{i}", [128, 2048], F32, st) for i in range(2)]
            sev = [kb.sb(f"p_sev{i}", [128, 16, 128], F32, st) for i in range(2)]
            pss = [[kb.ps(f"p_ps{i}_{j}", [128, 512], F32, st) for j in range(3)] for i in range(2)]
            w_v = D.w_pq.ap().rearrange("(dc p) c -> p dc c", p=128)
            for hp in range(16):
                sg = stg[hp % 2]
                kb.dma("sp" if hp % 2 == 0 else "act", sg[:], w_v[:, :, hp * 128:(hp + 1) * 128], writes=[sg])
                w = wb[hp % 2]
                kb.op("dve", lambda e: e.tensor_copy(out=w[:], in_=sg[:]), reads=[sg], writes=[w])
                qT = qTs[hp % 2]
                for tb in range(4):
                    ps = pss[tb % 2][0]
                    for dc in range(16):
                        kb.op("pe", lambda e: e.matmul(ps[:], lhsT=w[:, dc, :], rhs=h2T[:, dc, tb * 512:(tb + 1) * 512],
                                                       start=(dc == 0), stop=(dc == 15)), reads=[w, h2T], writes=[ps])
                    kb.op("act", lambda e: e.copy(out=qT[:, tb * 512:(tb + 1) * 512], in_=ps[:]), reads=[ps], writes=[qT])
                se = sev[hp % 2]
                for g in range(4):
                    ps = pss[g % 2][1 + (g // 2) % 2]
                    for j in range(4):
                        tt = g * 4 + j
                        kb.op("pe", lambda e: e.matmul(ps[:, j * 128:(j + 1) * 128], lhsT=qT[:, tt * 128:(tt + 1) * 128], rhs=skb[:, hp, :],
                                                       start=True, stop=True), reads=[qT, skb], writes=[ps])
                    kb.op("dve", lambda e: e.tensor_copy(out=se[:, g * 4:(g + 1) * 4, :], in_=ps[:].rearrange("p (j n) -> p j n", j=4)),
                          reads=[ps], writes=[se])
                kb.dma("pool", D.S_d.ap()[:, hp, :].rearrange("(tt p) n -> p tt n", p=128), se[:], reads=[se], writes=[D.S_d])
            kb.barrier()
        with contextlib.ExitStack() as st, kb.nc.named_scope("P23"):
            ETg = [kb.sb(f"p_ETg{g}", [128, 3, 256], F32, st) for g in range(8)]
            Ss = [kb.sb(f"p_S{i}", [128, 16, 128], F32, st) for i in range(1)]
            S2 = kb.sb("p_S2", [128, 16, 128], F32, st)
            M16 = kb.sb("p_M16", [128, 16, 16], F32, st)
            I16u = kb.sb("p_I16u", [128, 16, 16], U32, st)
            I16f = kb.sb("p_I16f", [128, 16, 16], F32, st)
            cand = kb.sb("p_cand", [128, 8, 16, 16], F32, st)
            cand2 = kb.sb("p_cand2", [128, 8, 256], F32, st)
            C16 = kb.sb("p_C16", [128, 8, 16], F32, st)
            CIu = kb.sb("p_CIu", [128, 8, 16], U32, st)
            IJu = kb.sb("p_IJu", [128, 2, 8, 16], U32, st)
            IJf = kb.sb("p_IJf", [128, 2, 8, 16], F32, st)
            ex = kb.sb("p_ex", [128, 8, 16], F32, st)
            Z = kb.sb("p_Z", [128, 8], F32, st)
            EG = kb.sb("p_EG", [128, 3, 8, 16], F32, st)
            eq = kb.sb("p_eq", [128, 8, 16, 16], F32, st)
            pt = kb.ps("p_pt", [128, 512], F32, st)
            Gs = kb.sb("p_Gs", [128, 128, 256], BF16, st)
            O1 = [kb.sb(f"p_O1{i}", [128, 32, 128], BF16, st) for i in range(2)]
            O2 = [kb.sb(f"p_O2{i}", [128, 32, 128], BF16, st) for i in range(2)]
            pg = [kb.ps(f"p_pg{i}", [128, 512], F32, st) for i in range(4)]
            def p2_tile(tt):
                S = Ss[0]
                kb.dma("sp", S[:].rearrange("p a n -> p (a n)"), D.S_d.ap()[tt * 128:(tt + 1) * 128].rearrange("p a n -> p (a n)"),
                       reads=[D.S_d], writes=[S])
                for hp in range(16):
                    kb.op("dve", lambda e: e.max(out=M16[:, hp, 0:8], in_=S[:, hp, :]), reads=[S], writes=[M16])
                for hp in range(16):
                    kb.op("dve", lambda e: e.max_index(out=I16u[:, hp, 0:8], in_max=M16[:, hp, 0:8], in_values=S[:, hp, :]),
                          reads=[S, M16], writes=[I16u])
                for hp in range(16):
                    kb.op("dve", lambda e: e.match_replace(out=S2[:, hp, :], in_to_replace=M16[:, hp, 0:8], in_values=S[:, hp, :], imm_value=-1e30),
                          reads=[S, M16], writes=[S2])
                for hp in range(16):
                    kb.op("dve", lambda e: e.max(out=M16[:, hp, 8:16], in_=S2[:, hp, :]), reads=[S2], writes=[M16])
                for hp in range(16):
                    kb.op("dve", lambda e: e.max_index(out=I16u[:, hp, 8:16], in_max=M16[:, hp, 8:16], in_values=S2[:, hp, :]),
                          reads=[S2, M16], writes=[I16u])
                kb.op("pool", lambda e: e.tensor_copy(out=I16f[:], in_=I16u[:]), reads=[I16u], writes=[I16f])
                M4 = M16[:].rearrange("p (h q) k -> p h q k", q=2)
                I4 = I16f[:].rearrange("p (h q) k -> p h q k", q=2)
                kb.op("pool", lambda e: e.tensor_tensor(out=cand[:], in0=M4[:, :, 0, :, None].to_broadcast([128, 8, 16, 16]),
                                                        in1=M4[:, :, 1, None, :].to_broadcast([128, 8, 16, 16]), op=ALU.add),
                      reads=[M16], writes=[cand])
                chv = lambda h: cand[:, h, :, :].rearrange("p a b -> p (a b)")
                for h in range(8):
                    kb.op("dve", lambda e: e.max(out=C16[:, h, 0:8], in_=chv(h)), reads=[cand], writes=[C16])
                for h in range(8):
                    kb.op("dve", lambda e: e.max_index(out=CIu[:, h, 0:8], in_max=C16[:, h, 0:8], in_values=chv(h)), reads=[cand, C16], writes=[CIu])
                for h in range(8):
                    kb.op("dve", lambda e: e.match_replace(out=cand2[:, h, :], in_to_replace=C16[:, h, 0:8], in_values=chv(h), imm_value=-1e30),
                          reads=[cand, C16], writes=[cand2])
                for h in range(8):
                    kb.op("dve", lambda e: e.max(out=C16[:, h, 8:16], in_=cand2[:, h, :]), reads=[cand2], writes=[C16])
                for h in range(8):
                    kb.op("dve", lambda e: e.max_index(out=CIu[:, h, 8:16], in_max=C16[:, h, 8:16], in_values=cand2[:, h, :]),
                          reads=[cand2, C16], writes=[CIu])
                kb.op("pool", lambda e: e.tensor_tensor(out=ex[:], in0=C16[:], in1=C16[:, :, 0:1].to_broadcast([128, 8, 16]), op=ALU.subtract),
                      reads=[C16], writes=[ex])
                kb.op("act", lambda e: e.activation(out=ex[:], in_=ex[:], func=AF.Exp), reads=[ex], writes=[ex])
                kb.op("dve", lambda e: e.tensor_reduce(out=Z[:], in_=ex[:], axis=AX.X, op=ALU.add), reads=[ex], writes=[Z])
                kb.op("dve", lambda e: e.reciprocal(out=Z[:], in_=Z[:]), reads=[Z], writes=[Z])
                kb.op("dve", lambda e: e.tensor_tensor(out=EG[:, 2], in0=ex[:], in1=Z[:, :, None].to_broadcast([128, 8, 16]), op=ALU.mult),
                      reads=[ex, Z], writes=[EG])
                kb.op("dve", lambda e: e.tensor_single_scalar(out=IJu[:, 0], in_=CIu[:], scalar=4, op=ALU.logical_shift_right),
                      reads=[CIu], writes=[IJu])
                kb.op("dve", lambda e: e.tensor_single_scalar(out=IJu[:, 1], in_=CIu[:], scalar=15, op=ALU.bitwise_and),
                      reads=[CIu], writes=[IJu])
                kb.op("pool", lambda e: e.tensor_copy(out=IJf[:], in_=IJu[:]), reads=[IJu], writes=[IJf])
                for q in range(2):
                    kb.op("dve", lambda e: e.tensor_tensor(out=eq[:], in0=iota[:, None, None, 0:16].to_broadcast([128, 8, 16, 16]),
                                                            in1=IJf[:, q, :, :, None].to_broadcast([128, 8, 16, 16]), op=ALU.is_equal),
                          reads=[iota, IJf], writes=[eq])
                    kb.op("pool", lambda e: e.tensor_tensor(out=eq[:], in0=eq[:], in1=I4[:, :, q, None, :].to_broadcast([128, 8, 16, 16]),
                                                            op=ALU.mult), reads=[eq, I16f], writes=[eq])
                    kb.op("dve", lambda e: e.tensor_reduce(out=EG[:, q], in_=eq[:], axis=AX.X, op=ALU.add), reads=[eq], writes=[EG])
                for a in range(3):
                    kb.op("pe", lambda e: e.transpose(out=pt[:, a * 128:(a + 1) * 128], in_=EG[:, a].rearrange("p h k -> p (h k)"),
                                                      identity=C.ident[:]), reads=[EG, C.ident], writes=[pt])
                kb.op("act", lambda e: e.copy(out=ETg[tt // 2][:, :, (tt % 2) * 128:(tt % 2 + 1) * 128], in_=pt[:, 0:384].rearrange("p (a t) -> p a t", a=3)),
                      reads=[pt], writes=[ETg[tt // 2]])

            itc = [0]

            def p3_group(tg):
                for sub in range(8):
                    t0 = sub * 32
                    ET = ETg[tg]
                    o1 = O1[sub % 2]
                    o2 = O2[sub % 2]
                    iob = iota[:, None, :].to_broadcast([128, 32, 128])
                    kb.op("dve", lambda e: e.tensor_tensor(out=o1[:], in0=iob, in1=ET[:, 0, t0:t0 + 32, None].to_broadcast([128, 32, 128]),
                                                            op=ALU.is_equal), reads=[iota, ETg[tg]], writes=[o1])
                    kb.op("dve", lambda e: e.tensor_tensor(out=o2[:], in0=iob, in1=ET[:, 1, t0:t0 + 32, None].to_broadcast([128, 32, 128]),
                                                           op=ALU.is_equal), reads=[iota, ETg[tg]], writes=[o2])
                    kb.op("pool", lambda e: e.tensor_tensor(out=o2[:], in0=o2[:], in1=ET[:, 2, t0:t0 + 32, None].to_broadcast([128, 32, 128]),
                                                            op=ALU.mult), reads=[o2, ETg[tg]], writes=[o2])
                    for q4 in range(8):
                        p = pg[itc[0] % 4]
                        itc[0] += 1
                        for j in range(4):
                            tl = q4 * 4 + j
                            kb.op("pe", lambda e: e.matmul(p[:].rearrange("p (e t) -> p e t", t=4)[:, :, j], lhsT=o2[:, tl, :], rhs=o1[:, tl, :],
                                                           start=True, stop=True), reads=[o1, o2], writes=[p])
                        tl0 = sub * 32 + q4 * 4
                        dst = Gs[:, :, tl0:tl0 + 4]
                        src = p[:].rearrange("p (e t) -> p e t", t=4)
                        kb.op("act", lambda e: e.copy(out=dst, in_=src), reads=[p], writes=[Gs])
                for k8 in range(8):
                    kb.dma(["sp", "act", "pool"][k8 % 3],
                           D.G_d.ap()[k8 * 16:(k8 + 1) * 16, :, tg * 256:(tg + 1) * 256].rearrange("e1 e2 t -> e2 e1 t"),
                           Gs[:, k8 * 16:(k8 + 1) * 16, :], reads=[Gs], writes=[D.G_d])

            for g in range(8):
                p2_tile(2 * g)
                p2_tile(2 * g + 1)
                if g >= 1:
                    p3_group(g - 1)
            p3_group(7)
            kb.barrier()
        with contextlib.ExitStack() as st, kb.nc.named_scope("P4"):
            h2T = kb.sb("p_h2Tb", [128, 16, 2048], BF16, st)
            for dc in range(16):
                kb.dma("sp" if dc % 2 == 0 else "act", h2T[:, dc, :], D.h2T_d.ap()[dc], reads=[D.h2T_d], writes=[h2T])
            stg = [kb.sb(f"p4_stg{i}", [128, 16, 128], F32, st) for i in range(2)]
            ub = [kb.sb(f"p4_ub{i}", [128, 16, 128], BF16, st) for i in range(2)]
            Gc = [kb.sb(f"p4_Gc{i}", [128, 2048], BF16, st) for i in range(2)]
            ge = [kb.sb(f"p4_ge{i}", [128, 2048], BF16, st) for i in range(2)]
            pss = [[kb.ps(f"p4_ps{i}_{j}", [128, 512], F32, st) for j in range(4)] for i in range(2)]
            def load(e1):
                sg = stg[e1 % 2]
                kb.dma("sp", sg[:].rearrange("p a b -> p (a b)"), D.u_tabT.ap()[e1], writes=[sg])
                g = Gc[e1 % 2]
                kb.dma("pool", g[:], D.G_d.ap()[e1], reads=[D.G_d], writes=[g])
            load(0)
            for e1 in range(128):
                if e1 + 1 < 128:
                    load(e1 + 1)
                sg = stg[e1 % 2]
                u = ub[e1 % 2]
                kb.op("dve", lambda e: e.tensor_copy(out=u[:], in_=sg[:]), reads=[sg], writes=[u])
                g = Gc[e1 % 2]
                ps = pss[e1 % 2]
                a = ge[e1 % 2]
                for tb in range(4):
                    for dc in range(16):
                        kb.op("pe", lambda e: e.matmul(ps[tb][:], lhsT=u[:, dc, :], rhs=h2T[:, dc, tb * 512:(tb + 1) * 512],
                                                       start=(dc == 0), stop=(dc == 15)), reads=[u, h2T], writes=[ps[tb]])
                    kb.op("act", lambda e: e.activation(out=a[:, tb * 512:(tb + 1) * 512], in_=ps[tb][:], func=AF.Gelu), reads=[ps[tb]], writes=[a])
                kb.op("pool", lambda e: e.tensor_tensor(out=a[:], in0=a[:], in1=g[:], op=ALU.mult), reads=[a, g], writes=[a])
                kb.dma("sp", D.A_d.ap()[e1], a[:], reads=[a], writes=[D.A_d])
            kb.barrier()
        with contextlib.ExitStack() as st, kb.nc.named_scope("P5"):
            NB = 4
            vb = [kb.sb(f"p5_vb{i}", [128, 512], BF16, st) for i in range(NB)]
            ac = [kb.sb(f"p5_ac{i}", [128, 1024], BF16, st) for i in range(NB)]
            ov = [kb.sb(f"p5_ov{i}", [128, 512], F32, st) for i in range(2)]
            pss = [[kb.ps(f"p5_ps{j}_{h}", [128, 512], F32, st) for h in range(2)] for j in range(4)]
            seq = [(tb2, dg, e1) for tb2 in range(2) for dg in range(4) for e1 in range(128)]

            def load(i):
                tb2, dg, e1 = seq[i]
                k = i % NB
                kb.dma("sp", vb[k][:], D.vb_d.ap()[e1 * 128:(e1 + 1) * 128, dg * 512:(dg + 1) * 512], reads=[D.vb_d], writes=[vb[k]])
                kb.dma("pool", ac[k][:], D.A_d.ap()[e1][:, tb2 * 1024:(tb2 + 1) * 1024], reads=[D.A_d], writes=[ac[k]])
            load(0)
            load(1)
            load(2)
            oi = 0
            for i, (tb2, dg, e1) in enumerate(seq):
                if i + 3 < len(seq):
                    load(i + 3)
                k = i % NB
                for j in range(4):
                    for h in range(2):
                        kb.op("pe", lambda e: e.matmul(pss[j][h][:], lhsT=vb[k][:, j * 128:(j + 1) * 128], rhs=ac[k][:, h * 512:(h + 1) * 512],
                                                       start=(e1 == 0), stop=(e1 == 127)), reads=[vb[k], ac[k]], writes=[pss[j][h]])
                if e1 == 127:
                    for j in range(4):
                        for h in range(2):
                            o = ov[oi % 2]
                            if oi % 2 == 0:
                                kb.op("act", lambda e: e.copy(out=o[:], in_=pss[j][h][:]), reads=[pss[j][h]], writes=[o])
                            else:
                                kb.op("dve", lambda e: e.tensor_copy(out=o[:], in_=pss[j][h][:]), reads=[pss[j][h]], writes=[o])
                            oi += 1
                            d0 = dg * 512 + j * 128
                            t0 = tb2 * 1024 + h * 512
                            kb.dma("sp", D.peT_d.ap()[d0:d0 + 128, t0:t0 + 512], o[:], reads=[o], writes=[D.peT_d])
            kb.barrier()


def phase_F(kb, C, D):
    with contextlib.ExitStack() as st:
        wb = kb.sb("f_wb", [128, 2048], F32, st)
        kb.dma("pool", wb[:], D.normf_w.ap().partition_broadcast(128), writes=[wb])
        xs = [kb.sb(f"f_x{i}", [128, 2048], F32, st) for i in range(2)]
        pes = [kb.sb(f"f_pe{i}", [128, 16, 128], F32, st) for i in range(2)]
        junk = kb.sb("f_junk", [128, 2048], BF16, st)
        ss = kb.sb("f_ss", [128, NT], F32, st)
        rs = kb.sb("f_rs", [128, NT], F32, st)
        kb.op("pool", lambda e: e.memset(ss[:], 0.0), writes=[ss])
        pss = [[kb.ps(f"f_ps{i}_{j}", [128, 512], F32, st) for j in range(4)] for i in range(2)]
        pe_v = D.peT_d.ap().rearrange("(dc p) t -> p dc t", p=128)
        for tt in range(NT):
            xt = xs[tt % 2]
            pe = pes[tt % 2]
            ps = pss[tt % 2]
            kb.dma("sp", xt[:], D.x1_d.ap()[tt * 128:(tt + 1) * 128, :], reads=[D.x1_d], writes=[xt])
            kb.dma("act", pe[:], pe_v[:, :, tt * 128:(tt + 1) * 128], reads=[D.peT_d], writes=[pe])
            for dc in range(16):
                kb.op("pe", lambda e: e.transpose(out=ps[dc // 4][:, (dc % 4) * 128:(dc % 4 + 1) * 128], in_=pe[:, dc, :], identity=C.ident[:]),
                      reads=[pe, C.ident], writes=[ps[dc // 4]])
            for j in range(4):
                kb.op("dve", lambda e: e.tensor_tensor(out=xt[:, j * 512:(j + 1) * 512], in0=ps[j][:], in1=xt[:, j * 512:(j + 1) * 512], op=ALU.add),
                      reads=[ps[j], xt], writes=[xt])
            kb.op("act", lambda e: e.activation(out=junk[:], in_=xt[:], func=AF.Square, accum_out=ss[:, tt:tt + 1]), reads=[xt], writes=[junk, ss])
            kb.op("dve", lambda e: e.tensor_scalar(out=rs[:, tt:tt + 1], in0=ss[:, tt:tt + 1], scalar1=1.0 / 2048, scalar2=EPS,
                                                   op0=ALU.mult, op1=ALU.add), reads=[ss], writes=[rs])
            kb.op("act", lambda e: e.activation(out=rs[:, tt:tt + 1], in_=rs[:, tt:tt + 1], func=AF.Sqrt), reads=[rs], writes=[rs])
            kb.op("dve", lambda e: e.reciprocal(out=rs[:, tt:tt + 1], in_=rs[:, tt:tt + 1]), reads=[rs], writes=[rs])
            kb.op("dve", lambda e: e.scalar_tensor_tensor(out=xt[:], in0=xt[:], scalar=rs[:, tt:tt + 1], in1=wb[:], op0=ALU.mult, op1=ALU.mult),
                  reads=[xt, rs, wb], writes=[xt])
            kb.dma("pool", D.out.ap()[tt * 128:(tt + 1) * 128, :], xt[:], reads=[xt], writes=[D.out])
        kb.barrier()


STOP = 0
FAST_F32 = True
F32R = mybir.dt.float32r
def fr(ap):
    return ap.bitcast(F32R) if FAST_F32 else ap
class StopBuild(Exception):
    pass
def chk(n):
    if STOP == n:
        raise StopBuild()
GN_EPS = 64e-5
NEG_E05 = -0.6065306597126334


def phase_Rpre(kb, C, D):
    with contextlib.ExitStack() as st:
        rw = kb.sb("rp_rw", [128, 8, 8], F32, st)
        kb.dma("sp", rw[:], D.rw_c.ap(), writes=[rw])
        tmp = kb.sb("rp_tmp", [128, 2048], F32, st)
        lin = [kb.sb(f"rp_lin{i}", [128, 2048], BF16, st) for i in range(4)]
        for i in range(4):
            kb.op("pool", lambda e: e.memset(lin[i][:], 0.0), writes=[lin[i]])
            r0 = 3072 + i * 96
            kb.dma("sp", tmp[0:96, :], D.zs_d.ap()[r0:r0 + 96, :], reads=[D.zs_d], writes=[tmp])
            if i < 2:
                kb.op("act", lambda e: e.activation(out=lin[i][0:96, :], in_=tmp[0:96, :], func=AF.Tanh), reads=[tmp], writes=[lin[i]])
            else:
                kb.op("act", lambda e: e.copy(out=lin[i][0:96, :], in_=tmp[0:96, :]), reads=[tmp], writes=[lin[i]])
        sgl = kb.sb("rp_sgl", [128, 2, 2048], BF16, st)
        for kc in range(2):
            kb.dma("sp", tmp[:], D.zs_d.ap()[3456 + kc * 128:3456 + (kc + 1) * 128, :], reads=[D.zs_d], writes=[tmp])
            kb.op("act", lambda e: e.activation(out=sgl[:, kc, :], in_=tmp[:], func=AF.Sigmoid), reads=[tmp], writes=[sgl])
        wst = kb.sb("rp_wst", [128, 2048], F32, st)
        w2b = kb.sb("rp_w2b", [128, 2, 1024], BF16, st)
        a2b = kb.sb("rp_a2b", [128, 2, 1024], BF16, st)
        g2b = kb.sb("rp_g2b", [128, 2, 1024], BF16, st)
        wv = wst[:].rearrange("p (a c) -> p a c", a=2)
        kb.op("pool", lambda e: e.memset(w2b[:], 0.0), writes=[w2b])
        kb.op("pool", lambda e: e.memset(a2b[:], 0.0), writes=[a2b])
        kb.dma("sp", wv[0:96], D.w2.ap().rearrange("d l c -> l d c"), writes=[wst])
        kb.op("pool", lambda e: e.tensor_copy(out=w2b[0:96], in_=wv[0:96]), reads=[wst], writes=[w2b])
        kb.dma("sp", wv[0:96], D.a2.ap().rearrange("d l c -> l d c"), writes=[wst])
        kb.op("pool", lambda e: e.tensor_copy(out=a2b[0:96], in_=wv[0:96]), reads=[wst], writes=[a2b])
        kb.dma("sp", wv, D.g2.ap().rearrange("(kc p) c -> p kc c", p=128), writes=[wst])
        kb.op("pool", lambda e: e.tensor_copy(out=g2b[:], in_=wv), reads=[wst], writes=[g2b])
        outs = [kb.sb(f"rp_o{i}", [128, 2048], F32, st) for i in range(2)]
        pss = [[kb.ps(f"rp_ps{i}_{j}", [128, 512], F32, st) for j in range(4)] for i in range(2)]
        it = 0
        for cc in range(8):
            for d in range(2):
                for which in range(2):
                    ps = pss[it % 2]
                    o = outs[it % 2]
                    it += 1
                    wmat = w2b if which == 0 else a2b
                    xin = lin[d] if which == 0 else lin[2 + d]
                    bias = rw[:, cc, d:d + 1] if which == 0 else rw[:, cc, 2 + d:3 + d]
                    for tb in range(4):
                        kb.op("pe", lambda e: e.matmul(ps[tb][:], lhsT=wmat[:, d, cc * 128:(cc + 1) * 128], rhs=xin[:, tb * 512:(tb + 1) * 512],
                                                       start=True, stop=True), reads=[wmat, xin], writes=[ps[tb]])
                        kb.op("act", lambda e: e.activation(out=o[:, tb * 512:(tb + 1) * 512], in_=ps[tb][:], func=AF.Sigmoid, bias=bias),
                              reads=[ps[tb], rw], writes=[o])
                    if which == 0:
                        kb.op("dve", lambda e: e.tensor_scalar(out=o[:], in0=o[:], scalar1=NEG_E05, scalar2=None, op0=ALU.mult), reads=[o], writes=[o])
                        kb.dma("sp", D.ld_d.ap()[d, cc * 128:(cc + 1) * 128, :], o[:], reads=[o], writes=[D.ld_d])
                    else:
                        kb.dma("sp", D.a_d.ap()[d, cc * 128:(cc + 1) * 128, :], o[:], reads=[o], writes=[D.a_d])
        for tt in range(NT):
            ps = pss[tt % 2]
            o = outs[tt % 2]
            for hf in range(2):
                for kc in range(2):
                    kb.op("pe", lambda e: e.matmul(ps[hf][:], lhsT=sgl[:, kc, tt * 128:(tt + 1) * 128], rhs=g2b[:, kc, hf * 512:(hf + 1) * 512],
                                                   start=(kc == 0), stop=(kc == 1)), reads=[sgl, g2b], writes=[ps[hf]])
                kb.op("act", lambda e: e.copy(out=o[:, hf * 512:(hf + 1) * 512], in_=ps[hf][:]), reads=[ps[hf]], writes=[o])
            kb.dma("sp", D.g_d.ap()[tt * 128:(tt + 1) * 128, :], o[:, 0:1024], reads=[o], writes=[D.g_d])
        kb.barrier()


def phase_R(kb, C, D, ccs=range(8), dbg=None):
    with contextlib.ExitStack() as st:
        rw = kb.sb("r_rw", [128, 8, 8], F32, st)
        kb.dma("sp", rw[:], D.rw_c.ap(), writes=[rw])
        masks = kb.sb("r_masks", [128, 6, 128], F32, st)
        kb.dma("sp", masks[:], D.masks.ap(), writes=[masks])
        cst = kb.sb("r_cst", [128, 66], F32, st)
        kb.dma("sp", cst[:], D.rcst.ap()[:, 0:66], writes=[cst])
        segt = kb.sb("r_segm", [128, 2048], BF16, st)
        kb.dma("sp", segt[:], D.segm.ap(), writes=[segt])
        ident2 = cst[:, 0:64]
        sel = cst[:, 64:66]
        lnw = kb.sb("r_lnw", [128, 128], F32, st)
        lnb = kb.sb("r_lnb", [128, 128], F32, st)
        Rr = kb.sb("r_R", [128, 2048], F32, st)
        Kk = kb.sb("r_K", [128, 2048], F32, st)
        Vv = kb.sb("r_V", [128, 2048], F32, st)
        KKn = kb.sb("r_KK", [128, 2048], F32, st)
        E1 = kb.sb("r_E1", [128, 2048], F32, st)
        XI = kb.sb("r_XI", [128, 2048], F32, st)
        XE = kb.sb("r_XE", [128, 2048], F32, st)
        Aa = kb.sb("r_A", [128, 2048], F32, st)
        KD = kb.sb("r_KD", [128, 2048], F32, st)
        KT = kb.sb("r_KT", [128, 2048], F32, st)
        AT = kb.sb("r_AT", [128, 2048], F32, st)
        BTb = kb.sb("r_BTb", [128, 2048], F32, st)
        stt = kb.sb("r_stt", [128, 160], F32, st)
        Vtm = kb.sb("r_Vtm", [128, NT, 128], F32, st)
        Ysum = kb.sb("r_Ysum", [128, NT, 128], F32, st)
        MTa = kb.sb("r_MTa", [128, 32, 64], F32, st)
        Ca = kb.sb("r_Ca", [128, 32, 64], F32, st)
        H = [kb.sb(f"r_H{i}", [128, 64], F32, st) for i in range(2)]
        tot = kb.sb("r_tot", [128, 32], F32, st)
        GL = kb.sb("r_GL", [128, 32], F32, st)
        RhT = Vv

        class Set:
            pass
        sets = []
        for p in range(2):
            S = Set()
            S.XA = kb.sb(f"r_XA{p}", [128, 2, 2, 128], F32, st)
            S.KBm = kb.sb(f"r_KBm{p}", [128, 2, 2, 128], F32, st)
            S.PQ = [kb.sb(f"r_PQ{p}_{i}", [128, 2, 2, 128], BF16, st) for i in range(2)]
            S.QT = [kb.sb(f"r_QT{p}_{i}", [128, 2, 128], BF16, st) for i in range(2)]
            S.XT = kb.sb(f"r_XT{p}", [128, 2, 128], F32, st)
            S.TT = kb.sb(f"r_TT{p}", [128, 2, 128], F32, st)
            S.BW = kb.sb(f"r_BW{p}", [128, 2, 128], F32, st)
            S.BU = kb.sb(f"r_BU{p}", [128, 2, 128], F32, st)
            S.AGtm = kb.sb(f"r_AGtm{p}", [128, 128], F32, st)
            S.KGtm = kb.sb(f"r_KGtm{p}", [128, 128], F32, st)
            S.B = [kb.ps(f"r_bk{p}_{i}", [128, 512], F32, st) for i in range(4)]
            sets.append(S)
        psX = sets[0].B[0]
        bkH = [sets[0].B[1], sets[0].B[2]]

        v4 = lambda b: b[:].rearrange("p (h q s) -> p h q s", h=2, q=2)
        v3 = lambda b, lo: b[:, lo:lo + 256].rearrange("p (h s) -> p h s", h=2)

        def tile_gen(S, d, tt, RT, BT):
            M2 = masks[:, 2 * d:2 * d + 2, :]
            MST = masks[:, 2 - 2 * d, :]
            XA, KBm, PQ, QT, BW, BU, AGtm, KGtm = S.XA, S.KBm, S.PQ, S.QT, S.BW, S.BU, S.AGtm, S.KGtm
            B0, B1, B2, B3 = S.B
            psA, psB, psN = v4(B0), v4(B1), v4(B3)
            psC = v3(B2, 0)
            cols = slice(tt * 128, (tt + 1) * 128)
            for hh in range(2):
                pr = slice(hh * 64, hh * 64 + 64)
                kb.pe_fence()
                kb.op("pe", lambda e: e.matmul(psA[:, hh, 0, :], lhsT=fr(AT[pr, cols]), rhs=fr(BT[pr, cols]), start=True, stop=True),
                      reads=[AT, BT], writes=[B0])
                kb.op("pe", lambda e: e.matmul(psA[:, hh, 1, :], lhsT=fr(AT[pr, cols]), rhs=fr(RT[pr, cols]), start=True, stop=True),
                      reads=[AT, RT], writes=[B0])
                kb.op("pe", lambda e: e.matmul(psB[:, hh, 0, :], lhsT=fr(KT[pr, cols]), rhs=fr(BT[pr, cols]), start=True, stop=True),
                      reads=[KT, BT], writes=[B1])
                kb.op("pe", lambda e: e.matmul(psB[:, hh, 1, :], lhsT=fr(KT[pr, cols]), rhs=fr(RT[pr, cols]), start=True, stop=True),
                      reads=[KT, RT], writes=[B1])
                kb.op("pe", lambda e: e.matmul(psC[:, hh, :], lhsT=fr(BT[pr, cols]), rhs=fr(AT[pr, cols]), start=True, stop=True),
                      reads=[AT, BT], writes=[B2])
            kb.pe_fence()
            yield
            M2b = M2[:, None, :, :].to_broadcast([128, 2, 2, 128])
            q0, q1 = S.XT, QT[1]
            kb.op("dve", lambda e: e.tensor_tensor(out=fr(XA[:]), in0=psA, in1=M2b, op=ALU.mult), reads=[B0, masks], writes=[XA])
            kb.op("dve", lambda e: e.tensor_tensor(out=fr(q0[:]), in0=psC, in1=MST[:, None, :].to_broadcast([128, 2, 128]), op=ALU.mult),
                  reads=[B2, masks], writes=[q0])
            kb.op("dve", lambda e: e.tensor_tensor(out=fr(KBm[:]), in0=psB, in1=M2b, op=ALU.mult), reads=[B1, masks], writes=[KBm])
            pq = PQ[0]
            kb.op("pool", lambda e: e.tensor_tensor(out=pq[:, :, 0, :], in0=XA[:, :, 0, :], in1=C.ident[:, None, :].to_broadcast([128, 2, 128]),
                                                    op=ALU.add), reads=[XA, C.ident], writes=[pq])
            yield
            for hh in range(2):
                kb.op("pe", lambda e: e.matmul(psN[:, hh, 1, :], lhsT=fr(q0[:, hh, :]), rhs=fr(XA[:, hh, 0, :]), start=True, stop=True),
                      reads=[q0, XA], writes=[B3])
                kb.op("pe", lambda e: e.matmul(psC[:, hh, :], lhsT=fr(XA[:, hh, 0, :]), rhs=fr(q0[:, hh, :]), start=True, stop=True),
                      reads=[q0, XA], writes=[B2])
            yield
            kb.op("act", lambda e: e.copy(out=pq[:, :, 1, :], in_=psN[:, :, 1, :]), reads=[B3], writes=[pq])
            kb.op("dve", lambda e: e.tensor_copy(out=q1[:], in_=psC), reads=[B2], writes=[q1])
            yield
            cur = 0
            qcur = 1
            for lev in range(1, 6):
                pq = PQ[cur]
                pqn = PQ[1 - cur]
                qt = QT[qcur]
                qtn = QT[1 - qcur]
                last = (lev == 5)
                for hh in range(2):
                    if last:
                        kb.op("pe", lambda e: e.matmul(psN[:, hh, 0, :], lhsT=qt[:, hh, :], rhs=pq[:, hh, 0, :], start=True, stop=True),
                              reads=[qt, pq], writes=[B3])
                    else:
                        kb.op("pe", lambda e: e.matmul(psN[:, hh, :, :], lhsT=qt[:, hh, :], rhs=pq[:, hh, :, :], start=True, stop=True),
                              reads=[qt, pq], writes=[B3])
                        kb.op("pe", lambda e: e.matmul(psC[:, hh, :], lhsT=pq[:, hh, 1, :], rhs=qt[:, hh, :], start=True, stop=True),
                              reads=[qt, pq], writes=[B2])
                yield
                if last:
                    kb.op("dve", lambda e: e.tensor_tensor(out=fr(S.TT[:]), in0=psN[:, :, 0, :], in1=pq[:, :, 0, :], op=ALU.add),
                          reads=[B3, pq], writes=[S.TT])
                else:
                    kb.op("dve", lambda e: e.tensor_tensor(out=pqn[:, :, 0, :], in0=psN[:, :, 0, :], in1=pq[:, :, 0, :], op=ALU.add),
                          reads=[B3, pq], writes=[pqn])
                    kb.op("act", lambda e: e.copy(out=pqn[:, :, 1, :], in_=psN[:, :, 1, :]), reads=[B3], writes=[pqn])
                    kb.op("dve", lambda e: e.tensor_copy(out=qtn[:], in_=psC), reads=[B2], writes=[qtn])
                yield
                cur = 1 - cur
                qcur = 1 - qcur
            TT = S.TT
            W_ = B0[:, 0:128].rearrange("p (h i) -> p h i", h=2)
            Bt_ = B0[:, 128:256]
            U_ = B0[:, 256:512].rearrange("p (h i) -> p h i", h=2)
            for hh in range(2):
                kb.op("pe", lambda e: e.matmul(W_[:, hh, :], lhsT=fr(KBm[:, hh, 0, :]), rhs=fr(Vtm[:, tt, hh * 64:(hh + 1) * 64]), start=True, stop=True),
                      reads=[KBm, Vtm], writes=[B0])
            kb.op("pe", lambda e: e.transpose(out=Bt_, in_=BT[:, cols], identity=C.ident[:]), reads=[BT, C.ident], writes=[B0])
            AG_ = B1[:, 256:384]
            KG_ = B1[:, 384:512]
            kb.op("pe", lambda e: e.transpose(out=AG_, in_=Aa[:, cols], identity=C.ident[:]), reads=[Aa, C.ident], writes=[B1])
            kb.op("pe", lambda e: e.transpose(out=KG_, in_=KD[:, cols], identity=C.ident[:]), reads=[KD, C.ident], writes=[B1])
            yield
            kb.op("act", lambda e: e.copy(out=fr(BW[:, :, 64:128]), in_=W_), reads=[B0], writes=[BW])
            kb.op("dve", lambda e: e.tensor_copy(out=fr(BW[:, :, 0:64]), in_=Bt_.rearrange("p (h j) -> p h j", h=2)), reads=[B0], writes=[BW])
            kb.op("act", lambda e: e.copy(out=AGtm[:], in_=AG_), reads=[B1], writes=[AGtm])
            kb.op("act", lambda e: e.copy(out=KGtm[:], in_=KG_), reads=[B1], writes=[KGtm])
            yield
            for hh in range(2):
                kb.op("pe", lambda e: e.matmul(U_[:, hh, :], lhsT=fr(TT[:, hh, :]), rhs=fr(BW[:, hh, :]), start=True, stop=True),
                      reads=[TT, BW], writes=[B0])
            yield
            kb.op("dve", lambda e: e.tensor_copy(out=fr(BU[:]), in_=U_), reads=[B0], writes=[BU])
            yield
            Y_ = B1[:, 0:128].rearrange("p (h i) -> p h i", h=2)
            R_ = B1[:, 128:256]
            for hh in range(2):
                kb.op("pe", lambda e: e.matmul(Y_[:, hh, :], lhsT=fr(XA[:, hh, 1, :]), rhs=fr(BU[:, hh, 64:128]), start=True, stop=False),
                      reads=[XA, BU], writes=[B1])
                kb.op("pe", lambda e: e.matmul(Y_[:, hh, :], lhsT=fr(KBm[:, hh, 1, :]), rhs=fr(Vtm[:, tt, hh * 64:(hh + 1) * 64]), start=False, stop=True),
                      reads=[KBm, Vtm], writes=[B1])
            for hh in range(2):
                kb.op("pe", lambda e: e.matmul(R_[hh * 64:(hh + 1) * 64, :], lhsT=BU[:, hh, 0:64], rhs=XA[:, hh, 1, :], start=True, stop=True),
                      reads=[BU, XA], writes=[B1])
            for n in range(2):
                tr = slice(n * 64, n * 64 + 64)
                bk = (B2, B3)[n]
                for hh in range(2):
                    pr = slice(hh * 64, hh * 64 + 64)
                    kb.op("pe", lambda e: e.matmul(bk[pr, 0:64], lhsT=BU[tr, hh, 0:64], rhs=AGtm[tr, pr], start=True, stop=True),
                          reads=[BU, AGtm], writes=[bk])
                    kb.op("pe", lambda e: e.matmul(bk[pr, 64:128], lhsT=AGtm[tr, pr], rhs=BU[tr, hh, 64:128], start=True, stop=False),
                          reads=[BU, AGtm], writes=[bk])
                    kb.op("pe", lambda e: e.matmul(bk[pr, 64:128], lhsT=KGtm[tr, pr], rhs=Vtm[tr, tt, pr], start=False, stop=True),
                          reads=[KGtm, Vtm], writes=[bk])
            kb.pe_fence()
            yield
            if d == 0:
                kb.op("act", lambda e: e.copy(out=Ysum[:, tt, :], in_=B1[:, 0:128]), reads=[B1], writes=[Ysum])
            else:
                kb.op("dve", lambda e: e.tensor_tensor(out=Ysum[:, tt, :], in0=B1[:, 0:128], in1=Ysum[:, tt, :], op=ALU.add),
                      reads=[B1, Ysum], writes=[Ysum])
            kb.op("dve", lambda e: e.tensor_tensor(out=RhT[:, cols], in0=R_, in1=RT[:, cols], op=ALU.add), reads=[B1, RT], writes=[RhT])
            for n in range(2):
                ch = tt * 2 + n
                bk = (B2, B3)[n]
                kb.op("dve", lambda e: e.scalar_tensor_tensor(out=MTa[:, ch, :], in0=ident2, scalar=GL[:, ch:ch + 1], in1=bk[:, 0:64],
                                                              op0=ALU.mult, op1=ALU.add), reads=[cst, GL, bk], writes=[MTa])
                kb.op("act", lambda e: e.copy(out=Ca[:, ch, :], in_=bk[:, 64:128]), reads=[bk], writes=[Ca])
            yield

        def run_tiles(d, RT, BT):
            pending = list(range(NT))
            active = []
            free_sets = [sets[0], sets[1]]
            S = free_sets.pop(0)
            g = tile_gen(S, d, pending.pop(0), RT, BT)
            active.append((g, S))
            for _ in range(8):
                next(g)
            while active or pending:
                if pending and free_sets:
                    S = free_sets.pop(0)
                    active.append((tile_gen(S, d, pending.pop(0), RT, BT), S))
                for item in list(active):
                    g, S = item
                    try:
                        next(g)
                    except StopIteration:
                        active.remove(item)
                        free_sets.append(S)

        for cc in ccs:
            rows = slice(cc * 128, (cc + 1) * 128)
            kb.dma("sp", Rr[:], D.zs_d.ap()[cc * 128:(cc + 1) * 128, :], reads=[D.zs_d], writes=[Rr])
            kb.dma("act", Kk[:], D.zs_d.ap()[1024 + cc * 128:1024 + (cc + 1) * 128, :], reads=[D.zs_d], writes=[Kk])
            kb.dma("sp", Vv[:], D.zs_d.ap()[2048 + cc * 128:2048 + (cc + 1) * 128, :], reads=[D.zs_d], writes=[Vv])
            kb.dma("pool", lnw[:], D.lnx_w.ap()[cc * 128:(cc + 1) * 128].partition_broadcast(128), writes=[lnw])
            kb.dma("pool", lnb[:], D.lnx_b.ap()[cc * 128:(cc + 1) * 128].partition_broadcast(128), writes=[lnb])
            for g in range(4):
                for j in range(4):
                    tt = g * 4 + j
                    kb.op("pe", lambda e: e.transpose(out=psX[:, j * 128:(j + 1) * 128], in_=Vv[:, tt * 128:(tt + 1) * 128], identity=C.ident[:]),
                          reads=[Vv, C.ident], writes=[psX])
                kb.op("act", lambda e: e.copy(out=fr(Vtm[:, g * 4:(g + 1) * 4, :]), in_=psX[:].rearrange("p (j c) -> p j c", j=4)), reads=[psX], writes=[Vtm])
            kb.op("act", lambda e: e.activation(out=KKn[:], in_=Kk[:], func=AF.Copy, scale=rw[:, cc, 4:5]),
                  reads=[Kk, rw], writes=[KKn])
            kb.op("act", lambda e: e.activation(out=XI[:], in_=KKn[:], func=AF.Square), reads=[KKn], writes=[XI])
            for tb in range(4):
                kb.op("pe", lambda e: e.matmul(psX[:], lhsT=masks[:, 5, :], rhs=XI[:, tb * 512:(tb + 1) * 512], start=True, stop=True),
                      reads=[masks, XI], writes=[psX])
                kb.op("act", lambda e: e.activation(out=XE[:, tb * 512:(tb + 1) * 512], in_=psX[:], func=AF.Sqrt), reads=[psX], writes=[XE])
            kb.op("dve", lambda e: e.tensor_scalar(out=XE[:], in0=XE[:], scalar1=1e-12, scalar2=None, op0=ALU.max), reads=[XE], writes=[XE])
            kb.op("dve", lambda e: e.reciprocal(out=XE[:], in_=XE[:]), reads=[XE], writes=[XE])
            kb.op("pool", lambda e: e.tensor_tensor(out=KKn[:], in0=KKn[:], in1=XE[:], op=ALU.mult), reads=[KKn, XE], writes=[KKn])
            chk(1)
            def prep_gen(d):
                yield
                kb.dma("sp", XE[:], D.ld_d.ap()[d, cc * 128:(cc + 1) * 128, :], reads=[D.ld_d], writes=[XE])
                yield
                kb.dma("act", Aa[:], D.a_d.ap()[d, cc * 128:(cc + 1) * 128, :], reads=[D.a_d], writes=[Aa])
                yield
                kb.op("dve", lambda e: e.tensor_scalar(out=KD[:], in0=Aa[:], scalar1=-1.0, scalar2=rw[:, cc, 5:6], op0=ALU.add, op1=ALU.mult),
                      reads=[Aa, rw], writes=[KD])
                yield
                kb.op("dve", lambda e: e.scalar_tensor_tensor(out=KD[:], in0=KD[:], scalar=1.0, in1=Kk[:], op0=ALU.add, op1=ALU.mult),
                      reads=[KD, Kk], writes=[KD])
                yield
                kb.op("dve", lambda e: e.scalar_tensor_tensor(out=XI[:], in0=KD[:], scalar=rw[:, cc, 6:7], in1=Rr[:], op0=ALU.mult, op1=ALU.mult),
                      reads=[KD, rw, Rr], writes=[XI])
                for tt in range(NT):
                    kb.op("pe", lambda e: e.matmul(psX[:, tt * 2:tt * 2 + 2], lhsT=XI[:, tt * 128:(tt + 1) * 128], rhs=sel, start=True, stop=True),
                          reads=[XI, cst], writes=[psX])
                yield
                kb.op("act", lambda e: e.copy(out=stt[:, 64 + 32 * d:96 + 32 * d], in_=psX[:, 0:32]), reads=[psX], writes=[stt])
                yield
                kb.op("pool", lambda e: e.tensor_tensor(out=Aa[:], in0=Aa[:], in1=KKn[:], op=ALU.mult), reads=[Aa, KKn], writes=[Aa])
                yield
                kb.op("dve", lambda e: e.tensor_tensor_scan(out=XI[:], data0=segt[:], data1=XE[:], initial=0.0, op0=ALU.mult, op1=ALU.add),
                      reads=[segt, XE], writes=[XI])
                yield
                kb.op("pool", lambda e: e.tensor_copy(out=tot[:], in_=XI[:].rearrange("p (n s) -> p n s", s=64)[:, :, 63]), reads=[XI], writes=[tot])
                yield
                kb.op("act", lambda e: e.activation(out=GL[:], in_=tot[:], func=AF.Exp), reads=[tot], writes=[GL])
                if d == 0:
                    kb.op("pool", lambda e: e.tensor_tensor(out=XE[:], in0=XI[:], in1=XE[:], op=ALU.subtract), reads=[XI, XE], writes=[XE])
                else:
                    kb.op("dve", lambda e: e.tensor_tensor(out=XI[:].rearrange("p (n s) -> p n s", s=64),
                                                           in0=tot[:, :, None].to_broadcast([128, 32, 64]),
                                                           in1=XI[:].rearrange("p (n s) -> p n s", s=64), op=ALU.subtract),
                          reads=[tot, XI], writes=[XI])
                    kb.op("pool", lambda e: e.tensor_tensor(out=XE[:], in0=XI[:], in1=XE[:], op=ALU.add), reads=[XI, XE], writes=[XE])
                cI, cE = (XI, XE) if d == 0 else (XE, XI)
                yield
                kb.op("act", lambda e: e.activation(out=fr(E1[:]), in_=cI[:], func=AF.Exp), reads=[cI], writes=[E1])
                yield
                kb.op("act", lambda e: e.activation(out=cI[:], in_=cI[:], func=AF.Exp, scale=-1.0), reads=[cI], writes=[cI])
                yield
                kb.op("act", lambda e: e.activation(out=cE[:], in_=cE[:], func=AF.Exp), reads=[cE], writes=[cE])
                yield
                kb.op("dve", lambda e: e.tensor_tensor(out=fr(E1[:]), in0=E1[:], in1=Rr[:], op=ALU.mult), reads=[E1, Rr], writes=[E1])
                yield
                kb.op("pool", lambda e: e.tensor_tensor(out=fr(KT[:]), in0=KD[:], in1=cI[:], op=ALU.mult), reads=[KD, cI], writes=[KT])
                yield
                kb.op("dve", lambda e: e.tensor_tensor(out=fr(AT[:]), in0=Aa[:], in1=cI[:], op=ALU.mult), reads=[Aa, cI], writes=[AT])
                yield
                kb.op("dve", lambda e: e.scalar_tensor_tensor(out=fr(BTb[:]), in0=cE[:], scalar=-1.0, in1=KKn[:], op0=ALU.mult, op1=ALU.mult),
                      reads=[cE, KKn], writes=[BTb])
                yield
                kb.op("pool", lambda e: e.tensor_tensor(out=cI[:].rearrange("p (n s) -> p n s", s=64), in0=cI[:].rearrange("p (n s) -> p n s", s=64),
                                                        in1=GL[:, :, None].to_broadcast([128, 32, 64]), op=ALU.mult), reads=[cI, GL], writes=[cI])
                yield
                kb.op("dve", lambda e: e.tensor_tensor(out=KD[:], in0=KD[:], in1=cI[:], op=ALU.mult), reads=[KD, cI], writes=[KD])
                yield
                kb.op("pool", lambda e: e.tensor_tensor(out=Aa[:], in0=Aa[:], in1=cI[:], op=ALU.mult), reads=[Aa, cI], writes=[Aa])
                yield

            def seq_gen(d):
                kb.op("pool", lambda e: e.memset(H[0][:], 0.0), writes=[H[0]])
                order = range(32) if d == 0 else range(31, -1, -1)
                hc = 0
                for ch in order:
                    tt, n = ch // 2, ch % 2
                    ccols = slice(ch * 64, ch * 64 + 64)
                    Hc, Hn = H[hc], H[1 - hc]
                    tr = slice(n * 64, n * 64 + 64)
                    for hh in range(2):
                        pr = slice(hh * 64, hh * 64 + 64)
                        bk = bkH[hh]
                        kb.op("pe", lambda e: e.matmul(bk[pr, 64:128], lhsT=MTa[pr, ch, :], rhs=Hc[pr, :], start=True, stop=True),
                              reads=[MTa, Hc], writes=[bk])
                        kb.op("pe", lambda e: e.matmul(bk[tr, 0:64], lhsT=RhT[pr, ccols], rhs=Hc[pr, :], start=True, stop=True),
                              reads=[RhT, Hc], writes=[bk])
                    for hh in range(2):
                        pr = slice(hh * 64, hh * 64 + 64)
                        bk = bkH[hh]
                        kb.op("dve", lambda e: e.tensor_tensor(out=Hn[pr, :], in0=bk[pr, 64:128], in1=Ca[pr, ch, :], op=ALU.add),
                              reads=[bk, Ca], writes=[Hn])
                    for hh in range(2):
                        pr = slice(hh * 64, hh * 64 + 64)
                        bk = bkH[hh]
                        kb.op("act" if False else "dve", lambda e: e.tensor_tensor(out=Ysum[tr, tt, pr], in0=bk[tr, 0:64], in1=Ysum[tr, tt, pr], op=ALU.add),
                              reads=[bk, Ysum], writes=[Ysum])
                    hc = 1 - hc
                    yield
                yield

            def exhaust(g):
                for _ in g:
                    pass

            exhaust(prep_gen(0))
            run_tiles(0, E1, BTb)
            sg = seq_gen(0)
            pg = prep_gen(1)
            done_p = done_s = False
            while not (done_p and done_s):
                if not done_p:
                    try:
                        next(pg)
                    except StopIteration:
                        done_p = True
                for _ in range(2):
                    if not done_s:
                        try:
                            next(sg)
                        except StopIteration:
                            done_s = True
            run_tiles(1, E1, BTb)
            exhaust(seq_gen(1))
            chk(10)
            if dbg is not None and "Ysum" in dbg:
                kb.dma("sp", D.dbgY.ap()[:, cc * 128:(cc + 1) * 128].rearrange("(tt p) c -> p tt c", p=128), Ysum[:], reads=[Ysum], writes=[D.dbgY])
            Y3 = Ysum[:].rearrange("p t (h i) -> p (t h) i", h=2)
            st_mu = stt[:, 0:32]
            st_var = stt[:, 32:64]
            bon = stt[:, 128:160]
            kb.op("dve", lambda e: e.tensor_reduce(out=st_mu, in_=Y3, axis=AX.X, op=ALU.add), reads=[Ysum], writes=[stt])
            kb.op("dve", lambda e: e.tensor_scalar(out=st_mu, in0=st_mu, scalar1=1.0 / 64, scalar2=None, op0=ALU.mult), reads=[stt], writes=[stt])
            kb.op("dve", lambda e: e.tensor_tensor(out=Y3, in0=Y3, in1=st_mu[:, :, None].to_broadcast([128, 32, 64]), op=ALU.subtract),
                  reads=[Ysum, stt], writes=[Ysum])
            sqv = XI[:].rearrange("p (a i) -> p a i", i=64)
            kb.op("act", lambda e: e.activation(out=sqv, in_=Y3, func=AF.Square), reads=[Ysum], writes=[XI])
            kb.op("dve", lambda e: e.tensor_reduce(out=st_var, in_=sqv, axis=AX.X, op=ALU.add), reads=[XI], writes=[stt])
            kb.op("dve", lambda e: e.tensor_scalar(out=st_var, in0=st_var, scalar1=1.0 / 64, scalar2=GN_EPS, op0=ALU.mult, op1=ALU.add),
                  reads=[stt], writes=[stt])
            kb.op("act", lambda e: e.activation(out=st_var, in_=st_var, func=AF.Sqrt), reads=[stt], writes=[stt])
            kb.op("dve", lambda e: e.reciprocal(out=st_var, in_=st_var), reads=[stt], writes=[stt])
            kb.op("dve", lambda e: e.tensor_tensor(out=Y3, in0=Y3, in1=st_var[:, :, None].to_broadcast([128, 32, 64]), op=ALU.mult),
                  reads=[Ysum, stt], writes=[Ysum])
            kb.op("pool", lambda e: e.tensor_tensor(out=Ysum[:], in0=Ysum[:], in1=lnw[:, None, :].to_broadcast([128, NT, 128]), op=ALU.mult),
                  reads=[Ysum, lnw], writes=[Ysum])
            kb.op("pool", lambda e: e.tensor_tensor(out=Ysum[:], in0=Ysum[:], in1=lnb[:, None, :].to_broadcast([128, NT, 128]), op=ALU.add),
                  reads=[Ysum, lnb], writes=[Ysum])
            kb.op("dve", lambda e: e.tensor_tensor(out=bon, in0=stt[:, 64:96], in1=stt[:, 96:128], op=ALU.add), reads=[stt], writes=[stt])
            kb.op("dve", lambda e: e.tensor_scalar(out=bon, in0=bon, scalar1=0.5, scalar2=None, op0=ALU.mult), reads=[stt], writes=[stt])
            V3 = Vtm[:].rearrange("p t (h i) -> p (t h) i", h=2)
            kb.op("pool", lambda e: e.tensor_tensor(out=sqv, in0=V3, in1=bon[:, :, None].to_broadcast([128, 32, 64]), op=ALU.mult),
                  reads=[Vtm, stt], writes=[XI])
            kb.op("pool", lambda e: e.tensor_tensor(out=Y3, in0=Y3, in1=sqv, op=ALU.add), reads=[Ysum, XI], writes=[Ysum])
            gt = XE[:].rearrange("p (t c) -> p t c", c=128)
            kb.dma("sp", gt, D.g_d.ap()[:, cc * 128:(cc + 1) * 128].rearrange("(tt p) c -> p tt c", p=128), reads=[D.g_d], writes=[XE])
            kb.op("dve", lambda e: e.tensor_tensor(out=Ysum[:], in0=Ysum[:], in1=gt, op=ALU.mult), reads=[Ysum, XE], writes=[Ysum])
            if dbg is not None and "yfin" in dbg:
                kb.dma("sp", D.dbgF.ap()[:, cc * 128:(cc + 1) * 128].rearrange("(tt p) c -> p tt c", p=128), Ysum[:], reads=[Ysum], writes=[D.dbgF])
            for g in range(4):
                for j in range(4):
                    tt = g * 4 + j
                    kb.op("pe", lambda e: e.transpose(out=psX[:, j * 128:(j + 1) * 128], in_=Ysum[:, tt, :], identity=C.ident[:]),
                          reads=[Ysum, C.ident], writes=[psX])
                kb.op("act", lambda e: e.copy(out=KD[:, g * 512:(g + 1) * 512], in_=psX[:]), reads=[psX], writes=[KD])
            yb = Aa[:].bitcast(BF16)[:, 0:2048]
            kb.op("dve", lambda e: e.tensor_copy(out=yb, in_=KD[:]), reads=[KD], writes=[Aa])
            kb.dma("sp", D.ymT_d.ap()[cc], yb, reads=[Aa], writes=[D.ymT_d])
        kb.barrier()


def host_consts():
    S = 2048
    rows = np.repeat(np.arange(32), 64).astype(np.float32)
    cols = np.tile(np.arange(64), 32).astype(np.float32)
    inv = (10000.0 ** (-np.arange(0, 32, 2, dtype=np.float32) / 32)).astype(np.float32)
    ar = rows[:, None] * inv[None]
    ac = cols[:, None] * inv[None]
    tabC = np.concatenate([np.cos(ar), np.cos(ac)], 1).astype(np.float32)
    tabS = np.concatenate([np.sin(ar), np.sin(ac)], 1).astype(np.float32)
    ident = np.eye(128, dtype=np.float32)
    iota = np.tile(np.arange(128, dtype=np.float32)[None], (128, 1))
    r = np.arange(128)[:, None]
    s = np.arange(128)[None, :]
    same = (r // 64) == (s // 64)
    masks = np.zeros((128, 6, 128), np.float32)
    masks[:, 0] = same & (r < s)
    masks[:, 1] = same & (r <= s)
    masks[:, 2] = same & (r > s)
    masks[:, 3] = same & (r >= s)
    masks[:, 4] = same & (r > s)
    masks[:, 5] = same
    rcst = np.zeros((128, 66 + 2048), np.float32)
    pp = np.arange(128)
    rcst[pp, pp % 64] = 1.0
    rcst[:, 64] = (pp // 64 == 0)
    rcst[:, 65] = (pp // 64 == 1)
    seg = np.ones(2048, np.float32); seg[::64] = 0.0
    rcst[:, 66:] = seg[None]
    import ml_dtypes
    segm = np.ascontiguousarray(np.broadcast_to(seg[None], (128, 2048))).astype(ml_dtypes.bfloat16)
    return dict(tabC=tabC, tabS=tabS, ident=ident, iota=iota, masks=masks, rcst=rcst, segm=segm)

def prep_shared(inp):
    L = 0
    d = host_consts()
    mu_c = np.zeros((128, 3 * NRCH), np.float32)
    for ci, (c0, cs) in enumerate(RCH):
        mu_c[:cs, ci] = inp["mu_prev"][L, c0:c0 + cs]
        mu_c[:cs, NRCH + ci] = inp["mu_next"][L, c0:c0 + cs]
    d["mu_c"] = mu_c
    w = inp["w_in"][L].reshape(16, 128, 5248)
    wt = np.empty((128, 16 * 5248), np.float32)
    for (c0, cs) in RCH:
        wt[:, 16 * c0:16 * (c0 + cs)] = w[:, :, c0:c0 + cs].transpose(1, 0, 2).reshape(128, 16 * cs)
    RWK = 3712
    for cg in range(3):
        for hf in range(2):
            o0 = 16 * RWK + (cg * 2 + hf) * 4096
            wt[:, o0:o0 + 4096] = w[hf * 8:(hf + 1) * 8, :, RWK + cg * 512:RWK + (cg + 1) * 512].transpose(1, 0, 2).reshape(128, 4096)
    d["w_in"] = wt
    for k in ["norm1_w", "norm2_w", "q_norm_w", "k_norm_w", "w_out", "w_pq", "w2", "a2", "g2", "lnx_w", "lnx_b", "v_tab"]:
        d[k] = np.ascontiguousarray(inp[k][L])
    d["normf_w"] = np.ascontiguousarray(inp["normf_w"])
    d["skT"] = np.ascontiguousarray(inp["sub_keys"][L].reshape(16, 128, 128).transpose(2, 0, 1))
    d["u_tabT"] = np.ascontiguousarray(inp["u_tab"][L].reshape(128, 128, 16, 128).transpose(0, 3, 2, 1)).reshape(128, 128, 2048)
    rw = np.zeros((128, 8, 8), np.float32)
    def ch(v):
        return v.reshape(8, 128).T
    rw[:, :, 0] = ch(inp["w0"][L, 0]); rw[:, :, 1] = ch(inp["w0"][L, 1])
    rw[:, :, 2] = ch(inp["a0"][L, 0]); rw[:, :, 3] = ch(inp["a0"][L, 1])
    rw[:, :, 4] = ch(inp["k_k"][L]); rw[:, :, 5] = ch(inp["k_a"][L]); rw[:, :, 6] = ch(inp["r_k"][L].reshape(-1))
    d["rw_c"] = rw
    return d


def build_program():
    nc = bass.Bass("TRN2", target_bir_lowering=False)
    kb = KB(nc)
    D = declare(kb, None)
    C = consts(kb, D)
    phase_A(kb, C, D)
    phase_Rpre(kb, C, D)
    phase_R(kb, C, D)
    phase_T(kb, C, D)
    phase_O(kb, C, D, D.ymT_d)
    phase_P(kb, C, D)
    phase_F(kb, C, D)
    kb.finish("sp")
    return nc


def kernel(**inputs):
    inp = {k: np.asarray(v) for k, v in inputs.items()}
    shared = prep_shared(inp)
    nc = build_program()
    in_maps = []
    for b in range(8):
        d = dict(shared)
        d["x"] = np.ascontiguousarray(inp["x"][b])
        in_maps.append(d)
    res = run_bass_kernel_spmd(nc, in_maps, core_ids=list(range(8)))
    out = np.stack([np.asarray(r["out"], dtype=np.float32) for r in res.results], axis=0)
    return out
```
